# Optimizing a Trainium2 kernel written in Bass

```python
import math
import jax
import jax.numpy as jnp
from jax import lax
import numpy as np

D_MODEL = 1024
BATCH = 4
SEQ = 8192
DEPTH = 2

GRID_W = 64
CTX_LEN = 256
EPS = 1e-6
N_MOD = 9
D_FF = 2816
CONV_K = 5
GDN_HEADS = 4
GDN_DK = 128
GDN_DV = 128
GDN_CHUNK = 64
DIFF_HEADS = 4
DIFF_DQK = 64
DIFF_DV = 2 * DIFF_DQK
Q_BLOCK = 128
ROPE_BASE = 10000.0
ROPE_PAIRS_PER_AXIS = DIFF_DQK // 4
SSD_HEADS = 8
SSD_HEADDIM = 64
SSD_INNER = SSD_HEADS * SSD_HEADDIM
SSD_GROUPS = 2
SSD_STATE = 128
SSD_CHUNK = 128
N_BRANCH = 3
BRANCH_W = 512
GDN_QKV = 2 * GDN_HEADS * GDN_DK + GDN_HEADS * GDN_DV
GDN_VW = GDN_HEADS * GDN_DV
DIFF_QK = DIFF_HEADS * 2 * DIFF_DQK
DIFF_VW = DIFF_HEADS * DIFF_DV
SSD_XBC = SSD_INNER + 2 * SSD_GROUPS * SSD_STATE
SPLIT_SIZES = (GDN_QKV, GDN_VW, 2 * GDN_HEADS, 2 * GDN_HEADS, DIFF_QK, DIFF_QK, DIFF_VW, SSD_INNER, SSD_XBC, 2 * SSD_HEADS, N_BRANCH * D_MODEL)
PROJ_W = sum(SPLIT_SIZES)

kernel_name = 'hybrid_gdn_diffattn_ssd_dit_block'


def split_cols(t, sizes):
    idx = [int(v) for v in np.cumsum(sizes)[:-1]]
    return jnp.split(t, idx, axis=-1)


def rms_norm(x, g):
    xf = x.astype(jnp.float32)
    y = xf * lax.rsqrt(jnp.mean(xf * xf, axis=-1, keepdims=True) + EPS)
    return (y * g.astype(jnp.float32)).astype(x.dtype)


def l2norm(x):
    return x * lax.rsqrt(jnp.sum(x * x, axis=-1, keepdims=True) + EPS)


def modulate(h, shift, scale):
    return h * (1 + scale) + shift


def swiglu(h, w1, w2):
    g, u = jnp.split(h @ w1, 2, axis=-1)
    return (jax.nn.silu(g) * u) @ w2


def flip_seq(t, rev):
    return jnp.flip(t, axis=1) if rev else t


def dw_conv_centred(x, w, b=None):
    pad = CONV_K // 2
    y = lax.conv_general_dilated(x, w.astype(x.dtype)[:, None, :], window_strides=(1,), padding=[(pad, pad)],
                                 dimension_numbers=('NWC', 'WIO', 'NWC'), feature_group_count=x.shape[-1])
    return y if b is None else y + b.astype(x.dtype)


def segsum(a):
    t = a.shape[-1]
    ae = jnp.broadcast_to(a[..., :, None], a.shape + (t,))
    cs = jnp.cumsum(jnp.where(jnp.tril(jnp.ones((t, t), bool), -1), ae, 0.0), axis=-2)
    return jnp.where(jnp.tril(jnp.ones((t, t), bool)), cs, -jnp.inf)


def gated_delta_chunked(q, k, v, g, beta, s0):
    b, L, H, dk = q.shape
    dv = v.shape[-1]
    n = L // GDN_CHUNK

    def blk(t):
        return jnp.moveaxis(t.reshape((b, n, GDN_CHUNK) + t.shape[2:]), 3, 1)

    q, k, v, g, beta = blk(q), blk(k), blk(v), blk(g), blk(beta)
    gc = jnp.cumsum(g, axis=-1)
    decay = jnp.exp(segsum(g))
    kb = k * beta[..., None]
    vb = v * beta[..., None]
    strict = jnp.tril(jnp.ones((GDN_CHUNK, GDN_CHUNK), bool), -1)
    m = jnp.where(strict, jnp.einsum('bhnid,bhnjd->bhnij', kb, k) * decay, 0.0)
    a_mat = m + jnp.eye(GDN_CHUNK, dtype=m.dtype)
    u = lax.linalg.triangular_solve(a_mat, vb, left_side=True, lower=True, unit_diagonal=True)
    w = lax.linalg.triangular_solve(a_mat, kb * jnp.exp(gc)[..., None], left_side=True, lower=True, unit_diagonal=True)
    attn = jnp.einsum('bhnid,bhnjd->bhnij', q, k) * decay
    q_dec = q * jnp.exp(gc)[..., None]
    k_dec = k * jnp.exp(gc[..., -1:] - gc)[..., None]
    g_tot = jnp.exp(gc[..., -1])

    def step(s, inp):
        u_c, w_c, q_c, a_c, k_c, gt_c = inp
        v_new = u_c - jnp.einsum('bhid,bhdv->bhiv', w_c, s)
        o = jnp.einsum('bhid,bhdv->bhiv', q_c, s) + jnp.einsum('bhij,bhjv->bhiv', a_c, v_new)
        s = s * gt_c[..., None, None] + jnp.einsum('bhid,bhiv->bhdv', k_c, v_new)
        return s, o

    xs = (jnp.moveaxis(u, 2, 0), jnp.moveaxis(w, 2, 0), jnp.moveaxis(q_dec, 2, 0),
          jnp.moveaxis(attn, 2, 0), jnp.moveaxis(k_dec, 2, 0), jnp.moveaxis(g_tot, 2, 0))
    s_fin, o = lax.scan(step, s0, xs)
    return jnp.transpose(o, (1, 0, 3, 2, 4)).reshape(b, L, H, dv), s_fin


def gdn_prep(qkv, a, b_raw, conv_w, a_log, dt_bias):
    bsz, L = qkv.shape[:2]
    qkv = jax.nn.silu(dw_conv_centred(qkv, conv_w)).astype(jnp.float32)
    q, k, v = split_cols(qkv, (GDN_HEADS * GDN_DK, GDN_HEADS * GDN_DK, GDN_HEADS * GDN_DV))
    q = l2norm(q.reshape(bsz, L, GDN_HEADS, GDN_DK)) * GDN_DK ** -0.5
    k = l2norm(k.reshape(bsz, L, GDN_HEADS, GDN_DK))
    v = v.reshape(bsz, L, GDN_HEADS, GDN_DV)
    g = -jnp.exp(a_log.astype(jnp.float32)) * jax.nn.softplus(a.astype(jnp.float32).reshape(bsz, L, 2, GDN_HEADS) + dt_bias.astype(jnp.float32))
    beta = jax.nn.sigmoid(b_raw.astype(jnp.float32).reshape(bsz, L, 2, GDN_HEADS))
    return q, k, v, g, beta


def gdn_out(o, z, norm_w):
    bsz, L = o.shape[:2]
    o = rms_norm(o, norm_w) * jax.nn.silu(z.astype(jnp.float32)).reshape(bsz, L, GDN_HEADS, GDN_DV)
    return o.reshape(bsz, L, GDN_VW).astype(z.dtype)


def gdn_mixer(qkv_x, z_x, a_x, b_x, qkv_c, z_c, a_c, b_c, conv_w, a_log, dt_bias, norm_w, with_ctx):
    qx, kx, vx, gx, bx = gdn_prep(qkv_x, a_x, b_x, conv_w, a_log, dt_bias)
    qc, kc, vc, gcx, bcx = gdn_prep(qkv_c, a_c, b_c, conv_w, a_log, dt_bias)
    s0 = jnp.zeros((qx.shape[0], GDN_HEADS, GDN_DK, GDN_DV), jnp.float32)
    ox, oc = 0.0, 0.0
    for d in range(2):
        rev = d == 1
        o_cd, s_d = gated_delta_chunked(flip_seq(qc, rev), flip_seq(kc, rev), flip_seq(vc, rev),
                                        flip_seq(gcx[:, :, d], rev), flip_seq(bcx[:, :, d], rev), s0)
        o_xd, _ = gated_delta_chunked(flip_seq(qx, rev), flip_seq(kx, rev), flip_seq(vx, rev),
                                      flip_seq(gx[:, :, d], rev), flip_seq(bx[:, :, d], rev), s_d)
        ox = ox + flip_seq(o_xd, rev)
        oc = oc + flip_seq(o_cd, rev)
    out_c = gdn_out(oc, z_c, norm_w) if with_ctx else None
    return gdn_out(ox, z_x, norm_w), out_c


def axial_rope_angles(rows):
    row = jnp.repeat(jnp.arange(rows, dtype=jnp.float32), GRID_W)
    col = jnp.tile(jnp.arange(GRID_W, dtype=jnp.float32), rows)
    inv = ROPE_BASE ** (-jnp.arange(ROPE_PAIRS_PER_AXIS, dtype=jnp.float32) / ROPE_PAIRS_PER_AXIS)
    ang = jnp.concatenate([row[:, None] * inv, col[:, None] * inv], axis=-1)
    return jnp.cos(ang), jnp.sin(ang)


def apply_rope(t, cos, sin):
    tp = t.reshape(t.shape[:-1] + (DIFF_DQK // 2, 2))
    t0, t1 = tp[..., 0], tp[..., 1]
    c = cos[:, None, None, :].astype(t.dtype)
    s = sin[:, None, None, :].astype(t.dtype)
    return jnp.stack([t0 * c - t1 * s, t0 * s + t1 * c], axis=-1).reshape(t.shape)


def diff_attend(q, k, v, lam_full):
    s = jnp.einsum('bqhjd,bkhjd->bhjqk', q, k).astype(jnp.float32) * DIFF_DQK ** -0.5
    p = jax.nn.softmax(s, axis=-1)
    pd = (p[:, :, 0] - lam_full * p[:, :, 1]).astype(v.dtype)
    return jnp.einsum('bhqk,bkhv->bqhv', pd, v)


def diff_out(o, norm_w, lam_init):
    return (rms_norm(o, norm_w) * (1.0 - lam_init)).reshape(o.shape[0], o.shape[1], DIFF_VW)


def diff_mixer(q_x, k_x, v_x, q_c, k_c, v_c, lam, norm_w, lam_init, cos, sin, with_ctx):
    bsz, L = q_x.shape[:2]

    def heads(q, k, v):
        n = q.shape[1]
        return (q.reshape(bsz, n, DIFF_HEADS, 2, DIFF_DQK), k.reshape(bsz, n, DIFF_HEADS, 2, DIFF_DQK),
                v.reshape(bsz, n, DIFF_HEADS, DIFF_DV))

    qx, kx, vx = heads(q_x, k_x, v_x)
    qc, kc, vc = heads(q_c, k_c, v_c)
    qx = apply_rope(qx, cos, sin)
    kx = apply_rope(kx, cos, sin)
    lam = lam.astype(jnp.float32)
    lam_full = jnp.exp(jnp.sum(lam[0] * lam[1])) - jnp.exp(jnp.sum(lam[2] * lam[3])) + lam_init
    k_all = jnp.concatenate([kx, kc], axis=1)
    v_all = jnp.concatenate([vx, vc], axis=1)
    qblocks = jnp.moveaxis(qx.reshape(bsz, L // Q_BLOCK, Q_BLOCK, DIFF_HEADS, 2, DIFF_DQK), 1, 0)
    ox = lax.map(lambda qb: diff_attend(qb, k_all, v_all, lam_full), qblocks)
    ox = jnp.moveaxis(ox, 0, 1).reshape(bsz, L, DIFF_HEADS, DIFF_DV)
    out_c = diff_out(diff_attend(qc, kc, vc, lam_full), norm_w, lam_init) if with_ctx else None
    return diff_out(ox, norm_w, lam_init), out_c


def ssd_chunked(xdt, a, bm, cm, s0):
    b, L, H, P = xdt.shape
    N = bm.shape[-1]
    n = L // SSD_CHUNK
    x = xdt.reshape(b, n, SSD_CHUNK, H, P)
    bc = bm.reshape(b, n, SSD_CHUNK, H, N)
    cc = cm.reshape(b, n, SSD_CHUNK, H, N)
    a = jnp.moveaxis(a.reshape(b, n, SSD_CHUNK, H), 3, 1)
    a_cs = jnp.cumsum(a, axis=-1)
    scores = jnp.einsum('bclhn,bcshn->bhcls', cc, bc) * jnp.exp(segsum(a))
    y_diag = jnp.einsum('bhcls,bcshp->bclhp', scores, x)
    decay_states = jnp.moveaxis(jnp.exp(a_cs[..., -1:] - a_cs), 1, 3)[..., None]
    states = jnp.einsum('bclhn,bclhp->bchpn', bc * decay_states, x)
    states = jnp.concatenate([s0[:, None], states], axis=1)
    chunk_a = jnp.pad(a_cs[..., -1], ((0, 0), (0, 0), (1, 0)))
    new_states = jnp.einsum('bhzc,bchpn->bzhpn', jnp.exp(segsum(chunk_a)), states)
    prev_states, s_fin = new_states[:, :-1], new_states[:, -1]
    y_off = jnp.einsum('bclhn,bchpn->bclhp', cc * jnp.moveaxis(jnp.exp(a_cs), 1, 3)[..., None], prev_states)
    return (y_diag + y_off).reshape(b, L, H, P), s_fin


def ssd_prep(xbc, dt_raw, conv_w, conv_b, dt_bias):
    bsz, L = xbc.shape[:2]
    xbc = jax.nn.silu(dw_conv_centred(xbc, conv_w, conv_b)).astype(jnp.float32)
    xs, bm, cm = split_cols(xbc, (SSD_INNER, SSD_GROUPS * SSD_STATE, SSD_GROUPS * SSD_STATE))
    hpg = SSD_HEADS // SSD_GROUPS
    xs = xs.reshape(bsz, L, SSD_HEADS, SSD_HEADDIM)
    bm = jnp.repeat(bm.reshape(bsz, L, SSD_GROUPS, SSD_STATE), hpg, axis=2)
    cm = jnp.repeat(cm.reshape(bsz, L, SSD_GROUPS, SSD_STATE), hpg, axis=2)
    dt = jax.nn.softplus(dt_raw.astype(jnp.float32).reshape(bsz, L, 2, SSD_HEADS) + dt_bias.astype(jnp.float32))
    return xs, bm, cm, dt


def ssd_out(y, z, norm_w):
    bsz, L = y.shape[:2]
    y = y.reshape(bsz, L, SSD_INNER) * jax.nn.silu(z.astype(jnp.float32))
    return rms_norm(y, norm_w).astype(z.dtype)


def ssd_mixer(z_x, xbc_x, dt_x, z_c, xbc_c, dt_c, conv_w, conv_b, a_log, dt_bias, d_skip, norm_w, with_ctx):
    a = -jnp.exp(a_log.astype(jnp.float32))
    xs_x, bm_x, cm_x, dtx = ssd_prep(xbc_x, dt_x, conv_w, conv_b, dt_bias)
    xs_c, bm_c, cm_c, dtc = ssd_prep(xbc_c, dt_c, conv_w, conv_b, dt_bias)
    s0 = jnp.zeros((xs_x.shape[0], SSD_HEADS, SSD_HEADDIM, SSD_STATE), jnp.float32)
    dsk = d_skip.astype(jnp.float32)[:, None]
    yx, yc = dsk * xs_x, dsk * xs_c
    for d in range(2):
        rev = d == 1
        y_cd, s_d = ssd_chunked(flip_seq(xs_c * dtc[:, :, d, :, None], rev), flip_seq(dtc[:, :, d] * a[d], rev),
                                flip_seq(bm_c, rev), flip_seq(cm_c, rev), s0)
        y_xd, _ = ssd_chunked(flip_seq(xs_x * dtx[:, :, d, :, None], rev), flip_seq(dtx[:, :, d] * a[d], rev),
                              flip_seq(bm_x, rev), flip_seq(cm_x, rev), s_d)
        yx = yx + flip_seq(y_xd, rev)
        yc = yc + flip_seq(y_cd, rev)
    out_c = ssd_out(yc, z_c, norm_w) if with_ctx else None
    return ssd_out(yx, z_x, norm_w), out_c


def merge_branches(outs, gate_raw, w_branch, w_out):
    gates = jax.nn.sigmoid(gate_raw.astype(jnp.float32)).astype(gate_raw.dtype)
    y = gates[..., :D_MODEL] * (outs[0] @ w_branch[0])
    for n in range(1, N_BRANCH):
        y = y + gates[..., n * D_MODEL:(n + 1) * D_MODEL] * (outs[n] @ w_branch[n])
    return y @ w_out


def token_mixing(hx, hc, with_ctx, lam_init, cos, sin, w_in, gdn_conv_w, gdn_a_log, gdn_dt_bias, gdn_norm_w,
                 diff_lambda, diff_norm_w, ssd_conv_w, ssd_conv_b, ssd_a_log, ssd_dt_bias, ssd_d, ssd_norm_w,
                 w_branch, w_out):
    (gqkv_x, gz_x, ga_x, gb_x, dq_x, dk_x, dv_x, sz_x, sxbc_x, sdt_x, gate_x) = split_cols(hx @ w_in, SPLIT_SIZES)
    (gqkv_c, gz_c, ga_c, gb_c, dq_c, dk_c, dv_c, sz_c, sxbc_c, sdt_c, gate_c) = split_cols(hc @ w_in, SPLIT_SIZES)
    oa_x, oa_c = gdn_mixer(gqkv_x, gz_x, ga_x, gb_x, gqkv_c, gz_c, ga_c, gb_c,
                           gdn_conv_w, gdn_a_log, gdn_dt_bias, gdn_norm_w, with_ctx)
    ob_x, ob_c = diff_mixer(dq_x, dk_x, dv_x, dq_c, dk_c, dv_c, diff_lambda, diff_norm_w, lam_init, cos, sin, with_ctx)
    oc_x, oc_c = ssd_mixer(sz_x, sxbc_x, sdt_x, sz_c, sxbc_c, sdt_c, ssd_conv_w, ssd_conv_b,
                           ssd_a_log, ssd_dt_bias, ssd_d, ssd_norm_w, with_ctx)
    yx = merge_branches((oa_x, ob_x, oc_x), gate_x, w_branch, w_out)
    yc = merge_branches((oa_c, ob_c, oc_c), gate_c, w_branch, w_out) if with_ctx else None
    return yx, yc


def half_ffn(h, g_pre, g_post, shift, scale, gate, w1, w2):
    return 0.5 * gate * rms_norm(swiglu(modulate(rms_norm(h, g_pre), shift, scale), w1, w2), g_post)


def setup_inputs(seed: int = 0) -> dict:
    key = jax.random.key(seed)
    ks = jax.random.split(key, 24)
    f32 = jnp.float32

    def nrm(k, shape, scale):
        return jax.random.normal(k, shape, f32) * scale

    def dt_bias_init(k, shape):
        dt = jnp.exp(jax.random.uniform(k, shape, f32, math.log(1e-3), math.log(1e-1)))
        return dt + jnp.log(-jnp.expm1(-dt))

    return {
        'x': nrm(ks[0], (BATCH, SEQ, D_MODEL), 1.0),
        'c': nrm(ks[1], (BATCH, D_MODEL), 1.0),
        'ctx': nrm(ks[2], (BATCH, CTX_LEN, D_MODEL), 1.0),
        'c_ctx': nrm(ks[3], (D_MODEL,), 1.0),
        'w_ada': nrm(ks[4], (DEPTH, D_MODEL, N_MOD * D_MODEL), 0.5 * D_MODEL ** -0.5),
        'b_ada': nrm(ks[5], (DEPTH, N_MOD * D_MODEL), 0.02),
        'norm_g': 1.0 + nrm(ks[6], (DEPTH, 6, D_MODEL), 0.05),
        'w_ffn_in': nrm(ks[7], (DEPTH, 2, D_MODEL, 2 * D_FF), D_MODEL ** -0.5),
        'w_ffn_out': nrm(ks[8], (DEPTH, 2, D_FF, D_MODEL), D_FF ** -0.5),
        'w_in': nrm(ks[9], (DEPTH, D_MODEL, PROJ_W), D_MODEL ** -0.5),
        'gdn_conv_w': nrm(ks[10], (DEPTH, CONV_K, GDN_QKV), CONV_K ** -0.5),
        'gdn_a_log': jnp.log(jax.random.uniform(ks[11], (DEPTH, 2, GDN_HEADS), f32, 1.0, 16.0)),
        'gdn_dt_bias': dt_bias_init(ks[12], (DEPTH, 2, GDN_HEADS)),
        'gdn_norm_w': 1.0 + nrm(ks[13], (DEPTH, GDN_DV), 0.05),
        'diff_lambda': nrm(ks[14], (DEPTH, 4, DIFF_DQK), 0.1),
        'diff_norm_w': 1.0 + nrm(ks[15], (DEPTH, DIFF_DV), 0.05),
        'ssd_conv_w': nrm(ks[16], (DEPTH, CONV_K, SSD_XBC), CONV_K ** -0.5),
        'ssd_conv_b': nrm(ks[17], (DEPTH, SSD_XBC), 0.02),
        'ssd_a_log': jnp.log(jax.random.uniform(ks[18], (DEPTH, 2, SSD_HEADS), f32, 1.0, 16.0)),
        'ssd_dt_bias': dt_bias_init(ks[19], (DEPTH, 2, SSD_HEADS)),
        'ssd_d': 1.0 + nrm(ks[20], (DEPTH, SSD_HEADS), 0.1),
        'ssd_norm_w': 1.0 + nrm(ks[21], (DEPTH, SSD_INNER), 0.05),
        'w_branch': nrm(ks[22], (DEPTH, N_BRANCH, BRANCH_W, D_MODEL), BRANCH_W ** -0.5),
        'w_out': nrm(ks[23], (DEPTH, D_MODEL, D_MODEL), D_MODEL ** -0.5),
    }


def reference(x, c, ctx, c_ctx, w_ada, b_ada, norm_g, w_ffn_in, w_ffn_out, w_in,
              gdn_conv_w, gdn_a_log, gdn_dt_bias, gdn_norm_w, diff_lambda, diff_norm_w,
              ssd_conv_w, ssd_conv_b, ssd_a_log, ssd_dt_bias, ssd_d, ssd_norm_w, w_branch, w_out):
    bsz, n_lat = x.shape[:2]
    rows = n_lat // GRID_W
    cos, sin = axial_rope_angles(rows)
    sc = jax.nn.silu(c)
    scc = jax.nn.silu(c_ctx)
    for i in range(DEPTH):
        last = i == DEPTH - 1
        lam_init = 0.8 - 0.6 * math.exp(-0.3 * i)
        mx = (sc @ w_ada[i] + b_ada[i]).reshape(bsz, N_MOD, 1, D_MODEL)
        mc = (scc @ w_ada[i] + b_ada[i]).reshape(N_MOD, D_MODEL)
        ng = norm_g[i]
        x = x + half_ffn(x, ng[0], ng[1], mx[:, 0], mx[:, 1], mx[:, 2], w_ffn_in[i, 0], w_ffn_out[i, 0])
        ctx = ctx + half_ffn(ctx, ng[0], ng[1], mc[0], mc[1], mc[2], w_ffn_in[i, 0], w_ffn_out[i, 0])
        hx = modulate(rms_norm(x, ng[2]), mx[:, 3], mx[:, 4])
        hc = modulate(rms_norm(ctx, ng[2]), mc[3], mc[4])
        yx, yc = token_mixing(hx, hc, not last, lam_init, cos, sin, w_in[i], gdn_conv_w[i], gdn_a_log[i],
                              gdn_dt_bias[i], gdn_norm_w[i], diff_lambda[i], diff_norm_w[i], ssd_conv_w[i],
                              ssd_conv_b[i], ssd_a_log[i], ssd_dt_bias[i], ssd_d[i], ssd_norm_w[i],
                              w_branch[i], w_out[i])
        x = x + mx[:, 5] * rms_norm(yx, ng[3])
        x = x + half_ffn(x, ng[4], ng[5], mx[:, 6], mx[:, 7], mx[:, 8], w_ffn_in[i, 1], w_ffn_out[i, 1])
        if not last:
            ctx = ctx + mc[5] * rms_norm(yc, ng[3])
            ctx = ctx + half_ffn(ctx, ng[4], ng[5], mc[6], mc[7], mc[8], w_ffn_in[i, 1], w_ffn_out[i, 1])
    return x
```

```python
import numpy as np
from contextlib import ExitStack, contextmanager
import concourse.bass as bass
import concourse.mybir as mybir

F32 = mybir.dt.float32
BF16 = mybir.dt.bfloat16
AF = mybir.ActivationFunctionType
ALU = mybir.AluOpType
AX = mybir.AxisListType


class Buf:
    __slots__ = ("name", "t", "w", "r", "dsem", "dcnt", "space", "dkey")

    def __init__(self, name, t, space="sbuf"):
        self.name = name
        self.space = space
        self.t = t
        self.w = {}
        self.r = {}
        self.dsem = None
        self.dcnt = 0
        self.dkey = None

    def __getitem__(self, idx):
        return View(self, self.t[idx])


class View:
    __slots__ = ("buf", "ap")

    def __init__(self, buf, ap):
        self.buf = buf
        self.ap = ap

    def __getitem__(self, idx):
        return View(self.buf, self.ap[idx])

    def rearrange(self, *a, **k):
        return View(self.buf, self.ap.rearrange(*a, **k))

    def bitcast(self, *a, **k):
        return View(self.buf, self.ap.bitcast(*a, **k))

    def to_broadcast(self, *a, **k):
        return View(self.buf, self.ap.to_broadcast(*a, **k))


class Sched:
    def __init__(self, nc, stack):
        self.nc = nc
        self.stack = stack
        self.eng = {"pe": nc.tensor, "act": nc.scalar, "dve": nc.vector, "pool": nc.gpsimd, "sp": nc.sync}
        self.sem = {}
        self.cnt = {}
        self.seen = {}
        for e in self.eng:
            self.sem[e] = stack.enter_context(nc.semaphore("prog_" + e))
            self.cnt[e] = 0
            self.seen[e] = {}
        self.semobj = {e: self.sem[e] for e in self.eng}
        self.nbuf = 0
        self.ninst = 0
        self.root = stack
        self.dbufs = []
        self.sempool = []
        self.scope_bufs = [[]]
        self.nsem = 0

    def sbuf(self, name, shape, dt):
        self.nbuf += 1
        name = "%s_%d" % (name, self.nbuf)
        t = self.stack.enter_context(self.nc.sbuf_tensor(name, list(shape), dt))
        b = Buf(name, t)
        self.scope_bufs[-1].append(b)
        return b

    def psum(self, name, shape, dt=F32):
        self.nbuf += 1
        name = "%s_%d" % (name, self.nbuf)
        t = self.stack.enter_context(self.nc.psum_tensor(name, list(shape), dt))
        return Buf(name, t, "psum")

    def dram(self, name, shape, dt, kind="Internal"):
        t = self.nc.dram_tensor(name, list(shape), dt, kind=kind)
        return Buf(name, t.ap(), "dram")

    def sub(self, name, ap, space="dram"):
        return Buf(name, ap, space)

    def barrier(self):
        deps = {e: self.cnt[e] for e in self.eng if self.cnt[e] > 0}
        for b in self.dbufs:
            if deps.get(b.dkey, 0) < b.dcnt:
                deps[b.dkey] = b.dcnt
        for e in self.eng:
            self._need(e, dict(deps))

    @contextmanager
    def scope(self):
        old = self.stack
        self.scope_bufs.append([])
        with ExitStack() as st:
            self.stack = st
            yield
            self.barrier()
        self.stack = old
        dead = self.scope_bufs.pop()
        for b in dead:
            if b.dsem is not None:
                self.sempool.append((b.dsem, b.dcnt, b.dkey))
        deadids = set(id(b) for b in dead)
        self.dbufs = [b for b in self.dbufs if id(b) not in deadids] + [b for b in dead if b.dsem is not None][:0]
        self._dead_keep = getattr(self, "_dead_keep", []) + dead

    def _need(self, e, deps):
        seen = self.seen[e]
        for k, v in deps.items():
            if e == "pe" and k == "pe":
                continue
            if seen.get(k, 0) < v:
                seen[k] = v
                self.eng[e].wait_ge(self.semobj[k], v)

    def _collect(self, reads, writes):
        deps = {}
        for v in reads:
            for k, val in v.buf.w.items():
                if deps.get(k, 0) < val:
                    deps[k] = val
        for v in writes:
            for d in (v.buf.w, v.buf.r):
                for k, val in d.items():
                    if deps.get(k, 0) < val:
                        deps[k] = val
        return deps

    def _mark(self, reads, writes, key, val):
        for v in reads:
            b = v.buf
            if b.r.get(key, 0) < val:
                b.r[key] = val
        for v in writes:
            b = v.buf
            b.w = {key: val}
            b.r = {}

    def op(self, e, fn, reads, writes):
        self._need(e, self._collect(reads, writes))
        ins = fn()
        self.cnt[e] += 1
        ins.then_inc(self.sem[e], 1)
        self._mark(reads, writes, e, self.cnt[e])
        self.ninst += 1
        return ins

    def dma(self, e, out, in_, sbuf_side=None, **kw):
        if sbuf_side is None:
            sbuf_side = out.buf if out.buf.space != "dram" else in_.buf
        b = sbuf_side
        if b.dsem is None:
            if self.sempool:
                b.dsem, b.dcnt, b.dkey = self.sempool.pop()
            else:
                self.nsem += 1
                b.dsem = self.root.enter_context(self.nc.semaphore("d_%d" % self.nsem))
                b.dkey = "d%d" % self.nsem
                self.semobj[b.dkey] = b.dsem
            self.dbufs.append(b)
        self._need(e, self._collect([in_], [out]))
        ins = self.eng[e].dma_start(out=out.ap, in_=in_.ap, **kw)
        b.dcnt += 16
        ins.then_inc(b.dsem, 16)
        self._mark([in_], [out], b.dkey, b.dcnt)
        self.ninst += 1
        return ins

    def wait_all(self, e, bufs):
        deps = {}
        for b in bufs:
            for d in (b.w, b.r):
                for k, val in d.items():
                    if deps.get(k, 0) < val:
                        deps[k] = val
        self._need(e, deps)

    def matmul(self, out, lhsT, rhs, start=True, stop=True, acc_reads=True):
        rd = [lhsT, rhs]
        return self.op("pe", lambda: self.nc.tensor.matmul(out.ap, lhsT.ap, rhs.ap, start=start, stop=stop),
                       rd, [out])

    def transpose(self, out, in_, ident):
        return self.op("pe", lambda: self.nc.tensor.transpose(out.ap, in_.ap, ident.ap), [in_, ident], [out])

    def act(self, out, in_, func, bias=None, scale=None, accum_out=None, e="act"):
        rd = [in_]
        kw = {}
        if bias is not None:
            if isinstance(bias, View):
                rd.append(bias)
                kw["bias"] = bias.ap
            else:
                kw["bias"] = bias
        if scale is not None:
            if isinstance(scale, View):
                rd.append(scale)
                kw["scale"] = scale.ap
            else:
                kw["scale"] = scale
        wr = [out]
        if accum_out is not None:
            wr.append(accum_out)
            kw["accum_out"] = accum_out.ap
        return self.op("act", lambda: self.nc.scalar.activation(out.ap, in_.ap, func, **kw), rd, wr)

    def _ve(self, e):
        return self.nc.vector if e == "dve" else self.nc.gpsimd

    def copy(self, out, in_, e="dve"):
        if e == "act":
            return self.op("act", lambda: self.nc.scalar.copy(out.ap, in_.ap), [in_], [out])
        return self.op(e, lambda: self._ve(e).tensor_copy(out.ap, in_.ap), [in_], [out])

    def tt(self, out, a, b, op, e="dve"):
        return self.op(e, lambda: self._ve(e).tensor_tensor(out.ap, a.ap, b.ap, op), [a, b], [out])

    def ts(self, out, a, s1, op0, s2=None, op1=None, accum_out=None, e="dve"):
        rd = [a]
        s1v = s1.ap if isinstance(s1, View) else s1
        s2v = s2.ap if isinstance(s2, View) else s2
        if isinstance(s1, View):
            rd.append(s1)
        if isinstance(s2, View):
            rd.append(s2)
        wr = [out]
        kw = {}
        if op1 is not None:
            kw["op1"] = op1
        if accum_out is not None:
            kw["accum_out"] = accum_out.ap
            wr.append(accum_out)
        if s2 is None and op1 is None and accum_out is None:
            return self.op(e, lambda: self._ve(e).tensor_single_scalar(out.ap, a.ap, s1v, op0), rd, wr)
        return self.op(e, lambda: self._ve(e).tensor_scalar(out.ap, a.ap, s1v, s2v, op0, **kw), rd, wr)

    def stt(self, out, a, s, b, op0, op1, e="dve"):
        rd = [a, b]
        sv = s.ap if isinstance(s, View) else s
        if isinstance(s, View):
            rd.append(s)
        return self.op(e, lambda: self._ve(e).scalar_tensor_tensor(out.ap, a.ap, sv, b.ap, op0, op1), rd, [out])

    def reduce(self, out, in_, op, axis=AX.X, e="dve"):
        return self.op(e, lambda: self._ve(e).tensor_reduce(out.ap, in_.ap, axis, op), [in_], [out])

    def memset(self, out, val, e="dve"):
        return self.op(e, lambda: self._ve(e).memset(out.ap, val), [], [out])

    def recip(self, out, in_):
        return self.op("dve", lambda: self.nc.vector.reciprocal(out.ap, in_.ap), [in_], [out])


P = 128
D = 1024
KC = 8
DFF = 2816
FC = 22
EPS = 1e-6


class NS:
    pass


def mk_consts(S, nc):
    C = NS()
    C.ones = S.sbuf("ones", [P, P], F32)
    S.memset(C.ones[:], 1.0)
    C.eps = S.sbuf("epsc", [P, 1], F32)
    S.memset(C.eps[:], EPS)
    C.ident = S.sbuf("ident", [P, P], F32)
    S.memset(C.ident[:], 1.0, e="pool")
    S.op("pool", lambda: nc.gpsimd.affine_select(C.ident.t[:], C.ident.t[:], [[-1, P]], ALU.is_equal, 0.0,
                                                 base=0, channel_multiplier=1), [C.ident[:]], [C.ident[:]])
    return C


def load_w(S, wd, dst, K, N, stages, blk=2048, col0=0):
    engs = ["pool", "dve", "act"]
    i = 0
    for k in range(K // P):
        for c0 in range(0, N, blk):
            w = min(blk, N - c0)
            st = stages[i % len(stages)]
            S.dma("sp", st[:, :w], wd[k * P:(k + 1) * P, col0 + c0:col0 + c0 + w])
            S.copy(dst[:, k, c0:c0 + w], st[:, :w], e=engs[i % 3])
            i += 1


def rms_rstd(S, C, src, n, nch, dim, ps, out):
    for c in range(nch):
        sq = C.sq[c % 2]
        S.act(sq[:, :n], src(c), AF.Square)
        S.matmul(ps[:, :n], C.ones[:], sq[:, :n], start=(c == 0), stop=(c == nch - 1))
    S.act(C.lnt[:, :n], ps[:, :n], AF.Ln, scale=1.0 / dim, bias=C.eps[:, 0:1])
    S.act(out, C.lnt[:, :n], AF.Exp, scale=-0.5)


def norm_mod(S, C, xt, n, A, B, col, h, ps):
    rms_rstd(S, C, lambda c: xt[:, c, :n], n, KC, D, ps, C.rstd[:, :n])
    for c in range(KC):
        t = C.tmp[c % 2]
        S.tt(t[:, :n], xt[:, c, :n], C.rstd[:, :n], ALU.mult)
        S.act(h[:, c, :n], t[:, :n], AF.Identity, scale=A[:, c, col:col + 1], bias=B[:, c, col:col + 1])


def compute_mods(S, C, cv, wada, bada, nmod, stg, psm, mods):
    scv = S.sbuf("scv", [P, KC, 2], F32)
    cvt = S.sbuf("cvt", [P, KC, 2], F32)
    S.dma("sp", cvt[:], cv[:])
    S.act(scv[:], cvt[:], AF.Silu)
    nn = nmod * KC
    for k in range(KC):
        S.dma("sp" if k % 2 else "pool", stg[k][:, :nn * P], wada[k * P:(k + 1) * P, :])
    for j in range(nn):
        for k in range(KC):
            S.matmul(psm[:, 2 * j:2 * j + 2], stg[k][:, j * P:(j + 1) * P], scv[:, k, :], start=(k == 0), stop=(k == KC - 1))
    bt = S.sbuf("badat", [P, nn], F32)
    S.dma("sp", bt[:], bada[:])
    S.tt(mods[:], psm[:, 0:2 * nn].rearrange("p (j t) -> p j t", t=2),
         View(bt, bt.t[:].rearrange("p (j o) -> p j o", o=1).to_broadcast([P, nn, 2])), ALU.add)


def ffn_sweep(S, C, tiles, x_in, x_out, w1b, w2b, A, B, G, PS):
    for j, (s0, n, col) in enumerate(tiles):
        xt = C.xt[j % 2]
        S.dma("sp", xt[:, :, :n], x_in(j))
        norm_mod(S, C, xt, n, A, B, col, C.h, PS.ss)
        for f in range(FC):
            pg = PS.g[f % 2]
            pu = PS.u[f % 2]
            for k in range(KC):
                S.matmul(pg[:, :n], w1b[:, k, f * P:(f + 1) * P], C.h[:, k, :n], start=(k == 0), stop=(k == KC - 1))
            for k in range(KC):
                S.matmul(pu[:, :n], w1b[:, k, DFF + f * P:DFF + (f + 1) * P], C.h[:, k, :n], start=(k == 0), stop=(k == KC - 1))
            sg = C.sg[f % 2]
            S.act(sg[:, :n], pg[:, :n], AF.Silu)
            S.tt(C.aT[:, f, :n], sg[:, :n], pu[:, :n], ALU.mult)
        for d in range(KC):
            py = PS.y[d % 2]
            for f in range(FC):
                S.matmul(py[:, :n], w2b[:, f, d * P:(d + 1) * P], C.aT[:, f, :n], start=(f == 0), stop=(f == FC - 1))
            S.copy(C.y[:, d, :n], py[:, :n], e="dve")
        rms_rstd(S, C, lambda c: C.y[:, c, :n], n, KC, D, PS.ss2, C.rstd2[:, :n])
        for c in range(KC):
            t = C.tmp[c % 2]
            S.tt(t[:, :n], C.y[:, c, :n], C.rstd2[:, :n], ALU.mult)
            S.stt(xt[:, c, :n], t[:, :n], G[:, c, col:col + 1], xt[:, c, :n], ALU.mult, ALU.add)
        S.dma("pool", x_out(j), xt[:, :, :n])


def alloc_ffn_work(S, C):
    C.xt = [S.sbuf("xt0", [P, KC, 512], F32)] * 2
    C.h = S.sbuf("h", [P, KC, 512], BF16)
    C.aT = S.sbuf("aT", [P, FC, 512], BF16)
    C.y = S.sbuf("y", [P, KC, 512], F32)
    C.sg = C.tmp


def alloc_small(S, C):
    C.sq = [S.sbuf("sq%d" % i, [P, 512], F32) for i in range(2)]
    C.tmp = [S.sbuf("tmp%d" % i, [P, 512], F32) for i in range(2)]
    C.lnt = C.sq[0]
    C.rstd = S.sbuf("rstd", [P, 512], F32)
    C.rstd2 = C.rstd


def mk_tiles(NT, NCX):
    tiles = [(j * 512, 512, 0) for j in range(NT // 512)]
    if NCX:
        tiles.append((NT, NCX, 1))
    return tiles


DEBUG = False
NPC = 49


def build_R1(NT, NCX):
    TT = NT + NCX
    nc = bass.Bass("TRN2", target_bir_lowering=False)
    with ExitStack() as st:
        S = Sched(nc, st)
        xT = S.dram("xT", [P, KC, TT], F32, kind="ExternalInput")
        cv = S.dram("cv", [P, KC, 2], F32, kind="ExternalInput")
        wada = S.dram("wada", [D, 5 * D], F32, kind="ExternalInput")
        bada = S.dram("bada", [P, 40], F32, kind="ExternalInput")
        ng = S.dram("ng", [P, 48], F32, kind="ExternalInput")
        w1 = S.dram("w1", [D, 2 * DFF], F32, kind="ExternalInput")
        w2 = S.dram("w2", [DFF, D], F32, kind="ExternalInput")
        win = S.dram("win", [D, NPC * P], F32, kind="ExternalInput")
        x1T = S.dram("x1T", [P, KC, TT], F32, kind="ExternalOutput")
        PT = S.dram("PT", [NPC, P, TT], F32, kind="ExternalOutput")
        tiles = mk_tiles(NT, NCX)
        C = mk_consts(S, nc)
        alloc_small(S, C)
        PS = NS()
        PS.ss = S.psum("ps_ss", [P, 512])
        PS.ss2 = S.psum("ps_ss2", [P, 512])
        PS.g = [S.psum("ps_g%d" % i, [P, 512]) for i in range(2)]
        PS.u = [S.psum("ps_u%d" % i, [P, 512]) for i in range(2)]
        PS.y = [S.psum("ps_y%d" % i, [P, 512]) for i in range(2)]
        mods = S.sbuf("mods", [P, 40, 2], F32)
        ngt = S.sbuf("ngt", [P, 6, KC], F32)
        S.dma("sp", ngt[:], ng[:].rearrange("p (m c) -> p m c", c=KC))
        A1 = S.sbuf("A1", [P, KC, 2], F32)
        G1 = S.sbuf("G1", [P, KC, 2], F32)
        A2 = S.sbuf("A2", [P, KC, 2], F32)
        with S.scope():
            stg = [S.sbuf("stgm%d" % i, [P, 5 * D], F32) for i in range(KC)]
            compute_mods(S, C, cv, wada, bada, 5, stg, PS.g[0], mods)

        def bc(v):
            return View(v.buf, v.ap.rearrange("p (c o) -> p c o", o=1).to_broadcast([P, KC, 2]))
        S.stt(A1[:], mods[:, 8:16, :], 1.0, bc(ngt[:, 0, :]), ALU.add, ALU.mult)
        S.stt(G1[:], mods[:, 16:24, :], 0.5, bc(ngt[:, 1, :]), ALU.mult, ALU.mult)
        S.stt(A2[:], mods[:, 32:40, :], 1.0, bc(ngt[:, 2, :]), ALU.add, ALU.mult)
        B1 = mods[:, 0:8, :]
        B2 = mods[:, 24:32, :]
        if DEBUG:
            dbg = S.dram("dbg_mods", [P, 80], F32, kind="ExternalOutput")
            S.dma("sp", dbg[:], mods[:].rearrange("p j t -> p (j t)"))
        x1tiles = [S.sub("x1t%d" % j, x1T.t[:, :, s0:s0 + n]) for j, (s0, n, col) in enumerate(tiles)]
        with S.scope():
            w1b = S.sbuf("w1b", [P, KC, 2 * DFF], BF16)
            w2b = S.sbuf("w2b", [P, FC, D], BF16)
            with S.scope():
                stages = [S.sbuf("wst%d" % i, [P, 2048], F32) for i in range(3)]
                load_w(S, w1, w1b, D, 2 * DFF, stages)
                load_w(S, w2, w2b, DFF, D, stages)
            alloc_ffn_work(S, C)
            ffn_sweep(S, C, tiles, lambda j: xT[:, :, tiles[j][0]:tiles[j][0] + tiles[j][1]],
                      lambda j: x1tiles[j][:], w1b, w2b, A1[:], B1, G1[:], PS)
        with S.scope():
            winb = S.sbuf("winb", [P, KC, NPC * P], BF16)
            with S.scope():
                stages = [S.sbuf("wst%d" % i, [P, 2048], F32) for i in range(3)]
                load_w(S, win, winb, D, NPC * P, stages)
            xt2 = [S.sbuf("xq%d" % i, [P, KC, 512], F32) for i in range(2)]
            h = S.sbuf("h2", [P, KC, 512], BF16)
            ost = [S.sbuf("ost%d" % i, [P, 4, 512], F32) for i in range(3)]
            pps = PS.g + PS.u + PS.y
            gi = 0
            for j, (s0, n, col) in enumerate(tiles):
                xt = xt2[j % 2]
                S.dma("sp", xt[:, :, :n], x1tiles[j][:])
                norm_mod(S, C, xt, n, A2[:], B2, col, h, PS.ss)
                for c0 in range(0, NPC, 4):
                    nn = min(4, NPC - c0)
                    o = ost[gi % 3]
                    gi += 1
                    for cc in range(nn):
                        pp = pps[(c0 + cc) % 6]
                        for k in range(KC):
                            S.matmul(pp[:, :n], winb[:, k, (c0 + cc) * P:(c0 + cc + 1) * P], h[:, k, :n],
                                     start=(k == 0), stop=(k == KC - 1))
                        S.copy(o[:, cc, :n], pp[:, :n], e=("act" if cc % 2 else "dve"))
                    S.dma("pool", S.sub("pt", PT.t[c0:c0 + nn, :, s0:s0 + n].rearrange("c p t -> p c t"))[:], o[:, :nn, :n])
            S.wait_all("sp", ost + xt2)
        S.barrier()
    return nc


def build_R2(NT, NCX):
    TT = NT + NCX
    nc = bass.Bass("TRN2", target_bir_lowering=False)
    with ExitStack() as st:
        S = Sched(nc, st)
        x1T = S.dram("x1T", [P, KC, TT], F32, kind="ExternalInput")
        oin = [S.dram(nm, [P, 4, TT], F32, kind="ExternalInput") for nm in ("oaT", "obT", "ocT")]
        cv = S.dram("cv", [P, KC, 2], F32, kind="ExternalInput")
        wada = S.dram("wada", [D, 6 * D], F32, kind="ExternalInput")
        bada = S.dram("bada", [P, 48], F32, kind="ExternalInput")
        ng = S.dram("ng", [P, 48], F32, kind="ExternalInput")
        snw = S.dram("snw", [P, 4], F32, kind="ExternalInput")
        wg = S.dram("wg", [D, 3 * D], F32, kind="ExternalInput")
        wb = S.dram("wb", [1536, D], F32, kind="ExternalInput")
        wo = S.dram("wo", [D, D], F32, kind="ExternalInput")
        w1 = S.dram("w1", [D, 2 * DFF], F32, kind="ExternalInput")
        w2 = S.dram("w2", [DFF, D], F32, kind="ExternalInput")
        x3T = S.dram("x3T", [P, KC, TT], F32, kind="ExternalOutput")
        x2T = S.dram("x2T", [P, KC, TT], F32, kind="Internal")
        tiles = mk_tiles(NT, NCX)
        C = mk_consts(S, nc)
        alloc_small(S, C)
        PS = NS()
        PS.ss = S.psum("ps_ss", [P, 512])
        PS.ss2 = S.psum("ps_ss2", [P, 512])
        PS.g = [S.psum("ps_g%d" % i, [P, 512]) for i in range(2)]
        PS.u = [S.psum("ps_u%d" % i, [P, 512]) for i in range(2)]
        PS.y = [S.psum("ps_y%d" % i, [P, 512]) for i in range(2)]
        mods = S.sbuf("mods", [P, 48, 2], F32)
        ngt = S.sbuf("ngt", [P, 6, KC], F32)
        S.dma("sp", ngt[:], ng[:].rearrange("p (m c) -> p m c", c=KC))
        snt = S.sbuf("snt", [P, 4], F32)
        S.dma("sp", snt[:], snw[:])
        with S.scope():
            stg = [S.sbuf("stgm%d" % i, [P, 6 * D], F32) for i in range(KC)]
            compute_mods(S, C, cv, wada, bada, 6, stg, PS.g[0], mods)

        def bc(v):
            return View(v.buf, v.ap.rearrange("p (c o) -> p c o", o=1).to_broadcast([P, KC, 2]))
        A2 = S.sbuf("A2", [P, KC, 2], F32)
        G3 = S.sbuf("G3", [P, KC, 2], F32)
        A4 = S.sbuf("A4", [P, KC, 2], F32)
        G5 = S.sbuf("G5", [P, KC, 2], F32)
        S.stt(A2[:], mods[:, 8:16, :], 1.0, bc(ngt[:, 2, :]), ALU.add, ALU.mult)
        S.tt(G3[:], mods[:, 16:24, :], bc(ngt[:, 3, :]), ALU.mult)
        S.stt(A4[:], mods[:, 32:40, :], 1.0, bc(ngt[:, 4, :]), ALU.add, ALU.mult)
        S.stt(G5[:], mods[:, 40:48, :], 0.5, bc(ngt[:, 5, :]), ALU.mult, ALU.mult)
        B2 = mods[:, 0:8, :]
        B4 = mods[:, 24:32, :]
        x2tiles = [S.sub("x2t%d" % j, x2T.t[:, :, s0:s0 + n]) for j, (s0, n, col) in enumerate(tiles)]
        with S.scope():
            wgb = S.sbuf("wgb", [P, KC, 3 * D], BF16)
            wbb = S.sbuf("wbb", [P, 12, D], BF16)
            wob = S.sbuf("wob", [P, KC, D], BF16)
            with S.scope():
                stages = [S.sbuf("wst%d" % i, [P, 2048], F32) for i in range(3)]
                load_w(S, wg, wgb, D, 3 * D, stages)
                load_w(S, wb, wbb, 1536, D, stages)
                load_w(S, wo, wob, D, D, stages)
            xt = S.sbuf("xm", [P, KC, 512], F32)
            h = S.sbuf("hm", [P, KC, 512], BF16)
            ost = S.sbuf("ostg", [P, 4, 512], F32)
            ob16 = S.sbuf("ob16", [P, 12, 512], BF16)
            yacc = S.sbuf("yacc", [P, 512], F32)
            ybf = S.sbuf("ybf", [P, KC, 512], BF16)
            yy = S.sbuf("yy", [P, KC, 512], F32)
            gt = [S.sbuf("gt%d" % i, [P, 512], F32) for i in range(2)]
            for j, (s0, n, col) in enumerate(tiles):
                S.dma("sp", xt[:, :, :n], x1T[:, :, s0:s0 + n])
                norm_mod(S, C, xt, n, A2[:], B2, col, h, PS.ss)
                for br in range(3):
                    S.dma("sp", ost[:, :, :n], oin[br][:, :, s0:s0 + n])
                    if br < 2:
                        S.copy(ob16[:, br * 4:(br + 1) * 4, :n], ost[:, :, :n], e="pool")
                    else:
                        rms_rstd(S, C, lambda c: ost[:, c, :n], n, 4, 512, PS.ss2, C.rstd2[:, :n])
                        for c in range(4):
                            t = C.tmp[c % 2]
                            S.tt(t[:, :n], ost[:, c, :n], C.rstd2[:, :n], ALU.mult)
                            S.act(ob16[:, 8 + c, :n], t[:, :n], AF.Copy, scale=snt[:, c:c + 1])
                for d in range(KC):
                    for br in range(3):
                        pg = PS.g[br % 2]
                        pu = PS.u[br % 2]
                        cg = br * KC + d
                        for k in range(KC):
                            S.matmul(pg[:, :n], wgb[:, k, cg * P:(cg + 1) * P], h[:, k, :n], start=(k == 0), stop=(k == KC - 1))
                        for k in range(4):
                            S.matmul(pu[:, :n], wbb[:, br * 4 + k, d * P:(d + 1) * P], ob16[:, br * 4 + k, :n], start=(k == 0), stop=(k == 3))
                        g = gt[br % 2]
                        S.act(g[:, :n], pg[:, :n], AF.Sigmoid)
                        if br == 0:
                            S.tt(yacc[:, :n], g[:, :n], pu[:, :n], ALU.mult)
                        else:
                            t = C.tmp[br % 2]
                            S.tt(t[:, :n], g[:, :n], pu[:, :n], ALU.mult)
                            if br == 1:
                                S.tt(yacc[:, :n], yacc[:, :n], t[:, :n], ALU.add)
                            else:
                                S.tt(ybf[:, d, :n], yacc[:, :n], t[:, :n], ALU.add)
                for d in range(KC):
                    py = PS.y[d % 2]
                    for k in range(KC):
                        S.matmul(py[:, :n], wob[:, k, d * P:(d + 1) * P], ybf[:, k, :n], start=(k == 0), stop=(k == KC - 1))
                    S.copy(yy[:, d, :n], py[:, :n], e="dve")
                rms_rstd(S, C, lambda c: yy[:, c, :n], n, KC, D, PS.ss2, C.rstd2[:, :n])
                for c in range(KC):
                    t = C.tmp[c % 2]
                    S.tt(t[:, :n], yy[:, c, :n], C.rstd2[:, :n], ALU.mult)
                    S.stt(xt[:, c, :n], t[:, :n], G3[:, c, col:col + 1], xt[:, c, :n], ALU.mult, ALU.add)
                S.dma("pool", x2tiles[j][:], xt[:, :, :n])
        with S.scope():
            w1b = S.sbuf("w1b", [P, KC, 2 * DFF], BF16)
            w2b = S.sbuf("w2b", [P, FC, D], BF16)
            with S.scope():
                stages = [S.sbuf("wst%d" % i, [P, 2048], F32) for i in range(3)]
                load_w(S, w1, w1b, D, 2 * DFF, stages)
                load_w(S, w2, w2b, DFF, D, stages)
            alloc_ffn_work(S, C)
            ffn_sweep(S, C, tiles, lambda j: x2tiles[j][:],
                      lambda j: S.sub("x3", x3T.t[:, :, tiles[j][0]:tiles[j][0] + tiles[j][1]])[:], w1b, w2b, A4[:], B4, G5[:], PS)
        S.barrier()
    return nc


I32 = mybir.dt.int32
import math


def rope_tables(S, nc, C, L, cosb, sinb):
    GW = 64
    rows = L // GW
    TWO_PI = 2 * math.pi
    with S.scope():
        ti = S.sbuf("ti", [P, P], I32)
        tf = S.sbuf("tf", [P, P], F32)

        def ppc(name, pattern):
            o = S.sbuf(name, [P, 1], F32)
            S.op("pool", lambda: nc.gpsimd.iota(ti.t[:], pattern, base=0, channel_multiplier=0), [], [ti[:]])
            S.copy(tf[:], ti[:])
            S.tt(tf[:], tf[:], C.ident[:], ALU.mult)
            S.reduce(o[:], tf[:], ALU.add)
            return o
        i16 = ppc("i16", [[0, 2], [0, 2], [1, 16], [0, 2]])
        sel = ppc("sel", [[0, 2], [1, 2], [0, 16], [0, 2]])
        dd = ppc("dd", [[0, 2], [0, 2], [0, 16], [1, 2]])
        sgn = S.sbuf("sgn", [P, 1], F32)
        inv = S.sbuf("inv", [P, 1], F32)
        S.ts(sgn[:], dd[:], 2.0, ALU.mult, -1.0, ALU.add)
        S.act(inv[:], i16[:], AF.Exp, scale=-math.log(10000.0) / 16.0)
        S.ts(inv[:], inv[:], 1.0 / TWO_PI, ALU.mult)
        CH = 1024
        with S.scope():
            ri = S.sbuf("ri", [P, CH], I32)
            ci = S.sbuf("ci", [P, CH], I32)
            rf = S.sbuf("rf", [P, CH], F32)
            cf = S.sbuf("cf", [P, CH], F32)
            xt = S.sbuf("xtn", [P, CH], F32)
            ni = S.sbuf("ni", [P, CH], I32)
            nf = S.sbuf("nf", [P, CH], F32)
            for c0 in range(0, L, CH):
                w = min(CH, L - c0)
                S.op("pool", lambda c0=c0, w=w: nc.gpsimd.iota(ri.t[:, :w], [[1, w // GW], [0, GW]], base=c0 // GW, channel_multiplier=0), [], [ri[:]])
                S.op("pool", lambda w=w: nc.gpsimd.iota(ci.t[:, :w], [[0, w // GW], [1, GW]], base=0, channel_multiplier=0), [], [ci[:]])
                S.copy(rf[:, :w], ri[:, :w])
                S.copy(cf[:, :w], ci[:, :w])
                S.tt(cf[:, :w], cf[:, :w], rf[:, :w], ALU.subtract)
                S.stt(xt[:, :w], cf[:, :w], sel[:, 0:1], rf[:, :w], ALU.mult, ALU.add)
                S.ts(xt[:, :w], xt[:, :w], inv[:, 0:1], ALU.mult)
                for (dst, off) in ((sinb, 0.0), (cosb, 0.25)):
                    if off:
                        S.ts(xt[:, :w], xt[:, :w], off, ALU.add)
                    S.copy(ni[:, :w], xt[:, :w])
                    S.copy(nf[:, :w], ni[:, :w])
                    S.tt(nf[:, :w], xt[:, :w], nf[:, :w], ALU.subtract)
                    S.act(dst[:, c0:c0 + w], nf[:, :w], AF.Sin, scale=TWO_PI * (1 - 1e-6))
                S.ts(sinb[:, c0:c0 + w], sinb[:, c0:c0 + w], sgn[:, 0:1], ALU.mult)


def build_Mdiff(L, LC):
    LK = L + LC
    NKC = LK // P
    nc = bass.Bass("TRN2", target_bir_lowering=False)
    with ExitStack() as st:
        S = Sched(nc, st)
        qT = S.dram("qT", [2, P, L], F32, kind="ExternalInput")
        qsT = S.dram("qsT", [2, P, L], F32, kind="ExternalInput")
        kT = S.dram("kT", [2, P, LK], F32, kind="ExternalInput")
        ksT = S.dram("ksT", [2, P, L], F32, kind="ExternalInput")
        qcT = S.dram("qcT", [2, P, LC], F32, kind="ExternalInput")
        vd = S.dram("v", [P, NKC, 256], F32, kind="ExternalInput")
        lamd = S.dram("lam", [P, 256], F32, kind="ExternalInput")
        nwd = S.dram("nw", [P, 1], F32, kind="ExternalInput")
        lid = S.dram("li", [P, 1], F32, kind="ExternalInput")
        obT = S.dram("obT", [2, P, L], F32, kind="ExternalOutput")
        obcT = S.dram("obcT", [2, P, LC], F32, kind="ExternalOutput")
        C = mk_consts(S, nc)
        C.sq = [S.sbuf("sq%d" % i, [P, 512], F32) for i in range(2)]
        C.lnt = C.sq[0]
        onesb = S.sbuf("onesb", [P, P], BF16)
        S.memset(onesb[:], 1.0)
        Q = [S.sbuf("Q%d" % h, [P, L], BF16) for h in range(2)]
        QC = [S.sbuf("QC%d" % h, [P, LC], BF16) for h in range(2)]
        K = [S.sbuf("K%d" % h, [P, LK], BF16) for h in range(2)]
        V = S.sbuf("V", [P, NKC, 256], BF16)
        lam = S.sbuf("lamt", [P, 4, 64], F32)
        S.dma("sp", lam[:], lamd[:].rearrange("p (a b) -> p a b", b=64))
        nw = S.sbuf("nwt", [P, 1], F32)
        li = S.sbuf("lit", [P, 1], F32)
        S.dma("sp", nw[:], nwd[:])
        S.dma("sp", li[:], lid[:])
        pr = S.sbuf("pr", [P, 2, 64], F32)
        s12 = S.sbuf("s12", [P, 2], F32)
        S.tt(pr[:, 0, :], lam[:, 0, :], lam[:, 1, :], ALU.mult)
        S.tt(pr[:, 1, :], lam[:, 2, :], lam[:, 3, :], ALU.mult)
        S.reduce(s12[:], pr[:], ALU.add)
        e12 = S.sbuf("e12", [P, 2], F32)
        S.act(e12[:], s12[:], AF.Exp)
        neglam = S.sbuf("neglam", [P, 1], F32)
        S.tt(neglam[:], e12[:, 1:2], e12[:, 0:1], ALU.subtract)
        S.tt(neglam[:], neglam[:], li[:], ALU.subtract)
        sc2 = S.sbuf("sc2", [P, 1], F32)
        S.ts(sc2[:], li[:], -1.0, ALU.mult, 1.0, ALU.add)
        S.tt(sc2[:], sc2[:], nw[:], ALU.mult)
        with S.scope():
            cosb = S.sbuf("cosb", [P, L], F32)
            sinb = S.sbuf("sinb", [P, L], F32)
            rope_tables(S, nc, C, L, cosb, sinb)
            with S.scope():
                a = [S.sbuf("la%d" % i, [P, 512], F32) for i in range(2)]
                b = [S.sbuf("lb%d" % i, [P, 512], F32) for i in range(2)]
                vst = [S.sbuf("vst%d" % i, [P, 4, 256], F32) for i in range(2)]
                i = 0
                for h in range(2):
                    for (src, ssw, dst) in ((qT, qsT, Q[h]), (kT, ksT, K[h])):
                        for c0 in range(0, L, 512):
                            ta, tb = a[i % 2], b[i % 2]
                            i += 1
                            S.dma("sp", ta[:], src[h, :, c0:c0 + 512])
                            S.dma("pool", tb[:], ssw[h, :, c0:c0 + 512])
                            S.tt(ta[:], ta[:], cosb[:, c0:c0 + 512], ALU.mult)
                            S.tt(tb[:], tb[:], sinb[:, c0:c0 + 512], ALU.mult, e="pool")
                            S.tt(dst[:, c0:c0 + 512], ta[:], tb[:], ALU.add)
                    ta = a[i % 2]
                    i += 1
                    S.dma("sp", ta[:, :LC], kT[h, :, L:LK])
                    S.copy(K[h][:, L:LK], ta[:, :LC])
                    ta = a[i % 2]
                    i += 1
                    S.dma("sp", ta[:, :LC], qcT[h, :, :])
                    S.copy(QC[h][:], ta[:, :LC])
                for c0 in range(0, NKC, 4):
                    w = min(4, NKC - c0)
                    t = vst[(c0 // 4) % 2]
                    S.dma("sp", t[:, :w, :], vd[:, c0:c0 + w, :])
                    S.copy(V[:, c0:c0 + w, :], t[:, :w, :], e="pool")
        ps_s = [[S.psum("ps_s%d%d" % (j, i), [P, 512]) for i in range(2)] for j in range(2)]
        ps_o = [S.psum("ps_o%d" % j, [P, 512]) for j in range(2)]
        ps_z = [S.psum("ps_z%d" % j, [P, 512]) for j in range(2)]
        pt = [[S.sbuf("pt%d%d" % (j, i), [P, 512], BF16) for i in range(2)] for j in range(2)]
        rz = [S.sbuf("rz%d" % j, [P, 512], F32) for j in range(2)]
        t0 = S.sbuf("t0", [P, 512], F32)
        t1 = S.sbuf("t1", [P, 512], F32)
        rstd = S.sbuf("rstd", [P, 512], F32)
        oo = [S.sbuf("oo%d" % i, [P, 512], F32) for i in range(2)]
        jobs = []
        for h in range(2):
            for q0 in range(0, L, 512):
                jobs.append((h, Q[h][:, q0:q0 + 512], 512, 0, NKC, obT[h, :, q0:q0 + 512]))
            jobs.append((h, QC[h][:], LC, L // P, NKC, obcT[h, :, :]))
        for ji, (h, qv, n, kc0, kc1, outv) in enumerate(jobs):
            for kc in range(kc0, kc1):
                bi = kc % 2
                for j in range(2):
                    S.matmul(ps_s[j][bi][:, :n], K[h][j * 64:(j + 1) * 64, kc * P:(kc + 1) * P], qv[j * 64:(j + 1) * 64, :],
                             start=True, stop=True)
                for j in range(2):
                    S.act(pt[j][bi][:, :n], ps_s[j][bi][:, :n], AF.Exp, scale=0.125)
                for j in range(2):
                    S.matmul(ps_o[j][:, :n], V[:, kc, h * P:(h + 1) * P], pt[j][bi][:, :n], start=(kc == kc0), stop=(kc == kc1 - 1))
                    S.matmul(ps_z[j][:, :n], onesb[:], pt[j][bi][:, :n], start=(kc == kc0), stop=(kc == kc1 - 1))
            for j in range(2):
                S.recip(rz[j][:, :n], ps_z[j][:, :n])
            S.tt(t0[:, :n], ps_o[0][:, :n], rz[0][:, :n], ALU.mult)
            S.tt(t1[:, :n], ps_o[1][:, :n], rz[1][:, :n], ALU.mult)
            S.stt(t0[:, :n], t1[:, :n], neglam[:, 0:1], t0[:, :n], ALU.mult, ALU.add)
            pss = ps_s[0][0]
            rms_rstd(S, C, lambda c: t0[:, :n], n, 1, P, pss, rstd[:, :n])
            o = oo[ji % 2]
            S.tt(t1[:, :n], t0[:, :n], rstd[:, :n], ALU.mult)
            S.ts(o[:, :n], t1[:, :n], sc2[:, 0:1], ALU.mult)
            S.dma("pool", outv, o[:, :n])
        S.barrier()
    return nc


def tri_mask(S, nc, name, kind, blk=None):
    m = S.sbuf(name, [P, P], F32)
    S.memset(m[:], 1.0, e="pool")
    pat, cm, op = {"le": ([[1, P]], -1, ALU.is_ge), "ge": ([[-1, P]], 1, ALU.is_ge),
                   "gt": ([[-1, P]], 1, ALU.is_gt), "lt": ([[1, P]], -1, ALU.is_gt)}[kind]
    S.op("pool", lambda: nc.gpsimd.affine_select(m.t[:], m.t[:], pat, op, 0.0, base=0, channel_multiplier=cm), [m[:]], [m[:]])
    if blk:
        S.memset(m[0:blk, blk:P], 0.0, e="pool")
        S.memset(m[blk:P, 0:blk], 0.0, e="pool")
    return m


def build_Mssd(L, LC, dbg_stop=99):
    LT = L + LC
    NCH = LT // P
    NCC = LC // P
    nc = bass.Bass("TRN2", target_bir_lowering=False)
    with ExitStack() as st:
        S = Sched(nc, st)
        xl = S.dram("xbcl", [4, P, L + 4], F32, kind="ExternalInput")
        xc = S.dram("xbcc", [4, P, LC + 4], F32, kind="ExternalInput")
        cwd = S.dram("cw", [P, 4, 5], F32, kind="ExternalInput")
        cbd = S.dram("cb", [P, 4], F32, kind="ExternalInput")
        zd = S.dram("z", [P, NCH, 256], F32, kind="ExternalInput")
        dtd = S.dram("dt", [P, NCH, 8], F32, kind="ExternalInput")
        dbd = S.dram("dtb", [P, 8], F32, kind="ExternalInput")
        ald = S.dram("alog", [P, 8], F32, kind="ExternalInput")
        dsd = S.dram("dskip", [P, 4], F32, kind="ExternalInput")
        yo = S.dram("y", [NCH, P, 256], F32, kind="ExternalOutput")
        yf = S.dram("yf", [NCH, P, 256], F32, kind="Internal")
        C = mk_consts(S, nc)
        tri = {0: tri_mask(S, nc, "tri_f", "le"), 1: tri_mask(S, nc, "tri_b", "ge")}
        strict = {0: tri_mask(S, nc, "str_f", "gt"), 1: tri_mask(S, nc, "str_b", "lt")}
        cw = S.sbuf("cw", [P, 4, 5], F32)
        cb = S.sbuf("cb", [P, 4], F32)
        S.dma("sp", cw[:], cwd[:])
        S.dma("sp", cb[:], cbd[:])
        xs_tok = S.sbuf("xs_tok", [P, NCH, 256], F32)
        B_tok = S.sbuf("B_tok", [P, NCH, P], F32)
        BT = S.sbuf("BT", [P, LT], F32)
        CT = S.sbuf("CT", [P, LT], F32)
        dtv = S.sbuf("dtv", [P, NCH, 8], F32)
        aall = S.sbuf("aall", [P, NCH, 8], F32)
        dtb = S.sbuf("dtb", [P, 8], F32)
        aneg = S.sbuf("aneg", [P, 8], F32)
        dsk = S.sbuf("dsk", [P, 4], F32)
        S.dma("sp", dtv[:], dtd[:])
        S.dma("sp", dtb[:], dbd[:])
        S.dma("sp", aneg[:], ald[:])
        S.dma("sp", dsk[:], dsd[:])
        S.tt(dtv[:], dtv[:], View(dtb, dtb.t[:].rearrange("p (o e) -> p o e", o=1).to_broadcast([P, NCH, 8])), ALU.add)
        S.act(dtv[:], dtv[:], AF.Exp)
        S.act(dtv[:], dtv[:], AF.Ln, bias=C.ones[:, 0:1])
        S.act(aneg[:], aneg[:], AF.Exp)
        S.ts(aneg[:], aneg[:], -1.0, ALU.mult)
        S.tt(aall[:], dtv[:], View(aneg, aneg.t[:].rearrange("p (o e) -> p o e", o=1).to_broadcast([P, NCH, 8])), ALU.mult)
        ps_t = [S.psum("ps_t%d" % i, [P, 512]) for i in range(2)]
        with S.scope():
            raw = [S.sbuf("raw%d" % i, [P, 4, 516], F32) for i in range(2)]
            acc = [S.sbuf("acc%d" % i, [P, 512], F32) for i in range(2)]
            xsT = [S.sbuf("xsT%d" % i, [P, 512], F32) for i in range(2)]
            segs = [(xc, 0, LC)] + [(xl, LC, L)]
            ti = 0
            if dbg_stop < 0:
                segs = []
            for (src, base, seglen) in segs:
                for t0 in range(0, seglen, 512):
                    n = min(512, seglen - t0)
                    r = raw[ti % 2]
                    ti += 1
                    S.dma("sp", r[:, :, :n + 4], src[:, :, t0:t0 + n + 4].rearrange("c p t -> p c t"))
                    for c in range(4):
                        a = acc[c % 2]
                        eng = "dve"
                        S.ts(a[:, :n], r[:, c, 0:n], cw[:, c, 0:1], ALU.mult, e=eng)
                        for j in range(1, 5):
                            S.stt(a[:, :n], r[:, c, j:j + n], cw[:, c, j:j + 1], a[:, :n], ALU.mult, ALU.add, e=eng)
                        g0 = base + t0
                        if c < 2:
                            dst = xsT[c]
                            S.act(dst[:, :n], a[:, :n], AF.Silu, bias=cb[:, c:c + 1])
                        elif c == 2:
                            S.act(BT[:, g0:g0 + n], a[:, :n], AF.Silu, bias=cb[:, c:c + 1])
                        else:
                            S.act(CT[:, g0:g0 + n], a[:, :n], AF.Silu, bias=cb[:, c:c + 1])
                    for bl in range(n // P):
                        gc = (base + t0) // P + bl
                        pt = ps_t[bl % 2]
                        dbgv = None
                        S.transpose(pt[:, 0:P], xsT[0][:, bl * P:(bl + 1) * P], C.ident[:])
                        if dbgv == "T1":
                            S.copy(xs_tok[:, gc, 0:P], pt[:, 0:P], e="dve")
                            continue
                        S.transpose(pt[:, P:2 * P], xsT[1][:, bl * P:(bl + 1) * P], C.ident[:])
                        if dbgv == "T2":
                            S.copy(xs_tok[:, gc, :], pt[:, 0:2 * P], e="dve")
                            continue
                        S.transpose(pt[:, 2 * P:3 * P], BT[:, gc * P:(gc + 1) * P], C.ident[:])
                        S.copy(xs_tok[:, gc, :], pt[:, 0:2 * P], e="dve")
                        S.copy(B_tok[:, gc, :], pt[:, 2 * P:3 * P], e="dve")
        ps_arg = [S.psum("ps_arg%d" % i, [P, 512]) for i in range(2)]
        ps_cb = S.psum("ps_cb", [P, 512])
        ps_y = S.psum("ps_y", [P, 512])
        ps_st = S.psum("ps_st", [P, 512])
        ps_sm = S.psum("ps_sm", [P, 512])
        X = [S.sbuf("X%d" % i, [P, 4, P], F32) for i in range(2)]
        LTt = [S.sbuf("LT%d" % i, [P, 4, P], F32) for i in range(2)]
        CBm = S.sbuf("CBm", [P, P], F32)
        scT = [S.sbuf("scT%d" % i, [P, 4, P], F32) for i in range(2)]
        sm = S.sbuf("sm", [P, 8], F32)
        eacs = S.sbuf("eacs", [P, 4], F32)
        edec = S.sbuf("edec", [P, 4], F32)
        etot = S.sbuf("etot", [P, 4], F32)
        dif = S.sbuf("dif", [P, 4], F32)
        xdt = [S.sbuf("xdt%d" % i, [P, 4, 64], F32) for i in range(2)]
        xdtd = [S.sbuf("xdtd%d" % i, [P, 4, 64], F32) for i in range(2)]
        ST = S.sbuf("ST", [P, 4, 64], F32)
        yt = [S.sbuf("yt%d" % i, [P, 256], F32) for i in range(2)]
        y2 = [S.sbuf("y2%d" % i, [P, 256], F32) for i in range(2)]
        yfl = [S.sbuf("yfl%d" % i, [P, 256], F32) for i in range(2)]
        zt = [S.sbuf("zt%d" % i, [P, 256], F32) for i in range(2)]
        yfb = [S.sub("yf%d" % c, yf.t[c]) for c in range(NCH)]

        def bc4(v):
            return View(v.buf, v.ap.rearrange("p (h o) -> p h o", o=1).to_broadcast([P, 4, 64]))
        if dbg_stop < 2:
            for c in range(NCH):
                S.dma("sp", zt[c % 2][:], zd[:, c, :])
                if dbg_stop == 1:
                    S.tt(zt[c % 2][:], zt[c % 2][:], xs_tok[:, c, :], ALU.add)
                    S.tt(zt[c % 2][:, 0:P], zt[c % 2][:, 0:P], B_tok[:, c, :], ALU.add)
                S.dma("pool", S.sub("yo", yo.t[c])[:], zt[c % 2][:])
        for dr in range(2 if dbg_stop >= 2 else 0):
            order = list(range(NCH)) if dr == 0 else (list(range(NCC - 1, -1, -1)) + list(range(NCH - 1, NCC - 1, -1)))
            S.memset(ST[:], 0.0)
            for it, c in enumerate(order):
                bi = it % 2
                a4 = aall[:, c, dr * 4:(dr + 1) * 4]
                for h in range(4):
                    S.ts(X[bi][:, h, :], strict[dr][:], aall[:, c, dr * 4 + h:dr * 4 + h + 1], ALU.mult, e=("dve" if h % 2 else "pool"))
                for h in range(4):
                    S.matmul(ps_arg[bi][:, h * P:(h + 1) * P], X[bi][:, h, :], tri[dr][:])
                S.act(LTt[bi][:].rearrange("p h l -> p (h l)"), ps_arg[bi][:], AF.Exp)
                S.matmul(ps_cb[:, 0:P], BT[:, c * P:(c + 1) * P], CT[:, c * P:(c + 1) * P])
                S.tt(CBm[:], ps_cb[:, 0:P], tri[dr][:], ALU.mult)
                S.tt(scT[bi][:], LTt[bi][:], View(CBm, CBm.t[:].rearrange("p (o l) -> p o l", o=1).to_broadcast([P, 4, P])), ALU.mult)
                S.matmul(ps_sm[:, 0:4], tri[dr][:], a4)
                S.matmul(ps_sm[:, 4:8], C.ones[:], a4)
                S.copy(sm[:], ps_sm[:, 0:8])
                S.act(eacs[:], sm[:, 0:4], AF.Exp)
                S.act(etot[:], sm[:, 4:8], AF.Exp)
                S.tt(dif[:], sm[:, 4:8], sm[:, 0:4], ALU.subtract)
                S.act(edec[:], dif[:], AF.Exp)
                xv = xs_tok[:, c, :].rearrange("p (h d) -> p h d", h=4)
                S.tt(xdt[bi][:], xv, bc4(dtv[:, c, dr * 4:(dr + 1) * 4]), ALU.mult, e="pool")
                S.tt(xdtd[bi][:], xdt[bi][:], bc4(edec[:]), ALU.mult)
                for h in range(4):
                    S.matmul(ps_y[:, h * 64:(h + 1) * 64], scT[bi][:, h, :], xdt[bi][:, h, :])
                S.matmul(ps_y[:, 256:512], CT[:, c * P:(c + 1) * P], ST[:].rearrange("p h d -> p (h d)"))
                y = yt[bi]
                S.tt(y[:].rearrange("p (h d) -> p h d", h=4), ps_y[:, 256:512].rearrange("p (h d) -> p h d", h=4), bc4(eacs[:]), ALU.mult)
                S.tt(y[:], y[:], ps_y[:, 0:256], ALU.add)
                S.matmul(ps_st[:, 0:256], B_tok[:, c, :], xdtd[bi][:].rearrange("p h d -> p (h d)"))
                S.tt(ST[:], ST[:], bc4(etot[:]), ALU.mult)
                S.tt(ST[:].rearrange("p h d -> p (h d)"), ST[:].rearrange("p h d -> p (h d)"), ps_st[:, 0:256], ALU.add)
                if dr == 0:
                    S.dma("pool", yfb[c][:], y[:])
                else:
                    S.dma("sp", yfl[bi][:], yfb[c][:])
                    S.dma("sp", zt[bi][:], zd[:, c, :])
                    o = y2[bi]
                    S.tt(o[:].rearrange("p (h d) -> p h d", h=4), xv, bc4(dsk[:]), ALU.mult, e="pool")
                    S.tt(y[:], y[:], yfl[bi][:], ALU.add)
                    S.tt(o[:], o[:], y[:], ALU.add)
                    S.act(zt[bi][:], zt[bi][:], AF.Silu)
                    S.tt(o[:], o[:], zt[bi][:], ALU.mult)
                    S.dma("pool", S.sub("yo", yo.t[c])[:], o[:])
        S.barrier()
    return nc


def build_Mgdn(L, LC):
    LT = L + LC
    NCH = LT // P
    NCC = LC // P
    nc = bass.Bass("TRN2", target_bir_lowering=False)
    with ExitStack() as st:
        S = Sched(nc, st)
        ql = S.dram("qkvl", [6, P, L + 4], F32, kind="ExternalInput")
        qc = S.dram("qkvc", [6, P, LC + 4], F32, kind="ExternalInput")
        cwd = S.dram("cw", [P, 6, 5], F32, kind="ExternalInput")
        zd = S.dram("z", [P, NCH, 256], F32, kind="ExternalInput")
        ad = S.dram("araw", [P, NCH, 4], F32, kind="ExternalInput")
        bd = S.dram("braw", [P, NCH, 4], F32, kind="ExternalInput")
        ald = S.dram("alog", [P, 4], F32, kind="ExternalInput")
        dbd = S.dram("dtb", [P, 4], F32, kind="ExternalInput")
        nwd = S.dram("nw", [P, P], F32, kind="ExternalInput")
        oa = S.dram("oa", [NCH, P, 256], F32, kind="ExternalOutput")
        ofd = S.dram("of", [NCH, P, 256], F32, kind="Internal")
        C = mk_consts(S, nc)
        M = {k: tri_mask(S, nc, "m_" + k, k, blk=64) for k in ("le", "ge", "gt", "lt")}
        halfA = S.sbuf("halfA", [P, P], F32)
        halfB = S.sbuf("halfB", [P, P], F32)
        S.memset(halfA[:], 0.0)
        S.memset(halfB[:], 0.0)
        S.memset(halfA[0:64, :], 1.0)
        S.memset(halfB[64:128, :], 1.0)
        cw = S.sbuf("cw", [P, 6, 5], F32)
        S.dma("sp", cw[:], cwd[:])
        nw = S.sbuf("nw", [P, P], F32)
        S.dma("sp", nw[:], nwd[:])
        gall = S.sbuf("gall", [P, NCH, 4], F32)
        ball = S.sbuf("ball", [P, NCH, 4], F32)
        negb = S.sbuf("negb", [P, NCH, 4], F32)
        aneg = S.sbuf("aneg", [P, 4], F32)
        dtb = S.sbuf("dtb", [P, 4], F32)
        S.dma("sp", gall[:], ad[:])
        S.dma("sp", ball[:], bd[:])
        S.dma("sp", aneg[:], ald[:])
        S.dma("sp", dtb[:], dbd[:])

        def bcn(v):
            return View(v.buf, v.ap.rearrange("p (o e) -> p o e", o=1).to_broadcast([P, NCH, 4]))
        S.tt(gall[:], gall[:], bcn(dtb[:]), ALU.add)
        S.act(gall[:], gall[:], AF.Exp)
        S.act(gall[:], gall[:], AF.Ln, bias=C.ones[:, 0:1])
        S.act(aneg[:], aneg[:], AF.Exp)
        S.ts(aneg[:], aneg[:], -1.0, ALU.mult)
        S.tt(gall[:], gall[:], bcn(aneg[:]), ALU.mult)
        S.act(ball[:], ball[:], AF.Sigmoid)
        S.ts(negb[:], ball[:], -1.0, ALU.mult)
        BA = [S.psum("BA%d" % h, [P, 512]) for h in range(2)]
        B1 = [S.psum("B1%d" % h, [P, 512]) for h in range(2)]
        B2 = [S.psum("B2%d" % h, [P, 512]) for h in range(2)]
        B3 = [S.psum("B3%d" % h, [P, 512]) for h in range(2)]
        Wk = []
        for h in range(2):
            W = NS()
            for nm in ("X", "Dm", "Dv", "Ds", "kbg", "kdec", "vb", "vnew", "oq", "o", "of_", "zt", "t1"):
                setattr(W, nm, S.sbuf("%s%d" % (nm, h), [P, P], F32))
            for nm in ("NA", "RA", "uw"):
                setattr(W, nm, S.sbuf("%s%d" % (nm, h), [P, 2 * P], F32))
            W.NR = [S.sbuf("NR%d%d" % (h, i), [P, 2 * P], F32) for i in range(2)]
            W.Xc = [S.sbuf("Xc%d%d" % (h, i), [P, P], F32) for i in range(2)]
            W.esm = S.sbuf("esm%d" % h, [P, 4], F32)
            W.bg = S.sbuf("bg%d" % h, [P, 1], F32)
            W.ss = S.sbuf("ss%d" % h, [P, 1], F32)
            W.oo = [S.sbuf("oo%d%d" % (h, i), [P, P], F32) for i in range(2)]
            Wk.append(W)
        state = [S.sbuf("state%d" % h, [P, P], F32) for h in range(2)]
        raw = S.sbuf("raw", [P, 6, 516], F32)
        acc = [S.sbuf("acc%d" % i, [P, 512], F32) for i in range(2)]
        sqb = S.sbuf("sqb", [P, 512], F32)
        lnb = S.sbuf("lnb", [P, 512], F32)
        rsb = S.sbuf("rsb", [P, 512], F32)
        qkv = [S.sbuf("qkv%d" % i, [P, 6, 512], F32) for i in range(2)]
        ofb = [[S.sub("of%d_%d" % (c, h), ofd.t[c][:, h * P:(h + 1) * P]) for h in range(2)] for c in range(NCH)]

        def prep(src, t0, n, dst):
            S.dma("sp", raw[:, :, :n + 4], src[:, :, t0:t0 + n + 4].rearrange("c p t -> p c t"))
            for c in range(6):
                a = acc[c % 2]
                S.ts(a[:, :n], raw[:, c, 0:n], cw[:, c, 0:1], ALU.mult)
                for j in range(1, 5):
                    S.stt(a[:, :n], raw[:, c, j:j + n], cw[:, c, j:j + 1], a[:, :n], ALU.mult, ALU.add)
                if c >= 4:
                    S.act(dst[:, c, :n], a[:, :n], AF.Silu)
                else:
                    S.act(a[:, :n], a[:, :n], AF.Silu)
                    S.act(sqb[:, :n], a[:, :n], AF.Square)
                    pb = B3[c % 2]
                    S.matmul(pb[:, :n], C.ones[:], sqb[:, :n])
                    S.act(lnb[:, :n], pb[:, :n], AF.Ln, bias=C.eps[:, 0:1])
                    S.act(rsb[:, :n], lnb[:, :n], AF.Exp, scale=-0.5)
                    S.stt(dst[:, c, :n], a[:, :n], (128.0 ** -0.5) if c < 2 else 1.0, rsb[:, :n], ALU.mult, ALU.mult)

        def unit(hl, dr, gp, qv, kv, vv):
            col = dr * 2 + hl
            g = gall[:, gp, col:col + 1]
            nb = negb[:, gp, col:col + 1]
            bt = ball[:, gp, col:col + 1]
            W = Wk[hl]
            bA, b1, b2, b3 = BA[hl], B1[hl], B2[hl], B3[hl]
            Tri, Xm, Val, SVal = (M["le"], M["gt"], M["ge"], M["gt"]) if dr == 0 else (M["ge"], M["lt"], M["le"], M["lt"])
            S.ts(W.X[:], Xm[:], g, ALU.mult)
            S.matmul(bA[:, 0:128], Tri[:], W.X[:])
            S.matmul(bA[:, 128:129], Tri[:], g)
            S.matmul(bA[:, 129:130], Xm[:], g)
            S.matmul(bA[:, 130:131], halfA[:], g)
            S.matmul(bA[:, 131:132], halfB[:], g)
            S.matmul(b1[:, 0:128], kv, kv)
            S.matmul(b1[:, 128:256], qv, kv)
            S.transpose(bA[:, 256:384], kv, C.ident[:])
            S.transpose(bA[:, 384:512], vv, C.ident[:])
            yield
            S.act(W.Dm[:], bA[:, 0:128], AF.Exp)
            S.act(W.esm[:], bA[:, 128:132], AF.Exp)
            S.tt(W.bg[:], W.esm[:, 0:1], bt, ALU.mult)
            S.act(W.kdec[:], bA[:, 256:384], AF.Identity, scale=W.esm[:, 1:2])
            S.act(W.vb[:], bA[:, 384:512], AF.Identity, scale=bt)
            S.act(W.kbg[:], bA[:, 256:384], AF.Identity, scale=W.bg[:, 0:1])
            S.tt(W.Dv[:], W.Dm[:], Val[:], ALU.mult)
            S.tt(W.Ds[:], W.Dm[:], SVal[:], ALU.mult)
            S.stt(W.NA[:, 0:128], b1[:, 0:128], nb, W.Ds[:], ALU.mult, ALU.mult)
            S.tt(W.NA[:, 128:256], b1[:, 128:256], W.Dv[:], ALU.mult)
            yield
            S.transpose(b1[:, 256:384], W.NA[:, 0:128], C.ident[:])
            S.transpose(b1[:, 384:512], W.NA[:, 128:256], C.ident[:])
            S.copy(W.RA[:], b1[:, 256:512])
            X = W.Xc[0]
            S.tt(X[:], W.RA[:, 0:128], C.ident[:], ALU.add)
            yield
            Ncur = W.NA[:, 0:128]
            Rcur = W.RA[:, 0:128]
            for lev in range(5):
                NR = W.NR[lev % 2]
                S.matmul(b2[:, 0:128], Rcur, Ncur)
                if lev < 4:
                    S.matmul(b2[:, 128:256], Ncur, Rcur)
                    S.copy(NR[:], b2[:, 0:256])
                else:
                    S.copy(NR[:, 0:128], b2[:, 0:128])
                yield
                S.matmul(b2[:, 256:384], NR[:, 0:128], X[:])
                Xn = W.Xc[(lev + 1) % 2]
                S.tt(Xn[:], X[:], b2[:, 256:384], ALU.add)
                X = Xn
                Ncur = NR[:, 0:128]
                Rcur = NR[:, 128:256]
                yield
            S.matmul(b3[:, 0:128], X[:], W.vb[:])
            S.matmul(b3[:, 128:256], W.kbg[:], X[:])
            S.copy(W.uw[:], b3[:, 0:256])
            yield
            blocks = [(0, 64), (64, 128)] if dr == 0 else [(64, 128), (0, 64)]
            Sst = state[hl]
            for bi, (r0, r1) in enumerate(blocks):
                reg = b3[:, 256:512] if bi == 0 else b3[:, 0:256]
                S.matmul(reg[:, 0:128], W.uw[:, 128:256], Sst[:])
                S.matmul(reg[:, 128:256], qv, Sst[:])
                S.tt(W.vnew[r0:r1, :], W.uw[r0:r1, 0:128], reg[r0:r1, 0:128], ALU.subtract)
                S.ts(W.oq[r0:r1, :], reg[r0:r1, 128:256], W.esm[r0:r1, 0:1], ALU.mult)
                yield
                S.matmul(b1[:, 0:128], W.kdec[r0:r1, :], W.vnew[r0:r1, :])
                egX = W.esm[:, 2:3] if r0 == 0 else W.esm[:, 3:4]
                S.stt(Sst[:], Sst[:], egX, b1[:, 0:128], ALU.mult, ALU.add)
                yield
            S.matmul(b1[:, 128:256], W.RA[:, 128:256], W.vnew[:])
            S.tt(W.o[:], W.oq[:], b1[:, 128:256], ALU.add)
            if dr == 0:
                S.dma("pool", ofb[gp][hl][:], W.o[:])
            else:
                S.dma("sp", W.of_[:], ofb[gp][hl][:])
                S.dma("sp", W.zt[:], zd[:, gp, hl * P:(hl + 1) * P])
                S.tt(W.o[:], W.o[:], W.of_[:], ALU.add)
                S.act(W.t1[:], W.o[:], AF.Square, accum_out=W.ss[:, 0:1])
                yield
                S.act(W.ss[:], W.ss[:], AF.Ln, scale=1.0 / 128.0, bias=C.eps[:, 0:1])
                S.act(W.ss[:], W.ss[:], AF.Exp, scale=-0.5)
                S.act(W.zt[:], W.zt[:], AF.Silu)
                S.stt(W.t1[:], W.o[:], W.ss[:, 0:1], nw[:], ALU.mult, ALU.mult)
                oo = W.oo[gp % 2]
                S.tt(oo[:], W.t1[:], W.zt[:], ALU.mult)
                S.dma("pool", S.sub("oa", oa.t[gp][:, hl * P:(hl + 1) * P])[:], oo[:])
            yield

        for dr in range(2):
            for h in range(2):
                S.memset(state[h][:], 0.0)
            segs = [(qc, 0, LC), (ql, LC, L)]
            tl = []
            for (src, base, seglen) in segs:
                tt_ = [(src, base, t0, min(512, seglen - t0)) for t0 in range(0, seglen, 512)]
                if dr == 1:
                    tt_ = tt_[::-1]
                tl += tt_
            for ti, (src, base, t0, n) in enumerate(tl):
                dst = qkv[ti % 2]
                prep(src, t0, n, dst)
                prs = list(range(n // P))
                if dr == 1:
                    prs = prs[::-1]
                for pi in prs:
                    gp = (base + t0) // P + pi
                    sl = slice(pi * P, (pi + 1) * P)
                    gens = [unit(h, dr, gp, dst[:, 0 + h, sl], dst[:, 2 + h, sl], dst[:, 4 + h, sl]) for h in range(2)]
                    alive = [True, True]
                    while any(alive):
                        for h in range(2):
                            if alive[h]:
                                try:
                                    next(gens[h])
                                except StopIteration:
                                    alive[h] = False
        S.barrier()
    return nc


NFM = 36
NTK = 1568
GRP = [[0, 1], [2, 3], [4, 5], [6, 7]]


def build_fused(L, LC, depth=2):
    NT, NCX = L // 2, LC // 2
    TT = NT + NCX
    LT = L + LC
    NCH = LT // P
    NCC = LC // P
    LK = LT
    NKC = LK // P
    NLC = NT // P
    assert NCX == P
    nc = bass.Bass("TRN2", target_bir_lowering=False)
    with ExitStack() as st:
        S = Sched(nc, st)
        ccsem = st.enter_context(nc.semaphore("ccsem"))
        cc = [0]
        xT = S.dram("xT", [P, KC, TT], F32, kind="ExternalInput")
        cv = S.dram("cv", [P, KC, 2], F32, kind="ExternalInput")
        selv = S.dram("selv", [P, 2], F32, kind="ExternalInput")
        yT = S.dram("yT", [P, KC, TT], F32, kind="ExternalOutput")
        Wl = []
        for i in range(depth):
            W = NS()
            sfx = "_%d" % i
            W.wada1 = S.dram("wada1" + sfx, [D, 5 * D], F32, kind="ExternalInput")
            W.bada1 = S.dram("bada1" + sfx, [P, 40], F32, kind="ExternalInput")
            W.wada2 = S.dram("wada2" + sfx, [D, 6 * D], F32, kind="ExternalInput")
            W.bada2 = S.dram("bada2" + sfx, [P, 48], F32, kind="ExternalInput")
            W.ng = S.dram("ng" + sfx, [P, 48], F32, kind="ExternalInput")
            W.w1a = S.dram("w1a" + sfx, [D, 2 * DFF], F32, kind="ExternalInput")
            W.w2a = S.dram("w2a" + sfx, [DFF, D], F32, kind="ExternalInput")
            W.w1b = S.dram("w1b" + sfx, [D, 2 * DFF], F32, kind="ExternalInput")
            W.w2b = S.dram("w2b" + sfx, [DFF, D], F32, kind="ExternalInput")
            W.win = S.dram("win" + sfx, [D, NFM * P + NTK], F32, kind="ExternalInput")
            W.wg = S.dram("wg" + sfx, [D, 3 * D], F32, kind="ExternalInput")
            W.wb = S.dram("wb" + sfx, [1536, D], F32, kind="ExternalInput")
            W.wo = S.dram("wo" + sfx, [D, D], F32, kind="ExternalInput")
            W.snw = S.dram("snw" + sfx, [P, 4], F32, kind="ExternalInput")
            W.lam = S.dram("lam" + sfx, [P, 256], F32, kind="ExternalInput")
            W.dnw = S.dram("dnw" + sfx, [P, 1], F32, kind="ExternalInput")
            W.li = S.dram("li" + sfx, [P, 1], F32, kind="ExternalInput")
            W.scw = S.dram("scw" + sfx, [P, 4, 5], F32, kind="ExternalInput")
            W.scb = S.dram("scb" + sfx, [P, 4], F32, kind="ExternalInput")
            W.sdtb = S.dram("sdtb" + sfx, [P, 8], F32, kind="ExternalInput")
            W.salog = S.dram("salog" + sfx, [P, 8], F32, kind="ExternalInput")
            W.sdsk = S.dram("sdsk" + sfx, [P, 4], F32, kind="ExternalInput")
            W.gcw = S.dram("gcw" + sfx, [P, 6, 5], F32, kind="ExternalInput")
            W.galog = S.dram("galog" + sfx, [P, 4], F32, kind="ExternalInput")
            W.gdtb = S.dram("gdtb" + sfx, [P, 4], F32, kind="ExternalInput")
            W.gnw = S.dram("gnw" + sfx, [P, P], F32, kind="ExternalInput")
            Wl.append(W)
        Xs = S.dram("Xs", [P, KC, TT], F32)
        X1 = S.dram("X1s", [P, KC, TT], F32)
        X2 = S.dram("X2s", [P, KC, TT], F32)
        NB = NT // 256
        PTL = nc.dram_tensor("PTL", [NFM, P, NT], F32)
        PTLG = nc.dram_tensor("PTLG", [NFM, 2, P, NT], F32)
        PTC = nc.dram_tensor("PTC", [NFM, P, NCX], F32)
        PTCG = nc.dram_tensor("PTCG", [2, 2, 18, P, NCX], F32)
        PKL = nc.dram_tensor("PKL", [NB, 256, NTK], F32)
        PKLG = nc.dram_tensor("PKLG", [NB, 2, 256, NTK], F32)
        PKC = nc.dram_tensor("PKC", [NCX, NTK], F32)
        PKCG = nc.dram_tensor("PKCG", [2, NCX, NTK], F32)
        MOL = nc.dram_tensor("MOL", [6, 2, P, NT], F32)
        MOLG = nc.dram_tensor("MOLG", [6, 2, 2, P, NT], F32)
        MOC = nc.dram_tensor("MOC", [6, 2, P, NCX], F32)
        MOCG = nc.dram_tensor("MOCG", [2, 6, 2, P, NCX], F32)
        OF = S.dram("OFs", [NCH, P, 256], F32)

        def mo_dst(c0, c1, s_, off, n):
            if off >= NT:
                return MOC.ap()[c0:c1, s_, :, off - NT:off - NT + n]
            return MOL.ap()[c0:c1, s_, :, off:off + n]

        def dsub(ap):
            return Buf("u", ap, "dram")[:] if False else View(Buf("u", ap, "dram"), ap)

        tiles = mk_tiles(NT, NCX)
        C = mk_consts(S, nc)
        sel = S.sbuf("sel", [P, 2], F32)
        S.dma("sp", sel[:], selv[:])
        dummy = S.sbuf("dummy", [P, 1], F32)

        def blend(dst, alt):
            S.ts(dst, dst, sel[:, 0:1], ALU.mult)
            S.stt(dst, alt, sel[:, 1:2], dst, ALU.mult, ALU.add)

        def gather_many(pairs):
            S.barrier()
            for (i_ap, o_ap) in pairs:
                cc[0] += 1
                nc.gpsimd.collective_compute("AllGather", ALU.bypass, replica_groups=GRP, ins=[i_ap], outs=[o_ap]).then_inc(ccsem, 1)
            nc.gpsimd.wait_ge(ccsem, cc[0])
            S.memset(dummy[:], 0.0, e="pool")
            S.barrier()

        def gather_P():
            pr = [(PTL.ap()[c], PTLG.ap()[c].rearrange("r p t -> (r p) t")) for c in range(NFM)]
            pr += [(PTC.ap()[h * 18:(h + 1) * 18].rearrange("c p t -> (c p) t"), PTCG.ap()[h].rearrange("r c p t -> (r c p) t")) for h in range(2)]
            pr += [(PKL.ap()[b], PKLG.ap()[b].rearrange("r t e -> (r t) e")) for b in range(NB)]
            pr += [(PKC.ap(), PKCG.ap().rearrange("r t e -> (r t) e"))]
            gather_many(pr)

        def gather_M():
            pr = [(MOL.ap()[c, s_], MOLG.ap()[c, s_].rearrange("r p t -> (r p) t")) for c in range(6) for s_ in range(2)]
            pr += [(MOC.ap().rearrange("c s p t -> (c s p) t"), MOCG.ap().rearrange("r c s p t -> (r c s p) t"))]
            gather_many(pr)

        def tokpos(gc):
            if gc < NCC:
                return gc, NT
            t = (gc - NCC) * P
            return t // NT, t % NT

        def mk_ps():
            PS = NS()
            PS.ss = S.psum("ps_ss", [P, 512])
            PS.ss2 = S.psum("ps_ss2", [P, 512])
            PS.g = [S.psum("ps_g%d" % i, [P, 512]) for i in range(2)]
            PS.u = [S.psum("ps_u%d" % i, [P, 512]) for i in range(2)]
            PS.y = [S.psum("ps_y%d" % i, [P, 512]) for i in range(2)]
            return PS

        def bc(v):
            return View(v.buf, v.ap.rearrange("p (c o) -> p c o", o=1).to_broadcast([P, KC, 2]))

        def ph_R1(W, xin, x1t):
            with S.scope():
                alloc_small(S, C)
                PS = mk_ps()
                mods = S.sbuf("mods", [P, 40, 2], F32)
                ngt = S.sbuf("ngt", [P, 6, KC], F32)
                S.dma("sp", ngt[:], W.ng[:].rearrange("p (m c) -> p m c", c=KC))
                A1 = S.sbuf("A1", [P, KC, 2], F32)
                G1 = S.sbuf("G1", [P, KC, 2], F32)
                A2 = S.sbuf("A2", [P, KC, 2], F32)
                with S.scope():
                    stg = [S.sbuf("stgm%d" % i, [P, 5 * D], F32) for i in range(KC)]
                    compute_mods(S, C, cv, W.wada1, W.bada1, 5, stg, PS.g[0], mods)
                S.stt(A1[:], mods[:, 8:16, :], 1.0, bc(ngt[:, 0, :]), ALU.add, ALU.mult)
                S.stt(G1[:], mods[:, 16:24, :], 0.5, bc(ngt[:, 1, :]), ALU.mult, ALU.mult)
                S.stt(A2[:], mods[:, 32:40, :], 1.0, bc(ngt[:, 2, :]), ALU.add, ALU.mult)
                B1 = mods[:, 0:8, :]
                B2 = mods[:, 24:32, :]
                with S.scope():
                    w1b = S.sbuf("w1b", [P, KC, 2 * DFF], BF16)
                    w2b = S.sbuf("w2b", [P, FC, D], BF16)
                    with S.scope():
                        stages = [S.sbuf("wst%d" % i, [P, 2048], F32) for i in range(3)]
                        load_w(S, W.w1a, w1b, D, 2 * DFF, stages)
                        load_w(S, W.w2a, w2b, DFF, D, stages)
                    alloc_ffn_work(S, C)
                    ffn_sweep(S, C, tiles, lambda j: xin[j][:], lambda j: x1t[j][:], w1b, w2b, A1[:], B1, G1[:], PS)
                with S.scope():
                    NW = NFM * P + NTK
                    winb = S.sbuf("winb", [P, KC, NW], BF16)
                    with S.scope():
                        stages = [S.sbuf("wst%d" % i, [P, 2048], F32) for i in range(3)]
                        load_w(S, W.win, winb, D, NW, stages)
                    xt2 = [S.sbuf("xq%d" % i, [P, KC, 512], F32) for i in range(2)]
                    h = S.sbuf("h2", [P, KC, 512], BF16)
                    ost = [S.sbuf("ost%d" % i, [P, 4, 512], F32) for i in range(3)]
                    tst = [S.sbuf("tst%d" % i, [P, NTK], F32) for i in range(2)]
                    pps = PS.g + PS.u + PS.y
                    gi = 0
                    ti = 0
                    for j, (s0, n, col) in enumerate(tiles):
                        xt = xt2[j % 2]
                        S.dma("sp", xt[:, :, :n], x1t[j][:])
                        norm_mod(S, C, xt, n, A2[:], B2, col, h, PS.ss)
                        for c0 in range(0, NFM, 4):
                            nn = min(4, NFM - c0)
                            o = ost[gi % 3]
                            gi += 1
                            for cc_ in range(nn):
                                pp = pps[(c0 + cc_) % 6]
                                for k in range(KC):
                                    S.matmul(pp[:, :n], winb[:, k, (c0 + cc_) * P:(c0 + cc_ + 1) * P], h[:, k, :n],
                                             start=(k == 0), stop=(k == KC - 1))
                                S.copy(o[:, cc_, :n], pp[:, :n], e=("act" if cc_ % 2 else "dve"))
                            pdst = PTL.ap()[c0:c0 + nn, :, s0:s0 + n] if col == 0 else PTC.ap()[c0:c0 + nn, :, 0:n]
                            S.dma("pool", dsub(pdst.rearrange("c p t -> p c t")), o[:, :nn, :n])
                        for sb in range(n // P):
                            tt_ = tst[ti % 2]
                            ti += 1
                            for q, c0 in enumerate(range(0, NTK, 512)):
                                w = min(512, NTK - c0)
                                pp = pps[q % 6]
                                for k in range(KC):
                                    S.matmul(pp[:, :w], h[:, k, sb * P:(sb + 1) * P], winb[:, k, NFM * P + c0:NFM * P + c0 + w],
                                             start=(k == 0), stop=(k == KC - 1))
                                S.copy(tt_[:, c0:c0 + w], pp[:, :w], e=("act" if q % 2 else "dve"))
                            trow = s0 + sb * P
                            kdst = PKL.ap()[trow // 256, trow % 256:trow % 256 + P, :] if col == 0 else PKC.ap()[0:P, :]
                            S.dma("pool", dsub(kdst), tt_[:])

        def lat_rc(t0):
            return t0 // NT, t0 % NT

        def load_fm(q, dst, alt, c0, nch, seg, t0, n, halo):
            seglen = L if seg == "lat" else LC
            a, b = max(0, t0 - halo), min(seglen, t0 + n + halo)
            if halo and (t0 - halo < 0):
                S.memset(dst[:, :, 0:halo], 0.0)
                S.memset(alt[:, :, 0:halo], 0.0)
            if halo and (t0 + n + halo > seglen):
                S.memset(dst[:, :, n + halo:n + 2 * halo], 0.0)
                S.memset(alt[:, :, n + halo:n + 2 * halo], 0.0)
            pieces = []
            per = NT if seg == "lat" else NCX
            base = 0 if seg == "lat" else NT
            p = a
            while p < b:
                r = p // per
                e = min(b, (r + 1) * per)
                pieces.append((r, p % per, e - p, p - (t0 - halo)))
                p = e
            for g, tgt in ((0, dst), (1, alt)):
                for (r, col0, ln, d0) in pieces:
                    if seg == "lat":
                        sap = PTLG.ap()[g * 18 + c0:g * 18 + c0 + nch, r, :, col0:col0 + ln]
                    else:
                        sap = PTCG.ap()[g, r, c0:c0 + nch, :, col0:col0 + ln]
                    S.dma(q, tgt[:, :, d0:d0 + ln], dsub(sap.rearrange("c p t -> p c t")))
            blend(dst[:, :, :], alt[:, :, :])

        def load_tok(q, dst, alt, gc, e0, ne):
            s, off = tokpos(gc)
            for g, tgt in ((0, dst), (1, alt)):
                if off >= NT:
                    sap = PKCG.ap()[s, 0:P, g * 784 + e0:g * 784 + e0 + ne]
                else:
                    sap = PKLG.ap()[off // 256, s, off % 256:off % 256 + P, g * 784 + e0:g * 784 + e0 + ne]
                S.dma(q, tgt, dsub(sap))
            blend(dst, alt)

        def load_small(smallst, alt):
            for g, tgt in ((0, smallst), (1, alt)):
                for r in range(2):
                    S.dma("sp", tgt[:, r, :], dsub(PKCG.ap()[r, 0:P, g * 784 + 768:g * 784 + 784]))
                    for bq in range(NB):
                        c_ = NCC + r * NLC + 2 * bq
                        S.dma("sp" if bq % 2 else "pool", tgt[:, c_:c_ + 2, :],
                              dsub(PKLG.ap()[bq, r, :, g * 784 + 768:g * 784 + 784].rearrange("(c p) e -> p c e", p=P)))
            blend(smallst[:], alt[:])

        def ph_Mdiff(W):
            with S.scope():
                C.sq = [S.sbuf("sq%d" % i, [P, 512], F32) for i in range(2)]
                C.lnt = C.sq[0]
                onesb = S.sbuf("onesb", [P, P], BF16)
                S.memset(onesb[:], 1.0)
                Q = [S.sbuf("Q%d" % h, [P, L], BF16) for h in range(2)]
                QC = [S.sbuf("QC%d" % h, [P, LC], BF16) for h in range(2)]
                K = [S.sbuf("K%d" % h, [P, LK], BF16) for h in range(2)]
                V = S.sbuf("V", [P, NKC, 256], BF16)
                lam = S.sbuf("lamt", [P, 4, 64], F32)
                S.dma("sp", lam[:], W.lam[:].rearrange("p (a b) -> p a b", b=64))
                nw = S.sbuf("nwt", [P, 1], F32)
                li = S.sbuf("lit", [P, 1], F32)
                S.dma("sp", nw[:], W.dnw[:])
                S.dma("sp", li[:], W.li[:])
                pr = S.sbuf("pr", [P, 2, 64], F32)
                s12 = S.sbuf("s12", [P, 2], F32)
                S.tt(pr[:, 0, :], lam[:, 0, :], lam[:, 1, :], ALU.mult)
                S.tt(pr[:, 1, :], lam[:, 2, :], lam[:, 3, :], ALU.mult)
                S.reduce(s12[:], pr[:], ALU.add)
                e12 = S.sbuf("e12", [P, 2], F32)
                S.act(e12[:], s12[:], AF.Exp)
                neglam = S.sbuf("neglam", [P, 1], F32)
                S.tt(neglam[:], e12[:, 1:2], e12[:, 0:1], ALU.subtract)
                S.tt(neglam[:], neglam[:], li[:], ALU.subtract)
                sc2 = S.sbuf("sc2", [P, 1], F32)
                S.ts(sc2[:], li[:], -1.0, ALU.mult, 1.0, ALU.add)
                S.tt(sc2[:], sc2[:], nw[:], ALU.mult)
                with S.scope():
                    cosb = S.sbuf("cosb", [P, L], F32)
                    sinb = S.sbuf("sinb", [P, L], F32)
                    rope_tables(S, nc, C, L, cosb, sinb)
                    with S.scope():
                        a = [S.sbuf("la%d" % i, [P, 1, 512], F32) for i in range(2)]
                        a2 = [S.sbuf("la2%d" % i, [P, 1, 512], F32) for i in range(2)]
                        b = [S.sbuf("lb%d" % i, [P, 1, 512], F32) for i in range(2)]
                        b2 = [S.sbuf("lb2%d" % i, [P, 1, 512], F32) for i in range(2)]
                        vst = [S.sbuf("vst%d" % i, [P, 4, 256], F32) for i in range(2)]
                        vs2 = [S.sbuf("vs2%d" % i, [P, 4, 256], F32) for i in range(2)]
                        i = 0
                        for h in range(2):
                            for (cq, csw, dst) in ((6 + h, 8 + h, Q[h]), (10 + h, 12 + h, K[h])):
                                for c0 in range(0, L, 512):
                                    ta, tb = a[i % 2], b[i % 2]
                                    load_fm("sp", ta[:], a2[i % 2][:], cq, 1, "lat", c0, 512, 0)
                                    load_fm("pool", tb[:], b2[i % 2][:], csw, 1, "lat", c0, 512, 0)
                                    i += 1
                                    S.tt(ta[:, 0, :], ta[:, 0, :], cosb[:, c0:c0 + 512], ALU.mult)
                                    S.tt(tb[:, 0, :], tb[:, 0, :], sinb[:, c0:c0 + 512], ALU.mult, e="pool")
                                    S.tt(dst[:, c0:c0 + 512], ta[:, 0, :], tb[:, 0, :], ALU.add)
                            ta = a[i % 2]
                            load_fm("sp", ta[:, :, :LC], a2[i % 2][:, :, :LC], 10 + h, 1, "ctx", 0, LC, 0)
                            i += 1
                            S.copy(K[h][:, L:LK], ta[:, 0, :LC])
                            ta = a[i % 2]
                            load_fm("sp", ta[:, :, :LC], a2[i % 2][:, :, :LC], 6 + h, 1, "ctx", 0, LC, 0)
                            i += 1
                            S.copy(QC[h][:], ta[:, 0, :LC])
                        vi = 0
                        for r in range(2):
                            for bq in range(NB):
                                t, t2 = vst[vi % 2], vs2[vi % 2]
                                vi += 1
                                for g, tgt in ((0, t), (1, t2)):
                                    S.dma("sp", tgt[:, 0:2, :], dsub(PKLG.ap()[bq, r, :, g * 784:g * 784 + 256].rearrange("(c p) e -> p c e", p=P)))
                                blend(t[:, 0:2, :], t2[:, 0:2, :])
                                kc = r * NLC + 2 * bq
                                S.copy(V[:, kc:kc + 2, :], t[:, 0:2, :], e="pool")
                            t, t2 = vst[vi % 2], vs2[vi % 2]
                            vi += 1
                            for g, tgt in ((0, t), (1, t2)):
                                S.dma("sp", tgt[:, 0, :], dsub(PKCG.ap()[r, 0:P, g * 784:g * 784 + 256]))
                            blend(t[:, 0, :], t2[:, 0, :])
                            S.copy(V[:, L // P + r, :], t[:, 0, :], e="pool")
                ps_s = [[S.psum("ps_s%d%d" % (j, i), [P, 512]) for i in range(2)] for j in range(2)]
                ps_o = [S.psum("ps_o%d" % j, [P, 512]) for j in range(2)]
                ps_z = [S.psum("ps_z%d" % j, [P, 512]) for j in range(2)]
                pt = [[S.sbuf("pt%d%d" % (j, i), [P, 512], BF16) for i in range(2)] for j in range(2)]
                rz = [S.sbuf("rz%d" % j, [P, 512], F32) for j in range(2)]
                t0_ = S.sbuf("t0", [P, 512], F32)
                t1_ = S.sbuf("t1", [P, 512], F32)
                rstd = S.sbuf("rstd", [P, 512], F32)
                oo = [S.sbuf("oo%d" % i, [P, 512], F32) for i in range(2)]
                jobs = []
                for h in range(2):
                    for q0 in range(0, L, 512):
                        s_, off = lat_rc(q0)
                        jobs.append((h, Q[h][:, q0:q0 + 512], 512, 0, NKC, [(mo_dst(2 + h, 3 + h, s_, off, 512)[0], 0, 512)]))
                    jobs.append((h, QC[h][:], LC, L // P, NKC, [(mo_dst(2 + h, 3 + h, r, NT, NCX)[0], r * NCX, NCX) for r in range(2)]))
                for ji, (h, qv, n, kc0, kc1, outs) in enumerate(jobs):
                    for kc in range(kc0, kc1):
                        bi = kc % 2
                        for j in range(2):
                            S.matmul(ps_s[j][bi][:, :n], K[h][j * 64:(j + 1) * 64, kc * P:(kc + 1) * P], qv[j * 64:(j + 1) * 64, :],
                                     start=True, stop=True)
                        for j in range(2):
                            S.act(pt[j][bi][:, :n], ps_s[j][bi][:, :n], AF.Exp, scale=0.125)
                        for j in range(2):
                            S.matmul(ps_o[j][:, :n], V[:, kc, h * P:(h + 1) * P], pt[j][bi][:, :n], start=(kc == kc0), stop=(kc == kc1 - 1))
                            S.matmul(ps_z[j][:, :n], onesb[:], pt[j][bi][:, :n], start=(kc == kc0), stop=(kc == kc1 - 1))
                    for j in range(2):
                        S.recip(rz[j][:, :n], ps_z[j][:, :n])
                    S.tt(t0_[:, :n], ps_o[0][:, :n], rz[0][:, :n], ALU.mult)
                    S.tt(t1_[:, :n], ps_o[1][:, :n], rz[1][:, :n], ALU.mult)
                    S.stt(t0_[:, :n], t1_[:, :n], neglam[:, 0:1], t0_[:, :n], ALU.mult, ALU.add)
                    rms_rstd(S, C, lambda c: t0_[:, :n], n, 1, P, ps_s[0][0], rstd[:, :n])
                    o = oo[ji % 2]
                    S.tt(t1_[:, :n], t0_[:, :n], rstd[:, :n], ALU.mult)
                    S.ts(o[:, :n], t1_[:, :n], sc2[:, 0:1], ALU.mult)
                    for (oap, o0, on) in outs:
                        S.dma("pool", dsub(oap), o[:, o0:o0 + on])

        def ph_Mssd(W):
            with S.scope():
                tri = {0: tri_mask(S, nc, "tri_f", "le"), 1: tri_mask(S, nc, "tri_b", "ge")}
                strict = {0: tri_mask(S, nc, "str_f", "gt"), 1: tri_mask(S, nc, "str_b", "lt")}
                cw = S.sbuf("cw", [P, 4, 5], F32)
                cb = S.sbuf("cb", [P, 4], F32)
                S.dma("sp", cw[:], W.scw[:])
                S.dma("sp", cb[:], W.scb[:])
                xs_tok = S.sbuf("xs_tok", [P, NCH, 256], F32)
                B_tok = S.sbuf("B_tok", [P, NCH, P], F32)
                BT = S.sbuf("BT", [P, LT], F32)
                CT = S.sbuf("CT", [P, LT], F32)
                dtv = S.sbuf("dtv", [P, NCH, 8], F32)
                aall = S.sbuf("aall", [P, NCH, 8], F32)
                dtb = S.sbuf("dtb", [P, 8], F32)
                aneg = S.sbuf("aneg", [P, 8], F32)
                dsk = S.sbuf("dsk", [P, 4], F32)
                with S.scope():
                    sm1 = S.sbuf("sm1", [P, NCH, 16], F32)
                    sm2 = S.sbuf("sm2", [P, NCH, 16], F32)
                    load_small(sm1, sm2)
                    S.copy(dtv[:], sm1[:, :, 8:16])
                S.dma("sp", dtb[:], W.sdtb[:])
                S.dma("sp", aneg[:], W.salog[:])
                S.dma("sp", dsk[:], W.sdsk[:])
                S.tt(dtv[:], dtv[:], View(dtb, dtb.t[:].rearrange("p (o e) -> p o e", o=1).to_broadcast([P, NCH, 8])), ALU.add)
                S.act(dtv[:], dtv[:], AF.Exp)
                S.act(dtv[:], dtv[:], AF.Ln, bias=C.ones[:, 0:1])
                S.act(aneg[:], aneg[:], AF.Exp)
                S.ts(aneg[:], aneg[:], -1.0, ALU.mult)
                S.tt(aall[:], dtv[:], View(aneg, aneg.t[:].rearrange("p (o e) -> p o e", o=1).to_broadcast([P, NCH, 8])), ALU.mult)
                ps_t = [S.psum("ps_t%d" % i, [P, 512]) for i in range(2)]
                with S.scope():
                    raw = [S.sbuf("raw%d" % i, [P, 4, 516], F32) for i in range(2)]
                    raw2 = [S.sbuf("rawb", [P, 4, 516], F32)] * 2
                    acc = [S.sbuf("acc%d" % i, [P, 512], F32) for i in range(2)]
                    xsT = [S.sbuf("xsT%d" % i, [P, 512], F32) for i in range(2)]
                    segs = [("ctx", 0, LC), ("lat", LC, L)]
                    ti = 0
                    for (seg, base, seglen) in segs:
                        for t0 in range(0, seglen, 512):
                            n = min(512, seglen - t0)
                            r = raw[ti % 2]
                            load_fm("sp" if ti % 2 else "pool", r[:, :, :n + 4], raw2[ti % 2][:, :, :n + 4], 14, 4, seg, t0, n, 2)
                            ti += 1
                            for c in range(4):
                                a = acc[c % 2]
                                S.ts(a[:, :n], r[:, c, 0:n], cw[:, c, 0:1], ALU.mult)
                                for j in range(1, 5):
                                    S.stt(a[:, :n], r[:, c, j:j + n], cw[:, c, j:j + 1], a[:, :n], ALU.mult, ALU.add)
                                g0 = base + t0
                                if c < 2:
                                    S.act(xsT[c][:, :n], a[:, :n], AF.Silu, bias=cb[:, c:c + 1])
                                elif c == 2:
                                    S.act(BT[:, g0:g0 + n], a[:, :n], AF.Silu, bias=cb[:, c:c + 1])
                                else:
                                    S.act(CT[:, g0:g0 + n], a[:, :n], AF.Silu, bias=cb[:, c:c + 1])
                            for bl in range(n // P):
                                gc = (base + t0) // P + bl
                                pt = ps_t[bl % 2]
                                S.transpose(pt[:, 0:P], xsT[0][:, bl * P:(bl + 1) * P], C.ident[:])
                                S.transpose(pt[:, P:2 * P], xsT[1][:, bl * P:(bl + 1) * P], C.ident[:])
                                S.transpose(pt[:, 2 * P:3 * P], BT[:, gc * P:(gc + 1) * P], C.ident[:])
                                S.copy(xs_tok[:, gc, :], pt[:, 0:2 * P], e="dve")
                                S.copy(B_tok[:, gc, :], pt[:, 2 * P:3 * P], e="dve")
                ps_arg = [S.psum("ps_arg%d" % i, [P, 512]) for i in range(2)]
                ps_cb = S.psum("ps_cb", [P, 512])
                ps_y = S.psum("ps_y", [P, 512])
                ps_st = S.psum("ps_st", [P, 512])
                ps_sm = S.psum("ps_sm", [P, 512])
                X = [S.sbuf("X%d" % i, [P, 4, P], F32) for i in range(2)]
                LTt = [S.sbuf("LT%d" % i, [P, 4, P], F32) for i in range(2)]
                CBm = S.sbuf("CBm", [P, P], F32)
                scT = [S.sbuf("scT%d" % i, [P, 4, P], F32) for i in range(2)]
                sm = S.sbuf("sm", [P, 8], F32)
                eacs = S.sbuf("eacs", [P, 4], F32)
                edec = S.sbuf("edec", [P, 4], F32)
                etot = S.sbuf("etot", [P, 4], F32)
                dif = S.sbuf("dif", [P, 4], F32)
                xdt = [S.sbuf("xdt%d" % i, [P, 4, 64], F32) for i in range(2)]
                xdtd = [S.sbuf("xdtd%d" % i, [P, 4, 64], F32) for i in range(2)]
                ST = S.sbuf("ST", [P, 4, 64], F32)
                yt = [S.sbuf("yt%d" % i, [P, 256], F32) for i in range(2)]
                y2 = [S.sbuf("y2%d" % i, [P, 256], F32) for i in range(2)]
                yfl = [S.sbuf("yfl%d" % i, [P, 256], F32) for i in range(2)]
                zt = [S.sbuf("zt%d" % i, [P, 256], F32) for i in range(2)]
                zt2 = [S.sbuf("ztb%d" % i, [P, 256], F32) for i in range(2)]
                oT = [S.sbuf("oT%d" % i, [P, 2, P], F32) for i in range(2)]
                yfb = [S.sub("yf%d" % c, OF.t[c]) for c in range(NCH)]

                def bc4(v):
                    return View(v.buf, v.ap.rearrange("p (h o) -> p h o", o=1).to_broadcast([P, 4, 64]))
                for dr in range(2):
                    order = list(range(NCH)) if dr == 0 else (list(range(NCC - 1, -1, -1)) + list(range(NCH - 1, NCC - 1, -1)))
                    S.memset(ST[:], 0.0)
                    for it, c in enumerate(order):
                        bi = it % 2
                        a4 = aall[:, c, dr * 4:(dr + 1) * 4]
                        for h in range(4):
                            S.ts(X[bi][:, h, :], strict[dr][:], aall[:, c, dr * 4 + h:dr * 4 + h + 1], ALU.mult, e=("dve" if h % 2 else "pool"))
                        for h in range(4):
                            S.matmul(ps_arg[bi][:, h * P:(h + 1) * P], X[bi][:, h, :], tri[dr][:])
                        S.act(LTt[bi][:].rearrange("p h l -> p (h l)"), ps_arg[bi][:], AF.Exp)
                        S.matmul(ps_cb[:, 0:P], BT[:, c * P:(c + 1) * P], CT[:, c * P:(c + 1) * P])
                        S.tt(CBm[:], ps_cb[:, 0:P], tri[dr][:], ALU.mult)
                        S.tt(scT[bi][:], LTt[bi][:], View(CBm, CBm.t[:].rearrange("p (o l) -> p o l", o=1).to_broadcast([P, 4, P])), ALU.mult)
                        S.matmul(ps_sm[:, 0:4], tri[dr][:], a4)
                        S.matmul(ps_sm[:, 4:8], C.ones[:], a4)
                        S.copy(sm[:], ps_sm[:, 0:8])
                        S.act(eacs[:], sm[:, 0:4], AF.Exp)
                        S.act(etot[:], sm[:, 4:8], AF.Exp)
                        S.tt(dif[:], sm[:, 4:8], sm[:, 0:4], ALU.subtract)
                        S.act(edec[:], dif[:], AF.Exp)
                        xv = xs_tok[:, c, :].rearrange("p (h d) -> p h d", h=4)
                        S.tt(xdt[bi][:], xv, bc4(dtv[:, c, dr * 4:(dr + 1) * 4]), ALU.mult, e="pool")
                        S.tt(xdtd[bi][:], xdt[bi][:], bc4(edec[:]), ALU.mult)
                        for h in range(4):
                            S.matmul(ps_y[:, h * 64:(h + 1) * 64], scT[bi][:, h, :], xdt[bi][:, h, :])
                        S.matmul(ps_y[:, 256:512], CT[:, c * P:(c + 1) * P], ST[:].rearrange("p h d -> p (h d)"))
                        y = yt[bi]
                        S.tt(y[:].rearrange("p (h d) -> p h d", h=4), ps_y[:, 256:512].rearrange("p (h d) -> p h d", h=4), bc4(eacs[:]), ALU.mult)
                        S.tt(y[:], y[:], ps_y[:, 0:256], ALU.add)
                        S.matmul(ps_st[:, 0:256], B_tok[:, c, :], xdtd[bi][:].rearrange("p h d -> p (h d)"))
                        S.tt(ST[:], ST[:], bc4(etot[:]), ALU.mult)
                        S.tt(ST[:].rearrange("p h d -> p (h d)"), ST[:].rearrange("p h d -> p (h d)"), ps_st[:, 0:256], ALU.add)
                        if dr == 0:
                            S.dma("pool", yfb[c][:], y[:])
                        else:
                            S.dma("sp", yfl[bi][:], yfb[c][:])
                            load_tok("sp", zt[bi][:], zt2[bi][:], c, 512, 256)
                            o = y2[bi]
                            S.tt(o[:].rearrange("p (h d) -> p h d", h=4), xv, bc4(dsk[:]), ALU.mult, e="pool")
                            S.tt(y[:], y[:], yfl[bi][:], ALU.add)
                            S.tt(o[:], o[:], y[:], ALU.add)
                            S.act(zt[bi][:], zt[bi][:], AF.Silu)
                            S.tt(o[:], o[:], zt[bi][:], ALU.mult)
                            S.transpose(ps_cb[:, P:2 * P], o[:, 0:P], C.ident[:])
                            S.transpose(ps_cb[:, 2 * P:3 * P], o[:, P:2 * P], C.ident[:])
                            S.copy(oT[bi][:].rearrange("p a b -> p (a b)"), ps_cb[:, P:3 * P])
                            s_, off = tokpos(c)
                            S.dma("pool", dsub(mo_dst(4, 6, s_, off, P).rearrange("c p t -> p c t")), oT[bi][:])

        def ph_Mgdn(W):
            with S.scope():
                M = {k: tri_mask(S, nc, "m_" + k, k, blk=64) for k in ("le", "ge", "gt", "lt")}
                halfA = S.sbuf("halfA", [P, P], F32)
                halfB = S.sbuf("halfB", [P, P], F32)
                S.memset(halfA[:], 0.0)
                S.memset(halfB[:], 0.0)
                S.memset(halfA[0:64, :], 1.0)
                S.memset(halfB[64:128, :], 1.0)
                cw = S.sbuf("cw", [P, 6, 5], F32)
                S.dma("sp", cw[:], W.gcw[:])
                nw = S.sbuf("nw", [P, P], F32)
                S.dma("sp", nw[:], W.gnw[:])
                gall = S.sbuf("gall", [P, NCH, 4], F32)
                ball = S.sbuf("ball", [P, NCH, 4], F32)
                negb = S.sbuf("negb", [P, NCH, 4], F32)
                aneg = S.sbuf("aneg", [P, 4], F32)
                dtb = S.sbuf("dtb", [P, 4], F32)
                with S.scope():
                    sm1 = S.sbuf("sm1", [P, NCH, 16], F32)
                    sm2 = S.sbuf("sm2", [P, NCH, 16], F32)
                    load_small(sm1, sm2)
                    S.copy(gall[:], sm1[:, :, 0:4])
                    S.copy(ball[:], sm1[:, :, 4:8])
                S.dma("sp", aneg[:], W.galog[:])
                S.dma("sp", dtb[:], W.gdtb[:])

                def bcn(v):
                    return View(v.buf, v.ap.rearrange("p (o e) -> p o e", o=1).to_broadcast([P, NCH, 4]))
                S.tt(gall[:], gall[:], bcn(dtb[:]), ALU.add)
                S.act(gall[:], gall[:], AF.Exp)
                S.act(gall[:], gall[:], AF.Ln, bias=C.ones[:, 0:1])
                S.act(aneg[:], aneg[:], AF.Exp)
                S.ts(aneg[:], aneg[:], -1.0, ALU.mult)
                S.tt(gall[:], gall[:], bcn(aneg[:]), ALU.mult)
                S.act(ball[:], ball[:], AF.Sigmoid)
                S.ts(negb[:], ball[:], -1.0, ALU.mult)
                BA = [S.psum("BA%d" % h, [P, 512]) for h in range(2)]
                B1 = [S.psum("B1%d" % h, [P, 512]) for h in range(2)]
                B2 = [S.psum("B2%d" % h, [P, 512]) for h in range(2)]
                B3 = [S.psum("B3%d" % h, [P, 512]) for h in range(2)]
                Wk = []
                for h in range(2):
                    Wn = NS()
                    for nm in ("X", "Dm", "Dv", "Ds", "kbg", "kdec", "vb", "vnew", "oq", "o", "of_", "zt", "zt2", "t1"):
                        setattr(Wn, nm, S.sbuf("%s%d" % (nm, h), [P, P], F32))
                    for nm in ("NA", "RA", "uw"):
                        setattr(Wn, nm, S.sbuf("%s%d" % (nm, h), [P, 2 * P], F32))
                    Wn.NR = [S.sbuf("NR%d%d" % (h, i), [P, 2 * P], F32) for i in range(2)]
                    Wn.Xc = [S.sbuf("Xc%d%d" % (h, i), [P, P], F32) for i in range(2)]
                    Wn.esm = S.sbuf("esm%d" % h, [P, 4], F32)
                    Wn.bg = S.sbuf("bg%d" % h, [P, 1], F32)
                    Wn.ss = S.sbuf("ss%d" % h, [P, 1], F32)
                    Wn.oo = [S.sbuf("oo%d%d" % (h, i), [P, P], F32) for i in range(2)]
                    Wn.oT = [S.sbuf("oT%d%d" % (h, i), [P, P], F32) for i in range(2)]
                    Wk.append(Wn)
                state = [S.sbuf("state%d" % h, [P, P], F32) for h in range(2)]
                raw = S.sbuf("raw", [P, 6, 516], F32)
                raw2 = S.sbuf("rawb", [P, 6, 516], F32)
                acc = [S.sbuf("acc%d" % i, [P, 512], F32) for i in range(2)]
                sqb = S.sbuf("sqb", [P, 512], F32)
                lnb = S.sbuf("lnb", [P, 512], F32)
                rsb = S.sbuf("rsb", [P, 512], F32)
                qkv = [S.sbuf("qkv%d" % i, [P, 6, 512], F32) for i in range(2)]
                ofb = [[S.sub("of%d_%d" % (c, h), OF.t[c][:, h * P:(h + 1) * P]) for h in range(2)] for c in range(NCH)]

                def prep(seg, t0, n, dst, q):
                    load_fm(q, raw[:, :, :n + 4], raw2[:, :, :n + 4], 0, 6, seg, t0, n, 2)
                    for c in range(6):
                        a = acc[c % 2]
                        S.ts(a[:, :n], raw[:, c, 0:n], cw[:, c, 0:1], ALU.mult)
                        for j in range(1, 5):
                            S.stt(a[:, :n], raw[:, c, j:j + n], cw[:, c, j:j + 1], a[:, :n], ALU.mult, ALU.add)
                        if c >= 4:
                            S.act(dst[:, c, :n], a[:, :n], AF.Silu)
                        else:
                            S.act(a[:, :n], a[:, :n], AF.Silu)
                            S.act(sqb[:, :n], a[:, :n], AF.Square)
                            pb = B3[c % 2]
                            S.matmul(pb[:, :n], C.ones[:], sqb[:, :n])
                            S.act(lnb[:, :n], pb[:, :n], AF.Ln, bias=C.eps[:, 0:1])
                            S.act(rsb[:, :n], lnb[:, :n], AF.Exp, scale=-0.5)
                            S.stt(dst[:, c, :n], a[:, :n], (128.0 ** -0.5) if c < 2 else 1.0, rsb[:, :n], ALU.mult, ALU.mult)

                def unit(hl, dr, gp, qv, kv, vv):
                    col = dr * 2 + hl
                    g = gall[:, gp, col:col + 1]
                    nb = negb[:, gp, col:col + 1]
                    bt = ball[:, gp, col:col + 1]
                    Wn = Wk[hl]
                    bA, b1, b2, b3 = BA[hl], B1[hl], B2[hl], B3[hl]
                    Tri, Xm, Val, SVal = (M["le"], M["gt"], M["ge"], M["gt"]) if dr == 0 else (M["ge"], M["lt"], M["le"], M["lt"])
                    S.ts(Wn.X[:], Xm[:], g, ALU.mult)
                    S.matmul(bA[:, 0:128], Tri[:], Wn.X[:])
                    S.matmul(bA[:, 128:129], Tri[:], g)
                    S.matmul(bA[:, 129:130], Xm[:], g)
                    S.matmul(bA[:, 130:131], halfA[:], g)
                    S.matmul(bA[:, 131:132], halfB[:], g)
                    S.matmul(b1[:, 0:128], kv, kv)
                    S.matmul(b1[:, 128:256], qv, kv)
                    S.transpose(bA[:, 256:384], kv, C.ident[:])
                    S.transpose(bA[:, 384:512], vv, C.ident[:])
                    yield
                    S.act(Wn.Dm[:], bA[:, 0:128], AF.Exp)
                    S.act(Wn.esm[:], bA[:, 128:132], AF.Exp)
                    S.tt(Wn.bg[:], Wn.esm[:, 0:1], bt, ALU.mult)
                    S.act(Wn.kdec[:], bA[:, 256:384], AF.Identity, scale=Wn.esm[:, 1:2])
                    S.act(Wn.vb[:], bA[:, 384:512], AF.Identity, scale=bt)
                    S.act(Wn.kbg[:], bA[:, 256:384], AF.Identity, scale=Wn.bg[:, 0:1])
                    S.tt(Wn.Dv[:], Wn.Dm[:], Val[:], ALU.mult)
                    S.tt(Wn.Ds[:], Wn.Dm[:], SVal[:], ALU.mult)
                    S.stt(Wn.NA[:, 0:128], b1[:, 0:128], nb, Wn.Ds[:], ALU.mult, ALU.mult)
                    S.tt(Wn.NA[:, 128:256], b1[:, 128:256], Wn.Dv[:], ALU.mult)
                    yield
                    S.transpose(b1[:, 256:384], Wn.NA[:, 0:128], C.ident[:])
                    S.transpose(b1[:, 384:512], Wn.NA[:, 128:256], C.ident[:])
                    S.copy(Wn.RA[:], b1[:, 256:512])
                    X = Wn.Xc[0]
                    S.tt(X[:], Wn.RA[:, 0:128], C.ident[:], ALU.add)
                    yield
                    Ncur = Wn.NA[:, 0:128]
                    Rcur = Wn.RA[:, 0:128]
                    for lev in range(5):
                        NR = Wn.NR[lev % 2]
                        S.matmul(b2[:, 0:128], Rcur, Ncur)
                        if lev < 4:
                            S.matmul(b2[:, 128:256], Ncur, Rcur)
                            S.copy(NR[:], b2[:, 0:256])
                        else:
                            S.copy(NR[:, 0:128], b2[:, 0:128])
                        yield
                        S.matmul(b2[:, 256:384], NR[:, 0:128], X[:])
                        Xn = Wn.Xc[(lev + 1) % 2]
                        S.tt(Xn[:], X[:], b2[:, 256:384], ALU.add)
                        X = Xn
                        Ncur = NR[:, 0:128]
                        Rcur = NR[:, 128:256]
                        yield
                    S.matmul(b3[:, 0:128], X[:], Wn.vb[:])
                    S.matmul(b3[:, 128:256], Wn.kbg[:], X[:])
                    S.copy(Wn.uw[:], b3[:, 0:256])
                    yield
                    blocks = [(0, 64), (64, 128)] if dr == 0 else [(64, 128), (0, 64)]
                    Sst = state[hl]
                    for bi, (r0, r1) in enumerate(blocks):
                        reg = b3[:, 256:512] if bi == 0 else b3[:, 0:256]
                        S.matmul(reg[:, 0:128], Wn.uw[:, 128:256], Sst[:])
                        S.matmul(reg[:, 128:256], qv, Sst[:])
                        S.tt(Wn.vnew[r0:r1, :], Wn.uw[r0:r1, 0:128], reg[r0:r1, 0:128], ALU.subtract)
                        S.ts(Wn.oq[r0:r1, :], reg[r0:r1, 128:256], Wn.esm[r0:r1, 0:1], ALU.mult)
                        yield
                        S.matmul(b1[:, 0:128], Wn.kdec[r0:r1, :], Wn.vnew[r0:r1, :])
                        egX = Wn.esm[:, 2:3] if r0 == 0 else Wn.esm[:, 3:4]
                        S.stt(Sst[:], Sst[:], egX, b1[:, 0:128], ALU.mult, ALU.add)
                        yield
                    S.matmul(b1[:, 128:256], Wn.RA[:, 128:256], Wn.vnew[:])
                    S.tt(Wn.o[:], Wn.oq[:], b1[:, 128:256], ALU.add)
                    if dr == 0:
                        S.dma("pool", ofb[gp][hl][:], Wn.o[:])
                    else:
                        S.dma("sp", Wn.of_[:], ofb[gp][hl][:])
                        load_tok("sp", Wn.zt[:], Wn.zt2[:], gp, 256 + hl * P, P)
                        S.tt(Wn.o[:], Wn.o[:], Wn.of_[:], ALU.add)
                        S.act(Wn.t1[:], Wn.o[:], AF.Square, accum_out=Wn.ss[:, 0:1])
                        yield
                        S.act(Wn.ss[:], Wn.ss[:], AF.Ln, scale=1.0 / 128.0, bias=C.eps[:, 0:1])
                        S.act(Wn.ss[:], Wn.ss[:], AF.Exp, scale=-0.5)
                        S.act(Wn.zt[:], Wn.zt[:], AF.Silu)
                        S.stt(Wn.t1[:], Wn.o[:], Wn.ss[:, 0:1], nw[:], ALU.mult, ALU.mult)
                        oo = Wn.oo[gp % 2]
                        S.tt(oo[:], Wn.t1[:], Wn.zt[:], ALU.mult)
                        S.transpose(b1[:, 256:384], oo[:], C.ident[:])
                        oT = Wn.oT[gp % 2]
                        S.copy(oT[:], b1[:, 256:384])
                        s_, off = tokpos(gp)
                        S.dma("pool", dsub(mo_dst(hl, hl + 1, s_, off, P)[0]), oT[:])
                    yield

                for dr in range(2):
                    for h in range(2):
                        S.memset(state[h][:], 0.0)
                    segs = [("ctx", 0, LC), ("lat", LC, L)]
                    tl = []
                    for (seg, base, seglen) in segs:
                        tt_ = [(seg, base, t0, min(512, seglen - t0)) for t0 in range(0, seglen, 512)]
                        if dr == 1:
                            tt_ = tt_[::-1]
                        tl += tt_
                    for ti, (seg, base, t0, n) in enumerate(tl):
                        dst = qkv[ti % 2]
                        prep(seg, t0, n, dst, "sp" if ti % 2 else "pool")
                        prs = list(range(n // P))
                        if dr == 1:
                            prs = prs[::-1]
                        for pi in prs:
                            gp = (base + t0) // P + pi
                            sl = slice(pi * P, (pi + 1) * P)
                            gens = [unit(h, dr, gp, dst[:, 0 + h, sl], dst[:, 2 + h, sl], dst[:, 4 + h, sl]) for h in range(2)]
                            alive = [True, True]
                            while any(alive):
                                for h in range(2):
                                    if alive[h]:
                                        try:
                                            next(gens[h])
                                        except StopIteration:
                                            alive[h] = False

        def ph_R2(W, x1t, xout):
            with S.scope():
                alloc_small(S, C)
                PS = mk_ps()
                mods = S.sbuf("mods", [P, 48, 2], F32)
                ngt = S.sbuf("ngt", [P, 6, KC], F32)
                S.dma("sp", ngt[:], W.ng[:].rearrange("p (m c) -> p m c", c=KC))
                snt = S.sbuf("snt", [P, 4], F32)
                S.dma("sp", snt[:], W.snw[:])
                with S.scope():
                    stg = [S.sbuf("stgm%d" % i, [P, 6 * D], F32) for i in range(KC)]
                    compute_mods(S, C, cv, W.wada2, W.bada2, 6, stg, PS.g[0], mods)
                A2 = S.sbuf("A2", [P, KC, 2], F32)
                G3 = S.sbuf("G3", [P, KC, 2], F32)
                A4 = S.sbuf("A4", [P, KC, 2], F32)
                G5 = S.sbuf("G5", [P, KC, 2], F32)
                S.stt(A2[:], mods[:, 8:16, :], 1.0, bc(ngt[:, 2, :]), ALU.add, ALU.mult)
                S.tt(G3[:], mods[:, 16:24, :], bc(ngt[:, 3, :]), ALU.mult)
                S.stt(A4[:], mods[:, 32:40, :], 1.0, bc(ngt[:, 4, :]), ALU.add, ALU.mult)
                S.stt(G5[:], mods[:, 40:48, :], 0.5, bc(ngt[:, 5, :]), ALU.mult, ALU.mult)
                B2 = mods[:, 0:8, :]
                B4 = mods[:, 24:32, :]
                x2t = [S.sub("x2t%d" % j, X2.t[:, :, s0:s0 + n]) for j, (s0, n, col) in enumerate(tiles)]
                with S.scope():
                    wgb = S.sbuf("wgb", [P, KC, 3 * D], BF16)
                    wbb = S.sbuf("wbb", [P, 12, D], BF16)
                    wob = S.sbuf("wob", [P, KC, D], BF16)
                    with S.scope():
                        stages = [S.sbuf("wst%d" % i, [P, 2048], F32) for i in range(3)]
                        load_w(S, W.wg, wgb, D, 3 * D, stages)
                        load_w(S, W.wb, wbb, 1536, D, stages)
                        load_w(S, W.wo, wob, D, D, stages)
                    xt = S.sbuf("xm", [P, KC, 512], F32)
                    h = S.sbuf("hm", [P, KC, 512], BF16)
                    ost = S.sbuf("ostg", [P, 4, 512], F32)
                    ost2 = S.sbuf("ostg2", [P, 4, 512], F32)
                    ob16 = S.sbuf("ob16", [P, 12, 512], BF16)
                    yacc = S.sbuf("yacc", [P, 512], F32)
                    ybf = S.sbuf("ybf", [P, KC, 512], BF16)
                    yy = S.sbuf("yy", [P, KC, 512], F32)
                    gt = [S.sbuf("gt%d" % i, [P, 512], F32) for i in range(2)]
                    for j, (s0, n, col) in enumerate(tiles):
                        S.dma("sp", xt[:, :, :n], x1t[j][:])
                        norm_mod(S, C, xt, n, A2[:], B2, col, h, PS.ss)
                        for br in range(3):
                            for sc_, tgt in ((0, ost), (1, ost2)):
                                for r in range(2):
                                    if col == 0:
                                        sap = MOLG.ap()[2 * br:2 * br + 2, sc_, r, :, s0:s0 + n]
                                    else:
                                        sap = MOCG.ap()[r, 2 * br:2 * br + 2, sc_, :, 0:n]
                                    S.dma("sp" if r else "pool", tgt[:, 2 * r:2 * r + 2, :n], dsub(sap.rearrange("c p t -> p c t")))
                            blend(ost[:, :, :n], ost2[:, :, :n])
                            if br < 2:
                                S.copy(ob16[:, br * 4:(br + 1) * 4, :n], ost[:, :, :n], e="pool")
                            else:
                                rms_rstd(S, C, lambda c: ost[:, c, :n], n, 4, 512, PS.ss2, C.rstd2[:, :n])
                                for c in range(4):
                                    t = C.tmp[c % 2]
                                    S.tt(t[:, :n], ost[:, c, :n], C.rstd2[:, :n], ALU.mult)
                                    S.act(ob16[:, 8 + c, :n], t[:, :n], AF.Copy, scale=snt[:, c:c + 1])
                        for d in range(KC):
                            for br in range(3):
                                pg = PS.g[br % 2]
                                pu = PS.u[br % 2]
                                cg = br * KC + d
                                for k in range(KC):
                                    S.matmul(pg[:, :n], wgb[:, k, cg * P:(cg + 1) * P], h[:, k, :n], start=(k == 0), stop=(k == KC - 1))
                                for k in range(4):
                                    S.matmul(pu[:, :n], wbb[:, br * 4 + k, d * P:(d + 1) * P], ob16[:, br * 4 + k, :n], start=(k == 0), stop=(k == 3))
                                g = gt[br % 2]
                                S.act(g[:, :n], pg[:, :n], AF.Sigmoid)
                                if br == 0:
                                    S.tt(yacc[:, :n], g[:, :n], pu[:, :n], ALU.mult)
                                else:
                                    t = C.tmp[br % 2]
                                    S.tt(t[:, :n], g[:, :n], pu[:, :n], ALU.mult)
                                    if br == 1:
                                        S.tt(yacc[:, :n], yacc[:, :n], t[:, :n], ALU.add)
                                    else:
                                        S.tt(ybf[:, d, :n], yacc[:, :n], t[:, :n], ALU.add)
                        for d in range(KC):
                            py = PS.y[d % 2]
                            for k in range(KC):
                                S.matmul(py[:, :n], wob[:, k, d * P:(d + 1) * P], ybf[:, k, :n], start=(k == 0), stop=(k == KC - 1))
                            S.copy(yy[:, d, :n], py[:, :n], e="dve")
                        rms_rstd(S, C, lambda c: yy[:, c, :n], n, KC, D, PS.ss2, C.rstd2[:, :n])
                        for c in range(KC):
                            t = C.tmp[c % 2]
                            S.tt(t[:, :n], yy[:, c, :n], C.rstd2[:, :n], ALU.mult)
                            S.stt(xt[:, c, :n], t[:, :n], G3[:, c, col:col + 1], xt[:, c, :n], ALU.mult, ALU.add)
                        S.dma("pool", x2t[j][:], xt[:, :, :n])
                with S.scope():
                    w1b = S.sbuf("w1b", [P, KC, 2 * DFF], BF16)
                    w2b = S.sbuf("w2b", [P, FC, D], BF16)
                    with S.scope():
                        stages = [S.sbuf("wst%d" % i, [P, 2048], F32) for i in range(3)]
                        load_w(S, W.w1b, w1b, D, 2 * DFF, stages)
                        load_w(S, W.w2b, w2b, DFF, D, stages)
                    alloc_ffn_work(S, C)
                    ffn_sweep(S, C, tiles, lambda j: x2t[j][:], lambda j: xout[j][:], w1b, w2b, A4[:], B4, G5[:], PS)

        xin = [S.sub("xin%d" % j, xT.t[:, :, s0:s0 + n]) for j, (s0, n, col) in enumerate(tiles)]
        for i in range(depth):
            W = Wl[i]
            x1t = [S.sub("x1t%d_%d" % (i, j), X1.t[:, :, s0:s0 + n]) for j, (s0, n, col) in enumerate(tiles)]
            dbg = "Z"
            ph_R1(W, xin, x1t)
            if dbg >= "B":
                gather_P()
            if dbg >= "C":
                ph_Mdiff(W)
            if dbg >= "D":
                ph_Mssd(W)
            if dbg >= "E":
                ph_Mgdn(W)
            if dbg >= "F":
                gather_M()
            last = i == depth - 1
            dstT = yT if last else Xs
            xout = [S.sub("xo%d_%d" % (i, j), dstT.t[:, :, s0:s0 + n]) for j, (s0, n, col) in enumerate(tiles)]
            ph_R2(W, x1t, xout)
            xin = xout
        S.barrier()
    return nc


def fm(a):
    T, F = a.shape
    return np.ascontiguousarray(a.T.reshape(F // P, P, T).transpose(1, 0, 2))

def unfm(a):
    p, C, T = a.shape
    return np.ascontiguousarray(a.transpose(2, 1, 0).reshape(T, C * P))

def vec_fm(v):
    return np.ascontiguousarray(v.reshape(-1, P).T)

def r1_cols():
    sw = np.arange(512) ^ 1
    cols = []
    cols += list(range(0, 2048))
    cols += list(range(2064, 2576))
    cols += list(2064 + sw)
    cols += list(range(2576, 3088))
    cols += list(2576 + sw)
    cols += list(range(3088, 3600))
    cols += list(range(3600, 5136))
    cols += list(range(2048, 2064)) + list(range(5136, 5152)) + [0] * 96
    cols = np.array(cols)
    assert len(cols) == 49 * 128
    return cols

def r1_inputs(inp, i, core, NT, NCX, xcur, ctxcur):
    b, s = core // 2, core % 2
    tok = np.concatenate([xcur[b, s * NT:(s + 1) * NT], ctxcur[b, s * NCX:(s + 1) * NCX]], 0)
    cv = np.stack([inp["c"][b], inp["c_ctx"]], -1)
    cv = np.ascontiguousarray(cv.reshape(8, P, 2).transpose(1, 0, 2))
    return {
        "xT": fm(tok),
        "cv": cv,
        "wada": np.ascontiguousarray(inp["w_ada"][i][:, :5 * 1024]),
        "bada": vec_fm(inp["b_ada"][i][:5 * 1024]),
        "ng": np.ascontiguousarray(inp["norm_g"][i].reshape(6, 8, P).transpose(2, 0, 1).reshape(P, 48)),
        "w1": np.ascontiguousarray(inp["w_ffn_in"][i, 0]),
        "w2": np.ascontiguousarray(inp["w_ffn_out"][i, 0]),
        "win": np.ascontiguousarray(inp["w_in"][i][:, r1_cols()]),
    }

def r2_inputs(inp, i, core, x1T, oaT, obT, ocT):
    b = core // 2
    cv = np.stack([inp["c"][b], inp["c_ctx"]], -1)
    cv = np.ascontiguousarray(cv.reshape(8, P, 2).transpose(1, 0, 2))
    return {
        "x1T": x1T, "oaT": oaT, "obT": obT, "ocT": ocT, "cv": cv,
        "wada": np.ascontiguousarray(inp["w_ada"][i][:, 3 * 1024:]),
        "bada": vec_fm(inp["b_ada"][i][3 * 1024:]),
        "ng": np.ascontiguousarray(inp["norm_g"][i].reshape(6, 8, P).transpose(2, 0, 1).reshape(P, 48)),
        "snw": vec_fm(inp["ssd_norm_w"][i]),
        "wg": np.ascontiguousarray(inp["w_in"][i][:, 5152:8224]),
        "wb": np.ascontiguousarray(inp["w_branch"][i].reshape(1536, 1024)),
        "wo": np.ascontiguousarray(inp["w_out"][i]),
        "w1": np.ascontiguousarray(inp["w_ffn_in"][i, 1]),
        "w2": np.ascontiguousarray(inp["w_ffn_out"][i, 1]),
    }

def split_P(PT_cores, NT, NCX, b):
    a0, a1 = PT_cores[2 * b], PT_cores[2 * b + 1]
    lat = np.concatenate([a0[:, :, :NT], a1[:, :, :NT]], 2)
    cx = np.concatenate([a0[:, :, NT:], a1[:, :, NT:]], 2)
    return lat, cx

def mdiff_inputs(inp, i, core, lat, cx):
    b, hh = core // 2, core % 2
    hs = [2 * hh, 2 * hh + 1]
    L = lat.shape[2]; LC = cx.shape[2]
    qT = lat[[16 + h for h in hs]]
    qsT = lat[[20 + h for h in hs]]
    kT = np.concatenate([lat[[24 + h for h in hs]], cx[[24 + h for h in hs]]], 2)
    ksT = lat[[28 + h for h in hs]]
    qcT = cx[[16 + h for h in hs]]
    v = np.concatenate([lat[[32 + h for h in hs]], cx[[32 + h for h in hs]]], 2)
    LK = L + LC
    v = v.transpose(2, 0, 1).reshape(LK // P, P, 256).transpose(1, 0, 2)
    lam_init = 0.8 - 0.6 * np.exp(-0.3 * i)
    return {"qT": np.ascontiguousarray(qT), "qsT": np.ascontiguousarray(qsT), "kT": np.ascontiguousarray(kT),
            "ksT": np.ascontiguousarray(ksT), "qcT": np.ascontiguousarray(qcT), "v": np.ascontiguousarray(v),
            "lam": np.ascontiguousarray(np.broadcast_to(inp["diff_lambda"][i].reshape(1, 256), (P, 256))),
            "nw": np.ascontiguousarray(inp["diff_norm_w"][i].reshape(P, 1)),
            "li": np.full((P, 1), lam_init, np.float32)}

def pad2(a):
    return np.pad(a, ((0, 0), (0, 0), (2, 2)))

def tokmaj(a):
    Cc, p, T = a.shape
    return np.ascontiguousarray(a.transpose(2, 0, 1).reshape(T // P, P, Cc * P).transpose(1, 0, 2))

def mssd_inputs(inp, i, core, lat, cx):
    b, g = core // 2, core % 2
    ch = [40 + 2 * g, 41 + 2 * g, 44 + g, 46 + g]
    wcols = np.concatenate([np.arange(256 * g, 256 * g + 256), 512 + 128 * g + np.arange(128), 768 + 128 * g + np.arange(128)])
    cwv = inp["ssd_conv_w"][i][:, wcols]
    cbv = inp["ssd_conv_b"][i][wcols]
    zc = [36 + 2 * g, 37 + 2 * g]
    z = np.concatenate([tokmaj(cx[zc]), tokmaj(lat[zc])], 1)
    rows = [16 + d * 8 + 4 * g + h for d in range(2) for h in range(4)]
    dtl = lat[48][rows]; dtc = cx[48][rows]
    dt = np.concatenate([dtc, dtl], 1)
    LT = dt.shape[1]
    dt = np.ascontiguousarray(dt.T.reshape(LT // P, P, 8).transpose(1, 0, 2))
    hsel = [4 * g + h for h in range(4)]
    return {"xbcl": np.ascontiguousarray(pad2(lat[ch])), "xbcc": np.ascontiguousarray(pad2(cx[ch])),
            "cw": np.ascontiguousarray(cwv.T.reshape(4, P, 5).transpose(1, 0, 2)),
            "cb": np.ascontiguousarray(cbv.reshape(4, P).T),
            "z": z, "dt": dt,
            "dtb": np.ascontiguousarray(np.broadcast_to(inp["ssd_dt_bias"][i][:, hsel].reshape(1, 8), (P, 8))),
            "alog": np.ascontiguousarray(np.broadcast_to(inp["ssd_a_log"][i][:, hsel].reshape(1, 8), (P, 8))),
            "dskip": np.ascontiguousarray(np.broadcast_to(inp["ssd_d"][i][hsel].reshape(1, 4), (P, 4)))}

def mgdn_inputs(inp, i, core, lat, cx):
    b, hh = core // 2, core % 2
    hs = [2 * hh, 2 * hh + 1]
    ch = [0 + hs[0], 0 + hs[1], 4 + hs[0], 4 + hs[1], 8 + hs[0], 8 + hs[1]]
    wcols = np.concatenate([off + h * 128 + np.arange(128) for off in (0, 512, 1024) for h in hs])
    cwv = inp["gdn_conv_w"][i][:, wcols]
    zc = [12 + hs[0], 12 + hs[1]]
    z = np.concatenate([tokmaj(cx[zc]), tokmaj(lat[zc])], 1)
    def small(rows):
        v = np.concatenate([cx[48][rows], lat[48][rows]], 1)
        LT = v.shape[1]
        return np.ascontiguousarray(v.T.reshape(LT // P, P, 4).transpose(1, 0, 2))
    arows = [d * 4 + h for d in range(2) for h in hs]
    brows = [8 + d * 4 + h for d in range(2) for h in hs]
    return {"qkvl": np.ascontiguousarray(pad2(lat[ch])), "qkvc": np.ascontiguousarray(pad2(cx[ch])),
            "cw": np.ascontiguousarray(cwv.T.reshape(6, P, 5).transpose(1, 0, 2)),
            "z": z, "araw": small(arows), "braw": small(brows),
            "alog": np.ascontiguousarray(np.broadcast_to(inp["gdn_a_log"][i][:, hs].reshape(1, 4), (P, 4))),
            "dtb": np.ascontiguousarray(np.broadcast_to(inp["gdn_dt_bias"][i][:, hs].reshape(1, 4), (P, 4))),
            "nw": np.ascontiguousarray(np.broadcast_to(inp["gdn_norm_w"][i].reshape(1, P), (P, P)))}


def fused_cols():
    sw = np.arange(128) ^ 1
    ar = np.arange(128)
    fmc, tkc = [], []
    for g in (0, 1):
        hs = [2 * g, 2 * g + 1]
        for off in (0, 512, 1024):
            for h in hs:
                fmc += list(off + h * 128 + ar)
        for h in hs:
            fmc += list(2064 + h * 128 + ar)
        for h in hs:
            fmc += list(2064 + h * 128 + sw)
        for h in hs:
            fmc += list(2576 + h * 128 + ar)
        for h in hs:
            fmc += list(2576 + h * 128 + sw)
        fmc += list(4112 + g * 256 + np.arange(256))
        fmc += list(4112 + 512 + g * 128 + ar)
        fmc += list(4112 + 768 + g * 128 + ar)
    for g in (0, 1):
        hs = [2 * g, 2 * g + 1]
        for h in hs:
            tkc += list(3088 + h * 128 + ar)
        for h in hs:
            tkc += list(1536 + h * 128 + ar)
        tkc += list(3600 + g * 256 + np.arange(256))
        tkc += [2048 + d * 4 + h for d in range(2) for h in hs]
        tkc += [2056 + d * 4 + h for d in range(2) for h in hs]
        tkc += [5136 + d * 8 + 4 * g + h for d in range(2) for h in range(4)]
    cols = np.array(fmc + tkc)
    assert len(cols) == 36 * 128 + 1568
    return cols


def fused_inputs(inp, core, NT, NCX, depth=2):
    b, hh = core // 2, core % 2
    s = hh
    tok = np.concatenate([inp["x"][b, s * NT:(s + 1) * NT], inp["ctx"][b, s * NCX:(s + 1) * NCX]], 0)
    cv = np.stack([inp["c"][b], inp["c_ctx"]], -1)
    cv = np.ascontiguousarray(cv.reshape(8, P, 2).transpose(1, 0, 2))
    d = {"xT": fm(tok), "cv": cv, "selv": np.ascontiguousarray(np.broadcast_to(np.array([[1.0 - hh, float(hh)]], np.float32), (P, 2)))}
    cols = fused_cols()
    hs = [2 * hh, 2 * hh + 1]
    g = hh
    for i in range(depth):
        sfx = "_%d" % i
        d["wada1" + sfx] = np.ascontiguousarray(inp["w_ada"][i][:, :5 * 1024])
        d["bada1" + sfx] = vec_fm(inp["b_ada"][i][:5 * 1024])
        d["wada2" + sfx] = np.ascontiguousarray(inp["w_ada"][i][:, 3 * 1024:])
        d["bada2" + sfx] = vec_fm(inp["b_ada"][i][3 * 1024:])
        d["ng" + sfx] = np.ascontiguousarray(inp["norm_g"][i].reshape(6, 8, P).transpose(2, 0, 1).reshape(P, 48))
        d["w1a" + sfx] = np.ascontiguousarray(inp["w_ffn_in"][i, 0])
        d["w2a" + sfx] = np.ascontiguousarray(inp["w_ffn_out"][i, 0])
        d["w1b" + sfx] = np.ascontiguousarray(inp["w_ffn_in"][i, 1])
        d["w2b" + sfx] = np.ascontiguousarray(inp["w_ffn_out"][i, 1])
        d["win" + sfx] = np.ascontiguousarray(inp["w_in"][i][:, cols])
        d["wg" + sfx] = np.ascontiguousarray(inp["w_in"][i][:, 5152:8224])
        d["wb" + sfx] = np.ascontiguousarray(inp["w_branch"][i].reshape(1536, 1024))
        d["wo" + sfx] = np.ascontiguousarray(inp["w_out"][i])
        d["snw" + sfx] = vec_fm(inp["ssd_norm_w"][i])
        lam_init = 0.8 - 0.6 * np.exp(-0.3 * i)
        d["lam" + sfx] = np.ascontiguousarray(np.broadcast_to(inp["diff_lambda"][i].reshape(1, 256), (P, 256)))
        d["dnw" + sfx] = np.ascontiguousarray(inp["diff_norm_w"][i].reshape(P, 1))
        d["li" + sfx] = np.full((P, 1), lam_init, np.float32)
        wcols = np.concatenate([np.arange(256 * g, 256 * g + 256), 512 + 128 * g + np.arange(128), 768 + 128 * g + np.arange(128)])
        d["scw" + sfx] = np.ascontiguousarray(inp["ssd_conv_w"][i][:, wcols].T.reshape(4, P, 5).transpose(1, 0, 2))
        d["scb" + sfx] = np.ascontiguousarray(inp["ssd_conv_b"][i][wcols].reshape(4, P).T)
        hsel = [4 * g + h for h in range(4)]
        d["sdtb" + sfx] = np.ascontiguousarray(np.broadcast_to(inp["ssd_dt_bias"][i][:, hsel].reshape(1, 8), (P, 8)))
        d["salog" + sfx] = np.ascontiguousarray(np.broadcast_to(inp["ssd_a_log"][i][:, hsel].reshape(1, 8), (P, 8)))
        d["sdsk" + sfx] = np.ascontiguousarray(np.broadcast_to(inp["ssd_d"][i][hsel].reshape(1, 4), (P, 4)))
        gcols = np.concatenate([off + h * 128 + np.arange(128) for off in (0, 512, 1024) for h in hs])
        d["gcw" + sfx] = np.ascontiguousarray(inp["gdn_conv_w"][i][:, gcols].T.reshape(6, P, 5).transpose(1, 0, 2))
        d["galog" + sfx] = np.ascontiguousarray(np.broadcast_to(inp["gdn_a_log"][i][:, hs].reshape(1, 4), (P, 4)))
        d["gdtb" + sfx] = np.ascontiguousarray(np.broadcast_to(inp["gdn_dt_bias"][i][:, hs].reshape(1, 4), (P, 4)))
        d["gnw" + sfx] = np.ascontiguousarray(np.broadcast_to(inp["gdn_norm_w"][i].reshape(1, P), (P, P)))
    return d


from concourse.bass_utils import run_bass_kernel_spmd


def kernel(**inp):
    inp = {k: np.ascontiguousarray(np.asarray(v), dtype=np.float32) for k, v in inp.items()}
    B, L, Dm = inp["x"].shape
    LC = inp["ctx"].shape[1]
    NT, NCX = L // 2, LC // 2
    nc = build_fused(L, LC, 2)
    ims = [fused_inputs(inp, c, NT, NCX, 2) for c in range(8)]
    res = run_bass_kernel_spmd(nc, ims, core_ids=list(range(8))).results
    out = np.empty((B, L, Dm), np.float32)
    for c in range(8):
        b, s = c // 2, c % 2
        out[b, s * NT:(s + 1) * NT] = unfm(res[c]["yT"])[:NT]
    return out
```

```python
import numpy as np
from contextlib import ExitStack, contextmanager
import concourse.bass as bass
import concourse.mybir as mybir

F32 = mybir.dt.float32
BF16 = mybir.dt.bfloat16
AF = mybir.ActivationFunctionType
ALU = mybir.AluOpType
AX = mybir.AxisListType


class Buf:
    __slots__ = ("name", "t", "w", "r", "dsem", "dcnt", "space", "dkey")

    def __init__(self, name, t, space="sbuf"):
        self.name = name
        self.space = space
        self.t = t
        self.w = {}
        self.r = {}
        self.dsem = None
        self.dcnt = 0
        self.dkey = None

    def __getitem__(self, idx):
        return View(self, self.t[idx])


class View:
    __slots__ = ("buf", "ap")

    def __init__(self, buf, ap):
        self.buf = buf
        self.ap = ap

    def __getitem__(self, idx):
        return View(self.buf, self.ap[idx])

    def rearrange(self, *a, **k):
        return View(self.buf, self.ap.rearrange(*a, **k))

    def bitcast(self, *a, **k):
        return View(self.buf, self.ap.bitcast(*a, **k))

    def to_broadcast(self, *a, **k):
        return View(self.buf, self.ap.to_broadcast(*a, **k))


class Sched:
    def __init__(self, nc, stack):
        self.nc = nc
        self.stack = stack
        self.eng = {"pe": nc.tensor, "act": nc.scalar, "dve": nc.vector, "pool": nc.gpsimd, "sp": nc.sync}
        self.sem = {}
        self.cnt = {}
        self.seen = {}
        for e in self.eng:
            self.sem[e] = stack.enter_context(nc.semaphore("prog_" + e))
            self.cnt[e] = 0
            self.seen[e] = {}
        self.semobj = {e: self.sem[e] for e in self.eng}
        self.nbuf = 0
        self.ninst = 0
        self.root = stack
        self.dbufs = []
        self.sempool = []
        self.scope_bufs = [[]]
        self.nsem = 0

    def sbuf(self, name, shape, dt):
        self.nbuf += 1
        name = "%s_%d" % (name, self.nbuf)
        t = self.stack.enter_context(self.nc.sbuf_tensor(name, list(shape), dt))
        b = Buf(name, t)
        self.scope_bufs[-1].append(b)
        return b

    def psum(self, name, shape, dt=F32):
        self.nbuf += 1
        name = "%s_%d" % (name, self.nbuf)
        t = self.stack.enter_context(self.nc.psum_tensor(name, list(shape), dt))
        return Buf(name, t, "psum")

    def dram(self, name, shape, dt, kind="Internal"):
        t = self.nc.dram_tensor(name, list(shape), dt, kind=kind)
        return Buf(name, t.ap(), "dram")

    def sub(self, name, ap, space="dram"):
        return Buf(name, ap, space)

    def barrier(self):
        deps = {e: self.cnt[e] for e in self.eng if self.cnt[e] > 0}
        for b in self.dbufs:
            if deps.get(b.dkey, 0) < b.dcnt:
                deps[b.dkey] = b.dcnt
        for e in self.eng:
            self._need(e, dict(deps))

    @contextmanager
    def scope(self):
        old = self.stack
        self.scope_bufs.append([])
        with ExitStack() as st:
            self.stack = st
            yield
            self.barrier()
        self.stack = old
        dead = self.scope_bufs.pop()
        for b in dead:
            if b.dsem is not None:
                self.sempool.append((b.dsem, b.dcnt, b.dkey))
        deadids = set(id(b) for b in dead)
        self.dbufs = [b for b in self.dbufs if id(b) not in deadids] + [b for b in dead if b.dsem is not None][:0]
        self._dead_keep = getattr(self, "_dead_keep", []) + dead

    def _need(self, e, deps):
        seen = self.seen[e]
        for k, v in deps.items():
            if e == "pe" and k == "pe":
                continue
            if seen.get(k, 0) < v:
                seen[k] = v
                self.eng[e].wait_ge(self.semobj[k], v)

    def _collect(self, reads, writes):
        deps = {}
        for v in reads:
            for k, val in v.buf.w.items():
                if deps.get(k, 0) < val:
                    deps[k] = val
        for v in writes:
            for d in (v.buf.w, v.buf.r):
                for k, val in d.items():
                    if deps.get(k, 0) < val:
                        deps[k] = val
        return deps

    def _mark(self, reads, writes, key, val):
        for v in reads:
            b = v.buf
            if b.r.get(key, 0) < val:
                b.r[key] = val
        for v in writes:
            b = v.buf
            b.w = {key: val}
            b.r = {}

    def op(self, e, fn, reads, writes):
        self._need(e, self._collect(reads, writes))
        ins = fn()
        self.cnt[e] += 1
        ins.then_inc(self.sem[e], 1)
        self._mark(reads, writes, e, self.cnt[e])
        self.ninst += 1
        return ins

    def dma(self, e, out, in_, sbuf_side=None, **kw):
        if sbuf_side is None:
            sbuf_side = out.buf if out.buf.space != "dram" else in_.buf
        b = sbuf_side
        if b.dsem is None:
            if self.sempool:
                b.dsem, b.dcnt, b.dkey = self.sempool.pop()
            else:
                self.nsem += 1
                b.dsem = self.root.enter_context(self.nc.semaphore("d_%d" % self.nsem))
                b.dkey = "d%d" % self.nsem
                self.semobj[b.dkey] = b.dsem
            self.dbufs.append(b)
        self._need(e, self._collect([in_], [out]))
        ins = self.eng[e].dma_start(out=out.ap, in_=in_.ap, **kw)
        b.dcnt += 16
        ins.then_inc(b.dsem, 16)
        self._mark([in_], [out], b.dkey, b.dcnt)
        self.ninst += 1
        return ins

    def wait_all(self, e, bufs):
        deps = {}
        for b in bufs:
            for d in (b.w, b.r):
                for k, val in d.items():
                    if deps.get(k, 0) < val:
                        deps[k] = val
        self._need(e, deps)

    def matmul(self, out, lhsT, rhs, start=True, stop=True, acc_reads=True):
        rd = [lhsT, rhs]
        return self.op("pe", lambda: self.nc.tensor.matmul(out.ap, lhsT.ap, rhs.ap, start=start, stop=stop),
                       rd, [out])

    def transpose(self, out, in_, ident):
        return self.op("pe", lambda: self.nc.tensor.transpose(out.ap, in_.ap, ident.ap), [in_, ident], [out])

    def act(self, out, in_, func, bias=None, scale=None, accum_out=None, e="act"):
        rd = [in_]
        kw = {}
        if bias is not None:
            if isinstance(bias, View):
                rd.append(bias)
                kw["bias"] = bias.ap
            else:
                kw["bias"] = bias
        if scale is not None:
            if isinstance(scale, View):
                rd.append(scale)
                kw["scale"] = scale.ap
            else:
                kw["scale"] = scale
        wr = [out]
        if accum_out is not None:
            wr.append(accum_out)
            kw["accum_out"] = accum_out.ap
        return self.op("act", lambda: self.nc.scalar.activation(out.ap, in_.ap, func, **kw), rd, wr)

    def _ve(self, e):
        return self.nc.vector if e == "dve" else self.nc.gpsimd

    def copy(self, out, in_, e="dve"):
        if e == "act":
            return self.op("act", lambda: self.nc.scalar.copy(out.ap, in_.ap), [in_], [out])
        return self.op(e, lambda: self._ve(e).tensor_copy(out.ap, in_.ap), [in_], [out])

    def tt(self, out, a, b, op, e="dve"):
        return self.op(e, lambda: self._ve(e).tensor_tensor(out.ap, a.ap, b.ap, op), [a, b], [out])

    def ts(self, out, a, s1, op0, s2=None, op1=None, accum_out=None, e="dve"):
        rd = [a]
        s1v = s1.ap if isinstance(s1, View) else s1
        s2v = s2.ap if isinstance(s2, View) else s2
        if isinstance(s1, View):
            rd.append(s1)
        if isinstance(s2, View):
            rd.append(s2)
        wr = [out]
        kw = {}
        if op1 is not None:
            kw["op1"] = op1
        if accum_out is not None:
            kw["accum_out"] = accum_out.ap
            wr.append(accum_out)
        if s2 is None and op1 is None and accum_out is None:
            return self.op(e, lambda: self._ve(e).tensor_single_scalar(out.ap, a.ap, s1v, op0), rd, wr)
        return self.op(e, lambda: self._ve(e).tensor_scalar(out.ap, a.ap, s1v, s2v, op0, **kw), rd, wr)

    def stt(self, out, a, s, b, op0, op1, e="dve"):
        rd = [a, b]
        sv = s.ap if isinstance(s, View) else s
        if isinstance(s, View):
            rd.append(s)
        return self.op(e, lambda: self._ve(e).scalar_tensor_tensor(out.ap, a.ap, sv, b.ap, op0, op1), rd, [out])

    def reduce(self, out, in_, op, axis=AX.X, e="dve"):
        return self.op(e, lambda: self._ve(e).tensor_reduce(out.ap, in_.ap, axis, op), [in_], [out])

    def memset(self, out, val, e="dve"):
        return self.op(e, lambda: self._ve(e).memset(out.ap, val), [], [out])

    def recip(self, out, in_):
        return self.op("dve", lambda: self.nc.vector.reciprocal(out.ap, in_.ap), [in_], [out])


P = 128
D = 1024
KC = 8
DFF = 2816
FC = 22
EPS = 1e-6


class NS:
    pass


def mk_consts(S, nc):
    C = NS()
    C.ones = S.sbuf("ones", [P, P], F32)
    S.memset(C.ones[:], 1.0)
    C.eps = S.sbuf("epsc", [P, 1], F32)
    S.memset(C.eps[:], EPS)
    C.ident = S.sbuf("ident", [P, P], F32)
    S.memset(C.ident[:], 1.0, e="pool")
    S.op("pool", lambda: nc.gpsimd.affine_select(C.ident.t[:], C.ident.t[:], [[-1, P]], ALU.is_equal, 0.0,
                                                 base=0, channel_multiplier=1), [C.ident[:]], [C.ident[:]])
    return C


def load_w(S, wd, dst, K, N, stages, blk=2048, col0=0):
    engs = ["pool", "dve", "act"]
    i = 0
    for k in range(K // P):
        for c0 in range(0, N, blk):
            w = min(blk, N - c0)
            st = stages[i % len(stages)]
            S.dma("sp", st[:, :w], wd[k * P:(k + 1) * P, col0 + c0:col0 + c0 + w])
            S.copy(dst[:, k, c0:c0 + w], st[:, :w], e=engs[i % 3])
            i += 1


def rms_rstd(S, C, src, n, nch, dim, ps, out):
    for c in range(nch):
        sq = C.sq[c % 2]
        S.act(sq[:, :n], src(c), AF.Square)
        S.matmul(ps[:, :n], C.ones[:], sq[:, :n], start=(c == 0), stop=(c == nch - 1))
    S.act(C.lnt[:, :n], ps[:, :n], AF.Ln, scale=1.0 / dim, bias=C.eps[:, 0:1])
    S.act(out, C.lnt[:, :n], AF.Exp, scale=-0.5)


def norm_mod(S, C, xt, n, A, B, col, h, ps):
    rms_rstd(S, C, lambda c: xt[:, c, :n], n, KC, D, ps, C.rstd[:, :n])
    for c in range(KC):
        t = C.tmp[c % 2]
        S.tt(t[:, :n], xt[:, c, :n], C.rstd[:, :n], ALU.mult)
        S.act(h[:, c, :n], t[:, :n], AF.Identity, scale=A[:, c, col:col + 1], bias=B[:, c, col:col + 1])


def compute_mods(S, C, cv, wada, bada, nmod, stg, psm, mods):
    scv = S.sbuf("scv", [P, KC, 2], F32)
    cvt = S.sbuf("cvt", [P, KC, 2], F32)
    S.dma("sp", cvt[:], cv[:])
    S.act(scv[:], cvt[:], AF.Silu)
    nn = nmod * KC
    for k in range(KC):
        S.dma("sp" if k % 2 else "pool", stg[k][:, :nn * P], wada[k * P:(k + 1) * P, :])
    for j in range(nn):
        for k in range(KC):
            S.matmul(psm[:, 2 * j:2 * j + 2], stg[k][:, j * P:(j + 1) * P], scv[:, k, :], start=(k == 0), stop=(k == KC - 1))
    bt = S.sbuf("badat", [P, nn], F32)
    S.dma("sp", bt[:], bada[:])
    S.tt(mods[:], psm[:, 0:2 * nn].rearrange("p (j t) -> p j t", t=2),
         View(bt, bt.t[:].rearrange("p (j o) -> p j o", o=1).to_broadcast([P, nn, 2])), ALU.add)


def ffn_sweep(S, C, tiles, x_in, x_out, w1b, w2b, A, B, G, PS):
    for j, (s0, n, col) in enumerate(tiles):
        xt = C.xt[j % 2]
        S.dma("sp", xt[:, :, :n], x_in(j))
        norm_mod(S, C, xt, n, A, B, col, C.h, PS.ss)
        for f in range(FC):
            pg = PS.g[f % 2]
            pu = PS.u[f % 2]
            for k in range(KC):
                S.matmul(pg[:, :n], w1b[:, k, f * P:(f + 1) * P], C.h[:, k, :n], start=(k == 0), stop=(k == KC - 1))
            for k in range(KC):
                S.matmul(pu[:, :n], w1b[:, k, DFF + f * P:DFF + (f + 1) * P], C.h[:, k, :n], start=(k == 0), stop=(k == KC - 1))
            sg = C.sg[f % 2]
            S.act(sg[:, :n], pg[:, :n], AF.Silu)
            S.tt(C.aT[:, f, :n], sg[:, :n], pu[:, :n], ALU.mult)
        for d in range(KC):
            py = PS.y[d % 2]
            for f in range(FC):
                S.matmul(py[:, :n], w2b[:, f, d * P:(d + 1) * P], C.aT[:, f, :n], start=(f == 0), stop=(f == FC - 1))
            S.copy(C.y[:, d, :n], py[:, :n], e="dve")
        rms_rstd(S, C, lambda c: C.y[:, c, :n], n, KC, D, PS.ss2, C.rstd2[:, :n])
        for c in range(KC):
            t = C.tmp[c % 2]
            S.tt(t[:, :n], C.y[:, c, :n], C.rstd2[:, :n], ALU.mult)
            S.stt(xt[:, c, :n], t[:, :n], G[:, c, col:col + 1], xt[:, c, :n], ALU.mult, ALU.add)
        S.dma("pool", x_out(j), xt[:, :, :n])


def alloc_ffn_work(S, C):
    C.xt = [S.sbuf("xt0", [P, KC, 512], F32)] * 2
    C.h = S.sbuf("h", [P, KC, 512], BF16)
    C.aT = S.sbuf("aT", [P, FC, 512], BF16)
    C.y = S.sbuf("y", [P, KC, 512], F32)
    C.sg = C.tmp


def alloc_small(S, C):
    C.sq = [S.sbuf("sq%d" % i, [P, 512], F32) for i in range(2)]
    C.tmp = [S.sbuf("tmp%d" % i, [P, 512], F32) for i in range(2)]
    C.lnt = C.sq[0]
    C.rstd = S.sbuf("rstd", [P, 512], F32)
    C.rstd2 = C.rstd


def mk_tiles(NT, NCX):
    tiles = [(j * 512, 512, 0) for j in range(NT // 512)]
    if NCX:
        tiles.append((NT, NCX, 1))
    return tiles


DEBUG = False
NPC = 49


def build_R1(NT, NCX):
    TT = NT + NCX
    nc = bass.Bass("TRN2", target_bir_lowering=False)
    with ExitStack() as st:
        S = Sched(nc, st)
        xT = S.dram("xT", [P, KC, TT], F32, kind="ExternalInput")
        cv = S.dram("cv", [P, KC, 2], F32, kind="ExternalInput")
        wada = S.dram("wada", [D, 5 * D], F32, kind="ExternalInput")
        bada = S.dram("bada", [P, 40], F32, kind="ExternalInput")
        ng = S.dram("ng", [P, 48], F32, kind="ExternalInput")
        w1 = S.dram("w1", [D, 2 * DFF], F32, kind="ExternalInput")
        w2 = S.dram("w2", [DFF, D], F32, kind="ExternalInput")
        win = S.dram("win", [D, NPC * P], F32, kind="ExternalInput")
        x1T = S.dram("x1T", [P, KC, TT], F32, kind="ExternalOutput")
        PT = S.dram("PT", [NPC, P, TT], F32, kind="ExternalOutput")
        tiles = mk_tiles(NT, NCX)
        C = mk_consts(S, nc)
        alloc_small(S, C)
        PS = NS()
        PS.ss = S.psum("ps_ss", [P, 512])
        PS.ss2 = S.psum("ps_ss2", [P, 512])
        PS.g = [S.psum("ps_g%d" % i, [P, 512]) for i in range(2)]
        PS.u = [S.psum("ps_u%d" % i, [P, 512]) for i in range(2)]
        PS.y = [S.psum("ps_y%d" % i, [P, 512]) for i in range(2)]
        mods = S.sbuf("mods", [P, 40, 2], F32)
        ngt = S.sbuf("ngt", [P, 6, KC], F32)
        S.dma("sp", ngt[:], ng[:].rearrange("p (m c) -> p m c", c=KC))
        A1 = S.sbuf("A1", [P, KC, 2], F32)
        G1 = S.sbuf("G1", [P, KC, 2], F32)
        A2 = S.sbuf("A2", [P, KC, 2], F32)
        with S.scope():
            stg = [S.sbuf("stgm%d" % i, [P, 5 * D], F32) for i in range(KC)]
            compute_mods(S, C, cv, wada, bada, 5, stg, PS.g[0], mods)

        def bc(v):
            return View(v.buf, v.ap.rearrange("p (c o) -> p c o", o=1).to_broadcast([P, KC, 2]))
        S.stt(A1[:], mods[:, 8:16, :], 1.0, bc(ngt[:, 0, :]), ALU.add, ALU.mult)
        S.stt(G1[:], mods[:, 16:24, :], 0.5, bc(ngt[:, 1, :]), ALU.mult, ALU.mult)
        S.stt(A2[:], mods[:, 32:40, :], 1.0, bc(ngt[:, 2, :]), ALU.add, ALU.mult)
        B1 = mods[:, 0:8, :]
        B2 = mods[:, 24:32, :]
        if DEBUG:
            dbg = S.dram("dbg_mods", [P, 80], F32, kind="ExternalOutput")
            S.dma("sp", dbg[:], mods[:].rearrange("p j t -> p (j t)"))
        x1tiles = [S.sub("x1t%d" % j, x1T.t[:, :, s0:s0 + n]) for j, (s0, n, col) in enumerate(tiles)]
        with S.scope():
            w1b = S.sbuf("w1b", [P, KC, 2 * DFF], BF16)
            w2b = S.sbuf("w2b", [P, FC, D], BF16)
            with S.scope():
                stages = [S.sbuf("wst%d" % i, [P, 2048], F32) for i in range(3)]
                load_w(S, w1, w1b, D, 2 * DFF, stages)
                load_w(S, w2, w2b, DFF, D, stages)
            alloc_ffn_work(S, C)
            ffn_sweep(S, C, tiles, lambda j: xT[:, :, tiles[j][0]:tiles[j][0] + tiles[j][1]],
                      lambda j: x1tiles[j][:], w1b, w2b, A1[:], B1, G1[:], PS)
        with S.scope():
            winb = S.sbuf("winb", [P, KC, NPC * P], BF16)
            with S.scope():
                stages = [S.sbuf("wst%d" % i, [P, 2048], F32) for i in range(3)]
                load_w(S, win, winb, D, NPC * P, stages)
            xt2 = [S.sbuf("xq%d" % i, [P, KC, 512], F32) for i in range(2)]
            h = S.sbuf("h2", [P, KC, 512], BF16)
            ost = [S.sbuf("ost%d" % i, [P, 4, 512], F32) for i in range(3)]
            pps = PS.g + PS.u + PS.y
            gi = 0
            for j, (s0, n, col) in enumerate(tiles):
                xt = xt2[j % 2]
                S.dma("sp", xt[:, :, :n], x1tiles[j][:])
                norm_mod(S, C, xt, n, A2[:], B2, col, h, PS.ss)
                for c0 in range(0, NPC, 4):
                    nn = min(4, NPC - c0)
                    o = ost[gi % 3]
                    gi += 1
                    for cc in range(nn):
                        pp = pps[(c0 + cc) % 6]
                        for k in range(KC):
                            S.matmul(pp[:, :n], winb[:, k, (c0 + cc) * P:(c0 + cc + 1) * P], h[:, k, :n],
                                     start=(k == 0), stop=(k == KC - 1))
                        S.copy(o[:, cc, :n], pp[:, :n], e=("act" if cc % 2 else "dve"))
                    S.dma("pool", S.sub("pt", PT.t[c0:c0 + nn, :, s0:s0 + n].rearrange("c p t -> p c t"))[:], o[:, :nn, :n])
            S.wait_all("sp", ost + xt2)
        S.barrier()
    return nc


def build_R2(NT, NCX):
    TT = NT + NCX
    nc = bass.Bass("TRN2", target_bir_lowering=False)
    with ExitStack() as st:
        S = Sched(nc, st)
        x1T = S.dram("x1T", [P, KC, TT], F32, kind="ExternalInput")
        oin = [S.dram(nm, [P, 4, TT], F32, kind="ExternalInput") for nm in ("oaT", "obT", "ocT")]
        cv = S.dram("cv", [P, KC, 2], F32, kind="ExternalInput")
        wada = S.dram("wada", [D, 6 * D], F32, kind="ExternalInput")
        bada = S.dram("bada", [P, 48], F32, kind="ExternalInput")
        ng = S.dram("ng", [P, 48], F32, kind="ExternalInput")
        snw = S.dram("snw", [P, 4], F32, kind="ExternalInput")
        wg = S.dram("wg", [D, 3 * D], F32, kind="ExternalInput")
        wb = S.dram("wb", [1536, D], F32, kind="ExternalInput")
        wo = S.dram("wo", [D, D], F32, kind="ExternalInput")
        w1 = S.dram("w1", [D, 2 * DFF], F32, kind="ExternalInput")
        w2 = S.dram("w2", [DFF, D], F32, kind="ExternalInput")
        x3T = S.dram("x3T", [P, KC, TT], F32, kind="ExternalOutput")
        x2T = S.dram("x2T", [P, KC, TT], F32, kind="Internal")
        tiles = mk_tiles(NT, NCX)
        C = mk_consts(S, nc)
        alloc_small(S, C)
        PS = NS()
        PS.ss = S.psum("ps_ss", [P, 512])
        PS.ss2 = S.psum("ps_ss2", [P, 512])
        PS.g = [S.psum("ps_g%d" % i, [P, 512]) for i in range(2)]
        PS.u = [S.psum("ps_u%d" % i, [P, 512]) for i in range(2)]
        PS.y = [S.psum("ps_y%d" % i, [P, 512]) for i in range(2)]
        mods = S.sbuf("mods", [P, 48, 2], F32)
        ngt = S.sbuf("ngt", [P, 6, KC], F32)
        S.dma("sp", ngt[:], ng[:].rearrange("p (m c) -> p m c", c=KC))
        snt = S.sbuf("snt", [P, 4], F32)
        S.dma("sp", snt[:], snw[:])
        with S.scope():
            stg = [S.sbuf("stgm%d" % i, [P, 6 * D], F32) for i in range(KC)]
            compute_mods(S, C, cv, wada, bada, 6, stg, PS.g[0], mods)

        def bc(v):
            return View(v.buf, v.ap.rearrange("p (c o) -> p c o", o=1).to_broadcast([P, KC, 2]))
        A2 = S.sbuf("A2", [P, KC, 2], F32)
        G3 = S.sbuf("G3", [P, KC, 2], F32)
        A4 = S.sbuf("A4", [P, KC, 2], F32)
        G5 = S.sbuf("G5", [P, KC, 2], F32)
        S.stt(A2[:], mods[:, 8:16, :], 1.0, bc(ngt[:, 2, :]), ALU.add, ALU.mult)
        S.tt(G3[:], mods[:, 16:24, :], bc(ngt[:, 3, :]), ALU.mult)
        S.stt(A4[:], mods[:, 32:40, :], 1.0, bc(ngt[:, 4, :]), ALU.add, ALU.mult)
        S.stt(G5[:], mods[:, 40:48, :], 0.5, bc(ngt[:, 5, :]), ALU.mult, ALU.mult)
        B2 = mods[:, 0:8, :]
        B4 = mods[:, 24:32, :]
        x2tiles = [S.sub("x2t%d" % j, x2T.t[:, :, s0:s0 + n]) for j, (s0, n, col) in enumerate(tiles)]
        with S.scope():
            wgb = S.sbuf("wgb", [P, KC, 3 * D], BF16)
            wbb = S.sbuf("wbb", [P, 12, D], BF16)
            wob = S.sbuf("wob", [P, KC, D], BF16)
            with S.scope():
                stages = [S.sbuf("wst%d" % i, [P, 2048], F32) for i in range(3)]
                load_w(S, wg, wgb, D, 3 * D, stages)
                load_w(S, wb, wbb, 1536, D, stages)
                load_w(S, wo, wob, D, D, stages)
            xt = S.sbuf("xm", [P, KC, 512], F32)
            h = S.sbuf("hm", [P, KC, 512], BF16)
            ost = S.sbuf("ostg", [P, 4, 512], F32)
            ob16 = S.sbuf("ob16", [P, 12, 512], BF16)
            yacc = S.sbuf("yacc", [P, 512], F32)
            ybf = S.sbuf("ybf", [P, KC, 512], BF16)
            yy = S.sbuf("yy", [P, KC, 512], F32)
            gt = [S.sbuf("gt%d" % i, [P, 512], F32) for i in range(2)]
            for j, (s0, n, col) in enumerate(tiles):
                S.dma("sp", xt[:, :, :n], x1T[:, :, s0:s0 + n])
                norm_mod(S, C, xt, n, A2[:], B2, col, h, PS.ss)
                for br in range(3):
                    S.dma("sp", ost[:, :, :n], oin[br][:, :, s0:s0 + n])
                    if br < 2:
                        S.copy(ob16[:, br * 4:(br + 1) * 4, :n], ost[:, :, :n], e="pool")
                    else:
                        rms_rstd(S, C, lambda c: ost[:, c, :n], n, 4, 512, PS.ss2, C.rstd2[:, :n])
                        for c in range(4):
                            t = C.tmp[c % 2]
                            S.tt(t[:, :n], ost[:, c, :n], C.rstd2[:, :n], ALU.mult)
                            S.act(ob16[:, 8 + c, :n], t[:, :n], AF.Copy, scale=snt[:, c:c + 1])
                for d in range(KC):
                    for br in range(3):
                        pg = PS.g[br % 2]
                        pu = PS.u[br % 2]
                        cg = br * KC + d
                        for k in range(KC):
                            S.matmul(pg[:, :n], wgb[:, k, cg * P:(cg + 1) * P], h[:, k, :n], start=(k == 0), stop=(k == KC - 1))
                        for k in range(4):
                            S.matmul(pu[:, :n], wbb[:, br * 4 + k, d * P:(d + 1) * P], ob16[:, br * 4 + k, :n], start=(k == 0), stop=(k == 3))
                        g = gt[br % 2]
                        S.act(g[:, :n], pg[:, :n], AF.Sigmoid)
                        if br == 0:
                            S.tt(yacc[:, :n], g[:, :n], pu[:, :n], ALU.mult)
                        else:
                            t = C.tmp[br % 2]
                            S.tt(t[:, :n], g[:, :n], pu[:, :n], ALU.mult)
                            if br == 1:
                                S.tt(yacc[:, :n], yacc[:, :n], t[:, :n], ALU.add)
                            else:
                                S.tt(ybf[:, d, :n], yacc[:, :n], t[:, :n], ALU.add)
                for d in range(KC):
                    py = PS.y[d % 2]
                    for k in range(KC):
                        S.matmul(py[:, :n], wob[:, k, d * P:(d + 1) * P], ybf[:, k, :n], start=(k == 0), stop=(k == KC - 1))
                    S.copy(yy[:, d, :n], py[:, :n], e="dve")
                rms_rstd(S, C, lambda c: yy[:, c, :n], n, KC, D, PS.ss2, C.rstd2[:, :n])
                for c in range(KC):
                    t = C.tmp[c % 2]
                    S.tt(t[:, :n], yy[:, c, :n], C.rstd2[:, :n], ALU.mult)
                    S.stt(xt[:, c, :n], t[:, :n], G3[:, c, col:col + 1], xt[:, c, :n], ALU.mult, ALU.add)
                S.dma("pool", x2tiles[j][:], xt[:, :, :n])
        with S.scope():
            w1b = S.sbuf("w1b", [P, KC, 2 * DFF], BF16)
            w2b = S.sbuf("w2b", [P, FC, D], BF16)
            with S.scope():
                stages = [S.sbuf("wst%d" % i, [P, 2048], F32) for i in range(3)]
                load_w(S, w1, w1b, D, 2 * DFF, stages)
                load_w(S, w2, w2b, DFF, D, stages)
            alloc_ffn_work(S, C)
            ffn_sweep(S, C, tiles, lambda j: x2tiles[j][:],
                      lambda j: S.sub("x3", x3T.t[:, :, tiles[j][0]:tiles[j][0] + tiles[j][1]])[:], w1b, w2b, A4[:], B4, G5[:], PS)
        S.barrier()
    return nc


I32 = mybir.dt.int32
import math


def rope_tables(S, nc, C, L, cosb, sinb):
    GW = 64
    rows = L // GW
    TWO_PI = 2 * math.pi
    with S.scope():
        ti = S.sbuf("ti", [P, P], I32)
        tf = S.sbuf("tf", [P, P], F32)

        def ppc(name, pattern):
            o = S.sbuf(name, [P, 1], F32)
            S.op("pool", lambda: nc.gpsimd.iota(ti.t[:], pattern, base=0, channel_multiplier=0), [], [ti[:]])
            S.copy(tf[:], ti[:])
            S.tt(tf[:], tf[:], C.ident[:], ALU.mult)
            S.reduce(o[:], tf[:], ALU.add)
            return o
        i16 = ppc("i16", [[0, 2], [0, 2], [1, 16], [0, 2]])
        sel = ppc("sel", [[0, 2], [1, 2], [0, 16], [0, 2]])
        dd = ppc("dd", [[0, 2], [0, 2], [0, 16], [1, 2]])
        sgn = S.sbuf("sgn", [P, 1], F32)
        inv = S.sbuf("inv", [P, 1], F32)
        S.ts(sgn[:], dd[:], 2.0, ALU.mult, -1.0, ALU.add)
        S.act(inv[:], i16[:], AF.Exp, scale=-math.log(10000.0) / 16.0)
        S.ts(inv[:], inv[:], 1.0 / TWO_PI, ALU.mult)
        CH = 1024
        with S.scope():
            ri = S.sbuf("ri", [P, CH], I32)
            ci = S.sbuf("ci", [P, CH], I32)
            rf = S.sbuf("rf", [P, CH], F32)
            cf = S.sbuf("cf", [P, CH], F32)
            xt = S.sbuf("xtn", [P, CH], F32)
            ni = S.sbuf("ni", [P, CH], I32)
            nf = S.sbuf("nf", [P, CH], F32)
            for c0 in range(0, L, CH):
                w = min(CH, L - c0)
                S.op("pool", lambda c0=c0, w=w: nc.gpsimd.iota(ri.t[:, :w], [[1, w // GW], [0, GW]], base=c0 // GW, channel_multiplier=0), [], [ri[:]])
                S.op("pool", lambda w=w: nc.gpsimd.iota(ci.t[:, :w], [[0, w // GW], [1, GW]], base=0, channel_multiplier=0), [], [ci[:]])
                S.copy(rf[:, :w], ri[:, :w])
                S.copy(cf[:, :w], ci[:, :w])
                S.tt(cf[:, :w], cf[:, :w], rf[:, :w], ALU.subtract)
                S.stt(xt[:, :w], cf[:, :w], sel[:, 0:1], rf[:, :w], ALU.mult, ALU.add)
                S.ts(xt[:, :w], xt[:, :w], inv[:, 0:1], ALU.mult)
                for (dst, off) in ((sinb, 0.0), (cosb, 0.25)):
                    if off:
                        S.ts(xt[:, :w], xt[:, :w], off, ALU.add)
                    S.copy(ni[:, :w], xt[:, :w])
                    S.copy(nf[:, :w], ni[:, :w])
                    S.tt(nf[:, :w], xt[:, :w], nf[:, :w], ALU.subtract)
                    S.act(dst[:, c0:c0 + w], nf[:, :w], AF.Sin, scale=TWO_PI * (1 - 1e-6))
                S.ts(sinb[:, c0:c0 + w], sinb[:, c0:c0 + w], sgn[:, 0:1], ALU.mult)


def build_Mdiff(L, LC):
    LK = L + LC
    NKC = LK // P
    nc = bass.Bass("TRN2", target_bir_lowering=False)
    with ExitStack() as st:
        S = Sched(nc, st)
        qT = S.dram("qT", [2, P, L], F32, kind="ExternalInput")
        qsT = S.dram("qsT", [2, P, L], F32, kind="ExternalInput")
        kT = S.dram("kT", [2, P, LK], F32, kind="ExternalInput")
        ksT = S.dram("ksT", [2, P, L], F32, kind="ExternalInput")
        qcT = S.dram("qcT", [2, P, LC], F32, kind="ExternalInput")
        vd = S.dram("v", [P, NKC, 256], F32, kind="ExternalInput")
        lamd = S.dram("lam", [P, 256], F32, kind="ExternalInput")
        nwd = S.dram("nw", [P, 1], F32, kind="ExternalInput")
        lid = S.dram("li", [P, 1], F32, kind="ExternalInput")
        obT = S.dram("obT", [2, P, L], F32, kind="ExternalOutput")
        obcT = S.dram("obcT", [2, P, LC], F32, kind="ExternalOutput")
        C = mk_consts(S, nc)
        C.sq = [S.sbuf("sq%d" % i, [P, 512], F32) for i in range(2)]
        C.lnt = C.sq[0]
        onesb = S.sbuf("onesb", [P, P], BF16)
        S.memset(onesb[:], 1.0)
        Q = [S.sbuf("Q%d" % h, [P, L], BF16) for h in range(2)]
        QC = [S.sbuf("QC%d" % h, [P, LC], BF16) for h in range(2)]
        K = [S.sbuf("K%d" % h, [P, LK], BF16) for h in range(2)]
        V = S.sbuf("V", [P, NKC, 256], BF16)
        lam = S.sbuf("lamt", [P, 4, 64], F32)
        S.dma("sp", lam[:], lamd[:].rearrange("p (a b) -> p a b", b=64))
        nw = S.sbuf("nwt", [P, 1], F32)
        li = S.sbuf("lit", [P, 1], F32)
        S.dma("sp", nw[:], nwd[:])
        S.dma("sp", li[:], lid[:])
        pr = S.sbuf("pr", [P, 2, 64], F32)
        s12 = S.sbuf("s12", [P, 2], F32)
        S.tt(pr[:, 0, :], lam[:, 0, :], lam[:, 1, :], ALU.mult)
        S.tt(pr[:, 1, :], lam[:, 2, :], lam[:, 3, :], ALU.mult)
        S.reduce(s12[:], pr[:], ALU.add)
        e12 = S.sbuf("e12", [P, 2], F32)
        S.act(e12[:], s12[:], AF.Exp)
        neglam = S.sbuf("neglam", [P, 1], F32)
        S.tt(neglam[:], e12[:, 1:2], e12[:, 0:1], ALU.subtract)
        S.tt(neglam[:], neglam[:], li[:], ALU.subtract)
        sc2 = S.sbuf("sc2", [P, 1], F32)
        S.ts(sc2[:], li[:], -1.0, ALU.mult, 1.0, ALU.add)
        S.tt(sc2[:], sc2[:], nw[:], ALU.mult)
        with S.scope():
            cosb = S.sbuf("cosb", [P, L], F32)
            sinb = S.sbuf("sinb", [P, L], F32)
            rope_tables(S, nc, C, L, cosb, sinb)
            with S.scope():
                a = [S.sbuf("la%d" % i, [P, 512], F32) for i in range(2)]
                b = [S.sbuf("lb%d" % i, [P, 512], F32) for i in range(2)]
                vst = [S.sbuf("vst%d" % i, [P, 4, 256], F32) for i in range(2)]
                i = 0
                for h in range(2):
                    for (src, ssw, dst) in ((qT, qsT, Q[h]), (kT, ksT, K[h])):
                        for c0 in range(0, L, 512):
                            ta, tb = a[i % 2], b[i % 2]
                            i += 1
                            S.dma("sp", ta[:], src[h, :, c0:c0 + 512])
                            S.dma("pool", tb[:], ssw[h, :, c0:c0 + 512])
                            S.tt(ta[:], ta[:], cosb[:, c0:c0 + 512], ALU.mult)
                            S.tt(tb[:], tb[:], sinb[:, c0:c0 + 512], ALU.mult, e="pool")
                            S.tt(dst[:, c0:c0 + 512], ta[:], tb[:], ALU.add)
                    ta = a[i % 2]
                    i += 1
                    S.dma("sp", ta[:, :LC], kT[h, :, L:LK])
                    S.copy(K[h][:, L:LK], ta[:, :LC])
                    ta = a[i % 2]
                    i += 1
                    S.dma("sp", ta[:, :LC], qcT[h, :, :])
                    S.copy(QC[h][:], ta[:, :LC])
                for c0 in range(0, NKC, 4):
                    w = min(4, NKC - c0)
                    t = vst[(c0 // 4) % 2]
                    S.dma("sp", t[:, :w, :], vd[:, c0:c0 + w, :])
                    S.copy(V[:, c0:c0 + w, :], t[:, :w, :], e="pool")
        ps_s = [[S.psum("ps_s%d%d" % (j, i), [P, 512]) for i in range(2)] for j in range(2)]
        ps_o = [S.psum("ps_o%d" % j, [P, 512]) for j in range(2)]
        ps_z = [S.psum("ps_z%d" % j, [P, 512]) for j in range(2)]
        pt = [[S.sbuf("pt%d%d" % (j, i), [P, 512], BF16) for i in range(2)] for j in range(2)]
        rz = [S.sbuf("rz%d" % j, [P, 512], F32) for j in range(2)]
        t0 = S.sbuf("t0", [P, 512], F32)
        t1 = S.sbuf("t1", [P, 512], F32)
        rstd = S.sbuf("rstd", [P, 512], F32)
        oo = [S.sbuf("oo%d" % i, [P, 512], F32) for i in range(2)]
        jobs = []
        for h in range(2):
            for q0 in range(0, L, 512):
                jobs.append((h, Q[h][:, q0:q0 + 512], 512, 0, NKC, obT[h, :, q0:q0 + 512]))
            jobs.append((h, QC[h][:], LC, L // P, NKC, obcT[h, :, :]))
        for ji, (h, qv, n, kc0, kc1, outv) in enumerate(jobs):
            for kc in range(kc0, kc1):
                bi = kc % 2
                for j in range(2):
                    S.matmul(ps_s[j][bi][:, :n], K[h][j * 64:(j + 1) * 64, kc * P:(kc + 1) * P], qv[j * 64:(j + 1) * 64, :],
                             start=True, stop=True)
                for j in range(2):
                    S.act(pt[j][bi][:, :n], ps_s[j][bi][:, :n], AF.Exp, scale=0.125)
                for j in range(2):
                    S.matmul(ps_o[j][:, :n], V[:, kc, h * P:(h + 1) * P], pt[j][bi][:, :n], start=(kc == kc0), stop=(kc == kc1 - 1))
                    S.matmul(ps_z[j][:, :n], onesb[:], pt[j][bi][:, :n], start=(kc == kc0), stop=(kc == kc1 - 1))
            for j in range(2):
                S.recip(rz[j][:, :n], ps_z[j][:, :n])
            S.tt(t0[:, :n], ps_o[0][:, :n], rz[0][:, :n], ALU.mult)
            S.tt(t1[:, :n], ps_o[1][:, :n], rz[1][:, :n], ALU.mult)
            S.stt(t0[:, :n], t1[:, :n], neglam[:, 0:1], t0[:, :n], ALU.mult, ALU.add)
            pss = ps_s[0][0]
            rms_rstd(S, C, lambda c: t0[:, :n], n, 1, P, pss, rstd[:, :n])
            o = oo[ji % 2]
            S.tt(t1[:, :n], t0[:, :n], rstd[:, :n], ALU.mult)
            S.ts(o[:, :n], t1[:, :n], sc2[:, 0:1], ALU.mult)
            S.dma("pool", outv, o[:, :n])
        S.barrier()
    return nc


def tri_mask(S, nc, name, kind, blk=None):
    m = S.sbuf(name, [P, P], F32)
    S.memset(m[:], 1.0, e="pool")
    pat, cm, op = {"le": ([[1, P]], -1, ALU.is_ge), "ge": ([[-1, P]], 1, ALU.is_ge),
                   "gt": ([[-1, P]], 1, ALU.is_gt), "lt": ([[1, P]], -1, ALU.is_gt)}[kind]
    S.op("pool", lambda: nc.gpsimd.affine_select(m.t[:], m.t[:], pat, op, 0.0, base=0, channel_multiplier=cm), [m[:]], [m[:]])
    if blk:
        S.memset(m[0:blk, blk:P], 0.0, e="pool")
        S.memset(m[blk:P, 0:blk], 0.0, e="pool")
    return m


def build_Mssd(L, LC, dbg_stop=99):
    LT = L + LC
    NCH = LT // P
    NCC = LC // P
    nc = bass.Bass("TRN2", target_bir_lowering=False)
    with ExitStack() as st:
        S = Sched(nc, st)
        xl = S.dram("xbcl", [4, P, L + 4], F32, kind="ExternalInput")
        xc = S.dram("xbcc", [4, P, LC + 4], F32, kind="ExternalInput")
        cwd = S.dram("cw", [P, 4, 5], F32, kind="ExternalInput")
        cbd = S.dram("cb", [P, 4], F32, kind="ExternalInput")
        zd = S.dram("z", [P, NCH, 256], F32, kind="ExternalInput")
        dtd = S.dram("dt", [P, NCH, 8], F32, kind="ExternalInput")
        dbd = S.dram("dtb", [P, 8], F32, kind="ExternalInput")
        ald = S.dram("alog", [P, 8], F32, kind="ExternalInput")
        dsd = S.dram("dskip", [P, 4], F32, kind="ExternalInput")
        yo = S.dram("y", [NCH, P, 256], F32, kind="ExternalOutput")
        yf = S.dram("yf", [NCH, P, 256], F32, kind="Internal")
        C = mk_consts(S, nc)
        tri = {0: tri_mask(S, nc, "tri_f", "le"), 1: tri_mask(S, nc, "tri_b", "ge")}
        strict = {0: tri_mask(S, nc, "str_f", "gt"), 1: tri_mask(S, nc, "str_b", "lt")}
        cw = S.sbuf("cw", [P, 4, 5], F32)
        cb = S.sbuf("cb", [P, 4], F32)
        S.dma("sp", cw[:], cwd[:])
        S.dma("sp", cb[:], cbd[:])
        xs_tok = S.sbuf("xs_tok", [P, NCH, 256], F32)
        B_tok = S.sbuf("B_tok", [P, NCH, P], F32)
        BT = S.sbuf("BT", [P, LT], F32)
        CT = S.sbuf("CT", [P, LT], F32)
        dtv = S.sbuf("dtv", [P, NCH, 8], F32)
        aall = S.sbuf("aall", [P, NCH, 8], F32)
        dtb = S.sbuf("dtb", [P, 8], F32)
        aneg = S.sbuf("aneg", [P, 8], F32)
        dsk = S.sbuf("dsk", [P, 4], F32)
        S.dma("sp", dtv[:], dtd[:])
        S.dma("sp", dtb[:], dbd[:])
        S.dma("sp", aneg[:], ald[:])
        S.dma("sp", dsk[:], dsd[:])
        S.tt(dtv[:], dtv[:], View(dtb, dtb.t[:].rearrange("p (o e) -> p o e", o=1).to_broadcast([P, NCH, 8])), ALU.add)
        S.act(dtv[:], dtv[:], AF.Exp)
        S.act(dtv[:], dtv[:], AF.Ln, bias=C.ones[:, 0:1])
        S.act(aneg[:], aneg[:], AF.Exp)
        S.ts(aneg[:], aneg[:], -1.0, ALU.mult)
        S.tt(aall[:], dtv[:], View(aneg, aneg.t[:].rearrange("p (o e) -> p o e", o=1).to_broadcast([P, NCH, 8])), ALU.mult)
        ps_t = [S.psum("ps_t%d" % i, [P, 512]) for i in range(2)]
        with S.scope():
            raw = [S.sbuf("raw%d" % i, [P, 4, 516], F32) for i in range(2)]
            acc = [S.sbuf("acc%d" % i, [P, 512], F32) for i in range(2)]
            xsT = [S.sbuf("xsT%d" % i, [P, 512], F32) for i in range(2)]
            segs = [(xc, 0, LC)] + [(xl, LC, L)]
            ti = 0
            if dbg_stop < 0:
                segs = []
            for (src, base, seglen) in segs:
                for t0 in range(0, seglen, 512):
                    n = min(512, seglen - t0)
                    r = raw[ti % 2]
                    ti += 1
                    S.dma("sp", r[:, :, :n + 4], src[:, :, t0:t0 + n + 4].rearrange("c p t -> p c t"))
                    for c in range(4):
                        a = acc[c % 2]
                        eng = "dve"
                        S.ts(a[:, :n], r[:, c, 0:n], cw[:, c, 0:1], ALU.mult, e=eng)
                        for j in range(1, 5):
                            S.stt(a[:, :n], r[:, c, j:j + n], cw[:, c, j:j + 1], a[:, :n], ALU.mult, ALU.add, e=eng)
                        g0 = base + t0
                        if c < 2:
                            dst = xsT[c]
                            S.act(dst[:, :n], a[:, :n], AF.Silu, bias=cb[:, c:c + 1])
                        elif c == 2:
                            S.act(BT[:, g0:g0 + n], a[:, :n], AF.Silu, bias=cb[:, c:c + 1])
                        else:
                            S.act(CT[:, g0:g0 + n], a[:, :n], AF.Silu, bias=cb[:, c:c + 1])
                    for bl in range(n // P):
                        gc = (base + t0) // P + bl
                        pt = ps_t[bl % 2]
                        dbgv = None
                        S.transpose(pt[:, 0:P], xsT[0][:, bl * P:(bl + 1) * P], C.ident[:])
                        if dbgv == "T1":
                            S.copy(xs_tok[:, gc, 0:P], pt[:, 0:P], e="dve")
                            continue
                        S.transpose(pt[:, P:2 * P], xsT[1][:, bl * P:(bl + 1) * P], C.ident[:])
                        if dbgv == "T2":
                            S.copy(xs_tok[:, gc, :], pt[:, 0:2 * P], e="dve")
                            continue
                        S.transpose(pt[:, 2 * P:3 * P], BT[:, gc * P:(gc + 1) * P], C.ident[:])
                        S.copy(xs_tok[:, gc, :], pt[:, 0:2 * P], e="dve")
                        S.copy(B_tok[:, gc, :], pt[:, 2 * P:3 * P], e="dve")
        ps_arg = [S.psum("ps_arg%d" % i, [P, 512]) for i in range(2)]
        ps_cb = S.psum("ps_cb", [P, 512])
        ps_y = S.psum("ps_y", [P, 512])
        ps_st = S.psum("ps_st", [P, 512])
        ps_sm = S.psum("ps_sm", [P, 512])
        X = [S.sbuf("X%d" % i, [P, 4, P], F32) for i in range(2)]
        LTt = [S.sbuf("LT%d" % i, [P, 4, P], F32) for i in range(2)]
        CBm = S.sbuf("CBm", [P, P], F32)
        scT = [S.sbuf("scT%d" % i, [P, 4, P], F32) for i in range(2)]
        sm = S.sbuf("sm", [P, 8], F32)
        eacs = S.sbuf("eacs", [P, 4], F32)
        edec = S.sbuf("edec", [P, 4], F32)
        etot = S.sbuf("etot", [P, 4], F32)
        dif = S.sbuf("dif", [P, 4], F32)
        xdt = [S.sbuf("xdt%d" % i, [P, 4, 64], F32) for i in range(2)]
        xdtd = [S.sbuf("xdtd%d" % i, [P, 4, 64], F32) for i in range(2)]
        ST = S.sbuf("ST", [P, 4, 64], F32)
        yt = [S.sbuf("yt%d" % i, [P, 256], F32) for i in range(2)]
        y2 = [S.sbuf("y2%d" % i, [P, 256], F32) for i in range(2)]
        yfl = [S.sbuf("yfl%d" % i, [P, 256], F32) for i in range(2)]
        zt = [S.sbuf("zt%d" % i, [P, 256], F32) for i in range(2)]
        yfb = [S.sub("yf%d" % c, yf.t[c]) for c in range(NCH)]

        def bc4(v):
            return View(v.buf, v.ap.rearrange("p (h o) -> p h o", o=1).to_broadcast([P, 4, 64]))
        if dbg_stop < 2:
            for c in range(NCH):
                S.dma("sp", zt[c % 2][:], zd[:, c, :])
                if dbg_stop == 1:
                    S.tt(zt[c % 2][:], zt[c % 2][:], xs_tok[:, c, :], ALU.add)
                    S.tt(zt[c % 2][:, 0:P], zt[c % 2][:, 0:P], B_tok[:, c, :], ALU.add)
                S.dma("pool", S.sub("yo", yo.t[c])[:], zt[c % 2][:])
        for dr in range(2 if dbg_stop >= 2 else 0):
            order = list(range(NCH)) if dr == 0 else (list(range(NCC - 1, -1, -1)) + list(range(NCH - 1, NCC - 1, -1)))
            S.memset(ST[:], 0.0)
            for it, c in enumerate(order):
                bi = it % 2
                a4 = aall[:, c, dr * 4:(dr + 1) * 4]
                for h in range(4):
                    S.ts(X[bi][:, h, :], strict[dr][:], aall[:, c, dr * 4 + h:dr * 4 + h + 1], ALU.mult, e=("dve" if h % 2 else "pool"))
                for h in range(4):
                    S.matmul(ps_arg[bi][:, h * P:(h + 1) * P], X[bi][:, h, :], tri[dr][:])
                S.act(LTt[bi][:].rearrange("p h l -> p (h l)"), ps_arg[bi][:], AF.Exp)
                S.matmul(ps_cb[:, 0:P], BT[:, c * P:(c + 1) * P], CT[:, c * P:(c + 1) * P])
                S.tt(CBm[:], ps_cb[:, 0:P], tri[dr][:], ALU.mult)
                S.tt(scT[bi][:], LTt[bi][:], View(CBm, CBm.t[:].rearrange("p (o l) -> p o l", o=1).to_broadcast([P, 4, P])), ALU.mult)
                S.matmul(ps_sm[:, 0:4], tri[dr][:], a4)
                S.matmul(ps_sm[:, 4:8], C.ones[:], a4)
                S.copy(sm[:], ps_sm[:, 0:8])
                S.act(eacs[:], sm[:, 0:4], AF.Exp)
                S.act(etot[:], sm[:, 4:8], AF.Exp)
                S.tt(dif[:], sm[:, 4:8], sm[:, 0:4], ALU.subtract)
                S.act(edec[:], dif[:], AF.Exp)
                xv = xs_tok[:, c, :].rearrange("p (h d) -> p h d", h=4)
                S.tt(xdt[bi][:], xv, bc4(dtv[:, c, dr * 4:(dr + 1) * 4]), ALU.mult, e="pool")
                S.tt(xdtd[bi][:], xdt[bi][:], bc4(edec[:]), ALU.mult)
                for h in range(4):
                    S.matmul(ps_y[:, h * 64:(h + 1) * 64], scT[bi][:, h, :], xdt[bi][:, h, :])
                S.matmul(ps_y[:, 256:512], CT[:, c * P:(c + 1) * P], ST[:].rearrange("p h d -> p (h d)"))
                y = yt[bi]
                S.tt(y[:].rearrange("p (h d) -> p h d", h=4), ps_y[:, 256:512].rearrange("p (h d) -> p h d", h=4), bc4(eacs[:]), ALU.mult)
                S.tt(y[:], y[:], ps_y[:, 0:256], ALU.add)
                S.matmul(ps_st[:, 0:256], B_tok[:, c, :], xdtd[bi][:].rearrange("p h d -> p (h d)"))
                S.tt(ST[:], ST[:], bc4(etot[:]), ALU.mult)
                S.tt(ST[:].rearrange("p h d -> p (h d)"), ST[:].rearrange("p h d -> p (h d)"), ps_st[:, 0:256], ALU.add)
                if dr == 0:
                    S.dma("pool", yfb[c][:], y[:])
                else:
                    S.dma("sp", yfl[bi][:], yfb[c][:])
                    S.dma("sp", zt[bi][:], zd[:, c, :])
                    o = y2[bi]
                    S.tt(o[:].rearrange("p (h d) -> p h d", h=4), xv, bc4(dsk[:]), ALU.mult, e="pool")
                    S.tt(y[:], y[:], yfl[bi][:], ALU.add)
                    S.tt(o[:], o[:], y[:], ALU.add)
                    S.act(zt[bi][:], zt[bi][:], AF.Silu)
                    S.tt(o[:], o[:], zt[bi][:], ALU.mult)
                    S.dma("pool", S.sub("yo", yo.t[c])[:], o[:])
        S.barrier()
    return nc


def build_Mgdn(L, LC):
    LT = L + LC
    NCH = LT // P
    NCC = LC // P
    nc = bass.Bass("TRN2", target_bir_lowering=False)
    with ExitStack() as st:
        S = Sched(nc, st)
        ql = S.dram("qkvl", [6, P, L + 4], F32, kind="ExternalInput")
        qc = S.dram("qkvc", [6, P, LC + 4], F32, kind="ExternalInput")
        cwd = S.dram("cw", [P, 6, 5], F32, kind="ExternalInput")
        zd = S.dram("z", [P, NCH, 256], F32, kind="ExternalInput")
        ad = S.dram("araw", [P, NCH, 4], F32, kind="ExternalInput")
        bd = S.dram("braw", [P, NCH, 4], F32, kind="ExternalInput")
        ald = S.dram("alog", [P, 4], F32, kind="ExternalInput")
        dbd = S.dram("dtb", [P, 4], F32, kind="ExternalInput")
        nwd = S.dram("nw", [P, P], F32, kind="ExternalInput")
        oa = S.dram("oa", [NCH, P, 256], F32, kind="ExternalOutput")
        ofd = S.dram("of", [NCH, P, 256], F32, kind="Internal")
        C = mk_consts(S, nc)
        M = {k: tri_mask(S, nc, "m_" + k, k, blk=64) for k in ("le", "ge", "gt", "lt")}
        halfA = S.sbuf("halfA", [P, P], F32)
        halfB = S.sbuf("halfB", [P, P], F32)
        S.memset(halfA[:], 0.0)
        S.memset(halfB[:], 0.0)
        S.memset(halfA[0:64, :], 1.0)
        S.memset(halfB[64:128, :], 1.0)
        cw = S.sbuf("cw", [P, 6, 5], F32)
        S.dma("sp", cw[:], cwd[:])
        nw = S.sbuf("nw", [P, P], F32)
        S.dma("sp", nw[:], nwd[:])
        gall = S.sbuf("gall", [P, NCH, 4], F32)
        ball = S.sbuf("ball", [P, NCH, 4], F32)
        negb = S.sbuf("negb", [P, NCH, 4], F32)
        aneg = S.sbuf("aneg", [P, 4], F32)
        dtb = S.sbuf("dtb", [P, 4], F32)
        S.dma("sp", gall[:], ad[:])
        S.dma("sp", ball[:], bd[:])
        S.dma("sp", aneg[:], ald[:])
        S.dma("sp", dtb[:], dbd[:])

        def bcn(v):
            return View(v.buf, v.ap.rearrange("p (o e) -> p o e", o=1).to_broadcast([P, NCH, 4]))
        S.tt(gall[:], gall[:], bcn(dtb[:]), ALU.add)
        S.act(gall[:], gall[:], AF.Exp)
        S.act(gall[:], gall[:], AF.Ln, bias=C.ones[:, 0:1])
        S.act(aneg[:], aneg[:], AF.Exp)
        S.ts(aneg[:], aneg[:], -1.0, ALU.mult)
        S.tt(gall[:], gall[:], bcn(aneg[:]), ALU.mult)
        S.act(ball[:], ball[:], AF.Sigmoid)
        S.ts(negb[:], ball[:], -1.0, ALU.mult)
        BA = [S.psum("BA%d" % h, [P, 512]) for h in range(2)]
        B1 = [S.psum("B1%d" % h, [P, 512]) for h in range(2)]
        B2 = [S.psum("B2%d" % h, [P, 512]) for h in range(2)]
        B3 = [S.psum("B3%d" % h, [P, 512]) for h in range(2)]
        Wk = []
        for h in range(2):
            W = NS()
            for nm in ("X", "Dm", "Dv", "Ds", "kbg", "kdec", "vb", "vnew", "oq", "o", "of_", "zt", "t1"):
                setattr(W, nm, S.sbuf("%s%d" % (nm, h), [P, P], F32))
            for nm in ("NA", "RA", "uw"):
                setattr(W, nm, S.sbuf("%s%d" % (nm, h), [P, 2 * P], F32))
            W.NR = [S.sbuf("NR%d%d" % (h, i), [P, 2 * P], F32) for i in range(2)]
            W.Xc = [S.sbuf("Xc%d%d" % (h, i), [P, P], F32) for i in range(2)]
            W.esm = S.sbuf("esm%d" % h, [P, 4], F32)
            W.bg = S.sbuf("bg%d" % h, [P, 1], F32)
            W.ss = S.sbuf("ss%d" % h, [P, 1], F32)
            W.oo = [S.sbuf("oo%d%d" % (h, i), [P, P], F32) for i in range(2)]
            Wk.append(W)
        state = [S.sbuf("state%d" % h, [P, P], F32) for h in range(2)]
        raw = S.sbuf("raw", [P, 6, 516], F32)
        acc = [S.sbuf("acc%d" % i, [P, 512], F32) for i in range(2)]
        sqb = S.sbuf("sqb", [P, 512], F32)
        lnb = S.sbuf("lnb", [P, 512], F32)
        rsb = S.sbuf("rsb", [P, 512], F32)
        qkv = [S.sbuf("qkv%d" % i, [P, 6, 512], F32) for i in range(2)]
        ofb = [[S.sub("of%d_%d" % (c, h), ofd.t[c][:, h * P:(h + 1) * P]) for h in range(2)] for c in range(NCH)]

        def prep(src, t0, n, dst):
            S.dma("sp", raw[:, :, :n + 4], src[:, :, t0:t0 + n + 4].rearrange("c p t -> p c t"))
            for c in range(6):
                a = acc[c % 2]
                S.ts(a[:, :n], raw[:, c, 0:n], cw[:, c, 0:1], ALU.mult)
                for j in range(1, 5):
                    S.stt(a[:, :n], raw[:, c, j:j + n], cw[:, c, j:j + 1], a[:, :n], ALU.mult, ALU.add)
                if c >= 4:
                    S.act(dst[:, c, :n], a[:, :n], AF.Silu)
                else:
                    S.act(a[:, :n], a[:, :n], AF.Silu)
                    S.act(sqb[:, :n], a[:, :n], AF.Square)
                    pb = B3[c % 2]
                    S.matmul(pb[:, :n], C.ones[:], sqb[:, :n])
                    S.act(lnb[:, :n], pb[:, :n], AF.Ln, bias=C.eps[:, 0:1])
                    S.act(rsb[:, :n], lnb[:, :n], AF.Exp, scale=-0.5)
                    S.stt(dst[:, c, :n], a[:, :n], (128.0 ** -0.5) if c < 2 else 1.0, rsb[:, :n], ALU.mult, ALU.mult)

        def unit(hl, dr, gp, qv, kv, vv):
            col = dr * 2 + hl
            g = gall[:, gp, col:col + 1]
            nb = negb[:, gp, col:col + 1]
            bt = ball[:, gp, col:col + 1]
            W = Wk[hl]
            bA, b1, b2, b3 = BA[hl], B1[hl], B2[hl], B3[hl]
            Tri, Xm, Val, SVal = (M["le"], M["gt"], M["ge"], M["gt"]) if dr == 0 else (M["ge"], M["lt"], M["le"], M["lt"])
            S.ts(W.X[:], Xm[:], g, ALU.mult)
            S.matmul(bA[:, 0:128], Tri[:], W.X[:])
            S.matmul(bA[:, 128:129], Tri[:], g)
            S.matmul(bA[:, 129:130], Xm[:], g)
            S.matmul(bA[:, 130:131], halfA[:], g)
            S.matmul(bA[:, 131:132], halfB[:], g)
            S.matmul(b1[:, 0:128], kv, kv)
            S.matmul(b1[:, 128:256], qv, kv)
            S.transpose(bA[:, 256:384], kv, C.ident[:])
            S.transpose(bA[:, 384:512], vv, C.ident[:])
            yield
            S.act(W.Dm[:], bA[:, 0:128], AF.Exp)
            S.act(W.esm[:], bA[:, 128:132], AF.Exp)
            S.tt(W.bg[:], W.esm[:, 0:1], bt, ALU.mult)
            S.act(W.kdec[:], bA[:, 256:384], AF.Identity, scale=W.esm[:, 1:2])
            S.act(W.vb[:], bA[:, 384:512], AF.Identity, scale=bt)
            S.act(W.kbg[:], bA[:, 256:384], AF.Identity, scale=W.bg[:, 0:1])
            S.tt(W.Dv[:], W.Dm[:], Val[:], ALU.mult)
            S.tt(W.Ds[:], W.Dm[:], SVal[:], ALU.mult)
            S.stt(W.NA[:, 0:128], b1[:, 0:128], nb, W.Ds[:], ALU.mult, ALU.mult)
            S.tt(W.NA[:, 128:256], b1[:, 128:256], W.Dv[:], ALU.mult)
            yield
            S.transpose(b1[:, 256:384], W.NA[:, 0:128], C.ident[:])
            S.transpose(b1[:, 384:512], W.NA[:, 128:256], C.ident[:])
            S.copy(W.RA[:], b1[:, 256:512])
            X = W.Xc[0]
            S.tt(X[:], W.RA[:, 0:128], C.ident[:], ALU.add)
            yield
            Ncur = W.NA[:, 0:128]
            Rcur = W.RA[:, 0:128]
            for lev in range(5):
                NR = W.NR[lev % 2]
                S.matmul(b2[:, 0:128], Rcur, Ncur)
                if lev < 4:
                    S.matmul(b2[:, 128:256], Ncur, Rcur)
                    S.copy(NR[:], b2[:, 0:256])
                else:
                    S.copy(NR[:, 0:128], b2[:, 0:128])
                yield
                S.matmul(b2[:, 256:384], NR[:, 0:128], X[:])
                Xn = W.Xc[(lev + 1) % 2]
                S.tt(Xn[:], X[:], b2[:, 256:384], ALU.add)
                X = Xn
                Ncur = NR[:, 0:128]
                Rcur = NR[:, 128:256]
                yield
            S.matmul(b3[:, 0:128], X[:], W.vb[:])
            S.matmul(b3[:, 128:256], W.kbg[:], X[:])
            S.copy(W.uw[:], b3[:, 0:256])
            yield
            blocks = [(0, 64), (64, 128)] if dr == 0 else [(64, 128), (0, 64)]
            Sst = state[hl]
            for bi, (r0, r1) in enumerate(blocks):
                reg = b3[:, 256:512] if bi == 0 else b3[:, 0:256]
                S.matmul(reg[:, 0:128], W.uw[:, 128:256], Sst[:])
                S.matmul(reg[:, 128:256], qv, Sst[:])
                S.tt(W.vnew[r0:r1, :], W.uw[r0:r1, 0:128], reg[r0:r1, 0:128], ALU.subtract)
                S.ts(W.oq[r0:r1, :], reg[r0:r1, 128:256], W.esm[r0:r1, 0:1], ALU.mult)
                yield
                S.matmul(b1[:, 0:128], W.kdec[r0:r1, :], W.vnew[r0:r1, :])
                egX = W.esm[:, 2:3] if r0 == 0 else W.esm[:, 3:4]
                S.stt(Sst[:], Sst[:], egX, b1[:, 0:128], ALU.mult, ALU.add)
                yield
            S.matmul(b1[:, 128:256], W.RA[:, 128:256], W.vnew[:])
            S.tt(W.o[:], W.oq[:], b1[:, 128:256], ALU.add)
            if dr == 0:
                S.dma("pool", ofb[gp][hl][:], W.o[:])
            else:
                S.dma("sp", W.of_[:], ofb[gp][hl][:])
                S.dma("sp", W.zt[:], zd[:, gp, hl * P:(hl + 1) * P])
                S.tt(W.o[:], W.o[:], W.of_[:], ALU.add)
                S.act(W.t1[:], W.o[:], AF.Square, accum_out=W.ss[:, 0:1])
                yield
                S.act(W.ss[:], W.ss[:], AF.Ln, scale=1.0 / 128.0, bias=C.eps[:, 0:1])
                S.act(W.ss[:], W.ss[:], AF.Exp, scale=-0.5)
                S.act(W.zt[:], W.zt[:], AF.Silu)
                S.stt(W.t1[:], W.o[:], W.ss[:, 0:1], nw[:], ALU.mult, ALU.mult)
                oo = W.oo[gp % 2]
                S.tt(oo[:], W.t1[:], W.zt[:], ALU.mult)
                S.dma("pool", S.sub("oa", oa.t[gp][:, hl * P:(hl + 1) * P])[:], oo[:])
            yield

        for dr in range(2):
            for h in range(2):
                S.memset(state[h][:], 0.0)
            segs = [(qc, 0, LC), (ql, LC, L)]
            tl = []
            for (src, base, seglen) in segs:
                tt_ = [(src, base, t0, min(512, seglen - t0)) for t0 in range(0, seglen, 512)]
                if dr == 1:
                    tt_ = tt_[::-1]
                tl += tt_
            for ti, (src, base, t0, n) in enumerate(tl):
                dst = qkv[ti % 2]
                prep(src, t0, n, dst)
                prs = list(range(n // P))
                if dr == 1:
                    prs = prs[::-1]
                for pi in prs:
                    gp = (base + t0) // P + pi
                    sl = slice(pi * P, (pi + 1) * P)
                    gens = [unit(h, dr, gp, dst[:, 0 + h, sl], dst[:, 2 + h, sl], dst[:, 4 + h, sl]) for h in range(2)]
                    alive = [True, True]
                    while any(alive):
                        for h in range(2):
                            if alive[h]:
                                try:
                                    next(gens[h])
                                except StopIteration:
                                    alive[h] = False
        S.barrier()
    return nc


NFM = 36
NTK = 1568
GRP = [[0, 1], [2, 3], [4, 5], [6, 7]]


def build_fused(L, LC, depth=2):
    NT, NCX = L // 2, LC // 2
    TT = NT + NCX
    LT = L + LC
    NCH = LT // P
    NCC = LC // P
    LK = LT
    NKC = LK // P
    NLC = NT // P
    assert NCX == P
    nc = bass.Bass("TRN2", target_bir_lowering=False)
    with ExitStack() as st:
        S = Sched(nc, st)
        ccsem = st.enter_context(nc.semaphore("ccsem"))
        cc = [0]
        xT = S.dram("xT", [P, KC, TT], F32, kind="ExternalInput")
        cv = S.dram("cv", [P, KC, 2], F32, kind="ExternalInput")
        selv = S.dram("selv", [P, 2], F32, kind="ExternalInput")
        yT = S.dram("yT", [P, KC, TT], F32, kind="ExternalOutput")
        Wl = []
        for i in range(depth):
            W = NS()
            sfx = "_%d" % i
            W.wada1 = S.dram("wada1" + sfx, [D, 5 * D], F32, kind="ExternalInput")
            W.bada1 = S.dram("bada1" + sfx, [P, 40], F32, kind="ExternalInput")
            W.wada2 = S.dram("wada2" + sfx, [D, 6 * D], F32, kind="ExternalInput")
            W.bada2 = S.dram("bada2" + sfx, [P, 48], F32, kind="ExternalInput")
            W.ng = S.dram("ng" + sfx, [P, 48], F32, kind="ExternalInput")
            W.w1a = S.dram("w1a" + sfx, [D, 2 * DFF], F32, kind="ExternalInput")
            W.w2a = S.dram("w2a" + sfx, [DFF, D], F32, kind="ExternalInput")
            W.w1b = S.dram("w1b" + sfx, [D, 2 * DFF], F32, kind="ExternalInput")
            W.w2b = S.dram("w2b" + sfx, [DFF, D], F32, kind="ExternalInput")
            W.win = S.dram("win" + sfx, [D, NFM * P + NTK], F32, kind="ExternalInput")
            W.wg = S.dram("wg" + sfx, [D, 3 * D], F32, kind="ExternalInput")
            W.wb = S.dram("wb" + sfx, [1536, D], F32, kind="ExternalInput")
            W.wo = S.dram("wo" + sfx, [D, D], F32, kind="ExternalInput")
            W.snw = S.dram("snw" + sfx, [P, 4], F32, kind="ExternalInput")
            W.lam = S.dram("lam" + sfx, [P, 256], F32, kind="ExternalInput")
            W.dnw = S.dram("dnw" + sfx, [P, 1], F32, kind="ExternalInput")
            W.li = S.dram("li" + sfx, [P, 1], F32, kind="ExternalInput")
            W.scw = S.dram("scw" + sfx, [P, 4, 5], F32, kind="ExternalInput")
            W.scb = S.dram("scb" + sfx, [P, 4], F32, kind="ExternalInput")
            W.sdtb = S.dram("sdtb" + sfx, [P, 8], F32, kind="ExternalInput")
            W.salog = S.dram("salog" + sfx, [P, 8], F32, kind="ExternalInput")
            W.sdsk = S.dram("sdsk" + sfx, [P, 4], F32, kind="ExternalInput")
            W.gcw = S.dram("gcw" + sfx, [P, 6, 5], F32, kind="ExternalInput")
            W.galog = S.dram("galog" + sfx, [P, 4], F32, kind="ExternalInput")
            W.gdtb = S.dram("gdtb" + sfx, [P, 4], F32, kind="ExternalInput")
            W.gnw = S.dram("gnw" + sfx, [P, P], F32, kind="ExternalInput")
            Wl.append(W)
        Xs = S.dram("Xs", [P, KC, TT], F32)
        X1 = S.dram("X1s", [P, KC, TT], F32)
        X2 = S.dram("X2s", [P, KC, TT], F32)
        NB = NT // 256
        PTL = nc.dram_tensor("PTL", [NFM, P, NT], F32)
        PTLG = nc.dram_tensor("PTLG", [NFM, 2, P, NT], F32)
        PTC = nc.dram_tensor("PTC", [NFM, P, NCX], F32)
        PTCG = nc.dram_tensor("PTCG", [2, 2, 18, P, NCX], F32)
        PKL = nc.dram_tensor("PKL", [NB, 256, NTK], F32)
        PKLG = nc.dram_tensor("PKLG", [NB, 2, 256, NTK], F32)
        PKC = nc.dram_tensor("PKC", [NCX, NTK], F32)
        PKCG = nc.dram_tensor("PKCG", [2, NCX, NTK], F32)
        MOL = nc.dram_tensor("MOL", [6, 2, P, NT], F32)
        MOLG = nc.dram_tensor("MOLG", [6, 2, 2, P, NT], F32)
        MOC = nc.dram_tensor("MOC", [6, 2, P, NCX], F32)
        MOCG = nc.dram_tensor("MOCG", [2, 6, 2, P, NCX], F32)
        OF = S.dram("OFs", [NCH, P, 256], F32)

        def mo_dst(c0, c1, s_, off, n):
            if off >= NT:
                return MOC.ap()[c0:c1, s_, :, off - NT:off - NT + n]
            return MOL.ap()[c0:c1, s_, :, off:off + n]

        def dsub(ap):
            return Buf("u", ap, "dram")[:] if False else View(Buf("u", ap, "dram"), ap)

        tiles = mk_tiles(NT, NCX)
        C = mk_consts(S, nc)
        sel = S.sbuf("sel", [P, 2], F32)
        S.dma("sp", sel[:], selv[:])
        dummy = S.sbuf("dummy", [P, 1], F32)

        def blend(dst, alt):
            S.ts(dst, dst, sel[:, 0:1], ALU.mult)
            S.stt(dst, alt, sel[:, 1:2], dst, ALU.mult, ALU.add)

        def gather_many(pairs):
            S.barrier()
            for (i_ap, o_ap) in pairs:
                cc[0] += 1
                nc.gpsimd.collective_compute("AllGather", ALU.bypass, replica_groups=GRP, ins=[i_ap], outs=[o_ap]).then_inc(ccsem, 1)
            nc.gpsimd.wait_ge(ccsem, cc[0])
            S.memset(dummy[:], 0.0, e="pool")
            S.barrier()

        def gather_P():
            pr = [(PTL.ap()[c], PTLG.ap()[c].rearrange("r p t -> (r p) t")) for c in range(NFM)]
            pr += [(PTC.ap()[h * 18:(h + 1) * 18].rearrange("c p t -> (c p) t"), PTCG.ap()[h].rearrange("r c p t -> (r c p) t")) for h in range(2)]
            pr += [(PKL.ap()[b], PKLG.ap()[b].rearrange("r t e -> (r t) e")) for b in range(NB)]
            pr += [(PKC.ap(), PKCG.ap().rearrange("r t e -> (r t) e"))]
            gather_many(pr)

        def gather_M():
            pr = [(MOL.ap()[c, s_], MOLG.ap()[c, s_].rearrange("r p t -> (r p) t")) for c in range(6) for s_ in range(2)]
            pr += [(MOC.ap().rearrange("c s p t -> (c s p) t"), MOCG.ap().rearrange("r c s p t -> (r c s p) t"))]
            gather_many(pr)

        def tokpos(gc):
            if gc < NCC:
                return gc, NT
            t = (gc - NCC) * P
            return t // NT, t % NT

        def mk_ps():
            PS = NS()
            PS.ss = S.psum("ps_ss", [P, 512])
            PS.ss2 = S.psum("ps_ss2", [P, 512])
            PS.g = [S.psum("ps_g%d" % i, [P, 512]) for i in range(2)]
            PS.u = [S.psum("ps_u%d" % i, [P, 512]) for i in range(2)]
            PS.y = [S.psum("ps_y%d" % i, [P, 512]) for i in range(2)]
            return PS

        def bc(v):
            return View(v.buf, v.ap.rearrange("p (c o) -> p c o", o=1).to_broadcast([P, KC, 2]))

        def ph_R1(W, xin, x1t):
            with S.scope():
                alloc_small(S, C)
                PS = mk_ps()
                mods = S.sbuf("mods", [P, 40, 2], F32)
                ngt = S.sbuf("ngt", [P, 6, KC], F32)
                S.dma("sp", ngt[:], W.ng[:].rearrange("p (m c) -> p m c", c=KC))
                A1 = S.sbuf("A1", [P, KC, 2], F32)
                G1 = S.sbuf("G1", [P, KC, 2], F32)
                A2 = S.sbuf("A2", [P, KC, 2], F32)
                with S.scope():
                    stg = [S.sbuf("stgm%d" % i, [P, 5 * D], F32) for i in range(KC)]
                    compute_mods(S, C, cv, W.wada1, W.bada1, 5, stg, PS.g[0], mods)
                S.stt(A1[:], mods[:, 8:16, :], 1.0, bc(ngt[:, 0, :]), ALU.add, ALU.mult)
                S.stt(G1[:], mods[:, 16:24, :], 0.5, bc(ngt[:, 1, :]), ALU.mult, ALU.mult)
                S.stt(A2[:], mods[:, 32:40, :], 1.0, bc(ngt[:, 2, :]), ALU.add, ALU.mult)
                B1 = mods[:, 0:8, :]
                B2 = mods[:, 24:32, :]
                with S.scope():
                    w1b = S.sbuf("w1b", [P, KC, 2 * DFF], BF16)
                    w2b = S.sbuf("w2b", [P, FC, D], BF16)
                    with S.scope():
                        stages = [S.sbuf("wst%d" % i, [P, 2048], F32) for i in range(3)]
                        load_w(S, W.w1a, w1b, D, 2 * DFF, stages)
                        load_w(S, W.w2a, w2b, DFF, D, stages)
                    alloc_ffn_work(S, C)
                    ffn_sweep(S, C, tiles, lambda j: xin[j][:], lambda j: x1t[j][:], w1b, w2b, A1[:], B1, G1[:], PS)
                with S.scope():
                    NW = NFM * P + NTK
                    winb = S.sbuf("winb", [P, KC, NW], BF16)
                    with S.scope():
                        stages = [S.sbuf("wst%d" % i, [P, 2048], F32) for i in range(3)]
                        load_w(S, W.win, winb, D, NW, stages)
                    xt2 = [S.sbuf("xq%d" % i, [P, KC, 512], F32) for i in range(2)]
                    h = S.sbuf("h2", [P, KC, 512], BF16)
                    ost = [S.sbuf("ost%d" % i, [P, 4, 512], F32) for i in range(3)]
                    tst = [S.sbuf("tst%d" % i, [P, NTK], F32) for i in range(2)]
                    pps = PS.g + PS.u + PS.y
                    gi = 0
                    ti = 0
                    for j, (s0, n, col) in enumerate(tiles):
                        xt = xt2[j % 2]
                        S.dma("sp", xt[:, :, :n], x1t[j][:])
                        norm_mod(S, C, xt, n, A2[:], B2, col, h, PS.ss)
                        for c0 in range(0, NFM, 4):
                            nn = min(4, NFM - c0)
                            o = ost[gi % 3]
                            gi += 1
                            for cc_ in range(nn):
                                pp = pps[(c0 + cc_) % 6]
                                for k in range(KC):
                                    S.matmul(pp[:, :n], winb[:, k, (c0 + cc_) * P:(c0 + cc_ + 1) * P], h[:, k, :n],
                                             start=(k == 0), stop=(k == KC - 1))
                                S.copy(o[:, cc_, :n], pp[:, :n], e=("act" if cc_ % 2 else "dve"))
                            pdst = PTL.ap()[c0:c0 + nn, :, s0:s0 + n] if col == 0 else PTC.ap()[c0:c0 + nn, :, 0:n]
                            S.dma("pool", dsub(pdst.rearrange("c p t -> p c t")), o[:, :nn, :n])
                        for sb in range(n // P):
                            tt_ = tst[ti % 2]
                            ti += 1
                            for q, c0 in enumerate(range(0, NTK, 512)):
                                w = min(512, NTK - c0)
                                pp = pps[q % 6]
                                for k in range(KC):
                                    S.matmul(pp[:, :w], h[:, k, sb * P:(sb + 1) * P], winb[:, k, NFM * P + c0:NFM * P + c0 + w],
                                             start=(k == 0), stop=(k == KC - 1))
                                S.copy(tt_[:, c0:c0 + w], pp[:, :w], e=("act" if q % 2 else "dve"))
                            trow = s0 + sb * P
                            kdst = PKL.ap()[trow // 256, trow % 256:trow % 256 + P, :] if col == 0 else PKC.ap()[0:P, :]
                            S.dma("pool", dsub(kdst), tt_[:])

        def lat_rc(t0):
            return t0 // NT, t0 % NT

        def load_fm(q, dst, alt, c0, nch, seg, t0, n, halo):
            seglen = L if seg == "lat" else LC
            a, b = max(0, t0 - halo), min(seglen, t0 + n + halo)
            if halo and (t0 - halo < 0):
                S.memset(dst[:, :, 0:halo], 0.0)
                S.memset(alt[:, :, 0:halo], 0.0)
            if halo and (t0 + n + halo > seglen):
                S.memset(dst[:, :, n + halo:n + 2 * halo], 0.0)
                S.memset(alt[:, :, n + halo:n + 2 * halo], 0.0)
            pieces = []
            per = NT if seg == "lat" else NCX
            base = 0 if seg == "lat" else NT
            p = a
            while p < b:
                r = p // per
                e = min(b, (r + 1) * per)
                pieces.append((r, p % per, e - p, p - (t0 - halo)))
                p = e
            for g, tgt in ((0, dst), (1, alt)):
                for (r, col0, ln, d0) in pieces:
                    if seg == "lat":
                        sap = PTLG.ap()[g * 18 + c0:g * 18 + c0 + nch, r, :, col0:col0 + ln]
                    else:
                        sap = PTCG.ap()[g, r, c0:c0 + nch, :, col0:col0 + ln]
                    S.dma(q, tgt[:, :, d0:d0 + ln], dsub(sap.rearrange("c p t -> p c t")))
            blend(dst[:, :, :], alt[:, :, :])

        def load_tok(q, dst, alt, gc, e0, ne):
            s, off = tokpos(gc)
            for g, tgt in ((0, dst), (1, alt)):
                if off >= NT:
                    sap = PKCG.ap()[s, 0:P, g * 784 + e0:g * 784 + e0 + ne]
                else:
                    sap = PKLG.ap()[off // 256, s, off % 256:off % 256 + P, g * 784 + e0:g * 784 + e0 + ne]
                S.dma(q, tgt, dsub(sap))
            blend(dst, alt)

        def load_small(smallst, alt):
            for g, tgt in ((0, smallst), (1, alt)):
                for r in range(2):
                    S.dma("sp", tgt[:, r, :], dsub(PKCG.ap()[r, 0:P, g * 784 + 768:g * 784 + 784]))
                    for bq in range(NB):
                        c_ = NCC + r * NLC + 2 * bq
                        S.dma("sp" if bq % 2 else "pool", tgt[:, c_:c_ + 2, :],
                              dsub(PKLG.ap()[bq, r, :, g * 784 + 768:g * 784 + 784].rearrange("(c p) e -> p c e", p=P)))
            blend(smallst[:], alt[:])

        def ph_Mdiff(W):
            with S.scope():
                C.sq = [S.sbuf("sq%d" % i, [P, 512], F32) for i in range(2)]
                C.lnt = C.sq[0]
                onesb = S.sbuf("onesb", [P, P], BF16)
                S.memset(onesb[:], 1.0)
                Q = [S.sbuf("Q%d" % h, [P, L], BF16) for h in range(2)]
                QC = [S.sbuf("QC%d" % h, [P, LC], BF16) for h in range(2)]
                K = [S.sbuf("K%d" % h, [P, LK], BF16) for h in range(2)]
                V = S.sbuf("V", [P, NKC, 256], BF16)
                lam = S.sbuf("lamt", [P, 4, 64], F32)
                S.dma("sp", lam[:], W.lam[:].rearrange("p (a b) -> p a b", b=64))
                nw = S.sbuf("nwt", [P, 1], F32)
                li = S.sbuf("lit", [P, 1], F32)
                S.dma("sp", nw[:], W.dnw[:])
                S.dma("sp", li[:], W.li[:])
                pr = S.sbuf("pr", [P, 2, 64], F32)
                s12 = S.sbuf("s12", [P, 2], F32)
                S.tt(pr[:, 0, :], lam[:, 0, :], lam[:, 1, :], ALU.mult)
                S.tt(pr[:, 1, :], lam[:, 2, :], lam[:, 3, :], ALU.mult)
                S.reduce(s12[:], pr[:], ALU.add)
                e12 = S.sbuf("e12", [P, 2], F32)
                S.act(e12[:], s12[:], AF.Exp)
                neglam = S.sbuf("neglam", [P, 1], F32)
                S.tt(neglam[:], e12[:, 1:2], e12[:, 0:1], ALU.subtract)
                S.tt(neglam[:], neglam[:], li[:], ALU.subtract)
                sc2 = S.sbuf("sc2", [P, 1], F32)
                S.ts(sc2[:], li[:], -1.0, ALU.mult, 1.0, ALU.add)
                S.tt(sc2[:], sc2[:], nw[:], ALU.mult)
                with S.scope():
                    cosb = S.sbuf("cosb", [P, L], F32)
                    sinb = S.sbuf("sinb", [P, L], F32)
                    rope_tables(S, nc, C, L, cosb, sinb)
                    with S.scope():
                        a = [S.sbuf("la%d" % i, [P, 1, 512], F32) for i in range(2)]
                        a2 = [S.sbuf("la2%d" % i, [P, 1, 512], F32) for i in range(2)]
                        b = [S.sbuf("lb%d" % i, [P, 1, 512], F32) for i in range(2)]
                        b2 = [S.sbuf("lb2%d" % i, [P, 1, 512], F32) for i in range(2)]
                        vst = [S.sbuf("vst%d" % i, [P, 4, 256], F32) for i in range(2)]
                        vs2 = [S.sbuf("vs2%d" % i, [P, 4, 256], F32) for i in range(2)]
                        i = 0
                        for h in range(2):
                            for (cq, csw, dst) in ((6 + h, 8 + h, Q[h]), (10 + h, 12 + h, K[h])):
                                for c0 in range(0, L, 512):
                                    ta, tb = a[i % 2], b[i % 2]
                                    load_fm("sp", ta[:], a2[i % 2][:], cq, 1, "lat", c0, 512, 0)
                                    load_fm("pool", tb[:], b2[i % 2][:], csw, 1, "lat", c0, 512, 0)
                                    i += 1
                                    S.tt(ta[:, 0, :], ta[:, 0, :], cosb[:, c0:c0 + 512], ALU.mult)
                                    S.tt(tb[:, 0, :], tb[:, 0, :], sinb[:, c0:c0 + 512], ALU.mult, e="pool")
                                    S.tt(dst[:, c0:c0 + 512], ta[:, 0, :], tb[:, 0, :], ALU.add)
                            ta = a[i % 2]
                            load_fm("sp", ta[:, :, :LC], a2[i % 2][:, :, :LC], 10 + h, 1, "ctx", 0, LC, 0)
                            i += 1
                            S.copy(K[h][:, L:LK], ta[:, 0, :LC])
                            ta = a[i % 2]
                            load_fm("sp", ta[:, :, :LC], a2[i % 2][:, :, :LC], 6 + h, 1, "ctx", 0, LC, 0)
                            i += 1
                            S.copy(QC[h][:], ta[:, 0, :LC])
                        vi = 0
                        for r in range(2):
                            for bq in range(NB):
                                t, t2 = vst[vi % 2], vs2[vi % 2]
                                vi += 1
                                for g, tgt in ((0, t), (1, t2)):
                                    S.dma("sp", tgt[:, 0:2, :], dsub(PKLG.ap()[bq, r, :, g * 784:g * 784 + 256].rearrange("(c p) e -> p c e", p=P)))
                                blend(t[:, 0:2, :], t2[:, 0:2, :])
                                kc = r * NLC + 2 * bq
                                S.copy(V[:, kc:kc + 2, :], t[:, 0:2, :], e="pool")
                            t, t2 = vst[vi % 2], vs2[vi % 2]
                            vi += 1
                            for g, tgt in ((0, t), (1, t2)):
                                S.dma("sp", tgt[:, 0, :], dsub(PKCG.ap()[r, 0:P, g * 784:g * 784 + 256]))
                            blend(t[:, 0, :], t2[:, 0, :])
                            S.copy(V[:, L // P + r, :], t[:, 0, :], e="pool")
                ps_s = [[S.psum("ps_s%d%d" % (j, i), [P, 512]) for i in range(2)] for j in range(2)]
                ps_o = [S.psum("ps_o%d" % j, [P, 512]) for j in range(2)]
                ps_z = [S.psum("ps_z%d" % j, [P, 512]) for j in range(2)]
                pt = [[S.sbuf("pt%d%d" % (j, i), [P, 512], BF16) for i in range(2)] for j in range(2)]
                rz = [S.sbuf("rz%d" % j, [P, 512], F32) for j in range(2)]
                t0_ = S.sbuf("t0", [P, 512], F32)
                t1_ = S.sbuf("t1", [P, 512], F32)
                rstd = S.sbuf("rstd", [P, 512], F32)
                oo = [S.sbuf("oo%d" % i, [P, 512], F32) for i in range(2)]
                jobs = []
                for h in range(2):
                    for q0 in range(0, L, 512):
                        s_, off = lat_rc(q0)
                        jobs.append((h, Q[h][:, q0:q0 + 512], 512, 0, NKC, [(mo_dst(2 + h, 3 + h, s_, off, 512)[0], 0, 512)]))
                    jobs.append((h, QC[h][:], LC, L // P, NKC, [(mo_dst(2 + h, 3 + h, r, NT, NCX)[0], r * NCX, NCX) for r in range(2)]))
                zacc = [S.sbuf("zacc%d" % j, [P, 512], F32) for j in range(2)]
                zeng = ["dve", "pool"]
                for ji, (h, qv, n, kc0, kc1, outs) in enumerate(jobs):
                    def qk(kc):
                        for j in range(2):
                            S.matmul(ps_s[j][kc % 2][:, :n], K[h][j * 64:(j + 1) * 64, kc * P:(kc + 1) * P], qv[j * 64:(j + 1) * 64, :],
                                     start=True, stop=True)
                    qk(kc0)
                    for kc in range(kc0, kc1):
                        bi = kc % 2
                        for j in range(2):
                            S.act(pt[j][bi][:, :n], ps_s[j][bi][:, :n], AF.Exp, scale=0.125)
                        if kc + 1 < kc1:
                            qk(kc + 1)
                        for j in range(2):
                            S.matmul(ps_o[j][:, :n], V[:, kc, h * P:(h + 1) * P], pt[j][bi][:, :n], start=(kc == kc0), stop=(kc == kc1 - 1))
                            if kc == kc0:
                                S.copy(zacc[j][:, :n], pt[j][bi][:, :n], e=zeng[j])
                            else:
                                S.tt(zacc[j][:, :n], zacc[j][:, :n], pt[j][bi][:, :n], ALU.add, e=zeng[j])
                    for j in range(2):
                        S.matmul(ps_z[j][:, :n], C.ones[:], zacc[j][:, :n], start=True, stop=True)
                    for j in range(2):
                        S.recip(rz[j][:, :n], ps_z[j][:, :n])
                    S.tt(t0_[:, :n], ps_o[0][:, :n], rz[0][:, :n], ALU.mult)
                    S.tt(t1_[:, :n], ps_o[1][:, :n], rz[1][:, :n], ALU.mult)
                    S.stt(t0_[:, :n], t1_[:, :n], neglam[:, 0:1], t0_[:, :n], ALU.mult, ALU.add)
                    rms_rstd(S, C, lambda c: t0_[:, :n], n, 1, P, ps_s[0][0], rstd[:, :n])
                    o = oo[ji % 2]
                    S.tt(t1_[:, :n], t0_[:, :n], rstd[:, :n], ALU.mult)
                    S.ts(o[:, :n], t1_[:, :n], sc2[:, 0:1], ALU.mult)
                    for (oap, o0, on) in outs:
                        S.dma("pool", dsub(oap), o[:, o0:o0 + on])

        def ph_Mssd(W):
            with S.scope():
                tri = {0: tri_mask(S, nc, "tri_f", "le"), 1: tri_mask(S, nc, "tri_b", "ge")}
                strict = {0: tri_mask(S, nc, "str_f", "gt"), 1: tri_mask(S, nc, "str_b", "lt")}
                cw = S.sbuf("cw", [P, 4, 5], F32)
                cb = S.sbuf("cb", [P, 4], F32)
                S.dma("sp", cw[:], W.scw[:])
                S.dma("sp", cb[:], W.scb[:])
                xs_tok = S.sbuf("xs_tok", [P, NCH, 256], F32)
                B_tok = S.sbuf("B_tok", [P, NCH, P], F32)
                BT = S.sbuf("BT", [P, LT], F32)
                CT = S.sbuf("CT", [P, LT], F32)
                dtv = S.sbuf("dtv", [P, NCH, 8], F32)
                aall = S.sbuf("aall", [P, NCH, 8], F32)
                dtb = S.sbuf("dtb", [P, 8], F32)
                aneg = S.sbuf("aneg", [P, 8], F32)
                dsk = S.sbuf("dsk", [P, 4], F32)
                with S.scope():
                    sm1 = S.sbuf("sm1", [P, NCH, 16], F32)
                    sm2 = S.sbuf("sm2", [P, NCH, 16], F32)
                    load_small(sm1, sm2)
                    S.copy(dtv[:], sm1[:, :, 8:16])
                S.dma("sp", dtb[:], W.sdtb[:])
                S.dma("sp", aneg[:], W.salog[:])
                S.dma("sp", dsk[:], W.sdsk[:])
                S.tt(dtv[:], dtv[:], View(dtb, dtb.t[:].rearrange("p (o e) -> p o e", o=1).to_broadcast([P, NCH, 8])), ALU.add)
                S.act(dtv[:], dtv[:], AF.Exp)
                S.act(dtv[:], dtv[:], AF.Ln, bias=C.ones[:, 0:1])
                S.act(aneg[:], aneg[:], AF.Exp)
                S.ts(aneg[:], aneg[:], -1.0, ALU.mult)
                S.tt(aall[:], dtv[:], View(aneg, aneg.t[:].rearrange("p (o e) -> p o e", o=1).to_broadcast([P, NCH, 8])), ALU.mult)
                ps_t = [S.psum("ps_t%d" % i, [P, 512]) for i in range(2)]
                with S.scope():
                    raw = [S.sbuf("raw%d" % i, [P, 4, 516], F32) for i in range(2)]
                    raw2 = [S.sbuf("rawb", [P, 4, 516], F32)] * 2
                    acc = [S.sbuf("acc%d" % i, [P, 512], F32) for i in range(2)]
                    xsT = [S.sbuf("xsT%d" % i, [P, 512], F32) for i in range(2)]
                    segs = [("ctx", 0, LC), ("lat", LC, L)]
                    ti = 0
                    for (seg, base, seglen) in segs:
                        for t0 in range(0, seglen, 512):
                            n = min(512, seglen - t0)
                            r = raw[ti % 2]
                            load_fm("sp" if ti % 2 else "pool", r[:, :, :n + 4], raw2[ti % 2][:, :, :n + 4], 14, 4, seg, t0, n, 2)
                            ti += 1
                            for c in range(4):
                                a = acc[c % 2]
                                S.ts(a[:, :n], r[:, c, 0:n], cw[:, c, 0:1], ALU.mult)
                                for j in range(1, 5):
                                    S.stt(a[:, :n], r[:, c, j:j + n], cw[:, c, j:j + 1], a[:, :n], ALU.mult, ALU.add)
                                g0 = base + t0
                                if c < 2:
                                    S.act(xsT[c][:, :n], a[:, :n], AF.Silu, bias=cb[:, c:c + 1])
                                elif c == 2:
                                    S.act(BT[:, g0:g0 + n], a[:, :n], AF.Silu, bias=cb[:, c:c + 1])
                                else:
                                    S.act(CT[:, g0:g0 + n], a[:, :n], AF.Silu, bias=cb[:, c:c + 1])
                            for bl in range(n // P):
                                gc = (base + t0) // P + bl
                                pt = ps_t[bl % 2]
                                S.transpose(pt[:, 0:P], xsT[0][:, bl * P:(bl + 1) * P], C.ident[:])
                                S.transpose(pt[:, P:2 * P], xsT[1][:, bl * P:(bl + 1) * P], C.ident[:])
                                S.transpose(pt[:, 2 * P:3 * P], BT[:, gc * P:(gc + 1) * P], C.ident[:])
                                S.copy(xs_tok[:, gc, :], pt[:, 0:2 * P], e="dve")
                                S.copy(B_tok[:, gc, :], pt[:, 2 * P:3 * P], e="dve")
                ps_arg = [S.psum("ps_arg%d" % i, [P, 512]) for i in range(2)]
                ps_cb = S.psum("ps_cb", [P, 512])
                ps_y = S.psum("ps_y", [P, 512])
                ps_st = S.psum("ps_st", [P, 512])
                ps_sm = S.psum("ps_sm", [P, 512])
                X = [S.sbuf("X%d" % i, [P, 4, P], F32) for i in range(2)]
                LTt = [S.sbuf("LT%d" % i, [P, 4, P], F32) for i in range(2)]
                CBm = S.sbuf("CBm", [P, P], F32)
                scT = [S.sbuf("scT%d" % i, [P, 4, P], F32) for i in range(2)]
                sm = S.sbuf("sm", [P, 8], F32)
                eacs = S.sbuf("eacs", [P, 4], F32)
                edec = S.sbuf("edec", [P, 4], F32)
                etot = S.sbuf("etot", [P, 4], F32)
                dif = S.sbuf("dif", [P, 4], F32)
                xdt = [S.sbuf("xdt%d" % i, [P, 4, 64], F32) for i in range(2)]
                xdtd = [S.sbuf("xdtd%d" % i, [P, 4, 64], F32) for i in range(2)]
                ST = S.sbuf("ST", [P, 4, 64], F32)
                yt = [S.sbuf("yt%d" % i, [P, 256], F32) for i in range(2)]
                y2 = [S.sbuf("y2%d" % i, [P, 256], F32) for i in range(2)]
                yfl = [S.sbuf("yfl%d" % i, [P, 256], F32) for i in range(2)]
                zt = [S.sbuf("zt%d" % i, [P, 256], F32) for i in range(2)]
                zt2 = [S.sbuf("ztb%d" % i, [P, 256], F32) for i in range(2)]
                oT = [S.sbuf("oT%d" % i, [P, 2, P], F32) for i in range(2)]
                yfb = [S.sub("yf%d" % c, OF.t[c]) for c in range(NCH)]

                def bc4(v):
                    return View(v.buf, v.ap.rearrange("p (h o) -> p h o", o=1).to_broadcast([P, 4, 64]))
                for dr in range(2):
                    order = list(range(NCH)) if dr == 0 else (list(range(NCC - 1, -1, -1)) + list(range(NCH - 1, NCC - 1, -1)))
                    S.memset(ST[:], 0.0)
                    for it, c in enumerate(order):
                        bi = it % 2
                        a4 = aall[:, c, dr * 4:(dr + 1) * 4]
                        for h in range(4):
                            S.ts(X[bi][:, h, :], strict[dr][:], aall[:, c, dr * 4 + h:dr * 4 + h + 1], ALU.mult, e=("dve" if h % 2 else "pool"))
                        for h in range(4):
                            S.matmul(ps_arg[bi][:, h * P:(h + 1) * P], X[bi][:, h, :], tri[dr][:])
                        S.act(LTt[bi][:].rearrange("p h l -> p (h l)"), ps_arg[bi][:], AF.Exp)
                        S.matmul(ps_cb[:, 0:P], BT[:, c * P:(c + 1) * P], CT[:, c * P:(c + 1) * P])
                        S.tt(CBm[:], ps_cb[:, 0:P], tri[dr][:], ALU.mult)
                        S.tt(scT[bi][:], LTt[bi][:], View(CBm, CBm.t[:].rearrange("p (o l) -> p o l", o=1).to_broadcast([P, 4, P])), ALU.mult)
                        S.matmul(ps_sm[:, 0:4], tri[dr][:], a4)
                        S.matmul(ps_sm[:, 4:8], C.ones[:], a4)
                        S.copy(sm[:], ps_sm[:, 0:8])
                        S.act(eacs[:], sm[:, 0:4], AF.Exp)
                        S.act(etot[:], sm[:, 4:8], AF.Exp)
                        S.tt(dif[:], sm[:, 4:8], sm[:, 0:4], ALU.subtract)
                        S.act(edec[:], dif[:], AF.Exp)
                        xv = xs_tok[:, c, :].rearrange("p (h d) -> p h d", h=4)
                        S.tt(xdt[bi][:], xv, bc4(dtv[:, c, dr * 4:(dr + 1) * 4]), ALU.mult, e="pool")
                        S.tt(xdtd[bi][:], xdt[bi][:], bc4(edec[:]), ALU.mult)
                        for h in range(4):
                            S.matmul(ps_y[:, h * 64:(h + 1) * 64], scT[bi][:, h, :], xdt[bi][:, h, :])
                        S.matmul(ps_y[:, 256:512], CT[:, c * P:(c + 1) * P], ST[:].rearrange("p h d -> p (h d)"))
                        y = yt[bi]
                        S.tt(y[:].rearrange("p (h d) -> p h d", h=4), ps_y[:, 256:512].rearrange("p (h d) -> p h d", h=4), bc4(eacs[:]), ALU.mult)
                        S.tt(y[:], y[:], ps_y[:, 0:256], ALU.add)
                        S.matmul(ps_st[:, 0:256], B_tok[:, c, :], xdtd[bi][:].rearrange("p h d -> p (h d)"))
                        S.tt(ST[:], ST[:], bc4(etot[:]), ALU.mult)
                        S.tt(ST[:].rearrange("p h d -> p (h d)"), ST[:].rearrange("p h d -> p (h d)"), ps_st[:, 0:256], ALU.add)
                        if dr == 0:
                            S.dma("pool", yfb[c][:], y[:])
                        else:
                            S.dma("sp", yfl[bi][:], yfb[c][:])
                            load_tok("sp", zt[bi][:], zt2[bi][:], c, 512, 256)
                            o = y2[bi]
                            S.tt(o[:].rearrange("p (h d) -> p h d", h=4), xv, bc4(dsk[:]), ALU.mult, e="pool")
                            S.tt(y[:], y[:], yfl[bi][:], ALU.add)
                            S.tt(o[:], o[:], y[:], ALU.add)
                            S.act(zt[bi][:], zt[bi][:], AF.Silu)
                            S.tt(o[:], o[:], zt[bi][:], ALU.mult)
                            S.transpose(ps_cb[:, P:2 * P], o[:, 0:P], C.ident[:])
                            S.transpose(ps_cb[:, 2 * P:3 * P], o[:, P:2 * P], C.ident[:])
                            S.copy(oT[bi][:].rearrange("p a b -> p (a b)"), ps_cb[:, P:3 * P])
                            s_, off = tokpos(c)
                            S.dma("pool", dsub(mo_dst(4, 6, s_, off, P).rearrange("c p t -> p c t")), oT[bi][:])

        def ph_Mgdn(W):
            with S.scope():
                M = {k: tri_mask(S, nc, "m_" + k, k, blk=64) for k in ("le", "ge", "gt", "lt")}
                halfA = S.sbuf("halfA", [P, P], F32)
                halfB = S.sbuf("halfB", [P, P], F32)
                S.memset(halfA[:], 0.0)
                S.memset(halfB[:], 0.0)
                S.memset(halfA[0:64, :], 1.0)
                S.memset(halfB[64:128, :], 1.0)
                cw = S.sbuf("cw", [P, 6, 5], F32)
                S.dma("sp", cw[:], W.gcw[:])
                nw = S.sbuf("nw", [P, P], F32)
                S.dma("sp", nw[:], W.gnw[:])
                gall = S.sbuf("gall", [P, NCH, 4], F32)
                ball = S.sbuf("ball", [P, NCH, 4], F32)
                negb = S.sbuf("negb", [P, NCH, 4], F32)
                aneg = S.sbuf("aneg", [P, 4], F32)
                dtb = S.sbuf("dtb", [P, 4], F32)
                with S.scope():
                    sm1 = S.sbuf("sm1", [P, NCH, 16], F32)
                    sm2 = S.sbuf("sm2", [P, NCH, 16], F32)
                    load_small(sm1, sm2)
                    S.copy(gall[:], sm1[:, :, 0:4])
                    S.copy(ball[:], sm1[:, :, 4:8])
                S.dma("sp", aneg[:], W.galog[:])
                S.dma("sp", dtb[:], W.gdtb[:])

                def bcn(v):
                    return View(v.buf, v.ap.rearrange("p (o e) -> p o e", o=1).to_broadcast([P, NCH, 4]))
                S.tt(gall[:], gall[:], bcn(dtb[:]), ALU.add)
                S.act(gall[:], gall[:], AF.Exp)
                S.act(gall[:], gall[:], AF.Ln, bias=C.ones[:, 0:1])
                S.act(aneg[:], aneg[:], AF.Exp)
                S.ts(aneg[:], aneg[:], -1.0, ALU.mult)
                S.tt(gall[:], gall[:], bcn(aneg[:]), ALU.mult)
                S.act(ball[:], ball[:], AF.Sigmoid)
                S.ts(negb[:], ball[:], -1.0, ALU.mult)
                BA = [S.psum("BA%d" % h, [P, 512]) for h in range(2)]
                B1 = [S.psum("B1%d" % h, [P, 512]) for h in range(2)]
                B2 = [S.psum("B2%d" % h, [P, 512]) for h in range(2)]
                B3 = [S.psum("B3%d" % h, [P, 512]) for h in range(2)]
                Wk = []
                for h in range(2):
                    Wn = NS()
                    for nm in ("X", "Dm", "Dv", "Ds", "kbg", "kdec", "vb", "vnew", "oq", "o", "of_", "zt", "zt2", "t1"):
                        setattr(Wn, nm, S.sbuf("%s%d" % (nm, h), [P, P], F32))
                    for nm in ("NA", "RA", "uw"):
                        setattr(Wn, nm, S.sbuf("%s%d" % (nm, h), [P, 2 * P], F32))
                    Wn.NR = [S.sbuf("NR%d%d" % (h, i), [P, 2 * P], F32) for i in range(2)]
                    Wn.Xc = [S.sbuf("Xc%d%d" % (h, i), [P, P], F32) for i in range(2)]
                    Wn.esm = S.sbuf("esm%d" % h, [P, 4], F32)
                    Wn.bg = S.sbuf("bg%d" % h, [P, 1], F32)
                    Wn.ss = S.sbuf("ss%d" % h, [P, 1], F32)
                    Wn.oo = [S.sbuf("oo%d%d" % (h, i), [P, P], F32) for i in range(2)]
                    Wn.oT = [S.sbuf("oT%d%d" % (h, i), [P, P], F32) for i in range(2)]
                    Wk.append(Wn)
                state = [S.sbuf("state%d" % h, [P, P], F32) for h in range(2)]
                raw = S.sbuf("raw", [P, 6, 516], F32)
                raw2 = S.sbuf("rawb", [P, 6, 516], F32)
                acc = [S.sbuf("acc%d" % i, [P, 512], F32) for i in range(2)]
                sqb = S.sbuf("sqb", [P, 512], F32)
                lnb = S.sbuf("lnb", [P, 512], F32)
                rsb = S.sbuf("rsb", [P, 512], F32)
                qkv = [S.sbuf("qkv%d" % i, [P, 6, 512], F32) for i in range(2)]
                ofb = [[S.sub("of%d_%d" % (c, h), OF.t[c][:, h * P:(h + 1) * P]) for h in range(2)] for c in range(NCH)]

                def prep(seg, t0, n, dst, q):
                    load_fm(q, raw[:, :, :n + 4], raw2[:, :, :n + 4], 0, 6, seg, t0, n, 2)
                    for c in range(6):
                        a = acc[c % 2]
                        S.ts(a[:, :n], raw[:, c, 0:n], cw[:, c, 0:1], ALU.mult)
                        for j in range(1, 5):
                            S.stt(a[:, :n], raw[:, c, j:j + n], cw[:, c, j:j + 1], a[:, :n], ALU.mult, ALU.add)
                        if c >= 4:
                            S.act(dst[:, c, :n], a[:, :n], AF.Silu)
                        else:
                            S.act(a[:, :n], a[:, :n], AF.Silu)
                            S.act(sqb[:, :n], a[:, :n], AF.Square)
                            pb = B3[c % 2]
                            S.matmul(pb[:, :n], C.ones[:], sqb[:, :n])
                            S.act(lnb[:, :n], pb[:, :n], AF.Ln, bias=C.eps[:, 0:1])
                            S.act(rsb[:, :n], lnb[:, :n], AF.Exp, scale=-0.5)
                            S.stt(dst[:, c, :n], a[:, :n], (128.0 ** -0.5) if c < 2 else 1.0, rsb[:, :n], ALU.mult, ALU.mult)

                def unit(hl, dr, gp, qv, kv, vv):
                    col = dr * 2 + hl
                    g = gall[:, gp, col:col + 1]
                    nb = negb[:, gp, col:col + 1]
                    bt = ball[:, gp, col:col + 1]
                    Wn = Wk[hl]
                    bA, b1, b2, b3 = BA[hl], B1[hl], B2[hl], B3[hl]
                    Tri, Xm, Val, SVal = (M["le"], M["gt"], M["ge"], M["gt"]) if dr == 0 else (M["ge"], M["lt"], M["le"], M["lt"])
                    S.ts(Wn.X[:], Xm[:], g, ALU.mult)
                    S.matmul(bA[:, 0:128], Tri[:], Wn.X[:])
                    S.matmul(bA[:, 128:129], Tri[:], g)
                    S.matmul(bA[:, 129:130], Xm[:], g)
                    S.matmul(bA[:, 130:131], halfA[:], g)
                    S.matmul(bA[:, 131:132], halfB[:], g)
                    S.matmul(b1[:, 0:128], kv, kv)
                    S.matmul(b1[:, 128:256], qv, kv)
                    S.transpose(bA[:, 256:384], kv, C.ident[:])
                    S.transpose(bA[:, 384:512], vv, C.ident[:])
                    yield
                    S.act(Wn.Dm[:], bA[:, 0:128], AF.Exp)
                    S.act(Wn.esm[:], bA[:, 128:132], AF.Exp)
                    S.tt(Wn.bg[:], Wn.esm[:, 0:1], bt, ALU.mult)
                    S.act(Wn.kdec[:], bA[:, 256:384], AF.Identity, scale=Wn.esm[:, 1:2])
                    S.act(Wn.vb[:], bA[:, 384:512], AF.Identity, scale=bt)
                    S.act(Wn.kbg[:], bA[:, 256:384], AF.Identity, scale=Wn.bg[:, 0:1])
                    S.tt(Wn.Dv[:], Wn.Dm[:], Val[:], ALU.mult)
                    S.tt(Wn.Ds[:], Wn.Dm[:], SVal[:], ALU.mult)
                    S.stt(Wn.NA[:, 0:128], b1[:, 0:128], nb, Wn.Ds[:], ALU.mult, ALU.mult)
                    S.tt(Wn.NA[:, 128:256], b1[:, 128:256], Wn.Dv[:], ALU.mult)
                    yield
                    S.transpose(b1[:, 256:384], Wn.NA[:, 0:128], C.ident[:])
                    S.transpose(b1[:, 384:512], Wn.NA[:, 128:256], C.ident[:])
                    S.copy(Wn.RA[:], b1[:, 256:512])
                    X = Wn.Xc[0]
                    S.tt(X[:], Wn.RA[:, 0:128], C.ident[:], ALU.add)
                    yield
                    Ncur = Wn.NA[:, 0:128]
                    Rcur = Wn.RA[:, 0:128]
                    for lev in range(5):
                        NR = Wn.NR[lev % 2]
                        S.matmul(b2[:, 0:128], Rcur, Ncur)
                        if lev < 4:
                            S.matmul(b2[:, 128:256], Ncur, Rcur)
                            S.copy(NR[:], b2[:, 0:256])
                        else:
                            S.copy(NR[:, 0:128], b2[:, 0:128])
                        yield
                        S.matmul(b2[:, 256:384], NR[:, 0:128], X[:])
                        Xn = Wn.Xc[(lev + 1) % 2]
                        S.tt(Xn[:], X[:], b2[:, 256:384], ALU.add)
                        X = Xn
                        Ncur = NR[:, 0:128]
                        Rcur = NR[:, 128:256]
                        yield
                    S.matmul(b3[:, 0:128], X[:], Wn.vb[:])
                    S.matmul(b3[:, 128:256], Wn.kbg[:], X[:])
                    S.copy(Wn.uw[:], b3[:, 0:256])
                    yield
                    blocks = [(0, 64), (64, 128)] if dr == 0 else [(64, 128), (0, 64)]
                    Sst = state[hl]
                    for bi, (r0, r1) in enumerate(blocks):
                        reg = b3[:, 256:512] if bi == 0 else b3[:, 0:256]
                        S.matmul(reg[:, 0:128], Wn.uw[:, 128:256], Sst[:])
                        S.matmul(reg[:, 128:256], qv, Sst[:])
                        S.tt(Wn.vnew[r0:r1, :], Wn.uw[r0:r1, 0:128], reg[r0:r1, 0:128], ALU.subtract)
                        S.ts(Wn.oq[r0:r1, :], reg[r0:r1, 128:256], Wn.esm[r0:r1, 0:1], ALU.mult)
                        yield
                        S.matmul(b1[:, 0:128], Wn.kdec[r0:r1, :], Wn.vnew[r0:r1, :])
                        egX = Wn.esm[:, 2:3] if r0 == 0 else Wn.esm[:, 3:4]
                        S.stt(Sst[:], Sst[:], egX, b1[:, 0:128], ALU.mult, ALU.add)
                        yield
                    S.matmul(b1[:, 128:256], Wn.RA[:, 128:256], Wn.vnew[:])
                    S.tt(Wn.o[:], Wn.oq[:], b1[:, 128:256], ALU.add)
                    if dr == 0:
                        S.dma("pool", ofb[gp][hl][:], Wn.o[:])
                    else:
                        S.dma("sp", Wn.of_[:], ofb[gp][hl][:])
                        load_tok("sp", Wn.zt[:], Wn.zt2[:], gp, 256 + hl * P, P)
                        S.tt(Wn.o[:], Wn.o[:], Wn.of_[:], ALU.add)
                        S.act(Wn.t1[:], Wn.o[:], AF.Square, accum_out=Wn.ss[:, 0:1])
                        yield
                        S.act(Wn.ss[:], Wn.ss[:], AF.Ln, scale=1.0 / 128.0, bias=C.eps[:, 0:1])
                        S.act(Wn.ss[:], Wn.ss[:], AF.Exp, scale=-0.5)
                        S.act(Wn.zt[:], Wn.zt[:], AF.Silu)
                        S.stt(Wn.t1[:], Wn.o[:], Wn.ss[:, 0:1], nw[:], ALU.mult, ALU.mult)
                        oo = Wn.oo[gp % 2]
                        S.tt(oo[:], Wn.t1[:], Wn.zt[:], ALU.mult)
                        S.transpose(b1[:, 256:384], oo[:], C.ident[:])
                        oT = Wn.oT[gp % 2]
                        S.copy(oT[:], b1[:, 256:384])
                        s_, off = tokpos(gp)
                        S.dma("pool", dsub(mo_dst(hl, hl + 1, s_, off, P)[0]), oT[:])
                    yield

                for dr in range(2):
                    for h in range(2):
                        S.memset(state[h][:], 0.0)
                    segs = [("ctx", 0, LC), ("lat", LC, L)]
                    tl = []
                    for (seg, base, seglen) in segs:
                        tt_ = [(seg, base, t0, min(512, seglen - t0)) for t0 in range(0, seglen, 512)]
                        if dr == 1:
                            tt_ = tt_[::-1]
                        tl += tt_
                    for ti, (seg, base, t0, n) in enumerate(tl):
                        dst = qkv[ti % 2]
                        prep(seg, t0, n, dst, "sp" if ti % 2 else "pool")
                        prs = list(range(n // P))
                        if dr == 1:
                            prs = prs[::-1]
                        for pi in prs:
                            gp = (base + t0) // P + pi
                            sl = slice(pi * P, (pi + 1) * P)
                            gens = [unit(h, dr, gp, dst[:, 0 + h, sl], dst[:, 2 + h, sl], dst[:, 4 + h, sl]) for h in range(2)]
                            alive = [True, True]
                            while any(alive):
                                for h in range(2):
                                    if alive[h]:
                                        try:
                                            next(gens[h])
                                        except StopIteration:
                                            alive[h] = False

        def ph_R2(W, x1t, xout):
            with S.scope():
                alloc_small(S, C)
                PS = mk_ps()
                mods = S.sbuf("mods", [P, 48, 2], F32)
                ngt = S.sbuf("ngt", [P, 6, KC], F32)
                S.dma("sp", ngt[:], W.ng[:].rearrange("p (m c) -> p m c", c=KC))
                snt = S.sbuf("snt", [P, 4], F32)
                S.dma("sp", snt[:], W.snw[:])
                with S.scope():
                    stg = [S.sbuf("stgm%d" % i, [P, 6 * D], F32) for i in range(KC)]
                    compute_mods(S, C, cv, W.wada2, W.bada2, 6, stg, PS.g[0], mods)
                A2 = S.sbuf("A2", [P, KC, 2], F32)
                G3 = S.sbuf("G3", [P, KC, 2], F32)
                A4 = S.sbuf("A4", [P, KC, 2], F32)
                G5 = S.sbuf("G5", [P, KC, 2], F32)
                S.stt(A2[:], mods[:, 8:16, :], 1.0, bc(ngt[:, 2, :]), ALU.add, ALU.mult)
                S.tt(G3[:], mods[:, 16:24, :], bc(ngt[:, 3, :]), ALU.mult)
                S.stt(A4[:], mods[:, 32:40, :], 1.0, bc(ngt[:, 4, :]), ALU.add, ALU.mult)
                S.stt(G5[:], mods[:, 40:48, :], 0.5, bc(ngt[:, 5, :]), ALU.mult, ALU.mult)
                B2 = mods[:, 0:8, :]
                B4 = mods[:, 24:32, :]
                x2t = [S.sub("x2t%d" % j, X2.t[:, :, s0:s0 + n]) for j, (s0, n, col) in enumerate(tiles)]
                with S.scope():
                    wgb = S.sbuf("wgb", [P, KC, 3 * D], BF16)
                    wbb = S.sbuf("wbb", [P, 12, D], BF16)
                    wob = S.sbuf("wob", [P, KC, D], BF16)
                    with S.scope():
                        stages = [S.sbuf("wst%d" % i, [P, 2048], F32) for i in range(3)]
                        load_w(S, W.wg, wgb, D, 3 * D, stages)
                        load_w(S, W.wb, wbb, 1536, D, stages)
                        load_w(S, W.wo, wob, D, D, stages)
                    xt = S.sbuf("xm", [P, KC, 512], F32)
                    h = S.sbuf("hm", [P, KC, 512], BF16)
                    ost = S.sbuf("ostg", [P, 4, 512], F32)
                    ost2 = S.sbuf("ostg2", [P, 4, 512], F32)
                    ob16 = S.sbuf("ob16", [P, 12, 512], BF16)
                    yacc = S.sbuf("yacc", [P, 512], F32)
                    ybf = S.sbuf("ybf", [P, KC, 512], BF16)
                    yy = S.sbuf("yy", [P, KC, 512], F32)
                    gt = [S.sbuf("gt%d" % i, [P, 512], F32) for i in range(2)]
                    for j, (s0, n, col) in enumerate(tiles):
                        S.dma("sp", xt[:, :, :n], x1t[j][:])
                        norm_mod(S, C, xt, n, A2[:], B2, col, h, PS.ss)
                        for br in range(3):
                            for sc_, tgt in ((0, ost), (1, ost2)):
                                for r in range(2):
                                    if col == 0:
                                        sap = MOLG.ap()[2 * br:2 * br + 2, sc_, r, :, s0:s0 + n]
                                    else:
                                        sap = MOCG.ap()[r, 2 * br:2 * br + 2, sc_, :, 0:n]
                                    S.dma("sp" if r else "pool", tgt[:, 2 * r:2 * r + 2, :n], dsub(sap.rearrange("c p t -> p c t")))
                            blend(ost[:, :, :n], ost2[:, :, :n])
                            if br < 2:
                                S.copy(ob16[:, br * 4:(br + 1) * 4, :n], ost[:, :, :n], e="pool")
                            else:
                                rms_rstd(S, C, lambda c: ost[:, c, :n], n, 4, 512, PS.ss2, C.rstd2[:, :n])
                                for c in range(4):
                                    t = C.tmp[c % 2]
                                    S.tt(t[:, :n], ost[:, c, :n], C.rstd2[:, :n], ALU.mult)
                                    S.act(ob16[:, 8 + c, :n], t[:, :n], AF.Copy, scale=snt[:, c:c + 1])
                        for d in range(KC):
                            for br in range(3):
                                pg = PS.g[br % 2]
                                pu = PS.u[br % 2]
                                cg = br * KC + d
                                for k in range(KC):
                                    S.matmul(pg[:, :n], wgb[:, k, cg * P:(cg + 1) * P], h[:, k, :n], start=(k == 0), stop=(k == KC - 1))
                                for k in range(4):
                                    S.matmul(pu[:, :n], wbb[:, br * 4 + k, d * P:(d + 1) * P], ob16[:, br * 4 + k, :n], start=(k == 0), stop=(k == 3))
                                g = gt[br % 2]
                                S.act(g[:, :n], pg[:, :n], AF.Sigmoid)
                                if br == 0:
                                    S.tt(yacc[:, :n], g[:, :n], pu[:, :n], ALU.mult)
                                else:
                                    t = C.tmp[br % 2]
                                    S.tt(t[:, :n], g[:, :n], pu[:, :n], ALU.mult)
                                    if br == 1:
                                        S.tt(yacc[:, :n], yacc[:, :n], t[:, :n], ALU.add)
                                    else:
                                        S.tt(ybf[:, d, :n], yacc[:, :n], t[:, :n], ALU.add)
                        for d in range(KC):
                            py = PS.y[d % 2]
                            for k in range(KC):
                                S.matmul(py[:, :n], wob[:, k, d * P:(d + 1) * P], ybf[:, k, :n], start=(k == 0), stop=(k == KC - 1))
                            S.copy(yy[:, d, :n], py[:, :n], e="dve")
                        rms_rstd(S, C, lambda c: yy[:, c, :n], n, KC, D, PS.ss2, C.rstd2[:, :n])
                        for c in range(KC):
                            t = C.tmp[c % 2]
                            S.tt(t[:, :n], yy[:, c, :n], C.rstd2[:, :n], ALU.mult)
                            S.stt(xt[:, c, :n], t[:, :n], G3[:, c, col:col + 1], xt[:, c, :n], ALU.mult, ALU.add)
                        S.dma("pool", x2t[j][:], xt[:, :, :n])
                with S.scope():
                    w1b = S.sbuf("w1b", [P, KC, 2 * DFF], BF16)
                    w2b = S.sbuf("w2b", [P, FC, D], BF16)
                    with S.scope():
                        stages = [S.sbuf("wst%d" % i, [P, 2048], F32) for i in range(3)]
                        load_w(S, W.w1b, w1b, D, 2 * DFF, stages)
                        load_w(S, W.w2b, w2b, DFF, D, stages)
                    alloc_ffn_work(S, C)
                    ffn_sweep(S, C, tiles, lambda j: x2t[j][:], lambda j: xout[j][:], w1b, w2b, A4[:], B4, G5[:], PS)

        xin = [S.sub("xin%d" % j, xT.t[:, :, s0:s0 + n]) for j, (s0, n, col) in enumerate(tiles)]
        for i in range(depth):
            W = Wl[i]
            x1t = [S.sub("x1t%d_%d" % (i, j), X1.t[:, :, s0:s0 + n]) for j, (s0, n, col) in enumerate(tiles)]
            dbg = "Z"
            ph_R1(W, xin, x1t)
            if dbg >= "B":
                gather_P()
            if dbg >= "C":
                ph_Mdiff(W)
            if dbg >= "D":
                ph_Mssd(W)
            if dbg >= "E":
                ph_Mgdn(W)
            if dbg >= "F":
                gather_M()
            last = i == depth - 1
            dstT = yT if last else Xs
            xout = [S.sub("xo%d_%d" % (i, j), dstT.t[:, :, s0:s0 + n]) for j, (s0, n, col) in enumerate(tiles)]
            ph_R2(W, x1t, xout)
            xin = xout
        S.barrier()
    return nc


def fm(a):
    T, F = a.shape
    return np.ascontiguousarray(a.T.reshape(F // P, P, T).transpose(1, 0, 2))

def unfm(a):
    p, C, T = a.shape
    return np.ascontiguousarray(a.transpose(2, 1, 0).reshape(T, C * P))

def vec_fm(v):
    return np.ascontiguousarray(v.reshape(-1, P).T)

def r1_cols():
    sw = np.arange(512) ^ 1
    cols = []
    cols += list(range(0, 2048))
    cols += list(range(2064, 2576))
    cols += list(2064 + sw)
    cols += list(range(2576, 3088))
    cols += list(2576 + sw)
    cols += list(range(3088, 3600))
    cols += list(range(3600, 5136))
    cols += list(range(2048, 2064)) + list(range(5136, 5152)) + [0] * 96
    cols = np.array(cols)
    assert len(cols) == 49 * 128
    return cols

def r1_inputs(inp, i, core, NT, NCX, xcur, ctxcur):
    b, s = core // 2, core % 2
    tok = np.concatenate([xcur[b, s * NT:(s + 1) * NT], ctxcur[b, s * NCX:(s + 1) * NCX]], 0)
    cv = np.stack([inp["c"][b], inp["c_ctx"]], -1)
    cv = np.ascontiguousarray(cv.reshape(8, P, 2).transpose(1, 0, 2))
    return {
        "xT": fm(tok),
        "cv": cv,
        "wada": np.ascontiguousarray(inp["w_ada"][i][:, :5 * 1024]),
        "bada": vec_fm(inp["b_ada"][i][:5 * 1024]),
        "ng": np.ascontiguousarray(inp["norm_g"][i].reshape(6, 8, P).transpose(2, 0, 1).reshape(P, 48)),
        "w1": np.ascontiguousarray(inp["w_ffn_in"][i, 0]),
        "w2": np.ascontiguousarray(inp["w_ffn_out"][i, 0]),
        "win": np.ascontiguousarray(inp["w_in"][i][:, r1_cols()]),
    }

def r2_inputs(inp, i, core, x1T, oaT, obT, ocT):
    b = core // 2
    cv = np.stack([inp["c"][b], inp["c_ctx"]], -1)
    cv = np.ascontiguousarray(cv.reshape(8, P, 2).transpose(1, 0, 2))
    return {
        "x1T": x1T, "oaT": oaT, "obT": obT, "ocT": ocT, "cv": cv,
        "wada": np.ascontiguousarray(inp["w_ada"][i][:, 3 * 1024:]),
        "bada": vec_fm(inp["b_ada"][i][3 * 1024:]),
        "ng": np.ascontiguousarray(inp["norm_g"][i].reshape(6, 8, P).transpose(2, 0, 1).reshape(P, 48)),
        "snw": vec_fm(inp["ssd_norm_w"][i]),
        "wg": np.ascontiguousarray(inp["w_in"][i][:, 5152:8224]),
        "wb": np.ascontiguousarray(inp["w_branch"][i].reshape(1536, 1024)),
        "wo": np.ascontiguousarray(inp["w_out"][i]),
        "w1": np.ascontiguousarray(inp["w_ffn_in"][i, 1]),
        "w2": np.ascontiguousarray(inp["w_ffn_out"][i, 1]),
    }

def split_P(PT_cores, NT, NCX, b):
    a0, a1 = PT_cores[2 * b], PT_cores[2 * b + 1]
    lat = np.concatenate([a0[:, :, :NT], a1[:, :, :NT]], 2)
    cx = np.concatenate([a0[:, :, NT:], a1[:, :, NT:]], 2)
    return lat, cx

def mdiff_inputs(inp, i, core, lat, cx):
    b, hh = core // 2, core % 2
    hs = [2 * hh, 2 * hh + 1]
    L = lat.shape[2]; LC = cx.shape[2]
    qT = lat[[16 + h for h in hs]]
    qsT = lat[[20 + h for h in hs]]
    kT = np.concatenate([lat[[24 + h for h in hs]], cx[[24 + h for h in hs]]], 2)
    ksT = lat[[28 + h for h in hs]]
    qcT = cx[[16 + h for h in hs]]
    v = np.concatenate([lat[[32 + h for h in hs]], cx[[32 + h for h in hs]]], 2)
    LK = L + LC
    v = v.transpose(2, 0, 1).reshape(LK // P, P, 256).transpose(1, 0, 2)
    lam_init = 0.8 - 0.6 * np.exp(-0.3 * i)
    return {"qT": np.ascontiguousarray(qT), "qsT": np.ascontiguousarray(qsT), "kT": np.ascontiguousarray(kT),
            "ksT": np.ascontiguousarray(ksT), "qcT": np.ascontiguousarray(qcT), "v": np.ascontiguousarray(v),
            "lam": np.ascontiguousarray(np.broadcast_to(inp["diff_lambda"][i].reshape(1, 256), (P, 256))),
            "nw": np.ascontiguousarray(inp["diff_norm_w"][i].reshape(P, 1)),
            "li": np.full((P, 1), lam_init, np.float32)}

def pad2(a):
    return np.pad(a, ((0, 0), (0, 0), (2, 2)))

def tokmaj(a):
    Cc, p, T = a.shape
    return np.ascontiguousarray(a.transpose(2, 0, 1).reshape(T // P, P, Cc * P).transpose(1, 0, 2))

def mssd_inputs(inp, i, core, lat, cx):
    b, g = core // 2, core % 2
    ch = [40 + 2 * g, 41 + 2 * g, 44 + g, 46 + g]
    wcols = np.concatenate([np.arange(256 * g, 256 * g + 256), 512 + 128 * g + np.arange(128), 768 + 128 * g + np.arange(128)])
    cwv = inp["ssd_conv_w"][i][:, wcols]
    cbv = inp["ssd_conv_b"][i][wcols]
    zc = [36 + 2 * g, 37 + 2 * g]
    z = np.concatenate([tokmaj(cx[zc]), tokmaj(lat[zc])], 1)
    rows = [16 + d * 8 + 4 * g + h for d in range(2) for h in range(4)]
    dtl = lat[48][rows]; dtc = cx[48][rows]
    dt = np.concatenate([dtc, dtl], 1)
    LT = dt.shape[1]
    dt = np.ascontiguousarray(dt.T.reshape(LT // P, P, 8).transpose(1, 0, 2))
    hsel = [4 * g + h for h in range(4)]
    return {"xbcl": np.ascontiguousarray(pad2(lat[ch])), "xbcc": np.ascontiguousarray(pad2(cx[ch])),
            "cw": np.ascontiguousarray(cwv.T.reshape(4, P, 5).transpose(1, 0, 2)),
            "cb": np.ascontiguousarray(cbv.reshape(4, P).T),
            "z": z, "dt": dt,
            "dtb": np.ascontiguousarray(np.broadcast_to(inp["ssd_dt_bias"][i][:, hsel].reshape(1, 8), (P, 8))),
            "alog": np.ascontiguousarray(np.broadcast_to(inp["ssd_a_log"][i][:, hsel].reshape(1, 8), (P, 8))),
            "dskip": np.ascontiguousarray(np.broadcast_to(inp["ssd_d"][i][hsel].reshape(1, 4), (P, 4)))}

def mgdn_inputs(inp, i, core, lat, cx):
    b, hh = core // 2, core % 2
    hs = [2 * hh, 2 * hh + 1]
    ch = [0 + hs[0], 0 + hs[1], 4 + hs[0], 4 + hs[1], 8 + hs[0], 8 + hs[1]]
    wcols = np.concatenate([off + h * 128 + np.arange(128) for off in (0, 512, 1024) for h in hs])
    cwv = inp["gdn_conv_w"][i][:, wcols]
    zc = [12 + hs[0], 12 + hs[1]]
    z = np.concatenate([tokmaj(cx[zc]), tokmaj(lat[zc])], 1)
    def small(rows):
        v = np.concatenate([cx[48][rows], lat[48][rows]], 1)
        LT = v.shape[1]
        return np.ascontiguousarray(v.T.reshape(LT // P, P, 4).transpose(1, 0, 2))
    arows = [d * 4 + h for d in range(2) for h in hs]
    brows = [8 + d * 4 + h for d in range(2) for h in hs]
    return {"qkvl": np.ascontiguousarray(pad2(lat[ch])), "qkvc": np.ascontiguousarray(pad2(cx[ch])),
            "cw": np.ascontiguousarray(cwv.T.reshape(6, P, 5).transpose(1, 0, 2)),
            "z": z, "araw": small(arows), "braw": small(brows),
            "alog": np.ascontiguousarray(np.broadcast_to(inp["gdn_a_log"][i][:, hs].reshape(1, 4), (P, 4))),
            "dtb": np.ascontiguousarray(np.broadcast_to(inp["gdn_dt_bias"][i][:, hs].reshape(1, 4), (P, 4))),
            "nw": np.ascontiguousarray(np.broadcast_to(inp["gdn_norm_w"][i].reshape(1, P), (P, P)))}


def fused_cols():
    sw = np.arange(128) ^ 1
    ar = np.arange(128)
    fmc, tkc = [], []
    for g in (0, 1):
        hs = [2 * g, 2 * g + 1]
        for off in (0, 512, 1024):
            for h in hs:
                fmc += list(off + h * 128 + ar)
        for h in hs:
            fmc += list(2064 + h * 128 + ar)
        for h in hs:
            fmc += list(2064 + h * 128 + sw)
        for h in hs:
            fmc += list(2576 + h * 128 + ar)
        for h in hs:
            fmc += list(2576 + h * 128 + sw)
        fmc += list(4112 + g * 256 + np.arange(256))
        fmc += list(4112 + 512 + g * 128 + ar)
        fmc += list(4112 + 768 + g * 128 + ar)
    for g in (0, 1):
        hs = [2 * g, 2 * g + 1]
        for h in hs:
            tkc += list(3088 + h * 128 + ar)
        for h in hs:
            tkc += list(1536 + h * 128 + ar)
        tkc += list(3600 + g * 256 + np.arange(256))
        tkc += [2048 + d * 4 + h for d in range(2) for h in hs]
        tkc += [2056 + d * 4 + h for d in range(2) for h in hs]
        tkc += [5136 + d * 8 + 4 * g + h for d in range(2) for h in range(4)]
    cols = np.array(fmc + tkc)
    assert len(cols) == 36 * 128 + 1568
    return cols


def fused_inputs(inp, core, NT, NCX, depth=2):
    b, hh = core // 2, core % 2
    s = hh
    tok = np.concatenate([inp["x"][b, s * NT:(s + 1) * NT], inp["ctx"][b, s * NCX:(s + 1) * NCX]], 0)
    cv = np.stack([inp["c"][b], inp["c_ctx"]], -1)
    cv = np.ascontiguousarray(cv.reshape(8, P, 2).transpose(1, 0, 2))
    d = {"xT": fm(tok), "cv": cv, "selv": np.ascontiguousarray(np.broadcast_to(np.array([[1.0 - hh, float(hh)]], np.float32), (P, 2)))}
    cols = fused_cols()
    hs = [2 * hh, 2 * hh + 1]
    g = hh
    for i in range(depth):
        sfx = "_%d" % i
        d["wada1" + sfx] = np.ascontiguousarray(inp["w_ada"][i][:, :5 * 1024])
        d["bada1" + sfx] = vec_fm(inp["b_ada"][i][:5 * 1024])
        d["wada2" + sfx] = np.ascontiguousarray(inp["w_ada"][i][:, 3 * 1024:])
        d["bada2" + sfx] = vec_fm(inp["b_ada"][i][3 * 1024:])
        d["ng" + sfx] = np.ascontiguousarray(inp["norm_g"][i].reshape(6, 8, P).transpose(2, 0, 1).reshape(P, 48))
        d["w1a" + sfx] = np.ascontiguousarray(inp["w_ffn_in"][i, 0])
        d["w2a" + sfx] = np.ascontiguousarray(inp["w_ffn_out"][i, 0])
        d["w1b" + sfx] = np.ascontiguousarray(inp["w_ffn_in"][i, 1])
        d["w2b" + sfx] = np.ascontiguousarray(inp["w_ffn_out"][i, 1])
        d["win" + sfx] = np.ascontiguousarray(inp["w_in"][i][:, cols])
        d["wg" + sfx] = np.ascontiguousarray(inp["w_in"][i][:, 5152:8224])
        d["wb" + sfx] = np.ascontiguousarray(inp["w_branch"][i].reshape(1536, 1024))
        d["wo" + sfx] = np.ascontiguousarray(inp["w_out"][i])
        d["snw" + sfx] = vec_fm(inp["ssd_norm_w"][i])
        lam_init = 0.8 - 0.6 * np.exp(-0.3 * i)
        d["lam" + sfx] = np.ascontiguousarray(np.broadcast_to(inp["diff_lambda"][i].reshape(1, 256), (P, 256)))
        d["dnw" + sfx] = np.ascontiguousarray(inp["diff_norm_w"][i].reshape(P, 1))
        d["li" + sfx] = np.full((P, 1), lam_init, np.float32)
        wcols = np.concatenate([np.arange(256 * g, 256 * g + 256), 512 + 128 * g + np.arange(128), 768 + 128 * g + np.arange(128)])
        d["scw" + sfx] = np.ascontiguousarray(inp["ssd_conv_w"][i][:, wcols].T.reshape(4, P, 5).transpose(1, 0, 2))
        d["scb" + sfx] = np.ascontiguousarray(inp["ssd_conv_b"][i][wcols].reshape(4, P).T)
        hsel = [4 * g + h for h in range(4)]
        d["sdtb" + sfx] = np.ascontiguousarray(np.broadcast_to(inp["ssd_dt_bias"][i][:, hsel].reshape(1, 8), (P, 8)))
        d["salog" + sfx] = np.ascontiguousarray(np.broadcast_to(inp["ssd_a_log"][i][:, hsel].reshape(1, 8), (P, 8)))
        d["sdsk" + sfx] = np.ascontiguousarray(np.broadcast_to(inp["ssd_d"][i][hsel].reshape(1, 4), (P, 4)))
        gcols = np.concatenate([off + h * 128 + np.arange(128) for off in (0, 512, 1024) for h in hs])
        d["gcw" + sfx] = np.ascontiguousarray(inp["gdn_conv_w"][i][:, gcols].T.reshape(6, P, 5).transpose(1, 0, 2))
        d["galog" + sfx] = np.ascontiguousarray(np.broadcast_to(inp["gdn_a_log"][i][:, hs].reshape(1, 4), (P, 4)))
        d["gdtb" + sfx] = np.ascontiguousarray(np.broadcast_to(inp["gdn_dt_bias"][i][:, hs].reshape(1, 4), (P, 4)))
        d["gnw" + sfx] = np.ascontiguousarray(np.broadcast_to(inp["gdn_norm_w"][i].reshape(1, P), (P, P)))
    return d


from concourse.bass_utils import run_bass_kernel_spmd


def kernel(**inp):
    inp = {k: np.ascontiguousarray(np.asarray(v), dtype=np.float32) for k, v in inp.items()}
    B, L, Dm = inp["x"].shape
    LC = inp["ctx"].shape[1]
    NT, NCX = L // 2, LC // 2
    nc = build_fused(L, LC, 2)
    ims = [fused_inputs(inp, c, NT, NCX, 2) for c in range(8)]
    res = run_bass_kernel_spmd(nc, ims, core_ids=list(range(8))).results
    out = np.empty((B, L, Dm), np.float32)
    for c in range(8):
        b, s = c // 2, c % 2
        out[b, s * NT:(s + 1) * NT] = unfm(res[c]["yT"])[:NT]
    return out
```

```python
import numpy as np
from contextlib import ExitStack, contextmanager
import concourse.bass as bass
import concourse.mybir as mybir

F32 = mybir.dt.float32
BF16 = mybir.dt.bfloat16
AF = mybir.ActivationFunctionType
ALU = mybir.AluOpType
AX = mybir.AxisListType


class Buf:
    __slots__ = ("name", "t", "w", "r", "dsem", "dcnt", "space", "dkey")

    def __init__(self, name, t, space="sbuf"):
        self.name = name
        self.space = space
        self.t = t
        self.w = {}
        self.r = {}
        self.dsem = None
        self.dcnt = 0
        self.dkey = None

    def __getitem__(self, idx):
        return View(self, self.t[idx])


class View:
    __slots__ = ("buf", "ap")

    def __init__(self, buf, ap):
        self.buf = buf
        self.ap = ap

    def __getitem__(self, idx):
        return View(self.buf, self.ap[idx])

    def rearrange(self, *a, **k):
        return View(self.buf, self.ap.rearrange(*a, **k))

    def bitcast(self, *a, **k):
        return View(self.buf, self.ap.bitcast(*a, **k))

    def to_broadcast(self, *a, **k):
        return View(self.buf, self.ap.to_broadcast(*a, **k))


class Sched:
    def __init__(self, nc, stack):
        self.nc = nc
        self.stack = stack
        self.eng = {"pe": nc.tensor, "act": nc.scalar, "dve": nc.vector, "pool": nc.gpsimd, "sp": nc.sync}
        self.sem = {}
        self.cnt = {}
        self.seen = {}
        for e in self.eng:
            self.sem[e] = stack.enter_context(nc.semaphore("prog_" + e))
            self.cnt[e] = 0
            self.seen[e] = {}
        self.semobj = {e: self.sem[e] for e in self.eng}
        self.nbuf = 0
        self.ninst = 0
        self.root = stack
        self.dbufs = []
        self.sempool = []
        self.scope_bufs = [[]]
        self.nsem = 0

    def sbuf(self, name, shape, dt):
        self.nbuf += 1
        name = "%s_%d" % (name, self.nbuf)
        t = self.stack.enter_context(self.nc.sbuf_tensor(name, list(shape), dt))
        b = Buf(name, t)
        self.scope_bufs[-1].append(b)
        return b

    def psum(self, name, shape, dt=F32):
        self.nbuf += 1
        name = "%s_%d" % (name, self.nbuf)
        t = self.stack.enter_context(self.nc.psum_tensor(name, list(shape), dt))
        return Buf(name, t, "psum")

    def dram(self, name, shape, dt, kind="Internal"):
        t = self.nc.dram_tensor(name, list(shape), dt, kind=kind)
        return Buf(name, t.ap(), "dram")

    def sub(self, name, ap, space="dram"):
        return Buf(name, ap, space)

    def barrier(self):
        deps = {e: self.cnt[e] for e in self.eng if self.cnt[e] > 0}
        for b in self.dbufs:
            if deps.get(b.dkey, 0) < b.dcnt:
                deps[b.dkey] = b.dcnt
        for e in self.eng:
            self._need(e, dict(deps))

    @contextmanager
    def scope(self):
        old = self.stack
        self.scope_bufs.append([])
        with ExitStack() as st:
            self.stack = st
            yield
            self.barrier()
        self.stack = old
        dead = self.scope_bufs.pop()
        for b in dead:
            if b.dsem is not None:
                self.sempool.append((b.dsem, b.dcnt, b.dkey))
        deadids = set(id(b) for b in dead)
        self.dbufs = [b for b in self.dbufs if id(b) not in deadids] + [b for b in dead if b.dsem is not None][:0]
        self._dead_keep = getattr(self, "_dead_keep", []) + dead

    def _need(self, e, deps):
        seen = self.seen[e]
        for k, v in deps.items():
            if e == "pe" and k == "pe":
                continue
            if seen.get(k, 0) < v:
                seen[k] = v
                self.eng[e].wait_ge(self.semobj[k], v)

    def _collect(self, reads, writes):
        deps = {}
        for v in reads:
            for k, val in v.buf.w.items():
                if deps.get(k, 0) < val:
                    deps[k] = val
        for v in writes:
            for d in (v.buf.w, v.buf.r):
                for k, val in d.items():
                    if deps.get(k, 0) < val:
                        deps[k] = val
        return deps

    def _mark(self, reads, writes, key, val):
        for v in reads:
            b = v.buf
            if b.r.get(key, 0) < val:
                b.r[key] = val
        for v in writes:
            b = v.buf
            b.w = {key: val}
            b.r = {}

    def op(self, e, fn, reads, writes):
        self._need(e, self._collect(reads, writes))
        ins = fn()
        self.cnt[e] += 1
        ins.then_inc(self.sem[e], 1)
        self._mark(reads, writes, e, self.cnt[e])
        self.ninst += 1
        return ins

    def dma(self, e, out, in_, sbuf_side=None, **kw):
        if sbuf_side is None:
            sbuf_side = out.buf if out.buf.space != "dram" else in_.buf
        b = sbuf_side
        if b.dsem is None:
            if self.sempool:
                b.dsem, b.dcnt, b.dkey = self.sempool.pop()
            else:
                self.nsem += 1
                b.dsem = self.root.enter_context(self.nc.semaphore("d_%d" % self.nsem))
                b.dkey = "d%d" % self.nsem
                self.semobj[b.dkey] = b.dsem
            self.dbufs.append(b)
        self._need(e, self._collect([in_], [out]))
        ins = self.eng[e].dma_start(out=out.ap, in_=in_.ap, **kw)
        b.dcnt += 16
        ins.then_inc(b.dsem, 16)
        self._mark([in_], [out], b.dkey, b.dcnt)
        self.ninst += 1
        return ins

    def wait_all(self, e, bufs):
        deps = {}
        for b in bufs:
            for d in (b.w, b.r):
                for k, val in d.items():
                    if deps.get(k, 0) < val:
                        deps[k] = val
        self._need(e, deps)

    def matmul(self, out, lhsT, rhs, start=True, stop=True, acc_reads=True):
        rd = [lhsT, rhs]
        return self.op("pe", lambda: self.nc.tensor.matmul(out.ap, lhsT.ap, rhs.ap, start=start, stop=stop),
                       rd, [out])

    def transpose(self, out, in_, ident):
        return self.op("pe", lambda: self.nc.tensor.transpose(out.ap, in_.ap, ident.ap), [in_, ident], [out])

    def act(self, out, in_, func, bias=None, scale=None, accum_out=None, e="act"):
        rd = [in_]
        kw = {}
        if bias is not None:
            if isinstance(bias, View):
                rd.append(bias)
                kw["bias"] = bias.ap
            else:
                kw["bias"] = bias
        if scale is not None:
            if isinstance(scale, View):
                rd.append(scale)
                kw["scale"] = scale.ap
            else:
                kw["scale"] = scale
        wr = [out]
        if accum_out is not None:
            wr.append(accum_out)
            kw["accum_out"] = accum_out.ap
        return self.op("act", lambda: self.nc.scalar.activation(out.ap, in_.ap, func, **kw), rd, wr)

    def _ve(self, e):
        return self.nc.vector if e == "dve" else self.nc.gpsimd

    def copy(self, out, in_, e="dve"):
        if e == "act":
            return self.op("act", lambda: self.nc.scalar.copy(out.ap, in_.ap), [in_], [out])
        return self.op(e, lambda: self._ve(e).tensor_copy(out.ap, in_.ap), [in_], [out])

    def tt(self, out, a, b, op, e="dve"):
        return self.op(e, lambda: self._ve(e).tensor_tensor(out.ap, a.ap, b.ap, op), [a, b], [out])

    def ts(self, out, a, s1, op0, s2=None, op1=None, accum_out=None, e="dve"):
        rd = [a]
        s1v = s1.ap if isinstance(s1, View) else s1
        s2v = s2.ap if isinstance(s2, View) else s2
        if isinstance(s1, View):
            rd.append(s1)
        if isinstance(s2, View):
            rd.append(s2)
        wr = [out]
        kw = {}
        if op1 is not None:
            kw["op1"] = op1
        if accum_out is not None:
            kw["accum_out"] = accum_out.ap
            wr.append(accum_out)
        if s2 is None and op1 is None and accum_out is None:
            return self.op(e, lambda: self._ve(e).tensor_single_scalar(out.ap, a.ap, s1v, op0), rd, wr)
        return self.op(e, lambda: self._ve(e).tensor_scalar(out.ap, a.ap, s1v, s2v, op0, **kw), rd, wr)

    def stt(self, out, a, s, b, op0, op1, e="dve"):
        rd = [a, b]
        sv = s.ap if isinstance(s, View) else s
        if isinstance(s, View):
            rd.append(s)
        return self.op(e, lambda: self._ve(e).scalar_tensor_tensor(out.ap, a.ap, sv, b.ap, op0, op1), rd, [out])

    def reduce(self, out, in_, op, axis=AX.X, e="dve"):
        return self.op(e, lambda: self._ve(e).tensor_reduce(out.ap, in_.ap, axis, op), [in_], [out])

    def memset(self, out, val, e="dve"):
        return self.op(e, lambda: self._ve(e).memset(out.ap, val), [], [out])

    def recip(self, out, in_):
        return self.op("dve", lambda: self.nc.vector.reciprocal(out.ap, in_.ap), [in_], [out])


P = 128
D = 1024
KC = 8
DFF = 2816
FC = 22
EPS = 1e-6


class NS:
    pass


def mk_consts(S, nc):
    C = NS()
    C.ones = S.sbuf("ones", [P, P], F32)
    S.memset(C.ones[:], 1.0)
    C.eps = S.sbuf("epsc", [P, 1], F32)
    S.memset(C.eps[:], EPS)
    C.ident = S.sbuf("ident", [P, P], F32)
    S.memset(C.ident[:], 1.0, e="pool")
    S.op("pool", lambda: nc.gpsimd.affine_select(C.ident.t[:], C.ident.t[:], [[-1, P]], ALU.is_equal, 0.0,
                                                 base=0, channel_multiplier=1), [C.ident[:]], [C.ident[:]])
    return C


def load_w(S, wd, dst, K, N, stages, blk=2048, col0=0):
    engs = ["pool", "dve", "act"]
    i = 0
    for k in range(K // P):
        for c0 in range(0, N, blk):
            w = min(blk, N - c0)
            st = stages[i % len(stages)]
            S.dma("sp", st[:, :w], wd[k * P:(k + 1) * P, col0 + c0:col0 + c0 + w])
            S.copy(dst[:, k, c0:c0 + w], st[:, :w], e=engs[i % 3])
            i += 1


def rms_rstd(S, C, src, n, nch, dim, ps, out):
    for c in range(nch):
        sq = C.sq[c % 2]
        S.act(sq[:, :n], src(c), AF.Square)
        S.matmul(ps[:, :n], C.ones[:], sq[:, :n], start=(c == 0), stop=(c == nch - 1))
    S.act(C.lnt[:, :n], ps[:, :n], AF.Ln, scale=1.0 / dim, bias=C.eps[:, 0:1])
    S.act(out, C.lnt[:, :n], AF.Exp, scale=-0.5)


def norm_mod(S, C, xt, n, A, B, col, h, ps):
    rms_rstd(S, C, lambda c: xt[:, c, :n], n, KC, D, ps, C.rstd[:, :n])
    for c in range(KC):
        t = C.tmp[c % 2]
        S.tt(t[:, :n], xt[:, c, :n], C.rstd[:, :n], ALU.mult)
        S.act(h[:, c, :n], t[:, :n], AF.Identity, scale=A[:, c, col:col + 1], bias=B[:, c, col:col + 1])


def compute_mods(S, C, cv, wada, bada, nmod, stg, psm, mods):
    scv = S.sbuf("scv", [P, KC, 2], F32)
    cvt = S.sbuf("cvt", [P, KC, 2], F32)
    S.dma("sp", cvt[:], cv[:])
    S.act(scv[:], cvt[:], AF.Silu)
    nn = nmod * KC
    for k in range(KC):
        S.dma("sp" if k % 2 else "pool", stg[k][:, :nn * P], wada[k * P:(k + 1) * P, :])
    for j in range(nn):
        for k in range(KC):
            S.matmul(psm[:, 2 * j:2 * j + 2], stg[k][:, j * P:(j + 1) * P], scv[:, k, :], start=(k == 0), stop=(k == KC - 1))
    bt = S.sbuf("badat", [P, nn], F32)
    S.dma("sp", bt[:], bada[:])
    S.tt(mods[:], psm[:, 0:2 * nn].rearrange("p (j t) -> p j t", t=2),
         View(bt, bt.t[:].rearrange("p (j o) -> p j o", o=1).to_broadcast([P, nn, 2])), ALU.add)


def ffn_sweep(S, C, tiles, x_in, x_out, w1b, w2b, A, B, G, PS):
    for j, (s0, n, col) in enumerate(tiles):
        xt = C.xt[j % 2]
        S.dma("sp", xt[:, :, :n], x_in(j))
        norm_mod(S, C, xt, n, A, B, col, C.h, PS.ss)
        for f in range(FC):
            pg = PS.g[f % 2]
            pu = PS.u[f % 2]
            for k in range(KC):
                S.matmul(pg[:, :n], w1b[:, k, f * P:(f + 1) * P], C.h[:, k, :n], start=(k == 0), stop=(k == KC - 1))
            for k in range(KC):
                S.matmul(pu[:, :n], w1b[:, k, DFF + f * P:DFF + (f + 1) * P], C.h[:, k, :n], start=(k == 0), stop=(k == KC - 1))
            sg = C.sg[f % 2]
            S.act(sg[:, :n], pg[:, :n], AF.Silu)
            S.tt(C.aT[:, f, :n], sg[:, :n], pu[:, :n], ALU.mult)
        for d in range(KC):
            py = PS.y[d % 2]
            for f in range(FC):
                S.matmul(py[:, :n], w2b[:, f, d * P:(d + 1) * P], C.aT[:, f, :n], start=(f == 0), stop=(f == FC - 1))
            S.copy(C.y[:, d, :n], py[:, :n], e="dve")
        rms_rstd(S, C, lambda c: C.y[:, c, :n], n, KC, D, PS.ss2, C.rstd2[:, :n])
        for c in range(KC):
            t = C.tmp[c % 2]
            S.tt(t[:, :n], C.y[:, c, :n], C.rstd2[:, :n], ALU.mult)
            S.stt(xt[:, c, :n], t[:, :n], G[:, c, col:col + 1], xt[:, c, :n], ALU.mult, ALU.add)
        S.dma("pool", x_out(j), xt[:, :, :n])


def alloc_ffn_work(S, C):
    C.xt = [S.sbuf("xt0", [P, KC, 512], F32)] * 2
    C.h = S.sbuf("h", [P, KC, 512], BF16)
    C.aT = S.sbuf("aT", [P, FC, 512], BF16)
    C.y = S.sbuf("y", [P, KC, 512], F32)
    C.sg = C.tmp


def alloc_small(S, C):
    C.sq = [S.sbuf("sq%d" % i, [P, 512], F32) for i in range(2)]
    C.tmp = [S.sbuf("tmp%d" % i, [P, 512], F32) for i in range(2)]
    C.lnt = C.sq[0]
    C.rstd = S.sbuf("rstd", [P, 512], F32)
    C.rstd2 = C.rstd


def mk_tiles(NT, NCX):
    tiles = [(j * 512, 512, 0) for j in range(NT // 512)]
    if NCX:
        tiles.append((NT, NCX, 1))
    return tiles


DEBUG = False
NPC = 49


def build_R1(NT, NCX):
    TT = NT + NCX
    nc = bass.Bass("TRN2", target_bir_lowering=False)
    with ExitStack() as st:
        S = Sched(nc, st)
        xT = S.dram("xT", [P, KC, TT], F32, kind="ExternalInput")
        cv = S.dram("cv", [P, KC, 2], F32, kind="ExternalInput")
        wada = S.dram("wada", [D, 5 * D], F32, kind="ExternalInput")
        bada = S.dram("bada", [P, 40], F32, kind="ExternalInput")
        ng = S.dram("ng", [P, 48], F32, kind="ExternalInput")
        w1 = S.dram("w1", [D, 2 * DFF], F32, kind="ExternalInput")
        w2 = S.dram("w2", [DFF, D], F32, kind="ExternalInput")
        win = S.dram("win", [D, NPC * P], F32, kind="ExternalInput")
        x1T = S.dram("x1T", [P, KC, TT], F32, kind="ExternalOutput")
        PT = S.dram("PT", [NPC, P, TT], F32, kind="ExternalOutput")
        tiles = mk_tiles(NT, NCX)
        C = mk_consts(S, nc)
        alloc_small(S, C)
        PS = NS()
        PS.ss = S.psum("ps_ss", [P, 512])
        PS.ss2 = S.psum("ps_ss2", [P, 512])
        PS.g = [S.psum("ps_g%d" % i, [P, 512]) for i in range(2)]
        PS.u = [S.psum("ps_u%d" % i, [P, 512]) for i in range(2)]
        PS.y = [S.psum("ps_y%d" % i, [P, 512]) for i in range(2)]
        mods = S.sbuf("mods", [P, 40, 2], F32)
        ngt = S.sbuf("ngt", [P, 6, KC], F32)
        S.dma("sp", ngt[:], ng[:].rearrange("p (m c) -> p m c", c=KC))
        A1 = S.sbuf("A1", [P, KC, 2], F32)
        G1 = S.sbuf("G1", [P, KC, 2], F32)
        A2 = S.sbuf("A2", [P, KC, 2], F32)
        with S.scope():
            stg = [S.sbuf("stgm%d" % i, [P, 5 * D], F32) for i in range(KC)]
            compute_mods(S, C, cv, wada, bada, 5, stg, PS.g[0], mods)

        def bc(v):
            return View(v.buf, v.ap.rearrange("p (c o) -> p c o", o=1).to_broadcast([P, KC, 2]))
        S.stt(A1[:], mods[:, 8:16, :], 1.0, bc(ngt[:, 0, :]), ALU.add, ALU.mult)
        S.stt(G1[:], mods[:, 16:24, :], 0.5, bc(ngt[:, 1, :]), ALU.mult, ALU.mult)
        S.stt(A2[:], mods[:, 32:40, :], 1.0, bc(ngt[:, 2, :]), ALU.add, ALU.mult)
        B1 = mods[:, 0:8, :]
        B2 = mods[:, 24:32, :]
        if DEBUG:
            dbg = S.dram("dbg_mods", [P, 80], F32, kind="ExternalOutput")
            S.dma("sp", dbg[:], mods[:].rearrange("p j t -> p (j t)"))
        x1tiles = [S.sub("x1t%d" % j, x1T.t[:, :, s0:s0 + n]) for j, (s0, n, col) in enumerate(tiles)]
        with S.scope():
            w1b = S.sbuf("w1b", [P, KC, 2 * DFF], BF16)
            w2b = S.sbuf("w2b", [P, FC, D], BF16)
            with S.scope():
                stages = [S.sbuf("wst%d" % i, [P, 2048], F32) for i in range(3)]
                load_w(S, w1, w1b, D, 2 * DFF, stages)
                load_w(S, w2, w2b, DFF, D, stages)
            alloc_ffn_work(S, C)
            ffn_sweep(S, C, tiles, lambda j: xT[:, :, tiles[j][0]:tiles[j][0] + tiles[j][1]],
                      lambda j: x1tiles[j][:], w1b, w2b, A1[:], B1, G1[:], PS)
        with S.scope():
            winb = S.sbuf("winb", [P, KC, NPC * P], BF16)
            with S.scope():
                stages = [S.sbuf("wst%d" % i, [P, 2048], F32) for i in range(3)]
                load_w(S, win, winb, D, NPC * P, stages)
            xt2 = [S.sbuf("xq%d" % i, [P, KC, 512], F32) for i in range(2)]
            h = S.sbuf("h2", [P, KC, 512], BF16)
            ost = [S.sbuf("ost%d" % i, [P, 4, 512], F32) for i in range(3)]
            pps = PS.g + PS.u + PS.y
            gi = 0
            for j, (s0, n, col) in enumerate(tiles):
                xt = xt2[j % 2]
                S.dma("sp", xt[:, :, :n], x1tiles[j][:])
                norm_mod(S, C, xt, n, A2[:], B2, col, h, PS.ss)
                for c0 in range(0, NPC, 4):
                    nn = min(4, NPC - c0)
                    o = ost[gi % 3]
                    gi += 1
                    for cc in range(nn):
                        pp = pps[(c0 + cc) % 6]
                        for k in range(KC):
                            S.matmul(pp[:, :n], winb[:, k, (c0 + cc) * P:(c0 + cc + 1) * P], h[:, k, :n],
                                     start=(k == 0), stop=(k == KC - 1))
                        S.copy(o[:, cc, :n], pp[:, :n], e=("act" if cc % 2 else "dve"))
                    S.dma("pool", S.sub("pt", PT.t[c0:c0 + nn, :, s0:s0 + n].rearrange("c p t -> p c t"))[:], o[:, :nn, :n])
            S.wait_all("sp", ost + xt2)
        S.barrier()
    return nc


def build_R2(NT, NCX):
    TT = NT + NCX
    nc = bass.Bass("TRN2", target_bir_lowering=False)
    with ExitStack() as st:
        S = Sched(nc, st)
        x1T = S.dram("x1T", [P, KC, TT], F32, kind="ExternalInput")
        oin = [S.dram(nm, [P, 4, TT], F32, kind="ExternalInput") for nm in ("oaT", "obT", "ocT")]
        cv = S.dram("cv", [P, KC, 2], F32, kind="ExternalInput")
        wada = S.dram("wada", [D, 6 * D], F32, kind="ExternalInput")
        bada = S.dram("bada", [P, 48], F32, kind="ExternalInput")
        ng = S.dram("ng", [P, 48], F32, kind="ExternalInput")
        snw = S.dram("snw", [P, 4], F32, kind="ExternalInput")
        wg = S.dram("wg", [D, 3 * D], F32, kind="ExternalInput")
        wb = S.dram("wb", [1536, D], F32, kind="ExternalInput")
        wo = S.dram("wo", [D, D], F32, kind="ExternalInput")
        w1 = S.dram("w1", [D, 2 * DFF], F32, kind="ExternalInput")
        w2 = S.dram("w2", [DFF, D], F32, kind="ExternalInput")
        x3T = S.dram("x3T", [P, KC, TT], F32, kind="ExternalOutput")
        x2T = S.dram("x2T", [P, KC, TT], F32, kind="Internal")
        tiles = mk_tiles(NT, NCX)
        C = mk_consts(S, nc)
        alloc_small(S, C)
        PS = NS()
        PS.ss = S.psum("ps_ss", [P, 512])
        PS.ss2 = S.psum("ps_ss2", [P, 512])
        PS.g = [S.psum("ps_g%d" % i, [P, 512]) for i in range(2)]
        PS.u = [S.psum("ps_u%d" % i, [P, 512]) for i in range(2)]
        PS.y = [S.psum("ps_y%d" % i, [P, 512]) for i in range(2)]
        mods = S.sbuf("mods", [P, 48, 2], F32)
        ngt = S.sbuf("ngt", [P, 6, KC], F32)
        S.dma("sp", ngt[:], ng[:].rearrange("p (m c) -> p m c", c=KC))
        snt = S.sbuf("snt", [P, 4], F32)
        S.dma("sp", snt[:], snw[:])
        with S.scope():
            stg = [S.sbuf("stgm%d" % i, [P, 6 * D], F32) for i in range(KC)]
            compute_mods(S, C, cv, wada, bada, 6, stg, PS.g[0], mods)

        def bc(v):
            return View(v.buf, v.ap.rearrange("p (c o) -> p c o", o=1).to_broadcast([P, KC, 2]))
        A2 = S.sbuf("A2", [P, KC, 2], F32)
        G3 = S.sbuf("G3", [P, KC, 2], F32)
        A4 = S.sbuf("A4", [P, KC, 2], F32)
        G5 = S.sbuf("G5", [P, KC, 2], F32)
        S.stt(A2[:], mods[:, 8:16, :], 1.0, bc(ngt[:, 2, :]), ALU.add, ALU.mult)
        S.tt(G3[:], mods[:, 16:24, :], bc(ngt[:, 3, :]), ALU.mult)
        S.stt(A4[:], mods[:, 32:40, :], 1.0, bc(ngt[:, 4, :]), ALU.add, ALU.mult)
        S.stt(G5[:], mods[:, 40:48, :], 0.5, bc(ngt[:, 5, :]), ALU.mult, ALU.mult)
        B2 = mods[:, 0:8, :]
        B4 = mods[:, 24:32, :]
        x2tiles = [S.sub("x2t%d" % j, x2T.t[:, :, s0:s0 + n]) for j, (s0, n, col) in enumerate(tiles)]
        with S.scope():
            wgb = S.sbuf("wgb", [P, KC, 3 * D], BF16)
            wbb = S.sbuf("wbb", [P, 12, D], BF16)
            wob = S.sbuf("wob", [P, KC, D], BF16)
            with S.scope():
                stages = [S.sbuf("wst%d" % i, [P, 2048], F32) for i in range(3)]
                load_w(S, wg, wgb, D, 3 * D, stages)
                load_w(S, wb, wbb, 1536, D, stages)
                load_w(S, wo, wob, D, D, stages)
            xt = S.sbuf("xm", [P, KC, 512], F32)
            h = S.sbuf("hm", [P, KC, 512], BF16)
            ost = S.sbuf("ostg", [P, 4, 512], F32)
            ob16 = S.sbuf("ob16", [P, 12, 512], BF16)
            yacc = S.sbuf("yacc", [P, 512], F32)
            ybf = S.sbuf("ybf", [P, KC, 512], BF16)
            yy = S.sbuf("yy", [P, KC, 512], F32)
            gt = [S.sbuf("gt%d" % i, [P, 512], F32) for i in range(2)]
            for j, (s0, n, col) in enumerate(tiles):
                S.dma("sp", xt[:, :, :n], x1T[:, :, s0:s0 + n])
                norm_mod(S, C, xt, n, A2[:], B2, col, h, PS.ss)
                for br in range(3):
                    S.dma("sp", ost[:, :, :n], oin[br][:, :, s0:s0 + n])
                    if br < 2:
                        S.copy(ob16[:, br * 4:(br + 1) * 4, :n], ost[:, :, :n], e="pool")
                    else:
                        rms_rstd(S, C, lambda c: ost[:, c, :n], n, 4, 512, PS.ss2, C.rstd2[:, :n])
                        for c in range(4):
                            t = C.tmp[c % 2]
                            S.tt(t[:, :n], ost[:, c, :n], C.rstd2[:, :n], ALU.mult)
                            S.act(ob16[:, 8 + c, :n], t[:, :n], AF.Copy, scale=snt[:, c:c + 1])
                for d in range(KC):
                    for br in range(3):
                        pg = PS.g[br % 2]
                        pu = PS.u[br % 2]
                        cg = br * KC + d
                        for k in range(KC):
                            S.matmul(pg[:, :n], wgb[:, k, cg * P:(cg + 1) * P], h[:, k, :n], start=(k == 0), stop=(k == KC - 1))
                        for k in range(4):
                            S.matmul(pu[:, :n], wbb[:, br * 4 + k, d * P:(d + 1) * P], ob16[:, br * 4 + k, :n], start=(k == 0), stop=(k == 3))
                        g = gt[br % 2]
                        S.act(g[:, :n], pg[:, :n], AF.Sigmoid)
                        if br == 0:
                            S.tt(yacc[:, :n], g[:, :n], pu[:, :n], ALU.mult)
                        else:
                            t = C.tmp[br % 2]
                            S.tt(t[:, :n], g[:, :n], pu[:, :n], ALU.mult)
                            if br == 1:
                                S.tt(yacc[:, :n], yacc[:, :n], t[:, :n], ALU.add)
                            else:
                                S.tt(ybf[:, d, :n], yacc[:, :n], t[:, :n], ALU.add)
                for d in range(KC):
                    py = PS.y[d % 2]
                    for k in range(KC):
                        S.matmul(py[:, :n], wob[:, k, d * P:(d + 1) * P], ybf[:, k, :n], start=(k == 0), stop=(k == KC - 1))
                    S.copy(yy[:, d, :n], py[:, :n], e="dve")
                rms_rstd(S, C, lambda c: yy[:, c, :n], n, KC, D, PS.ss2, C.rstd2[:, :n])
                for c in range(KC):
                    t = C.tmp[c % 2]
                    S.tt(t[:, :n], yy[:, c, :n], C.rstd2[:, :n], ALU.mult)
                    S.stt(xt[:, c, :n], t[:, :n], G3[:, c, col:col + 1], xt[:, c, :n], ALU.mult, ALU.add)
                S.dma("pool", x2tiles[j][:], xt[:, :, :n])
        with S.scope():
            w1b = S.sbuf("w1b", [P, KC, 2 * DFF], BF16)
            w2b = S.sbuf("w2b", [P, FC, D], BF16)
            with S.scope():
                stages = [S.sbuf("wst%d" % i, [P, 2048], F32) for i in range(3)]
                load_w(S, w1, w1b, D, 2 * DFF, stages)
                load_w(S, w2, w2b, DFF, D, stages)
            alloc_ffn_work(S, C)
            ffn_sweep(S, C, tiles, lambda j: x2tiles[j][:],
                      lambda j: S.sub("x3", x3T.t[:, :, tiles[j][0]:tiles[j][0] + tiles[j][1]])[:], w1b, w2b, A4[:], B4, G5[:], PS)
        S.barrier()
    return nc


I32 = mybir.dt.int32
import math


def rope_tables(S, nc, C, L, cosb, sinb):
    GW = 64
    rows = L // GW
    TWO_PI = 2 * math.pi
    with S.scope():
        ti = S.sbuf("ti", [P, P], I32)
        tf = S.sbuf("tf", [P, P], F32)

        def ppc(name, pattern):
            o = S.sbuf(name, [P, 1], F32)
            S.op("pool", lambda: nc.gpsimd.iota(ti.t[:], pattern, base=0, channel_multiplier=0), [], [ti[:]])
            S.copy(tf[:], ti[:])
            S.tt(tf[:], tf[:], C.ident[:], ALU.mult)
            S.reduce(o[:], tf[:], ALU.add)
            return o
        i16 = ppc("i16", [[0, 2], [0, 2], [1, 16], [0, 2]])
        sel = ppc("sel", [[0, 2], [1, 2], [0, 16], [0, 2]])
        dd = ppc("dd", [[0, 2], [0, 2], [0, 16], [1, 2]])
        sgn = S.sbuf("sgn", [P, 1], F32)
        inv = S.sbuf("inv", [P, 1], F32)
        S.ts(sgn[:], dd[:], 2.0, ALU.mult, -1.0, ALU.add)
        S.act(inv[:], i16[:], AF.Exp, scale=-math.log(10000.0) / 16.0)
        S.ts(inv[:], inv[:], 1.0 / TWO_PI, ALU.mult)
        CH = 1024
        with S.scope():
            ri = S.sbuf("ri", [P, CH], I32)
            ci = S.sbuf("ci", [P, CH], I32)
            rf = S.sbuf("rf", [P, CH], F32)
            cf = S.sbuf("cf", [P, CH], F32)
            xt = S.sbuf("xtn", [P, CH], F32)
            ni = S.sbuf("ni", [P, CH], I32)
            nf = S.sbuf("nf", [P, CH], F32)
            for c0 in range(0, L, CH):
                w = min(CH, L - c0)
                S.op("pool", lambda c0=c0, w=w: nc.gpsimd.iota(ri.t[:, :w], [[1, w // GW], [0, GW]], base=c0 // GW, channel_multiplier=0), [], [ri[:]])
                S.op("pool", lambda w=w: nc.gpsimd.iota(ci.t[:, :w], [[0, w // GW], [1, GW]], base=0, channel_multiplier=0), [], [ci[:]])
                S.copy(rf[:, :w], ri[:, :w])
                S.copy(cf[:, :w], ci[:, :w])
                S.tt(cf[:, :w], cf[:, :w], rf[:, :w], ALU.subtract)
                S.stt(xt[:, :w], cf[:, :w], sel[:, 0:1], rf[:, :w], ALU.mult, ALU.add)
                S.ts(xt[:, :w], xt[:, :w], inv[:, 0:1], ALU.mult)
                for (dst, off) in ((sinb, 0.0), (cosb, 0.25)):
                    if off:
                        S.ts(xt[:, :w], xt[:, :w], off, ALU.add)
                    S.copy(ni[:, :w], xt[:, :w])
                    S.copy(nf[:, :w], ni[:, :w])
                    S.tt(nf[:, :w], xt[:, :w], nf[:, :w], ALU.subtract)
                    S.act(dst[:, c0:c0 + w], nf[:, :w], AF.Sin, scale=TWO_PI * (1 - 1e-6))
                S.ts(sinb[:, c0:c0 + w], sinb[:, c0:c0 + w], sgn[:, 0:1], ALU.mult)


def build_Mdiff(L, LC):
    LK = L + LC
    NKC = LK // P
    nc = bass.Bass("TRN2", target_bir_lowering=False)
    with ExitStack() as st:
        S = Sched(nc, st)
        qT = S.dram("qT", [2, P, L], F32, kind="ExternalInput")
        qsT = S.dram("qsT", [2, P, L], F32, kind="ExternalInput")
        kT = S.dram("kT", [2, P, LK], F32, kind="ExternalInput")
        ksT = S.dram("ksT", [2, P, L], F32, kind="ExternalInput")
        qcT = S.dram("qcT", [2, P, LC], F32, kind="ExternalInput")
        vd = S.dram("v", [P, NKC, 256], F32, kind="ExternalInput")
        lamd = S.dram("lam", [P, 256], F32, kind="ExternalInput")
        nwd = S.dram("nw", [P, 1], F32, kind="ExternalInput")
        lid = S.dram("li", [P, 1], F32, kind="ExternalInput")
        obT = S.dram("obT", [2, P, L], F32, kind="ExternalOutput")
        obcT = S.dram("obcT", [2, P, LC], F32, kind="ExternalOutput")
        C = mk_consts(S, nc)
        C.sq = [S.sbuf("sq%d" % i, [P, 512], F32) for i in range(2)]
        C.lnt = C.sq[0]
        onesb = S.sbuf("onesb", [P, P], BF16)
        S.memset(onesb[:], 1.0)
        Q = [S.sbuf("Q%d" % h, [P, L], BF16) for h in range(2)]
        QC = [S.sbuf("QC%d" % h, [P, LC], BF16) for h in range(2)]
        K = [S.sbuf("K%d" % h, [P, LK], BF16) for h in range(2)]
        V = S.sbuf("V", [P, NKC, 256], BF16)
        lam = S.sbuf("lamt", [P, 4, 64], F32)
        S.dma("sp", lam[:], lamd[:].rearrange("p (a b) -> p a b", b=64))
        nw = S.sbuf("nwt", [P, 1], F32)
        li = S.sbuf("lit", [P, 1], F32)
        S.dma("sp", nw[:], nwd[:])
        S.dma("sp", li[:], lid[:])
        pr = S.sbuf("pr", [P, 2, 64], F32)
        s12 = S.sbuf("s12", [P, 2], F32)
        S.tt(pr[:, 0, :], lam[:, 0, :], lam[:, 1, :], ALU.mult)
        S.tt(pr[:, 1, :], lam[:, 2, :], lam[:, 3, :], ALU.mult)
        S.reduce(s12[:], pr[:], ALU.add)
        e12 = S.sbuf("e12", [P, 2], F32)
        S.act(e12[:], s12[:], AF.Exp)
        neglam = S.sbuf("neglam", [P, 1], F32)
        S.tt(neglam[:], e12[:, 1:2], e12[:, 0:1], ALU.subtract)
        S.tt(neglam[:], neglam[:], li[:], ALU.subtract)
        sc2 = S.sbuf("sc2", [P, 1], F32)
        S.ts(sc2[:], li[:], -1.0, ALU.mult, 1.0, ALU.add)
        S.tt(sc2[:], sc2[:], nw[:], ALU.mult)
        with S.scope():
            cosb = S.sbuf("cosb", [P, L], F32)
            sinb = S.sbuf("sinb", [P, L], F32)
            rope_tables(S, nc, C, L, cosb, sinb)
            with S.scope():
                a = [S.sbuf("la%d" % i, [P, 512], F32) for i in range(2)]
                b = [S.sbuf("lb%d" % i, [P, 512], F32) for i in range(2)]
                vst = [S.sbuf("vst%d" % i, [P, 4, 256], F32) for i in range(2)]
                i = 0
                for h in range(2):
                    for (src, ssw, dst) in ((qT, qsT, Q[h]), (kT, ksT, K[h])):
                        for c0 in range(0, L, 512):
                            ta, tb = a[i % 2], b[i % 2]
                            i += 1
                            S.dma("sp", ta[:], src[h, :, c0:c0 + 512])
                            S.dma("pool", tb[:], ssw[h, :, c0:c0 + 512])
                            S.tt(ta[:], ta[:], cosb[:, c0:c0 + 512], ALU.mult)
                            S.tt(tb[:], tb[:], sinb[:, c0:c0 + 512], ALU.mult, e="pool")
                            S.tt(dst[:, c0:c0 + 512], ta[:], tb[:], ALU.add)
                    ta = a[i % 2]
                    i += 1
                    S.dma("sp", ta[:, :LC], kT[h, :, L:LK])
                    S.copy(K[h][:, L:LK], ta[:, :LC])
                    ta = a[i % 2]
                    i += 1
                    S.dma("sp", ta[:, :LC], qcT[h, :, :])
                    S.copy(QC[h][:], ta[:, :LC])
                for c0 in range(0, NKC, 4):
                    w = min(4, NKC - c0)
                    t = vst[(c0 // 4) % 2]
                    S.dma("sp", t[:, :w, :], vd[:, c0:c0 + w, :])
                    S.copy(V[:, c0:c0 + w, :], t[:, :w, :], e="pool")
        ps_s = [[S.psum("ps_s%d%d" % (j, i), [P, 512]) for i in range(2)] for j in range(2)]
        ps_o = [S.psum("ps_o%d" % j, [P, 512]) for j in range(2)]
        ps_z = [S.psum("ps_z%d" % j, [P, 512]) for j in range(2)]
        pt = [[S.sbuf("pt%d%d" % (j, i), [P, 512], BF16) for i in range(2)] for j in range(2)]
        rz = [S.sbuf("rz%d" % j, [P, 512], F32) for j in range(2)]
        t0 = S.sbuf("t0", [P, 512], F32)
        t1 = S.sbuf("t1", [P, 512], F32)
        rstd = S.sbuf("rstd", [P, 512], F32)
        oo = [S.sbuf("oo%d" % i, [P, 512], F32) for i in range(2)]
        jobs = []
        for h in range(2):
            for q0 in range(0, L, 512):
                jobs.append((h, Q[h][:, q0:q0 + 512], 512, 0, NKC, obT[h, :, q0:q0 + 512]))
            jobs.append((h, QC[h][:], LC, L // P, NKC, obcT[h, :, :]))
        for ji, (h, qv, n, kc0, kc1, outv) in enumerate(jobs):
            for kc in range(kc0, kc1):
                bi = kc % 2
                for j in range(2):
                    S.matmul(ps_s[j][bi][:, :n], K[h][j * 64:(j + 1) * 64, kc * P:(kc + 1) * P], qv[j * 64:(j + 1) * 64, :],
                             start=True, stop=True)
                for j in range(2):
                    S.act(pt[j][bi][:, :n], ps_s[j][bi][:, :n], AF.Exp, scale=0.125)
                for j in range(2):
                    S.matmul(ps_o[j][:, :n], V[:, kc, h * P:(h + 1) * P], pt[j][bi][:, :n], start=(kc == kc0), stop=(kc == kc1 - 1))
                    S.matmul(ps_z[j][:, :n], onesb[:], pt[j][bi][:, :n], start=(kc == kc0), stop=(kc == kc1 - 1))
            for j in range(2):
                S.recip(rz[j][:, :n], ps_z[j][:, :n])
            S.tt(t0[:, :n], ps_o[0][:, :n], rz[0][:, :n], ALU.mult)
            S.tt(t1[:, :n], ps_o[1][:, :n], rz[1][:, :n], ALU.mult)
            S.stt(t0[:, :n], t1[:, :n], neglam[:, 0:1], t0[:, :n], ALU.mult, ALU.add)
            pss = ps_s[0][0]
            rms_rstd(S, C, lambda c: t0[:, :n], n, 1, P, pss, rstd[:, :n])
            o = oo[ji % 2]
            S.tt(t1[:, :n], t0[:, :n], rstd[:, :n], ALU.mult)
            S.ts(o[:, :n], t1[:, :n], sc2[:, 0:1], ALU.mult)
            S.dma("pool", outv, o[:, :n])
        S.barrier()
    return nc


def tri_mask(S, nc, name, kind, blk=None):
    m = S.sbuf(name, [P, P], F32)
    S.memset(m[:], 1.0, e="pool")
    pat, cm, op = {"le": ([[1, P]], -1, ALU.is_ge), "ge": ([[-1, P]], 1, ALU.is_ge),
                   "gt": ([[-1, P]], 1, ALU.is_gt), "lt": ([[1, P]], -1, ALU.is_gt)}[kind]
    S.op("pool", lambda: nc.gpsimd.affine_select(m.t[:], m.t[:], pat, op, 0.0, base=0, channel_multiplier=cm), [m[:]], [m[:]])
    if blk:
        S.memset(m[0:blk, blk:P], 0.0, e="pool")
        S.memset(m[blk:P, 0:blk], 0.0, e="pool")
    return m


def build_Mssd(L, LC, dbg_stop=99):
    LT = L + LC
    NCH = LT // P
    NCC = LC // P
    nc = bass.Bass("TRN2", target_bir_lowering=False)
    with ExitStack() as st:
        S = Sched(nc, st)
        xl = S.dram("xbcl", [4, P, L + 4], F32, kind="ExternalInput")
        xc = S.dram("xbcc", [4, P, LC + 4], F32, kind="ExternalInput")
        cwd = S.dram("cw", [P, 4, 5], F32, kind="ExternalInput")
        cbd = S.dram("cb", [P, 4], F32, kind="ExternalInput")
        zd = S.dram("z", [P, NCH, 256], F32, kind="ExternalInput")
        dtd = S.dram("dt", [P, NCH, 8], F32, kind="ExternalInput")
        dbd = S.dram("dtb", [P, 8], F32, kind="ExternalInput")
        ald = S.dram("alog", [P, 8], F32, kind="ExternalInput")
        dsd = S.dram("dskip", [P, 4], F32, kind="ExternalInput")
        yo = S.dram("y", [NCH, P, 256], F32, kind="ExternalOutput")
        yf = S.dram("yf", [NCH, P, 256], F32, kind="Internal")
        C = mk_consts(S, nc)
        tri = {0: tri_mask(S, nc, "tri_f", "le"), 1: tri_mask(S, nc, "tri_b", "ge")}
        strict = {0: tri_mask(S, nc, "str_f", "gt"), 1: tri_mask(S, nc, "str_b", "lt")}
        cw = S.sbuf("cw", [P, 4, 5], F32)
        cb = S.sbuf("cb", [P, 4], F32)
        S.dma("sp", cw[:], cwd[:])
        S.dma("sp", cb[:], cbd[:])
        xs_tok = S.sbuf("xs_tok", [P, NCH, 256], F32)
        B_tok = S.sbuf("B_tok", [P, NCH, P], F32)
        BT = S.sbuf("BT", [P, LT], F32)
        CT = S.sbuf("CT", [P, LT], F32)
        dtv = S.sbuf("dtv", [P, NCH, 8], F32)
        aall = S.sbuf("aall", [P, NCH, 8], F32)
        dtb = S.sbuf("dtb", [P, 8], F32)
        aneg = S.sbuf("aneg", [P, 8], F32)
        dsk = S.sbuf("dsk", [P, 4], F32)
        S.dma("sp", dtv[:], dtd[:])
        S.dma("sp", dtb[:], dbd[:])
        S.dma("sp", aneg[:], ald[:])
        S.dma("sp", dsk[:], dsd[:])
        S.tt(dtv[:], dtv[:], View(dtb, dtb.t[:].rearrange("p (o e) -> p o e", o=1).to_broadcast([P, NCH, 8])), ALU.add)
        S.act(dtv[:], dtv[:], AF.Exp)
        S.act(dtv[:], dtv[:], AF.Ln, bias=C.ones[:, 0:1])
        S.act(aneg[:], aneg[:], AF.Exp)
        S.ts(aneg[:], aneg[:], -1.0, ALU.mult)
        S.tt(aall[:], dtv[:], View(aneg, aneg.t[:].rearrange("p (o e) -> p o e", o=1).to_broadcast([P, NCH, 8])), ALU.mult)
        ps_t = [S.psum("ps_t%d" % i, [P, 512]) for i in range(2)]
        with S.scope():
            raw = [S.sbuf("raw%d" % i, [P, 4, 516], F32) for i in range(2)]
            acc = [S.sbuf("acc%d" % i, [P, 512], F32) for i in range(2)]
            xsT = [S.sbuf("xsT%d" % i, [P, 512], F32) for i in range(2)]
            segs = [(xc, 0, LC)] + [(xl, LC, L)]
            ti = 0
            if dbg_stop < 0:
                segs = []
            for (src, base, seglen) in segs:
                for t0 in range(0, seglen, 512):
                    n = min(512, seglen - t0)
                    r = raw[ti % 2]
                    ti += 1
                    S.dma("sp", r[:, :, :n + 4], src[:, :, t0:t0 + n + 4].rearrange("c p t -> p c t"))
                    for c in range(4):
                        a = acc[c % 2]
                        eng = "dve"
                        S.ts(a[:, :n], r[:, c, 0:n], cw[:, c, 0:1], ALU.mult, e=eng)
                        for j in range(1, 5):
                            S.stt(a[:, :n], r[:, c, j:j + n], cw[:, c, j:j + 1], a[:, :n], ALU.mult, ALU.add, e=eng)
                        g0 = base + t0
                        if c < 2:
                            dst = xsT[c]
                            S.act(dst[:, :n], a[:, :n], AF.Silu, bias=cb[:, c:c + 1])
                        elif c == 2:
                            S.act(BT[:, g0:g0 + n], a[:, :n], AF.Silu, bias=cb[:, c:c + 1])
                        else:
                            S.act(CT[:, g0:g0 + n], a[:, :n], AF.Silu, bias=cb[:, c:c + 1])
                    for bl in range(n // P):
                        gc = (base + t0) // P + bl
                        pt = ps_t[bl % 2]
                        dbgv = None
                        S.transpose(pt[:, 0:P], xsT[0][:, bl * P:(bl + 1) * P], C.ident[:])
                        if dbgv == "T1":
                            S.copy(xs_tok[:, gc, 0:P], pt[:, 0:P], e="dve")
                            continue
                        S.transpose(pt[:, P:2 * P], xsT[1][:, bl * P:(bl + 1) * P], C.ident[:])
                        if dbgv == "T2":
                            S.copy(xs_tok[:, gc, :], pt[:, 0:2 * P], e="dve")
                            continue
                        S.transpose(pt[:, 2 * P:3 * P], BT[:, gc * P:(gc + 1) * P], C.ident[:])
                        S.copy(xs_tok[:, gc, :], pt[:, 0:2 * P], e="dve")
                        S.copy(B_tok[:, gc, :], pt[:, 2 * P:3 * P], e="dve")
        ps_arg = [S.psum("ps_arg%d" % i, [P, 512]) for i in range(2)]
        ps_cb = S.psum("ps_cb", [P, 512])
        ps_y = S.psum("ps_y", [P, 512])
        ps_st = S.psum("ps_st", [P, 512])
        ps_sm = S.psum("ps_sm", [P, 512])
        X = [S.sbuf("X%d" % i, [P, 4, P], F32) for i in range(2)]
        LTt = [S.sbuf("LT%d" % i, [P, 4, P], F32) for i in range(2)]
        CBm = S.sbuf("CBm", [P, P], F32)
        scT = [S.sbuf("scT%d" % i, [P, 4, P], F32) for i in range(2)]
        sm = S.sbuf("sm", [P, 8], F32)
        eacs = S.sbuf("eacs", [P, 4], F32)
        edec = S.sbuf("edec", [P, 4], F32)
        etot = S.sbuf("etot", [P, 4], F32)
        dif = S.sbuf("dif", [P, 4], F32)
        xdt = [S.sbuf("xdt%d" % i, [P, 4, 64], F32) for i in range(2)]
        xdtd = [S.sbuf("xdtd%d" % i, [P, 4, 64], F32) for i in range(2)]
        ST = S.sbuf("ST", [P, 4, 64], F32)
        yt = [S.sbuf("yt%d" % i, [P, 256], F32) for i in range(2)]
        y2 = [S.sbuf("y2%d" % i, [P, 256], F32) for i in range(2)]
        yfl = [S.sbuf("yfl%d" % i, [P, 256], F32) for i in range(2)]
        zt = [S.sbuf("zt%d" % i, [P, 256], F32) for i in range(2)]
        yfb = [S.sub("yf%d" % c, yf.t[c]) for c in range(NCH)]

        def bc4(v):
            return View(v.buf, v.ap.rearrange("p (h o) -> p h o", o=1).to_broadcast([P, 4, 64]))
        if dbg_stop < 2:
            for c in range(NCH):
                S.dma("sp", zt[c % 2][:], zd[:, c, :])
                if dbg_stop == 1:
                    S.tt(zt[c % 2][:], zt[c % 2][:], xs_tok[:, c, :], ALU.add)
                    S.tt(zt[c % 2][:, 0:P], zt[c % 2][:, 0:P], B_tok[:, c, :], ALU.add)
                S.dma("pool", S.sub("yo", yo.t[c])[:], zt[c % 2][:])
        for dr in range(2 if dbg_stop >= 2 else 0):
            order = list(range(NCH)) if dr == 0 else (list(range(NCC - 1, -1, -1)) + list(range(NCH - 1, NCC - 1, -1)))
            S.memset(ST[:], 0.0)
            for it, c in enumerate(order):
                bi = it % 2
                a4 = aall[:, c, dr * 4:(dr + 1) * 4]
                for h in range(4):
                    S.ts(X[bi][:, h, :], strict[dr][:], aall[:, c, dr * 4 + h:dr * 4 + h + 1], ALU.mult, e=("dve" if h % 2 else "pool"))
                for h in range(4):
                    S.matmul(ps_arg[bi][:, h * P:(h + 1) * P], X[bi][:, h, :], tri[dr][:])
                S.act(LTt[bi][:].rearrange("p h l -> p (h l)"), ps_arg[bi][:], AF.Exp)
                S.matmul(ps_cb[:, 0:P], BT[:, c * P:(c + 1) * P], CT[:, c * P:(c + 1) * P])
                S.tt(CBm[:], ps_cb[:, 0:P], tri[dr][:], ALU.mult)
                S.tt(scT[bi][:], LTt[bi][:], View(CBm, CBm.t[:].rearrange("p (o l) -> p o l", o=1).to_broadcast([P, 4, P])), ALU.mult)
                S.matmul(ps_sm[:, 0:4], tri[dr][:], a4)
                S.matmul(ps_sm[:, 4:8], C.ones[:], a4)
                S.copy(sm[:], ps_sm[:, 0:8])
                S.act(eacs[:], sm[:, 0:4], AF.Exp)
                S.act(etot[:], sm[:, 4:8], AF.Exp)
                S.tt(dif[:], sm[:, 4:8], sm[:, 0:4], ALU.subtract)
                S.act(edec[:], dif[:], AF.Exp)
                xv = xs_tok[:, c, :].rearrange("p (h d) -> p h d", h=4)
                S.tt(xdt[bi][:], xv, bc4(dtv[:, c, dr * 4:(dr + 1) * 4]), ALU.mult, e="pool")
                S.tt(xdtd[bi][:], xdt[bi][:], bc4(edec[:]), ALU.mult)
                for h in range(4):
                    S.matmul(ps_y[:, h * 64:(h + 1) * 64], scT[bi][:, h, :], xdt[bi][:, h, :])
                S.matmul(ps_y[:, 256:512], CT[:, c * P:(c + 1) * P], ST[:].rearrange("p h d -> p (h d)"))
                y = yt[bi]
                S.tt(y[:].rearrange("p (h d) -> p h d", h=4), ps_y[:, 256:512].rearrange("p (h d) -> p h d", h=4), bc4(eacs[:]), ALU.mult)
                S.tt(y[:], y[:], ps_y[:, 0:256], ALU.add)
                S.matmul(ps_st[:, 0:256], B_tok[:, c, :], xdtd[bi][:].rearrange("p h d -> p (h d)"))
                S.tt(ST[:], ST[:], bc4(etot[:]), ALU.mult)
                S.tt(ST[:].rearrange("p h d -> p (h d)"), ST[:].rearrange("p h d -> p (h d)"), ps_st[:, 0:256], ALU.add)
                if dr == 0:
                    S.dma("pool", yfb[c][:], y[:])
                else:
                    S.dma("sp", yfl[bi][:], yfb[c][:])
                    S.dma("sp", zt[bi][:], zd[:, c, :])
                    o = y2[bi]
                    S.tt(o[:].rearrange("p (h d) -> p h d", h=4), xv, bc4(dsk[:]), ALU.mult, e="pool")
                    S.tt(y[:], y[:], yfl[bi][:], ALU.add)
                    S.tt(o[:], o[:], y[:], ALU.add)
                    S.act(zt[bi][:], zt[bi][:], AF.Silu)
                    S.tt(o[:], o[:], zt[bi][:], ALU.mult)
                    S.dma("pool", S.sub("yo", yo.t[c])[:], o[:])
        S.barrier()
    return nc


def build_Mgdn(L, LC):
    LT = L + LC
    NCH = LT // P
    NCC = LC // P
    nc = bass.Bass("TRN2", target_bir_lowering=False)
    with ExitStack() as st:
        S = Sched(nc, st)
        ql = S.dram("qkvl", [6, P, L + 4], F32, kind="ExternalInput")
        qc = S.dram("qkvc", [6, P, LC + 4], F32, kind="ExternalInput")
        cwd = S.dram("cw", [P, 6, 5], F32, kind="ExternalInput")
        zd = S.dram("z", [P, NCH, 256], F32, kind="ExternalInput")
        ad = S.dram("araw", [P, NCH, 4], F32, kind="ExternalInput")
        bd = S.dram("braw", [P, NCH, 4], F32, kind="ExternalInput")
        ald = S.dram("alog", [P, 4], F32, kind="ExternalInput")
        dbd = S.dram("dtb", [P, 4], F32, kind="ExternalInput")
        nwd = S.dram("nw", [P, P], F32, kind="ExternalInput")
        oa = S.dram("oa", [NCH, P, 256], F32, kind="ExternalOutput")
        ofd = S.dram("of", [NCH, P, 256], F32, kind="Internal")
        C = mk_consts(S, nc)
        M = {k: tri_mask(S, nc, "m_" + k, k, blk=64) for k in ("le", "ge", "gt", "lt")}
        halfA = S.sbuf("halfA", [P, P], F32)
        halfB = S.sbuf("halfB", [P, P], F32)
        S.memset(halfA[:], 0.0)
        S.memset(halfB[:], 0.0)
        S.memset(halfA[0:64, :], 1.0)
        S.memset(halfB[64:128, :], 1.0)
        cw = S.sbuf("cw", [P, 6, 5], F32)
        S.dma("sp", cw[:], cwd[:])
        nw = S.sbuf("nw", [P, P], F32)
        S.dma("sp", nw[:], nwd[:])
        gall = S.sbuf("gall", [P, NCH, 4], F32)
        ball = S.sbuf("ball", [P, NCH, 4], F32)
        negb = S.sbuf("negb", [P, NCH, 4], F32)
        aneg = S.sbuf("aneg", [P, 4], F32)
        dtb = S.sbuf("dtb", [P, 4], F32)
        S.dma("sp", gall[:], ad[:])
        S.dma("sp", ball[:], bd[:])
        S.dma("sp", aneg[:], ald[:])
        S.dma("sp", dtb[:], dbd[:])

        def bcn(v):
            return View(v.buf, v.ap.rearrange("p (o e) -> p o e", o=1).to_broadcast([P, NCH, 4]))
        S.tt(gall[:], gall[:], bcn(dtb[:]), ALU.add)
        S.act(gall[:], gall[:], AF.Exp)
        S.act(gall[:], gall[:], AF.Ln, bias=C.ones[:, 0:1])
        S.act(aneg[:], aneg[:], AF.Exp)
        S.ts(aneg[:], aneg[:], -1.0, ALU.mult)
        S.tt(gall[:], gall[:], bcn(aneg[:]), ALU.mult)
        S.act(ball[:], ball[:], AF.Sigmoid)
        S.ts(negb[:], ball[:], -1.0, ALU.mult)
        BA = [S.psum("BA%d" % h, [P, 512]) for h in range(2)]
        B1 = [S.psum("B1%d" % h, [P, 512]) for h in range(2)]
        B2 = [S.psum("B2%d" % h, [P, 512]) for h in range(2)]
        B3 = [S.psum("B3%d" % h, [P, 512]) for h in range(2)]
        Wk = []
        for h in range(2):
            W = NS()
            for nm in ("X", "Dm", "Dv", "Ds", "kbg", "kdec", "vb", "vnew", "oq", "o", "of_", "zt", "t1"):
                setattr(W, nm, S.sbuf("%s%d" % (nm, h), [P, P], F32))
            for nm in ("NA", "RA", "uw"):
                setattr(W, nm, S.sbuf("%s%d" % (nm, h), [P, 2 * P], F32))
            W.NR = [S.sbuf("NR%d%d" % (h, i), [P, 2 * P], F32) for i in range(2)]
            W.Xc = [S.sbuf("Xc%d%d" % (h, i), [P, P], F32) for i in range(2)]
            W.esm = S.sbuf("esm%d" % h, [P, 4], F32)
            W.bg = S.sbuf("bg%d" % h, [P, 1], F32)
            W.ss = S.sbuf("ss%d" % h, [P, 1], F32)
            W.oo = [S.sbuf("oo%d%d" % (h, i), [P, P], F32) for i in range(2)]
            Wk.append(W)
        state = [S.sbuf("state%d" % h, [P, P], F32) for h in range(2)]
        raw = S.sbuf("raw", [P, 6, 516], F32)
        acc = [S.sbuf("acc%d" % i, [P, 512], F32) for i in range(2)]
        sqb = S.sbuf("sqb", [P, 512], F32)
        lnb = S.sbuf("lnb", [P, 512], F32)
        rsb = S.sbuf("rsb", [P, 512], F32)
        qkv = [S.sbuf("qkv%d" % i, [P, 6, 512], F32) for i in range(2)]
        ofb = [[S.sub("of%d_%d" % (c, h), ofd.t[c][:, h * P:(h + 1) * P]) for h in range(2)] for c in range(NCH)]

        def prep(src, t0, n, dst):
            S.dma("sp", raw[:, :, :n + 4], src[:, :, t0:t0 + n + 4].rearrange("c p t -> p c t"))
            for c in range(6):
                a = acc[c % 2]
                S.ts(a[:, :n], raw[:, c, 0:n], cw[:, c, 0:1], ALU.mult)
                for j in range(1, 5):
                    S.stt(a[:, :n], raw[:, c, j:j + n], cw[:, c, j:j + 1], a[:, :n], ALU.mult, ALU.add)
                if c >= 4:
                    S.act(dst[:, c, :n], a[:, :n], AF.Silu)
                else:
                    S.act(a[:, :n], a[:, :n], AF.Silu)
                    S.act(sqb[:, :n], a[:, :n], AF.Square)
                    pb = B3[c % 2]
                    S.matmul(pb[:, :n], C.ones[:], sqb[:, :n])
                    S.act(lnb[:, :n], pb[:, :n], AF.Ln, bias=C.eps[:, 0:1])
                    S.act(rsb[:, :n], lnb[:, :n], AF.Exp, scale=-0.5)
                    S.stt(dst[:, c, :n], a[:, :n], (128.0 ** -0.5) if c < 2 else 1.0, rsb[:, :n], ALU.mult, ALU.mult)

        def unit(hl, dr, gp, qv, kv, vv):
            col = dr * 2 + hl
            g = gall[:, gp, col:col + 1]
            nb = negb[:, gp, col:col + 1]
            bt = ball[:, gp, col:col + 1]
            W = Wk[hl]
            bA, b1, b2, b3 = BA[hl], B1[hl], B2[hl], B3[hl]
            Tri, Xm, Val, SVal = (M["le"], M["gt"], M["ge"], M["gt"]) if dr == 0 else (M["ge"], M["lt"], M["le"], M["lt"])
            S.ts(W.X[:], Xm[:], g, ALU.mult)
            S.matmul(bA[:, 0:128], Tri[:], W.X[:])
            S.matmul(bA[:, 128:129], Tri[:], g)
            S.matmul(bA[:, 129:130], Xm[:], g)
            S.matmul(bA[:, 130:131], halfA[:], g)
            S.matmul(bA[:, 131:132], halfB[:], g)
            S.matmul(b1[:, 0:128], kv, kv)
            S.matmul(b1[:, 128:256], qv, kv)
            S.transpose(bA[:, 256:384], kv, C.ident[:])
            S.transpose(bA[:, 384:512], vv, C.ident[:])
            yield
            S.act(W.Dm[:], bA[:, 0:128], AF.Exp)
            S.act(W.esm[:], bA[:, 128:132], AF.Exp)
            S.tt(W.bg[:], W.esm[:, 0:1], bt, ALU.mult)
            S.act(W.kdec[:], bA[:, 256:384], AF.Identity, scale=W.esm[:, 1:2])
            S.act(W.vb[:], bA[:, 384:512], AF.Identity, scale=bt)
            S.act(W.kbg[:], bA[:, 256:384], AF.Identity, scale=W.bg[:, 0:1])
            S.tt(W.Dv[:], W.Dm[:], Val[:], ALU.mult)
            S.tt(W.Ds[:], W.Dm[:], SVal[:], ALU.mult)
            S.stt(W.NA[:, 0:128], b1[:, 0:128], nb, W.Ds[:], ALU.mult, ALU.mult)
            S.tt(W.NA[:, 128:256], b1[:, 128:256], W.Dv[:], ALU.mult)
            yield
            S.transpose(b1[:, 256:384], W.NA[:, 0:128], C.ident[:])
            S.transpose(b1[:, 384:512], W.NA[:, 128:256], C.ident[:])
            S.copy(W.RA[:], b1[:, 256:512])
            X = W.Xc[0]
            S.tt(X[:], W.RA[:, 0:128], C.ident[:], ALU.add)
            yield
            Ncur = W.NA[:, 0:128]
            Rcur = W.RA[:, 0:128]
            for lev in range(5):
                NR = W.NR[lev % 2]
                S.matmul(b2[:, 0:128], Rcur, Ncur)
                if lev < 4:
                    S.matmul(b2[:, 128:256], Ncur, Rcur)
                    S.copy(NR[:], b2[:, 0:256])
                else:
                    S.copy(NR[:, 0:128], b2[:, 0:128])
                yield
                S.matmul(b2[:, 256:384], NR[:, 0:128], X[:])
                Xn = W.Xc[(lev + 1) % 2]
                S.tt(Xn[:], X[:], b2[:, 256:384], ALU.add)
                X = Xn
                Ncur = NR[:, 0:128]
                Rcur = NR[:, 128:256]
                yield
            S.matmul(b3[:, 0:128], X[:], W.vb[:])
            S.matmul(b3[:, 128:256], W.kbg[:], X[:])
            S.copy(W.uw[:], b3[:, 0:256])
            yield
            blocks = [(0, 64), (64, 128)] if dr == 0 else [(64, 128), (0, 64)]
            Sst = state[hl]
            for bi, (r0, r1) in enumerate(blocks):
                reg = b3[:, 256:512] if bi == 0 else b3[:, 0:256]
                S.matmul(reg[:, 0:128], W.uw[:, 128:256], Sst[:])
                S.matmul(reg[:, 128:256], qv, Sst[:])
                S.tt(W.vnew[r0:r1, :], W.uw[r0:r1, 0:128], reg[r0:r1, 0:128], ALU.subtract)
                S.ts(W.oq[r0:r1, :], reg[r0:r1, 128:256], W.esm[r0:r1, 0:1], ALU.mult)
                yield
                S.matmul(b1[:, 0:128], W.kdec[r0:r1, :], W.vnew[r0:r1, :])
                egX = W.esm[:, 2:3] if r0 == 0 else W.esm[:, 3:4]
                S.stt(Sst[:], Sst[:], egX, b1[:, 0:128], ALU.mult, ALU.add)
                yield
            S.matmul(b1[:, 128:256], W.RA[:, 128:256], W.vnew[:])
            S.tt(W.o[:], W.oq[:], b1[:, 128:256], ALU.add)
            if dr == 0:
                S.dma("pool", ofb[gp][hl][:], W.o[:])
            else:
                S.dma("sp", W.of_[:], ofb[gp][hl][:])
                S.dma("sp", W.zt[:], zd[:, gp, hl * P:(hl + 1) * P])
                S.tt(W.o[:], W.o[:], W.of_[:], ALU.add)
                S.act(W.t1[:], W.o[:], AF.Square, accum_out=W.ss[:, 0:1])
                yield
                S.act(W.ss[:], W.ss[:], AF.Ln, scale=1.0 / 128.0, bias=C.eps[:, 0:1])
                S.act(W.ss[:], W.ss[:], AF.Exp, scale=-0.5)
                S.act(W.zt[:], W.zt[:], AF.Silu)
                S.stt(W.t1[:], W.o[:], W.ss[:, 0:1], nw[:], ALU.mult, ALU.mult)
                oo = W.oo[gp % 2]
                S.tt(oo[:], W.t1[:], W.zt[:], ALU.mult)
                S.dma("pool", S.sub("oa", oa.t[gp][:, hl * P:(hl + 1) * P])[:], oo[:])
            yield

        for dr in range(2):
            for h in range(2):
                S.memset(state[h][:], 0.0)
            segs = [(qc, 0, LC), (ql, LC, L)]
            tl = []
            for (src, base, seglen) in segs:
                tt_ = [(src, base, t0, min(512, seglen - t0)) for t0 in range(0, seglen, 512)]
                if dr == 1:
                    tt_ = tt_[::-1]
                tl += tt_
            for ti, (src, base, t0, n) in enumerate(tl):
                dst = qkv[ti % 2]
                prep(src, t0, n, dst)
                prs = list(range(n // P))
                if dr == 1:
                    prs = prs[::-1]
                for pi in prs:
                    gp = (base + t0) // P + pi
                    sl = slice(pi * P, (pi + 1) * P)
                    gens = [unit(h, dr, gp, dst[:, 0 + h, sl], dst[:, 2 + h, sl], dst[:, 4 + h, sl]) for h in range(2)]
                    alive = [True, True]
                    while any(alive):
                        for h in range(2):
                            if alive[h]:
                                try:
                                    next(gens[h])
                                except StopIteration:
                                    alive[h] = False
        S.barrier()
    return nc


NFM = 36
NTK = 1568
GRP = [[0, 1], [2, 3], [4, 5], [6, 7]]


def build_fused(L, LC, depth=2):
    NT, NCX = L // 2, LC // 2
    TT = NT + NCX
    LT = L + LC
    NCH = LT // P
    NCC = LC // P
    LK = LT
    NKC = LK // P
    NLC = NT // P
    assert NCX == P
    nc = bass.Bass("TRN2", target_bir_lowering=False)
    with ExitStack() as st:
        S = Sched(nc, st)
        ccsem = st.enter_context(nc.semaphore("ccsem"))
        cc = [0]
        xT = S.dram("xT", [P, KC, TT], F32, kind="ExternalInput")
        cv = S.dram("cv", [P, KC, 2], F32, kind="ExternalInput")
        selv = S.dram("selv", [P, 2], F32, kind="ExternalInput")
        yT = S.dram("yT", [P, KC, TT], F32, kind="ExternalOutput")
        Wl = []
        for i in range(depth):
            W = NS()
            sfx = "_%d" % i
            W.wada1 = S.dram("wada1" + sfx, [D, 5 * D], F32, kind="ExternalInput")
            W.bada1 = S.dram("bada1" + sfx, [P, 40], F32, kind="ExternalInput")
            W.wada2 = S.dram("wada2" + sfx, [D, 6 * D], F32, kind="ExternalInput")
            W.bada2 = S.dram("bada2" + sfx, [P, 48], F32, kind="ExternalInput")
            W.ng = S.dram("ng" + sfx, [P, 48], F32, kind="ExternalInput")
            W.w1a = S.dram("w1a" + sfx, [D, 2 * DFF], F32, kind="ExternalInput")
            W.w2a = S.dram("w2a" + sfx, [DFF, D], F32, kind="ExternalInput")
            W.w1b = S.dram("w1b" + sfx, [D, 2 * DFF], F32, kind="ExternalInput")
            W.w2b = S.dram("w2b" + sfx, [DFF, D], F32, kind="ExternalInput")
            W.win = S.dram("win" + sfx, [D, NFM * P + NTK], F32, kind="ExternalInput")
            W.wg = S.dram("wg" + sfx, [D, 3 * D], F32, kind="ExternalInput")
            W.wb = S.dram("wb" + sfx, [1536, D], F32, kind="ExternalInput")
            W.wo = S.dram("wo" + sfx, [D, D], F32, kind="ExternalInput")
            W.snw = S.dram("snw" + sfx, [P, 4], F32, kind="ExternalInput")
            W.lam = S.dram("lam" + sfx, [P, 256], F32, kind="ExternalInput")
            W.dnw = S.dram("dnw" + sfx, [P, 1], F32, kind="ExternalInput")
            W.li = S.dram("li" + sfx, [P, 1], F32, kind="ExternalInput")
            W.scw = S.dram("scw" + sfx, [P, 4, 5], F32, kind="ExternalInput")
            W.scb = S.dram("scb" + sfx, [P, 4], F32, kind="ExternalInput")
            W.sdtb = S.dram("sdtb" + sfx, [P, 8], F32, kind="ExternalInput")
            W.salog = S.dram("salog" + sfx, [P, 8], F32, kind="ExternalInput")
            W.sdsk = S.dram("sdsk" + sfx, [P, 4], F32, kind="ExternalInput")
            W.gcw = S.dram("gcw" + sfx, [P, 6, 5], F32, kind="ExternalInput")
            W.galog = S.dram("galog" + sfx, [P, 4], F32, kind="ExternalInput")
            W.gdtb = S.dram("gdtb" + sfx, [P, 4], F32, kind="ExternalInput")
            W.gnw = S.dram("gnw" + sfx, [P, P], F32, kind="ExternalInput")
            Wl.append(W)
        Xs = S.dram("Xs", [P, KC, TT], F32)
        X1 = S.dram("X1s", [P, KC, TT], F32)
        X2 = S.dram("X2s", [P, KC, TT], F32)
        NB = NT // 256
        PTL = nc.dram_tensor("PTL", [NFM, P, NT], F32)
        PTLG = nc.dram_tensor("PTLG", [NFM, 2, P, NT], F32)
        PTC = nc.dram_tensor("PTC", [NFM, P, NCX], F32)
        PTCG = nc.dram_tensor("PTCG", [2, 2, 18, P, NCX], F32)
        PKL = nc.dram_tensor("PKL", [NB, 256, NTK], F32)
        PKLG = nc.dram_tensor("PKLG", [NB, 2, 256, NTK], F32)
        PKC = nc.dram_tensor("PKC", [NCX, NTK], F32)
        PKCG = nc.dram_tensor("PKCG", [2, NCX, NTK], F32)
        MOL = nc.dram_tensor("MOL", [6, 2, P, NT], F32)
        MOLG = nc.dram_tensor("MOLG", [6, 2, 2, P, NT], F32)
        MOC = nc.dram_tensor("MOC", [6, 2, P, NCX], F32)
        MOCG = nc.dram_tensor("MOCG", [2, 6, 2, P, NCX], F32)
        OF = S.dram("OFs", [NCH, P, 256], F32)

        def mo_dst(c0, c1, s_, off, n):
            if off >= NT:
                return MOC.ap()[c0:c1, s_, :, off - NT:off - NT + n]
            return MOL.ap()[c0:c1, s_, :, off:off + n]

        def dsub(ap):
            return Buf("u", ap, "dram")[:] if False else View(Buf("u", ap, "dram"), ap)

        tiles = mk_tiles(NT, NCX)
        C = mk_consts(S, nc)
        sel = S.sbuf("sel", [P, 2], F32)
        S.dma("sp", sel[:], selv[:])
        dummy = S.sbuf("dummy", [P, 1], F32)

        def blend(dst, alt):
            S.ts(dst, dst, sel[:, 0:1], ALU.mult)
            S.stt(dst, alt, sel[:, 1:2], dst, ALU.mult, ALU.add)

        def gather_many(pairs):
            S.barrier()
            for (i_ap, o_ap) in pairs:
                cc[0] += 1
                nc.gpsimd.collective_compute("AllGather", ALU.bypass, replica_groups=GRP, ins=[i_ap], outs=[o_ap]).then_inc(ccsem, 1)
            nc.gpsimd.wait_ge(ccsem, cc[0])
            S.memset(dummy[:], 0.0, e="pool")
            S.barrier()

        def gather_P():
            pr = [(PTL.ap()[c], PTLG.ap()[c].rearrange("r p t -> (r p) t")) for c in range(NFM)]
            pr += [(PTC.ap()[h * 18:(h + 1) * 18].rearrange("c p t -> (c p) t"), PTCG.ap()[h].rearrange("r c p t -> (r c p) t")) for h in range(2)]
            pr += [(PKL.ap()[b], PKLG.ap()[b].rearrange("r t e -> (r t) e")) for b in range(NB)]
            pr += [(PKC.ap(), PKCG.ap().rearrange("r t e -> (r t) e"))]
            gather_many(pr)

        def gather_M():
            pr = [(MOL.ap()[c, s_], MOLG.ap()[c, s_].rearrange("r p t -> (r p) t")) for c in range(6) for s_ in range(2)]
            pr += [(MOC.ap().rearrange("c s p t -> (c s p) t"), MOCG.ap().rearrange("r c s p t -> (r c s p) t"))]
            gather_many(pr)

        def tokpos(gc):
            if gc < NCC:
                return gc, NT
            t = (gc - NCC) * P
            return t // NT, t % NT

        def mk_ps():
            PS = NS()
            PS.ss = S.psum("ps_ss", [P, 512])
            PS.ss2 = S.psum("ps_ss2", [P, 512])
            PS.g = [S.psum("ps_g%d" % i, [P, 512]) for i in range(2)]
            PS.u = [S.psum("ps_u%d" % i, [P, 512]) for i in range(2)]
            PS.y = [S.psum("ps_y%d" % i, [P, 512]) for i in range(2)]
            return PS

        def bc(v):
            return View(v.buf, v.ap.rearrange("p (c o) -> p c o", o=1).to_broadcast([P, KC, 2]))

        def ph_R1(W, xin, x1t):
            with S.scope():
                alloc_small(S, C)
                PS = mk_ps()
                mods = S.sbuf("mods", [P, 40, 2], F32)
                ngt = S.sbuf("ngt", [P, 6, KC], F32)
                S.dma("sp", ngt[:], W.ng[:].rearrange("p (m c) -> p m c", c=KC))
                A1 = S.sbuf("A1", [P, KC, 2], F32)
                G1 = S.sbuf("G1", [P, KC, 2], F32)
                A2 = S.sbuf("A2", [P, KC, 2], F32)
                with S.scope():
                    stg = [S.sbuf("stgm%d" % i, [P, 5 * D], F32) for i in range(KC)]
                    compute_mods(S, C, cv, W.wada1, W.bada1, 5, stg, PS.g[0], mods)
                S.stt(A1[:], mods[:, 8:16, :], 1.0, bc(ngt[:, 0, :]), ALU.add, ALU.mult)
                S.stt(G1[:], mods[:, 16:24, :], 0.5, bc(ngt[:, 1, :]), ALU.mult, ALU.mult)
                S.stt(A2[:], mods[:, 32:40, :], 1.0, bc(ngt[:, 2, :]), ALU.add, ALU.mult)
                B1 = mods[:, 0:8, :]
                B2 = mods[:, 24:32, :]
                with S.scope():
                    w1b = S.sbuf("w1b", [P, KC, 2 * DFF], BF16)
                    w2b = S.sbuf("w2b", [P, FC, D], BF16)
                    with S.scope():
                        stages = [S.sbuf("wst%d" % i, [P, 2048], F32) for i in range(3)]
                        load_w(S, W.w1a, w1b, D, 2 * DFF, stages)
                        load_w(S, W.w2a, w2b, DFF, D, stages)
                    alloc_ffn_work(S, C)
                    ffn_sweep(S, C, tiles, lambda j: xin[j][:], lambda j: x1t[j][:], w1b, w2b, A1[:], B1, G1[:], PS)
                with S.scope():
                    NW = NFM * P + NTK
                    winb = S.sbuf("winb", [P, KC, NW], BF16)
                    with S.scope():
                        stages = [S.sbuf("wst%d" % i, [P, 2048], F32) for i in range(3)]
                        load_w(S, W.win, winb, D, NW, stages)
                    xt2 = [S.sbuf("xq%d" % i, [P, KC, 512], F32) for i in range(2)]
                    h = S.sbuf("h2", [P, KC, 512], BF16)
                    ost = [S.sbuf("ost%d" % i, [P, 4, 512], F32) for i in range(3)]
                    tst = [S.sbuf("tst%d" % i, [P, NTK], F32) for i in range(2)]
                    pps = PS.g + PS.u + PS.y
                    gi = 0
                    ti = 0
                    for j, (s0, n, col) in enumerate(tiles):
                        xt = xt2[j % 2]
                        S.dma("sp", xt[:, :, :n], x1t[j][:])
                        norm_mod(S, C, xt, n, A2[:], B2, col, h, PS.ss)
                        for c0 in range(0, NFM, 4):
                            nn = min(4, NFM - c0)
                            o = ost[gi % 3]
                            gi += 1
                            for cc_ in range(nn):
                                pp = pps[(c0 + cc_) % 6]
                                for k in range(KC):
                                    S.matmul(pp[:, :n], winb[:, k, (c0 + cc_) * P:(c0 + cc_ + 1) * P], h[:, k, :n],
                                             start=(k == 0), stop=(k == KC - 1))
                                S.copy(o[:, cc_, :n], pp[:, :n], e=("act" if cc_ % 2 else "dve"))
                            pdst = PTL.ap()[c0:c0 + nn, :, s0:s0 + n] if col == 0 else PTC.ap()[c0:c0 + nn, :, 0:n]
                            S.dma("pool", dsub(pdst.rearrange("c p t -> p c t")), o[:, :nn, :n])
                        for sb in range(n // P):
                            tt_ = tst[ti % 2]
                            ti += 1
                            for q, c0 in enumerate(range(0, NTK, 512)):
                                w = min(512, NTK - c0)
                                pp = pps[q % 6]
                                for k in range(KC):
                                    S.matmul(pp[:, :w], h[:, k, sb * P:(sb + 1) * P], winb[:, k, NFM * P + c0:NFM * P + c0 + w],
                                             start=(k == 0), stop=(k == KC - 1))
                                S.copy(tt_[:, c0:c0 + w], pp[:, :w], e=("act" if q % 2 else "dve"))
                            trow = s0 + sb * P
                            kdst = PKL.ap()[trow // 256, trow % 256:trow % 256 + P, :] if col == 0 else PKC.ap()[0:P, :]
                            S.dma("pool", dsub(kdst), tt_[:])

        def lat_rc(t0):
            return t0 // NT, t0 % NT

        def load_fm(q, dst, alt, c0, nch, seg, t0, n, halo):
            seglen = L if seg == "lat" else LC
            a, b = max(0, t0 - halo), min(seglen, t0 + n + halo)
            if halo and (t0 - halo < 0):
                S.memset(dst[:, :, 0:halo], 0.0)
                S.memset(alt[:, :, 0:halo], 0.0)
            if halo and (t0 + n + halo > seglen):
                S.memset(dst[:, :, n + halo:n + 2 * halo], 0.0)
                S.memset(alt[:, :, n + halo:n + 2 * halo], 0.0)
            pieces = []
            per = NT if seg == "lat" else NCX
            base = 0 if seg == "lat" else NT
            p = a
            while p < b:
                r = p // per
                e = min(b, (r + 1) * per)
                pieces.append((r, p % per, e - p, p - (t0 - halo)))
                p = e
            for g, tgt in ((0, dst), (1, alt)):
                for (r, col0, ln, d0) in pieces:
                    if seg == "lat":
                        sap = PTLG.ap()[g * 18 + c0:g * 18 + c0 + nch, r, :, col0:col0 + ln]
                    else:
                        sap = PTCG.ap()[g, r, c0:c0 + nch, :, col0:col0 + ln]
                    S.dma(q, tgt[:, :, d0:d0 + ln], dsub(sap.rearrange("c p t -> p c t")))
            blend(dst[:, :, :], alt[:, :, :])

        def load_tok(q, dst, alt, gc, e0, ne):
            s, off = tokpos(gc)
            for g, tgt in ((0, dst), (1, alt)):
                if off >= NT:
                    sap = PKCG.ap()[s, 0:P, g * 784 + e0:g * 784 + e0 + ne]
                else:
                    sap = PKLG.ap()[off // 256, s, off % 256:off % 256 + P, g * 784 + e0:g * 784 + e0 + ne]
                S.dma(q, tgt, dsub(sap))
            blend(dst, alt)

        def load_small(smallst, alt):
            for g, tgt in ((0, smallst), (1, alt)):
                for r in range(2):
                    S.dma("sp", tgt[:, r, :], dsub(PKCG.ap()[r, 0:P, g * 784 + 768:g * 784 + 784]))
                    for bq in range(NB):
                        c_ = NCC + r * NLC + 2 * bq
                        S.dma("sp" if bq % 2 else "pool", tgt[:, c_:c_ + 2, :],
                              dsub(PKLG.ap()[bq, r, :, g * 784 + 768:g * 784 + 784].rearrange("(c p) e -> p c e", p=P)))
            blend(smallst[:], alt[:])

        def ph_Mdiff(W):
            with S.scope():
                C.sq = [S.sbuf("sq%d" % i, [P, 512], F32) for i in range(2)]
                C.lnt = C.sq[0]
                onesb = S.sbuf("onesb", [P, P], BF16)
                S.memset(onesb[:], 1.0)
                Q = [S.sbuf("Q%d" % h, [P, L], BF16) for h in range(2)]
                QC = [S.sbuf("QC%d" % h, [P, LC], BF16) for h in range(2)]
                K = [S.sbuf("K%d" % h, [P, LK], BF16) for h in range(2)]
                V = S.sbuf("V", [P, NKC, 256], BF16)
                lam = S.sbuf("lamt", [P, 4, 64], F32)
                S.dma("sp", lam[:], W.lam[:].rearrange("p (a b) -> p a b", b=64))
                nw = S.sbuf("nwt", [P, 1], F32)
                li = S.sbuf("lit", [P, 1], F32)
                S.dma("sp", nw[:], W.dnw[:])
                S.dma("sp", li[:], W.li[:])
                pr = S.sbuf("pr", [P, 2, 64], F32)
                s12 = S.sbuf("s12", [P, 2], F32)
                S.tt(pr[:, 0, :], lam[:, 0, :], lam[:, 1, :], ALU.mult)
                S.tt(pr[:, 1, :], lam[:, 2, :], lam[:, 3, :], ALU.mult)
                S.reduce(s12[:], pr[:], ALU.add)
                e12 = S.sbuf("e12", [P, 2], F32)
                S.act(e12[:], s12[:], AF.Exp)
                neglam = S.sbuf("neglam", [P, 1], F32)
                S.tt(neglam[:], e12[:, 1:2], e12[:, 0:1], ALU.subtract)
                S.tt(neglam[:], neglam[:], li[:], ALU.subtract)
                sc2 = S.sbuf("sc2", [P, 1], F32)
                S.ts(sc2[:], li[:], -1.0, ALU.mult, 1.0, ALU.add)
                S.tt(sc2[:], sc2[:], nw[:], ALU.mult)
                with S.scope():
                    cosb = S.sbuf("cosb", [P, L], F32)
                    sinb = S.sbuf("sinb", [P, L], F32)
                    rope_tables(S, nc, C, L, cosb, sinb)
                    with S.scope():
                        a = [S.sbuf("la%d" % i, [P, 1, 512], F32) for i in range(2)]
                        a2 = [S.sbuf("la2%d" % i, [P, 1, 512], F32) for i in range(2)]
                        b = [S.sbuf("lb%d" % i, [P, 1, 512], F32) for i in range(2)]
                        b2 = [S.sbuf("lb2%d" % i, [P, 1, 512], F32) for i in range(2)]
                        vst = [S.sbuf("vst%d" % i, [P, 4, 256], F32) for i in range(2)]
                        vs2 = [S.sbuf("vs2%d" % i, [P, 4, 256], F32) for i in range(2)]
                        i = 0
                        for h in range(2):
                            for (cq, csw, dst) in ((6 + h, 8 + h, Q[h]), (10 + h, 12 + h, K[h])):
                                for c0 in range(0, L, 512):
                                    ta, tb = a[i % 2], b[i % 2]
                                    load_fm("sp", ta[:], a2[i % 2][:], cq, 1, "lat", c0, 512, 0)
                                    load_fm("pool", tb[:], b2[i % 2][:], csw, 1, "lat", c0, 512, 0)
                                    i += 1
                                    S.tt(ta[:, 0, :], ta[:, 0, :], cosb[:, c0:c0 + 512], ALU.mult)
                                    S.tt(tb[:, 0, :], tb[:, 0, :], sinb[:, c0:c0 + 512], ALU.mult, e="pool")
                                    S.tt(dst[:, c0:c0 + 512], ta[:, 0, :], tb[:, 0, :], ALU.add)
                            ta = a[i % 2]
                            load_fm("sp", ta[:, :, :LC], a2[i % 2][:, :, :LC], 10 + h, 1, "ctx", 0, LC, 0)
                            i += 1
                            S.copy(K[h][:, L:LK], ta[:, 0, :LC])
                            ta = a[i % 2]
                            load_fm("sp", ta[:, :, :LC], a2[i % 2][:, :, :LC], 6 + h, 1, "ctx", 0, LC, 0)
                            i += 1
                            S.copy(QC[h][:], ta[:, 0, :LC])
                        vi = 0
                        for r in range(2):
                            for bq in range(NB):
                                t, t2 = vst[vi % 2], vs2[vi % 2]
                                vi += 1
                                for g, tgt in ((0, t), (1, t2)):
                                    S.dma("sp", tgt[:, 0:2, :], dsub(PKLG.ap()[bq, r, :, g * 784:g * 784 + 256].rearrange("(c p) e -> p c e", p=P)))
                                blend(t[:, 0:2, :], t2[:, 0:2, :])
                                kc = r * NLC + 2 * bq
                                S.copy(V[:, kc:kc + 2, :], t[:, 0:2, :], e="pool")
                            t, t2 = vst[vi % 2], vs2[vi % 2]
                            vi += 1
                            for g, tgt in ((0, t), (1, t2)):
                                S.dma("sp", tgt[:, 0, :], dsub(PKCG.ap()[r, 0:P, g * 784:g * 784 + 256]))
                            blend(t[:, 0, :], t2[:, 0, :])
                            S.copy(V[:, L // P + r, :], t[:, 0, :], e="pool")
                ps_s = [[S.psum("ps_s%d%d" % (j, i), [P, 512]) for i in range(2)] for j in range(2)]
                ps_o = [S.psum("ps_o%d" % j, [P, 512]) for j in range(2)]
                ps_z = [S.psum("ps_z%d" % j, [P, 512]) for j in range(2)]
                pt = [[S.sbuf("pt%d%d" % (j, i), [P, 512], BF16) for i in range(2)] for j in range(2)]
                rz = [S.sbuf("rz%d" % j, [P, 512], F32) for j in range(2)]
                t0_ = S.sbuf("t0", [P, 512], F32)
                t1_ = S.sbuf("t1", [P, 512], F32)
                rstd = S.sbuf("rstd", [P, 512], F32)
                oo = [S.sbuf("oo%d" % i, [P, 512], F32) for i in range(2)]
                jobs = []
                for h in range(2):
                    for q0 in range(0, L, 512):
                        s_, off = lat_rc(q0)
                        jobs.append((h, Q[h][:, q0:q0 + 512], 512, 0, NKC, [(mo_dst(2 + h, 3 + h, s_, off, 512)[0], 0, 512)]))
                    jobs.append((h, QC[h][:], LC, L // P, NKC, [(mo_dst(2 + h, 3 + h, r, NT, NCX)[0], r * NCX, NCX) for r in range(2)]))
                zacc = [S.sbuf("zacc%d" % j, [P, 512], F32) for j in range(2)]
                zeng = ["dve", "dve"]
                for ji, (h, qv, n, kc0, kc1, outs) in enumerate(jobs):
                    def qk(kc):
                        for j in range(2):
                            S.matmul(ps_s[j][kc % 2][:, :n], K[h][j * 64:(j + 1) * 64, kc * P:(kc + 1) * P], qv[j * 64:(j + 1) * 64, :],
                                     start=True, stop=True)
                    qk(kc0)
                    for kc in range(kc0, kc1):
                        bi = kc % 2
                        for j in range(2):
                            S.act(pt[j][bi][:, :n], ps_s[j][bi][:, :n], AF.Exp, scale=0.125)
                        if kc + 1 < kc1:
                            qk(kc + 1)
                        for j in range(2):
                            S.matmul(ps_o[j][:, :n], V[:, kc, h * P:(h + 1) * P], pt[j][bi][:, :n], start=(kc == kc0), stop=(kc == kc1 - 1))
                            if kc == kc0:
                                S.copy(zacc[j][:, :n], pt[j][bi][:, :n], e=zeng[j])
                            else:
                                S.tt(zacc[j][:, :n], zacc[j][:, :n], pt[j][bi][:, :n], ALU.add, e=zeng[j])
                    for j in range(2):
                        S.matmul(ps_z[j][:, :n], C.ones[:], zacc[j][:, :n], start=True, stop=True)
                    for j in range(2):
                        S.recip(rz[j][:, :n], ps_z[j][:, :n])
                    S.tt(t0_[:, :n], ps_o[0][:, :n], rz[0][:, :n], ALU.mult)
                    S.tt(t1_[:, :n], ps_o[1][:, :n], rz[1][:, :n], ALU.mult)
                    S.stt(t0_[:, :n], t1_[:, :n], neglam[:, 0:1], t0_[:, :n], ALU.mult, ALU.add)
                    rms_rstd(S, C, lambda c: t0_[:, :n], n, 1, P, ps_s[0][0], rstd[:, :n])
                    o = oo[ji % 2]
                    S.tt(t1_[:, :n], t0_[:, :n], rstd[:, :n], ALU.mult)
                    S.ts(o[:, :n], t1_[:, :n], sc2[:, 0:1], ALU.mult)
                    for (oap, o0, on) in outs:
                        S.dma("pool", dsub(oap), o[:, o0:o0 + on])

        def ph_Mssd(W):
            with S.scope():
                tri = {0: tri_mask(S, nc, "tri_f", "le"), 1: tri_mask(S, nc, "tri_b", "ge")}
                strict = {0: tri_mask(S, nc, "str_f", "gt"), 1: tri_mask(S, nc, "str_b", "lt")}
                cw = S.sbuf("cw", [P, 4, 5], F32)
                cb = S.sbuf("cb", [P, 4], F32)
                S.dma("sp", cw[:], W.scw[:])
                S.dma("sp", cb[:], W.scb[:])
                xs_tok = S.sbuf("xs_tok", [P, NCH, 256], F32)
                B_tok = S.sbuf("B_tok", [P, NCH, P], F32)
                BT = S.sbuf("BT", [P, LT], F32)
                CT = S.sbuf("CT", [P, LT], F32)
                dtv = S.sbuf("dtv", [P, NCH, 8], F32)
                aall = S.sbuf("aall", [P, NCH, 8], F32)
                dtb = S.sbuf("dtb", [P, 8], F32)
                aneg = S.sbuf("aneg", [P, 8], F32)
                dsk = S.sbuf("dsk", [P, 4], F32)
                with S.scope():
                    sm1 = S.sbuf("sm1", [P, NCH, 16], F32)
                    sm2 = S.sbuf("sm2", [P, NCH, 16], F32)
                    load_small(sm1, sm2)
                    S.copy(dtv[:], sm1[:, :, 8:16])
                S.dma("sp", dtb[:], W.sdtb[:])
                S.dma("sp", aneg[:], W.salog[:])
                S.dma("sp", dsk[:], W.sdsk[:])
                S.tt(dtv[:], dtv[:], View(dtb, dtb.t[:].rearrange("p (o e) -> p o e", o=1).to_broadcast([P, NCH, 8])), ALU.add)
                S.act(dtv[:], dtv[:], AF.Exp)
                S.act(dtv[:], dtv[:], AF.Ln, bias=C.ones[:, 0:1])
                S.act(aneg[:], aneg[:], AF.Exp)
                S.ts(aneg[:], aneg[:], -1.0, ALU.mult)
                S.tt(aall[:], dtv[:], View(aneg, aneg.t[:].rearrange("p (o e) -> p o e", o=1).to_broadcast([P, NCH, 8])), ALU.mult)
                ps_t = [S.psum("ps_t%d" % i, [P, 512]) for i in range(2)]
                with S.scope():
                    raw = [S.sbuf("raw%d" % i, [P, 4, 516], F32) for i in range(2)]
                    raw2 = [S.sbuf("rawb", [P, 4, 516], F32)] * 2
                    acc = [S.sbuf("acc%d" % i, [P, 512], F32) for i in range(2)]
                    xsT = [S.sbuf("xsT%d" % i, [P, 512], F32) for i in range(2)]
                    segs = [("ctx", 0, LC), ("lat", LC, L)]
                    ti = 0
                    for (seg, base, seglen) in segs:
                        for t0 in range(0, seglen, 512):
                            n = min(512, seglen - t0)
                            r = raw[ti % 2]
                            load_fm("sp" if ti % 2 else "pool", r[:, :, :n + 4], raw2[ti % 2][:, :, :n + 4], 14, 4, seg, t0, n, 2)
                            ti += 1
                            for c in range(4):
                                a = acc[c % 2]
                                S.ts(a[:, :n], r[:, c, 0:n], cw[:, c, 0:1], ALU.mult)
                                for j in range(1, 5):
                                    S.stt(a[:, :n], r[:, c, j:j + n], cw[:, c, j:j + 1], a[:, :n], ALU.mult, ALU.add)
                                g0 = base + t0
                                if c < 2:
                                    S.act(xsT[c][:, :n], a[:, :n], AF.Silu, bias=cb[:, c:c + 1])
                                elif c == 2:
                                    S.act(BT[:, g0:g0 + n], a[:, :n], AF.Silu, bias=cb[:, c:c + 1])
                                else:
                                    S.act(CT[:, g0:g0 + n], a[:, :n], AF.Silu, bias=cb[:, c:c + 1])
                            for bl in range(n // P):
                                gc = (base + t0) // P + bl
                                pt = ps_t[bl % 2]
                                S.transpose(pt[:, 0:P], xsT[0][:, bl * P:(bl + 1) * P], C.ident[:])
                                S.transpose(pt[:, P:2 * P], xsT[1][:, bl * P:(bl + 1) * P], C.ident[:])
                                S.transpose(pt[:, 2 * P:3 * P], BT[:, gc * P:(gc + 1) * P], C.ident[:])
                                S.copy(xs_tok[:, gc, :], pt[:, 0:2 * P], e="dve")
                                S.copy(B_tok[:, gc, :], pt[:, 2 * P:3 * P], e="dve")
                ps_arg = [S.psum("ps_arg%d" % i, [P, 512]) for i in range(2)]
                ps_cb = S.psum("ps_cb", [P, 512])
                ps_y = S.psum("ps_y", [P, 512])
                ps_st = S.psum("ps_st", [P, 512])
                ps_sm = S.psum("ps_sm", [P, 512])
                X = [S.sbuf("X%d" % i, [P, 4, P], F32) for i in range(2)]
                LTt = [S.sbuf("LT%d" % i, [P, 4, P], F32) for i in range(2)]
                CBm = S.sbuf("CBm", [P, P], F32)
                scT = [S.sbuf("scT%d" % i, [P, 4, P], F32) for i in range(2)]
                sm = S.sbuf("sm", [P, 8], F32)
                eacs = S.sbuf("eacs", [P, 4], F32)
                edec = S.sbuf("edec", [P, 4], F32)
                etot = S.sbuf("etot", [P, 4], F32)
                dif = S.sbuf("dif", [P, 4], F32)
                xdt = [S.sbuf("xdt%d" % i, [P, 4, 64], F32) for i in range(2)]
                xdtd = [S.sbuf("xdtd%d" % i, [P, 4, 64], F32) for i in range(2)]
                ST = S.sbuf("ST", [P, 4, 64], F32)
                yt = [S.sbuf("yt%d" % i, [P, 256], F32) for i in range(2)]
                y2 = [S.sbuf("y2%d" % i, [P, 256], F32) for i in range(2)]
                yfl = [S.sbuf("yfl%d" % i, [P, 256], F32) for i in range(2)]
                zt = [S.sbuf("zt%d" % i, [P, 256], F32) for i in range(2)]
                zt2 = [S.sbuf("ztb%d" % i, [P, 256], F32) for i in range(2)]
                oT = [S.sbuf("oT%d" % i, [P, 2, P], F32) for i in range(2)]
                yfb = [S.sub("yf%d" % c, OF.t[c]) for c in range(NCH)]

                def bc4(v):
                    return View(v.buf, v.ap.rearrange("p (h o) -> p h o", o=1).to_broadcast([P, 4, 64]))
                for dr in range(2):
                    order = list(range(NCH)) if dr == 0 else (list(range(NCC - 1, -1, -1)) + list(range(NCH - 1, NCC - 1, -1)))
                    S.memset(ST[:], 0.0)
                    for it, c in enumerate(order):
                        bi = it % 2
                        a4 = aall[:, c, dr * 4:(dr + 1) * 4]
                        for h in range(4):
                            S.ts(X[bi][:, h, :], strict[dr][:], aall[:, c, dr * 4 + h:dr * 4 + h + 1], ALU.mult)
                        for h in range(4):
                            S.matmul(ps_arg[bi][:, h * P:(h + 1) * P], X[bi][:, h, :], tri[dr][:])
                        S.act(LTt[bi][:].rearrange("p h l -> p (h l)"), ps_arg[bi][:], AF.Exp)
                        S.matmul(ps_cb[:, 0:P], BT[:, c * P:(c + 1) * P], CT[:, c * P:(c + 1) * P])
                        S.tt(CBm[:], ps_cb[:, 0:P], tri[dr][:], ALU.mult)
                        S.tt(scT[bi][:], LTt[bi][:], View(CBm, CBm.t[:].rearrange("p (o l) -> p o l", o=1).to_broadcast([P, 4, P])), ALU.mult)
                        S.matmul(ps_sm[:, 0:4], tri[dr][:], a4)
                        S.matmul(ps_sm[:, 4:8], C.ones[:], a4)
                        S.copy(sm[:], ps_sm[:, 0:8])
                        S.act(eacs[:], sm[:, 0:4], AF.Exp)
                        S.act(etot[:], sm[:, 4:8], AF.Exp)
                        S.tt(dif[:], sm[:, 4:8], sm[:, 0:4], ALU.subtract)
                        S.act(edec[:], dif[:], AF.Exp)
                        xv = xs_tok[:, c, :].rearrange("p (h d) -> p h d", h=4)
                        S.tt(xdt[bi][:], xv, bc4(dtv[:, c, dr * 4:(dr + 1) * 4]), ALU.mult)
                        S.tt(xdtd[bi][:], xdt[bi][:], bc4(edec[:]), ALU.mult)
                        for h in range(4):
                            S.matmul(ps_y[:, h * 64:(h + 1) * 64], scT[bi][:, h, :], xdt[bi][:, h, :])
                        S.matmul(ps_y[:, 256:512], CT[:, c * P:(c + 1) * P], ST[:].rearrange("p h d -> p (h d)"))
                        y = yt[bi]
                        S.tt(y[:].rearrange("p (h d) -> p h d", h=4), ps_y[:, 256:512].rearrange("p (h d) -> p h d", h=4), bc4(eacs[:]), ALU.mult)
                        S.tt(y[:], y[:], ps_y[:, 0:256], ALU.add)
                        S.matmul(ps_st[:, 0:256], B_tok[:, c, :], xdtd[bi][:].rearrange("p h d -> p (h d)"))
                        S.tt(ST[:], ST[:], bc4(etot[:]), ALU.mult)
                        S.tt(ST[:].rearrange("p h d -> p (h d)"), ST[:].rearrange("p h d -> p (h d)"), ps_st[:, 0:256], ALU.add)
                        if dr == 0:
                            S.dma("pool", yfb[c][:], y[:])
                        else:
                            S.dma("sp", yfl[bi][:], yfb[c][:])
                            load_tok("sp", zt[bi][:], zt2[bi][:], c, 512, 256)
                            o = y2[bi]
                            S.tt(o[:].rearrange("p (h d) -> p h d", h=4), xv, bc4(dsk[:]), ALU.mult)
                            S.tt(y[:], y[:], yfl[bi][:], ALU.add)
                            S.tt(o[:], o[:], y[:], ALU.add)
                            S.act(zt[bi][:], zt[bi][:], AF.Silu)
                            S.tt(o[:], o[:], zt[bi][:], ALU.mult)
                            S.transpose(ps_cb[:, P:2 * P], o[:, 0:P], C.ident[:])
                            S.transpose(ps_cb[:, 2 * P:3 * P], o[:, P:2 * P], C.ident[:])
                            S.copy(oT[bi][:].rearrange("p a b -> p (a b)"), ps_cb[:, P:3 * P])
                            s_, off = tokpos(c)
                            S.dma("pool", dsub(mo_dst(4, 6, s_, off, P).rearrange("c p t -> p c t")), oT[bi][:])

        def ph_Mgdn(W):
            with S.scope():
                M = {k: tri_mask(S, nc, "m_" + k, k, blk=64) for k in ("le", "ge", "gt", "lt")}
                halfA = S.sbuf("halfA", [P, P], F32)
                halfB = S.sbuf("halfB", [P, P], F32)
                S.memset(halfA[:], 0.0)
                S.memset(halfB[:], 0.0)
                S.memset(halfA[0:64, :], 1.0)
                S.memset(halfB[64:128, :], 1.0)
                cw = S.sbuf("cw", [P, 6, 5], F32)
                S.dma("sp", cw[:], W.gcw[:])
                nw = S.sbuf("nw", [P, P], F32)
                S.dma("sp", nw[:], W.gnw[:])
                gall = S.sbuf("gall", [P, NCH, 4], F32)
                ball = S.sbuf("ball", [P, NCH, 4], F32)
                negb = S.sbuf("negb", [P, NCH, 4], F32)
                aneg = S.sbuf("aneg", [P, 4], F32)
                dtb = S.sbuf("dtb", [P, 4], F32)
                with S.scope():
                    sm1 = S.sbuf("sm1", [P, NCH, 16], F32)
                    sm2 = S.sbuf("sm2", [P, NCH, 16], F32)
                    load_small(sm1, sm2)
                    S.copy(gall[:], sm1[:, :, 0:4])
                    S.copy(ball[:], sm1[:, :, 4:8])
                S.dma("sp", aneg[:], W.galog[:])
                S.dma("sp", dtb[:], W.gdtb[:])

                def bcn(v):
                    return View(v.buf, v.ap.rearrange("p (o e) -> p o e", o=1).to_broadcast([P, NCH, 4]))
                S.tt(gall[:], gall[:], bcn(dtb[:]), ALU.add)
                S.act(gall[:], gall[:], AF.Exp)
                S.act(gall[:], gall[:], AF.Ln, bias=C.ones[:, 0:1])
                S.act(aneg[:], aneg[:], AF.Exp)
                S.ts(aneg[:], aneg[:], -1.0, ALU.mult)
                S.tt(gall[:], gall[:], bcn(aneg[:]), ALU.mult)
                S.act(ball[:], ball[:], AF.Sigmoid)
                S.ts(negb[:], ball[:], -1.0, ALU.mult)
                BA = [S.psum("BA%d" % h, [P, 512]) for h in range(2)]
                B1 = [S.psum("B1%d" % h, [P, 512]) for h in range(2)]
                B2 = [S.psum("B2%d" % h, [P, 512]) for h in range(2)]
                B3 = [S.psum("B3%d" % h, [P, 512]) for h in range(2)]
                Wk = []
                for h in range(2):
                    Wn = NS()
                    for nm in ("X", "Dm", "Dv", "Ds", "kbg", "kdec", "vb", "vnew", "oq", "o", "of_", "zt", "zt2", "t1"):
                        setattr(Wn, nm, S.sbuf("%s%d" % (nm, h), [P, P], F32))
                    for nm in ("NA", "RA", "uw"):
                        setattr(Wn, nm, S.sbuf("%s%d" % (nm, h), [P, 2 * P], F32))
                    Wn.NR = [S.sbuf("NR%d%d" % (h, i), [P, 2 * P], F32) for i in range(2)]
                    Wn.Xc = [S.sbuf("Xc%d%d" % (h, i), [P, P], F32) for i in range(2)]
                    Wn.esm = S.sbuf("esm%d" % h, [P, 4], F32)
                    Wn.bg = S.sbuf("bg%d" % h, [P, 1], F32)
                    Wn.ss = S.sbuf("ss%d" % h, [P, 1], F32)
                    Wn.oo = [S.sbuf("oo%d%d" % (h, i), [P, P], F32) for i in range(2)]
                    Wn.oT = [S.sbuf("oT%d%d" % (h, i), [P, P], F32) for i in range(2)]
                    Wk.append(Wn)
                state = [S.sbuf("state%d" % h, [P, P], F32) for h in range(2)]
                raw = S.sbuf("raw", [P, 6, 516], F32)
                raw2 = S.sbuf("rawb", [P, 6, 516], F32)
                acc = [S.sbuf("acc%d" % i, [P, 512], F32) for i in range(2)]
                sqb = S.sbuf("sqb", [P, 512], F32)
                lnb = S.sbuf("lnb", [P, 512], F32)
                rsb = S.sbuf("rsb", [P, 512], F32)
                qkv = [S.sbuf("qkv%d" % i, [P, 6, 512], F32) for i in range(2)]
                ofb = [[S.sub("of%d_%d" % (c, h), OF.t[c][:, h * P:(h + 1) * P]) for h in range(2)] for c in range(NCH)]

                def prep(seg, t0, n, dst, q):
                    load_fm(q, raw[:, :, :n + 4], raw2[:, :, :n + 4], 0, 6, seg, t0, n, 2)
                    for c in range(6):
                        a = acc[c % 2]
                        S.ts(a[:, :n], raw[:, c, 0:n], cw[:, c, 0:1], ALU.mult)
                        for j in range(1, 5):
                            S.stt(a[:, :n], raw[:, c, j:j + n], cw[:, c, j:j + 1], a[:, :n], ALU.mult, ALU.add)
                        if c >= 4:
                            S.act(dst[:, c, :n], a[:, :n], AF.Silu)
                        else:
                            S.act(a[:, :n], a[:, :n], AF.Silu)
                            S.act(sqb[:, :n], a[:, :n], AF.Square)
                            pb = B3[c % 2]
                            S.matmul(pb[:, :n], C.ones[:], sqb[:, :n])
                            S.act(lnb[:, :n], pb[:, :n], AF.Ln, bias=C.eps[:, 0:1])
                            S.act(rsb[:, :n], lnb[:, :n], AF.Exp, scale=-0.5)
                            S.stt(dst[:, c, :n], a[:, :n], (128.0 ** -0.5) if c < 2 else 1.0, rsb[:, :n], ALU.mult, ALU.mult)

                def unit(hl, dr, gp, qv, kv, vv):
                    col = dr * 2 + hl
                    g = gall[:, gp, col:col + 1]
                    nb = negb[:, gp, col:col + 1]
                    bt = ball[:, gp, col:col + 1]
                    Wn = Wk[hl]
                    bA, b1, b2, b3 = BA[hl], B1[hl], B2[hl], B3[hl]
                    Tri, Xm, Val, SVal = (M["le"], M["gt"], M["ge"], M["gt"]) if dr == 0 else (M["ge"], M["lt"], M["le"], M["lt"])
                    S.ts(Wn.X[:], Xm[:], g, ALU.mult)
                    S.matmul(bA[:, 0:128], Tri[:], Wn.X[:])
                    S.matmul(bA[:, 128:129], Tri[:], g)
                    S.matmul(bA[:, 129:130], Xm[:], g)
                    S.matmul(bA[:, 130:131], halfA[:], g)
                    S.matmul(bA[:, 131:132], halfB[:], g)
                    S.matmul(b1[:, 0:128], kv, kv)
                    S.matmul(b1[:, 128:256], qv, kv)
                    S.transpose(bA[:, 256:384], kv, C.ident[:])
                    S.transpose(bA[:, 384:512], vv, C.ident[:])
                    yield
                    S.act(Wn.Dm[:], bA[:, 0:128], AF.Exp)
                    S.act(Wn.esm[:], bA[:, 128:132], AF.Exp)
                    S.tt(Wn.bg[:], Wn.esm[:, 0:1], bt, ALU.mult)
                    S.act(Wn.kdec[:], bA[:, 256:384], AF.Identity, scale=Wn.esm[:, 1:2])
                    S.act(Wn.vb[:], bA[:, 384:512], AF.Identity, scale=bt)
                    S.act(Wn.kbg[:], bA[:, 256:384], AF.Identity, scale=Wn.bg[:, 0:1])
                    S.tt(Wn.Dv[:], Wn.Dm[:], Val[:], ALU.mult)
                    S.tt(Wn.Ds[:], Wn.Dm[:], SVal[:], ALU.mult)
                    S.stt(Wn.NA[:, 0:128], b1[:, 0:128], nb, Wn.Ds[:], ALU.mult, ALU.mult)
                    S.tt(Wn.NA[:, 128:256], b1[:, 128:256], Wn.Dv[:], ALU.mult)
                    yield
                    S.transpose(b1[:, 256:384], Wn.NA[:, 0:128], C.ident[:])
                    S.transpose(b1[:, 384:512], Wn.NA[:, 128:256], C.ident[:])
                    S.copy(Wn.RA[:], b1[:, 256:512])
                    X = Wn.Xc[0]
                    S.tt(X[:], Wn.RA[:, 0:128], C.ident[:], ALU.add)
                    yield
                    Ncur = Wn.NA[:, 0:128]
                    Rcur = Wn.RA[:, 0:128]
                    for lev in range(5):
                        NR = Wn.NR[lev % 2]
                        S.matmul(b2[:, 0:128], Rcur, Ncur)
                        if lev < 4:
                            S.matmul(b2[:, 128:256], Ncur, Rcur)
                            S.copy(NR[:], b2[:, 0:256])
                        else:
                            S.copy(NR[:, 0:128], b2[:, 0:128])
                        yield
                        S.matmul(b2[:, 256:384], NR[:, 0:128], X[:])
                        Xn = Wn.Xc[(lev + 1) % 2]
                        S.tt(Xn[:], X[:], b2[:, 256:384], ALU.add)
                        X = Xn
                        Ncur = NR[:, 0:128]
                        Rcur = NR[:, 128:256]
                        yield
                    S.matmul(b3[:, 0:128], X[:], Wn.vb[:])
                    S.matmul(b3[:, 128:256], Wn.kbg[:], X[:])
                    S.copy(Wn.uw[:], b3[:, 0:256])
                    yield
                    blocks = [(0, 64), (64, 128)] if dr == 0 else [(64, 128), (0, 64)]
                    Sst = state[hl]
                    for bi, (r0, r1) in enumerate(blocks):
                        reg = b3[:, 256:512] if bi == 0 else b3[:, 0:256]
                        S.matmul(reg[:, 0:128], Wn.uw[:, 128:256], Sst[:])
                        S.matmul(reg[:, 128:256], qv, Sst[:])
                        S.tt(Wn.vnew[r0:r1, :], Wn.uw[r0:r1, 0:128], reg[r0:r1, 0:128], ALU.subtract)
                        S.ts(Wn.oq[r0:r1, :], reg[r0:r1, 128:256], Wn.esm[r0:r1, 0:1], ALU.mult)
                        yield
                        S.matmul(b1[:, 0:128], Wn.kdec[r0:r1, :], Wn.vnew[r0:r1, :])
                        egX = Wn.esm[:, 2:3] if r0 == 0 else Wn.esm[:, 3:4]
                        S.stt(Sst[:], Sst[:], egX, b1[:, 0:128], ALU.mult, ALU.add)
                        yield
                    S.matmul(b1[:, 128:256], Wn.RA[:, 128:256], Wn.vnew[:])
                    S.tt(Wn.o[:], Wn.oq[:], b1[:, 128:256], ALU.add)
                    if dr == 0:
                        S.dma("pool", ofb[gp][hl][:], Wn.o[:])
                    else:
                        S.dma("sp", Wn.of_[:], ofb[gp][hl][:])
                        load_tok("sp", Wn.zt[:], Wn.zt2[:], gp, 256 + hl * P, P)
                        S.tt(Wn.o[:], Wn.o[:], Wn.of_[:], ALU.add)
                        S.act(Wn.t1[:], Wn.o[:], AF.Square, accum_out=Wn.ss[:, 0:1])
                        yield
                        S.act(Wn.ss[:], Wn.ss[:], AF.Ln, scale=1.0 / 128.0, bias=C.eps[:, 0:1])
                        S.act(Wn.ss[:], Wn.ss[:], AF.Exp, scale=-0.5)
                        S.act(Wn.zt[:], Wn.zt[:], AF.Silu)
                        S.stt(Wn.t1[:], Wn.o[:], Wn.ss[:, 0:1], nw[:], ALU.mult, ALU.mult)
                        oo = Wn.oo[gp % 2]
                        S.tt(oo[:], Wn.t1[:], Wn.zt[:], ALU.mult)
                        S.transpose(b1[:, 256:384], oo[:], C.ident[:])
                        oT = Wn.oT[gp % 2]
                        S.copy(oT[:], b1[:, 256:384])
                        s_, off = tokpos(gp)
                        S.dma("pool", dsub(mo_dst(hl, hl + 1, s_, off, P)[0]), oT[:])
                    yield

                for dr in range(2):
                    for h in range(2):
                        S.memset(state[h][:], 0.0)
                    segs = [("ctx", 0, LC), ("lat", LC, L)]
                    tl = []
                    for (seg, base, seglen) in segs:
                        tt_ = [(seg, base, t0, min(512, seglen - t0)) for t0 in range(0, seglen, 512)]
                        if dr == 1:
                            tt_ = tt_[::-1]
                        tl += tt_
                    for ti, (seg, base, t0, n) in enumerate(tl):
                        dst = qkv[ti % 2]
                        prep(seg, t0, n, dst, "sp" if ti % 2 else "pool")
                        prs = list(range(n // P))
                        if dr == 1:
                            prs = prs[::-1]
                        for pi in prs:
                            gp = (base + t0) // P + pi
                            sl = slice(pi * P, (pi + 1) * P)
                            gens = [unit(h, dr, gp, dst[:, 0 + h, sl], dst[:, 2 + h, sl], dst[:, 4 + h, sl]) for h in range(2)]
                            alive = [True, True]
                            while any(alive):
                                for h in range(2):
                                    if alive[h]:
                                        try:
                                            next(gens[h])
                                        except StopIteration:
                                            alive[h] = False

        def ph_R2(W, x1t, xout):
            with S.scope():
                alloc_small(S, C)
                PS = mk_ps()
                mods = S.sbuf("mods", [P, 48, 2], F32)
                ngt = S.sbuf("ngt", [P, 6, KC], F32)
                S.dma("sp", ngt[:], W.ng[:].rearrange("p (m c) -> p m c", c=KC))
                snt = S.sbuf("snt", [P, 4], F32)
                S.dma("sp", snt[:], W.snw[:])
                with S.scope():
                    stg = [S.sbuf("stgm%d" % i, [P, 6 * D], F32) for i in range(KC)]
                    compute_mods(S, C, cv, W.wada2, W.bada2, 6, stg, PS.g[0], mods)
                A2 = S.sbuf("A2", [P, KC, 2], F32)
                G3 = S.sbuf("G3", [P, KC, 2], F32)
                A4 = S.sbuf("A4", [P, KC, 2], F32)
                G5 = S.sbuf("G5", [P, KC, 2], F32)
                S.stt(A2[:], mods[:, 8:16, :], 1.0, bc(ngt[:, 2, :]), ALU.add, ALU.mult)
                S.tt(G3[:], mods[:, 16:24, :], bc(ngt[:, 3, :]), ALU.mult)
                S.stt(A4[:], mods[:, 32:40, :], 1.0, bc(ngt[:, 4, :]), ALU.add, ALU.mult)
                S.stt(G5[:], mods[:, 40:48, :], 0.5, bc(ngt[:, 5, :]), ALU.mult, ALU.mult)
                B2 = mods[:, 0:8, :]
                B4 = mods[:, 24:32, :]
                x2t = [S.sub("x2t%d" % j, X2.t[:, :, s0:s0 + n]) for j, (s0, n, col) in enumerate(tiles)]
                with S.scope():
                    wgb = S.sbuf("wgb", [P, KC, 3 * D], BF16)
                    wbb = S.sbuf("wbb", [P, 12, D], BF16)
                    wob = S.sbuf("wob", [P, KC, D], BF16)
                    with S.scope():
                        stages = [S.sbuf("wst%d" % i, [P, 2048], F32) for i in range(3)]
                        load_w(S, W.wg, wgb, D, 3 * D, stages)
                        load_w(S, W.wb, wbb, 1536, D, stages)
                        load_w(S, W.wo, wob, D, D, stages)
                    xt = S.sbuf("xm", [P, KC, 512], F32)
                    h = S.sbuf("hm", [P, KC, 512], BF16)
                    ost = S.sbuf("ostg", [P, 4, 512], F32)
                    ost2 = S.sbuf("ostg2", [P, 4, 512], F32)
                    ob16 = S.sbuf("ob16", [P, 12, 512], BF16)
                    yacc = S.sbuf("yacc", [P, 512], F32)
                    ybf = S.sbuf("ybf", [P, KC, 512], BF16)
                    yy = S.sbuf("yy", [P, KC, 512], F32)
                    gt = [S.sbuf("gt%d" % i, [P, 512], F32) for i in range(2)]
                    for j, (s0, n, col) in enumerate(tiles):
                        S.dma("sp", xt[:, :, :n], x1t[j][:])
                        norm_mod(S, C, xt, n, A2[:], B2, col, h, PS.ss)
                        for br in range(3):
                            for sc_, tgt in ((0, ost), (1, ost2)):
                                for r in range(2):
                                    if col == 0:
                                        sap = MOLG.ap()[2 * br:2 * br + 2, sc_, r, :, s0:s0 + n]
                                    else:
                                        sap = MOCG.ap()[r, 2 * br:2 * br + 2, sc_, :, 0:n]
                                    S.dma("sp" if r else "pool", tgt[:, 2 * r:2 * r + 2, :n], dsub(sap.rearrange("c p t -> p c t")))
                            blend(ost[:, :, :n], ost2[:, :, :n])
                            if br < 2:
                                S.copy(ob16[:, br * 4:(br + 1) * 4, :n], ost[:, :, :n], e="pool")
                            else:
                                rms_rstd(S, C, lambda c: ost[:, c, :n], n, 4, 512, PS.ss2, C.rstd2[:, :n])
                                for c in range(4):
                                    t = C.tmp[c % 2]
                                    S.tt(t[:, :n], ost[:, c, :n], C.rstd2[:, :n], ALU.mult)
                                    S.act(ob16[:, 8 + c, :n], t[:, :n], AF.Copy, scale=snt[:, c:c + 1])
                        for d in range(KC):
                            for br in range(3):
                                pg = PS.g[br % 2]
                                pu = PS.u[br % 2]
                                cg = br * KC + d
                                for k in range(KC):
                                    S.matmul(pg[:, :n], wgb[:, k, cg * P:(cg + 1) * P], h[:, k, :n], start=(k == 0), stop=(k == KC - 1))
                                for k in range(4):
                                    S.matmul(pu[:, :n], wbb[:, br * 4 + k, d * P:(d + 1) * P], ob16[:, br * 4 + k, :n], start=(k == 0), stop=(k == 3))
                                g = gt[br % 2]
                                S.act(g[:, :n], pg[:, :n], AF.Sigmoid)
                                if br == 0:
                                    S.tt(yacc[:, :n], g[:, :n], pu[:, :n], ALU.mult)
                                else:
                                    t = C.tmp[br % 2]
                                    S.tt(t[:, :n], g[:, :n], pu[:, :n], ALU.mult)
                                    if br == 1:
                                        S.tt(yacc[:, :n], yacc[:, :n], t[:, :n], ALU.add)
                                    else:
                                        S.tt(ybf[:, d, :n], yacc[:, :n], t[:, :n], ALU.add)
                        for d in range(KC):
                            py = PS.y[d % 2]
                            for k in range(KC):
                                S.matmul(py[:, :n], wob[:, k, d * P:(d + 1) * P], ybf[:, k, :n], start=(k == 0), stop=(k == KC - 1))
                            S.copy(yy[:, d, :n], py[:, :n], e="dve")
                        rms_rstd(S, C, lambda c: yy[:, c, :n], n, KC, D, PS.ss2, C.rstd2[:, :n])
                        for c in range(KC):
                            t = C.tmp[c % 2]
                            S.tt(t[:, :n], yy[:, c, :n], C.rstd2[:, :n], ALU.mult)
                            S.stt(xt[:, c, :n], t[:, :n], G3[:, c, col:col + 1], xt[:, c, :n], ALU.mult, ALU.add)
                        S.dma("pool", x2t[j][:], xt[:, :, :n])
                with S.scope():
                    w1b = S.sbuf("w1b", [P, KC, 2 * DFF], BF16)
                    w2b = S.sbuf("w2b", [P, FC, D], BF16)
                    with S.scope():
                        stages = [S.sbuf("wst%d" % i, [P, 2048], F32) for i in range(3)]
                        load_w(S, W.w1b, w1b, D, 2 * DFF, stages)
                        load_w(S, W.w2b, w2b, DFF, D, stages)
                    alloc_ffn_work(S, C)
                    ffn_sweep(S, C, tiles, lambda j: x2t[j][:], lambda j: xout[j][:], w1b, w2b, A4[:], B4, G5[:], PS)

        xin = [S.sub("xin%d" % j, xT.t[:, :, s0:s0 + n]) for j, (s0, n, col) in enumerate(tiles)]
        for i in range(depth):
            W = Wl[i]
            x1t = [S.sub("x1t%d_%d" % (i, j), X1.t[:, :, s0:s0 + n]) for j, (s0, n, col) in enumerate(tiles)]
            dbg = "Z"
            ph_R1(W, xin, x1t)
            if dbg >= "B":
                gather_P()
            if dbg >= "C":
                ph_Mdiff(W)
            if dbg >= "D":
                ph_Mssd(W)
            if dbg >= "E":
                ph_Mgdn(W)
            if dbg >= "F":
                gather_M()
            last = i == depth - 1
            dstT = yT if last else Xs
            xout = [S.sub("xo%d_%d" % (i, j), dstT.t[:, :, s0:s0 + n]) for j, (s0, n, col) in enumerate(tiles)]
            ph_R2(W, x1t, xout)
            xin = xout
        S.barrier()
    return nc


def fm(a):
    T, F = a.shape
    return np.ascontiguousarray(a.T.reshape(F // P, P, T).transpose(1, 0, 2))

def unfm(a):
    p, C, T = a.shape
    return np.ascontiguousarray(a.transpose(2, 1, 0).reshape(T, C * P))

def vec_fm(v):
    return np.ascontiguousarray(v.reshape(-1, P).T)

def r1_cols():
    sw = np.arange(512) ^ 1
    cols = []
    cols += list(range(0, 2048))
    cols += list(range(2064, 2576))
    cols += list(2064 + sw)
    cols += list(range(2576, 3088))
    cols += list(2576 + sw)
    cols += list(range(3088, 3600))
    cols += list(range(3600, 5136))
    cols += list(range(2048, 2064)) + list(range(5136, 5152)) + [0] * 96
    cols = np.array(cols)
    assert len(cols) == 49 * 128
    return cols

def r1_inputs(inp, i, core, NT, NCX, xcur, ctxcur):
    b, s = core // 2, core % 2
    tok = np.concatenate([xcur[b, s * NT:(s + 1) * NT], ctxcur[b, s * NCX:(s + 1) * NCX]], 0)
    cv = np.stack([inp["c"][b], inp["c_ctx"]], -1)
    cv = np.ascontiguousarray(cv.reshape(8, P, 2).transpose(1, 0, 2))
    return {
        "xT": fm(tok),
        "cv": cv,
        "wada": np.ascontiguousarray(inp["w_ada"][i][:, :5 * 1024]),
        "bada": vec_fm(inp["b_ada"][i][:5 * 1024]),
        "ng": np.ascontiguousarray(inp["norm_g"][i].reshape(6, 8, P).transpose(2, 0, 1).reshape(P, 48)),
        "w1": np.ascontiguousarray(inp["w_ffn_in"][i, 0]),
        "w2": np.ascontiguousarray(inp["w_ffn_out"][i, 0]),
        "win": np.ascontiguousarray(inp["w_in"][i][:, r1_cols()]),
    }

def r2_inputs(inp, i, core, x1T, oaT, obT, ocT):
    b = core // 2
    cv = np.stack([inp["c"][b], inp["c_ctx"]], -1)
    cv = np.ascontiguousarray(cv.reshape(8, P, 2).transpose(1, 0, 2))
    return {
        "x1T": x1T, "oaT": oaT, "obT": obT, "ocT": ocT, "cv": cv,
        "wada": np.ascontiguousarray(inp["w_ada"][i][:, 3 * 1024:]),
        "bada": vec_fm(inp["b_ada"][i][3 * 1024:]),
        "ng": np.ascontiguousarray(inp["norm_g"][i].reshape(6, 8, P).transpose(2, 0, 1).reshape(P, 48)),
        "snw": vec_fm(inp["ssd_norm_w"][i]),
        "wg": np.ascontiguousarray(inp["w_in"][i][:, 5152:8224]),
        "wb": np.ascontiguousarray(inp["w_branch"][i].reshape(1536, 1024)),
        "wo": np.ascontiguousarray(inp["w_out"][i]),
        "w1": np.ascontiguousarray(inp["w_ffn_in"][i, 1]),
        "w2": np.ascontiguousarray(inp["w_ffn_out"][i, 1]),
    }

def split_P(PT_cores, NT, NCX, b):
    a0, a1 = PT_cores[2 * b], PT_cores[2 * b + 1]
    lat = np.concatenate([a0[:, :, :NT], a1[:, :, :NT]], 2)
    cx = np.concatenate([a0[:, :, NT:], a1[:, :, NT:]], 2)
    return lat, cx

def mdiff_inputs(inp, i, core, lat, cx):
    b, hh = core // 2, core % 2
    hs = [2 * hh, 2 * hh + 1]
    L = lat.shape[2]; LC = cx.shape[2]
    qT = lat[[16 + h for h in hs]]
    qsT = lat[[20 + h for h in hs]]
    kT = np.concatenate([lat[[24 + h for h in hs]], cx[[24 + h for h in hs]]], 2)
    ksT = lat[[28 + h for h in hs]]
    qcT = cx[[16 + h for h in hs]]
    v = np.concatenate([lat[[32 + h for h in hs]], cx[[32 + h for h in hs]]], 2)
    LK = L + LC
    v = v.transpose(2, 0, 1).reshape(LK // P, P, 256).transpose(1, 0, 2)
    lam_init = 0.8 - 0.6 * np.exp(-0.3 * i)
    return {"qT": np.ascontiguousarray(qT), "qsT": np.ascontiguousarray(qsT), "kT": np.ascontiguousarray(kT),
            "ksT": np.ascontiguousarray(ksT), "qcT": np.ascontiguousarray(qcT), "v": np.ascontiguousarray(v),
            "lam": np.ascontiguousarray(np.broadcast_to(inp["diff_lambda"][i].reshape(1, 256), (P, 256))),
            "nw": np.ascontiguousarray(inp["diff_norm_w"][i].reshape(P, 1)),
            "li": np.full((P, 1), lam_init, np.float32)}

def pad2(a):
    return np.pad(a, ((0, 0), (0, 0), (2, 2)))

def tokmaj(a):
    Cc, p, T = a.shape
    return np.ascontiguousarray(a.transpose(2, 0, 1).reshape(T // P, P, Cc * P).transpose(1, 0, 2))

def mssd_inputs(inp, i, core, lat, cx):
    b, g = core // 2, core % 2
    ch = [40 + 2 * g, 41 + 2 * g, 44 + g, 46 + g]
    wcols = np.concatenate([np.arange(256 * g, 256 * g + 256), 512 + 128 * g + np.arange(128), 768 + 128 * g + np.arange(128)])
    cwv = inp["ssd_conv_w"][i][:, wcols]
    cbv = inp["ssd_conv_b"][i][wcols]
    zc = [36 + 2 * g, 37 + 2 * g]
    z = np.concatenate([tokmaj(cx[zc]), tokmaj(lat[zc])], 1)
    rows = [16 + d * 8 + 4 * g + h for d in range(2) for h in range(4)]
    dtl = lat[48][rows]; dtc = cx[48][rows]
    dt = np.concatenate([dtc, dtl], 1)
    LT = dt.shape[1]
    dt = np.ascontiguousarray(dt.T.reshape(LT // P, P, 8).transpose(1, 0, 2))
    hsel = [4 * g + h for h in range(4)]
    return {"xbcl": np.ascontiguousarray(pad2(lat[ch])), "xbcc": np.ascontiguousarray(pad2(cx[ch])),
            "cw": np.ascontiguousarray(cwv.T.reshape(4, P, 5).transpose(1, 0, 2)),
            "cb": np.ascontiguousarray(cbv.reshape(4, P).T),
            "z": z, "dt": dt,
            "dtb": np.ascontiguousarray(np.broadcast_to(inp["ssd_dt_bias"][i][:, hsel].reshape(1, 8), (P, 8))),
            "alog": np.ascontiguousarray(np.broadcast_to(inp["ssd_a_log"][i][:, hsel].reshape(1, 8), (P, 8))),
            "dskip": np.ascontiguousarray(np.broadcast_to(inp["ssd_d"][i][hsel].reshape(1, 4), (P, 4)))}

def mgdn_inputs(inp, i, core, lat, cx):
    b, hh = core // 2, core % 2
    hs = [2 * hh, 2 * hh + 1]
    ch = [0 + hs[0], 0 + hs[1], 4 + hs[0], 4 + hs[1], 8 + hs[0], 8 + hs[1]]
    wcols = np.concatenate([off + h * 128 + np.arange(128) for off in (0, 512, 1024) for h in hs])
    cwv = inp["gdn_conv_w"][i][:, wcols]
    zc = [12 + hs[0], 12 + hs[1]]
    z = np.concatenate([tokmaj(cx[zc]), tokmaj(lat[zc])], 1)
    def small(rows):
        v = np.concatenate([cx[48][rows], lat[48][rows]], 1)
        LT = v.shape[1]
        return np.ascontiguousarray(v.T.reshape(LT // P, P, 4).transpose(1, 0, 2))
    arows = [d * 4 + h for d in range(2) for h in hs]
    brows = [8 + d * 4 + h for d in range(2) for h in hs]
    return {"qkvl": np.ascontiguousarray(pad2(lat[ch])), "qkvc": np.ascontiguousarray(pad2(cx[ch])),
            "cw": np.ascontiguousarray(cwv.T.reshape(6, P, 5).transpose(1, 0, 2)),
            "z": z, "araw": small(arows), "braw": small(brows),
            "alog": np.ascontiguousarray(np.broadcast_to(inp["gdn_a_log"][i][:, hs].reshape(1, 4), (P, 4))),
            "dtb": np.ascontiguousarray(np.broadcast_to(inp["gdn_dt_bias"][i][:, hs].reshape(1, 4), (P, 4))),
            "nw": np.ascontiguousarray(np.broadcast_to(inp["gdn_norm_w"][i].reshape(1, P), (P, P)))}


def fused_cols():
    sw = np.arange(128) ^ 1
    ar = np.arange(128)
    fmc, tkc = [], []
    for g in (0, 1):
        hs = [2 * g, 2 * g + 1]
        for off in (0, 512, 1024):
            for h in hs:
                fmc += list(off + h * 128 + ar)
        for h in hs:
            fmc += list(2064 + h * 128 + ar)
        for h in hs:
            fmc += list(2064 + h * 128 + sw)
        for h in hs:
            fmc += list(2576 + h * 128 + ar)
        for h in hs:
            fmc += list(2576 + h * 128 + sw)
        fmc += list(4112 + g * 256 + np.arange(256))
        fmc += list(4112 + 512 + g * 128 + ar)
        fmc += list(4112 + 768 + g * 128 + ar)
    for g in (0, 1):
        hs = [2 * g, 2 * g + 1]
        for h in hs:
            tkc += list(3088 + h * 128 + ar)
        for h in hs:
            tkc += list(1536 + h * 128 + ar)
        tkc += list(3600 + g * 256 + np.arange(256))
        tkc += [2048 + d * 4 + h for d in range(2) for h in hs]
        tkc += [2056 + d * 4 + h for d in range(2) for h in hs]
        tkc += [5136 + d * 8 + 4 * g + h for d in range(2) for h in range(4)]
    cols = np.array(fmc + tkc)
    assert len(cols) == 36 * 128 + 1568
    return cols


def fused_inputs(inp, core, NT, NCX, depth=2):
    b, hh = core // 2, core % 2
    s = hh
    tok = np.concatenate([inp["x"][b, s * NT:(s + 1) * NT], inp["ctx"][b, s * NCX:(s + 1) * NCX]], 0)
    cv = np.stack([inp["c"][b], inp["c_ctx"]], -1)
    cv = np.ascontiguousarray(cv.reshape(8, P, 2).transpose(1, 0, 2))
    d = {"xT": fm(tok), "cv": cv, "selv": np.ascontiguousarray(np.broadcast_to(np.array([[1.0 - hh, float(hh)]], np.float32), (P, 2)))}
    cols = fused_cols()
    hs = [2 * hh, 2 * hh + 1]
    g = hh
    for i in range(depth):
        sfx = "_%d" % i
        d["wada1" + sfx] = np.ascontiguousarray(inp["w_ada"][i][:, :5 * 1024])
        d["bada1" + sfx] = vec_fm(inp["b_ada"][i][:5 * 1024])
        d["wada2" + sfx] = np.ascontiguousarray(inp["w_ada"][i][:, 3 * 1024:])
        d["bada2" + sfx] = vec_fm(inp["b_ada"][i][3 * 1024:])
        d["ng" + sfx] = np.ascontiguousarray(inp["norm_g"][i].reshape(6, 8, P).transpose(2, 0, 1).reshape(P, 48))
        d["w1a" + sfx] = np.ascontiguousarray(inp["w_ffn_in"][i, 0])
        d["w2a" + sfx] = np.ascontiguousarray(inp["w_ffn_out"][i, 0])
        d["w1b" + sfx] = np.ascontiguousarray(inp["w_ffn_in"][i, 1])
        d["w2b" + sfx] = np.ascontiguousarray(inp["w_ffn_out"][i, 1])
        d["win" + sfx] = np.ascontiguousarray(inp["w_in"][i][:, cols])
        d["wg" + sfx] = np.ascontiguousarray(inp["w_in"][i][:, 5152:8224])
        d["wb" + sfx] = np.ascontiguousarray(inp["w_branch"][i].reshape(1536, 1024))
        d["wo" + sfx] = np.ascontiguousarray(inp["w_out"][i])
        d["snw" + sfx] = vec_fm(inp["ssd_norm_w"][i])
        lam_init = 0.8 - 0.6 * np.exp(-0.3 * i)
        d["lam" + sfx] = np.ascontiguousarray(np.broadcast_to(inp["diff_lambda"][i].reshape(1, 256), (P, 256)))
        d["dnw" + sfx] = np.ascontiguousarray(inp["diff_norm_w"][i].reshape(P, 1))
        d["li" + sfx] = np.full((P, 1), lam_init, np.float32)
        wcols = np.concatenate([np.arange(256 * g, 256 * g + 256), 512 + 128 * g + np.arange(128), 768 + 128 * g + np.arange(128)])
        d["scw" + sfx] = np.ascontiguousarray(inp["ssd_conv_w"][i][:, wcols].T.reshape(4, P, 5).transpose(1, 0, 2))
        d["scb" + sfx] = np.ascontiguousarray(inp["ssd_conv_b"][i][wcols].reshape(4, P).T)
        hsel = [4 * g + h for h in range(4)]
        d["sdtb" + sfx] = np.ascontiguousarray(np.broadcast_to(inp["ssd_dt_bias"][i][:, hsel].reshape(1, 8), (P, 8)))
        d["salog" + sfx] = np.ascontiguousarray(np.broadcast_to(inp["ssd_a_log"][i][:, hsel].reshape(1, 8), (P, 8)))
        d["sdsk" + sfx] = np.ascontiguousarray(np.broadcast_to(inp["ssd_d"][i][hsel].reshape(1, 4), (P, 4)))
        gcols = np.concatenate([off + h * 128 + np.arange(128) for off in (0, 512, 1024) for h in hs])
        d["gcw" + sfx] = np.ascontiguousarray(inp["gdn_conv_w"][i][:, gcols].T.reshape(6, P, 5).transpose(1, 0, 2))
        d["galog" + sfx] = np.ascontiguousarray(np.broadcast_to(inp["gdn_a_log"][i][:, hs].reshape(1, 4), (P, 4)))
        d["gdtb" + sfx] = np.ascontiguousarray(np.broadcast_to(inp["gdn_dt_bias"][i][:, hs].reshape(1, 4), (P, 4)))
        d["gnw" + sfx] = np.ascontiguousarray(np.broadcast_to(inp["gdn_norm_w"][i].reshape(1, P), (P, P)))
    return d


from concourse.bass_utils import run_bass_kernel_spmd


def kernel(**inp):
    inp = {k: np.ascontiguousarray(np.asarray(v), dtype=np.float32) for k, v in inp.items()}
    B, L, Dm = inp["x"].shape
    LC = inp["ctx"].shape[1]
    NT, NCX = L // 2, LC // 2
    nc = build_fused(L, LC, 2)
    ims = [fused_inputs(inp, c, NT, NCX, 2) for c in range(8)]
    res = run_bass_kernel_spmd(nc, ims, core_ids=list(range(8))).results
    out = np.empty((B, L, Dm), np.float32)
    for c in range(8):
        b, s = c // 2, c % 2
        out[b, s * NT:(s + 1) * NT] = unfm(res[c]["yT"])[:NT]
    return out
```

```python
import numpy as np
from contextlib import ExitStack, contextmanager
import concourse.bass as bass
import concourse.mybir as mybir

F32 = mybir.dt.float32
BF16 = mybir.dt.bfloat16
AF = mybir.ActivationFunctionType
ALU = mybir.AluOpType
AX = mybir.AxisListType


class Buf:
    __slots__ = ("name", "t", "w", "r", "dsem", "dcnt", "space", "dkey")

    def __init__(self, name, t, space="sbuf"):
        self.name = name
        self.space = space
        self.t = t
        self.w = {}
        self.r = {}
        self.dsem = None
        self.dcnt = 0
        self.dkey = None

    def __getitem__(self, idx):
        return View(self, self.t[idx])


class View:
    __slots__ = ("buf", "ap")

    def __init__(self, buf, ap):
        self.buf = buf
        self.ap = ap

    def __getitem__(self, idx):
        return View(self.buf, self.ap[idx])

    def rearrange(self, *a, **k):
        return View(self.buf, self.ap.rearrange(*a, **k))

    def bitcast(self, *a, **k):
        return View(self.buf, self.ap.bitcast(*a, **k))

    def to_broadcast(self, *a, **k):
        return View(self.buf, self.ap.to_broadcast(*a, **k))


class Sched:
    def __init__(self, nc, stack):
        self.nc = nc
        self.stack = stack
        self.eng = {"pe": nc.tensor, "act": nc.scalar, "dve": nc.vector, "pool": nc.gpsimd, "sp": nc.sync}
        self.sem = {}
        self.cnt = {}
        self.seen = {}
        for e in self.eng:
            self.sem[e] = stack.enter_context(nc.semaphore("prog_" + e))
            self.cnt[e] = 0
            self.seen[e] = {}
        self.semobj = {e: self.sem[e] for e in self.eng}
        self.nbuf = 0
        self.ninst = 0
        self.root = stack
        self.dbufs = []
        self.sempool = []
        self.scope_bufs = [[]]
        self.nsem = 0

    def sbuf(self, name, shape, dt):
        self.nbuf += 1
        name = "%s_%d" % (name, self.nbuf)
        t = self.stack.enter_context(self.nc.sbuf_tensor(name, list(shape), dt))
        b = Buf(name, t)
        self.scope_bufs[-1].append(b)
        return b

    def psum(self, name, shape, dt=F32):
        self.nbuf += 1
        name = "%s_%d" % (name, self.nbuf)
        t = self.stack.enter_context(self.nc.psum_tensor(name, list(shape), dt))
        return Buf(name, t, "psum")

    def dram(self, name, shape, dt, kind="Internal"):
        t = self.nc.dram_tensor(name, list(shape), dt, kind=kind)
        return Buf(name, t.ap(), "dram")

    def sub(self, name, ap, space="dram"):
        return Buf(name, ap, space)

    def barrier(self):
        deps = {e: self.cnt[e] for e in self.eng if self.cnt[e] > 0}
        for b in self.dbufs:
            if deps.get(b.dkey, 0) < b.dcnt:
                deps[b.dkey] = b.dcnt
        for e in self.eng:
            self._need(e, dict(deps))

    @contextmanager
    def scope(self):
        old = self.stack
        self.scope_bufs.append([])
        with ExitStack() as st:
            self.stack = st
            yield
            self.barrier()
        self.stack = old
        dead = self.scope_bufs.pop()
        for b in dead:
            if b.dsem is not None:
                self.sempool.append((b.dsem, b.dcnt, b.dkey))
        deadids = set(id(b) for b in dead)
        self.dbufs = [b for b in self.dbufs if id(b) not in deadids] + [b for b in dead if b.dsem is not None][:0]
        self._dead_keep = getattr(self, "_dead_keep", []) + dead

    def _need(self, e, deps):
        seen = self.seen[e]
        for k, v in deps.items():
            if e == "pe" and k == "pe":
                continue
            if seen.get(k, 0) < v:
                seen[k] = v
                self.eng[e].wait_ge(self.semobj[k], v)

    def _collect(self, reads, writes):
        deps = {}
        for v in reads:
            for k, val in v.buf.w.items():
                if deps.get(k, 0) < val:
                    deps[k] = val
        for v in writes:
            for d in (v.buf.w, v.buf.r):
                for k, val in d.items():
                    if deps.get(k, 0) < val:
                        deps[k] = val
        return deps

    def _mark(self, reads, writes, key, val):
        for v in reads:
            b = v.buf
            if b.r.get(key, 0) < val:
                b.r[key] = val
        for v in writes:
            b = v.buf
            b.w = {key: val}
            b.r = {}

    def op(self, e, fn, reads, writes):
        self._need(e, self._collect(reads, writes))
        ins = fn()
        self.cnt[e] += 1
        ins.then_inc(self.sem[e], 1)
        self._mark(reads, writes, e, self.cnt[e])
        self.ninst += 1
        return ins

    def dma(self, e, out, in_, sbuf_side=None, **kw):
        if sbuf_side is None:
            sbuf_side = out.buf if out.buf.space != "dram" else in_.buf
        b = sbuf_side
        if b.dsem is None:
            if self.sempool:
                b.dsem, b.dcnt, b.dkey = self.sempool.pop()
            else:
                self.nsem += 1
                b.dsem = self.root.enter_context(self.nc.semaphore("d_%d" % self.nsem))
                b.dkey = "d%d" % self.nsem
                self.semobj[b.dkey] = b.dsem
            self.dbufs.append(b)
        self._need(e, self._collect([in_], [out]))
        ins = self.eng[e].dma_start(out=out.ap, in_=in_.ap, **kw)
        b.dcnt += 16
        ins.then_inc(b.dsem, 16)
        self._mark([in_], [out], b.dkey, b.dcnt)
        self.ninst += 1
        return ins

    def wait_all(self, e, bufs):
        deps = {}
        for b in bufs:
            for d in (b.w, b.r):
                for k, val in d.items():
                    if deps.get(k, 0) < val:
                        deps[k] = val
        self._need(e, deps)

    def matmul(self, out, lhsT, rhs, start=True, stop=True, acc_reads=True):
        rd = [lhsT, rhs]
        return self.op("pe", lambda: self.nc.tensor.matmul(out.ap, lhsT.ap, rhs.ap, start=start, stop=stop),
                       rd, [out])

    def transpose(self, out, in_, ident):
        return self.op("pe", lambda: self.nc.tensor.transpose(out.ap, in_.ap, ident.ap), [in_, ident], [out])

    def act(self, out, in_, func, bias=None, scale=None, accum_out=None, e="act"):
        rd = [in_]
        kw = {}
        if bias is not None:
            if isinstance(bias, View):
                rd.append(bias)
                kw["bias"] = bias.ap
            else:
                kw["bias"] = bias
        if scale is not None:
            if isinstance(scale, View):
                rd.append(scale)
                kw["scale"] = scale.ap
            else:
                kw["scale"] = scale
        wr = [out]
        if accum_out is not None:
            wr.append(accum_out)
            kw["accum_out"] = accum_out.ap
        return self.op("act", lambda: self.nc.scalar.activation(out.ap, in_.ap, func, **kw), rd, wr)

    def _ve(self, e):
        return self.nc.vector if e == "dve" else self.nc.gpsimd

    def copy(self, out, in_, e="dve"):
        if e == "act":
            return self.op("act", lambda: self.nc.scalar.copy(out.ap, in_.ap), [in_], [out])
        return self.op(e, lambda: self._ve(e).tensor_copy(out.ap, in_.ap), [in_], [out])

    def tt(self, out, a, b, op, e="dve"):
        return self.op(e, lambda: self._ve(e).tensor_tensor(out.ap, a.ap, b.ap, op), [a, b], [out])

    def ts(self, out, a, s1, op0, s2=None, op1=None, accum_out=None, e="dve"):
        rd = [a]
        s1v = s1.ap if isinstance(s1, View) else s1
        s2v = s2.ap if isinstance(s2, View) else s2
        if isinstance(s1, View):
            rd.append(s1)
        if isinstance(s2, View):
            rd.append(s2)
        wr = [out]
        kw = {}
        if op1 is not None:
            kw["op1"] = op1
        if accum_out is not None:
            kw["accum_out"] = accum_out.ap
            wr.append(accum_out)
        if s2 is None and op1 is None and accum_out is None:
            return self.op(e, lambda: self._ve(e).tensor_single_scalar(out.ap, a.ap, s1v, op0), rd, wr)
        return self.op(e, lambda: self._ve(e).tensor_scalar(out.ap, a.ap, s1v, s2v, op0, **kw), rd, wr)

    def stt(self, out, a, s, b, op0, op1, e="dve"):
        rd = [a, b]
        sv = s.ap if isinstance(s, View) else s
        if isinstance(s, View):
            rd.append(s)
        return self.op(e, lambda: self._ve(e).scalar_tensor_tensor(out.ap, a.ap, sv, b.ap, op0, op1), rd, [out])

    def reduce(self, out, in_, op, axis=AX.X, e="dve"):
        return self.op(e, lambda: self._ve(e).tensor_reduce(out.ap, in_.ap, axis, op), [in_], [out])

    def memset(self, out, val, e="dve"):
        return self.op(e, lambda: self._ve(e).memset(out.ap, val), [], [out])

    def recip(self, out, in_):
        return self.op("dve", lambda: self.nc.vector.reciprocal(out.ap, in_.ap), [in_], [out])


P = 128
D = 1024
KC = 8
DFF = 2816
FC = 22
EPS = 1e-6


class NS:
    pass


def mk_consts(S, nc):
    C = NS()
    C.ones = S.sbuf("ones", [P, P], F32)
    S.memset(C.ones[:], 1.0)
    C.eps = S.sbuf("epsc", [P, 1], F32)
    S.memset(C.eps[:], EPS)
    C.ident = S.sbuf("ident", [P, P], F32)
    S.memset(C.ident[:], 1.0, e="pool")
    S.op("pool", lambda: nc.gpsimd.affine_select(C.ident.t[:], C.ident.t[:], [[-1, P]], ALU.is_equal, 0.0,
                                                 base=0, channel_multiplier=1), [C.ident[:]], [C.ident[:]])
    return C


def load_w(S, wd, dst, K, N, stages, blk=2048, col0=0):
    engs = ["pool", "dve", "act"]
    i = 0
    for k in range(K // P):
        for c0 in range(0, N, blk):
            w = min(blk, N - c0)
            st = stages[i % len(stages)]
            S.dma("sp", st[:, :w], wd[k * P:(k + 1) * P, col0 + c0:col0 + c0 + w])
            S.copy(dst[:, k, c0:c0 + w], st[:, :w], e=engs[i % 3])
            i += 1


def rms_rstd(S, C, src, n, nch, dim, ps, out):
    for c in range(nch):
        sq = C.sq[c % 2]
        S.act(sq[:, :n], src(c), AF.Square)
        S.matmul(ps[:, :n], C.ones[:], sq[:, :n], start=(c == 0), stop=(c == nch - 1))
    S.act(C.lnt[:, :n], ps[:, :n], AF.Ln, scale=1.0 / dim, bias=C.eps[:, 0:1])
    S.act(out, C.lnt[:, :n], AF.Exp, scale=-0.5)


def norm_mod(S, C, xt, n, A, B, col, h, ps):
    rms_rstd(S, C, lambda c: xt[:, c, :n], n, KC, D, ps, C.rstd[:, :n])
    for c in range(KC):
        t = C.tmp[c % 2]
        S.tt(t[:, :n], xt[:, c, :n], C.rstd[:, :n], ALU.mult)
        S.act(h[:, c, :n], t[:, :n], AF.Identity, scale=A[:, c, col:col + 1], bias=B[:, c, col:col + 1])


def compute_mods(S, C, cv, wada, bada, nmod, stg, psm, mods):
    scv = S.sbuf("scv", [P, KC, 2], F32)
    cvt = S.sbuf("cvt", [P, KC, 2], F32)
    S.dma("sp", cvt[:], cv[:])
    S.act(scv[:], cvt[:], AF.Silu)
    nn = nmod * KC
    for k in range(KC):
        S.dma("sp" if k % 2 else "pool", stg[k][:, :nn * P], wada[k * P:(k + 1) * P, :])
    for j in range(nn):
        for k in range(KC):
            S.matmul(psm[:, 2 * j:2 * j + 2], stg[k][:, j * P:(j + 1) * P], scv[:, k, :], start=(k == 0), stop=(k == KC - 1))
    bt = S.sbuf("badat", [P, nn], F32)
    S.dma("sp", bt[:], bada[:])
    S.tt(mods[:], psm[:, 0:2 * nn].rearrange("p (j t) -> p j t", t=2),
         View(bt, bt.t[:].rearrange("p (j o) -> p j o", o=1).to_broadcast([P, nn, 2])), ALU.add)


def ffn_sweep(S, C, tiles, x_in, x_out, w1b, w2b, A, B, G, PS):
    for j, (s0, n, col) in enumerate(tiles):
        xt = C.xt[j % 2]
        S.dma("sp", xt[:, :, :n], x_in(j))
        norm_mod(S, C, xt, n, A, B, col, C.h, PS.ss)
        for f in range(FC):
            pg = PS.g[f % 2]
            pu = PS.u[f % 2]
            for k in range(KC):
                S.matmul(pg[:, :n], w1b[:, k, f * P:(f + 1) * P], C.h[:, k, :n], start=(k == 0), stop=(k == KC - 1))
            for k in range(KC):
                S.matmul(pu[:, :n], w1b[:, k, DFF + f * P:DFF + (f + 1) * P], C.h[:, k, :n], start=(k == 0), stop=(k == KC - 1))
            sg = C.sg[f % 2]
            S.act(sg[:, :n], pg[:, :n], AF.Silu)
            S.tt(C.aT[:, f, :n], sg[:, :n], pu[:, :n], ALU.mult)
        for d in range(KC):
            py = PS.y[d % 2]
            for f in range(FC):
                S.matmul(py[:, :n], w2b[:, f, d * P:(d + 1) * P], C.aT[:, f, :n], start=(f == 0), stop=(f == FC - 1))
            S.copy(C.y[:, d, :n], py[:, :n], e="dve")
        rms_rstd(S, C, lambda c: C.y[:, c, :n], n, KC, D, PS.ss2, C.rstd2[:, :n])
        for c in range(KC):
            t = C.tmp[c % 2]
            S.tt(t[:, :n], C.y[:, c, :n], C.rstd2[:, :n], ALU.mult)
            S.stt(xt[:, c, :n], t[:, :n], G[:, c, col:col + 1], xt[:, c, :n], ALU.mult, ALU.add)
        S.dma("pool", x_out(j), xt[:, :, :n])


def alloc_ffn_work(S, C):
    C.xt = [S.sbuf("xt0", [P, KC, 512], F32)] * 2
    C.h = S.sbuf("h", [P, KC, 512], BF16)
    C.aT = S.sbuf("aT", [P, FC, 512], BF16)
    C.y = S.sbuf("y", [P, KC, 512], F32)
    C.sg = C.tmp


def alloc_small(S, C):
    C.sq = [S.sbuf("sq%d" % i, [P, 512], F32) for i in range(2)]
    C.tmp = [S.sbuf("tmp%d" % i, [P, 512], F32) for i in range(2)]
    C.lnt = C.sq[0]
    C.rstd = S.sbuf("rstd", [P, 512], F32)
    C.rstd2 = C.rstd


def mk_tiles(NT, NCX):
    tiles = [(j * 512, 512, 0) for j in range(NT // 512)]
    if NCX:
        tiles.append((NT, NCX, 1))
    return tiles


DEBUG = False
NPC = 49


def build_R1(NT, NCX):
    TT = NT + NCX
    nc = bass.Bass("TRN2", target_bir_lowering=False)
    with ExitStack() as st:
        S = Sched(nc, st)
        xT = S.dram("xT", [P, KC, TT], F32, kind="ExternalInput")
        cv = S.dram("cv", [P, KC, 2], F32, kind="ExternalInput")
        wada = S.dram("wada", [D, 5 * D], F32, kind="ExternalInput")
        bada = S.dram("bada", [P, 40], F32, kind="ExternalInput")
        ng = S.dram("ng", [P, 48], F32, kind="ExternalInput")
        w1 = S.dram("w1", [D, 2 * DFF], F32, kind="ExternalInput")
        w2 = S.dram("w2", [DFF, D], F32, kind="ExternalInput")
        win = S.dram("win", [D, NPC * P], F32, kind="ExternalInput")
        x1T = S.dram("x1T", [P, KC, TT], F32, kind="ExternalOutput")
        PT = S.dram("PT", [NPC, P, TT], F32, kind="ExternalOutput")
        tiles = mk_tiles(NT, NCX)
        C = mk_consts(S, nc)
        alloc_small(S, C)
        PS = NS()
        PS.ss = S.psum("ps_ss", [P, 512])
        PS.ss2 = S.psum("ps_ss2", [P, 512])
        PS.g = [S.psum("ps_g%d" % i, [P, 512]) for i in range(2)]
        PS.u = [S.psum("ps_u%d" % i, [P, 512]) for i in range(2)]
        PS.y = [S.psum("ps_y%d" % i, [P, 512]) for i in range(2)]
        mods = S.sbuf("mods", [P, 40, 2], F32)
        ngt = S.sbuf("ngt", [P, 6, KC], F32)
        S.dma("sp", ngt[:], ng[:].rearrange("p (m c) -> p m c", c=KC))
        A1 = S.sbuf("A1", [P, KC, 2], F32)
        G1 = S.sbuf("G1", [P, KC, 2], F32)
        A2 = S.sbuf("A2", [P, KC, 2], F32)
        with S.scope():
            stg = [S.sbuf("stgm%d" % i, [P, 5 * D], F32) for i in range(KC)]
            compute_mods(S, C, cv, wada, bada, 5, stg, PS.g[0], mods)

        def bc(v):
            return View(v.buf, v.ap.rearrange("p (c o) -> p c o", o=1).to_broadcast([P, KC, 2]))
        S.stt(A1[:], mods[:, 8:16, :], 1.0, bc(ngt[:, 0, :]), ALU.add, ALU.mult)
        S.stt(G1[:], mods[:, 16:24, :], 0.5, bc(ngt[:, 1, :]), ALU.mult, ALU.mult)
        S.stt(A2[:], mods[:, 32:40, :], 1.0, bc(ngt[:, 2, :]), ALU.add, ALU.mult)
        B1 = mods[:, 0:8, :]
        B2 = mods[:, 24:32, :]
        if DEBUG:
            dbg = S.dram("dbg_mods", [P, 80], F32, kind="ExternalOutput")
            S.dma("sp", dbg[:], mods[:].rearrange("p j t -> p (j t)"))
        x1tiles = [S.sub("x1t%d" % j, x1T.t[:, :, s0:s0 + n]) for j, (s0, n, col) in enumerate(tiles)]
        with S.scope():
            w1b = S.sbuf("w1b", [P, KC, 2 * DFF], BF16)
            w2b = S.sbuf("w2b", [P, FC, D], BF16)
            with S.scope():
                stages = [S.sbuf("wst%d" % i, [P, 2048], F32) for i in range(3)]
                load_w(S, w1, w1b, D, 2 * DFF, stages)
                load_w(S, w2, w2b, DFF, D, stages)
            alloc_ffn_work(S, C)
            ffn_sweep(S, C, tiles, lambda j: xT[:, :, tiles[j][0]:tiles[j][0] + tiles[j][1]],
                      lambda j: x1tiles[j][:], w1b, w2b, A1[:], B1, G1[:], PS)
        with S.scope():
            winb = S.sbuf("winb", [P, KC, NPC * P], BF16)
            with S.scope():
                stages = [S.sbuf("wst%d" % i, [P, 2048], F32) for i in range(3)]
                load_w(S, win, winb, D, NPC * P, stages)
            xt2 = [S.sbuf("xq%d" % i, [P, KC, 512], F32) for i in range(2)]
            h = S.sbuf("h2", [P, KC, 512], BF16)
            ost = [S.sbuf("ost%d" % i, [P, 4, 512], F32) for i in range(3)]
            pps = PS.g + PS.u + PS.y
            gi = 0
            for j, (s0, n, col) in enumerate(tiles):
                xt = xt2[j % 2]
                S.dma("sp", xt[:, :, :n], x1tiles[j][:])
                norm_mod(S, C, xt, n, A2[:], B2, col, h, PS.ss)
                for c0 in range(0, NPC, 4):
                    nn = min(4, NPC - c0)
                    o = ost[gi % 3]
                    gi += 1
                    for cc in range(nn):
                        pp = pps[(c0 + cc) % 6]
                        for k in range(KC):
                            S.matmul(pp[:, :n], winb[:, k, (c0 + cc) * P:(c0 + cc + 1) * P], h[:, k, :n],
                                     start=(k == 0), stop=(k == KC - 1))
                        S.copy(o[:, cc, :n], pp[:, :n], e=("act" if cc % 2 else "dve"))
                    S.dma("pool", S.sub("pt", PT.t[c0:c0 + nn, :, s0:s0 + n].rearrange("c p t -> p c t"))[:], o[:, :nn, :n])
            S.wait_all("sp", ost + xt2)
        S.barrier()
    return nc


def build_R2(NT, NCX):
    TT = NT + NCX
    nc = bass.Bass("TRN2", target_bir_lowering=False)
    with ExitStack() as st:
        S = Sched(nc, st)
        x1T = S.dram("x1T", [P, KC, TT], F32, kind="ExternalInput")
        oin = [S.dram(nm, [P, 4, TT], F32, kind="ExternalInput") for nm in ("oaT", "obT", "ocT")]
        cv = S.dram("cv", [P, KC, 2], F32, kind="ExternalInput")
        wada = S.dram("wada", [D, 6 * D], F32, kind="ExternalInput")
        bada = S.dram("bada", [P, 48], F32, kind="ExternalInput")
        ng = S.dram("ng", [P, 48], F32, kind="ExternalInput")
        snw = S.dram("snw", [P, 4], F32, kind="ExternalInput")
        wg = S.dram("wg", [D, 3 * D], F32, kind="ExternalInput")
        wb = S.dram("wb", [1536, D], F32, kind="ExternalInput")
        wo = S.dram("wo", [D, D], F32, kind="ExternalInput")
        w1 = S.dram("w1", [D, 2 * DFF], F32, kind="ExternalInput")
        w2 = S.dram("w2", [DFF, D], F32, kind="ExternalInput")
        x3T = S.dram("x3T", [P, KC, TT], F32, kind="ExternalOutput")
        x2T = S.dram("x2T", [P, KC, TT], F32, kind="Internal")
        tiles = mk_tiles(NT, NCX)
        C = mk_consts(S, nc)
        alloc_small(S, C)
        PS = NS()
        PS.ss = S.psum("ps_ss", [P, 512])
        PS.ss2 = S.psum("ps_ss2", [P, 512])
        PS.g = [S.psum("ps_g%d" % i, [P, 512]) for i in range(2)]
        PS.u = [S.psum("ps_u%d" % i, [P, 512]) for i in range(2)]
        PS.y = [S.psum("ps_y%d" % i, [P, 512]) for i in range(2)]
        mods = S.sbuf("mods", [P, 48, 2], F32)
        ngt = S.sbuf("ngt", [P, 6, KC], F32)
        S.dma("sp", ngt[:], ng[:].rearrange("p (m c) -> p m c", c=KC))
        snt = S.sbuf("snt", [P, 4], F32)
        S.dma("sp", snt[:], snw[:])
        with S.scope():
            stg = [S.sbuf("stgm%d" % i, [P, 6 * D], F32) for i in range(KC)]
            compute_mods(S, C, cv, wada, bada, 6, stg, PS.g[0], mods)

        def bc(v):
            return View(v.buf, v.ap.rearrange("p (c o) -> p c o", o=1).to_broadcast([P, KC, 2]))
        A2 = S.sbuf("A2", [P, KC, 2], F32)
        G3 = S.sbuf("G3", [P, KC, 2], F32)
        A4 = S.sbuf("A4", [P, KC, 2], F32)
        G5 = S.sbuf("G5", [P, KC, 2], F32)
        S.stt(A2[:], mods[:, 8:16, :], 1.0, bc(ngt[:, 2, :]), ALU.add, ALU.mult)
        S.tt(G3[:], mods[:, 16:24, :], bc(ngt[:, 3, :]), ALU.mult)
        S.stt(A4[:], mods[:, 32:40, :], 1.0, bc(ngt[:, 4, :]), ALU.add, ALU.mult)
        S.stt(G5[:], mods[:, 40:48, :], 0.5, bc(ngt[:, 5, :]), ALU.mult, ALU.mult)
        B2 = mods[:, 0:8, :]
        B4 = mods[:, 24:32, :]
        x2tiles = [S.sub("x2t%d" % j, x2T.t[:, :, s0:s0 + n]) for j, (s0, n, col) in enumerate(tiles)]
        with S.scope():
            wgb = S.sbuf("wgb", [P, KC, 3 * D], BF16)
            wbb = S.sbuf("wbb", [P, 12, D], BF16)
            wob = S.sbuf("wob", [P, KC, D], BF16)
            with S.scope():
                stages = [S.sbuf("wst%d" % i, [P, 2048], F32) for i in range(3)]
                load_w(S, wg, wgb, D, 3 * D, stages)
                load_w(S, wb, wbb, 1536, D, stages)
                load_w(S, wo, wob, D, D, stages)
            xt = S.sbuf("xm", [P, KC, 512], F32)
            h = S.sbuf("hm", [P, KC, 512], BF16)
            ost = S.sbuf("ostg", [P, 4, 512], F32)
            ob16 = S.sbuf("ob16", [P, 12, 512], BF16)
            yacc = S.sbuf("yacc", [P, 512], F32)
            ybf = S.sbuf("ybf", [P, KC, 512], BF16)
            yy = S.sbuf("yy", [P, KC, 512], F32)
            gt = [S.sbuf("gt%d" % i, [P, 512], F32) for i in range(2)]
            for j, (s0, n, col) in enumerate(tiles):
                S.dma("sp", xt[:, :, :n], x1T[:, :, s0:s0 + n])
                norm_mod(S, C, xt, n, A2[:], B2, col, h, PS.ss)
                for br in range(3):
                    S.dma("sp", ost[:, :, :n], oin[br][:, :, s0:s0 + n])
                    if br < 2:
                        S.copy(ob16[:, br * 4:(br + 1) * 4, :n], ost[:, :, :n], e="pool")
                    else:
                        rms_rstd(S, C, lambda c: ost[:, c, :n], n, 4, 512, PS.ss2, C.rstd2[:, :n])
                        for c in range(4):
                            t = C.tmp[c % 2]
                            S.tt(t[:, :n], ost[:, c, :n], C.rstd2[:, :n], ALU.mult)
                            S.act(ob16[:, 8 + c, :n], t[:, :n], AF.Copy, scale=snt[:, c:c + 1])
                for d in range(KC):
                    for br in range(3):
                        pg = PS.g[br % 2]
                        pu = PS.u[br % 2]
                        cg = br * KC + d
                        for k in range(KC):
                            S.matmul(pg[:, :n], wgb[:, k, cg * P:(cg + 1) * P], h[:, k, :n], start=(k == 0), stop=(k == KC - 1))
                        for k in range(4):
                            S.matmul(pu[:, :n], wbb[:, br * 4 + k, d * P:(d + 1) * P], ob16[:, br * 4 + k, :n], start=(k == 0), stop=(k == 3))
                        g = gt[br % 2]
                        S.act(g[:, :n], pg[:, :n], AF.Sigmoid)
                        if br == 0:
                            S.tt(yacc[:, :n], g[:, :n], pu[:, :n], ALU.mult)
                        else:
                            t = C.tmp[br % 2]
                            S.tt(t[:, :n], g[:, :n], pu[:, :n], ALU.mult)
                            if br == 1:
                                S.tt(yacc[:, :n], yacc[:, :n], t[:, :n], ALU.add)
                            else:
                                S.tt(ybf[:, d, :n], yacc[:, :n], t[:, :n], ALU.add)
                for d in range(KC):
                    py = PS.y[d % 2]
                    for k in range(KC):
                        S.matmul(py[:, :n], wob[:, k, d * P:(d + 1) * P], ybf[:, k, :n], start=(k == 0), stop=(k == KC - 1))
                    S.copy(yy[:, d, :n], py[:, :n], e="dve")
                rms_rstd(S, C, lambda c: yy[:, c, :n], n, KC, D, PS.ss2, C.rstd2[:, :n])
                for c in range(KC):
                    t = C.tmp[c % 2]
                    S.tt(t[:, :n], yy[:, c, :n], C.rstd2[:, :n], ALU.mult)
                    S.stt(xt[:, c, :n], t[:, :n], G3[:, c, col:col + 1], xt[:, c, :n], ALU.mult, ALU.add)
                S.dma("pool", x2tiles[j][:], xt[:, :, :n])
        with S.scope():
            w1b = S.sbuf("w1b", [P, KC, 2 * DFF], BF16)
            w2b = S.sbuf("w2b", [P, FC, D], BF16)
            with S.scope():
                stages = [S.sbuf("wst%d" % i, [P, 2048], F32) for i in range(3)]
                load_w(S, w1, w1b, D, 2 * DFF, stages)
                load_w(S, w2, w2b, DFF, D, stages)
            alloc_ffn_work(S, C)
            ffn_sweep(S, C, tiles, lambda j: x2tiles[j][:],
                      lambda j: S.sub("x3", x3T.t[:, :, tiles[j][0]:tiles[j][0] + tiles[j][1]])[:], w1b, w2b, A4[:], B4, G5[:], PS)
        S.barrier()
    return nc


I32 = mybir.dt.int32
import math


def rope_tables(S, nc, C, L, cosb, sinb):
    GW = 64
    rows = L // GW
    TWO_PI = 2 * math.pi
    with S.scope():
        ti = S.sbuf("ti", [P, P], I32)
        tf = S.sbuf("tf", [P, P], F32)

        def ppc(name, pattern):
            o = S.sbuf(name, [P, 1], F32)
            S.op("pool", lambda: nc.gpsimd.iota(ti.t[:], pattern, base=0, channel_multiplier=0), [], [ti[:]])
            S.copy(tf[:], ti[:])
            S.tt(tf[:], tf[:], C.ident[:], ALU.mult)
            S.reduce(o[:], tf[:], ALU.add)
            return o
        i16 = ppc("i16", [[0, 2], [0, 2], [1, 16], [0, 2]])
        sel = ppc("sel", [[0, 2], [1, 2], [0, 16], [0, 2]])
        dd = ppc("dd", [[0, 2], [0, 2], [0, 16], [1, 2]])
        sgn = S.sbuf("sgn", [P, 1], F32)
        inv = S.sbuf("inv", [P, 1], F32)
        S.ts(sgn[:], dd[:], 2.0, ALU.mult, -1.0, ALU.add)
        S.act(inv[:], i16[:], AF.Exp, scale=-math.log(10000.0) / 16.0)
        S.ts(inv[:], inv[:], 1.0 / TWO_PI, ALU.mult)
        CH = 1024
        with S.scope():
            ri = S.sbuf("ri", [P, CH], I32)
            ci = S.sbuf("ci", [P, CH], I32)
            rf = S.sbuf("rf", [P, CH], F32)
            cf = S.sbuf("cf", [P, CH], F32)
            xt = S.sbuf("xtn", [P, CH], F32)
            ni = S.sbuf("ni", [P, CH], I32)
            nf = S.sbuf("nf", [P, CH], F32)
            for c0 in range(0, L, CH):
                w = min(CH, L - c0)
                S.op("pool", lambda c0=c0, w=w: nc.gpsimd.iota(ri.t[:, :w], [[1, w // GW], [0, GW]], base=c0 // GW, channel_multiplier=0), [], [ri[:]])
                S.op("pool", lambda w=w: nc.gpsimd.iota(ci.t[:, :w], [[0, w // GW], [1, GW]], base=0, channel_multiplier=0), [], [ci[:]])
                S.copy(rf[:, :w], ri[:, :w])
                S.copy(cf[:, :w], ci[:, :w])
                S.tt(cf[:, :w], cf[:, :w], rf[:, :w], ALU.subtract)
                S.stt(xt[:, :w], cf[:, :w], sel[:, 0:1], rf[:, :w], ALU.mult, ALU.add)
                S.ts(xt[:, :w], xt[:, :w], inv[:, 0:1], ALU.mult)
                for (dst, off) in ((sinb, 0.0), (cosb, 0.25)):
                    if off:
                        S.ts(xt[:, :w], xt[:, :w], off, ALU.add)
                    S.copy(ni[:, :w], xt[:, :w])
                    S.copy(nf[:, :w], ni[:, :w])
                    S.tt(nf[:, :w], xt[:, :w], nf[:, :w], ALU.subtract)
                    S.act(dst[:, c0:c0 + w], nf[:, :w], AF.Sin, scale=TWO_PI * (1 - 1e-6))
                S.ts(sinb[:, c0:c0 + w], sinb[:, c0:c0 + w], sgn[:, 0:1], ALU.mult)


def build_Mdiff(L, LC):
    LK = L + LC
    NKC = LK // P
    nc = bass.Bass("TRN2", target_bir_lowering=False)
    with ExitStack() as st:
        S = Sched(nc, st)
        qT = S.dram("qT", [2, P, L], F32, kind="ExternalInput")
        qsT = S.dram("qsT", [2, P, L], F32, kind="ExternalInput")
        kT = S.dram("kT", [2, P, LK], F32, kind="ExternalInput")
        ksT = S.dram("ksT", [2, P, L], F32, kind="ExternalInput")
        qcT = S.dram("qcT", [2, P, LC], F32, kind="ExternalInput")
        vd = S.dram("v", [P, NKC, 256], F32, kind="ExternalInput")
        lamd = S.dram("lam", [P, 256], F32, kind="ExternalInput")
        nwd = S.dram("nw", [P, 1], F32, kind="ExternalInput")
        lid = S.dram("li", [P, 1], F32, kind="ExternalInput")
        obT = S.dram("obT", [2, P, L], F32, kind="ExternalOutput")
        obcT = S.dram("obcT", [2, P, LC], F32, kind="ExternalOutput")
        C = mk_consts(S, nc)
        C.sq = [S.sbuf("sq%d" % i, [P, 512], F32) for i in range(2)]
        C.lnt = C.sq[0]
        onesb = S.sbuf("onesb", [P, P], BF16)
        S.memset(onesb[:], 1.0)
        Q = [S.sbuf("Q%d" % h, [P, L], BF16) for h in range(2)]
        QC = [S.sbuf("QC%d" % h, [P, LC], BF16) for h in range(2)]
        K = [S.sbuf("K%d" % h, [P, LK], BF16) for h in range(2)]
        V = S.sbuf("V", [P, NKC, 256], BF16)
        lam = S.sbuf("lamt", [P, 4, 64], F32)
        S.dma("sp", lam[:], lamd[:].rearrange("p (a b) -> p a b", b=64))
        nw = S.sbuf("nwt", [P, 1], F32)
        li = S.sbuf("lit", [P, 1], F32)
        S.dma("sp", nw[:], nwd[:])
        S.dma("sp", li[:], lid[:])
        pr = S.sbuf("pr", [P, 2, 64], F32)
        s12 = S.sbuf("s12", [P, 2], F32)
        S.tt(pr[:, 0, :], lam[:, 0, :], lam[:, 1, :], ALU.mult)
        S.tt(pr[:, 1, :], lam[:, 2, :], lam[:, 3, :], ALU.mult)
        S.reduce(s12[:], pr[:], ALU.add)
        e12 = S.sbuf("e12", [P, 2], F32)
        S.act(e12[:], s12[:], AF.Exp)
        neglam = S.sbuf("neglam", [P, 1], F32)
        S.tt(neglam[:], e12[:, 1:2], e12[:, 0:1], ALU.subtract)
        S.tt(neglam[:], neglam[:], li[:], ALU.subtract)
        sc2 = S.sbuf("sc2", [P, 1], F32)
        S.ts(sc2[:], li[:], -1.0, ALU.mult, 1.0, ALU.add)
        S.tt(sc2[:], sc2[:], nw[:], ALU.mult)
        with S.scope():
            cosb = S.sbuf("cosb", [P, L], F32)
            sinb = S.sbuf("sinb", [P, L], F32)
            rope_tables(S, nc, C, L, cosb, sinb)
            with S.scope():
                a = [S.sbuf("la%d" % i, [P, 512], F32) for i in range(2)]
                b = [S.sbuf("lb%d" % i, [P, 512], F32) for i in range(2)]
                vst = [S.sbuf("vst%d" % i, [P, 4, 256], F32) for i in range(2)]
                i = 0
                for h in range(2):
                    for (src, ssw, dst) in ((qT, qsT, Q[h]), (kT, ksT, K[h])):
                        for c0 in range(0, L, 512):
                            ta, tb = a[i % 2], b[i % 2]
                            i += 1
                            S.dma("sp", ta[:], src[h, :, c0:c0 + 512])
                            S.dma("pool", tb[:], ssw[h, :, c0:c0 + 512])
                            S.tt(ta[:], ta[:], cosb[:, c0:c0 + 512], ALU.mult)
                            S.tt(tb[:], tb[:], sinb[:, c0:c0 + 512], ALU.mult, e="pool")
                            S.tt(dst[:, c0:c0 + 512], ta[:], tb[:], ALU.add)
                    ta = a[i % 2]
                    i += 1
                    S.dma("sp", ta[:, :LC], kT[h, :, L:LK])
                    S.copy(K[h][:, L:LK], ta[:, :LC])
                    ta = a[i % 2]
                    i += 1
                    S.dma("sp", ta[:, :LC], qcT[h, :, :])
                    S.copy(QC[h][:], ta[:, :LC])
                for c0 in range(0, NKC, 4):
                    w = min(4, NKC - c0)
                    t = vst[(c0 // 4) % 2]
                    S.dma("sp", t[:, :w, :], vd[:, c0:c0 + w, :])
                    S.copy(V[:, c0:c0 + w, :], t[:, :w, :], e="pool")
        ps_s = [[S.psum("ps_s%d%d" % (j, i), [P, 512]) for i in range(2)] for j in range(2)]
        ps_o = [S.psum("ps_o%d" % j, [P, 512]) for j in range(2)]
        ps_z = [S.psum("ps_z%d" % j, [P, 512]) for j in range(2)]
        pt = [[S.sbuf("pt%d%d" % (j, i), [P, 512], BF16) for i in range(2)] for j in range(2)]
        rz = [S.sbuf("rz%d" % j, [P, 512], F32) for j in range(2)]
        t0 = S.sbuf("t0", [P, 512], F32)
        t1 = S.sbuf("t1", [P, 512], F32)
        rstd = S.sbuf("rstd", [P, 512], F32)
        oo = [S.sbuf("oo%d" % i, [P, 512], F32) for i in range(2)]
        jobs = []
        for h in range(2):
            for q0 in range(0, L, 512):
                jobs.append((h, Q[h][:, q0:q0 + 512], 512, 0, NKC, obT[h, :, q0:q0 + 512]))
            jobs.append((h, QC[h][:], LC, L // P, NKC, obcT[h, :, :]))
        for ji, (h, qv, n, kc0, kc1, outv) in enumerate(jobs):
            for kc in range(kc0, kc1):
                bi = kc % 2
                for j in range(2):
                    S.matmul(ps_s[j][bi][:, :n], K[h][j * 64:(j + 1) * 64, kc * P:(kc + 1) * P], qv[j * 64:(j + 1) * 64, :],
                             start=True, stop=True)
                for j in range(2):
                    S.act(pt[j][bi][:, :n], ps_s[j][bi][:, :n], AF.Exp, scale=0.125)
                for j in range(2):
                    S.matmul(ps_o[j][:, :n], V[:, kc, h * P:(h + 1) * P], pt[j][bi][:, :n], start=(kc == kc0), stop=(kc == kc1 - 1))
                    S.matmul(ps_z[j][:, :n], onesb[:], pt[j][bi][:, :n], start=(kc == kc0), stop=(kc == kc1 - 1))
            for j in range(2):
                S.recip(rz[j][:, :n], ps_z[j][:, :n])
            S.tt(t0[:, :n], ps_o[0][:, :n], rz[0][:, :n], ALU.mult)
            S.tt(t1[:, :n], ps_o[1][:, :n], rz[1][:, :n], ALU.mult)
            S.stt(t0[:, :n], t1[:, :n], neglam[:, 0:1], t0[:, :n], ALU.mult, ALU.add)
            pss = ps_s[0][0]
            rms_rstd(S, C, lambda c: t0[:, :n], n, 1, P, pss, rstd[:, :n])
            o = oo[ji % 2]
            S.tt(t1[:, :n], t0[:, :n], rstd[:, :n], ALU.mult)
            S.ts(o[:, :n], t1[:, :n], sc2[:, 0:1], ALU.mult)
            S.dma("pool", outv, o[:, :n])
        S.barrier()
    return nc


def tri_mask(S, nc, name, kind, blk=None):
    m = S.sbuf(name, [P, P], F32)
    S.memset(m[:], 1.0, e="pool")
    pat, cm, op = {"le": ([[1, P]], -1, ALU.is_ge), "ge": ([[-1, P]], 1, ALU.is_ge),
                   "gt": ([[-1, P]], 1, ALU.is_gt), "lt": ([[1, P]], -1, ALU.is_gt)}[kind]
    S.op("pool", lambda: nc.gpsimd.affine_select(m.t[:], m.t[:], pat, op, 0.0, base=0, channel_multiplier=cm), [m[:]], [m[:]])
    if blk:
        S.memset(m[0:blk, blk:P], 0.0, e="pool")
        S.memset(m[blk:P, 0:blk], 0.0, e="pool")
    return m


def build_Mssd(L, LC, dbg_stop=99):
    LT = L + LC
    NCH = LT // P
    NCC = LC // P
    nc = bass.Bass("TRN2", target_bir_lowering=False)
    with ExitStack() as st:
        S = Sched(nc, st)
        xl = S.dram("xbcl", [4, P, L + 4], F32, kind="ExternalInput")
        xc = S.dram("xbcc", [4, P, LC + 4], F32, kind="ExternalInput")
        cwd = S.dram("cw", [P, 4, 5], F32, kind="ExternalInput")
        cbd = S.dram("cb", [P, 4], F32, kind="ExternalInput")
        zd = S.dram("z", [P, NCH, 256], F32, kind="ExternalInput")
        dtd = S.dram("dt", [P, NCH, 8], F32, kind="ExternalInput")
        dbd = S.dram("dtb", [P, 8], F32, kind="ExternalInput")
        ald = S.dram("alog", [P, 8], F32, kind="ExternalInput")
        dsd = S.dram("dskip", [P, 4], F32, kind="ExternalInput")
        yo = S.dram("y", [NCH, P, 256], F32, kind="ExternalOutput")
        yf = S.dram("yf", [NCH, P, 256], F32, kind="Internal")
        C = mk_consts(S, nc)
        tri = {0: tri_mask(S, nc, "tri_f", "le"), 1: tri_mask(S, nc, "tri_b", "ge")}
        strict = {0: tri_mask(S, nc, "str_f", "gt"), 1: tri_mask(S, nc, "str_b", "lt")}
        cw = S.sbuf("cw", [P, 4, 5], F32)
        cb = S.sbuf("cb", [P, 4], F32)
        S.dma("sp", cw[:], cwd[:])
        S.dma("sp", cb[:], cbd[:])
        xs_tok = S.sbuf("xs_tok", [P, NCH, 256], F32)
        B_tok = S.sbuf("B_tok", [P, NCH, P], F32)
        BT = S.sbuf("BT", [P, LT], F32)
        CT = S.sbuf("CT", [P, LT], F32)
        dtv = S.sbuf("dtv", [P, NCH, 8], F32)
        aall = S.sbuf("aall", [P, NCH, 8], F32)
        dtb = S.sbuf("dtb", [P, 8], F32)
        aneg = S.sbuf("aneg", [P, 8], F32)
        dsk = S.sbuf("dsk", [P, 4], F32)
        S.dma("sp", dtv[:], dtd[:])
        S.dma("sp", dtb[:], dbd[:])
        S.dma("sp", aneg[:], ald[:])
        S.dma("sp", dsk[:], dsd[:])
        S.tt(dtv[:], dtv[:], View(dtb, dtb.t[:].rearrange("p (o e) -> p o e", o=1).to_broadcast([P, NCH, 8])), ALU.add)
        S.act(dtv[:], dtv[:], AF.Exp)
        S.act(dtv[:], dtv[:], AF.Ln, bias=C.ones[:, 0:1])
        S.act(aneg[:], aneg[:], AF.Exp)
        S.ts(aneg[:], aneg[:], -1.0, ALU.mult)
        S.tt(aall[:], dtv[:], View(aneg, aneg.t[:].rearrange("p (o e) -> p o e", o=1).to_broadcast([P, NCH, 8])), ALU.mult)
        ps_t = [S.psum("ps_t%d" % i, [P, 512]) for i in range(2)]
        with S.scope():
            raw = [S.sbuf("raw%d" % i, [P, 4, 516], F32) for i in range(2)]
            acc = [S.sbuf("acc%d" % i, [P, 512], F32) for i in range(2)]
            xsT = [S.sbuf("xsT%d" % i, [P, 512], F32) for i in range(2)]
            segs = [(xc, 0, LC)] + [(xl, LC, L)]
            ti = 0
            if dbg_stop < 0:
                segs = []
            for (src, base, seglen) in segs:
                for t0 in range(0, seglen, 512):
                    n = min(512, seglen - t0)
                    r = raw[ti % 2]
                    ti += 1
                    S.dma("sp", r[:, :, :n + 4], src[:, :, t0:t0 + n + 4].rearrange("c p t -> p c t"))
                    for c in range(4):
                        a = acc[c % 2]
                        eng = "dve"
                        S.ts(a[:, :n], r[:, c, 0:n], cw[:, c, 0:1], ALU.mult, e=eng)
                        for j in range(1, 5):
                            S.stt(a[:, :n], r[:, c, j:j + n], cw[:, c, j:j + 1], a[:, :n], ALU.mult, ALU.add, e=eng)
                        g0 = base + t0
                        if c < 2:
                            dst = xsT[c]
                            S.act(dst[:, :n], a[:, :n], AF.Silu, bias=cb[:, c:c + 1])
                        elif c == 2:
                            S.act(BT[:, g0:g0 + n], a[:, :n], AF.Silu, bias=cb[:, c:c + 1])
                        else:
                            S.act(CT[:, g0:g0 + n], a[:, :n], AF.Silu, bias=cb[:, c:c + 1])
                    for bl in range(n // P):
                        gc = (base + t0) // P + bl
                        pt = ps_t[bl % 2]
                        dbgv = None
                        S.transpose(pt[:, 0:P], xsT[0][:, bl * P:(bl + 1) * P], C.ident[:])
                        if dbgv == "T1":
                            S.copy(xs_tok[:, gc, 0:P], pt[:, 0:P], e="dve")
                            continue
                        S.transpose(pt[:, P:2 * P], xsT[1][:, bl * P:(bl + 1) * P], C.ident[:])
                        if dbgv == "T2":
                            S.copy(xs_tok[:, gc, :], pt[:, 0:2 * P], e="dve")
                            continue
                        S.transpose(pt[:, 2 * P:3 * P], BT[:, gc * P:(gc + 1) * P], C.ident[:])
                        S.copy(xs_tok[:, gc, :], pt[:, 0:2 * P], e="dve")
                        S.copy(B_tok[:, gc, :], pt[:, 2 * P:3 * P], e="dve")
        ps_arg = [S.psum("ps_arg%d" % i, [P, 512]) for i in range(2)]
        ps_cb = S.psum("ps_cb", [P, 512])
        ps_y = S.psum("ps_y", [P, 512])
        ps_st = S.psum("ps_st", [P, 512])
        ps_sm = S.psum("ps_sm", [P, 512])
        X = [S.sbuf("X%d" % i, [P, 4, P], F32) for i in range(2)]
        LTt = [S.sbuf("LT%d" % i, [P, 4, P], F32) for i in range(2)]
        CBm = S.sbuf("CBm", [P, P], F32)
        scT = [S.sbuf("scT%d" % i, [P, 4, P], F32) for i in range(2)]
        sm = S.sbuf("sm", [P, 8], F32)
        eacs = S.sbuf("eacs", [P, 4], F32)
        edec = S.sbuf("edec", [P, 4], F32)
        etot = S.sbuf("etot", [P, 4], F32)
        dif = S.sbuf("dif", [P, 4], F32)
        xdt = [S.sbuf("xdt%d" % i, [P, 4, 64], F32) for i in range(2)]
        xdtd = [S.sbuf("xdtd%d" % i, [P, 4, 64], F32) for i in range(2)]
        ST = S.sbuf("ST", [P, 4, 64], F32)
        yt = [S.sbuf("yt%d" % i, [P, 256], F32) for i in range(2)]
        y2 = [S.sbuf("y2%d" % i, [P, 256], F32) for i in range(2)]
        yfl = [S.sbuf("yfl%d" % i, [P, 256], F32) for i in range(2)]
        zt = [S.sbuf("zt%d" % i, [P, 256], F32) for i in range(2)]
        yfb = [S.sub("yf%d" % c, yf.t[c]) for c in range(NCH)]

        def bc4(v):
            return View(v.buf, v.ap.rearrange("p (h o) -> p h o", o=1).to_broadcast([P, 4, 64]))
        if dbg_stop < 2:
            for c in range(NCH):
                S.dma("sp", zt[c % 2][:], zd[:, c, :])
                if dbg_stop == 1:
                    S.tt(zt[c % 2][:], zt[c % 2][:], xs_tok[:, c, :], ALU.add)
                    S.tt(zt[c % 2][:, 0:P], zt[c % 2][:, 0:P], B_tok[:, c, :], ALU.add)
                S.dma("pool", S.sub("yo", yo.t[c])[:], zt[c % 2][:])
        for dr in range(2 if dbg_stop >= 2 else 0):
            order = list(range(NCH)) if dr == 0 else (list(range(NCC - 1, -1, -1)) + list(range(NCH - 1, NCC - 1, -1)))
            S.memset(ST[:], 0.0)
            for it, c in enumerate(order):
                bi = it % 2
                a4 = aall[:, c, dr * 4:(dr + 1) * 4]
                for h in range(4):
                    S.ts(X[bi][:, h, :], strict[dr][:], aall[:, c, dr * 4 + h:dr * 4 + h + 1], ALU.mult, e=("dve" if h % 2 else "pool"))
                for h in range(4):
                    S.matmul(ps_arg[bi][:, h * P:(h + 1) * P], X[bi][:, h, :], tri[dr][:])
                S.act(LTt[bi][:].rearrange("p h l -> p (h l)"), ps_arg[bi][:], AF.Exp)
                S.matmul(ps_cb[:, 0:P], BT[:, c * P:(c + 1) * P], CT[:, c * P:(c + 1) * P])
                S.tt(CBm[:], ps_cb[:, 0:P], tri[dr][:], ALU.mult)
                S.tt(scT[bi][:], LTt[bi][:], View(CBm, CBm.t[:].rearrange("p (o l) -> p o l", o=1).to_broadcast([P, 4, P])), ALU.mult)
                S.matmul(ps_sm[:, 0:4], tri[dr][:], a4)
                S.matmul(ps_sm[:, 4:8], C.ones[:], a4)
                S.copy(sm[:], ps_sm[:, 0:8])
                S.act(eacs[:], sm[:, 0:4], AF.Exp)
                S.act(etot[:], sm[:, 4:8], AF.Exp)
                S.tt(dif[:], sm[:, 4:8], sm[:, 0:4], ALU.subtract)
                S.act(edec[:], dif[:], AF.Exp)
                xv = xs_tok[:, c, :].rearrange("p (h d) -> p h d", h=4)
                S.tt(xdt[bi][:], xv, bc4(dtv[:, c, dr * 4:(dr + 1) * 4]), ALU.mult, e="pool")
                S.tt(xdtd[bi][:], xdt[bi][:], bc4(edec[:]), ALU.mult)
                for h in range(4):
                    S.matmul(ps_y[:, h * 64:(h + 1) * 64], scT[bi][:, h, :], xdt[bi][:, h, :])
                S.matmul(ps_y[:, 256:512], CT[:, c * P:(c + 1) * P], ST[:].rearrange("p h d -> p (h d)"))
                y = yt[bi]
                S.tt(y[:].rearrange("p (h d) -> p h d", h=4), ps_y[:, 256:512].rearrange("p (h d) -> p h d", h=4), bc4(eacs[:]), ALU.mult)
                S.tt(y[:], y[:], ps_y[:, 0:256], ALU.add)
                S.matmul(ps_st[:, 0:256], B_tok[:, c, :], xdtd[bi][:].rearrange("p h d -> p (h d)"))
                S.tt(ST[:], ST[:], bc4(etot[:]), ALU.mult)
                S.tt(ST[:].rearrange("p h d -> p (h d)"), ST[:].rearrange("p h d -> p (h d)"), ps_st[:, 0:256], ALU.add)
                if dr == 0:
                    S.dma("pool", yfb[c][:], y[:])
                else:
                    S.dma("sp", yfl[bi][:], yfb[c][:])
                    S.dma("sp", zt[bi][:], zd[:, c, :])
                    o = y2[bi]
                    S.tt(o[:].rearrange("p (h d) -> p h d", h=4), xv, bc4(dsk[:]), ALU.mult, e="pool")
                    S.tt(y[:], y[:], yfl[bi][:], ALU.add)
                    S.tt(o[:], o[:], y[:], ALU.add)
                    S.act(zt[bi][:], zt[bi][:], AF.Silu)
                    S.tt(o[:], o[:], zt[bi][:], ALU.mult)
                    S.dma("pool", S.sub("yo", yo.t[c])[:], o[:])
        S.barrier()
    return nc


def build_Mgdn(L, LC):
    LT = L + LC
    NCH = LT // P
    NCC = LC // P
    nc = bass.Bass("TRN2", target_bir_lowering=False)
    with ExitStack() as st:
        S = Sched(nc, st)
        ql = S.dram("qkvl", [6, P, L + 4], F32, kind="ExternalInput")
        qc = S.dram("qkvc", [6, P, LC + 4], F32, kind="ExternalInput")
        cwd = S.dram("cw", [P, 6, 5], F32, kind="ExternalInput")
        zd = S.dram("z", [P, NCH, 256], F32, kind="ExternalInput")
        ad = S.dram("araw", [P, NCH, 4], F32, kind="ExternalInput")
        bd = S.dram("braw", [P, NCH, 4], F32, kind="ExternalInput")
        ald = S.dram("alog", [P, 4], F32, kind="ExternalInput")
        dbd = S.dram("dtb", [P, 4], F32, kind="ExternalInput")
        nwd = S.dram("nw", [P, P], F32, kind="ExternalInput")
        oa = S.dram("oa", [NCH, P, 256], F32, kind="ExternalOutput")
        ofd = S.dram("of", [NCH, P, 256], F32, kind="Internal")
        C = mk_consts(S, nc)
        M = {k: tri_mask(S, nc, "m_" + k, k, blk=64) for k in ("le", "ge", "gt", "lt")}
        halfA = S.sbuf("halfA", [P, P], F32)
        halfB = S.sbuf("halfB", [P, P], F32)
        S.memset(halfA[:], 0.0)
        S.memset(halfB[:], 0.0)
        S.memset(halfA[0:64, :], 1.0)
        S.memset(halfB[64:128, :], 1.0)
        cw = S.sbuf("cw", [P, 6, 5], F32)
        S.dma("sp", cw[:], cwd[:])
        nw = S.sbuf("nw", [P, P], F32)
        S.dma("sp", nw[:], nwd[:])
        gall = S.sbuf("gall", [P, NCH, 4], F32)
        ball = S.sbuf("ball", [P, NCH, 4], F32)
        negb = S.sbuf("negb", [P, NCH, 4], F32)
        aneg = S.sbuf("aneg", [P, 4], F32)
        dtb = S.sbuf("dtb", [P, 4], F32)
        S.dma("sp", gall[:], ad[:])
        S.dma("sp", ball[:], bd[:])
        S.dma("sp", aneg[:], ald[:])
        S.dma("sp", dtb[:], dbd[:])

        def bcn(v):
            return View(v.buf, v.ap.rearrange("p (o e) -> p o e", o=1).to_broadcast([P, NCH, 4]))
        S.tt(gall[:], gall[:], bcn(dtb[:]), ALU.add)
        S.act(gall[:], gall[:], AF.Exp)
        S.act(gall[:], gall[:], AF.Ln, bias=C.ones[:, 0:1])
        S.act(aneg[:], aneg[:], AF.Exp)
        S.ts(aneg[:], aneg[:], -1.0, ALU.mult)
        S.tt(gall[:], gall[:], bcn(aneg[:]), ALU.mult)
        S.act(ball[:], ball[:], AF.Sigmoid)
        S.ts(negb[:], ball[:], -1.0, ALU.mult)
        BA = [S.psum("BA%d" % h, [P, 512]) for h in range(2)]
        B1 = [S.psum("B1%d" % h, [P, 512]) for h in range(2)]
        B2 = [S.psum("B2%d" % h, [P, 512]) for h in range(2)]
        B3 = [S.psum("B3%d" % h, [P, 512]) for h in range(2)]
        Wk = []
        for h in range(2):
            W = NS()
            for nm in ("X", "Dm", "Dv", "Ds", "kbg", "kdec", "vb", "vnew", "oq", "o", "of_", "zt", "t1"):
                setattr(W, nm, S.sbuf("%s%d" % (nm, h), [P, P], F32))
            for nm in ("NA", "RA", "uw"):
                setattr(W, nm, S.sbuf("%s%d" % (nm, h), [P, 2 * P], F32))
            W.NR = [S.sbuf("NR%d%d" % (h, i), [P, 2 * P], F32) for i in range(2)]
            W.Xc = [S.sbuf("Xc%d%d" % (h, i), [P, P], F32) for i in range(2)]
            W.esm = S.sbuf("esm%d" % h, [P, 4], F32)
            W.bg = S.sbuf("bg%d" % h, [P, 1], F32)
            W.ss = S.sbuf("ss%d" % h, [P, 1], F32)
            W.oo = [S.sbuf("oo%d%d" % (h, i), [P, P], F32) for i in range(2)]
            Wk.append(W)
        state = [S.sbuf("state%d" % h, [P, P], F32) for h in range(2)]
        raw = S.sbuf("raw", [P, 6, 516], F32)
        acc = [S.sbuf("acc%d" % i, [P, 512], F32) for i in range(2)]
        sqb = S.sbuf("sqb", [P, 512], F32)
        lnb = S.sbuf("lnb", [P, 512], F32)
        rsb = S.sbuf("rsb", [P, 512], F32)
        qkv = [S.sbuf("qkv%d" % i, [P, 6, 512], F32) for i in range(2)]
        ofb = [[S.sub("of%d_%d" % (c, h), ofd.t[c][:, h * P:(h + 1) * P]) for h in range(2)] for c in range(NCH)]

        def prep(src, t0, n, dst):
            S.dma("sp", raw[:, :, :n + 4], src[:, :, t0:t0 + n + 4].rearrange("c p t -> p c t"))
            for c in range(6):
                a = acc[c % 2]
                S.ts(a[:, :n], raw[:, c, 0:n], cw[:, c, 0:1], ALU.mult)
                for j in range(1, 5):
                    S.stt(a[:, :n], raw[:, c, j:j + n], cw[:, c, j:j + 1], a[:, :n], ALU.mult, ALU.add)
                if c >= 4:
                    S.act(dst[:, c, :n], a[:, :n], AF.Silu)
                else:
                    S.act(a[:, :n], a[:, :n], AF.Silu)
                    S.act(sqb[:, :n], a[:, :n], AF.Square)
                    pb = B3[c % 2]
                    S.matmul(pb[:, :n], C.ones[:], sqb[:, :n])
                    S.act(lnb[:, :n], pb[:, :n], AF.Ln, bias=C.eps[:, 0:1])
                    S.act(rsb[:, :n], lnb[:, :n], AF.Exp, scale=-0.5)
                    S.stt(dst[:, c, :n], a[:, :n], (128.0 ** -0.5) if c < 2 else 1.0, rsb[:, :n], ALU.mult, ALU.mult)

        def unit(hl, dr, gp, qv, kv, vv):
            col = dr * 2 + hl
            g = gall[:, gp, col:col + 1]
            nb = negb[:, gp, col:col + 1]
            bt = ball[:, gp, col:col + 1]
            W = Wk[hl]
            bA, b1, b2, b3 = BA[hl], B1[hl], B2[hl], B3[hl]
            Tri, Xm, Val, SVal = (M["le"], M["gt"], M["ge"], M["gt"]) if dr == 0 else (M["ge"], M["lt"], M["le"], M["lt"])
            S.ts(W.X[:], Xm[:], g, ALU.mult)
            S.matmul(bA[:, 0:128], Tri[:], W.X[:])
            S.matmul(bA[:, 128:129], Tri[:], g)
            S.matmul(bA[:, 129:130], Xm[:], g)
            S.matmul(bA[:, 130:131], halfA[:], g)
            S.matmul(bA[:, 131:132], halfB[:], g)
            S.matmul(b1[:, 0:128], kv, kv)
            S.matmul(b1[:, 128:256], qv, kv)
            S.transpose(bA[:, 256:384], kv, C.ident[:])
            S.transpose(bA[:, 384:512], vv, C.ident[:])
            yield
            S.act(W.Dm[:], bA[:, 0:128], AF.Exp)
            S.act(W.esm[:], bA[:, 128:132], AF.Exp)
            S.tt(W.bg[:], W.esm[:, 0:1], bt, ALU.mult)
            S.act(W.kdec[:], bA[:, 256:384], AF.Identity, scale=W.esm[:, 1:2])
            S.act(W.vb[:], bA[:, 384:512], AF.Identity, scale=bt)
            S.act(W.kbg[:], bA[:, 256:384], AF.Identity, scale=W.bg[:, 0:1])
            S.tt(W.Dv[:], W.Dm[:], Val[:], ALU.mult)
            S.tt(W.Ds[:], W.Dm[:], SVal[:], ALU.mult)
            S.stt(W.NA[:, 0:128], b1[:, 0:128], nb, W.Ds[:], ALU.mult, ALU.mult)
            S.tt(W.NA[:, 128:256], b1[:, 128:256], W.Dv[:], ALU.mult)
            yield
            S.transpose(b1[:, 256:384], W.NA[:, 0:128], C.ident[:])
            S.transpose(b1[:, 384:512], W.NA[:, 128:256], C.ident[:])
            S.copy(W.RA[:], b1[:, 256:512])
            X = W.Xc[0]
            S.tt(X[:], W.RA[:, 0:128], C.ident[:], ALU.add)
            yield
            Ncur = W.NA[:, 0:128]
            Rcur = W.RA[:, 0:128]
            for lev in range(5):
                NR = W.NR[lev % 2]
                S.matmul(b2[:, 0:128], Rcur, Ncur)
                if lev < 4:
                    S.matmul(b2[:, 128:256], Ncur, Rcur)
                    S.copy(NR[:], b2[:, 0:256])
                else:
                    S.copy(NR[:, 0:128], b2[:, 0:128])
                yield
                S.matmul(b2[:, 256:384], NR[:, 0:128], X[:])
                Xn = W.Xc[(lev + 1) % 2]
                S.tt(Xn[:], X[:], b2[:, 256:384], ALU.add)
                X = Xn
                Ncur = NR[:, 0:128]
                Rcur = NR[:, 128:256]
                yield
            S.matmul(b3[:, 0:128], X[:], W.vb[:])
            S.matmul(b3[:, 128:256], W.kbg[:], X[:])
            S.copy(W.uw[:], b3[:, 0:256])
            yield
            blocks = [(0, 64), (64, 128)] if dr == 0 else [(64, 128), (0, 64)]
            Sst = state[hl]
            for bi, (r0, r1) in enumerate(blocks):
                reg = b3[:, 256:512] if bi == 0 else b3[:, 0:256]
                S.matmul(reg[:, 0:128], W.uw[:, 128:256], Sst[:])
                S.matmul(reg[:, 128:256], qv, Sst[:])
                S.tt(W.vnew[r0:r1, :], W.uw[r0:r1, 0:128], reg[r0:r1, 0:128], ALU.subtract)
                S.ts(W.oq[r0:r1, :], reg[r0:r1, 128:256], W.esm[r0:r1, 0:1], ALU.mult)
                yield
                S.matmul(b1[:, 0:128], W.kdec[r0:r1, :], W.vnew[r0:r1, :])
                egX = W.esm[:, 2:3] if r0 == 0 else W.esm[:, 3:4]
                S.stt(Sst[:], Sst[:], egX, b1[:, 0:128], ALU.mult, ALU.add)
                yield
            S.matmul(b1[:, 128:256], W.RA[:, 128:256], W.vnew[:])
            S.tt(W.o[:], W.oq[:], b1[:, 128:256], ALU.add)
            if dr == 0:
                S.dma("pool", ofb[gp][hl][:], W.o[:])
            else:
                S.dma("sp", W.of_[:], ofb[gp][hl][:])
                S.dma("sp", W.zt[:], zd[:, gp, hl * P:(hl + 1) * P])
                S.tt(W.o[:], W.o[:], W.of_[:], ALU.add)
                S.act(W.t1[:], W.o[:], AF.Square, accum_out=W.ss[:, 0:1])
                yield
                S.act(W.ss[:], W.ss[:], AF.Ln, scale=1.0 / 128.0, bias=C.eps[:, 0:1])
                S.act(W.ss[:], W.ss[:], AF.Exp, scale=-0.5)
                S.act(W.zt[:], W.zt[:], AF.Silu)
                S.stt(W.t1[:], W.o[:], W.ss[:, 0:1], nw[:], ALU.mult, ALU.mult)
                oo = W.oo[gp % 2]
                S.tt(oo[:], W.t1[:], W.zt[:], ALU.mult)
                S.dma("pool", S.sub("oa", oa.t[gp][:, hl * P:(hl + 1) * P])[:], oo[:])
            yield

        for dr in range(2):
            for h in range(2):
                S.memset(state[h][:], 0.0)
            segs = [(qc, 0, LC), (ql, LC, L)]
            tl = []
            for (src, base, seglen) in segs:
                tt_ = [(src, base, t0, min(512, seglen - t0)) for t0 in range(0, seglen, 512)]
                if dr == 1:
                    tt_ = tt_[::-1]
                tl += tt_
            for ti, (src, base, t0, n) in enumerate(tl):
                dst = qkv[ti % 2]
                prep(src, t0, n, dst)
                prs = list(range(n // P))
                if dr == 1:
                    prs = prs[::-1]
                for pi in prs:
                    gp = (base + t0) // P + pi
                    sl = slice(pi * P, (pi + 1) * P)
                    gens = [unit(h, dr, gp, dst[:, 0 + h, sl], dst[:, 2 + h, sl], dst[:, 4 + h, sl]) for h in range(2)]
                    alive = [True, True]
                    while any(alive):
                        for h in range(2):
                            if alive[h]:
                                try:
                                    next(gens[h])
                                except StopIteration:
                                    alive[h] = False
        S.barrier()
    return nc


NFM = 36
NTK = 1568
GRP = [[0, 1], [2, 3], [4, 5], [6, 7]]


def build_fused(L, LC, depth=2):
    NT, NCX = L // 2, LC // 2
    TT = NT + NCX
    LT = L + LC
    NCH = LT // P
    NCC = LC // P
    LK = LT
    NKC = LK // P
    NLC = NT // P
    assert NCX == P
    nc = bass.Bass("TRN2", target_bir_lowering=False)
    with ExitStack() as st:
        S = Sched(nc, st)
        ccsem = st.enter_context(nc.semaphore("ccsem"))
        cc = [0]
        xT = S.dram("xT", [P, KC, TT], F32, kind="ExternalInput")
        cv = S.dram("cv", [P, KC, 2], F32, kind="ExternalInput")
        selv = S.dram("selv", [P, 2], F32, kind="ExternalInput")
        yT = S.dram("yT", [P, KC, TT], F32, kind="ExternalOutput")
        Wl = []
        for i in range(depth):
            W = NS()
            sfx = "_%d" % i
            W.wada1 = S.dram("wada1" + sfx, [D, 5 * D], F32, kind="ExternalInput")
            W.bada1 = S.dram("bada1" + sfx, [P, 40], F32, kind="ExternalInput")
            W.wada2 = S.dram("wada2" + sfx, [D, 6 * D], F32, kind="ExternalInput")
            W.bada2 = S.dram("bada2" + sfx, [P, 48], F32, kind="ExternalInput")
            W.ng = S.dram("ng" + sfx, [P, 48], F32, kind="ExternalInput")
            W.w1a = S.dram("w1a" + sfx, [D, 2 * DFF], F32, kind="ExternalInput")
            W.w2a = S.dram("w2a" + sfx, [DFF, D], F32, kind="ExternalInput")
            W.w1b = S.dram("w1b" + sfx, [D, 2 * DFF], F32, kind="ExternalInput")
            W.w2b = S.dram("w2b" + sfx, [DFF, D], F32, kind="ExternalInput")
            W.win = S.dram("win" + sfx, [D, NFM * P + NTK], F32, kind="ExternalInput")
            W.wg = S.dram("wg" + sfx, [D, 3 * D], F32, kind="ExternalInput")
            W.wb = S.dram("wb" + sfx, [1536, D], F32, kind="ExternalInput")
            W.wo = S.dram("wo" + sfx, [D, D], F32, kind="ExternalInput")
            W.snw = S.dram("snw" + sfx, [P, 4], F32, kind="ExternalInput")
            W.lam = S.dram("lam" + sfx, [P, 256], F32, kind="ExternalInput")
            W.dnw = S.dram("dnw" + sfx, [P, 1], F32, kind="ExternalInput")
            W.li = S.dram("li" + sfx, [P, 1], F32, kind="ExternalInput")
            W.scw = S.dram("scw" + sfx, [P, 4, 5], F32, kind="ExternalInput")
            W.scb = S.dram("scb" + sfx, [P, 4], F32, kind="ExternalInput")
            W.sdtb = S.dram("sdtb" + sfx, [P, 8], F32, kind="ExternalInput")
            W.salog = S.dram("salog" + sfx, [P, 8], F32, kind="ExternalInput")
            W.sdsk = S.dram("sdsk" + sfx, [P, 4], F32, kind="ExternalInput")
            W.gcw = S.dram("gcw" + sfx, [P, 6, 5], F32, kind="ExternalInput")
            W.galog = S.dram("galog" + sfx, [P, 4], F32, kind="ExternalInput")
            W.gdtb = S.dram("gdtb" + sfx, [P, 4], F32, kind="ExternalInput")
            W.gnw = S.dram("gnw" + sfx, [P, P], F32, kind="ExternalInput")
            Wl.append(W)
        Xs = S.dram("Xs", [P, KC, TT], F32)
        X1 = S.dram("X1s", [P, KC, TT], F32)
        X2 = S.dram("X2s", [P, KC, TT], F32)
        NB = NT // 256
        PTL = nc.dram_tensor("PTL", [NFM, P, NT], F32)
        PTLG = nc.dram_tensor("PTLG", [NFM, 2, P, NT], F32)
        PTC = nc.dram_tensor("PTC", [NFM, P, NCX], F32)
        PTCG = nc.dram_tensor("PTCG", [2, 2, 18, P, NCX], F32)
        PKL = nc.dram_tensor("PKL", [NB, 256, NTK], F32)
        PKLG = nc.dram_tensor("PKLG", [NB, 2, 256, NTK], F32)
        PKC = nc.dram_tensor("PKC", [NCX, NTK], F32)
        PKCG = nc.dram_tensor("PKCG", [2, NCX, NTK], F32)
        MOL = nc.dram_tensor("MOL", [6, 2, P, NT], F32)
        MOLG = nc.dram_tensor("MOLG", [6, 2, 2, P, NT], F32)
        MOC = nc.dram_tensor("MOC", [6, 2, P, NCX], F32)
        MOCG = nc.dram_tensor("MOCG", [2, 6, 2, P, NCX], F32)
        OF = S.dram("OFs", [NCH, P, 256], F32)
        OFD = [S.dram("OFD%d" % d_, [NCH, P, 256], F32) for d_ in range(2)]

        def mo_dst(c0, c1, s_, off, n):
            if off >= NT:
                return MOC.ap()[c0:c1, s_, :, off - NT:off - NT + n]
            return MOL.ap()[c0:c1, s_, :, off:off + n]

        def dsub(ap):
            return Buf("u", ap, "dram")[:] if False else View(Buf("u", ap, "dram"), ap)

        tiles = mk_tiles(NT, NCX)
        C = mk_consts(S, nc)
        sel = S.sbuf("sel", [P, 2], F32)
        S.dma("sp", sel[:], selv[:])
        dummy = S.sbuf("dummy", [P, 1], F32)

        def blend(dst, alt):
            S.ts(dst, dst, sel[:, 0:1], ALU.mult)
            S.stt(dst, alt, sel[:, 1:2], dst, ALU.mult, ALU.add)

        def gather_many(pairs):
            S.barrier()
            for (i_ap, o_ap) in pairs:
                cc[0] += 1
                nc.gpsimd.collective_compute("AllGather", ALU.bypass, replica_groups=GRP, ins=[i_ap], outs=[o_ap]).then_inc(ccsem, 1)
            nc.gpsimd.wait_ge(ccsem, cc[0])
            S.memset(dummy[:], 0.0, e="pool")
            S.barrier()

        def gather_P():
            pr = [(PTL.ap()[c], PTLG.ap()[c].rearrange("r p t -> (r p) t")) for c in range(NFM)]
            pr += [(PTC.ap()[h * 18:(h + 1) * 18].rearrange("c p t -> (c p) t"), PTCG.ap()[h].rearrange("r c p t -> (r c p) t")) for h in range(2)]
            pr += [(PKL.ap()[b], PKLG.ap()[b].rearrange("r t e -> (r t) e")) for b in range(NB)]
            pr += [(PKC.ap(), PKCG.ap().rearrange("r t e -> (r t) e"))]
            gather_many(pr)

        def gather_M():
            pr = [(MOL.ap()[c, s_], MOLG.ap()[c, s_].rearrange("r p t -> (r p) t")) for c in range(6) for s_ in range(2)]
            pr += [(MOC.ap().rearrange("c s p t -> (c s p) t"), MOCG.ap().rearrange("r c s p t -> (r c s p) t"))]
            gather_many(pr)

        def tokpos(gc):
            if gc < NCC:
                return gc, NT
            t = (gc - NCC) * P
            return t // NT, t % NT

        def mk_ps():
            PS = NS()
            PS.ss = S.psum("ps_ss", [P, 512])
            PS.ss2 = S.psum("ps_ss2", [P, 512])
            PS.g = [S.psum("ps_g%d" % i, [P, 512]) for i in range(2)]
            PS.u = [S.psum("ps_u%d" % i, [P, 512]) for i in range(2)]
            PS.y = [S.psum("ps_y%d" % i, [P, 512]) for i in range(2)]
            return PS

        def bc(v):
            return View(v.buf, v.ap.rearrange("p (c o) -> p c o", o=1).to_broadcast([P, KC, 2]))

        def ph_R1(W, xin, x1t):
            with S.scope():
                alloc_small(S, C)
                PS = mk_ps()
                mods = S.sbuf("mods", [P, 40, 2], F32)
                ngt = S.sbuf("ngt", [P, 6, KC], F32)
                S.dma("sp", ngt[:], W.ng[:].rearrange("p (m c) -> p m c", c=KC))
                A1 = S.sbuf("A1", [P, KC, 2], F32)
                G1 = S.sbuf("G1", [P, KC, 2], F32)
                A2 = S.sbuf("A2", [P, KC, 2], F32)
                with S.scope():
                    stg = [S.sbuf("stgm%d" % i, [P, 5 * D], F32) for i in range(KC)]
                    compute_mods(S, C, cv, W.wada1, W.bada1, 5, stg, PS.g[0], mods)
                S.stt(A1[:], mods[:, 8:16, :], 1.0, bc(ngt[:, 0, :]), ALU.add, ALU.mult)
                S.stt(G1[:], mods[:, 16:24, :], 0.5, bc(ngt[:, 1, :]), ALU.mult, ALU.mult)
                S.stt(A2[:], mods[:, 32:40, :], 1.0, bc(ngt[:, 2, :]), ALU.add, ALU.mult)
                B1 = mods[:, 0:8, :]
                B2 = mods[:, 24:32, :]
                with S.scope():
                    w1b = S.sbuf("w1b", [P, KC, 2 * DFF], BF16)
                    w2b = S.sbuf("w2b", [P, FC, D], BF16)
                    with S.scope():
                        stages = [S.sbuf("wst%d" % i, [P, 2048], F32) for i in range(3)]
                        load_w(S, W.w1a, w1b, D, 2 * DFF, stages)
                        load_w(S, W.w2a, w2b, DFF, D, stages)
                    alloc_ffn_work(S, C)
                    ffn_sweep(S, C, tiles, lambda j: xin[j][:], lambda j: x1t[j][:], w1b, w2b, A1[:], B1, G1[:], PS)
                with S.scope():
                    NW = NFM * P + NTK
                    winb = S.sbuf("winb", [P, KC, NW], BF16)
                    with S.scope():
                        stages = [S.sbuf("wst%d" % i, [P, 2048], F32) for i in range(3)]
                        load_w(S, W.win, winb, D, NW, stages)
                    xt2 = [S.sbuf("xq%d" % i, [P, KC, 512], F32) for i in range(2)]
                    h = S.sbuf("h2", [P, KC, 512], BF16)
                    ost = [S.sbuf("ost%d" % i, [P, 4, 512], F32) for i in range(3)]
                    tst = [S.sbuf("tst%d" % i, [P, NTK], F32) for i in range(2)]
                    pps = PS.g + PS.u + PS.y
                    gi = 0
                    ti = 0
                    for j, (s0, n, col) in enumerate(tiles):
                        xt = xt2[j % 2]
                        S.dma("sp", xt[:, :, :n], x1t[j][:])
                        norm_mod(S, C, xt, n, A2[:], B2, col, h, PS.ss)
                        for c0 in range(0, NFM, 4):
                            nn = min(4, NFM - c0)
                            o = ost[gi % 3]
                            gi += 1
                            for cc_ in range(nn):
                                pp = pps[(c0 + cc_) % 6]
                                for k in range(KC):
                                    S.matmul(pp[:, :n], winb[:, k, (c0 + cc_) * P:(c0 + cc_ + 1) * P], h[:, k, :n],
                                             start=(k == 0), stop=(k == KC - 1))
                                S.copy(o[:, cc_, :n], pp[:, :n], e=("act" if cc_ % 2 else "dve"))
                            pdst = PTL.ap()[c0:c0 + nn, :, s0:s0 + n] if col == 0 else PTC.ap()[c0:c0 + nn, :, 0:n]
                            S.dma("pool", dsub(pdst.rearrange("c p t -> p c t")), o[:, :nn, :n])
                        for sb in range(n // P):
                            tt_ = tst[ti % 2]
                            ti += 1
                            for q, c0 in enumerate(range(0, NTK, 512)):
                                w = min(512, NTK - c0)
                                pp = pps[q % 6]
                                for k in range(KC):
                                    S.matmul(pp[:, :w], h[:, k, sb * P:(sb + 1) * P], winb[:, k, NFM * P + c0:NFM * P + c0 + w],
                                             start=(k == 0), stop=(k == KC - 1))
                                S.copy(tt_[:, c0:c0 + w], pp[:, :w], e=("act" if q % 2 else "dve"))
                            trow = s0 + sb * P
                            kdst = PKL.ap()[trow // 256, trow % 256:trow % 256 + P, :] if col == 0 else PKC.ap()[0:P, :]
                            S.dma("pool", dsub(kdst), tt_[:])

        def lat_rc(t0):
            return t0 // NT, t0 % NT

        def load_fm(q, dst, alt, c0, nch, seg, t0, n, halo):
            seglen = L if seg == "lat" else LC
            a, b = max(0, t0 - halo), min(seglen, t0 + n + halo)
            if halo and (t0 - halo < 0):
                S.memset(dst[:, :, 0:halo], 0.0)
                S.memset(alt[:, :, 0:halo], 0.0)
            if halo and (t0 + n + halo > seglen):
                S.memset(dst[:, :, n + halo:n + 2 * halo], 0.0)
                S.memset(alt[:, :, n + halo:n + 2 * halo], 0.0)
            pieces = []
            per = NT if seg == "lat" else NCX
            base = 0 if seg == "lat" else NT
            p = a
            while p < b:
                r = p // per
                e = min(b, (r + 1) * per)
                pieces.append((r, p % per, e - p, p - (t0 - halo)))
                p = e
            for g, tgt in ((0, dst), (1, alt)):
                for (r, col0, ln, d0) in pieces:
                    if seg == "lat":
                        sap = PTLG.ap()[g * 18 + c0:g * 18 + c0 + nch, r, :, col0:col0 + ln]
                    else:
                        sap = PTCG.ap()[g, r, c0:c0 + nch, :, col0:col0 + ln]
                    S.dma(q, tgt[:, :, d0:d0 + ln], dsub(sap.rearrange("c p t -> p c t")))
            blend(dst[:, :, :], alt[:, :, :])

        def load_tok(q, dst, alt, gc, e0, ne):
            s, off = tokpos(gc)
            for g, tgt in ((0, dst), (1, alt)):
                if off >= NT:
                    sap = PKCG.ap()[s, 0:P, g * 784 + e0:g * 784 + e0 + ne]
                else:
                    sap = PKLG.ap()[off // 256, s, off % 256:off % 256 + P, g * 784 + e0:g * 784 + e0 + ne]
                S.dma(q, tgt, dsub(sap))
            blend(dst, alt)

        def load_small(smallst, alt):
            for g, tgt in ((0, smallst), (1, alt)):
                for r in range(2):
                    S.dma("sp", tgt[:, r, :], dsub(PKCG.ap()[r, 0:P, g * 784 + 768:g * 784 + 784]))
                    for bq in range(NB):
                        c_ = NCC + r * NLC + 2 * bq
                        S.dma("sp" if bq % 2 else "pool", tgt[:, c_:c_ + 2, :],
                              dsub(PKLG.ap()[bq, r, :, g * 784 + 768:g * 784 + 784].rearrange("(c p) e -> p c e", p=P)))
            blend(smallst[:], alt[:])

        def ph_Mdiff(W):
            with S.scope():
                C.sq = [S.sbuf("sq%d" % i, [P, 512], F32) for i in range(2)]
                C.lnt = C.sq[0]
                onesb = S.sbuf("onesb", [P, P], BF16)
                S.memset(onesb[:], 1.0)
                Q = [S.sbuf("Q%d" % h, [P, L], BF16) for h in range(2)]
                QC = [S.sbuf("QC%d" % h, [P, LC], BF16) for h in range(2)]
                K = [S.sbuf("K%d" % h, [P, LK], BF16) for h in range(2)]
                V = S.sbuf("V", [P, NKC, 256], BF16)
                lam = S.sbuf("lamt", [P, 4, 64], F32)
                S.dma("sp", lam[:], W.lam[:].rearrange("p (a b) -> p a b", b=64))
                nw = S.sbuf("nwt", [P, 1], F32)
                li = S.sbuf("lit", [P, 1], F32)
                S.dma("sp", nw[:], W.dnw[:])
                S.dma("sp", li[:], W.li[:])
                pr = S.sbuf("pr", [P, 2, 64], F32)
                s12 = S.sbuf("s12", [P, 2], F32)
                S.tt(pr[:, 0, :], lam[:, 0, :], lam[:, 1, :], ALU.mult)
                S.tt(pr[:, 1, :], lam[:, 2, :], lam[:, 3, :], ALU.mult)
                S.reduce(s12[:], pr[:], ALU.add)
                e12 = S.sbuf("e12", [P, 2], F32)
                S.act(e12[:], s12[:], AF.Exp)
                neglam = S.sbuf("neglam", [P, 1], F32)
                S.tt(neglam[:], e12[:, 1:2], e12[:, 0:1], ALU.subtract)
                S.tt(neglam[:], neglam[:], li[:], ALU.subtract)
                sc2 = S.sbuf("sc2", [P, 1], F32)
                S.ts(sc2[:], li[:], -1.0, ALU.mult, 1.0, ALU.add)
                S.tt(sc2[:], sc2[:], nw[:], ALU.mult)
                with S.scope():
                    cosb = S.sbuf("cosb", [P, L], F32)
                    sinb = S.sbuf("sinb", [P, L], F32)
                    rope_tables(S, nc, C, L, cosb, sinb)
                    with S.scope():
                        a = [S.sbuf("la%d" % i, [P, 1, 512], F32) for i in range(2)]
                        a2 = [S.sbuf("la2%d" % i, [P, 1, 512], F32) for i in range(2)]
                        b = [S.sbuf("lb%d" % i, [P, 1, 512], F32) for i in range(2)]
                        b2 = [S.sbuf("lb2%d" % i, [P, 1, 512], F32) for i in range(2)]
                        vst = [S.sbuf("vst%d" % i, [P, 4, 256], F32) for i in range(2)]
                        vs2 = [S.sbuf("vs2%d" % i, [P, 4, 256], F32) for i in range(2)]
                        i = 0
                        for h in range(2):
                            for (cq, csw, dst) in ((6 + h, 8 + h, Q[h]), (10 + h, 12 + h, K[h])):
                                for c0 in range(0, L, 512):
                                    ta, tb = a[i % 2], b[i % 2]
                                    load_fm("sp", ta[:], a2[i % 2][:], cq, 1, "lat", c0, 512, 0)
                                    load_fm("pool", tb[:], b2[i % 2][:], csw, 1, "lat", c0, 512, 0)
                                    i += 1
                                    S.tt(ta[:, 0, :], ta[:, 0, :], cosb[:, c0:c0 + 512], ALU.mult)
                                    S.tt(tb[:, 0, :], tb[:, 0, :], sinb[:, c0:c0 + 512], ALU.mult, e="pool")
                                    S.tt(dst[:, c0:c0 + 512], ta[:, 0, :], tb[:, 0, :], ALU.add)
                            ta = a[i % 2]
                            load_fm("sp", ta[:, :, :LC], a2[i % 2][:, :, :LC], 10 + h, 1, "ctx", 0, LC, 0)
                            i += 1
                            S.copy(K[h][:, L:LK], ta[:, 0, :LC])
                            ta = a[i % 2]
                            load_fm("sp", ta[:, :, :LC], a2[i % 2][:, :, :LC], 6 + h, 1, "ctx", 0, LC, 0)
                            i += 1
                            S.copy(QC[h][:], ta[:, 0, :LC])
                        vi = 0
                        for r in range(2):
                            for bq in range(NB):
                                t, t2 = vst[vi % 2], vs2[vi % 2]
                                vi += 1
                                for g, tgt in ((0, t), (1, t2)):
                                    S.dma("sp", tgt[:, 0:2, :], dsub(PKLG.ap()[bq, r, :, g * 784:g * 784 + 256].rearrange("(c p) e -> p c e", p=P)))
                                blend(t[:, 0:2, :], t2[:, 0:2, :])
                                kc = r * NLC + 2 * bq
                                S.copy(V[:, kc:kc + 2, :], t[:, 0:2, :], e="pool")
                            t, t2 = vst[vi % 2], vs2[vi % 2]
                            vi += 1
                            for g, tgt in ((0, t), (1, t2)):
                                S.dma("sp", tgt[:, 0, :], dsub(PKCG.ap()[r, 0:P, g * 784:g * 784 + 256]))
                            blend(t[:, 0, :], t2[:, 0, :])
                            S.copy(V[:, L // P + r, :], t[:, 0, :], e="pool")
                ps_s = [[S.psum("ps_s%d%d" % (j, i), [P, 512]) for i in range(2)] for j in range(2)]
                ps_o = [S.psum("ps_o%d" % j, [P, 512]) for j in range(2)]
                ps_z = [S.psum("ps_z%d" % j, [P, 512]) for j in range(2)]
                pt = [[S.sbuf("pt%d%d" % (j, i), [P, 512], BF16) for i in range(2)] for j in range(2)]
                rz = [S.sbuf("rz%d" % j, [P, 512], F32) for j in range(2)]
                t0_ = S.sbuf("t0", [P, 512], F32)
                t1_ = S.sbuf("t1", [P, 512], F32)
                rstd = S.sbuf("rstd", [P, 512], F32)
                oo = [S.sbuf("oo%d" % i, [P, 512], F32) for i in range(2)]
                jobs = []
                for h in range(2):
                    for q0 in range(0, L, 512):
                        s_, off = lat_rc(q0)
                        jobs.append((h, Q[h][:, q0:q0 + 512], 512, 0, NKC, [(mo_dst(2 + h, 3 + h, s_, off, 512)[0], 0, 512)]))
                    jobs.append((h, QC[h][:], LC, L // P, NKC, [(mo_dst(2 + h, 3 + h, r, NT, NCX)[0], r * NCX, NCX) for r in range(2)]))
                zacc = [S.sbuf("zacc%d" % j, [P, 512], F32) for j in range(2)]
                zeng = ["dve", "dve"]
                for ji, (h, qv, n, kc0, kc1, outs) in enumerate(jobs):
                    def qk(kc):
                        for j in range(2):
                            S.matmul(ps_s[j][kc % 2][:, :n], K[h][j * 64:(j + 1) * 64, kc * P:(kc + 1) * P], qv[j * 64:(j + 1) * 64, :],
                                     start=True, stop=True)
                    qk(kc0)
                    for kc in range(kc0, kc1):
                        bi = kc % 2
                        for j in range(2):
                            S.act(pt[j][bi][:, :n], ps_s[j][bi][:, :n], AF.Exp, scale=0.125)
                        if kc + 1 < kc1:
                            qk(kc + 1)
                        for j in range(2):
                            S.matmul(ps_o[j][:, :n], V[:, kc, h * P:(h + 1) * P], pt[j][bi][:, :n], start=(kc == kc0), stop=(kc == kc1 - 1))
                            if kc == kc0:
                                S.copy(zacc[j][:, :n], pt[j][bi][:, :n], e=zeng[j])
                            else:
                                S.tt(zacc[j][:, :n], zacc[j][:, :n], pt[j][bi][:, :n], ALU.add, e=zeng[j])
                    for j in range(2):
                        S.matmul(ps_z[j][:, :n], C.ones[:], zacc[j][:, :n], start=True, stop=True)
                    for j in range(2):
                        S.recip(rz[j][:, :n], ps_z[j][:, :n])
                    S.tt(t0_[:, :n], ps_o[0][:, :n], rz[0][:, :n], ALU.mult)
                    S.tt(t1_[:, :n], ps_o[1][:, :n], rz[1][:, :n], ALU.mult)
                    S.stt(t0_[:, :n], t1_[:, :n], neglam[:, 0:1], t0_[:, :n], ALU.mult, ALU.add)
                    rms_rstd(S, C, lambda c: t0_[:, :n], n, 1, P, ps_s[0][0], rstd[:, :n])
                    o = oo[ji % 2]
                    S.tt(t1_[:, :n], t0_[:, :n], rstd[:, :n], ALU.mult)
                    S.ts(o[:, :n], t1_[:, :n], sc2[:, 0:1], ALU.mult)
                    for (oap, o0, on) in outs:
                        S.dma("pool", dsub(oap), o[:, o0:o0 + on])

        def ph_Mssd(W):
            with S.scope():
                tri = {0: tri_mask(S, nc, "tri_f", "le"), 1: tri_mask(S, nc, "tri_b", "ge")}
                strict = {0: tri_mask(S, nc, "str_f", "gt"), 1: tri_mask(S, nc, "str_b", "lt")}
                cw = S.sbuf("cw", [P, 4, 5], F32)
                cb = S.sbuf("cb", [P, 4], F32)
                S.dma("sp", cw[:], W.scw[:])
                S.dma("sp", cb[:], W.scb[:])
                xs_tok = S.sbuf("xs_tok", [P, NCH, 256], F32)
                B_tok = S.sbuf("B_tok", [P, NCH, P], F32)
                BT = S.sbuf("BT", [P, LT], F32)
                CT = S.sbuf("CT", [P, LT], F32)
                dtv = S.sbuf("dtv", [P, NCH, 8], F32)
                aall = S.sbuf("aall", [P, NCH, 8], F32)
                dtb = S.sbuf("dtb", [P, 8], F32)
                aneg = S.sbuf("aneg", [P, 8], F32)
                dsk = S.sbuf("dsk", [P, 4], F32)
                with S.scope():
                    sm1 = S.sbuf("sm1", [P, NCH, 16], F32)
                    sm2 = S.sbuf("sm2", [P, NCH, 16], F32)
                    load_small(sm1, sm2)
                    S.copy(dtv[:], sm1[:, :, 8:16])
                S.dma("sp", dtb[:], W.sdtb[:])
                S.dma("sp", aneg[:], W.salog[:])
                S.dma("sp", dsk[:], W.sdsk[:])
                S.tt(dtv[:], dtv[:], View(dtb, dtb.t[:].rearrange("p (o e) -> p o e", o=1).to_broadcast([P, NCH, 8])), ALU.add)
                S.act(dtv[:], dtv[:], AF.Exp)
                S.act(dtv[:], dtv[:], AF.Ln, bias=C.ones[:, 0:1])
                S.act(aneg[:], aneg[:], AF.Exp)
                S.ts(aneg[:], aneg[:], -1.0, ALU.mult)
                S.tt(aall[:], dtv[:], View(aneg, aneg.t[:].rearrange("p (o e) -> p o e", o=1).to_broadcast([P, NCH, 8])), ALU.mult)
                ps_t = [S.psum("ps_t%d" % i, [P, 512]) for i in range(2)]
                with S.scope():
                    raw = [S.sbuf("raw%d" % i, [P, 4, 516], F32) for i in range(2)]
                    raw2 = [S.sbuf("rawb", [P, 4, 516], F32)] * 2
                    acc = [S.sbuf("acc%d" % i, [P, 512], F32) for i in range(2)]
                    xsT = [S.sbuf("xsT%d" % i, [P, 512], F32) for i in range(2)]
                    segs = [("ctx", 0, LC), ("lat", LC, L)]
                    ti = 0
                    for (seg, base, seglen) in segs:
                        for t0 in range(0, seglen, 512):
                            n = min(512, seglen - t0)
                            r = raw[ti % 2]
                            load_fm("sp" if ti % 2 else "pool", r[:, :, :n + 4], raw2[ti % 2][:, :, :n + 4], 14, 4, seg, t0, n, 2)
                            ti += 1
                            for c in range(4):
                                a = acc[c % 2]
                                S.ts(a[:, :n], r[:, c, 0:n], cw[:, c, 0:1], ALU.mult)
                                for j in range(1, 5):
                                    S.stt(a[:, :n], r[:, c, j:j + n], cw[:, c, j:j + 1], a[:, :n], ALU.mult, ALU.add)
                                g0 = base + t0
                                if c < 2:
                                    S.act(xsT[c][:, :n], a[:, :n], AF.Silu, bias=cb[:, c:c + 1])
                                elif c == 2:
                                    S.act(BT[:, g0:g0 + n], a[:, :n], AF.Silu, bias=cb[:, c:c + 1])
                                else:
                                    S.act(CT[:, g0:g0 + n], a[:, :n], AF.Silu, bias=cb[:, c:c + 1])
                            for bl in range(n // P):
                                gc = (base + t0) // P + bl
                                pt = ps_t[bl % 2]
                                S.transpose(pt[:, 0:P], xsT[0][:, bl * P:(bl + 1) * P], C.ident[:])
                                S.transpose(pt[:, P:2 * P], xsT[1][:, bl * P:(bl + 1) * P], C.ident[:])
                                S.transpose(pt[:, 2 * P:3 * P], BT[:, gc * P:(gc + 1) * P], C.ident[:])
                                S.copy(xs_tok[:, gc, :], pt[:, 0:2 * P], e="dve")
                                S.copy(B_tok[:, gc, :], pt[:, 2 * P:3 * P], e="dve")
                ps_arg = [S.psum("ps_arg%d" % i, [P, 512]) for i in range(2)]
                ps_cb = S.psum("ps_cb", [P, 512])
                ps_yd = ps_t
                ps_std = [S.psum("ps_st%d" % i, [P, 512]) for i in range(2)]
                ps_sm = S.psum("ps_sm", [P, 512])
                X = [S.sbuf("X%d" % i, [P, 4, P], F32) for i in range(2)]
                LTt = [S.sbuf("LT%d" % i, [P, 4, P], F32) for i in range(2)]
                CBm = [S.sbuf("CBm%d" % i, [P, P], F32) for i in range(2)]
                scT = [S.sbuf("scT%d" % i, [P, 4, P], F32) for i in range(2)]
                sm = [S.sbuf("sm%d" % i, [P, 8], F32) for i in range(2)]
                eacs = [S.sbuf("eacs%d" % i, [P, 4], F32) for i in range(2)]
                edec = [S.sbuf("edec%d" % i, [P, 4], F32) for i in range(2)]
                etot = [S.sbuf("etot%d" % i, [P, 4], F32) for i in range(2)]
                dif = [S.sbuf("dif%d" % i, [P, 4], F32) for i in range(2)]
                xdt = [S.sbuf("xdt%d" % i, [P, 4, 64], F32) for i in range(2)]
                xdtd = [S.sbuf("xdtd%d" % i, [P, 4, 64], F32) for i in range(2)]
                STd = [S.sbuf("ST%d" % i, [P, 4, 64], F32) for i in range(2)]
                yt = [[S.sbuf("yt%d%d" % (d_, i), [P, 256], F32) for i in range(2)] for d_ in range(2)]
                y2 = [S.sbuf("y2%d" % i, [P, 256], F32) for i in range(2)]
                yfl = [S.sbuf("yfl%d" % i, [P, 256], F32) for i in range(2)]
                ybl = [S.sbuf("ybl%d" % i, [P, 256], F32) for i in range(2)]
                zt = [S.sbuf("zt%d" % i, [P, 256], F32) for i in range(2)]
                zt2 = [S.sbuf("ztb%d" % i, [P, 256], F32) for i in range(2)]
                oT = [S.sbuf("oT%d" % i, [P, 2, P], F32) for i in range(2)]
                yfb = [[S.sub("yf%d_%d" % (d_, c), OF.t[c][:, d_ * P:(d_ + 1) * P] if False else OFD[d_].t[c]) for c in range(NCH)] for d_ in range(2)]

                def bc4(v):
                    return View(v.buf, v.ap.rearrange("p (h o) -> p h o", o=1).to_broadcast([P, 4, 64]))

                def scan(dr):
                    order = list(range(NCH)) if dr == 0 else (list(range(NCC - 1, -1, -1)) + list(range(NCH - 1, NCC - 1, -1)))
                    ST = STd[dr]
                    ps_y = ps_yd[dr]
                    S.memset(ST[:], 0.0)
                    for it, c in enumerate(order):
                        a4 = aall[:, c, dr * 4:(dr + 1) * 4]
                        for h in range(4):
                            S.ts(X[dr][:, h, :], strict[dr][:], aall[:, c, dr * 4 + h:dr * 4 + h + 1], ALU.mult)
                        for h in range(4):
                            S.matmul(ps_arg[dr][:, h * P:(h + 1) * P], X[dr][:, h, :], tri[dr][:])
                        yield
                        S.act(LTt[dr][:].rearrange("p h l -> p (h l)"), ps_arg[dr][:], AF.Exp)
                        S.matmul(ps_cb[:, 0:P], BT[:, c * P:(c + 1) * P], CT[:, c * P:(c + 1) * P])
                        S.matmul(ps_sm[:, 0:4], tri[dr][:], a4)
                        S.matmul(ps_sm[:, 4:8], C.ones[:], a4)
                        S.tt(CBm[dr][:], ps_cb[:, 0:P], tri[dr][:], ALU.mult)
                        S.copy(sm[dr][:], ps_sm[:, 0:8])
                        yield
                        S.tt(scT[dr][:], LTt[dr][:], View(CBm[dr], CBm[dr].t[:].rearrange("p (o l) -> p o l", o=1).to_broadcast([P, 4, P])), ALU.mult)
                        S.act(eacs[dr][:], sm[dr][:, 0:4], AF.Exp)
                        S.act(etot[dr][:], sm[dr][:, 4:8], AF.Exp)
                        S.tt(dif[dr][:], sm[dr][:, 4:8], sm[dr][:, 0:4], ALU.subtract)
                        S.act(edec[dr][:], dif[dr][:], AF.Exp)
                        xv = xs_tok[:, c, :].rearrange("p (h d) -> p h d", h=4)
                        S.tt(xdt[dr][:], xv, bc4(dtv[:, c, dr * 4:(dr + 1) * 4]), ALU.mult)
                        S.tt(xdtd[dr][:], xdt[dr][:], bc4(edec[dr][:]), ALU.mult)
                        yield
                        for h in range(4):
                            S.matmul(ps_y[:, h * 64:(h + 1) * 64], scT[dr][:, h, :], xdt[dr][:, h, :])
                        S.matmul(ps_y[:, 256:512], CT[:, c * P:(c + 1) * P], ST[:].rearrange("p h d -> p (h d)"))
                        S.matmul(ps_std[dr][:, 0:256], B_tok[:, c, :], xdtd[dr][:].rearrange("p h d -> p (h d)"))
                        yield
                        y = yt[dr][it % 2]
                        S.tt(y[:].rearrange("p (h d) -> p h d", h=4), ps_y[:, 256:512].rearrange("p (h d) -> p h d", h=4), bc4(eacs[dr][:]), ALU.mult)
                        S.tt(y[:], y[:], ps_y[:, 0:256], ALU.add)
                        S.tt(ST[:], ST[:], bc4(etot[dr][:]), ALU.mult)
                        S.tt(ST[:].rearrange("p h d -> p (h d)"), ST[:].rearrange("p h d -> p (h d)"), ps_std[dr][:, 0:256], ALU.add)
                        S.dma("pool", yfb[dr][c][:], y[:])
                        yield

                gens = [scan(0), scan(1)]
                alive = [True, True]
                while any(alive):
                    for d_ in range(2):
                        if alive[d_]:
                            try:
                                next(gens[d_])
                            except StopIteration:
                                alive[d_] = False
                for c in range(NCH):
                    bi = c % 2
                    S.dma("sp", yfl[bi][:], yfb[0][c][:])
                    S.dma("sp", ybl[bi][:], yfb[1][c][:])
                    load_tok("sp", zt[bi][:], zt2[bi][:], c, 512, 256)
                    o = y2[bi]
                    xv = xs_tok[:, c, :].rearrange("p (h d) -> p h d", h=4)
                    S.tt(o[:].rearrange("p (h d) -> p h d", h=4), xv, bc4(dsk[:]), ALU.mult)
                    S.tt(yfl[bi][:], yfl[bi][:], ybl[bi][:], ALU.add)
                    S.tt(o[:], o[:], yfl[bi][:], ALU.add)
                    S.act(zt[bi][:], zt[bi][:], AF.Silu)
                    S.tt(o[:], o[:], zt[bi][:], ALU.mult)
                    S.transpose(ps_cb[:, P:2 * P], o[:, 0:P], C.ident[:])
                    S.transpose(ps_cb[:, 2 * P:3 * P], o[:, P:2 * P], C.ident[:])
                    S.copy(oT[bi][:].rearrange("p a b -> p (a b)"), ps_cb[:, P:3 * P])
                    s_, off = tokpos(c)
                    S.dma("pool", dsub(mo_dst(4, 6, s_, off, P).rearrange("c p t -> p c t")), oT[bi][:])

        def ph_Mgdn(W):
            with S.scope():
                M = {k: tri_mask(S, nc, "m_" + k, k, blk=64) for k in ("le", "ge", "gt", "lt")}
                halfA = S.sbuf("halfA", [P, P], F32)
                halfB = S.sbuf("halfB", [P, P], F32)
                S.memset(halfA[:], 0.0)
                S.memset(halfB[:], 0.0)
                S.memset(halfA[0:64, :], 1.0)
                S.memset(halfB[64:128, :], 1.0)
                cw = S.sbuf("cw", [P, 6, 5], F32)
                S.dma("sp", cw[:], W.gcw[:])
                nw = S.sbuf("nw", [P, P], F32)
                S.dma("sp", nw[:], W.gnw[:])
                gall = S.sbuf("gall", [P, NCH, 4], F32)
                ball = S.sbuf("ball", [P, NCH, 4], F32)
                negb = S.sbuf("negb", [P, NCH, 4], F32)
                aneg = S.sbuf("aneg", [P, 4], F32)
                dtb = S.sbuf("dtb", [P, 4], F32)
                with S.scope():
                    sm1 = S.sbuf("sm1", [P, NCH, 16], F32)
                    sm2 = S.sbuf("sm2", [P, NCH, 16], F32)
                    load_small(sm1, sm2)
                    S.copy(gall[:], sm1[:, :, 0:4])
                    S.copy(ball[:], sm1[:, :, 4:8])
                S.dma("sp", aneg[:], W.galog[:])
                S.dma("sp", dtb[:], W.gdtb[:])

                def bcn(v):
                    return View(v.buf, v.ap.rearrange("p (o e) -> p o e", o=1).to_broadcast([P, NCH, 4]))
                S.tt(gall[:], gall[:], bcn(dtb[:]), ALU.add)
                S.act(gall[:], gall[:], AF.Exp)
                S.act(gall[:], gall[:], AF.Ln, bias=C.ones[:, 0:1])
                S.act(aneg[:], aneg[:], AF.Exp)
                S.ts(aneg[:], aneg[:], -1.0, ALU.mult)
                S.tt(gall[:], gall[:], bcn(aneg[:]), ALU.mult)
                S.act(ball[:], ball[:], AF.Sigmoid)
                S.ts(negb[:], ball[:], -1.0, ALU.mult)
                BA = [S.psum("BA%d" % h, [P, 512]) for h in range(2)]
                B1 = [S.psum("B1%d" % h, [P, 512]) for h in range(2)]
                B2 = [S.psum("B2%d" % h, [P, 512]) for h in range(2)]
                B3 = [S.psum("B3%d" % h, [P, 512]) for h in range(2)]
                Wk = []
                for h in range(2):
                    Wn = NS()
                    for nm in ("X", "Dm", "Dv", "Ds", "kbg", "kdec", "vb", "vnew", "oq", "o", "of_", "zt", "zt2", "t1"):
                        setattr(Wn, nm, S.sbuf("%s%d" % (nm, h), [P, P], F32))
                    for nm in ("NA", "RA", "uw"):
                        setattr(Wn, nm, S.sbuf("%s%d" % (nm, h), [P, 2 * P], F32))
                    Wn.NR = [S.sbuf("NR%d%d" % (h, i), [P, 2 * P], F32) for i in range(2)]
                    Wn.Xc = [S.sbuf("Xc%d%d" % (h, i), [P, P], F32) for i in range(2)]
                    Wn.esm = S.sbuf("esm%d" % h, [P, 4], F32)
                    Wn.bg = S.sbuf("bg%d" % h, [P, 1], F32)
                    Wn.ss = S.sbuf("ss%d" % h, [P, 1], F32)
                    Wn.oo = [S.sbuf("oo%d%d" % (h, i), [P, P], F32) for i in range(2)]
                    Wn.oT = [S.sbuf("oT%d%d" % (h, i), [P, P], F32) for i in range(2)]
                    Wk.append(Wn)
                state = [S.sbuf("state%d" % h, [P, P], F32) for h in range(2)]
                raw = S.sbuf("raw", [P, 6, 516], F32)
                raw2 = S.sbuf("rawb", [P, 6, 516], F32)
                acc = [S.sbuf("acc%d" % i, [P, 512], F32) for i in range(2)]
                sqb = S.sbuf("sqb", [P, 512], F32)
                lnb = S.sbuf("lnb", [P, 512], F32)
                rsb = S.sbuf("rsb", [P, 512], F32)
                qkv = [S.sbuf("qkv%d" % i, [P, 6, 512], F32) for i in range(2)]
                ofb = [[S.sub("of%d_%d" % (c, h), OF.t[c][:, h * P:(h + 1) * P]) for h in range(2)] for c in range(NCH)]

                def prep(seg, t0, n, dst, q):
                    load_fm(q, raw[:, :, :n + 4], raw2[:, :, :n + 4], 0, 6, seg, t0, n, 2)
                    for c in range(6):
                        a = acc[c % 2]
                        S.ts(a[:, :n], raw[:, c, 0:n], cw[:, c, 0:1], ALU.mult)
                        for j in range(1, 5):
                            S.stt(a[:, :n], raw[:, c, j:j + n], cw[:, c, j:j + 1], a[:, :n], ALU.mult, ALU.add)
                        if c >= 4:
                            S.act(dst[:, c, :n], a[:, :n], AF.Silu)
                        else:
                            S.act(a[:, :n], a[:, :n], AF.Silu)
                            S.act(sqb[:, :n], a[:, :n], AF.Square)
                            pb = B3[c % 2]
                            S.matmul(pb[:, :n], C.ones[:], sqb[:, :n])
                            S.act(lnb[:, :n], pb[:, :n], AF.Ln, bias=C.eps[:, 0:1])
                            S.act(rsb[:, :n], lnb[:, :n], AF.Exp, scale=-0.5)
                            S.stt(dst[:, c, :n], a[:, :n], (128.0 ** -0.5) if c < 2 else 1.0, rsb[:, :n], ALU.mult, ALU.mult)

                def unit(hl, dr, gp, qv, kv, vv):
                    col = dr * 2 + hl
                    g = gall[:, gp, col:col + 1]
                    nb = negb[:, gp, col:col + 1]
                    bt = ball[:, gp, col:col + 1]
                    Wn = Wk[hl]
                    bA, b1, b2, b3 = BA[hl], B1[hl], B2[hl], B3[hl]
                    Tri, Xm, Val, SVal = (M["le"], M["gt"], M["ge"], M["gt"]) if dr == 0 else (M["ge"], M["lt"], M["le"], M["lt"])
                    S.ts(Wn.X[:], Xm[:], g, ALU.mult)
                    S.matmul(bA[:, 0:128], Tri[:], Wn.X[:])
                    S.matmul(bA[:, 128:129], Tri[:], g)
                    S.matmul(bA[:, 129:130], Xm[:], g)
                    S.matmul(bA[:, 130:131], halfA[:], g)
                    S.matmul(bA[:, 131:132], halfB[:], g)
                    S.matmul(b1[:, 0:128], kv, kv)
                    S.matmul(b1[:, 128:256], qv, kv)
                    S.transpose(bA[:, 256:384], kv, C.ident[:])
                    S.transpose(bA[:, 384:512], vv, C.ident[:])
                    yield
                    S.act(Wn.Dm[:], bA[:, 0:128], AF.Exp)
                    S.act(Wn.esm[:], bA[:, 128:132], AF.Exp)
                    S.tt(Wn.bg[:], Wn.esm[:, 0:1], bt, ALU.mult)
                    S.act(Wn.kdec[:], bA[:, 256:384], AF.Identity, scale=Wn.esm[:, 1:2])
                    S.act(Wn.vb[:], bA[:, 384:512], AF.Identity, scale=bt)
                    S.act(Wn.kbg[:], bA[:, 256:384], AF.Identity, scale=Wn.bg[:, 0:1])
                    S.tt(Wn.Dv[:], Wn.Dm[:], Val[:], ALU.mult)
                    S.tt(Wn.Ds[:], Wn.Dm[:], SVal[:], ALU.mult)
                    S.stt(Wn.NA[:, 0:128], b1[:, 0:128], nb, Wn.Ds[:], ALU.mult, ALU.mult)
                    S.tt(Wn.NA[:, 128:256], b1[:, 128:256], Wn.Dv[:], ALU.mult)
                    yield
                    S.transpose(b1[:, 256:384], Wn.NA[:, 0:128], C.ident[:])
                    S.transpose(b1[:, 384:512], Wn.NA[:, 128:256], C.ident[:])
                    S.copy(Wn.RA[:], b1[:, 256:512])
                    X = Wn.Xc[0]
                    S.tt(X[:], Wn.RA[:, 0:128], C.ident[:], ALU.add)
                    yield
                    Ncur = Wn.NA[:, 0:128]
                    Rcur = Wn.RA[:, 0:128]
                    for lev in range(5):
                        NR = Wn.NR[lev % 2]
                        S.matmul(b2[:, 0:128], Rcur, Ncur)
                        if lev < 4:
                            S.matmul(b2[:, 128:256], Ncur, Rcur)
                            S.copy(NR[:], b2[:, 0:256])
                        else:
                            S.copy(NR[:, 0:128], b2[:, 0:128])
                        yield
                        S.matmul(b2[:, 256:384], NR[:, 0:128], X[:])
                        Xn = Wn.Xc[(lev + 1) % 2]
                        S.tt(Xn[:], X[:], b2[:, 256:384], ALU.add)
                        X = Xn
                        Ncur = NR[:, 0:128]
                        Rcur = NR[:, 128:256]
                        yield
                    S.matmul(b3[:, 0:128], X[:], Wn.vb[:])
                    S.matmul(b3[:, 128:256], Wn.kbg[:], X[:])
                    S.copy(Wn.uw[:], b3[:, 0:256])
                    yield
                    blocks = [(0, 64), (64, 128)] if dr == 0 else [(64, 128), (0, 64)]
                    Sst = state[hl]
                    for bi, (r0, r1) in enumerate(blocks):
                        reg = b3[:, 256:512] if bi == 0 else b3[:, 0:256]
                        S.matmul(reg[:, 0:128], Wn.uw[:, 128:256], Sst[:])
                        S.matmul(reg[:, 128:256], qv, Sst[:])
                        S.tt(Wn.vnew[r0:r1, :], Wn.uw[r0:r1, 0:128], reg[r0:r1, 0:128], ALU.subtract)
                        S.ts(Wn.oq[r0:r1, :], reg[r0:r1, 128:256], Wn.esm[r0:r1, 0:1], ALU.mult)
                        yield
                        S.matmul(b1[:, 0:128], Wn.kdec[r0:r1, :], Wn.vnew[r0:r1, :])
                        egX = Wn.esm[:, 2:3] if r0 == 0 else Wn.esm[:, 3:4]
                        S.stt(Sst[:], Sst[:], egX, b1[:, 0:128], ALU.mult, ALU.add)
                        yield
                    S.matmul(b1[:, 128:256], Wn.RA[:, 128:256], Wn.vnew[:])
                    S.tt(Wn.o[:], Wn.oq[:], b1[:, 128:256], ALU.add)
                    if dr == 0:
                        S.dma("pool", ofb[gp][hl][:], Wn.o[:])
                    else:
                        S.dma("sp", Wn.of_[:], ofb[gp][hl][:])
                        load_tok("sp", Wn.zt[:], Wn.zt2[:], gp, 256 + hl * P, P)
                        S.tt(Wn.o[:], Wn.o[:], Wn.of_[:], ALU.add)
                        S.act(Wn.t1[:], Wn.o[:], AF.Square, accum_out=Wn.ss[:, 0:1])
                        yield
                        S.act(Wn.ss[:], Wn.ss[:], AF.Ln, scale=1.0 / 128.0, bias=C.eps[:, 0:1])
                        S.act(Wn.ss[:], Wn.ss[:], AF.Exp, scale=-0.5)
                        S.act(Wn.zt[:], Wn.zt[:], AF.Silu)
                        S.stt(Wn.t1[:], Wn.o[:], Wn.ss[:, 0:1], nw[:], ALU.mult, ALU.mult)
                        oo = Wn.oo[gp % 2]
                        S.tt(oo[:], Wn.t1[:], Wn.zt[:], ALU.mult)
                        S.transpose(b1[:, 256:384], oo[:], C.ident[:])
                        oT = Wn.oT[gp % 2]
                        S.copy(oT[:], b1[:, 256:384])
                        s_, off = tokpos(gp)
                        S.dma("pool", dsub(mo_dst(hl, hl + 1, s_, off, P)[0]), oT[:])
                    yield

                for dr in range(2):
                    for h in range(2):
                        S.memset(state[h][:], 0.0)
                    segs = [("ctx", 0, LC), ("lat", LC, L)]
                    tl = []
                    for (seg, base, seglen) in segs:
                        tt_ = [(seg, base, t0, min(512, seglen - t0)) for t0 in range(0, seglen, 512)]
                        if dr == 1:
                            tt_ = tt_[::-1]
                        tl += tt_
                    for ti, (seg, base, t0, n) in enumerate(tl):
                        dst = qkv[ti % 2]
                        prep(seg, t0, n, dst, "sp" if ti % 2 else "pool")
                        prs = list(range(n // P))
                        if dr == 1:
                            prs = prs[::-1]
                        for pi in prs:
                            gp = (base + t0) // P + pi
                            sl = slice(pi * P, (pi + 1) * P)
                            gens = [unit(h, dr, gp, dst[:, 0 + h, sl], dst[:, 2 + h, sl], dst[:, 4 + h, sl]) for h in range(2)]
                            alive = [True, True]
                            while any(alive):
                                for h in range(2):
                                    if alive[h]:
                                        try:
                                            next(gens[h])
                                        except StopIteration:
                                            alive[h] = False

        def ph_R2(W, x1t, xout):
            with S.scope():
                alloc_small(S, C)
                PS = mk_ps()
                mods = S.sbuf("mods", [P, 48, 2], F32)
                ngt = S.sbuf("ngt", [P, 6, KC], F32)
                S.dma("sp", ngt[:], W.ng[:].rearrange("p (m c) -> p m c", c=KC))
                snt = S.sbuf("snt", [P, 4], F32)
                S.dma("sp", snt[:], W.snw[:])
                with S.scope():
                    stg = [S.sbuf("stgm%d" % i, [P, 6 * D], F32) for i in range(KC)]
                    compute_mods(S, C, cv, W.wada2, W.bada2, 6, stg, PS.g[0], mods)
                A2 = S.sbuf("A2", [P, KC, 2], F32)
                G3 = S.sbuf("G3", [P, KC, 2], F32)
                A4 = S.sbuf("A4", [P, KC, 2], F32)
                G5 = S.sbuf("G5", [P, KC, 2], F32)
                S.stt(A2[:], mods[:, 8:16, :], 1.0, bc(ngt[:, 2, :]), ALU.add, ALU.mult)
                S.tt(G3[:], mods[:, 16:24, :], bc(ngt[:, 3, :]), ALU.mult)
                S.stt(A4[:], mods[:, 32:40, :], 1.0, bc(ngt[:, 4, :]), ALU.add, ALU.mult)
                S.stt(G5[:], mods[:, 40:48, :], 0.5, bc(ngt[:, 5, :]), ALU.mult, ALU.mult)
                B2 = mods[:, 0:8, :]
                B4 = mods[:, 24:32, :]
                x2t = [S.sub("x2t%d" % j, X2.t[:, :, s0:s0 + n]) for j, (s0, n, col) in enumerate(tiles)]
                with S.scope():
                    wgb = S.sbuf("wgb", [P, KC, 3 * D], BF16)
                    wbb = S.sbuf("wbb", [P, 12, D], BF16)
                    wob = S.sbuf("wob", [P, KC, D], BF16)
                    with S.scope():
                        stages = [S.sbuf("wst%d" % i, [P, 2048], F32) for i in range(3)]
                        load_w(S, W.wg, wgb, D, 3 * D, stages)
                        load_w(S, W.wb, wbb, 1536, D, stages)
                        load_w(S, W.wo, wob, D, D, stages)
                    xt = S.sbuf("xm", [P, KC, 512], F32)
                    h = S.sbuf("hm", [P, KC, 512], BF16)
                    ost = S.sbuf("ostg", [P, 4, 512], F32)
                    ost2 = S.sbuf("ostg2", [P, 4, 512], F32)
                    ob16 = S.sbuf("ob16", [P, 12, 512], BF16)
                    yacc = S.sbuf("yacc", [P, 512], F32)
                    ybf = S.sbuf("ybf", [P, KC, 512], BF16)
                    yy = S.sbuf("yy", [P, KC, 512], F32)
                    gt = [S.sbuf("gt%d" % i, [P, 512], F32) for i in range(2)]
                    for j, (s0, n, col) in enumerate(tiles):
                        S.dma("sp", xt[:, :, :n], x1t[j][:])
                        norm_mod(S, C, xt, n, A2[:], B2, col, h, PS.ss)
                        for br in range(3):
                            for sc_, tgt in ((0, ost), (1, ost2)):
                                for r in range(2):
                                    if col == 0:
                                        sap = MOLG.ap()[2 * br:2 * br + 2, sc_, r, :, s0:s0 + n]
                                    else:
                                        sap = MOCG.ap()[r, 2 * br:2 * br + 2, sc_, :, 0:n]
                                    S.dma("sp" if r else "pool", tgt[:, 2 * r:2 * r + 2, :n], dsub(sap.rearrange("c p t -> p c t")))
                            blend(ost[:, :, :n], ost2[:, :, :n])
                            if br < 2:
                                S.copy(ob16[:, br * 4:(br + 1) * 4, :n], ost[:, :, :n], e="pool")
                            else:
                                rms_rstd(S, C, lambda c: ost[:, c, :n], n, 4, 512, PS.ss2, C.rstd2[:, :n])
                                for c in range(4):
                                    t = C.tmp[c % 2]
                                    S.tt(t[:, :n], ost[:, c, :n], C.rstd2[:, :n], ALU.mult)
                                    S.act(ob16[:, 8 + c, :n], t[:, :n], AF.Copy, scale=snt[:, c:c + 1])
                        for d in range(KC):
                            for br in range(3):
                                pg = PS.g[br % 2]
                                pu = PS.u[br % 2]
                                cg = br * KC + d
                                for k in range(KC):
                                    S.matmul(pg[:, :n], wgb[:, k, cg * P:(cg + 1) * P], h[:, k, :n], start=(k == 0), stop=(k == KC - 1))
                                for k in range(4):
                                    S.matmul(pu[:, :n], wbb[:, br * 4 + k, d * P:(d + 1) * P], ob16[:, br * 4 + k, :n], start=(k == 0), stop=(k == 3))
                                g = gt[br % 2]
                                S.act(g[:, :n], pg[:, :n], AF.Sigmoid)
                                if br == 0:
                                    S.tt(yacc[:, :n], g[:, :n], pu[:, :n], ALU.mult)
                                else:
                                    t = C.tmp[br % 2]
                                    S.tt(t[:, :n], g[:, :n], pu[:, :n], ALU.mult)
                                    if br == 1:
                                        S.tt(yacc[:, :n], yacc[:, :n], t[:, :n], ALU.add)
                                    else:
                                        S.tt(ybf[:, d, :n], yacc[:, :n], t[:, :n], ALU.add)
                        for d in range(KC):
                            py = PS.y[d % 2]
                            for k in range(KC):
                                S.matmul(py[:, :n], wob[:, k, d * P:(d + 1) * P], ybf[:, k, :n], start=(k == 0), stop=(k == KC - 1))
                            S.copy(yy[:, d, :n], py[:, :n], e="dve")
                        rms_rstd(S, C, lambda c: yy[:, c, :n], n, KC, D, PS.ss2, C.rstd2[:, :n])
                        for c in range(KC):
                            t = C.tmp[c % 2]
                            S.tt(t[:, :n], yy[:, c, :n], C.rstd2[:, :n], ALU.mult)
                            S.stt(xt[:, c, :n], t[:, :n], G3[:, c, col:col + 1], xt[:, c, :n], ALU.mult, ALU.add)
                        S.dma("pool", x2t[j][:], xt[:, :, :n])
                with S.scope():
                    w1b = S.sbuf("w1b", [P, KC, 2 * DFF], BF16)
                    w2b = S.sbuf("w2b", [P, FC, D], BF16)
                    with S.scope():
                        stages = [S.sbuf("wst%d" % i, [P, 2048], F32) for i in range(3)]
                        load_w(S, W.w1b, w1b, D, 2 * DFF, stages)
                        load_w(S, W.w2b, w2b, DFF, D, stages)
                    alloc_ffn_work(S, C)
                    ffn_sweep(S, C, tiles, lambda j: x2t[j][:], lambda j: xout[j][:], w1b, w2b, A4[:], B4, G5[:], PS)

        xin = [S.sub("xin%d" % j, xT.t[:, :, s0:s0 + n]) for j, (s0, n, col) in enumerate(tiles)]
        for i in range(depth):
            W = Wl[i]
            x1t = [S.sub("x1t%d_%d" % (i, j), X1.t[:, :, s0:s0 + n]) for j, (s0, n, col) in enumerate(tiles)]
            dbg = "Z"
            ph_R1(W, xin, x1t)
            if dbg >= "B":
                gather_P()
            if dbg >= "C":
                ph_Mdiff(W)
            if dbg >= "D":
                ph_Mssd(W)
            if dbg >= "E":
                ph_Mgdn(W)
            if dbg >= "F":
                gather_M()
            last = i == depth - 1
            dstT = yT if last else Xs
            xout = [S.sub("xo%d_%d" % (i, j), dstT.t[:, :, s0:s0 + n]) for j, (s0, n, col) in enumerate(tiles)]
            ph_R2(W, x1t, xout)
            xin = xout
        S.barrier()
    return nc


def fm(a):
    T, F = a.shape
    return np.ascontiguousarray(a.T.reshape(F // P, P, T).transpose(1, 0, 2))

def unfm(a):
    p, C, T = a.shape
    return np.ascontiguousarray(a.transpose(2, 1, 0).reshape(T, C * P))

def vec_fm(v):
    return np.ascontiguousarray(v.reshape(-1, P).T)

def r1_cols():
    sw = np.arange(512) ^ 1
    cols = []
    cols += list(range(0, 2048))
    cols += list(range(2064, 2576))
    cols += list(2064 + sw)
    cols += list(range(2576, 3088))
    cols += list(2576 + sw)
    cols += list(range(3088, 3600))
    cols += list(range(3600, 5136))
    cols += list(range(2048, 2064)) + list(range(5136, 5152)) + [0] * 96
    cols = np.array(cols)
    assert len(cols) == 49 * 128
    return cols

def r1_inputs(inp, i, core, NT, NCX, xcur, ctxcur):
    b, s = core // 2, core % 2
    tok = np.concatenate([xcur[b, s * NT:(s + 1) * NT], ctxcur[b, s * NCX:(s + 1) * NCX]], 0)
    cv = np.stack([inp["c"][b], inp["c_ctx"]], -1)
    cv = np.ascontiguousarray(cv.reshape(8, P, 2).transpose(1, 0, 2))
    return {
        "xT": fm(tok),
        "cv": cv,
        "wada": np.ascontiguousarray(inp["w_ada"][i][:, :5 * 1024]),
        "bada": vec_fm(inp["b_ada"][i][:5 * 1024]),
        "ng": np.ascontiguousarray(inp["norm_g"][i].reshape(6, 8, P).transpose(2, 0, 1).reshape(P, 48)),
        "w1": np.ascontiguousarray(inp["w_ffn_in"][i, 0]),
        "w2": np.ascontiguousarray(inp["w_ffn_out"][i, 0]),
        "win": np.ascontiguousarray(inp["w_in"][i][:, r1_cols()]),
    }

def r2_inputs(inp, i, core, x1T, oaT, obT, ocT):
    b = core // 2
    cv = np.stack([inp["c"][b], inp["c_ctx"]], -1)
    cv = np.ascontiguousarray(cv.reshape(8, P, 2).transpose(1, 0, 2))
    return {
        "x1T": x1T, "oaT": oaT, "obT": obT, "ocT": ocT, "cv": cv,
        "wada": np.ascontiguousarray(inp["w_ada"][i][:, 3 * 1024:]),
        "bada": vec_fm(inp["b_ada"][i][3 * 1024:]),
        "ng": np.ascontiguousarray(inp["norm_g"][i].reshape(6, 8, P).transpose(2, 0, 1).reshape(P, 48)),
        "snw": vec_fm(inp["ssd_norm_w"][i]),
        "wg": np.ascontiguousarray(inp["w_in"][i][:, 5152:8224]),
        "wb": np.ascontiguousarray(inp["w_branch"][i].reshape(1536, 1024)),
        "wo": np.ascontiguousarray(inp["w_out"][i]),
        "w1": np.ascontiguousarray(inp["w_ffn_in"][i, 1]),
        "w2": np.ascontiguousarray(inp["w_ffn_out"][i, 1]),
    }

def split_P(PT_cores, NT, NCX, b):
    a0, a1 = PT_cores[2 * b], PT_cores[2 * b + 1]
    lat = np.concatenate([a0[:, :, :NT], a1[:, :, :NT]], 2)
    cx = np.concatenate([a0[:, :, NT:], a1[:, :, NT:]], 2)
    return lat, cx

def mdiff_inputs(inp, i, core, lat, cx):
    b, hh = core // 2, core % 2
    hs = [2 * hh, 2 * hh + 1]
    L = lat.shape[2]; LC = cx.shape[2]
    qT = lat[[16 + h for h in hs]]
    qsT = lat[[20 + h for h in hs]]
    kT = np.concatenate([lat[[24 + h for h in hs]], cx[[24 + h for h in hs]]], 2)
    ksT = lat[[28 + h for h in hs]]
    qcT = cx[[16 + h for h in hs]]
    v = np.concatenate([lat[[32 + h for h in hs]], cx[[32 + h for h in hs]]], 2)
    LK = L + LC
    v = v.transpose(2, 0, 1).reshape(LK // P, P, 256).transpose(1, 0, 2)
    lam_init = 0.8 - 0.6 * np.exp(-0.3 * i)
    return {"qT": np.ascontiguousarray(qT), "qsT": np.ascontiguousarray(qsT), "kT": np.ascontiguousarray(kT),
            "ksT": np.ascontiguousarray(ksT), "qcT": np.ascontiguousarray(qcT), "v": np.ascontiguousarray(v),
            "lam": np.ascontiguousarray(np.broadcast_to(inp["diff_lambda"][i].reshape(1, 256), (P, 256))),
            "nw": np.ascontiguousarray(inp["diff_norm_w"][i].reshape(P, 1)),
            "li": np.full((P, 1), lam_init, np.float32)}

def pad2(a):
    return np.pad(a, ((0, 0), (0, 0), (2, 2)))

def tokmaj(a):
    Cc, p, T = a.shape
    return np.ascontiguousarray(a.transpose(2, 0, 1).reshape(T // P, P, Cc * P).transpose(1, 0, 2))

def mssd_inputs(inp, i, core, lat, cx):
    b, g = core // 2, core % 2
    ch = [40 + 2 * g, 41 + 2 * g, 44 + g, 46 + g]
    wcols = np.concatenate([np.arange(256 * g, 256 * g + 256), 512 + 128 * g + np.arange(128), 768 + 128 * g + np.arange(128)])
    cwv = inp["ssd_conv_w"][i][:, wcols]
    cbv = inp["ssd_conv_b"][i][wcols]
    zc = [36 + 2 * g, 37 + 2 * g]
    z = np.concatenate([tokmaj(cx[zc]), tokmaj(lat[zc])], 1)
    rows = [16 + d * 8 + 4 * g + h for d in range(2) for h in range(4)]
    dtl = lat[48][rows]; dtc = cx[48][rows]
    dt = np.concatenate([dtc, dtl], 1)
    LT = dt.shape[1]
    dt = np.ascontiguousarray(dt.T.reshape(LT // P, P, 8).transpose(1, 0, 2))
    hsel = [4 * g + h for h in range(4)]
    return {"xbcl": np.ascontiguousarray(pad2(lat[ch])), "xbcc": np.ascontiguousarray(pad2(cx[ch])),
            "cw": np.ascontiguousarray(cwv.T.reshape(4, P, 5).transpose(1, 0, 2)),
            "cb": np.ascontiguousarray(cbv.reshape(4, P).T),
            "z": z, "dt": dt,
            "dtb": np.ascontiguousarray(np.broadcast_to(inp["ssd_dt_bias"][i][:, hsel].reshape(1, 8), (P, 8))),
            "alog": np.ascontiguousarray(np.broadcast_to(inp["ssd_a_log"][i][:, hsel].reshape(1, 8), (P, 8))),
            "dskip": np.ascontiguousarray(np.broadcast_to(inp["ssd_d"][i][hsel].reshape(1, 4), (P, 4)))}

def mgdn_inputs(inp, i, core, lat, cx):
    b, hh = core // 2, core % 2
    hs = [2 * hh, 2 * hh + 1]
    ch = [0 + hs[0], 0 + hs[1], 4 + hs[0], 4 + hs[1], 8 + hs[0], 8 + hs[1]]
    wcols = np.concatenate([off + h * 128 + np.arange(128) for off in (0, 512, 1024) for h in hs])
    cwv = inp["gdn_conv_w"][i][:, wcols]
    zc = [12 + hs[0], 12 + hs[1]]
    z = np.concatenate([tokmaj(cx[zc]), tokmaj(lat[zc])], 1)
    def small(rows):
        v = np.concatenate([cx[48][rows], lat[48][rows]], 1)
        LT = v.shape[1]
        return np.ascontiguousarray(v.T.reshape(LT // P, P, 4).transpose(1, 0, 2))
    arows = [d * 4 + h for d in range(2) for h in hs]
    brows = [8 + d * 4 + h for d in range(2) for h in hs]
    return {"qkvl": np.ascontiguousarray(pad2(lat[ch])), "qkvc": np.ascontiguousarray(pad2(cx[ch])),
            "cw": np.ascontiguousarray(cwv.T.reshape(6, P, 5).transpose(1, 0, 2)),
            "z": z, "araw": small(arows), "braw": small(brows),
            "alog": np.ascontiguousarray(np.broadcast_to(inp["gdn_a_log"][i][:, hs].reshape(1, 4), (P, 4))),
            "dtb": np.ascontiguousarray(np.broadcast_to(inp["gdn_dt_bias"][i][:, hs].reshape(1, 4), (P, 4))),
            "nw": np.ascontiguousarray(np.broadcast_to(inp["gdn_norm_w"][i].reshape(1, P), (P, P)))}


def fused_cols():
    sw = np.arange(128) ^ 1
    ar = np.arange(128)
    fmc, tkc = [], []
    for g in (0, 1):
        hs = [2 * g, 2 * g + 1]
        for off in (0, 512, 1024):
            for h in hs:
                fmc += list(off + h * 128 + ar)
        for h in hs:
            fmc += list(2064 + h * 128 + ar)
        for h in hs:
            fmc += list(2064 + h * 128 + sw)
        for h in hs:
            fmc += list(2576 + h * 128 + ar)
        for h in hs:
            fmc += list(2576 + h * 128 + sw)
        fmc += list(4112 + g * 256 + np.arange(256))
        fmc += list(4112 + 512 + g * 128 + ar)
        fmc += list(4112 + 768 + g * 128 + ar)
    for g in (0, 1):
        hs = [2 * g, 2 * g + 1]
        for h in hs:
            tkc += list(3088 + h * 128 + ar)
        for h in hs:
            tkc += list(1536 + h * 128 + ar)
        tkc += list(3600 + g * 256 + np.arange(256))
        tkc += [2048 + d * 4 + h for d in range(2) for h in hs]
        tkc += [2056 + d * 4 + h for d in range(2) for h in hs]
        tkc += [5136 + d * 8 + 4 * g + h for d in range(2) for h in range(4)]
    cols = np.array(fmc + tkc)
    assert len(cols) == 36 * 128 + 1568
    return cols


def fused_inputs(inp, core, NT, NCX, depth=2):
    b, hh = core // 2, core % 2
    s = hh
    tok = np.concatenate([inp["x"][b, s * NT:(s + 1) * NT], inp["ctx"][b, s * NCX:(s + 1) * NCX]], 0)
    cv = np.stack([inp["c"][b], inp["c_ctx"]], -1)
    cv = np.ascontiguousarray(cv.reshape(8, P, 2).transpose(1, 0, 2))
    d = {"xT": fm(tok), "cv": cv, "selv": np.ascontiguousarray(np.broadcast_to(np.array([[1.0 - hh, float(hh)]], np.float32), (P, 2)))}
    cols = fused_cols()
    hs = [2 * hh, 2 * hh + 1]
    g = hh
    for i in range(depth):
        sfx = "_%d" % i
        d["wada1" + sfx] = np.ascontiguousarray(inp["w_ada"][i][:, :5 * 1024])
        d["bada1" + sfx] = vec_fm(inp["b_ada"][i][:5 * 1024])
        d["wada2" + sfx] = np.ascontiguousarray(inp["w_ada"][i][:, 3 * 1024:])
        d["bada2" + sfx] = vec_fm(inp["b_ada"][i][3 * 1024:])
        d["ng" + sfx] = np.ascontiguousarray(inp["norm_g"][i].reshape(6, 8, P).transpose(2, 0, 1).reshape(P, 48))
        d["w1a" + sfx] = np.ascontiguousarray(inp["w_ffn_in"][i, 0])
        d["w2a" + sfx] = np.ascontiguousarray(inp["w_ffn_out"][i, 0])
        d["w1b" + sfx] = np.ascontiguousarray(inp["w_ffn_in"][i, 1])
        d["w2b" + sfx] = np.ascontiguousarray(inp["w_ffn_out"][i, 1])
        d["win" + sfx] = np.ascontiguousarray(inp["w_in"][i][:, cols])
        d["wg" + sfx] = np.ascontiguousarray(inp["w_in"][i][:, 5152:8224])
        d["wb" + sfx] = np.ascontiguousarray(inp["w_branch"][i].reshape(1536, 1024))
        d["wo" + sfx] = np.ascontiguousarray(inp["w_out"][i])
        d["snw" + sfx] = vec_fm(inp["ssd_norm_w"][i])
        lam_init = 0.8 - 0.6 * np.exp(-0.3 * i)
        d["lam" + sfx] = np.ascontiguousarray(np.broadcast_to(inp["diff_lambda"][i].reshape(1, 256), (P, 256)))
        d["dnw" + sfx] = np.ascontiguousarray(inp["diff_norm_w"][i].reshape(P, 1))
        d["li" + sfx] = np.full((P, 1), lam_init, np.float32)
        wcols = np.concatenate([np.arange(256 * g, 256 * g + 256), 512 + 128 * g + np.arange(128), 768 + 128 * g + np.arange(128)])
        d["scw" + sfx] = np.ascontiguousarray(inp["ssd_conv_w"][i][:, wcols].T.reshape(4, P, 5).transpose(1, 0, 2))
        d["scb" + sfx] = np.ascontiguousarray(inp["ssd_conv_b"][i][wcols].reshape(4, P).T)
        hsel = [4 * g + h for h in range(4)]
        d["sdtb" + sfx] = np.ascontiguousarray(np.broadcast_to(inp["ssd_dt_bias"][i][:, hsel].reshape(1, 8), (P, 8)))
        d["salog" + sfx] = np.ascontiguousarray(np.broadcast_to(inp["ssd_a_log"][i][:, hsel].reshape(1, 8), (P, 8)))
        d["sdsk" + sfx] = np.ascontiguousarray(np.broadcast_to(inp["ssd_d"][i][hsel].reshape(1, 4), (P, 4)))
        gcols = np.concatenate([off + h * 128 + np.arange(128) for off in (0, 512, 1024) for h in hs])
        d["gcw" + sfx] = np.ascontiguousarray(inp["gdn_conv_w"][i][:, gcols].T.reshape(6, P, 5).transpose(1, 0, 2))
        d["galog" + sfx] = np.ascontiguousarray(np.broadcast_to(inp["gdn_a_log"][i][:, hs].reshape(1, 4), (P, 4)))
        d["gdtb" + sfx] = np.ascontiguousarray(np.broadcast_to(inp["gdn_dt_bias"][i][:, hs].reshape(1, 4), (P, 4)))
        d["gnw" + sfx] = np.ascontiguousarray(np.broadcast_to(inp["gdn_norm_w"][i].reshape(1, P), (P, P)))
    return d


from concourse.bass_utils import run_bass_kernel_spmd


def kernel(**inp):
    inp = {k: np.ascontiguousarray(np.asarray(v), dtype=np.float32) for k, v in inp.items()}
    B, L, Dm = inp["x"].shape
    LC = inp["ctx"].shape[1]
    NT, NCX = L // 2, LC // 2
    nc = build_fused(L, LC, 2)
    ims = [fused_inputs(inp, c, NT, NCX, 2) for c in range(8)]
    res = run_bass_kernel_spmd(nc, ims, core_ids=list(range(8))).results
    out = np.empty((B, L, Dm), np.float32)
    for c in range(8):
        b, s = c // 2, c % 2
        out[b, s * NT:(s + 1) * NT] = unfm(res[c]["yT"])[:NT]
    return out
```

```python
import numpy as np
from contextlib import ExitStack, contextmanager
import concourse.bass as bass
import concourse.mybir as mybir

F32 = mybir.dt.float32
BF16 = mybir.dt.bfloat16
AF = mybir.ActivationFunctionType
ALU = mybir.AluOpType
AX = mybir.AxisListType


class Buf:
    __slots__ = ("name", "t", "w", "r", "dsem", "dcnt", "space", "dkey")

    def __init__(self, name, t, space="sbuf"):
        self.name = name
        self.space = space
        self.t = t
        self.w = {}
        self.r = {}
        self.dsem = None
        self.dcnt = 0
        self.dkey = None

    def __getitem__(self, idx):
        return View(self, self.t[idx])


class View:
    __slots__ = ("buf", "ap")

    def __init__(self, buf, ap):
        self.buf = buf
        self.ap = ap

    def __getitem__(self, idx):
        return View(self.buf, self.ap[idx])

    def rearrange(self, *a, **k):
        return View(self.buf, self.ap.rearrange(*a, **k))

    def bitcast(self, *a, **k):
        return View(self.buf, self.ap.bitcast(*a, **k))

    def to_broadcast(self, *a, **k):
        return View(self.buf, self.ap.to_broadcast(*a, **k))


class Sched:
    def __init__(self, nc, stack):
        self.nc = nc
        self.stack = stack
        self.eng = {"pe": nc.tensor, "act": nc.scalar, "dve": nc.vector, "pool": nc.gpsimd, "sp": nc.sync}
        self.sem = {}
        self.cnt = {}
        self.seen = {}
        for e in self.eng:
            self.sem[e] = stack.enter_context(nc.semaphore("prog_" + e))
            self.cnt[e] = 0
            self.seen[e] = {}
        self.semobj = {e: self.sem[e] for e in self.eng}
        self.nbuf = 0
        self.ninst = 0
        self.root = stack
        self.dbufs = []
        self.sempool = []
        self.scope_bufs = [[]]
        self.nsem = 0

    def sbuf(self, name, shape, dt):
        self.nbuf += 1
        name = "%s_%d" % (name, self.nbuf)
        t = self.stack.enter_context(self.nc.sbuf_tensor(name, list(shape), dt))
        b = Buf(name, t)
        self.scope_bufs[-1].append(b)
        return b

    def psum(self, name, shape, dt=F32):
        self.nbuf += 1
        name = "%s_%d" % (name, self.nbuf)
        t = self.stack.enter_context(self.nc.psum_tensor(name, list(shape), dt))
        return Buf(name, t, "psum")

    def dram(self, name, shape, dt, kind="Internal"):
        t = self.nc.dram_tensor(name, list(shape), dt, kind=kind)
        return Buf(name, t.ap(), "dram")

    def sub(self, name, ap, space="dram"):
        return Buf(name, ap, space)

    def barrier(self):
        deps = {e: self.cnt[e] for e in self.eng if self.cnt[e] > 0}
        for b in self.dbufs:
            if deps.get(b.dkey, 0) < b.dcnt:
                deps[b.dkey] = b.dcnt
        for e in self.eng:
            self._need(e, dict(deps))

    @contextmanager
    def scope(self):
        old = self.stack
        self.scope_bufs.append([])
        with ExitStack() as st:
            self.stack = st
            yield
            self.barrier()
        self.stack = old
        dead = self.scope_bufs.pop()
        for b in dead:
            if b.dsem is not None:
                self.sempool.append((b.dsem, b.dcnt, b.dkey))
        deadids = set(id(b) for b in dead)
        self.dbufs = [b for b in self.dbufs if id(b) not in deadids] + [b for b in dead if b.dsem is not None][:0]
        self._dead_keep = getattr(self, "_dead_keep", []) + dead

    def _need(self, e, deps):
        seen = self.seen[e]
        for k, v in deps.items():
            if e == "pe" and k == "pe":
                continue
            if seen.get(k, 0) < v:
                seen[k] = v
                self.eng[e].wait_ge(self.semobj[k], v)

    def _collect(self, reads, writes):
        deps = {}
        for v in reads:
            for k, val in v.buf.w.items():
                if deps.get(k, 0) < val:
                    deps[k] = val
        for v in writes:
            for d in (v.buf.w, v.buf.r):
                for k, val in d.items():
                    if deps.get(k, 0) < val:
                        deps[k] = val
        return deps

    def _mark(self, reads, writes, key, val):
        for v in reads:
            b = v.buf
            if b.r.get(key, 0) < val:
                b.r[key] = val
        for v in writes:
            b = v.buf
            b.w = {key: val}
            b.r = {}

    def op(self, e, fn, reads, writes):
        self._need(e, self._collect(reads, writes))
        ins = fn()
        self.cnt[e] += 1
        ins.then_inc(self.sem[e], 1)
        self._mark(reads, writes, e, self.cnt[e])
        self.ninst += 1
        return ins

    def dma(self, e, out, in_, sbuf_side=None, **kw):
        if sbuf_side is None:
            sbuf_side = out.buf if out.buf.space != "dram" else in_.buf
        b = sbuf_side
        if b.dsem is None:
            if self.sempool:
                b.dsem, b.dcnt, b.dkey = self.sempool.pop()
            else:
                self.nsem += 1
                b.dsem = self.root.enter_context(self.nc.semaphore("d_%d" % self.nsem))
                b.dkey = "d%d" % self.nsem
                self.semobj[b.dkey] = b.dsem
            self.dbufs.append(b)
        self._need(e, self._collect([in_], [out]))
        ins = self.eng[e].dma_start(out=out.ap, in_=in_.ap, **kw)
        b.dcnt += 16
        ins.then_inc(b.dsem, 16)
        self._mark([in_], [out], b.dkey, b.dcnt)
        self.ninst += 1
        return ins

    def wait_all(self, e, bufs):
        deps = {}
        for b in bufs:
            for d in (b.w, b.r):
                for k, val in d.items():
                    if deps.get(k, 0) < val:
                        deps[k] = val
        self._need(e, deps)

    def matmul(self, out, lhsT, rhs, start=True, stop=True, acc_reads=True):
        rd = [lhsT, rhs]
        return self.op("pe", lambda: self.nc.tensor.matmul(out.ap, lhsT.ap, rhs.ap, start=start, stop=stop),
                       rd, [out])

    def transpose(self, out, in_, ident):
        return self.op("pe", lambda: self.nc.tensor.transpose(out.ap, in_.ap, ident.ap), [in_, ident], [out])

    def act(self, out, in_, func, bias=None, scale=None, accum_out=None, e="act"):
        rd = [in_]
        kw = {}
        if bias is not None:
            if isinstance(bias, View):
                rd.append(bias)
                kw["bias"] = bias.ap
            else:
                kw["bias"] = bias
        if scale is not None:
            if isinstance(scale, View):
                rd.append(scale)
                kw["scale"] = scale.ap
            else:
                kw["scale"] = scale
        wr = [out]
        if accum_out is not None:
            wr.append(accum_out)
            kw["accum_out"] = accum_out.ap
        return self.op("act", lambda: self.nc.scalar.activation(out.ap, in_.ap, func, **kw), rd, wr)

    def _ve(self, e):
        return self.nc.vector if e == "dve" else self.nc.gpsimd

    def copy(self, out, in_, e="dve"):
        if e == "act":
            return self.op("act", lambda: self.nc.scalar.copy(out.ap, in_.ap), [in_], [out])
        return self.op(e, lambda: self._ve(e).tensor_copy(out.ap, in_.ap), [in_], [out])

    def tt(self, out, a, b, op, e="dve"):
        return self.op(e, lambda: self._ve(e).tensor_tensor(out.ap, a.ap, b.ap, op), [a, b], [out])

    def ts(self, out, a, s1, op0, s2=None, op1=None, accum_out=None, e="dve"):
        rd = [a]
        s1v = s1.ap if isinstance(s1, View) else s1
        s2v = s2.ap if isinstance(s2, View) else s2
        if isinstance(s1, View):
            rd.append(s1)
        if isinstance(s2, View):
            rd.append(s2)
        wr = [out]
        kw = {}
        if op1 is not None:
            kw["op1"] = op1
        if accum_out is not None:
            kw["accum_out"] = accum_out.ap
            wr.append(accum_out)
        if s2 is None and op1 is None and accum_out is None:
            return self.op(e, lambda: self._ve(e).tensor_single_scalar(out.ap, a.ap, s1v, op0), rd, wr)
        return self.op(e, lambda: self._ve(e).tensor_scalar(out.ap, a.ap, s1v, s2v, op0, **kw), rd, wr)

    def stt(self, out, a, s, b, op0, op1, e="dve"):
        rd = [a, b]
        sv = s.ap if isinstance(s, View) else s
        if isinstance(s, View):
            rd.append(s)
        return self.op(e, lambda: self._ve(e).scalar_tensor_tensor(out.ap, a.ap, sv, b.ap, op0, op1), rd, [out])

    def reduce(self, out, in_, op, axis=AX.X, e="dve"):
        return self.op(e, lambda: self._ve(e).tensor_reduce(out.ap, in_.ap, axis, op), [in_], [out])

    def memset(self, out, val, e="dve"):
        return self.op(e, lambda: self._ve(e).memset(out.ap, val), [], [out])

    def recip(self, out, in_):
        return self.op("dve", lambda: self.nc.vector.reciprocal(out.ap, in_.ap), [in_], [out])


P = 128
D = 1024
KC = 8
DFF = 2816
FC = 22
EPS = 1e-6


class NS:
    pass


def mk_consts(S, nc):
    C = NS()
    C.ones = S.sbuf("ones", [P, P], F32)
    S.memset(C.ones[:], 1.0)
    C.eps = S.sbuf("epsc", [P, 1], F32)
    S.memset(C.eps[:], EPS)
    C.ident = S.sbuf("ident", [P, P], F32)
    S.memset(C.ident[:], 1.0, e="pool")
    S.op("pool", lambda: nc.gpsimd.affine_select(C.ident.t[:], C.ident.t[:], [[-1, P]], ALU.is_equal, 0.0,
                                                 base=0, channel_multiplier=1), [C.ident[:]], [C.ident[:]])
    return C


def load_w(S, wd, dst, K, N, stages, blk=2048, col0=0):
    engs = ["pool", "dve", "act"]
    i = 0
    for k in range(K // P):
        for c0 in range(0, N, blk):
            w = min(blk, N - c0)
            st = stages[i % len(stages)]
            S.dma("sp", st[:, :w], wd[k * P:(k + 1) * P, col0 + c0:col0 + c0 + w])
            S.copy(dst[:, k, c0:c0 + w], st[:, :w], e=engs[i % 3])
            i += 1


def rms_rstd(S, C, src, n, nch, dim, ps, out):
    for c in range(nch):
        sq = C.sq[c % 2]
        S.act(sq[:, :n], src(c), AF.Square)
        S.matmul(ps[:, :n], C.ones[:], sq[:, :n], start=(c == 0), stop=(c == nch - 1))
    S.act(C.lnt[:, :n], ps[:, :n], AF.Ln, scale=1.0 / dim, bias=C.eps[:, 0:1])
    S.act(out, C.lnt[:, :n], AF.Exp, scale=-0.5)


def norm_mod(S, C, xt, n, A, B, col, h, ps):
    rms_rstd(S, C, lambda c: xt[:, c, :n], n, KC, D, ps, C.rstd[:, :n])
    for c in range(KC):
        t = C.tmp[c % 2]
        S.tt(t[:, :n], xt[:, c, :n], C.rstd[:, :n], ALU.mult)
        S.act(h[:, c, :n], t[:, :n], AF.Identity, scale=A[:, c, col:col + 1], bias=B[:, c, col:col + 1])


def compute_mods(S, C, cv, wada, bada, nmod, stg, psm, mods):
    scv = S.sbuf("scv", [P, KC, 2], F32)
    cvt = S.sbuf("cvt", [P, KC, 2], F32)
    S.dma("sp", cvt[:], cv[:])
    S.act(scv[:], cvt[:], AF.Silu)
    nn = nmod * KC
    for k in range(KC):
        S.dma("sp" if k % 2 else "pool", stg[k][:, :nn * P], wada[k * P:(k + 1) * P, :])
    for j in range(nn):
        for k in range(KC):
            S.matmul(psm[:, 2 * j:2 * j + 2], stg[k][:, j * P:(j + 1) * P], scv[:, k, :], start=(k == 0), stop=(k == KC - 1))
    bt = S.sbuf("badat", [P, nn], F32)
    S.dma("sp", bt[:], bada[:])
    S.tt(mods[:], psm[:, 0:2 * nn].rearrange("p (j t) -> p j t", t=2),
         View(bt, bt.t[:].rearrange("p (j o) -> p j o", o=1).to_broadcast([P, nn, 2])), ALU.add)


def ffn_sweep(S, C, tiles, x_in, x_out, w1b, w2b, A, B, G, PS):
    for j, (s0, n, col) in enumerate(tiles):
        xt = C.xt[j % 2]
        S.dma("sp", xt[:, :, :n], x_in(j))
        norm_mod(S, C, xt, n, A, B, col, C.h, PS.ss)
        for f in range(FC):
            pg = PS.g[f % 2]
            pu = PS.u[f % 2]
            for k in range(KC):
                S.matmul(pg[:, :n], w1b[:, k, f * P:(f + 1) * P], C.h[:, k, :n], start=(k == 0), stop=(k == KC - 1))
            for k in range(KC):
                S.matmul(pu[:, :n], w1b[:, k, DFF + f * P:DFF + (f + 1) * P], C.h[:, k, :n], start=(k == 0), stop=(k == KC - 1))
            sg = C.sg[f % 2]
            S.act(sg[:, :n], pg[:, :n], AF.Silu)
            S.tt(C.aT[:, f, :n], sg[:, :n], pu[:, :n], ALU.mult)
        for d in range(KC):
            py = PS.y[d % 2]
            for f in range(FC):
                S.matmul(py[:, :n], w2b[:, f, d * P:(d + 1) * P], C.aT[:, f, :n], start=(f == 0), stop=(f == FC - 1))
            S.copy(C.y[:, d, :n], py[:, :n], e="dve")
        rms_rstd(S, C, lambda c: C.y[:, c, :n], n, KC, D, PS.ss2, C.rstd2[:, :n])
        for c in range(KC):
            t = C.tmp[c % 2]
            S.tt(t[:, :n], C.y[:, c, :n], C.rstd2[:, :n], ALU.mult)
            S.stt(xt[:, c, :n], t[:, :n], G[:, c, col:col + 1], xt[:, c, :n], ALU.mult, ALU.add)
        S.dma("pool", x_out(j), xt[:, :, :n])


def alloc_ffn_work(S, C):
    C.xt = [S.sbuf("xt0", [P, KC, 512], F32)] * 2
    C.h = S.sbuf("h", [P, KC, 512], BF16)
    C.aT = S.sbuf("aT", [P, FC, 512], BF16)
    C.y = S.sbuf("y", [P, KC, 512], F32)
    C.sg = C.tmp


def alloc_small(S, C):
    C.sq = [S.sbuf("sq%d" % i, [P, 512], F32) for i in range(2)]
    C.tmp = [S.sbuf("tmp%d" % i, [P, 512], F32) for i in range(2)]
    C.lnt = C.sq[0]
    C.rstd = S.sbuf("rstd", [P, 512], F32)
    C.rstd2 = C.rstd


def mk_tiles(NT, NCX):
    tiles = [(j * 512, 512, 0) for j in range(NT // 512)]
    if NCX:
        tiles.append((NT, NCX, 1))
    return tiles


DEBUG = False
NPC = 49


def build_R1(NT, NCX):
    TT = NT + NCX
    nc = bass.Bass("TRN2", target_bir_lowering=False)
    with ExitStack() as st:
        S = Sched(nc, st)
        xT = S.dram("xT", [P, KC, TT], F32, kind="ExternalInput")
        cv = S.dram("cv", [P, KC, 2], F32, kind="ExternalInput")
        wada = S.dram("wada", [D, 5 * D], F32, kind="ExternalInput")
        bada = S.dram("bada", [P, 40], F32, kind="ExternalInput")
        ng = S.dram("ng", [P, 48], F32, kind="ExternalInput")
        w1 = S.dram("w1", [D, 2 * DFF], F32, kind="ExternalInput")
        w2 = S.dram("w2", [DFF, D], F32, kind="ExternalInput")
        win = S.dram("win", [D, NPC * P], F32, kind="ExternalInput")
        x1T = S.dram("x1T", [P, KC, TT], F32, kind="ExternalOutput")
        PT = S.dram("PT", [NPC, P, TT], F32, kind="ExternalOutput")
        tiles = mk_tiles(NT, NCX)
        C = mk_consts(S, nc)
        alloc_small(S, C)
        PS = NS()
        PS.ss = S.psum("ps_ss", [P, 512])
        PS.ss2 = S.psum("ps_ss2", [P, 512])
        PS.g = [S.psum("ps_g%d" % i, [P, 512]) for i in range(2)]
        PS.u = [S.psum("ps_u%d" % i, [P, 512]) for i in range(2)]
        PS.y = [S.psum("ps_y%d" % i, [P, 512]) for i in range(2)]
        mods = S.sbuf("mods", [P, 40, 2], F32)
        ngt = S.sbuf("ngt", [P, 6, KC], F32)
        S.dma("sp", ngt[:], ng[:].rearrange("p (m c) -> p m c", c=KC))
        A1 = S.sbuf("A1", [P, KC, 2], F32)
        G1 = S.sbuf("G1", [P, KC, 2], F32)
        A2 = S.sbuf("A2", [P, KC, 2], F32)
        with S.scope():
            stg = [S.sbuf("stgm%d" % i, [P, 5 * D], F32) for i in range(KC)]
            compute_mods(S, C, cv, wada, bada, 5, stg, PS.g[0], mods)

        def bc(v):
            return View(v.buf, v.ap.rearrange("p (c o) -> p c o", o=1).to_broadcast([P, KC, 2]))
        S.stt(A1[:], mods[:, 8:16, :], 1.0, bc(ngt[:, 0, :]), ALU.add, ALU.mult)
        S.stt(G1[:], mods[:, 16:24, :], 0.5, bc(ngt[:, 1, :]), ALU.mult, ALU.mult)
        S.stt(A2[:], mods[:, 32:40, :], 1.0, bc(ngt[:, 2, :]), ALU.add, ALU.mult)
        B1 = mods[:, 0:8, :]
        B2 = mods[:, 24:32, :]
        if DEBUG:
            dbg = S.dram("dbg_mods", [P, 80], F32, kind="ExternalOutput")
            S.dma("sp", dbg[:], mods[:].rearrange("p j t -> p (j t)"))
        x1tiles = [S.sub("x1t%d" % j, x1T.t[:, :, s0:s0 + n]) for j, (s0, n, col) in enumerate(tiles)]
        with S.scope():
            w1b = S.sbuf("w1b", [P, KC, 2 * DFF], BF16)
            w2b = S.sbuf("w2b", [P, FC, D], BF16)
            with S.scope():
                stages = [S.sbuf("wst%d" % i, [P, 2048], F32) for i in range(3)]
                load_w(S, w1, w1b, D, 2 * DFF, stages)
                load_w(S, w2, w2b, DFF, D, stages)
            alloc_ffn_work(S, C)
            ffn_sweep(S, C, tiles, lambda j: xT[:, :, tiles[j][0]:tiles[j][0] + tiles[j][1]],
                      lambda j: x1tiles[j][:], w1b, w2b, A1[:], B1, G1[:], PS)
        with S.scope():
            winb = S.sbuf("winb", [P, KC, NPC * P], BF16)
            with S.scope():
                stages = [S.sbuf("wst%d" % i, [P, 2048], F32) for i in range(3)]
                load_w(S, win, winb, D, NPC * P, stages)
            xt2 = [S.sbuf("xq%d" % i, [P, KC, 512], F32) for i in range(2)]
            h = S.sbuf("h2", [P, KC, 512], BF16)
            ost = [S.sbuf("ost%d" % i, [P, 4, 512], F32) for i in range(3)]
            pps = PS.g + PS.u + PS.y
            gi = 0
            for j, (s0, n, col) in enumerate(tiles):
                xt = xt2[j % 2]
                S.dma("sp", xt[:, :, :n], x1tiles[j][:])
                norm_mod(S, C, xt, n, A2[:], B2, col, h, PS.ss)
                for c0 in range(0, NPC, 4):
                    nn = min(4, NPC - c0)
                    o = ost[gi % 3]
                    gi += 1
                    for cc in range(nn):
                        pp = pps[(c0 + cc) % 6]
                        for k in range(KC):
                            S.matmul(pp[:, :n], winb[:, k, (c0 + cc) * P:(c0 + cc + 1) * P], h[:, k, :n],
                                     start=(k == 0), stop=(k == KC - 1))
                        S.copy(o[:, cc, :n], pp[:, :n], e=("act" if cc % 2 else "dve"))
                    S.dma("pool", S.sub("pt", PT.t[c0:c0 + nn, :, s0:s0 + n].rearrange("c p t -> p c t"))[:], o[:, :nn, :n])
            S.wait_all("sp", ost + xt2)
        S.barrier()
    return nc


def build_R2(NT, NCX):
    TT = NT + NCX
    nc = bass.Bass("TRN2", target_bir_lowering=False)
    with ExitStack() as st:
        S = Sched(nc, st)
        x1T = S.dram("x1T", [P, KC, TT], F32, kind="ExternalInput")
        oin = [S.dram(nm, [P, 4, TT], F32, kind="ExternalInput") for nm in ("oaT", "obT", "ocT")]
        cv = S.dram("cv", [P, KC, 2], F32, kind="ExternalInput")
        wada = S.dram("wada", [D, 6 * D], F32, kind="ExternalInput")
        bada = S.dram("bada", [P, 48], F32, kind="ExternalInput")
        ng = S.dram("ng", [P, 48], F32, kind="ExternalInput")
        snw = S.dram("snw", [P, 4], F32, kind="ExternalInput")
        wg = S.dram("wg", [D, 3 * D], F32, kind="ExternalInput")
        wb = S.dram("wb", [1536, D], F32, kind="ExternalInput")
        wo = S.dram("wo", [D, D], F32, kind="ExternalInput")
        w1 = S.dram("w1", [D, 2 * DFF], F32, kind="ExternalInput")
        w2 = S.dram("w2", [DFF, D], F32, kind="ExternalInput")
        x3T = S.dram("x3T", [P, KC, TT], F32, kind="ExternalOutput")
        x2T = S.dram("x2T", [P, KC, TT], F32, kind="Internal")
        tiles = mk_tiles(NT, NCX)
        C = mk_consts(S, nc)
        alloc_small(S, C)
        PS = NS()
        PS.ss = S.psum("ps_ss", [P, 512])
        PS.ss2 = S.psum("ps_ss2", [P, 512])
        PS.g = [S.psum("ps_g%d" % i, [P, 512]) for i in range(2)]
        PS.u = [S.psum("ps_u%d" % i, [P, 512]) for i in range(2)]
        PS.y = [S.psum("ps_y%d" % i, [P, 512]) for i in range(2)]
        mods = S.sbuf("mods", [P, 48, 2], F32)
        ngt = S.sbuf("ngt", [P, 6, KC], F32)
        S.dma("sp", ngt[:], ng[:].rearrange("p (m c) -> p m c", c=KC))
        snt = S.sbuf("snt", [P, 4], F32)
        S.dma("sp", snt[:], snw[:])
        with S.scope():
            stg = [S.sbuf("stgm%d" % i, [P, 6 * D], F32) for i in range(KC)]
            compute_mods(S, C, cv, wada, bada, 6, stg, PS.g[0], mods)

        def bc(v):
            return View(v.buf, v.ap.rearrange("p (c o) -> p c o", o=1).to_broadcast([P, KC, 2]))
        A2 = S.sbuf("A2", [P, KC, 2], F32)
        G3 = S.sbuf("G3", [P, KC, 2], F32)
        A4 = S.sbuf("A4", [P, KC, 2], F32)
        G5 = S.sbuf("G5", [P, KC, 2], F32)
        S.stt(A2[:], mods[:, 8:16, :], 1.0, bc(ngt[:, 2, :]), ALU.add, ALU.mult)
        S.tt(G3[:], mods[:, 16:24, :], bc(ngt[:, 3, :]), ALU.mult)
        S.stt(A4[:], mods[:, 32:40, :], 1.0, bc(ngt[:, 4, :]), ALU.add, ALU.mult)
        S.stt(G5[:], mods[:, 40:48, :], 0.5, bc(ngt[:, 5, :]), ALU.mult, ALU.mult)
        B2 = mods[:, 0:8, :]
        B4 = mods[:, 24:32, :]
        x2tiles = [S.sub("x2t%d" % j, x2T.t[:, :, s0:s0 + n]) for j, (s0, n, col) in enumerate(tiles)]
        with S.scope():
            wgb = S.sbuf("wgb", [P, KC, 3 * D], BF16)
            wbb = S.sbuf("wbb", [P, 12, D], BF16)
            wob = S.sbuf("wob", [P, KC, D], BF16)
            with S.scope():
                stages = [S.sbuf("wst%d" % i, [P, 2048], F32) for i in range(3)]
                load_w(S, wg, wgb, D, 3 * D, stages)
                load_w(S, wb, wbb, 1536, D, stages)
                load_w(S, wo, wob, D, D, stages)
            xt = S.sbuf("xm", [P, KC, 512], F32)
            h = S.sbuf("hm", [P, KC, 512], BF16)
            ost = S.sbuf("ostg", [P, 4, 512], F32)
            ob16 = S.sbuf("ob16", [P, 12, 512], BF16)
            yacc = S.sbuf("yacc", [P, 512], F32)
            ybf = S.sbuf("ybf", [P, KC, 512], BF16)
            yy = S.sbuf("yy", [P, KC, 512], F32)
            gt = [S.sbuf("gt%d" % i, [P, 512], F32) for i in range(2)]
            for j, (s0, n, col) in enumerate(tiles):
                S.dma("sp", xt[:, :, :n], x1T[:, :, s0:s0 + n])
                norm_mod(S, C, xt, n, A2[:], B2, col, h, PS.ss)
                for br in range(3):
                    S.dma("sp", ost[:, :, :n], oin[br][:, :, s0:s0 + n])
                    if br < 2:
                        S.copy(ob16[:, br * 4:(br + 1) * 4, :n], ost[:, :, :n], e="pool")
                    else:
                        rms_rstd(S, C, lambda c: ost[:, c, :n], n, 4, 512, PS.ss2, C.rstd2[:, :n])
                        for c in range(4):
                            t = C.tmp[c % 2]
                            S.tt(t[:, :n], ost[:, c, :n], C.rstd2[:, :n], ALU.mult)
                            S.act(ob16[:, 8 + c, :n], t[:, :n], AF.Copy, scale=snt[:, c:c + 1])
                for d in range(KC):
                    for br in range(3):
                        pg = PS.g[br % 2]
                        pu = PS.u[br % 2]
                        cg = br * KC + d
                        for k in range(KC):
                            S.matmul(pg[:, :n], wgb[:, k, cg * P:(cg + 1) * P], h[:, k, :n], start=(k == 0), stop=(k == KC - 1))
                        for k in range(4):
                            S.matmul(pu[:, :n], wbb[:, br * 4 + k, d * P:(d + 1) * P], ob16[:, br * 4 + k, :n], start=(k == 0), stop=(k == 3))
                        g = gt[br % 2]
                        S.act(g[:, :n], pg[:, :n], AF.Sigmoid)
                        if br == 0:
                            S.tt(yacc[:, :n], g[:, :n], pu[:, :n], ALU.mult)
                        else:
                            t = C.tmp[br % 2]
                            S.tt(t[:, :n], g[:, :n], pu[:, :n], ALU.mult)
                            if br == 1:
                                S.tt(yacc[:, :n], yacc[:, :n], t[:, :n], ALU.add)
                            else:
                                S.tt(ybf[:, d, :n], yacc[:, :n], t[:, :n], ALU.add)
                for d in range(KC):
                    py = PS.y[d % 2]
                    for k in range(KC):
                        S.matmul(py[:, :n], wob[:, k, d * P:(d + 1) * P], ybf[:, k, :n], start=(k == 0), stop=(k == KC - 1))
                    S.copy(yy[:, d, :n], py[:, :n], e="dve")
                rms_rstd(S, C, lambda c: yy[:, c, :n], n, KC, D, PS.ss2, C.rstd2[:, :n])
                for c in range(KC):
                    t = C.tmp[c % 2]
                    S.tt(t[:, :n], yy[:, c, :n], C.rstd2[:, :n], ALU.mult)
                    S.stt(xt[:, c, :n], t[:, :n], G3[:, c, col:col + 1], xt[:, c, :n], ALU.mult, ALU.add)
                S.dma("pool", x2tiles[j][:], xt[:, :, :n])
        with S.scope():
            w1b = S.sbuf("w1b", [P, KC, 2 * DFF], BF16)
            w2b = S.sbuf("w2b", [P, FC, D], BF16)
            with S.scope():
                stages = [S.sbuf("wst%d" % i, [P, 2048], F32) for i in range(3)]
                load_w(S, w1, w1b, D, 2 * DFF, stages)
                load_w(S, w2, w2b, DFF, D, stages)
            alloc_ffn_work(S, C)
            ffn_sweep(S, C, tiles, lambda j: x2tiles[j][:],
                      lambda j: S.sub("x3", x3T.t[:, :, tiles[j][0]:tiles[j][0] + tiles[j][1]])[:], w1b, w2b, A4[:], B4, G5[:], PS)
        S.barrier()
    return nc


I32 = mybir.dt.int32
import math


def rope_tables(S, nc, C, L, cosb, sinb):
    GW = 64
    rows = L // GW
    TWO_PI = 2 * math.pi
    with S.scope():
        ti = S.sbuf("ti", [P, P], I32)
        tf = S.sbuf("tf", [P, P], F32)

        def ppc(name, pattern):
            o = S.sbuf(name, [P, 1], F32)
            S.op("pool", lambda: nc.gpsimd.iota(ti.t[:], pattern, base=0, channel_multiplier=0), [], [ti[:]])
            S.copy(tf[:], ti[:])
            S.tt(tf[:], tf[:], C.ident[:], ALU.mult)
            S.reduce(o[:], tf[:], ALU.add)
            return o
        i16 = ppc("i16", [[0, 2], [0, 2], [1, 16], [0, 2]])
        sel = ppc("sel", [[0, 2], [1, 2], [0, 16], [0, 2]])
        dd = ppc("dd", [[0, 2], [0, 2], [0, 16], [1, 2]])
        sgn = S.sbuf("sgn", [P, 1], F32)
        inv = S.sbuf("inv", [P, 1], F32)
        S.ts(sgn[:], dd[:], 2.0, ALU.mult, -1.0, ALU.add)
        S.act(inv[:], i16[:], AF.Exp, scale=-math.log(10000.0) / 16.0)
        S.ts(inv[:], inv[:], 1.0 / TWO_PI, ALU.mult)
        CH = 1024
        with S.scope():
            ri = S.sbuf("ri", [P, CH], I32)
            ci = S.sbuf("ci", [P, CH], I32)
            rf = S.sbuf("rf", [P, CH], F32)
            cf = S.sbuf("cf", [P, CH], F32)
            xt = S.sbuf("xtn", [P, CH], F32)
            ni = S.sbuf("ni", [P, CH], I32)
            nf = S.sbuf("nf", [P, CH], F32)
            for c0 in range(0, L, CH):
                w = min(CH, L - c0)
                S.op("pool", lambda c0=c0, w=w: nc.gpsimd.iota(ri.t[:, :w], [[1, w // GW], [0, GW]], base=c0 // GW, channel_multiplier=0), [], [ri[:]])
                S.op("pool", lambda w=w: nc.gpsimd.iota(ci.t[:, :w], [[0, w // GW], [1, GW]], base=0, channel_multiplier=0), [], [ci[:]])
                S.copy(rf[:, :w], ri[:, :w])
                S.copy(cf[:, :w], ci[:, :w])
                S.tt(cf[:, :w], cf[:, :w], rf[:, :w], ALU.subtract)
                S.stt(xt[:, :w], cf[:, :w], sel[:, 0:1], rf[:, :w], ALU.mult, ALU.add)
                S.ts(xt[:, :w], xt[:, :w], inv[:, 0:1], ALU.mult)
                for (dst, off) in ((sinb, 0.0), (cosb, 0.25)):
                    if off:
                        S.ts(xt[:, :w], xt[:, :w], off, ALU.add)
                    S.copy(ni[:, :w], xt[:, :w])
                    S.copy(nf[:, :w], ni[:, :w])
                    S.tt(nf[:, :w], xt[:, :w], nf[:, :w], ALU.subtract)
                    S.act(dst[:, c0:c0 + w], nf[:, :w], AF.Sin, scale=TWO_PI * (1 - 1e-6))
                S.ts(sinb[:, c0:c0 + w], sinb[:, c0:c0 + w], sgn[:, 0:1], ALU.mult)


def build_Mdiff(L, LC):
    LK = L + LC
    NKC = LK // P
    nc = bass.Bass("TRN2", target_bir_lowering=False)
    with ExitStack() as st:
        S = Sched(nc, st)
        qT = S.dram("qT", [2, P, L], F32, kind="ExternalInput")
        qsT = S.dram("qsT", [2, P, L], F32, kind="ExternalInput")
        kT = S.dram("kT", [2, P, LK], F32, kind="ExternalInput")
        ksT = S.dram("ksT", [2, P, L], F32, kind="ExternalInput")
        qcT = S.dram("qcT", [2, P, LC], F32, kind="ExternalInput")
        vd = S.dram("v", [P, NKC, 256], F32, kind="ExternalInput")
        lamd = S.dram("lam", [P, 256], F32, kind="ExternalInput")
        nwd = S.dram("nw", [P, 1], F32, kind="ExternalInput")
        lid = S.dram("li", [P, 1], F32, kind="ExternalInput")
        obT = S.dram("obT", [2, P, L], F32, kind="ExternalOutput")
        obcT = S.dram("obcT", [2, P, LC], F32, kind="ExternalOutput")
        C = mk_consts(S, nc)
        C.sq = [S.sbuf("sq%d" % i, [P, 512], F32) for i in range(2)]
        C.lnt = C.sq[0]
        onesb = S.sbuf("onesb", [P, P], BF16)
        S.memset(onesb[:], 1.0)
        Q = [S.sbuf("Q%d" % h, [P, L], BF16) for h in range(2)]
        QC = [S.sbuf("QC%d" % h, [P, LC], BF16) for h in range(2)]
        K = [S.sbuf("K%d" % h, [P, LK], BF16) for h in range(2)]
        V = S.sbuf("V", [P, NKC, 256], BF16)
        lam = S.sbuf("lamt", [P, 4, 64], F32)
        S.dma("sp", lam[:], lamd[:].rearrange("p (a b) -> p a b", b=64))
        nw = S.sbuf("nwt", [P, 1], F32)
        li = S.sbuf("lit", [P, 1], F32)
        S.dma("sp", nw[:], nwd[:])
        S.dma("sp", li[:], lid[:])
        pr = S.sbuf("pr", [P, 2, 64], F32)
        s12 = S.sbuf("s12", [P, 2], F32)
        S.tt(pr[:, 0, :], lam[:, 0, :], lam[:, 1, :], ALU.mult)
        S.tt(pr[:, 1, :], lam[:, 2, :], lam[:, 3, :], ALU.mult)
        S.reduce(s12[:], pr[:], ALU.add)
        e12 = S.sbuf("e12", [P, 2], F32)
        S.act(e12[:], s12[:], AF.Exp)
        neglam = S.sbuf("neglam", [P, 1], F32)
        S.tt(neglam[:], e12[:, 1:2], e12[:, 0:1], ALU.subtract)
        S.tt(neglam[:], neglam[:], li[:], ALU.subtract)
        sc2 = S.sbuf("sc2", [P, 1], F32)
        S.ts(sc2[:], li[:], -1.0, ALU.mult, 1.0, ALU.add)
        S.tt(sc2[:], sc2[:], nw[:], ALU.mult)
        with S.scope():
            cosb = S.sbuf("cosb", [P, L], F32)
            sinb = S.sbuf("sinb", [P, L], F32)
            rope_tables(S, nc, C, L, cosb, sinb)
            with S.scope():
                a = [S.sbuf("la%d" % i, [P, 512], F32) for i in range(2)]
                b = [S.sbuf("lb%d" % i, [P, 512], F32) for i in range(2)]
                vst = [S.sbuf("vst%d" % i, [P, 4, 256], F32) for i in range(2)]
                i = 0
                for h in range(2):
                    for (src, ssw, dst) in ((qT, qsT, Q[h]), (kT, ksT, K[h])):
                        for c0 in range(0, L, 512):
                            ta, tb = a[i % 2], b[i % 2]
                            i += 1
                            S.dma("sp", ta[:], src[h, :, c0:c0 + 512])
                            S.dma("pool", tb[:], ssw[h, :, c0:c0 + 512])
                            S.tt(ta[:], ta[:], cosb[:, c0:c0 + 512], ALU.mult)
                            S.tt(tb[:], tb[:], sinb[:, c0:c0 + 512], ALU.mult, e="pool")
                            S.tt(dst[:, c0:c0 + 512], ta[:], tb[:], ALU.add)
                    ta = a[i % 2]
                    i += 1
                    S.dma("sp", ta[:, :LC], kT[h, :, L:LK])
                    S.copy(K[h][:, L:LK], ta[:, :LC])
                    ta = a[i % 2]
                    i += 1
                    S.dma("sp", ta[:, :LC], qcT[h, :, :])
                    S.copy(QC[h][:], ta[:, :LC])
                for c0 in range(0, NKC, 4):
                    w = min(4, NKC - c0)
                    t = vst[(c0 // 4) % 2]
                    S.dma("sp", t[:, :w, :], vd[:, c0:c0 + w, :])
                    S.copy(V[:, c0:c0 + w, :], t[:, :w, :], e="pool")
        ps_s = [[S.psum("ps_s%d%d" % (j, i), [P, 512]) for i in range(2)] for j in range(2)]
        ps_o = [S.psum("ps_o%d" % j, [P, 512]) for j in range(2)]
        ps_z = [S.psum("ps_z%d" % j, [P, 512]) for j in range(2)]
        pt = [[S.sbuf("pt%d%d" % (j, i), [P, 512], BF16) for i in range(2)] for j in range(2)]
        rz = [S.sbuf("rz%d" % j, [P, 512], F32) for j in range(2)]
        t0 = S.sbuf("t0", [P, 512], F32)
        t1 = S.sbuf("t1", [P, 512], F32)
        rstd = S.sbuf("rstd", [P, 512], F32)
        oo = [S.sbuf("oo%d" % i, [P, 512], F32) for i in range(2)]
        jobs = []
        for h in range(2):
            for q0 in range(0, L, 512):
                jobs.append((h, Q[h][:, q0:q0 + 512], 512, 0, NKC, obT[h, :, q0:q0 + 512]))
            jobs.append((h, QC[h][:], LC, L // P, NKC, obcT[h, :, :]))
        for ji, (h, qv, n, kc0, kc1, outv) in enumerate(jobs):
            for kc in range(kc0, kc1):
                bi = kc % 2
                for j in range(2):
                    S.matmul(ps_s[j][bi][:, :n], K[h][j * 64:(j + 1) * 64, kc * P:(kc + 1) * P], qv[j * 64:(j + 1) * 64, :],
                             start=True, stop=True)
                for j in range(2):
                    S.act(pt[j][bi][:, :n], ps_s[j][bi][:, :n], AF.Exp, scale=0.125)
                for j in range(2):
                    S.matmul(ps_o[j][:, :n], V[:, kc, h * P:(h + 1) * P], pt[j][bi][:, :n], start=(kc == kc0), stop=(kc == kc1 - 1))
                    S.matmul(ps_z[j][:, :n], onesb[:], pt[j][bi][:, :n], start=(kc == kc0), stop=(kc == kc1 - 1))
            for j in range(2):
                S.recip(rz[j][:, :n], ps_z[j][:, :n])
            S.tt(t0[:, :n], ps_o[0][:, :n], rz[0][:, :n], ALU.mult)
            S.tt(t1[:, :n], ps_o[1][:, :n], rz[1][:, :n], ALU.mult)
            S.stt(t0[:, :n], t1[:, :n], neglam[:, 0:1], t0[:, :n], ALU.mult, ALU.add)
            pss = ps_s[0][0]
            rms_rstd(S, C, lambda c: t0[:, :n], n, 1, P, pss, rstd[:, :n])
            o = oo[ji % 2]
            S.tt(t1[:, :n], t0[:, :n], rstd[:, :n], ALU.mult)
            S.ts(o[:, :n], t1[:, :n], sc2[:, 0:1], ALU.mult)
            S.dma("pool", outv, o[:, :n])
        S.barrier()
    return nc


def tri_mask(S, nc, name, kind, blk=None):
    m = S.sbuf(name, [P, P], F32)
    S.memset(m[:], 1.0, e="pool")
    pat, cm, op = {"le": ([[1, P]], -1, ALU.is_ge), "ge": ([[-1, P]], 1, ALU.is_ge),
                   "gt": ([[-1, P]], 1, ALU.is_gt), "lt": ([[1, P]], -1, ALU.is_gt)}[kind]
    S.op("pool", lambda: nc.gpsimd.affine_select(m.t[:], m.t[:], pat, op, 0.0, base=0, channel_multiplier=cm), [m[:]], [m[:]])
    if blk:
        S.memset(m[0:blk, blk:P], 0.0, e="pool")
        S.memset(m[blk:P, 0:blk], 0.0, e="pool")
    return m


def build_Mssd(L, LC, dbg_stop=99):
    LT = L + LC
    NCH = LT // P
    NCC = LC // P
    nc = bass.Bass("TRN2", target_bir_lowering=False)
    with ExitStack() as st:
        S = Sched(nc, st)
        xl = S.dram("xbcl", [4, P, L + 4], F32, kind="ExternalInput")
        xc = S.dram("xbcc", [4, P, LC + 4], F32, kind="ExternalInput")
        cwd = S.dram("cw", [P, 4, 5], F32, kind="ExternalInput")
        cbd = S.dram("cb", [P, 4], F32, kind="ExternalInput")
        zd = S.dram("z", [P, NCH, 256], F32, kind="ExternalInput")
        dtd = S.dram("dt", [P, NCH, 8], F32, kind="ExternalInput")
        dbd = S.dram("dtb", [P, 8], F32, kind="ExternalInput")
        ald = S.dram("alog", [P, 8], F32, kind="ExternalInput")
        dsd = S.dram("dskip", [P, 4], F32, kind="ExternalInput")
        yo = S.dram("y", [NCH, P, 256], F32, kind="ExternalOutput")
        yf = S.dram("yf", [NCH, P, 256], F32, kind="Internal")
        C = mk_consts(S, nc)
        tri = {0: tri_mask(S, nc, "tri_f", "le"), 1: tri_mask(S, nc, "tri_b", "ge")}
        strict = {0: tri_mask(S, nc, "str_f", "gt"), 1: tri_mask(S, nc, "str_b", "lt")}
        cw = S.sbuf("cw", [P, 4, 5], F32)
        cb = S.sbuf("cb", [P, 4], F32)
        S.dma("sp", cw[:], cwd[:])
        S.dma("sp", cb[:], cbd[:])
        xs_tok = S.sbuf("xs_tok", [P, NCH, 256], F32)
        B_tok = S.sbuf("B_tok", [P, NCH, P], F32)
        BT = S.sbuf("BT", [P, LT], F32)
        CT = S.sbuf("CT", [P, LT], F32)
        dtv = S.sbuf("dtv", [P, NCH, 8], F32)
        aall = S.sbuf("aall", [P, NCH, 8], F32)
        dtb = S.sbuf("dtb", [P, 8], F32)
        aneg = S.sbuf("aneg", [P, 8], F32)
        dsk = S.sbuf("dsk", [P, 4], F32)
        S.dma("sp", dtv[:], dtd[:])
        S.dma("sp", dtb[:], dbd[:])
        S.dma("sp", aneg[:], ald[:])
        S.dma("sp", dsk[:], dsd[:])
        S.tt(dtv[:], dtv[:], View(dtb, dtb.t[:].rearrange("p (o e) -> p o e", o=1).to_broadcast([P, NCH, 8])), ALU.add)
        S.act(dtv[:], dtv[:], AF.Exp)
        S.act(dtv[:], dtv[:], AF.Ln, bias=C.ones[:, 0:1])
        S.act(aneg[:], aneg[:], AF.Exp)
        S.ts(aneg[:], aneg[:], -1.0, ALU.mult)
        S.tt(aall[:], dtv[:], View(aneg, aneg.t[:].rearrange("p (o e) -> p o e", o=1).to_broadcast([P, NCH, 8])), ALU.mult)
        ps_t = [S.psum("ps_t%d" % i, [P, 512]) for i in range(2)]
        with S.scope():
            raw = [S.sbuf("raw%d" % i, [P, 4, 516], F32) for i in range(2)]
            acc = [S.sbuf("acc%d" % i, [P, 512], F32) for i in range(2)]
            xsT = [S.sbuf("xsT%d" % i, [P, 512], F32) for i in range(2)]
            segs = [(xc, 0, LC)] + [(xl, LC, L)]
            ti = 0
            if dbg_stop < 0:
                segs = []
            for (src, base, seglen) in segs:
                for t0 in range(0, seglen, 512):
                    n = min(512, seglen - t0)
                    r = raw[ti % 2]
                    ti += 1
                    S.dma("sp", r[:, :, :n + 4], src[:, :, t0:t0 + n + 4].rearrange("c p t -> p c t"))
                    for c in range(4):
                        a = acc[c % 2]
                        eng = "dve"
                        S.ts(a[:, :n], r[:, c, 0:n], cw[:, c, 0:1], ALU.mult, e=eng)
                        for j in range(1, 5):
                            S.stt(a[:, :n], r[:, c, j:j + n], cw[:, c, j:j + 1], a[:, :n], ALU.mult, ALU.add, e=eng)
                        g0 = base + t0
                        if c < 2:
                            dst = xsT[c]
                            S.act(dst[:, :n], a[:, :n], AF.Silu, bias=cb[:, c:c + 1])
                        elif c == 2:
                            S.act(BT[:, g0:g0 + n], a[:, :n], AF.Silu, bias=cb[:, c:c + 1])
                        else:
                            S.act(CT[:, g0:g0 + n], a[:, :n], AF.Silu, bias=cb[:, c:c + 1])
                    for bl in range(n // P):
                        gc = (base + t0) // P + bl
                        pt = ps_t[bl % 2]
                        dbgv = None
                        S.transpose(pt[:, 0:P], xsT[0][:, bl * P:(bl + 1) * P], C.ident[:])
                        if dbgv == "T1":
                            S.copy(xs_tok[:, gc, 0:P], pt[:, 0:P], e="dve")
                            continue
                        S.transpose(pt[:, P:2 * P], xsT[1][:, bl * P:(bl + 1) * P], C.ident[:])
                        if dbgv == "T2":
                            S.copy(xs_tok[:, gc, :], pt[:, 0:2 * P], e="dve")
                            continue
                        S.transpose(pt[:, 2 * P:3 * P], BT[:, gc * P:(gc + 1) * P], C.ident[:])
                        S.copy(xs_tok[:, gc, :], pt[:, 0:2 * P], e="dve")
                        S.copy(B_tok[:, gc, :], pt[:, 2 * P:3 * P], e="dve")
        ps_arg = [S.psum("ps_arg%d" % i, [P, 512]) for i in range(2)]
        ps_cb = S.psum("ps_cb", [P, 512])
        ps_y = S.psum("ps_y", [P, 512])
        ps_st = S.psum("ps_st", [P, 512])
        ps_sm = S.psum("ps_sm", [P, 512])
        X = [S.sbuf("X%d" % i, [P, 4, P], F32) for i in range(2)]
        LTt = [S.sbuf("LT%d" % i, [P, 4, P], F32) for i in range(2)]
        CBm = S.sbuf("CBm", [P, P], F32)
        scT = [S.sbuf("scT%d" % i, [P, 4, P], F32) for i in range(2)]
        sm = S.sbuf("sm", [P, 8], F32)
        eacs = S.sbuf("eacs", [P, 4], F32)
        edec = S.sbuf("edec", [P, 4], F32)
        etot = S.sbuf("etot", [P, 4], F32)
        dif = S.sbuf("dif", [P, 4], F32)
        xdt = [S.sbuf("xdt%d" % i, [P, 4, 64], F32) for i in range(2)]
        xdtd = [S.sbuf("xdtd%d" % i, [P, 4, 64], F32) for i in range(2)]
        ST = S.sbuf("ST", [P, 4, 64], F32)
        yt = [S.sbuf("yt%d" % i, [P, 256], F32) for i in range(2)]
        y2 = [S.sbuf("y2%d" % i, [P, 256], F32) for i in range(2)]
        yfl = [S.sbuf("yfl%d" % i, [P, 256], F32) for i in range(2)]
        zt = [S.sbuf("zt%d" % i, [P, 256], F32) for i in range(2)]
        yfb = [S.sub("yf%d" % c, yf.t[c]) for c in range(NCH)]

        def bc4(v):
            return View(v.buf, v.ap.rearrange("p (h o) -> p h o", o=1).to_broadcast([P, 4, 64]))
        if dbg_stop < 2:
            for c in range(NCH):
                S.dma("sp", zt[c % 2][:], zd[:, c, :])
                if dbg_stop == 1:
                    S.tt(zt[c % 2][:], zt[c % 2][:], xs_tok[:, c, :], ALU.add)
                    S.tt(zt[c % 2][:, 0:P], zt[c % 2][:, 0:P], B_tok[:, c, :], ALU.add)
                S.dma("pool", S.sub("yo", yo.t[c])[:], zt[c % 2][:])
        for dr in range(2 if dbg_stop >= 2 else 0):
            order = list(range(NCH)) if dr == 0 else (list(range(NCC - 1, -1, -1)) + list(range(NCH - 1, NCC - 1, -1)))
            S.memset(ST[:], 0.0)
            for it, c in enumerate(order):
                bi = it % 2
                a4 = aall[:, c, dr * 4:(dr + 1) * 4]
                for h in range(4):
                    S.ts(X[bi][:, h, :], strict[dr][:], aall[:, c, dr * 4 + h:dr * 4 + h + 1], ALU.mult, e=("dve" if h % 2 else "pool"))
                for h in range(4):
                    S.matmul(ps_arg[bi][:, h * P:(h + 1) * P], X[bi][:, h, :], tri[dr][:])
                S.act(LTt[bi][:].rearrange("p h l -> p (h l)"), ps_arg[bi][:], AF.Exp)
                S.matmul(ps_cb[:, 0:P], BT[:, c * P:(c + 1) * P], CT[:, c * P:(c + 1) * P])
                S.tt(CBm[:], ps_cb[:, 0:P], tri[dr][:], ALU.mult)
                S.tt(scT[bi][:], LTt[bi][:], View(CBm, CBm.t[:].rearrange("p (o l) -> p o l", o=1).to_broadcast([P, 4, P])), ALU.mult)
                S.matmul(ps_sm[:, 0:4], tri[dr][:], a4)
                S.matmul(ps_sm[:, 4:8], C.ones[:], a4)
                S.copy(sm[:], ps_sm[:, 0:8])
                S.act(eacs[:], sm[:, 0:4], AF.Exp)
                S.act(etot[:], sm[:, 4:8], AF.Exp)
                S.tt(dif[:], sm[:, 4:8], sm[:, 0:4], ALU.subtract)
                S.act(edec[:], dif[:], AF.Exp)
                xv = xs_tok[:, c, :].rearrange("p (h d) -> p h d", h=4)
                S.tt(xdt[bi][:], xv, bc4(dtv[:, c, dr * 4:(dr + 1) * 4]), ALU.mult, e="pool")
                S.tt(xdtd[bi][:], xdt[bi][:], bc4(edec[:]), ALU.mult)
                for h in range(4):
                    S.matmul(ps_y[:, h * 64:(h + 1) * 64], scT[bi][:, h, :], xdt[bi][:, h, :])
                S.matmul(ps_y[:, 256:512], CT[:, c * P:(c + 1) * P], ST[:].rearrange("p h d -> p (h d)"))
                y = yt[bi]
                S.tt(y[:].rearrange("p (h d) -> p h d", h=4), ps_y[:, 256:512].rearrange("p (h d) -> p h d", h=4), bc4(eacs[:]), ALU.mult)
                S.tt(y[:], y[:], ps_y[:, 0:256], ALU.add)
                S.matmul(ps_st[:, 0:256], B_tok[:, c, :], xdtd[bi][:].rearrange("p h d -> p (h d)"))
                S.tt(ST[:], ST[:], bc4(etot[:]), ALU.mult)
                S.tt(ST[:].rearrange("p h d -> p (h d)"), ST[:].rearrange("p h d -> p (h d)"), ps_st[:, 0:256], ALU.add)
                if dr == 0:
                    S.dma("pool", yfb[c][:], y[:])
                else:
                    S.dma("sp", yfl[bi][:], yfb[c][:])
                    S.dma("sp", zt[bi][:], zd[:, c, :])
                    o = y2[bi]
                    S.tt(o[:].rearrange("p (h d) -> p h d", h=4), xv, bc4(dsk[:]), ALU.mult, e="pool")
                    S.tt(y[:], y[:], yfl[bi][:], ALU.add)
                    S.tt(o[:], o[:], y[:], ALU.add)
                    S.act(zt[bi][:], zt[bi][:], AF.Silu)
                    S.tt(o[:], o[:], zt[bi][:], ALU.mult)
                    S.dma("pool", S.sub("yo", yo.t[c])[:], o[:])
        S.barrier()
    return nc


def build_Mgdn(L, LC):
    LT = L + LC
    NCH = LT // P
    NCC = LC // P
    nc = bass.Bass("TRN2", target_bir_lowering=False)
    with ExitStack() as st:
        S = Sched(nc, st)
        ql = S.dram("qkvl", [6, P, L + 4], F32, kind="ExternalInput")
        qc = S.dram("qkvc", [6, P, LC + 4], F32, kind="ExternalInput")
        cwd = S.dram("cw", [P, 6, 5], F32, kind="ExternalInput")
        zd = S.dram("z", [P, NCH, 256], F32, kind="ExternalInput")
        ad = S.dram("araw", [P, NCH, 4], F32, kind="ExternalInput")
        bd = S.dram("braw", [P, NCH, 4], F32, kind="ExternalInput")
        ald = S.dram("alog", [P, 4], F32, kind="ExternalInput")
        dbd = S.dram("dtb", [P, 4], F32, kind="ExternalInput")
        nwd = S.dram("nw", [P, P], F32, kind="ExternalInput")
        oa = S.dram("oa", [NCH, P, 256], F32, kind="ExternalOutput")
        ofd = S.dram("of", [NCH, P, 256], F32, kind="Internal")
        C = mk_consts(S, nc)
        M = {k: tri_mask(S, nc, "m_" + k, k, blk=64) for k in ("le", "ge", "gt", "lt")}
        halfA = S.sbuf("halfA", [P, P], F32)
        halfB = S.sbuf("halfB", [P, P], F32)
        S.memset(halfA[:], 0.0)
        S.memset(halfB[:], 0.0)
        S.memset(halfA[0:64, :], 1.0)
        S.memset(halfB[64:128, :], 1.0)
        cw = S.sbuf("cw", [P, 6, 5], F32)
        S.dma("sp", cw[:], cwd[:])
        nw = S.sbuf("nw", [P, P], F32)
        S.dma("sp", nw[:], nwd[:])
        gall = S.sbuf("gall", [P, NCH, 4], F32)
        ball = S.sbuf("ball", [P, NCH, 4], F32)
        negb = S.sbuf("negb", [P, NCH, 4], F32)
        aneg = S.sbuf("aneg", [P, 4], F32)
        dtb = S.sbuf("dtb", [P, 4], F32)
        S.dma("sp", gall[:], ad[:])
        S.dma("sp", ball[:], bd[:])
        S.dma("sp", aneg[:], ald[:])
        S.dma("sp", dtb[:], dbd[:])

        def bcn(v):
            return View(v.buf, v.ap.rearrange("p (o e) -> p o e", o=1).to_broadcast([P, NCH, 4]))
        S.tt(gall[:], gall[:], bcn(dtb[:]), ALU.add)
        S.act(gall[:], gall[:], AF.Exp)
        S.act(gall[:], gall[:], AF.Ln, bias=C.ones[:, 0:1])
        S.act(aneg[:], aneg[:], AF.Exp)
        S.ts(aneg[:], aneg[:], -1.0, ALU.mult)
        S.tt(gall[:], gall[:], bcn(aneg[:]), ALU.mult)
        S.act(ball[:], ball[:], AF.Sigmoid)
        S.ts(negb[:], ball[:], -1.0, ALU.mult)
        BA = [S.psum("BA%d" % h, [P, 512]) for h in range(2)]
        B1 = [S.psum("B1%d" % h, [P, 512]) for h in range(2)]
        B2 = [S.psum("B2%d" % h, [P, 512]) for h in range(2)]
        B3 = [S.psum("B3%d" % h, [P, 512]) for h in range(2)]
        Wk = []
        for h in range(2):
            W = NS()
            for nm in ("X", "Dm", "Dv", "Ds", "kbg", "kdec", "vb", "vnew", "oq", "o", "of_", "zt", "t1"):
                setattr(W, nm, S.sbuf("%s%d" % (nm, h), [P, P], F32))
            for nm in ("NA", "RA", "uw"):
                setattr(W, nm, S.sbuf("%s%d" % (nm, h), [P, 2 * P], F32))
            W.NR = [S.sbuf("NR%d%d" % (h, i), [P, 2 * P], F32) for i in range(2)]
            W.Xc = [S.sbuf("Xc%d%d" % (h, i), [P, P], F32) for i in range(2)]
            W.esm = S.sbuf("esm%d" % h, [P, 4], F32)
            W.bg = S.sbuf("bg%d" % h, [P, 1], F32)
            W.ss = S.sbuf("ss%d" % h, [P, 1], F32)
            W.oo = [S.sbuf("oo%d%d" % (h, i), [P, P], F32) for i in range(2)]
            Wk.append(W)
        state = [S.sbuf("state%d" % h, [P, P], F32) for h in range(2)]
        raw = S.sbuf("raw", [P, 6, 516], F32)
        acc = [S.sbuf("acc%d" % i, [P, 512], F32) for i in range(2)]
        sqb = S.sbuf("sqb", [P, 512], F32)
        lnb = S.sbuf("lnb", [P, 512], F32)
        rsb = S.sbuf("rsb", [P, 512], F32)
        qkv = [S.sbuf("qkv%d" % i, [P, 6, 512], F32) for i in range(2)]
        ofb = [[S.sub("of%d_%d" % (c, h), ofd.t[c][:, h * P:(h + 1) * P]) for h in range(2)] for c in range(NCH)]

        def prep(src, t0, n, dst):
            S.dma("sp", raw[:, :, :n + 4], src[:, :, t0:t0 + n + 4].rearrange("c p t -> p c t"))
            for c in range(6):
                a = acc[c % 2]
                S.ts(a[:, :n], raw[:, c, 0:n], cw[:, c, 0:1], ALU.mult)
                for j in range(1, 5):
                    S.stt(a[:, :n], raw[:, c, j:j + n], cw[:, c, j:j + 1], a[:, :n], ALU.mult, ALU.add)
                if c >= 4:
                    S.act(dst[:, c, :n], a[:, :n], AF.Silu)
                else:
                    S.act(a[:, :n], a[:, :n], AF.Silu)
                    S.act(sqb[:, :n], a[:, :n], AF.Square)
                    pb = B3[c % 2]
                    S.matmul(pb[:, :n], C.ones[:], sqb[:, :n])
                    S.act(lnb[:, :n], pb[:, :n], AF.Ln, bias=C.eps[:, 0:1])
                    S.act(rsb[:, :n], lnb[:, :n], AF.Exp, scale=-0.5)
                    S.stt(dst[:, c, :n], a[:, :n], (128.0 ** -0.5) if c < 2 else 1.0, rsb[:, :n], ALU.mult, ALU.mult)

        def unit(hl, dr, gp, qv, kv, vv):
            col = dr * 2 + hl
            g = gall[:, gp, col:col + 1]
            nb = negb[:, gp, col:col + 1]
            bt = ball[:, gp, col:col + 1]
            W = Wk[hl]
            bA, b1, b2, b3 = BA[hl], B1[hl], B2[hl], B3[hl]
            Tri, Xm, Val, SVal = (M["le"], M["gt"], M["ge"], M["gt"]) if dr == 0 else (M["ge"], M["lt"], M["le"], M["lt"])
            S.ts(W.X[:], Xm[:], g, ALU.mult)
            S.matmul(bA[:, 0:128], Tri[:], W.X[:])
            S.matmul(bA[:, 128:129], Tri[:], g)
            S.matmul(bA[:, 129:130], Xm[:], g)
            S.matmul(bA[:, 130:131], halfA[:], g)
            S.matmul(bA[:, 131:132], halfB[:], g)
            S.matmul(b1[:, 0:128], kv, kv)
            S.matmul(b1[:, 128:256], qv, kv)
            S.transpose(bA[:, 256:384], kv, C.ident[:])
            S.transpose(bA[:, 384:512], vv, C.ident[:])
            yield
            S.act(W.Dm[:], bA[:, 0:128], AF.Exp)
            S.act(W.esm[:], bA[:, 128:132], AF.Exp)
            S.tt(W.bg[:], W.esm[:, 0:1], bt, ALU.mult)
            S.act(W.kdec[:], bA[:, 256:384], AF.Identity, scale=W.esm[:, 1:2])
            S.act(W.vb[:], bA[:, 384:512], AF.Identity, scale=bt)
            S.act(W.kbg[:], bA[:, 256:384], AF.Identity, scale=W.bg[:, 0:1])
            S.tt(W.Dv[:], W.Dm[:], Val[:], ALU.mult)
            S.tt(W.Ds[:], W.Dm[:], SVal[:], ALU.mult)
            S.stt(W.NA[:, 0:128], b1[:, 0:128], nb, W.Ds[:], ALU.mult, ALU.mult)
            S.tt(W.NA[:, 128:256], b1[:, 128:256], W.Dv[:], ALU.mult)
            yield
            S.transpose(b1[:, 256:384], W.NA[:, 0:128], C.ident[:])
            S.transpose(b1[:, 384:512], W.NA[:, 128:256], C.ident[:])
            S.copy(W.RA[:], b1[:, 256:512])
            X = W.Xc[0]
            S.tt(X[:], W.RA[:, 0:128], C.ident[:], ALU.add)
            yield
            Ncur = W.NA[:, 0:128]
            Rcur = W.RA[:, 0:128]
            for lev in range(5):
                NR = W.NR[lev % 2]
                S.matmul(b2[:, 0:128], Rcur, Ncur)
                if lev < 4:
                    S.matmul(b2[:, 128:256], Ncur, Rcur)
                    S.copy(NR[:], b2[:, 0:256])
                else:
                    S.copy(NR[:, 0:128], b2[:, 0:128])
                yield
                S.matmul(b2[:, 256:384], NR[:, 0:128], X[:])
                Xn = W.Xc[(lev + 1) % 2]
                S.tt(Xn[:], X[:], b2[:, 256:384], ALU.add)
                X = Xn
                Ncur = NR[:, 0:128]
                Rcur = NR[:, 128:256]
                yield
            S.matmul(b3[:, 0:128], X[:], W.vb[:])
            S.matmul(b3[:, 128:256], W.kbg[:], X[:])
            S.copy(W.uw[:], b3[:, 0:256])
            yield
            blocks = [(0, 64), (64, 128)] if dr == 0 else [(64, 128), (0, 64)]
            Sst = state[hl]
            for bi, (r0, r1) in enumerate(blocks):
                reg = b3[:, 256:512] if bi == 0 else b3[:, 0:256]
                S.matmul(reg[:, 0:128], W.uw[:, 128:256], Sst[:])
                S.matmul(reg[:, 128:256], qv, Sst[:])
                S.tt(W.vnew[r0:r1, :], W.uw[r0:r1, 0:128], reg[r0:r1, 0:128], ALU.subtract)
                S.ts(W.oq[r0:r1, :], reg[r0:r1, 128:256], W.esm[r0:r1, 0:1], ALU.mult)
                yield
                S.matmul(b1[:, 0:128], W.kdec[r0:r1, :], W.vnew[r0:r1, :])
                egX = W.esm[:, 2:3] if r0 == 0 else W.esm[:, 3:4]
                S.stt(Sst[:], Sst[:], egX, b1[:, 0:128], ALU.mult, ALU.add)
                yield
            S.matmul(b1[:, 128:256], W.RA[:, 128:256], W.vnew[:])
            S.tt(W.o[:], W.oq[:], b1[:, 128:256], ALU.add)
            if dr == 0:
                S.dma("pool", ofb[gp][hl][:], W.o[:])
            else:
                S.dma("sp", W.of_[:], ofb[gp][hl][:])
                S.dma("sp", W.zt[:], zd[:, gp, hl * P:(hl + 1) * P])
                S.tt(W.o[:], W.o[:], W.of_[:], ALU.add)
                S.act(W.t1[:], W.o[:], AF.Square, accum_out=W.ss[:, 0:1])
                yield
                S.act(W.ss[:], W.ss[:], AF.Ln, scale=1.0 / 128.0, bias=C.eps[:, 0:1])
                S.act(W.ss[:], W.ss[:], AF.Exp, scale=-0.5)
                S.act(W.zt[:], W.zt[:], AF.Silu)
                S.stt(W.t1[:], W.o[:], W.ss[:, 0:1], nw[:], ALU.mult, ALU.mult)
                oo = W.oo[gp % 2]
                S.tt(oo[:], W.t1[:], W.zt[:], ALU.mult)
                S.dma("pool", S.sub("oa", oa.t[gp][:, hl * P:(hl + 1) * P])[:], oo[:])
            yield

        for dr in range(2):
            for h in range(2):
                S.memset(state[h][:], 0.0)
            segs = [(qc, 0, LC), (ql, LC, L)]
            tl = []
            for (src, base, seglen) in segs:
                tt_ = [(src, base, t0, min(512, seglen - t0)) for t0 in range(0, seglen, 512)]
                if dr == 1:
                    tt_ = tt_[::-1]
                tl += tt_
            for ti, (src, base, t0, n) in enumerate(tl):
                dst = qkv[ti % 2]
                prep(src, t0, n, dst)
                prs = list(range(n // P))
                if dr == 1:
                    prs = prs[::-1]
                for pi in prs:
                    gp = (base + t0) // P + pi
                    sl = slice(pi * P, (pi + 1) * P)
                    gens = [unit(h, dr, gp, dst[:, 0 + h, sl], dst[:, 2 + h, sl], dst[:, 4 + h, sl]) for h in range(2)]
                    alive = [True, True]
                    while any(alive):
                        for h in range(2):
                            if alive[h]:
                                try:
                                    next(gens[h])
                                except StopIteration:
                                    alive[h] = False
        S.barrier()
    return nc


NFM = 36
NTK = 1568
GRP = [[0, 1], [2, 3], [4, 5], [6, 7]]


def build_fused(L, LC, depth=2):
    NT, NCX = L // 2, LC // 2
    TT = NT + NCX
    LT = L + LC
    NCH = LT // P
    NCC = LC // P
    LK = LT
    NKC = LK // P
    NLC = NT // P
    assert NCX == P
    nc = bass.Bass("TRN2", target_bir_lowering=False)
    with ExitStack() as st:
        S = Sched(nc, st)
        ccsem = st.enter_context(nc.semaphore("ccsem"))
        cc = [0]
        xT = S.dram("xT", [P, KC, TT], F32, kind="ExternalInput")
        cv = S.dram("cv", [P, KC, 2], F32, kind="ExternalInput")
        selv = S.dram("selv", [P, 2], F32, kind="ExternalInput")
        yT = S.dram("yT", [P, KC, TT], F32, kind="ExternalOutput")
        Wl = []
        for i in range(depth):
            W = NS()
            sfx = "_%d" % i
            W.wada1 = S.dram("wada1" + sfx, [D, 5 * D], F32, kind="ExternalInput")
            W.bada1 = S.dram("bada1" + sfx, [P, 40], F32, kind="ExternalInput")
            W.wada2 = S.dram("wada2" + sfx, [D, 6 * D], F32, kind="ExternalInput")
            W.bada2 = S.dram("bada2" + sfx, [P, 48], F32, kind="ExternalInput")
            W.ng = S.dram("ng" + sfx, [P, 48], F32, kind="ExternalInput")
            W.w1a = S.dram("w1a" + sfx, [D, 2 * DFF], F32, kind="ExternalInput")
            W.w2a = S.dram("w2a" + sfx, [DFF, D], F32, kind="ExternalInput")
            W.w1b = S.dram("w1b" + sfx, [D, 2 * DFF], F32, kind="ExternalInput")
            W.w2b = S.dram("w2b" + sfx, [DFF, D], F32, kind="ExternalInput")
            W.win = S.dram("win" + sfx, [D, NFM * P + NTK], F32, kind="ExternalInput")
            W.wg = S.dram("wg" + sfx, [D, 3 * D], F32, kind="ExternalInput")
            W.wb = S.dram("wb" + sfx, [1536, D], F32, kind="ExternalInput")
            W.wo = S.dram("wo" + sfx, [D, D], F32, kind="ExternalInput")
            W.snw = S.dram("snw" + sfx, [P, 4], F32, kind="ExternalInput")
            W.lam = S.dram("lam" + sfx, [P, 256], F32, kind="ExternalInput")
            W.dnw = S.dram("dnw" + sfx, [P, 1], F32, kind="ExternalInput")
            W.li = S.dram("li" + sfx, [P, 1], F32, kind="ExternalInput")
            W.scw = S.dram("scw" + sfx, [P, 4, 5], F32, kind="ExternalInput")
            W.scb = S.dram("scb" + sfx, [P, 4], F32, kind="ExternalInput")
            W.sdtb = S.dram("sdtb" + sfx, [P, 8], F32, kind="ExternalInput")
            W.salog = S.dram("salog" + sfx, [P, 8], F32, kind="ExternalInput")
            W.sdsk = S.dram("sdsk" + sfx, [P, 4], F32, kind="ExternalInput")
            W.gcw = S.dram("gcw" + sfx, [P, 6, 5], F32, kind="ExternalInput")
            W.galog = S.dram("galog" + sfx, [P, 4], F32, kind="ExternalInput")
            W.gdtb = S.dram("gdtb" + sfx, [P, 4], F32, kind="ExternalInput")
            W.gnw = S.dram("gnw" + sfx, [P, P], F32, kind="ExternalInput")
            Wl.append(W)
        Xs = S.dram("Xs", [P, KC, TT], F32)
        X1 = S.dram("X1s", [P, KC, TT], F32)
        X2 = S.dram("X2s", [P, KC, TT], F32)
        NB = NT // 256
        PTL = nc.dram_tensor("PTL", [NFM, P, NT], F32)
        PTLG = nc.dram_tensor("PTLG", [NFM, 2, P, NT], F32)
        PTC = nc.dram_tensor("PTC", [NFM, P, NCX], F32)
        PTCG = nc.dram_tensor("PTCG", [2, 2, 18, P, NCX], F32)
        PKL = nc.dram_tensor("PKL", [NB, 256, NTK], F32)
        PKLG = nc.dram_tensor("PKLG", [NB, 2, 256, NTK], F32)
        PKC = nc.dram_tensor("PKC", [NCX, NTK], F32)
        PKCG = nc.dram_tensor("PKCG", [2, NCX, NTK], F32)
        MOL = nc.dram_tensor("MOL", [6, 2, P, NT], F32)
        MOLG = nc.dram_tensor("MOLG", [6, 2, 2, P, NT], F32)
        MOC = nc.dram_tensor("MOC", [6, 2, P, NCX], F32)
        MOCG = nc.dram_tensor("MOCG", [2, 6, 2, P, NCX], F32)
        OF = S.dram("OFs", [NCH, P, 256], F32)
        OFD = [S.dram("OFD%d" % d_, [NCH, P, 256], F32) for d_ in range(2)]

        def mo_dst(c0, c1, s_, off, n):
            if off >= NT:
                return MOC.ap()[c0:c1, s_, :, off - NT:off - NT + n]
            return MOL.ap()[c0:c1, s_, :, off:off + n]

        def dsub(ap):
            return Buf("u", ap, "dram")[:] if False else View(Buf("u", ap, "dram"), ap)

        tiles = mk_tiles(NT, NCX)
        C = mk_consts(S, nc)
        sel = S.sbuf("sel", [P, 2], F32)
        S.dma("sp", sel[:], selv[:])
        dummy = S.sbuf("dummy", [P, 1], F32)

        def blend(dst, alt):
            S.ts(dst, dst, sel[:, 0:1], ALU.mult)
            S.stt(dst, alt, sel[:, 1:2], dst, ALU.mult, ALU.add)

        def gather_many(pairs):
            S.barrier()
            for (i_ap, o_ap) in pairs:
                cc[0] += 1
                nc.gpsimd.collective_compute("AllGather", ALU.bypass, replica_groups=GRP, ins=[i_ap], outs=[o_ap]).then_inc(ccsem, 1)
            nc.gpsimd.wait_ge(ccsem, cc[0])
            S.memset(dummy[:], 0.0, e="pool")
            S.barrier()

        def gather_P():
            pr = [(PTL.ap()[c], PTLG.ap()[c].rearrange("r p t -> (r p) t")) for c in range(NFM)]
            pr += [(PTC.ap()[h * 18:(h + 1) * 18].rearrange("c p t -> (c p) t"), PTCG.ap()[h].rearrange("r c p t -> (r c p) t")) for h in range(2)]
            pr += [(PKL.ap()[b], PKLG.ap()[b].rearrange("r t e -> (r t) e")) for b in range(NB)]
            pr += [(PKC.ap(), PKCG.ap().rearrange("r t e -> (r t) e"))]
            gather_many(pr)

        def gather_M():
            pr = [(MOL.ap()[c, s_], MOLG.ap()[c, s_].rearrange("r p t -> (r p) t")) for c in range(6) for s_ in range(2)]
            pr += [(MOC.ap().rearrange("c s p t -> (c s p) t"), MOCG.ap().rearrange("r c s p t -> (r c s p) t"))]
            gather_many(pr)

        def tokpos(gc):
            if gc < NCC:
                return gc, NT
            t = (gc - NCC) * P
            return t // NT, t % NT

        def mk_ps():
            PS = NS()
            PS.ss = S.psum("ps_ss", [P, 512])
            PS.ss2 = S.psum("ps_ss2", [P, 512])
            PS.g = [S.psum("ps_g%d" % i, [P, 512]) for i in range(2)]
            PS.u = [S.psum("ps_u%d" % i, [P, 512]) for i in range(2)]
            PS.y = [S.psum("ps_y%d" % i, [P, 512]) for i in range(2)]
            return PS

        def bc(v):
            return View(v.buf, v.ap.rearrange("p (c o) -> p c o", o=1).to_broadcast([P, KC, 2]))

        def ph_R1(W, xin, x1t):
            with S.scope():
                alloc_small(S, C)
                PS = mk_ps()
                mods = S.sbuf("mods", [P, 40, 2], F32)
                ngt = S.sbuf("ngt", [P, 6, KC], F32)
                S.dma("sp", ngt[:], W.ng[:].rearrange("p (m c) -> p m c", c=KC))
                A1 = S.sbuf("A1", [P, KC, 2], F32)
                G1 = S.sbuf("G1", [P, KC, 2], F32)
                A2 = S.sbuf("A2", [P, KC, 2], F32)
                with S.scope():
                    stg = [S.sbuf("stgm%d" % i, [P, 5 * D], F32) for i in range(KC)]
                    compute_mods(S, C, cv, W.wada1, W.bada1, 5, stg, PS.g[0], mods)
                S.stt(A1[:], mods[:, 8:16, :], 1.0, bc(ngt[:, 0, :]), ALU.add, ALU.mult)
                S.stt(G1[:], mods[:, 16:24, :], 0.5, bc(ngt[:, 1, :]), ALU.mult, ALU.mult)
                S.stt(A2[:], mods[:, 32:40, :], 1.0, bc(ngt[:, 2, :]), ALU.add, ALU.mult)
                B1 = mods[:, 0:8, :]
                B2 = mods[:, 24:32, :]
                with S.scope():
                    w1b = S.sbuf("w1b", [P, KC, 2 * DFF], BF16)
                    w2b = S.sbuf("w2b", [P, FC, D], BF16)
                    with S.scope():
                        stages = [S.sbuf("wst%d" % i, [P, 2048], F32) for i in range(3)]
                        load_w(S, W.w1a, w1b, D, 2 * DFF, stages)
                        load_w(S, W.w2a, w2b, DFF, D, stages)
                    alloc_ffn_work(S, C)
                    ffn_sweep(S, C, tiles, lambda j: xin[j][:], lambda j: x1t[j][:], w1b, w2b, A1[:], B1, G1[:], PS)
                with S.scope():
                    NW = NFM * P + NTK
                    winb = S.sbuf("winb", [P, KC, NW], BF16)
                    with S.scope():
                        stages = [S.sbuf("wst%d" % i, [P, 2048], F32) for i in range(3)]
                        load_w(S, W.win, winb, D, NW, stages)
                    xt2 = [S.sbuf("xq%d" % i, [P, KC, 512], F32) for i in range(2)]
                    h = S.sbuf("h2", [P, KC, 512], BF16)
                    ost = [S.sbuf("ost%d" % i, [P, 4, 512], F32) for i in range(3)]
                    tst = [S.sbuf("tst%d" % i, [P, NTK], F32) for i in range(2)]
                    pps = PS.g + PS.u + PS.y
                    gi = 0
                    ti = 0
                    for j, (s0, n, col) in enumerate(tiles):
                        xt = xt2[j % 2]
                        S.dma("sp", xt[:, :, :n], x1t[j][:])
                        norm_mod(S, C, xt, n, A2[:], B2, col, h, PS.ss)
                        for c0 in range(0, NFM, 4):
                            nn = min(4, NFM - c0)
                            o = ost[gi % 3]
                            gi += 1
                            for cc_ in range(nn):
                                pp = pps[(c0 + cc_) % 6]
                                for k in range(KC):
                                    S.matmul(pp[:, :n], winb[:, k, (c0 + cc_) * P:(c0 + cc_ + 1) * P], h[:, k, :n],
                                             start=(k == 0), stop=(k == KC - 1))
                                S.copy(o[:, cc_, :n], pp[:, :n], e=("act" if cc_ % 2 else "dve"))
                            pdst = PTL.ap()[c0:c0 + nn, :, s0:s0 + n] if col == 0 else PTC.ap()[c0:c0 + nn, :, 0:n]
                            S.dma("pool", dsub(pdst.rearrange("c p t -> p c t")), o[:, :nn, :n])
                        for sb in range(n // P):
                            tt_ = tst[ti % 2]
                            ti += 1
                            for q, c0 in enumerate(range(0, NTK, 512)):
                                w = min(512, NTK - c0)
                                pp = pps[q % 6]
                                for k in range(KC):
                                    S.matmul(pp[:, :w], h[:, k, sb * P:(sb + 1) * P], winb[:, k, NFM * P + c0:NFM * P + c0 + w],
                                             start=(k == 0), stop=(k == KC - 1))
                                S.copy(tt_[:, c0:c0 + w], pp[:, :w], e=("act" if q % 2 else "dve"))
                            trow = s0 + sb * P
                            kdst = PKL.ap()[trow // 256, trow % 256:trow % 256 + P, :] if col == 0 else PKC.ap()[0:P, :]
                            S.dma("pool", dsub(kdst), tt_[:])

        def lat_rc(t0):
            return t0 // NT, t0 % NT

        def load_fm(q, dst, alt, c0, nch, seg, t0, n, halo):
            seglen = L if seg == "lat" else LC
            a, b = max(0, t0 - halo), min(seglen, t0 + n + halo)
            if halo and (t0 - halo < 0):
                S.memset(dst[:, :, 0:halo], 0.0)
                S.memset(alt[:, :, 0:halo], 0.0)
            if halo and (t0 + n + halo > seglen):
                S.memset(dst[:, :, n + halo:n + 2 * halo], 0.0)
                S.memset(alt[:, :, n + halo:n + 2 * halo], 0.0)
            pieces = []
            per = NT if seg == "lat" else NCX
            base = 0 if seg == "lat" else NT
            p = a
            while p < b:
                r = p // per
                e = min(b, (r + 1) * per)
                pieces.append((r, p % per, e - p, p - (t0 - halo)))
                p = e
            for g, tgt in ((0, dst), (1, alt)):
                for (r, col0, ln, d0) in pieces:
                    if seg == "lat":
                        sap = PTLG.ap()[g * 18 + c0:g * 18 + c0 + nch, r, :, col0:col0 + ln]
                    else:
                        sap = PTCG.ap()[g, r, c0:c0 + nch, :, col0:col0 + ln]
                    S.dma(q, tgt[:, :, d0:d0 + ln], dsub(sap.rearrange("c p t -> p c t")))
            blend(dst[:, :, :], alt[:, :, :])

        def load_tok(q, dst, alt, gc, e0, ne):
            s, off = tokpos(gc)
            for g, tgt in ((0, dst), (1, alt)):
                if off >= NT:
                    sap = PKCG.ap()[s, 0:P, g * 784 + e0:g * 784 + e0 + ne]
                else:
                    sap = PKLG.ap()[off // 256, s, off % 256:off % 256 + P, g * 784 + e0:g * 784 + e0 + ne]
                S.dma(q, tgt, dsub(sap))
            blend(dst, alt)

        def load_small(smallst, alt):
            for g, tgt in ((0, smallst), (1, alt)):
                for r in range(2):
                    S.dma("sp", tgt[:, r, :], dsub(PKCG.ap()[r, 0:P, g * 784 + 768:g * 784 + 784]))
                    for bq in range(NB):
                        c_ = NCC + r * NLC + 2 * bq
                        S.dma("sp" if bq % 2 else "pool", tgt[:, c_:c_ + 2, :],
                              dsub(PKLG.ap()[bq, r, :, g * 784 + 768:g * 784 + 784].rearrange("(c p) e -> p c e", p=P)))
            blend(smallst[:], alt[:])

        def ph_Mdiff(W):
            with S.scope():
                C.sq = [S.sbuf("sq%d" % i, [P, 512], F32) for i in range(2)]
                C.lnt = C.sq[0]
                onesb = S.sbuf("onesb", [P, P], BF16)
                S.memset(onesb[:], 1.0)
                Q = [S.sbuf("Q%d" % h, [P, L], BF16) for h in range(2)]
                QC = [S.sbuf("QC%d" % h, [P, LC], BF16) for h in range(2)]
                K = [S.sbuf("K%d" % h, [P, LK], BF16) for h in range(2)]
                V = S.sbuf("V", [P, NKC, 256], BF16)
                lam = S.sbuf("lamt", [P, 4, 64], F32)
                S.dma("sp", lam[:], W.lam[:].rearrange("p (a b) -> p a b", b=64))
                nw = S.sbuf("nwt", [P, 1], F32)
                li = S.sbuf("lit", [P, 1], F32)
                S.dma("sp", nw[:], W.dnw[:])
                S.dma("sp", li[:], W.li[:])
                pr = S.sbuf("pr", [P, 2, 64], F32)
                s12 = S.sbuf("s12", [P, 2], F32)
                S.tt(pr[:, 0, :], lam[:, 0, :], lam[:, 1, :], ALU.mult)
                S.tt(pr[:, 1, :], lam[:, 2, :], lam[:, 3, :], ALU.mult)
                S.reduce(s12[:], pr[:], ALU.add)
                e12 = S.sbuf("e12", [P, 2], F32)
                S.act(e12[:], s12[:], AF.Exp)
                neglam = S.sbuf("neglam", [P, 1], F32)
                S.tt(neglam[:], e12[:, 1:2], e12[:, 0:1], ALU.subtract)
                S.tt(neglam[:], neglam[:], li[:], ALU.subtract)
                sc2 = S.sbuf("sc2", [P, 1], F32)
                S.ts(sc2[:], li[:], -1.0, ALU.mult, 1.0, ALU.add)
                S.tt(sc2[:], sc2[:], nw[:], ALU.mult)
                with S.scope():
                    cosb = S.sbuf("cosb", [P, L], F32)
                    sinb = S.sbuf("sinb", [P, L], F32)
                    rope_tables(S, nc, C, L, cosb, sinb)
                    with S.scope():
                        a = [S.sbuf("la%d" % i, [P, 1, 512], F32) for i in range(2)]
                        a2 = [S.sbuf("la2%d" % i, [P, 1, 512], F32) for i in range(2)]
                        b = [S.sbuf("lb%d" % i, [P, 1, 512], F32) for i in range(2)]
                        b2 = [S.sbuf("lb2%d" % i, [P, 1, 512], F32) for i in range(2)]
                        vst = [S.sbuf("vst%d" % i, [P, 4, 256], F32) for i in range(2)]
                        vs2 = [S.sbuf("vs2%d" % i, [P, 4, 256], F32) for i in range(2)]
                        i = 0
                        for h in range(2):
                            for (cq, csw, dst) in ((6 + h, 8 + h, Q[h]), (10 + h, 12 + h, K[h])):
                                for c0 in range(0, L, 512):
                                    ta, tb = a[i % 2], b[i % 2]
                                    load_fm("sp", ta[:], a2[i % 2][:], cq, 1, "lat", c0, 512, 0)
                                    load_fm("pool", tb[:], b2[i % 2][:], csw, 1, "lat", c0, 512, 0)
                                    i += 1
                                    S.tt(ta[:, 0, :], ta[:, 0, :], cosb[:, c0:c0 + 512], ALU.mult)
                                    S.tt(tb[:, 0, :], tb[:, 0, :], sinb[:, c0:c0 + 512], ALU.mult, e="pool")
                                    S.tt(dst[:, c0:c0 + 512], ta[:, 0, :], tb[:, 0, :], ALU.add)
                            ta = a[i % 2]
                            load_fm("sp", ta[:, :, :LC], a2[i % 2][:, :, :LC], 10 + h, 1, "ctx", 0, LC, 0)
                            i += 1
                            S.copy(K[h][:, L:LK], ta[:, 0, :LC])
                            ta = a[i % 2]
                            load_fm("sp", ta[:, :, :LC], a2[i % 2][:, :, :LC], 6 + h, 1, "ctx", 0, LC, 0)
                            i += 1
                            S.copy(QC[h][:], ta[:, 0, :LC])
                        vi = 0
                        for r in range(2):
                            for bq in range(NB):
                                t, t2 = vst[vi % 2], vs2[vi % 2]
                                vi += 1
                                for g, tgt in ((0, t), (1, t2)):
                                    S.dma("sp", tgt[:, 0:2, :], dsub(PKLG.ap()[bq, r, :, g * 784:g * 784 + 256].rearrange("(c p) e -> p c e", p=P)))
                                blend(t[:, 0:2, :], t2[:, 0:2, :])
                                kc = r * NLC + 2 * bq
                                S.copy(V[:, kc:kc + 2, :], t[:, 0:2, :], e="pool")
                            t, t2 = vst[vi % 2], vs2[vi % 2]
                            vi += 1
                            for g, tgt in ((0, t), (1, t2)):
                                S.dma("sp", tgt[:, 0, :], dsub(PKCG.ap()[r, 0:P, g * 784:g * 784 + 256]))
                            blend(t[:, 0, :], t2[:, 0, :])
                            S.copy(V[:, L // P + r, :], t[:, 0, :], e="pool")
                ps_s = [[S.psum("ps_s%d%d" % (j, i), [P, 512]) for i in range(2)] for j in range(2)]
                ps_o = [S.psum("ps_o%d" % j, [P, 512]) for j in range(2)]
                ps_z = [S.psum("ps_z%d" % j, [P, 512]) for j in range(2)]
                pt = [[S.sbuf("pt%d%d" % (j, i), [P, 512], BF16) for i in range(2)] for j in range(2)]
                rz = [S.sbuf("rz%d" % j, [P, 512], F32) for j in range(2)]
                t0_ = S.sbuf("t0", [P, 512], F32)
                t1_ = S.sbuf("t1", [P, 512], F32)
                rstd = S.sbuf("rstd", [P, 512], F32)
                oo = [S.sbuf("oo%d" % i, [P, 512], F32) for i in range(2)]
                jobs = []
                for h in range(2):
                    for q0 in range(0, L, 512):
                        s_, off = lat_rc(q0)
                        jobs.append((h, Q[h][:, q0:q0 + 512], 512, 0, NKC, [(mo_dst(2 + h, 3 + h, s_, off, 512)[0], 0, 512)]))
                    jobs.append((h, QC[h][:], LC, L // P, NKC, [(mo_dst(2 + h, 3 + h, r, NT, NCX)[0], r * NCX, NCX) for r in range(2)]))
                zacc = [S.sbuf("zacc%d" % j, [P, 512], F32) for j in range(2)]
                zeng = ["dve", "dve"]
                for ji, (h, qv, n, kc0, kc1, outs) in enumerate(jobs):
                    def qk(kc):
                        for j in range(2):
                            S.matmul(ps_s[j][kc % 2][:, :n], K[h][j * 64:(j + 1) * 64, kc * P:(kc + 1) * P], qv[j * 64:(j + 1) * 64, :],
                                     start=True, stop=True)
                    qk(kc0)
                    for kc in range(kc0, kc1):
                        bi = kc % 2
                        for j in range(2):
                            S.act(pt[j][bi][:, :n], ps_s[j][bi][:, :n], AF.Exp, scale=0.125)
                        if kc + 1 < kc1:
                            qk(kc + 1)
                        for j in range(2):
                            S.matmul(ps_o[j][:, :n], V[:, kc, h * P:(h + 1) * P], pt[j][bi][:, :n], start=(kc == kc0), stop=(kc == kc1 - 1))
                            if kc == kc0:
                                S.copy(zacc[j][:, :n], pt[j][bi][:, :n], e=zeng[j])
                            else:
                                S.tt(zacc[j][:, :n], zacc[j][:, :n], pt[j][bi][:, :n], ALU.add, e=zeng[j])
                    for j in range(2):
                        S.matmul(ps_z[j][:, :n], C.ones[:], zacc[j][:, :n], start=True, stop=True)
                    for j in range(2):
                        S.recip(rz[j][:, :n], ps_z[j][:, :n])
                    S.tt(t0_[:, :n], ps_o[0][:, :n], rz[0][:, :n], ALU.mult)
                    S.tt(t1_[:, :n], ps_o[1][:, :n], rz[1][:, :n], ALU.mult)
                    S.stt(t0_[:, :n], t1_[:, :n], neglam[:, 0:1], t0_[:, :n], ALU.mult, ALU.add)
                    rms_rstd(S, C, lambda c: t0_[:, :n], n, 1, P, ps_s[0][0], rstd[:, :n])
                    o = oo[ji % 2]
                    S.tt(t1_[:, :n], t0_[:, :n], rstd[:, :n], ALU.mult)
                    S.ts(o[:, :n], t1_[:, :n], sc2[:, 0:1], ALU.mult)
                    for (oap, o0, on) in outs:
                        S.dma("pool", dsub(oap), o[:, o0:o0 + on])

        def ph_Mssd(W):
            with S.scope():
                tri = {0: tri_mask(S, nc, "tri_f", "le"), 1: tri_mask(S, nc, "tri_b", "ge")}
                strict = {0: tri_mask(S, nc, "str_f", "gt"), 1: tri_mask(S, nc, "str_b", "lt")}
                cw = S.sbuf("cw", [P, 4, 5], F32)
                cb = S.sbuf("cb", [P, 4], F32)
                S.dma("sp", cw[:], W.scw[:])
                S.dma("sp", cb[:], W.scb[:])
                xs_tok = S.sbuf("xs_tok", [P, NCH, 256], F32)
                B_tok = S.sbuf("B_tok", [P, NCH, P], F32)
                BT = S.sbuf("BT", [P, LT], F32)
                CT = S.sbuf("CT", [P, LT], F32)
                dtv = S.sbuf("dtv", [P, NCH, 8], F32)
                aall = S.sbuf("aall", [P, NCH, 8], F32)
                dtb = S.sbuf("dtb", [P, 8], F32)
                aneg = S.sbuf("aneg", [P, 8], F32)
                dsk = S.sbuf("dsk", [P, 4], F32)
                with S.scope():
                    sm1 = S.sbuf("sm1", [P, NCH, 16], F32)
                    sm2 = S.sbuf("sm2", [P, NCH, 16], F32)
                    load_small(sm1, sm2)
                    S.copy(dtv[:], sm1[:, :, 8:16])
                S.dma("sp", dtb[:], W.sdtb[:])
                S.dma("sp", aneg[:], W.salog[:])
                S.dma("sp", dsk[:], W.sdsk[:])
                S.tt(dtv[:], dtv[:], View(dtb, dtb.t[:].rearrange("p (o e) -> p o e", o=1).to_broadcast([P, NCH, 8])), ALU.add)
                S.act(dtv[:], dtv[:], AF.Exp)
                S.act(dtv[:], dtv[:], AF.Ln, bias=C.ones[:, 0:1])
                S.act(aneg[:], aneg[:], AF.Exp)
                S.ts(aneg[:], aneg[:], -1.0, ALU.mult)
                S.tt(aall[:], dtv[:], View(aneg, aneg.t[:].rearrange("p (o e) -> p o e", o=1).to_broadcast([P, NCH, 8])), ALU.mult)
                ps_t = [S.psum("ps_t%d" % i, [P, 512]) for i in range(2)]
                with S.scope():
                    raw = [S.sbuf("raw%d" % i, [P, 4, 516], F32) for i in range(2)]
                    raw2 = [S.sbuf("rawb", [P, 4, 516], F32)] * 2
                    acc = [S.sbuf("acc%d" % i, [P, 512], F32) for i in range(2)]
                    xsT = [S.sbuf("xsT%d" % i, [P, 512], F32) for i in range(2)]
                    segs = [("ctx", 0, LC), ("lat", LC, L)]
                    ti = 0
                    for (seg, base, seglen) in segs:
                        for t0 in range(0, seglen, 512):
                            n = min(512, seglen - t0)
                            r = raw[ti % 2]
                            load_fm("sp" if ti % 2 else "pool", r[:, :, :n + 4], raw2[ti % 2][:, :, :n + 4], 14, 4, seg, t0, n, 2)
                            ti += 1
                            for c in range(4):
                                a = acc[c % 2]
                                S.ts(a[:, :n], r[:, c, 0:n], cw[:, c, 0:1], ALU.mult)
                                for j in range(1, 5):
                                    S.stt(a[:, :n], r[:, c, j:j + n], cw[:, c, j:j + 1], a[:, :n], ALU.mult, ALU.add)
                                g0 = base + t0
                                if c < 2:
                                    S.act(xsT[c][:, :n], a[:, :n], AF.Silu, bias=cb[:, c:c + 1])
                                elif c == 2:
                                    S.act(BT[:, g0:g0 + n], a[:, :n], AF.Silu, bias=cb[:, c:c + 1])
                                else:
                                    S.act(CT[:, g0:g0 + n], a[:, :n], AF.Silu, bias=cb[:, c:c + 1])
                            for bl in range(n // P):
                                gc = (base + t0) // P + bl
                                pt = ps_t[bl % 2]
                                S.transpose(pt[:, 0:P], xsT[0][:, bl * P:(bl + 1) * P], C.ident[:])
                                S.transpose(pt[:, P:2 * P], xsT[1][:, bl * P:(bl + 1) * P], C.ident[:])
                                S.transpose(pt[:, 2 * P:3 * P], BT[:, gc * P:(gc + 1) * P], C.ident[:])
                                S.copy(xs_tok[:, gc, :], pt[:, 0:2 * P], e="dve")
                                S.copy(B_tok[:, gc, :], pt[:, 2 * P:3 * P], e="dve")
                ps_arg = [S.psum("ps_arg%d" % i, [P, 512]) for i in range(2)]
                ps_cb = S.psum("ps_cb", [P, 512])
                ps_yd = ps_t
                ps_std = [S.psum("ps_st%d" % i, [P, 512]) for i in range(2)]
                ps_sm = S.psum("ps_sm", [P, 512])
                X = [S.sbuf("X%d" % i, [P, 4, P], F32) for i in range(2)]
                LTt = [S.sbuf("LT%d" % i, [P, 4, P], F32) for i in range(2)]
                CBm = [S.sbuf("CBm%d" % i, [P, P], F32) for i in range(2)]
                scT = [S.sbuf("scT%d" % i, [P, 4, P], F32) for i in range(2)]
                sm = [S.sbuf("sm%d" % i, [P, 8], F32) for i in range(2)]
                eacs = [S.sbuf("eacs%d" % i, [P, 4], F32) for i in range(2)]
                edec = [S.sbuf("edec%d" % i, [P, 4], F32) for i in range(2)]
                etot = [S.sbuf("etot%d" % i, [P, 4], F32) for i in range(2)]
                dif = [S.sbuf("dif%d" % i, [P, 4], F32) for i in range(2)]
                xdt = [S.sbuf("xdt%d" % i, [P, 4, 64], F32) for i in range(2)]
                xdtd = [S.sbuf("xdtd%d" % i, [P, 4, 64], F32) for i in range(2)]
                STd = [S.sbuf("ST%d" % i, [P, 4, 64], F32) for i in range(2)]
                yt = [[S.sbuf("yt%d%d" % (d_, i), [P, 256], F32) for i in range(2)] for d_ in range(2)]
                y2 = [S.sbuf("y2%d" % i, [P, 256], F32) for i in range(2)]
                yfl = [S.sbuf("yfl%d" % i, [P, 256], F32) for i in range(2)]
                ybl = [S.sbuf("ybl%d" % i, [P, 256], F32) for i in range(2)]
                zt = [S.sbuf("zt%d" % i, [P, 256], F32) for i in range(2)]
                zt2 = [S.sbuf("ztb%d" % i, [P, 256], F32) for i in range(2)]
                oT = [S.sbuf("oT%d" % i, [P, 2, P], F32) for i in range(2)]
                yfb = [[S.sub("yf%d_%d" % (d_, c), OF.t[c][:, d_ * P:(d_ + 1) * P] if False else OFD[d_].t[c]) for c in range(NCH)] for d_ in range(2)]

                def bc4(v):
                    return View(v.buf, v.ap.rearrange("p (h o) -> p h o", o=1).to_broadcast([P, 4, 64]))

                def scan(dr):
                    order = list(range(NCH)) if dr == 0 else (list(range(NCC - 1, -1, -1)) + list(range(NCH - 1, NCC - 1, -1)))
                    ST = STd[dr]
                    ps_y = ps_yd[dr]
                    S.memset(ST[:], 0.0)
                    for it, c in enumerate(order):
                        a4 = aall[:, c, dr * 4:(dr + 1) * 4]
                        for h in range(4):
                            S.ts(X[dr][:, h, :], strict[dr][:], aall[:, c, dr * 4 + h:dr * 4 + h + 1], ALU.mult)
                        for h in range(4):
                            S.matmul(ps_arg[dr][:, h * P:(h + 1) * P], X[dr][:, h, :], tri[dr][:])
                        yield
                        S.act(LTt[dr][:].rearrange("p h l -> p (h l)"), ps_arg[dr][:], AF.Exp)
                        S.matmul(ps_cb[:, 0:P], BT[:, c * P:(c + 1) * P], CT[:, c * P:(c + 1) * P])
                        S.matmul(ps_sm[:, 0:4], tri[dr][:], a4)
                        S.matmul(ps_sm[:, 4:8], C.ones[:], a4)
                        S.tt(CBm[dr][:], ps_cb[:, 0:P], tri[dr][:], ALU.mult)
                        S.copy(sm[dr][:], ps_sm[:, 0:8])
                        yield
                        S.tt(scT[dr][:], LTt[dr][:], View(CBm[dr], CBm[dr].t[:].rearrange("p (o l) -> p o l", o=1).to_broadcast([P, 4, P])), ALU.mult)
                        S.act(eacs[dr][:], sm[dr][:, 0:4], AF.Exp)
                        S.act(etot[dr][:], sm[dr][:, 4:8], AF.Exp)
                        S.tt(dif[dr][:], sm[dr][:, 4:8], sm[dr][:, 0:4], ALU.subtract)
                        S.act(edec[dr][:], dif[dr][:], AF.Exp)
                        xv = xs_tok[:, c, :].rearrange("p (h d) -> p h d", h=4)
                        S.tt(xdt[dr][:], xv, bc4(dtv[:, c, dr * 4:(dr + 1) * 4]), ALU.mult)
                        S.tt(xdtd[dr][:], xdt[dr][:], bc4(edec[dr][:]), ALU.mult)
                        yield
                        for h in range(4):
                            S.matmul(ps_y[:, h * 64:(h + 1) * 64], scT[dr][:, h, :], xdt[dr][:, h, :])
                        S.matmul(ps_y[:, 256:512], CT[:, c * P:(c + 1) * P], ST[:].rearrange("p h d -> p (h d)"))
                        S.matmul(ps_std[dr][:, 0:256], B_tok[:, c, :], xdtd[dr][:].rearrange("p h d -> p (h d)"))
                        yield
                        y = yt[dr][it % 2]
                        S.tt(y[:].rearrange("p (h d) -> p h d", h=4), ps_y[:, 256:512].rearrange("p (h d) -> p h d", h=4), bc4(eacs[dr][:]), ALU.mult)
                        S.tt(y[:], y[:], ps_y[:, 0:256], ALU.add)
                        S.tt(ST[:], ST[:], bc4(etot[dr][:]), ALU.mult)
                        S.tt(ST[:].rearrange("p h d -> p (h d)"), ST[:].rearrange("p h d -> p (h d)"), ps_std[dr][:, 0:256], ALU.add)
                        S.dma("pool", yfb[dr][c][:], y[:])
                        yield

                gens = [scan(0), scan(1)]
                alive = [True, True]
                while any(alive):
                    for d_ in range(2):
                        if alive[d_]:
                            try:
                                next(gens[d_])
                            except StopIteration:
                                alive[d_] = False
                for c in range(NCH):
                    bi = c % 2
                    S.dma("sp", yfl[bi][:], yfb[0][c][:])
                    S.dma("sp", ybl[bi][:], yfb[1][c][:])
                    load_tok("sp", zt[bi][:], zt2[bi][:], c, 512, 256)
                    o = y2[bi]
                    xv = xs_tok[:, c, :].rearrange("p (h d) -> p h d", h=4)
                    S.tt(o[:].rearrange("p (h d) -> p h d", h=4), xv, bc4(dsk[:]), ALU.mult)
                    S.tt(yfl[bi][:], yfl[bi][:], ybl[bi][:], ALU.add)
                    S.tt(o[:], o[:], yfl[bi][:], ALU.add)
                    S.act(zt[bi][:], zt[bi][:], AF.Silu)
                    S.tt(o[:], o[:], zt[bi][:], ALU.mult)
                    S.transpose(ps_cb[:, P:2 * P], o[:, 0:P], C.ident[:])
                    S.transpose(ps_cb[:, 2 * P:3 * P], o[:, P:2 * P], C.ident[:])
                    S.copy(oT[bi][:].rearrange("p a b -> p (a b)"), ps_cb[:, P:3 * P])
                    s_, off = tokpos(c)
                    S.dma("pool", dsub(mo_dst(4, 6, s_, off, P).rearrange("c p t -> p c t")), oT[bi][:])

        def ph_Mgdn(W):
            with S.scope():
                M = {k: tri_mask(S, nc, "m_" + k, k, blk=64) for k in ("le", "ge", "gt", "lt")}
                halfA = S.sbuf("halfA", [P, P], F32)
                halfB = S.sbuf("halfB", [P, P], F32)
                S.memset(halfA[:], 0.0)
                S.memset(halfB[:], 0.0)
                S.memset(halfA[0:64, :], 1.0)
                S.memset(halfB[64:128, :], 1.0)
                cw = S.sbuf("cw", [P, 6, 5], F32)
                S.dma("sp", cw[:], W.gcw[:])
                nw = S.sbuf("nw", [P, P], F32)
                S.dma("sp", nw[:], W.gnw[:])
                gall = S.sbuf("gall", [P, NCH, 4], F32)
                ball = S.sbuf("ball", [P, NCH, 4], F32)
                negb = S.sbuf("negb", [P, NCH, 4], F32)
                aneg = S.sbuf("aneg", [P, 4], F32)
                dtb = S.sbuf("dtb", [P, 4], F32)
                with S.scope():
                    sm1 = S.sbuf("sm1", [P, NCH, 16], F32)
                    sm2 = S.sbuf("sm2", [P, NCH, 16], F32)
                    load_small(sm1, sm2)
                    S.copy(gall[:], sm1[:, :, 0:4])
                    S.copy(ball[:], sm1[:, :, 4:8])
                S.dma("sp", aneg[:], W.galog[:])
                S.dma("sp", dtb[:], W.gdtb[:])

                def bcn(v):
                    return View(v.buf, v.ap.rearrange("p (o e) -> p o e", o=1).to_broadcast([P, NCH, 4]))
                S.tt(gall[:], gall[:], bcn(dtb[:]), ALU.add)
                S.act(gall[:], gall[:], AF.Exp)
                S.act(gall[:], gall[:], AF.Ln, bias=C.ones[:, 0:1])
                S.act(aneg[:], aneg[:], AF.Exp)
                S.ts(aneg[:], aneg[:], -1.0, ALU.mult)
                S.tt(gall[:], gall[:], bcn(aneg[:]), ALU.mult)
                S.act(ball[:], ball[:], AF.Sigmoid)
                S.ts(negb[:], ball[:], -1.0, ALU.mult)
                BA = [S.psum("BA%d" % h, [P, 512]) for h in range(4)]
                BD = [S.psum("BD%d" % h, [P, 512]) for h in range(4)]
                Wk = []
                for h in range(4):
                    Wn = NS()
                    for nm in ("X", "Dm", "Dv", "Ds", "kbg", "kdec", "vb", "vnew", "oq", "o", "of_", "zt", "zt2", "t1"):
                        setattr(Wn, nm, S.sbuf("%s%d" % (nm, h), [P, P], F32))
                    for nm in ("NA", "RA", "uw"):
                        setattr(Wn, nm, S.sbuf("%s%d" % (nm, h), [P, 2 * P], F32))
                    Wn.NR = [S.sbuf("NR%d%d" % (h, i), [P, 2 * P], F32) for i in range(2)]
                    Wn.Xc = [S.sbuf("Xc%d%d" % (h, i), [P, P], F32) for i in range(2)]
                    Wn.esm = S.sbuf("esm%d" % h, [P, 4], F32)
                    Wn.bg = S.sbuf("bg%d" % h, [P, 1], F32)
                    Wn.ss = S.sbuf("ss%d" % h, [P, 1], F32)
                    Wn.oo = [S.sbuf("oo%d%d" % (h, i), [P, P], F32) for i in range(2)]
                    Wn.oT = [S.sbuf("oT%d%d" % (h, i), [P, P], F32) for i in range(2)]
                    Wk.append(Wn)
                state = [S.sbuf("state%d" % h, [P, P], F32) for h in range(4)]
                rawd = [S.sbuf("raw%d" % i, [P, 6, 516], F32) for i in range(2)]
                raw2 = S.sbuf("rawb", [P, 6, 516], F32)
                acc = [S.sbuf("acc%d" % i, [P, 512], F32) for i in range(2)]
                sqb = S.sbuf("sqb", [P, 512], F32)
                lnb = S.sbuf("lnb", [P, 512], F32)
                rsb = S.sbuf("rsb", [P, 512], F32)
                qkv = [[S.sbuf("qkv%d%d" % (d_, i), [P, 6, 512], F32) for i in range(2)] for d_ in range(2)]
                ofb = [[[S.sub("of%d_%d_%d" % (d_, c, h), OFD[d_].t[c][:, h * P:(h + 1) * P]) for h in range(2)] for c in range(NCH)] for d_ in range(2)]

                def prep(dr, seg, t0, n, dst, q):
                    raw = rawd[dr]
                    load_fm(q, raw[:, :, :n + 4], raw2[:, :, :n + 4], 0, 6, seg, t0, n, 2)
                    for c in range(6):
                        a = acc[c % 2]
                        S.ts(a[:, :n], raw[:, c, 0:n], cw[:, c, 0:1], ALU.mult)
                        for j in range(1, 5):
                            S.stt(a[:, :n], raw[:, c, j:j + n], cw[:, c, j:j + 1], a[:, :n], ALU.mult, ALU.add)
                        if c >= 4:
                            S.act(dst[:, c, :n], a[:, :n], AF.Silu)
                        else:
                            S.act(a[:, :n], a[:, :n], AF.Silu)
                            S.act(sqb[:, :n], a[:, :n], AF.Square)
                            pb = BD[2 * dr + (c % 2)]
                            S.matmul(pb[:, :n], C.ones[:], sqb[:, :n])
                            S.act(lnb[:, :n], pb[:, :n], AF.Ln, bias=C.eps[:, 0:1])
                            S.act(rsb[:, :n], lnb[:, :n], AF.Exp, scale=-0.5)
                            S.stt(dst[:, c, :n], a[:, :n], (128.0 ** -0.5) if c < 2 else 1.0, rsb[:, :n], ALU.mult, ALU.mult)

                def unit(ch, hl, dr, gp, qv, kv, vv):
                    col = dr * 2 + hl
                    g = gall[:, gp, col:col + 1]
                    nb = negb[:, gp, col:col + 1]
                    bt = ball[:, gp, col:col + 1]
                    Wn = Wk[ch]
                    bA, b1, b2, b3 = BA[ch], BD[ch], BD[ch], BD[ch]
                    Tri, Xm, Val, SVal = (M["le"], M["gt"], M["ge"], M["gt"]) if dr == 0 else (M["ge"], M["lt"], M["le"], M["lt"])
                    S.ts(Wn.X[:], Xm[:], g, ALU.mult)
                    S.matmul(bA[:, 0:128], Tri[:], Wn.X[:])
                    S.matmul(bA[:, 128:129], Tri[:], g)
                    S.matmul(bA[:, 129:130], Xm[:], g)
                    S.matmul(bA[:, 130:131], halfA[:], g)
                    S.matmul(bA[:, 131:132], halfB[:], g)
                    S.matmul(b1[:, 0:128], kv, kv)
                    S.matmul(b1[:, 128:256], qv, kv)
                    S.transpose(bA[:, 256:384], kv, C.ident[:])
                    S.transpose(bA[:, 384:512], vv, C.ident[:])
                    yield
                    S.act(Wn.Dm[:], bA[:, 0:128], AF.Exp)
                    S.act(Wn.esm[:], bA[:, 128:132], AF.Exp)
                    S.tt(Wn.bg[:], Wn.esm[:, 0:1], bt, ALU.mult)
                    S.act(Wn.kdec[:], bA[:, 256:384], AF.Identity, scale=Wn.esm[:, 1:2])
                    S.act(Wn.vb[:], bA[:, 384:512], AF.Identity, scale=bt)
                    S.act(Wn.kbg[:], bA[:, 256:384], AF.Identity, scale=Wn.bg[:, 0:1])
                    S.tt(Wn.Dv[:], Wn.Dm[:], Val[:], ALU.mult)
                    S.tt(Wn.Ds[:], Wn.Dm[:], SVal[:], ALU.mult)
                    S.stt(Wn.NA[:, 0:128], b1[:, 0:128], nb, Wn.Ds[:], ALU.mult, ALU.mult)
                    S.tt(Wn.NA[:, 128:256], b1[:, 128:256], Wn.Dv[:], ALU.mult)
                    yield
                    S.transpose(b1[:, 256:384], Wn.NA[:, 0:128], C.ident[:])
                    S.transpose(b1[:, 384:512], Wn.NA[:, 128:256], C.ident[:])
                    S.copy(Wn.RA[:], b1[:, 256:512])
                    X = Wn.Xc[0]
                    S.tt(X[:], Wn.RA[:, 0:128], C.ident[:], ALU.add)
                    yield
                    Ncur = Wn.NA[:, 0:128]
                    Rcur = Wn.RA[:, 0:128]
                    for lev in range(5):
                        NR = Wn.NR[lev % 2]
                        S.matmul(b2[:, 0:128], Rcur, Ncur)
                        if lev < 4:
                            S.matmul(b2[:, 128:256], Ncur, Rcur)
                            S.copy(NR[:], b2[:, 0:256])
                        else:
                            S.copy(NR[:, 0:128], b2[:, 0:128])
                        yield
                        S.matmul(b2[:, 256:384], NR[:, 0:128], X[:])
                        Xn = Wn.Xc[(lev + 1) % 2]
                        S.tt(Xn[:], X[:], b2[:, 256:384], ALU.add)
                        X = Xn
                        Ncur = NR[:, 0:128]
                        Rcur = NR[:, 128:256]
                        yield
                    S.matmul(b3[:, 0:128], X[:], Wn.vb[:])
                    S.matmul(b3[:, 128:256], Wn.kbg[:], X[:])
                    S.copy(Wn.uw[:], b3[:, 0:256])
                    yield
                    blocks = [(0, 64), (64, 128)] if dr == 0 else [(64, 128), (0, 64)]
                    Sst = state[ch]
                    for bi, (r0, r1) in enumerate(blocks):
                        reg = b3[:, 256:512] if bi == 0 else b3[:, 0:256]
                        S.matmul(reg[:, 0:128], Wn.uw[:, 128:256], Sst[:])
                        S.matmul(reg[:, 128:256], qv, Sst[:])
                        S.tt(Wn.vnew[r0:r1, :], Wn.uw[r0:r1, 0:128], reg[r0:r1, 0:128], ALU.subtract)
                        S.ts(Wn.oq[r0:r1, :], reg[r0:r1, 128:256], Wn.esm[r0:r1, 0:1], ALU.mult)
                        yield
                        S.matmul(b1[:, 0:128], Wn.kdec[r0:r1, :], Wn.vnew[r0:r1, :])
                        egX = Wn.esm[:, 2:3] if r0 == 0 else Wn.esm[:, 3:4]
                        S.stt(Sst[:], Sst[:], egX, b1[:, 0:128], ALU.mult, ALU.add)
                        yield
                    S.matmul(b1[:, 128:256], Wn.RA[:, 128:256], Wn.vnew[:])
                    S.tt(Wn.o[:], Wn.oq[:], b1[:, 128:256], ALU.add)
                    S.dma("pool", ofb[dr][gp][hl][:], Wn.o[:])
                    yield

                for h in range(4):
                    S.memset(state[h][:], 0.0)
                segs = [("ctx", 0, LC), ("lat", LC, L)]
                tld = []
                for dr in range(2):
                    tl = []
                    for (seg, base, seglen) in segs:
                        tt_ = [(seg, base, t0, min(512, seglen - t0)) for t0 in range(0, seglen, 512)]
                        if dr == 1:
                            tt_ = tt_[::-1]
                        tl += tt_
                    tld.append(tl)
                for ti in range(len(tld[0])):
                    dsts = []
                    for dr in range(2):
                        (seg, base, t0, n) = tld[dr][ti]
                        dst = qkv[dr][ti % 2]
                        prep(dr, seg, t0, n, dst, "sp" if dr else "pool")
                        dsts.append(dst)
                    npairs = tld[0][ti][3] // P
                    for pj in range(npairs):
                        gens = []
                        for dr in range(2):
                            (seg, base, t0, n) = tld[dr][ti]
                            pi = pj if dr == 0 else npairs - 1 - pj
                            gp = (base + t0) // P + pi
                            sl = slice(pi * P, (pi + 1) * P)
                            for hl in range(2):
                                gens.append(unit(dr * 2 + hl, hl, dr, gp, dsts[dr][:, 0 + hl, sl], dsts[dr][:, 2 + hl, sl], dsts[dr][:, 4 + hl, sl]))
                        alive = [True] * 4
                        while any(alive):
                            for gi_ in range(4):
                                if alive[gi_]:
                                    try:
                                        next(gens[gi_])
                                    except StopIteration:
                                        alive[gi_] = False
                for gp in range(NCH):
                    for hl in range(2):
                        Wn = Wk[hl + 2 * (gp % 2)]
                        b1 = BD[hl + 2 * (gp % 2)]
                        S.dma("sp", Wn.o[:], ofb[0][gp][hl][:])
                        S.dma("sp", Wn.of_[:], ofb[1][gp][hl][:])
                        load_tok("sp", Wn.zt[:], Wn.zt2[:], gp, 256 + hl * P, P)
                        S.tt(Wn.o[:], Wn.o[:], Wn.of_[:], ALU.add)
                        S.act(Wn.t1[:], Wn.o[:], AF.Square, accum_out=Wn.ss[:, 0:1])
                        S.act(Wn.ss[:], Wn.ss[:], AF.Ln, scale=1.0 / 128.0, bias=C.eps[:, 0:1])
                        S.act(Wn.ss[:], Wn.ss[:], AF.Exp, scale=-0.5)
                        S.act(Wn.zt[:], Wn.zt[:], AF.Silu)
                        S.stt(Wn.t1[:], Wn.o[:], Wn.ss[:, 0:1], nw[:], ALU.mult, ALU.mult)
                        oo = Wn.oo[0]
                        S.tt(oo[:], Wn.t1[:], Wn.zt[:], ALU.mult)
                        S.transpose(b1[:, 256:384], oo[:], C.ident[:])
                        oT = Wn.oT[0]
                        S.copy(oT[:], b1[:, 256:384])
                        s_, off = tokpos(gp)
                        S.dma("pool", dsub(mo_dst(hl, hl + 1, s_, off, P)[0]), oT[:])

        def ph_R2(W, x1t, xout):
            with S.scope():
                alloc_small(S, C)
                PS = mk_ps()
                mods = S.sbuf("mods", [P, 48, 2], F32)
                ngt = S.sbuf("ngt", [P, 6, KC], F32)
                S.dma("sp", ngt[:], W.ng[:].rearrange("p (m c) -> p m c", c=KC))
                snt = S.sbuf("snt", [P, 4], F32)
                S.dma("sp", snt[:], W.snw[:])
                with S.scope():
                    stg = [S.sbuf("stgm%d" % i, [P, 6 * D], F32) for i in range(KC)]
                    compute_mods(S, C, cv, W.wada2, W.bada2, 6, stg, PS.g[0], mods)
                A2 = S.sbuf("A2", [P, KC, 2], F32)
                G3 = S.sbuf("G3", [P, KC, 2], F32)
                A4 = S.sbuf("A4", [P, KC, 2], F32)
                G5 = S.sbuf("G5", [P, KC, 2], F32)
                S.stt(A2[:], mods[:, 8:16, :], 1.0, bc(ngt[:, 2, :]), ALU.add, ALU.mult)
                S.tt(G3[:], mods[:, 16:24, :], bc(ngt[:, 3, :]), ALU.mult)
                S.stt(A4[:], mods[:, 32:40, :], 1.0, bc(ngt[:, 4, :]), ALU.add, ALU.mult)
                S.stt(G5[:], mods[:, 40:48, :], 0.5, bc(ngt[:, 5, :]), ALU.mult, ALU.mult)
                B2 = mods[:, 0:8, :]
                B4 = mods[:, 24:32, :]
                x2t = [S.sub("x2t%d" % j, X2.t[:, :, s0:s0 + n]) for j, (s0, n, col) in enumerate(tiles)]
                with S.scope():
                    wgb = S.sbuf("wgb", [P, KC, 3 * D], BF16)
                    wbb = S.sbuf("wbb", [P, 12, D], BF16)
                    wob = S.sbuf("wob", [P, KC, D], BF16)
                    with S.scope():
                        stages = [S.sbuf("wst%d" % i, [P, 2048], F32) for i in range(3)]
                        load_w(S, W.wg, wgb, D, 3 * D, stages)
                        load_w(S, W.wb, wbb, 1536, D, stages)
                        load_w(S, W.wo, wob, D, D, stages)
                    xt = S.sbuf("xm", [P, KC, 512], F32)
                    h = S.sbuf("hm", [P, KC, 512], BF16)
                    ost = S.sbuf("ostg", [P, 4, 512], F32)
                    ost2 = S.sbuf("ostg2", [P, 4, 512], F32)
                    ob16 = S.sbuf("ob16", [P, 12, 512], BF16)
                    yacc = S.sbuf("yacc", [P, 512], F32)
                    ybf = S.sbuf("ybf", [P, KC, 512], BF16)
                    yy = S.sbuf("yy", [P, KC, 512], F32)
                    gt = [S.sbuf("gt%d" % i, [P, 512], F32) for i in range(2)]
                    for j, (s0, n, col) in enumerate(tiles):
                        S.dma("sp", xt[:, :, :n], x1t[j][:])
                        norm_mod(S, C, xt, n, A2[:], B2, col, h, PS.ss)
                        for br in range(3):
                            for sc_, tgt in ((0, ost), (1, ost2)):
                                for r in range(2):
                                    if col == 0:
                                        sap = MOLG.ap()[2 * br:2 * br + 2, sc_, r, :, s0:s0 + n]
                                    else:
                                        sap = MOCG.ap()[r, 2 * br:2 * br + 2, sc_, :, 0:n]
                                    S.dma("sp" if r else "pool", tgt[:, 2 * r:2 * r + 2, :n], dsub(sap.rearrange("c p t -> p c t")))
                            blend(ost[:, :, :n], ost2[:, :, :n])
                            if br < 2:
                                S.copy(ob16[:, br * 4:(br + 1) * 4, :n], ost[:, :, :n], e="pool")
                            else:
                                rms_rstd(S, C, lambda c: ost[:, c, :n], n, 4, 512, PS.ss2, C.rstd2[:, :n])
                                for c in range(4):
                                    t = C.tmp[c % 2]
                                    S.tt(t[:, :n], ost[:, c, :n], C.rstd2[:, :n], ALU.mult)
                                    S.act(ob16[:, 8 + c, :n], t[:, :n], AF.Copy, scale=snt[:, c:c + 1])
                        for d in range(KC):
                            for br in range(3):
                                pg = PS.g[br % 2]
                                pu = PS.u[br % 2]
                                cg = br * KC + d
                                for k in range(KC):
                                    S.matmul(pg[:, :n], wgb[:, k, cg * P:(cg + 1) * P], h[:, k, :n], start=(k == 0), stop=(k == KC - 1))
                                for k in range(4):
                                    S.matmul(pu[:, :n], wbb[:, br * 4 + k, d * P:(d + 1) * P], ob16[:, br * 4 + k, :n], start=(k == 0), stop=(k == 3))
                                g = gt[br % 2]
                                S.act(g[:, :n], pg[:, :n], AF.Sigmoid)
                                if br == 0:
                                    S.tt(yacc[:, :n], g[:, :n], pu[:, :n], ALU.mult)
                                else:
                                    t = C.tmp[br % 2]
                                    S.tt(t[:, :n], g[:, :n], pu[:, :n], ALU.mult)
                                    if br == 1:
                                        S.tt(yacc[:, :n], yacc[:, :n], t[:, :n], ALU.add)
                                    else:
                                        S.tt(ybf[:, d, :n], yacc[:, :n], t[:, :n], ALU.add)
                        for d in range(KC):
                            py = PS.y[d % 2]
                            for k in range(KC):
                                S.matmul(py[:, :n], wob[:, k, d * P:(d + 1) * P], ybf[:, k, :n], start=(k == 0), stop=(k == KC - 1))
                            S.copy(yy[:, d, :n], py[:, :n], e="dve")
                        rms_rstd(S, C, lambda c: yy[:, c, :n], n, KC, D, PS.ss2, C.rstd2[:, :n])
                        for c in range(KC):
                            t = C.tmp[c % 2]
                            S.tt(t[:, :n], yy[:, c, :n], C.rstd2[:, :n], ALU.mult)
                            S.stt(xt[:, c, :n], t[:, :n], G3[:, c, col:col + 1], xt[:, c, :n], ALU.mult, ALU.add)
                        S.dma("pool", x2t[j][:], xt[:, :, :n])
                with S.scope():
                    w1b = S.sbuf("w1b", [P, KC, 2 * DFF], BF16)
                    w2b = S.sbuf("w2b", [P, FC, D], BF16)
                    with S.scope():
                        stages = [S.sbuf("wst%d" % i, [P, 2048], F32) for i in range(3)]
                        load_w(S, W.w1b, w1b, D, 2 * DFF, stages)
                        load_w(S, W.w2b, w2b, DFF, D, stages)
                    alloc_ffn_work(S, C)
                    ffn_sweep(S, C, tiles, lambda j: x2t[j][:], lambda j: xout[j][:], w1b, w2b, A4[:], B4, G5[:], PS)

        xin = [S.sub("xin%d" % j, xT.t[:, :, s0:s0 + n]) for j, (s0, n, col) in enumerate(tiles)]
        for i in range(depth):
            W = Wl[i]
            x1t = [S.sub("x1t%d_%d" % (i, j), X1.t[:, :, s0:s0 + n]) for j, (s0, n, col) in enumerate(tiles)]
            dbg = "Z"
            ph_R1(W, xin, x1t)
            if dbg >= "B":
                gather_P()
            if dbg >= "C":
                ph_Mdiff(W)
            if dbg >= "D":
                ph_Mssd(W)
            if dbg >= "E":
                ph_Mgdn(W)
            if dbg >= "F":
                gather_M()
            last = i == depth - 1
            dstT = yT if last else Xs
            xout = [S.sub("xo%d_%d" % (i, j), dstT.t[:, :, s0:s0 + n]) for j, (s0, n, col) in enumerate(tiles)]
            ph_R2(W, x1t, xout)
            xin = xout
        S.barrier()
    return nc


def fm(a):
    T, F = a.shape
    return np.ascontiguousarray(a.T.reshape(F // P, P, T).transpose(1, 0, 2))

def unfm(a):
    p, C, T = a.shape
    return np.ascontiguousarray(a.transpose(2, 1, 0).reshape(T, C * P))

def vec_fm(v):
    return np.ascontiguousarray(v.reshape(-1, P).T)

def r1_cols():
    sw = np.arange(512) ^ 1
    cols = []
    cols += list(range(0, 2048))
    cols += list(range(2064, 2576))
    cols += list(2064 + sw)
    cols += list(range(2576, 3088))
    cols += list(2576 + sw)
    cols += list(range(3088, 3600))
    cols += list(range(3600, 5136))
    cols += list(range(2048, 2064)) + list(range(5136, 5152)) + [0] * 96
    cols = np.array(cols)
    assert len(cols) == 49 * 128
    return cols

def r1_inputs(inp, i, core, NT, NCX, xcur, ctxcur):
    b, s = core // 2, core % 2
    tok = np.concatenate([xcur[b, s * NT:(s + 1) * NT], ctxcur[b, s * NCX:(s + 1) * NCX]], 0)
    cv = np.stack([inp["c"][b], inp["c_ctx"]], -1)
    cv = np.ascontiguousarray(cv.reshape(8, P, 2).transpose(1, 0, 2))
    return {
        "xT": fm(tok),
        "cv": cv,
        "wada": np.ascontiguousarray(inp["w_ada"][i][:, :5 * 1024]),
        "bada": vec_fm(inp["b_ada"][i][:5 * 1024]),
        "ng": np.ascontiguousarray(inp["norm_g"][i].reshape(6, 8, P).transpose(2, 0, 1).reshape(P, 48)),
        "w1": np.ascontiguousarray(inp["w_ffn_in"][i, 0]),
        "w2": np.ascontiguousarray(inp["w_ffn_out"][i, 0]),
        "win": np.ascontiguousarray(inp["w_in"][i][:, r1_cols()]),
    }

def r2_inputs(inp, i, core, x1T, oaT, obT, ocT):
    b = core // 2
    cv = np.stack([inp["c"][b], inp["c_ctx"]], -1)
    cv = np.ascontiguousarray(cv.reshape(8, P, 2).transpose(1, 0, 2))
    return {
        "x1T": x1T, "oaT": oaT, "obT": obT, "ocT": ocT, "cv": cv,
        "wada": np.ascontiguousarray(inp["w_ada"][i][:, 3 * 1024:]),
        "bada": vec_fm(inp["b_ada"][i][3 * 1024:]),
        "ng": np.ascontiguousarray(inp["norm_g"][i].reshape(6, 8, P).transpose(2, 0, 1).reshape(P, 48)),
        "snw": vec_fm(inp["ssd_norm_w"][i]),
        "wg": np.ascontiguousarray(inp["w_in"][i][:, 5152:8224]),
        "wb": np.ascontiguousarray(inp["w_branch"][i].reshape(1536, 1024)),
        "wo": np.ascontiguousarray(inp["w_out"][i]),
        "w1": np.ascontiguousarray(inp["w_ffn_in"][i, 1]),
        "w2": np.ascontiguousarray(inp["w_ffn_out"][i, 1]),
    }

def split_P(PT_cores, NT, NCX, b):
    a0, a1 = PT_cores[2 * b], PT_cores[2 * b + 1]
    lat = np.concatenate([a0[:, :, :NT], a1[:, :, :NT]], 2)
    cx = np.concatenate([a0[:, :, NT:], a1[:, :, NT:]], 2)
    return lat, cx

def mdiff_inputs(inp, i, core, lat, cx):
    b, hh = core // 2, core % 2
    hs = [2 * hh, 2 * hh + 1]
    L = lat.shape[2]; LC = cx.shape[2]
    qT = lat[[16 + h for h in hs]]
    qsT = lat[[20 + h for h in hs]]
    kT = np.concatenate([lat[[24 + h for h in hs]], cx[[24 + h for h in hs]]], 2)
    ksT = lat[[28 + h for h in hs]]
    qcT = cx[[16 + h for h in hs]]
    v = np.concatenate([lat[[32 + h for h in hs]], cx[[32 + h for h in hs]]], 2)
    LK = L + LC
    v = v.transpose(2, 0, 1).reshape(LK // P, P, 256).transpose(1, 0, 2)
    lam_init = 0.8 - 0.6 * np.exp(-0.3 * i)
    return {"qT": np.ascontiguousarray(qT), "qsT": np.ascontiguousarray(qsT), "kT": np.ascontiguousarray(kT),
            "ksT": np.ascontiguousarray(ksT), "qcT": np.ascontiguousarray(qcT), "v": np.ascontiguousarray(v),
            "lam": np.ascontiguousarray(np.broadcast_to(inp["diff_lambda"][i].reshape(1, 256), (P, 256))),
            "nw": np.ascontiguousarray(inp["diff_norm_w"][i].reshape(P, 1)),
            "li": np.full((P, 1), lam_init, np.float32)}

def pad2(a):
    return np.pad(a, ((0, 0), (0, 0), (2, 2)))

def tokmaj(a):
    Cc, p, T = a.shape
    return np.ascontiguousarray(a.transpose(2, 0, 1).reshape(T // P, P, Cc * P).transpose(1, 0, 2))

def mssd_inputs(inp, i, core, lat, cx):
    b, g = core // 2, core % 2
    ch = [40 + 2 * g, 41 + 2 * g, 44 + g, 46 + g]
    wcols = np.concatenate([np.arange(256 * g, 256 * g + 256), 512 + 128 * g + np.arange(128), 768 + 128 * g + np.arange(128)])
    cwv = inp["ssd_conv_w"][i][:, wcols]
    cbv = inp["ssd_conv_b"][i][wcols]
    zc = [36 + 2 * g, 37 + 2 * g]
    z = np.concatenate([tokmaj(cx[zc]), tokmaj(lat[zc])], 1)
    rows = [16 + d * 8 + 4 * g + h for d in range(2) for h in range(4)]
    dtl = lat[48][rows]; dtc = cx[48][rows]
    dt = np.concatenate([dtc, dtl], 1)
    LT = dt.shape[1]
    dt = np.ascontiguousarray(dt.T.reshape(LT // P, P, 8).transpose(1, 0, 2))
    hsel = [4 * g + h for h in range(4)]
    return {"xbcl": np.ascontiguousarray(pad2(lat[ch])), "xbcc": np.ascontiguousarray(pad2(cx[ch])),
            "cw": np.ascontiguousarray(cwv.T.reshape(4, P, 5).transpose(1, 0, 2)),
            "cb": np.ascontiguousarray(cbv.reshape(4, P).T),
            "z": z, "dt": dt,
            "dtb": np.ascontiguousarray(np.broadcast_to(inp["ssd_dt_bias"][i][:, hsel].reshape(1, 8), (P, 8))),
            "alog": np.ascontiguousarray(np.broadcast_to(inp["ssd_a_log"][i][:, hsel].reshape(1, 8), (P, 8))),
            "dskip": np.ascontiguousarray(np.broadcast_to(inp["ssd_d"][i][hsel].reshape(1, 4), (P, 4)))}

def mgdn_inputs(inp, i, core, lat, cx):
    b, hh = core // 2, core % 2
    hs = [2 * hh, 2 * hh + 1]
    ch = [0 + hs[0], 0 + hs[1], 4 + hs[0], 4 + hs[1], 8 + hs[0], 8 + hs[1]]
    wcols = np.concatenate([off + h * 128 + np.arange(128) for off in (0, 512, 1024) for h in hs])
    cwv = inp["gdn_conv_w"][i][:, wcols]
    zc = [12 + hs[0], 12 + hs[1]]
    z = np.concatenate([tokmaj(cx[zc]), tokmaj(lat[zc])], 1)
    def small(rows):
        v = np.concatenate([cx[48][rows], lat[48][rows]], 1)
        LT = v.shape[1]
        return np.ascontiguousarray(v.T.reshape(LT // P, P, 4).transpose(1, 0, 2))
    arows = [d * 4 + h for d in range(2) for h in hs]
    brows = [8 + d * 4 + h for d in range(2) for h in hs]
    return {"qkvl": np.ascontiguousarray(pad2(lat[ch])), "qkvc": np.ascontiguousarray(pad2(cx[ch])),
            "cw": np.ascontiguousarray(cwv.T.reshape(6, P, 5).transpose(1, 0, 2)),
            "z": z, "araw": small(arows), "braw": small(brows),
            "alog": np.ascontiguousarray(np.broadcast_to(inp["gdn_a_log"][i][:, hs].reshape(1, 4), (P, 4))),
            "dtb": np.ascontiguousarray(np.broadcast_to(inp["gdn_dt_bias"][i][:, hs].reshape(1, 4), (P, 4))),
            "nw": np.ascontiguousarray(np.broadcast_to(inp["gdn_norm_w"][i].reshape(1, P), (P, P)))}


def fused_cols():
    sw = np.arange(128) ^ 1
    ar = np.arange(128)
    fmc, tkc = [], []
    for g in (0, 1):
        hs = [2 * g, 2 * g + 1]
        for off in (0, 512, 1024):
            for h in hs:
                fmc += list(off + h * 128 + ar)
        for h in hs:
            fmc += list(2064 + h * 128 + ar)
        for h in hs:
            fmc += list(2064 + h * 128 + sw)
        for h in hs:
            fmc += list(2576 + h * 128 + ar)
        for h in hs:
            fmc += list(2576 + h * 128 + sw)
        fmc += list(4112 + g * 256 + np.arange(256))
        fmc += list(4112 + 512 + g * 128 + ar)
        fmc += list(4112 + 768 + g * 128 + ar)
    for g in (0, 1):
        hs = [2 * g, 2 * g + 1]
        for h in hs:
            tkc += list(3088 + h * 128 + ar)
        for h in hs:
            tkc += list(1536 + h * 128 + ar)
        tkc += list(3600 + g * 256 + np.arange(256))
        tkc += [2048 + d * 4 + h for d in range(2) for h in hs]
        tkc += [2056 + d * 4 + h for d in range(2) for h in hs]
        tkc += [5136 + d * 8 + 4 * g + h for d in range(2) for h in range(4)]
    cols = np.array(fmc + tkc)
    assert len(cols) == 36 * 128 + 1568
    return cols


def fused_inputs(inp, core, NT, NCX, depth=2):
    b, hh = core // 2, core % 2
    s = hh
    tok = np.concatenate([inp["x"][b, s * NT:(s + 1) * NT], inp["ctx"][b, s * NCX:(s + 1) * NCX]], 0)
    cv = np.stack([inp["c"][b], inp["c_ctx"]], -1)
    cv = np.ascontiguousarray(cv.reshape(8, P, 2).transpose(1, 0, 2))
    d = {"xT": fm(tok), "cv": cv, "selv": np.ascontiguousarray(np.broadcast_to(np.array([[1.0 - hh, float(hh)]], np.float32), (P, 2)))}
    cols = fused_cols()
    hs = [2 * hh, 2 * hh + 1]
    g = hh
    for i in range(depth):
        sfx = "_%d" % i
        d["wada1" + sfx] = np.ascontiguousarray(inp["w_ada"][i][:, :5 * 1024])
        d["bada1" + sfx] = vec_fm(inp["b_ada"][i][:5 * 1024])
        d["wada2" + sfx] = np.ascontiguousarray(inp["w_ada"][i][:, 3 * 1024:])
        d["bada2" + sfx] = vec_fm(inp["b_ada"][i][3 * 1024:])
        d["ng" + sfx] = np.ascontiguousarray(inp["norm_g"][i].reshape(6, 8, P).transpose(2, 0, 1).reshape(P, 48))
        d["w1a" + sfx] = np.ascontiguousarray(inp["w_ffn_in"][i, 0])
        d["w2a" + sfx] = np.ascontiguousarray(inp["w_ffn_out"][i, 0])
        d["w1b" + sfx] = np.ascontiguousarray(inp["w_ffn_in"][i, 1])
        d["w2b" + sfx] = np.ascontiguousarray(inp["w_ffn_out"][i, 1])
        d["win" + sfx] = np.ascontiguousarray(inp["w_in"][i][:, cols])
        d["wg" + sfx] = np.ascontiguousarray(inp["w_in"][i][:, 5152:8224])
        d["wb" + sfx] = np.ascontiguousarray(inp["w_branch"][i].reshape(1536, 1024))
        d["wo" + sfx] = np.ascontiguousarray(inp["w_out"][i])
        d["snw" + sfx] = vec_fm(inp["ssd_norm_w"][i])
        lam_init = 0.8 - 0.6 * np.exp(-0.3 * i)
        d["lam" + sfx] = np.ascontiguousarray(np.broadcast_to(inp["diff_lambda"][i].reshape(1, 256), (P, 256)))
        d["dnw" + sfx] = np.ascontiguousarray(inp["diff_norm_w"][i].reshape(P, 1))
        d["li" + sfx] = np.full((P, 1), lam_init, np.float32)
        wcols = np.concatenate([np.arange(256 * g, 256 * g + 256), 512 + 128 * g + np.arange(128), 768 + 128 * g + np.arange(128)])
        d["scw" + sfx] = np.ascontiguousarray(inp["ssd_conv_w"][i][:, wcols].T.reshape(4, P, 5).transpose(1, 0, 2))
        d["scb" + sfx] = np.ascontiguousarray(inp["ssd_conv_b"][i][wcols].reshape(4, P).T)
        hsel = [4 * g + h for h in range(4)]
        d["sdtb" + sfx] = np.ascontiguousarray(np.broadcast_to(inp["ssd_dt_bias"][i][:, hsel].reshape(1, 8), (P, 8)))
        d["salog" + sfx] = np.ascontiguousarray(np.broadcast_to(inp["ssd_a_log"][i][:, hsel].reshape(1, 8), (P, 8)))
        d["sdsk" + sfx] = np.ascontiguousarray(np.broadcast_to(inp["ssd_d"][i][hsel].reshape(1, 4), (P, 4)))
        gcols = np.concatenate([off + h * 128 + np.arange(128) for off in (0, 512, 1024) for h in hs])
        d["gcw" + sfx] = np.ascontiguousarray(inp["gdn_conv_w"][i][:, gcols].T.reshape(6, P, 5).transpose(1, 0, 2))
        d["galog" + sfx] = np.ascontiguousarray(np.broadcast_to(inp["gdn_a_log"][i][:, hs].reshape(1, 4), (P, 4)))
        d["gdtb" + sfx] = np.ascontiguousarray(np.broadcast_to(inp["gdn_dt_bias"][i][:, hs].reshape(1, 4), (P, 4)))
        d["gnw" + sfx] = np.ascontiguousarray(np.broadcast_to(inp["gdn_norm_w"][i].reshape(1, P), (P, P)))
    return d


from concourse.bass_utils import run_bass_kernel_spmd


def kernel(**inp):
    inp = {k: np.ascontiguousarray(np.asarray(v), dtype=np.float32) for k, v in inp.items()}
    B, L, Dm = inp["x"].shape
    LC = inp["ctx"].shape[1]
    NT, NCX = L // 2, LC // 2
    nc = build_fused(L, LC, 2)
    ims = [fused_inputs(inp, c, NT, NCX, 2) for c in range(8)]
    res = run_bass_kernel_spmd(nc, ims, core_ids=list(range(8))).results
    out = np.empty((B, L, Dm), np.float32)
    for c in range(8):
        b, s = c // 2, c % 2
        out[b, s * NT:(s + 1) * NT] = unfm(res[c]["yT"])[:NT]
    return out
```

```python
import numpy as np
from contextlib import ExitStack, contextmanager
import concourse.bass as bass
import concourse.mybir as mybir

F32 = mybir.dt.float32
BF16 = mybir.dt.bfloat16
AF = mybir.ActivationFunctionType
ALU = mybir.AluOpType
AX = mybir.AxisListType


class Buf:
    __slots__ = ("name", "t", "w", "r", "dsem", "dcnt", "space", "dkey", "ds")

    def __init__(self, name, t, space="sbuf"):
        self.name = name
        self.space = space
        self.t = t
        self.w = {}
        self.r = {}
        self.dsem = None
        self.dcnt = 0
        self.dkey = None
        self.ds = {}

    def __getitem__(self, idx):
        return View(self, self.t[idx])


class View:
    __slots__ = ("buf", "ap")

    def __init__(self, buf, ap):
        self.buf = buf
        self.ap = ap

    def __getitem__(self, idx):
        return View(self.buf, self.ap[idx])

    def rearrange(self, *a, **k):
        return View(self.buf, self.ap.rearrange(*a, **k))

    def bitcast(self, *a, **k):
        return View(self.buf, self.ap.bitcast(*a, **k))

    def to_broadcast(self, *a, **k):
        return View(self.buf, self.ap.to_broadcast(*a, **k))


class Sched:
    def __init__(self, nc, stack):
        self.nc = nc
        self.stack = stack
        self.eng = {"pe": nc.tensor, "act": nc.scalar, "dve": nc.vector, "pool": nc.gpsimd, "sp": nc.sync}
        self.sem = {}
        self.cnt = {}
        self.seen = {}
        for e in self.eng:
            self.sem[e] = stack.enter_context(nc.semaphore("prog_" + e))
            self.cnt[e] = 0
            self.seen[e] = {}
        self.semobj = {e: self.sem[e] for e in self.eng}
        self.nbuf = 0
        self.ninst = 0
        self.root = stack
        self.dbufs = []
        self.sempool = {"hw": [], "sw": []}
        self.scope_bufs = [[]]
        self.nsem = 0

    def sbuf(self, name, shape, dt):
        self.nbuf += 1
        name = "%s_%d" % (name, self.nbuf)
        t = self.stack.enter_context(self.nc.sbuf_tensor(name, list(shape), dt))
        b = Buf(name, t)
        self.scope_bufs[-1].append(b)
        return b

    def psum(self, name, shape, dt=F32):
        self.nbuf += 1
        name = "%s_%d" % (name, self.nbuf)
        t = self.stack.enter_context(self.nc.psum_tensor(name, list(shape), dt))
        return Buf(name, t, "psum")

    def dram(self, name, shape, dt, kind="Internal"):
        t = self.nc.dram_tensor(name, list(shape), dt, kind=kind)
        return Buf(name, t.ap(), "dram")

    def sub(self, name, ap, space="dram"):
        return Buf(name, ap, space)

    def barrier(self):
        deps = {e: self.cnt[e] for e in self.eng if self.cnt[e] > 0}
        for b in self.dbufs:
            for (sem, key, cnt) in b.ds.values():
                if deps.get(key, 0) < cnt:
                    deps[key] = cnt
        for e in self.eng:
            self._need(e, dict(deps))

    @contextmanager
    def scope(self):
        old = self.stack
        self.scope_bufs.append([])
        with ExitStack() as st:
            self.stack = st
            yield
            self.barrier()
        self.stack = old
        dead = self.scope_bufs.pop()
        for b in dead:
            for kind, slot in b.ds.items():
                self.sempool[kind].append(tuple(slot))
        deadids = set(id(b) for b in dead)
        self.dbufs = [b for b in self.dbufs if id(b) not in deadids]
        self._dead_keep = getattr(self, "_dead_keep", []) + dead

    def _need(self, e, deps):
        seen = self.seen[e]
        for k, v in deps.items():
            if e == "pe" and k == "pe":
                continue
            if seen.get(k, 0) < v:
                seen[k] = v
                self.eng[e].wait_ge(self.semobj[k], v)

    def _collect(self, reads, writes):
        deps = {}
        for v in reads:
            for k, val in v.buf.w.items():
                if deps.get(k, 0) < val:
                    deps[k] = val
        for v in writes:
            for d in (v.buf.w, v.buf.r):
                for k, val in d.items():
                    if deps.get(k, 0) < val:
                        deps[k] = val
        return deps

    def _mark(self, reads, writes, key, val):
        for v in reads:
            b = v.buf
            if b.r.get(key, 0) < val:
                b.r[key] = val
        for v in writes:
            b = v.buf
            b.w = {key: val}
            b.r = {}

    def op(self, e, fn, reads, writes):
        self._need(e, self._collect(reads, writes))
        ins = fn()
        self.cnt[e] += 1
        ins.then_inc(self.sem[e], 1)
        self._mark(reads, writes, e, self.cnt[e])
        self.ninst += 1
        return ins

    def dma(self, e, out, in_, sbuf_side=None, **kw):
        if sbuf_side is None:
            sbuf_side = out.buf if out.buf.space != "dram" else in_.buf
        b = sbuf_side
        kind = "sw" if e == "pool" else "hw"
        slot = b.ds.get(kind)
        if slot is None:
            if self.sempool[kind]:
                slot = list(self.sempool[kind].pop())
            else:
                self.nsem += 1
                sem = self.root.enter_context(self.nc.semaphore("d_%d" % self.nsem))
                key = "d%d" % self.nsem
                self.semobj[key] = sem
                slot = [sem, key, 0]
            b.ds[kind] = slot
            if b not in self.dbufs:
                self.dbufs.append(b)
        self._need(e, self._collect([in_], [out]))
        ins = self.eng[e].dma_start(out=out.ap, in_=in_.ap, **kw)
        slot[2] += 16
        ins.then_inc(slot[0], 16)
        self._mark([in_], [out], slot[1], slot[2])
        self.ninst += 1
        return ins

    def wait_all(self, e, bufs):
        deps = {}
        for b in bufs:
            for d in (b.w, b.r):
                for k, val in d.items():
                    if deps.get(k, 0) < val:
                        deps[k] = val
        self._need(e, deps)

    def matmul(self, out, lhsT, rhs, start=True, stop=True, acc_reads=True):
        rd = [lhsT, rhs]
        return self.op("pe", lambda: self.nc.tensor.matmul(out.ap, lhsT.ap, rhs.ap, start=start, stop=stop),
                       rd, [out])

    def transpose(self, out, in_, ident):
        return self.op("pe", lambda: self.nc.tensor.transpose(out.ap, in_.ap, ident.ap), [in_, ident], [out])

    def act(self, out, in_, func, bias=None, scale=None, accum_out=None, e="act"):
        rd = [in_]
        kw = {}
        if bias is not None:
            if isinstance(bias, View):
                rd.append(bias)
                kw["bias"] = bias.ap
            else:
                kw["bias"] = bias
        if scale is not None:
            if isinstance(scale, View):
                rd.append(scale)
                kw["scale"] = scale.ap
            else:
                kw["scale"] = scale
        wr = [out]
        if accum_out is not None:
            wr.append(accum_out)
            kw["accum_out"] = accum_out.ap
        return self.op("act", lambda: self.nc.scalar.activation(out.ap, in_.ap, func, **kw), rd, wr)

    def _ve(self, e):
        return self.nc.vector if e == "dve" else self.nc.gpsimd

    def copy(self, out, in_, e="dve"):
        if e == "act":
            return self.op("act", lambda: self.nc.scalar.copy(out.ap, in_.ap), [in_], [out])
        return self.op(e, lambda: self._ve(e).tensor_copy(out.ap, in_.ap), [in_], [out])

    def tt(self, out, a, b, op, e="dve"):
        return self.op(e, lambda: self._ve(e).tensor_tensor(out.ap, a.ap, b.ap, op), [a, b], [out])

    def ts(self, out, a, s1, op0, s2=None, op1=None, accum_out=None, e="dve"):
        rd = [a]
        s1v = s1.ap if isinstance(s1, View) else s1
        s2v = s2.ap if isinstance(s2, View) else s2
        if isinstance(s1, View):
            rd.append(s1)
        if isinstance(s2, View):
            rd.append(s2)
        wr = [out]
        kw = {}
        if op1 is not None:
            kw["op1"] = op1
        if accum_out is not None:
            kw["accum_out"] = accum_out.ap
            wr.append(accum_out)
        if s2 is None and op1 is None and accum_out is None:
            return self.op(e, lambda: self._ve(e).tensor_single_scalar(out.ap, a.ap, s1v, op0), rd, wr)
        return self.op(e, lambda: self._ve(e).tensor_scalar(out.ap, a.ap, s1v, s2v, op0, **kw), rd, wr)

    def stt(self, out, a, s, b, op0, op1, e="dve"):
        rd = [a, b]
        sv = s.ap if isinstance(s, View) else s
        if isinstance(s, View):
            rd.append(s)
        return self.op(e, lambda: self._ve(e).scalar_tensor_tensor(out.ap, a.ap, sv, b.ap, op0, op1), rd, [out])

    def reduce(self, out, in_, op, axis=AX.X, e="dve"):
        return self.op(e, lambda: self._ve(e).tensor_reduce(out.ap, in_.ap, axis, op), [in_], [out])

    def memset(self, out, val, e="dve"):
        return self.op(e, lambda: self._ve(e).memset(out.ap, val), [], [out])

    def recip(self, out, in_):
        return self.op("dve", lambda: self.nc.vector.reciprocal(out.ap, in_.ap), [in_], [out])


P = 128
D = 1024
KC = 8
DFF = 2816
FC = 22
EPS = 1e-6


class NS:
    pass


def mk_consts(S, nc):
    C = NS()
    C.ones = S.sbuf("ones", [P, P], F32)
    S.memset(C.ones[:], 1.0)
    C.eps = S.sbuf("epsc", [P, 1], F32)
    S.memset(C.eps[:], EPS)
    C.ident = S.sbuf("ident", [P, P], F32)
    S.memset(C.ident[:], 1.0, e="pool")
    S.op("pool", lambda: nc.gpsimd.affine_select(C.ident.t[:], C.ident.t[:], [[-1, P]], ALU.is_equal, 0.0,
                                                 base=0, channel_multiplier=1), [C.ident[:]], [C.ident[:]])
    return C


def load_w(S, wd, dst, K, N, stages, blk=2048, col0=0):
    engs = ["pool", "dve", "act"]
    i = 0
    for k in range(K // P):
        for c0 in range(0, N, blk):
            w = min(blk, N - c0)
            st = stages[i % len(stages)]
            S.dma("sp", st[:, :w], wd[k * P:(k + 1) * P, col0 + c0:col0 + c0 + w])
            S.copy(dst[:, k, c0:c0 + w], st[:, :w], e=engs[i % 3])
            i += 1


def rms_rstd(S, C, src, n, nch, dim, ps, out):
    for c in range(nch):
        sq = C.sq[c % 2]
        S.act(sq[:, :n], src(c), AF.Square)
        S.matmul(ps[:, :n], C.ones[:], sq[:, :n], start=(c == 0), stop=(c == nch - 1))
    S.act(C.lnt[:, :n], ps[:, :n], AF.Ln, scale=1.0 / dim, bias=C.eps[:, 0:1])
    S.act(out, C.lnt[:, :n], AF.Exp, scale=-0.5)


def norm_mod(S, C, xt, n, A, B, col, h, ps):
    rms_rstd(S, C, lambda c: xt[:, c, :n], n, KC, D, ps, C.rstd[:, :n])
    for c in range(KC):
        t = C.tmp[c % 2]
        S.tt(t[:, :n], xt[:, c, :n], C.rstd[:, :n], ALU.mult)
        S.act(h[:, c, :n], t[:, :n], AF.Identity, scale=A[:, c, col:col + 1], bias=B[:, c, col:col + 1])


def compute_mods(S, C, cv, wada, bada, nmod, stg, psm, mods):
    scv = S.sbuf("scv", [P, KC, 2], F32)
    cvt = S.sbuf("cvt", [P, KC, 2], F32)
    S.dma("sp", cvt[:], cv[:])
    S.act(scv[:], cvt[:], AF.Silu)
    nn = nmod * KC
    for k in range(KC):
        S.dma("sp" if k % 2 else "pool", stg[k][:, :nn * P], wada[k * P:(k + 1) * P, :])
    for j in range(nn):
        for k in range(KC):
            S.matmul(psm[:, 2 * j:2 * j + 2], stg[k][:, j * P:(j + 1) * P], scv[:, k, :], start=(k == 0), stop=(k == KC - 1))
    bt = S.sbuf("badat", [P, nn], F32)
    S.dma("sp", bt[:], bada[:])
    S.tt(mods[:], psm[:, 0:2 * nn].rearrange("p (j t) -> p j t", t=2),
         View(bt, bt.t[:].rearrange("p (j o) -> p j o", o=1).to_broadcast([P, nn, 2])), ALU.add)


def ffn_sweep(S, C, tiles, x_in, x_out, w1b, w2b, A, B, G, PS):
    for j, (s0, n, col) in enumerate(tiles):
        xt = C.xt[j % 2]
        S.dma("sp", xt[:, :, :n], x_in(j))
        norm_mod(S, C, xt, n, A, B, col, C.h, PS.ss)
        for f in range(FC):
            pg = PS.g[f % 2]
            pu = PS.u[f % 2]
            for k in range(KC):
                S.matmul(pg[:, :n], w1b[:, k, f * P:(f + 1) * P], C.h[:, k, :n], start=(k == 0), stop=(k == KC - 1))
            for k in range(KC):
                S.matmul(pu[:, :n], w1b[:, k, DFF + f * P:DFF + (f + 1) * P], C.h[:, k, :n], start=(k == 0), stop=(k == KC - 1))
            sg = C.sg[f % 2]
            S.act(sg[:, :n], pg[:, :n], AF.Silu)
            S.tt(C.aT[:, f, :n], sg[:, :n], pu[:, :n], ALU.mult)
        for d in range(KC):
            py = PS.y[d % 2]
            for f in range(FC):
                S.matmul(py[:, :n], w2b[:, f, d * P:(d + 1) * P], C.aT[:, f, :n], start=(f == 0), stop=(f == FC - 1))
            S.copy(C.y[:, d, :n], py[:, :n], e="dve")
        rms_rstd(S, C, lambda c: C.y[:, c, :n], n, KC, D, PS.ss2, C.rstd2[:, :n])
        for c in range(KC):
            t = C.tmp[c % 2]
            S.tt(t[:, :n], C.y[:, c, :n], C.rstd2[:, :n], ALU.mult)
            S.stt(xt[:, c, :n], t[:, :n], G[:, c, col:col + 1], xt[:, c, :n], ALU.mult, ALU.add)
        S.dma("pool", x_out(j), xt[:, :, :n])


def alloc_ffn_work(S, C):
    C.xt = [S.sbuf("xt0", [P, KC, 512], F32)] * 2
    C.h = S.sbuf("h", [P, KC, 512], BF16)
    C.aT = S.sbuf("aT", [P, FC, 512], BF16)
    C.y = S.sbuf("y", [P, KC, 512], F32)
    C.sg = C.tmp


def alloc_small(S, C):
    C.sq = [S.sbuf("sq%d" % i, [P, 512], F32) for i in range(2)]
    C.tmp = [S.sbuf("tmp%d" % i, [P, 512], F32) for i in range(2)]
    C.lnt = C.sq[0]
    C.rstd = S.sbuf("rstd", [P, 512], F32)
    C.rstd2 = C.rstd


def mk_tiles(NT, NCX):
    tiles = [(j * 512, 512, 0) for j in range(NT // 512)]
    if NCX:
        tiles.append((NT, NCX, 1))
    return tiles


DEBUG = False
NPC = 49


def build_R1(NT, NCX):
    TT = NT + NCX
    nc = bass.Bass("TRN2", target_bir_lowering=False)
    with ExitStack() as st:
        S = Sched(nc, st)
        xT = S.dram("xT", [P, KC, TT], F32, kind="ExternalInput")
        cv = S.dram("cv", [P, KC, 2], F32, kind="ExternalInput")
        wada = S.dram("wada", [D, 5 * D], F32, kind="ExternalInput")
        bada = S.dram("bada", [P, 40], F32, kind="ExternalInput")
        ng = S.dram("ng", [P, 48], F32, kind="ExternalInput")
        w1 = S.dram("w1", [D, 2 * DFF], F32, kind="ExternalInput")
        w2 = S.dram("w2", [DFF, D], F32, kind="ExternalInput")
        win = S.dram("win", [D, NPC * P], F32, kind="ExternalInput")
        x1T = S.dram("x1T", [P, KC, TT], F32, kind="ExternalOutput")
        PT = S.dram("PT", [NPC, P, TT], F32, kind="ExternalOutput")
        tiles = mk_tiles(NT, NCX)
        C = mk_consts(S, nc)
        alloc_small(S, C)
        PS = NS()
        PS.ss = S.psum("ps_ss", [P, 512])
        PS.ss2 = S.psum("ps_ss2", [P, 512])
        PS.g = [S.psum("ps_g%d" % i, [P, 512]) for i in range(2)]
        PS.u = [S.psum("ps_u%d" % i, [P, 512]) for i in range(2)]
        PS.y = [S.psum("ps_y%d" % i, [P, 512]) for i in range(2)]
        mods = S.sbuf("mods", [P, 40, 2], F32)
        ngt = S.sbuf("ngt", [P, 6, KC], F32)
        S.dma("sp", ngt[:], ng[:].rearrange("p (m c) -> p m c", c=KC))
        A1 = S.sbuf("A1", [P, KC, 2], F32)
        G1 = S.sbuf("G1", [P, KC, 2], F32)
        A2 = S.sbuf("A2", [P, KC, 2], F32)
        with S.scope():
            stg = [S.sbuf("stgm%d" % i, [P, 5 * D], F32) for i in range(KC)]
            compute_mods(S, C, cv, wada, bada, 5, stg, PS.g[0], mods)

        def bc(v):
            return View(v.buf, v.ap.rearrange("p (c o) -> p c o", o=1).to_broadcast([P, KC, 2]))
        S.stt(A1[:], mods[:, 8:16, :], 1.0, bc(ngt[:, 0, :]), ALU.add, ALU.mult)
        S.stt(G1[:], mods[:, 16:24, :], 0.5, bc(ngt[:, 1, :]), ALU.mult, ALU.mult)
        S.stt(A2[:], mods[:, 32:40, :], 1.0, bc(ngt[:, 2, :]), ALU.add, ALU.mult)
        B1 = mods[:, 0:8, :]
        B2 = mods[:, 24:32, :]
        if DEBUG:
            dbg = S.dram("dbg_mods", [P, 80], F32, kind="ExternalOutput")
            S.dma("sp", dbg[:], mods[:].rearrange("p j t -> p (j t)"))
        x1tiles = [S.sub("x1t%d" % j, x1T.t[:, :, s0:s0 + n]) for j, (s0, n, col) in enumerate(tiles)]
        with S.scope():
            w1b = S.sbuf("w1b", [P, KC, 2 * DFF], BF16)
            w2b = S.sbuf("w2b", [P, FC, D], BF16)
            with S.scope():
                stages = [S.sbuf("wst%d" % i, [P, 2048], F32) for i in range(3)]
                load_w(S, w1, w1b, D, 2 * DFF, stages)
                load_w(S, w2, w2b, DFF, D, stages)
            alloc_ffn_work(S, C)
            ffn_sweep(S, C, tiles, lambda j: xT[:, :, tiles[j][0]:tiles[j][0] + tiles[j][1]],
                      lambda j: x1tiles[j][:], w1b, w2b, A1[:], B1, G1[:], PS)
        with S.scope():
            winb = S.sbuf("winb", [P, KC, NPC * P], BF16)
            with S.scope():
                stages = [S.sbuf("wst%d" % i, [P, 2048], F32) for i in range(3)]
                load_w(S, win, winb, D, NPC * P, stages)
            xt2 = [S.sbuf("xq%d" % i, [P, KC, 512], F32) for i in range(2)]
            h = S.sbuf("h2", [P, KC, 512], BF16)
            ost = [S.sbuf("ost%d" % i, [P, 4, 512], F32) for i in range(3)]
            pps = PS.g + PS.u + PS.y
            gi = 0
            for j, (s0, n, col) in enumerate(tiles):
                xt = xt2[j % 2]
                S.dma("sp", xt[:, :, :n], x1tiles[j][:])
                norm_mod(S, C, xt, n, A2[:], B2, col, h, PS.ss)
                for c0 in range(0, NPC, 4):
                    nn = min(4, NPC - c0)
                    o = ost[gi % 3]
                    gi += 1
                    for cc in range(nn):
                        pp = pps[(c0 + cc) % 6]
                        for k in range(KC):
                            S.matmul(pp[:, :n], winb[:, k, (c0 + cc) * P:(c0 + cc + 1) * P], h[:, k, :n],
                                     start=(k == 0), stop=(k == KC - 1))
                        S.copy(o[:, cc, :n], pp[:, :n], e=("act" if cc % 2 else "dve"))
                    S.dma("pool", S.sub("pt", PT.t[c0:c0 + nn, :, s0:s0 + n].rearrange("c p t -> p c t"))[:], o[:, :nn, :n])
            S.wait_all("sp", ost + xt2)
        S.barrier()
    return nc


def build_R2(NT, NCX):
    TT = NT + NCX
    nc = bass.Bass("TRN2", target_bir_lowering=False)
    with ExitStack() as st:
        S = Sched(nc, st)
        x1T = S.dram("x1T", [P, KC, TT], F32, kind="ExternalInput")
        oin = [S.dram(nm, [P, 4, TT], F32, kind="ExternalInput") for nm in ("oaT", "obT", "ocT")]
        cv = S.dram("cv", [P, KC, 2], F32, kind="ExternalInput")
        wada = S.dram("wada", [D, 6 * D], F32, kind="ExternalInput")
        bada = S.dram("bada", [P, 48], F32, kind="ExternalInput")
        ng = S.dram("ng", [P, 48], F32, kind="ExternalInput")
        snw = S.dram("snw", [P, 4], F32, kind="ExternalInput")
        wg = S.dram("wg", [D, 3 * D], F32, kind="ExternalInput")
        wb = S.dram("wb", [1536, D], F32, kind="ExternalInput")
        wo = S.dram("wo", [D, D], F32, kind="ExternalInput")
        w1 = S.dram("w1", [D, 2 * DFF], F32, kind="ExternalInput")
        w2 = S.dram("w2", [DFF, D], F32, kind="ExternalInput")
        x3T = S.dram("x3T", [P, KC, TT], F32, kind="ExternalOutput")
        x2T = S.dram("x2T", [P, KC, TT], F32, kind="Internal")
        tiles = mk_tiles(NT, NCX)
        C = mk_consts(S, nc)
        alloc_small(S, C)
        PS = NS()
        PS.ss = S.psum("ps_ss", [P, 512])
        PS.ss2 = S.psum("ps_ss2", [P, 512])
        PS.g = [S.psum("ps_g%d" % i, [P, 512]) for i in range(2)]
        PS.u = [S.psum("ps_u%d" % i, [P, 512]) for i in range(2)]
        PS.y = [S.psum("ps_y%d" % i, [P, 512]) for i in range(2)]
        mods = S.sbuf("mods", [P, 48, 2], F32)
        ngt = S.sbuf("ngt", [P, 6, KC], F32)
        S.dma("sp", ngt[:], ng[:].rearrange("p (m c) -> p m c", c=KC))
        snt = S.sbuf("snt", [P, 4], F32)
        S.dma("sp", snt[:], snw[:])
        with S.scope():
            stg = [S.sbuf("stgm%d" % i, [P, 6 * D], F32) for i in range(KC)]
            compute_mods(S, C, cv, wada, bada, 6, stg, PS.g[0], mods)

        def bc(v):
            return View(v.buf, v.ap.rearrange("p (c o) -> p c o", o=1).to_broadcast([P, KC, 2]))
        A2 = S.sbuf("A2", [P, KC, 2], F32)
        G3 = S.sbuf("G3", [P, KC, 2], F32)
        A4 = S.sbuf("A4", [P, KC, 2], F32)
        G5 = S.sbuf("G5", [P, KC, 2], F32)
        S.stt(A2[:], mods[:, 8:16, :], 1.0, bc(ngt[:, 2, :]), ALU.add, ALU.mult)
        S.tt(G3[:], mods[:, 16:24, :], bc(ngt[:, 3, :]), ALU.mult)
        S.stt(A4[:], mods[:, 32:40, :], 1.0, bc(ngt[:, 4, :]), ALU.add, ALU.mult)
        S.stt(G5[:], mods[:, 40:48, :], 0.5, bc(ngt[:, 5, :]), ALU.mult, ALU.mult)
        B2 = mods[:, 0:8, :]
        B4 = mods[:, 24:32, :]
        x2tiles = [S.sub("x2t%d" % j, x2T.t[:, :, s0:s0 + n]) for j, (s0, n, col) in enumerate(tiles)]
        with S.scope():
            wgb = S.sbuf("wgb", [P, KC, 3 * D], BF16)
            wbb = S.sbuf("wbb", [P, 12, D], BF16)
            wob = S.sbuf("wob", [P, KC, D], BF16)
            with S.scope():
                stages = [S.sbuf("wst%d" % i, [P, 2048], F32) for i in range(3)]
                load_w(S, wg, wgb, D, 3 * D, stages)
                load_w(S, wb, wbb, 1536, D, stages)
                load_w(S, wo, wob, D, D, stages)
            xt = S.sbuf("xm", [P, KC, 512], F32)
            h = S.sbuf("hm", [P, KC, 512], BF16)
            ost = S.sbuf("ostg", [P, 4, 512], F32)
            ob16 = S.sbuf("ob16", [P, 12, 512], BF16)
            yacc = S.sbuf("yacc", [P, 512], F32)
            ybf = S.sbuf("ybf", [P, KC, 512], BF16)
            yy = S.sbuf("yy", [P, KC, 512], F32)
            gt = [S.sbuf("gt%d" % i, [P, 512], F32) for i in range(2)]
            for j, (s0, n, col) in enumerate(tiles):
                S.dma("sp", xt[:, :, :n], x1T[:, :, s0:s0 + n])
                norm_mod(S, C, xt, n, A2[:], B2, col, h, PS.ss)
                for br in range(3):
                    S.dma("sp", ost[:, :, :n], oin[br][:, :, s0:s0 + n])
                    if br < 2:
                        S.copy(ob16[:, br * 4:(br + 1) * 4, :n], ost[:, :, :n], e="pool")
                    else:
                        rms_rstd(S, C, lambda c: ost[:, c, :n], n, 4, 512, PS.ss2, C.rstd2[:, :n])
                        for c in range(4):
                            t = C.tmp[c % 2]
                            S.tt(t[:, :n], ost[:, c, :n], C.rstd2[:, :n], ALU.mult)
                            S.act(ob16[:, 8 + c, :n], t[:, :n], AF.Copy, scale=snt[:, c:c + 1])
                for d in range(KC):
                    for br in range(3):
                        pg = PS.g[br % 2]
                        pu = PS.u[br % 2]
                        cg = br * KC + d
                        for k in range(KC):
                            S.matmul(pg[:, :n], wgb[:, k, cg * P:(cg + 1) * P], h[:, k, :n], start=(k == 0), stop=(k == KC - 1))
                        for k in range(4):
                            S.matmul(pu[:, :n], wbb[:, br * 4 + k, d * P:(d + 1) * P], ob16[:, br * 4 + k, :n], start=(k == 0), stop=(k == 3))
                        g = gt[br % 2]
                        S.act(g[:, :n], pg[:, :n], AF.Sigmoid)
                        if br == 0:
                            S.tt(yacc[:, :n], g[:, :n], pu[:, :n], ALU.mult)
                        else:
                            t = C.tmp[br % 2]
                            S.tt(t[:, :n], g[:, :n], pu[:, :n], ALU.mult)
                            if br == 1:
                                S.tt(yacc[:, :n], yacc[:, :n], t[:, :n], ALU.add)
                            else:
                                S.tt(ybf[:, d, :n], yacc[:, :n], t[:, :n], ALU.add)
                for d in range(KC):
                    py = PS.y[d % 2]
                    for k in range(KC):
                        S.matmul(py[:, :n], wob[:, k, d * P:(d + 1) * P], ybf[:, k, :n], start=(k == 0), stop=(k == KC - 1))
                    S.copy(yy[:, d, :n], py[:, :n], e="dve")
                rms_rstd(S, C, lambda c: yy[:, c, :n], n, KC, D, PS.ss2, C.rstd2[:, :n])
                for c in range(KC):
                    t = C.tmp[c % 2]
                    S.tt(t[:, :n], yy[:, c, :n], C.rstd2[:, :n], ALU.mult)
                    S.stt(xt[:, c, :n], t[:, :n], G3[:, c, col:col + 1], xt[:, c, :n], ALU.mult, ALU.add)
                S.dma("pool", x2tiles[j][:], xt[:, :, :n])
        with S.scope():
            w1b = S.sbuf("w1b", [P, KC, 2 * DFF], BF16)
            w2b = S.sbuf("w2b", [P, FC, D], BF16)
            with S.scope():
                stages = [S.sbuf("wst%d" % i, [P, 2048], F32) for i in range(3)]
                load_w(S, w1, w1b, D, 2 * DFF, stages)
                load_w(S, w2, w2b, DFF, D, stages)
            alloc_ffn_work(S, C)
            ffn_sweep(S, C, tiles, lambda j: x2tiles[j][:],
                      lambda j: S.sub("x3", x3T.t[:, :, tiles[j][0]:tiles[j][0] + tiles[j][1]])[:], w1b, w2b, A4[:], B4, G5[:], PS)
        S.barrier()
    return nc


I32 = mybir.dt.int32
import math


def rope_tables(S, nc, C, L, cosb, sinb):
    GW = 64
    rows = L // GW
    TWO_PI = 2 * math.pi
    with S.scope():
        ti = S.sbuf("ti", [P, P], I32)
        tf = S.sbuf("tf", [P, P], F32)

        def ppc(name, pattern):
            o = S.sbuf(name, [P, 1], F32)
            S.op("pool", lambda: nc.gpsimd.iota(ti.t[:], pattern, base=0, channel_multiplier=0), [], [ti[:]])
            S.copy(tf[:], ti[:])
            S.tt(tf[:], tf[:], C.ident[:], ALU.mult)
            S.reduce(o[:], tf[:], ALU.add)
            return o
        i16 = ppc("i16", [[0, 2], [0, 2], [1, 16], [0, 2]])
        sel = ppc("sel", [[0, 2], [1, 2], [0, 16], [0, 2]])
        dd = ppc("dd", [[0, 2], [0, 2], [0, 16], [1, 2]])
        sgn = S.sbuf("sgn", [P, 1], F32)
        inv = S.sbuf("inv", [P, 1], F32)
        S.ts(sgn[:], dd[:], 2.0, ALU.mult, -1.0, ALU.add)
        S.act(inv[:], i16[:], AF.Exp, scale=-math.log(10000.0) / 16.0)
        S.ts(inv[:], inv[:], 1.0 / TWO_PI, ALU.mult)
        CH = 1024
        with S.scope():
            ri = S.sbuf("ri", [P, CH], I32)
            ci = S.sbuf("ci", [P, CH], I32)
            rf = S.sbuf("rf", [P, CH], F32)
            cf = S.sbuf("cf", [P, CH], F32)
            xt = S.sbuf("xtn", [P, CH], F32)
            ni = S.sbuf("ni", [P, CH], I32)
            nf = S.sbuf("nf", [P, CH], F32)
            for c0 in range(0, L, CH):
                w = min(CH, L - c0)
                S.op("pool", lambda c0=c0, w=w: nc.gpsimd.iota(ri.t[:, :w], [[1, w // GW], [0, GW]], base=c0 // GW, channel_multiplier=0), [], [ri[:]])
                S.op("pool", lambda w=w: nc.gpsimd.iota(ci.t[:, :w], [[0, w // GW], [1, GW]], base=0, channel_multiplier=0), [], [ci[:]])
                S.copy(rf[:, :w], ri[:, :w])
                S.copy(cf[:, :w], ci[:, :w])
                S.tt(cf[:, :w], cf[:, :w], rf[:, :w], ALU.subtract)
                S.stt(xt[:, :w], cf[:, :w], sel[:, 0:1], rf[:, :w], ALU.mult, ALU.add)
                S.ts(xt[:, :w], xt[:, :w], inv[:, 0:1], ALU.mult)
                for (dst, off) in ((sinb, 0.0), (cosb, 0.25)):
                    if off:
                        S.ts(xt[:, :w], xt[:, :w], off, ALU.add)
                    S.copy(ni[:, :w], xt[:, :w])
                    S.copy(nf[:, :w], ni[:, :w])
                    S.tt(nf[:, :w], xt[:, :w], nf[:, :w], ALU.subtract)
                    S.act(dst[:, c0:c0 + w], nf[:, :w], AF.Sin, scale=TWO_PI * (1 - 1e-6))
                S.ts(sinb[:, c0:c0 + w], sinb[:, c0:c0 + w], sgn[:, 0:1], ALU.mult)


def build_Mdiff(L, LC):
    LK = L + LC
    NKC = LK // P
    nc = bass.Bass("TRN2", target_bir_lowering=False)
    with ExitStack() as st:
        S = Sched(nc, st)
        qT = S.dram("qT", [2, P, L], F32, kind="ExternalInput")
        qsT = S.dram("qsT", [2, P, L], F32, kind="ExternalInput")
        kT = S.dram("kT", [2, P, LK], F32, kind="ExternalInput")
        ksT = S.dram("ksT", [2, P, L], F32, kind="ExternalInput")
        qcT = S.dram("qcT", [2, P, LC], F32, kind="ExternalInput")
        vd = S.dram("v", [P, NKC, 256], F32, kind="ExternalInput")
        lamd = S.dram("lam", [P, 256], F32, kind="ExternalInput")
        nwd = S.dram("nw", [P, 1], F32, kind="ExternalInput")
        lid = S.dram("li", [P, 1], F32, kind="ExternalInput")
        obT = S.dram("obT", [2, P, L], F32, kind="ExternalOutput")
        obcT = S.dram("obcT", [2, P, LC], F32, kind="ExternalOutput")
        C = mk_consts(S, nc)
        C.sq = [S.sbuf("sq%d" % i, [P, 512], F32) for i in range(2)]
        C.lnt = C.sq[0]
        onesb = S.sbuf("onesb", [P, P], BF16)
        S.memset(onesb[:], 1.0)
        Q = [S.sbuf("Q%d" % h, [P, L], BF16) for h in range(2)]
        QC = [S.sbuf("QC%d" % h, [P, LC], BF16) for h in range(2)]
        K = [S.sbuf("K%d" % h, [P, LK], BF16) for h in range(2)]
        V = S.sbuf("V", [P, NKC, 256], BF16)
        lam = S.sbuf("lamt", [P, 4, 64], F32)
        S.dma("sp", lam[:], lamd[:].rearrange("p (a b) -> p a b", b=64))
        nw = S.sbuf("nwt", [P, 1], F32)
        li = S.sbuf("lit", [P, 1], F32)
        S.dma("sp", nw[:], nwd[:])
        S.dma("sp", li[:], lid[:])
        pr = S.sbuf("pr", [P, 2, 64], F32)
        s12 = S.sbuf("s12", [P, 2], F32)
        S.tt(pr[:, 0, :], lam[:, 0, :], lam[:, 1, :], ALU.mult)
        S.tt(pr[:, 1, :], lam[:, 2, :], lam[:, 3, :], ALU.mult)
        S.reduce(s12[:], pr[:], ALU.add)
        e12 = S.sbuf("e12", [P, 2], F32)
        S.act(e12[:], s12[:], AF.Exp)
        neglam = S.sbuf("neglam", [P, 1], F32)
        S.tt(neglam[:], e12[:, 1:2], e12[:, 0:1], ALU.subtract)
        S.tt(neglam[:], neglam[:], li[:], ALU.subtract)
        sc2 = S.sbuf("sc2", [P, 1], F32)
        S.ts(sc2[:], li[:], -1.0, ALU.mult, 1.0, ALU.add)
        S.tt(sc2[:], sc2[:], nw[:], ALU.mult)
        with S.scope():
            cosb = S.sbuf("cosb", [P, L], F32)
            sinb = S.sbuf("sinb", [P, L], F32)
            rope_tables(S, nc, C, L, cosb, sinb)
            with S.scope():
                a = [S.sbuf("la%d" % i, [P, 512], F32) for i in range(2)]
                b = [S.sbuf("lb%d" % i, [P, 512], F32) for i in range(2)]
                vst = [S.sbuf("vst%d" % i, [P, 4, 256], F32) for i in range(2)]
                i = 0
                for h in range(2):
                    for (src, ssw, dst) in ((qT, qsT, Q[h]), (kT, ksT, K[h])):
                        for c0 in range(0, L, 512):
                            ta, tb = a[i % 2], b[i % 2]
                            i += 1
                            S.dma("sp", ta[:], src[h, :, c0:c0 + 512])
                            S.dma("pool", tb[:], ssw[h, :, c0:c0 + 512])
                            S.tt(ta[:], ta[:], cosb[:, c0:c0 + 512], ALU.mult)
                            S.tt(tb[:], tb[:], sinb[:, c0:c0 + 512], ALU.mult, e="pool")
                            S.tt(dst[:, c0:c0 + 512], ta[:], tb[:], ALU.add)
                    ta = a[i % 2]
                    i += 1
                    S.dma("sp", ta[:, :LC], kT[h, :, L:LK])
                    S.copy(K[h][:, L:LK], ta[:, :LC])
                    ta = a[i % 2]
                    i += 1
                    S.dma("sp", ta[:, :LC], qcT[h, :, :])
                    S.copy(QC[h][:], ta[:, :LC])
                for c0 in range(0, NKC, 4):
                    w = min(4, NKC - c0)
                    t = vst[(c0 // 4) % 2]
                    S.dma("sp", t[:, :w, :], vd[:, c0:c0 + w, :])
                    S.copy(V[:, c0:c0 + w, :], t[:, :w, :], e="pool")
        ps_s = [[S.psum("ps_s%d%d" % (j, i), [P, 512]) for i in range(2)] for j in range(2)]
        ps_o = [S.psum("ps_o%d" % j, [P, 512]) for j in range(2)]
        ps_z = [S.psum("ps_z%d" % j, [P, 512]) for j in range(2)]
        pt = [[S.sbuf("pt%d%d" % (j, i), [P, 512], BF16) for i in range(2)] for j in range(2)]
        rz = [S.sbuf("rz%d" % j, [P, 512], F32) for j in range(2)]
        t0 = S.sbuf("t0", [P, 512], F32)
        t1 = S.sbuf("t1", [P, 512], F32)
        rstd = S.sbuf("rstd", [P, 512], F32)
        oo = [S.sbuf("oo%d" % i, [P, 512], F32) for i in range(2)]
        jobs = []
        for h in range(2):
            for q0 in range(0, L, 512):
                jobs.append((h, Q[h][:, q0:q0 + 512], 512, 0, NKC, obT[h, :, q0:q0 + 512]))
            jobs.append((h, QC[h][:], LC, L // P, NKC, obcT[h, :, :]))
        for ji, (h, qv, n, kc0, kc1, outv) in enumerate(jobs):
            for kc in range(kc0, kc1):
                bi = kc % 2
                for j in range(2):
                    S.matmul(ps_s[j][bi][:, :n], K[h][j * 64:(j + 1) * 64, kc * P:(kc + 1) * P], qv[j * 64:(j + 1) * 64, :],
                             start=True, stop=True)
                for j in range(2):
                    S.act(pt[j][bi][:, :n], ps_s[j][bi][:, :n], AF.Exp, scale=0.125)
                for j in range(2):
                    S.matmul(ps_o[j][:, :n], V[:, kc, h * P:(h + 1) * P], pt[j][bi][:, :n], start=(kc == kc0), stop=(kc == kc1 - 1))
                    S.matmul(ps_z[j][:, :n], onesb[:], pt[j][bi][:, :n], start=(kc == kc0), stop=(kc == kc1 - 1))
            for j in range(2):
                S.recip(rz[j][:, :n], ps_z[j][:, :n])
            S.tt(t0[:, :n], ps_o[0][:, :n], rz[0][:, :n], ALU.mult)
            S.tt(t1[:, :n], ps_o[1][:, :n], rz[1][:, :n], ALU.mult)
            S.stt(t0[:, :n], t1[:, :n], neglam[:, 0:1], t0[:, :n], ALU.mult, ALU.add)
            pss = ps_s[0][0]
            rms_rstd(S, C, lambda c: t0[:, :n], n, 1, P, pss, rstd[:, :n])
            o = oo[ji % 2]
            S.tt(t1[:, :n], t0[:, :n], rstd[:, :n], ALU.mult)
            S.ts(o[:, :n], t1[:, :n], sc2[:, 0:1], ALU.mult)
            S.dma("pool", outv, o[:, :n])
        S.barrier()
    return nc


def tri_mask(S, nc, name, kind, blk=None):
    m = S.sbuf(name, [P, P], F32)
    S.memset(m[:], 1.0, e="pool")
    pat, cm, op = {"le": ([[1, P]], -1, ALU.is_ge), "ge": ([[-1, P]], 1, ALU.is_ge),
                   "gt": ([[-1, P]], 1, ALU.is_gt), "lt": ([[1, P]], -1, ALU.is_gt)}[kind]
    S.op("pool", lambda: nc.gpsimd.affine_select(m.t[:], m.t[:], pat, op, 0.0, base=0, channel_multiplier=cm), [m[:]], [m[:]])
    if blk:
        S.memset(m[0:blk, blk:P], 0.0, e="pool")
        S.memset(m[blk:P, 0:blk], 0.0, e="pool")
    return m


def build_Mssd(L, LC, dbg_stop=99):
    LT = L + LC
    NCH = LT // P
    NCC = LC // P
    nc = bass.Bass("TRN2", target_bir_lowering=False)
    with ExitStack() as st:
        S = Sched(nc, st)
        xl = S.dram("xbcl", [4, P, L + 4], F32, kind="ExternalInput")
        xc = S.dram("xbcc", [4, P, LC + 4], F32, kind="ExternalInput")
        cwd = S.dram("cw", [P, 4, 5], F32, kind="ExternalInput")
        cbd = S.dram("cb", [P, 4], F32, kind="ExternalInput")
        zd = S.dram("z", [P, NCH, 256], F32, kind="ExternalInput")
        dtd = S.dram("dt", [P, NCH, 8], F32, kind="ExternalInput")
        dbd = S.dram("dtb", [P, 8], F32, kind="ExternalInput")
        ald = S.dram("alog", [P, 8], F32, kind="ExternalInput")
        dsd = S.dram("dskip", [P, 4], F32, kind="ExternalInput")
        yo = S.dram("y", [NCH, P, 256], F32, kind="ExternalOutput")
        yf = S.dram("yf", [NCH, P, 256], F32, kind="Internal")
        C = mk_consts(S, nc)
        tri = {0: tri_mask(S, nc, "tri_f", "le"), 1: tri_mask(S, nc, "tri_b", "ge")}
        strict = {0: tri_mask(S, nc, "str_f", "gt"), 1: tri_mask(S, nc, "str_b", "lt")}
        cw = S.sbuf("cw", [P, 4, 5], F32)
        cb = S.sbuf("cb", [P, 4], F32)
        S.dma("sp", cw[:], cwd[:])
        S.dma("sp", cb[:], cbd[:])
        xs_tok = S.sbuf("xs_tok", [P, NCH, 256], F32)
        B_tok = S.sbuf("B_tok", [P, NCH, P], F32)
        BT = S.sbuf("BT", [P, LT], F32)
        CT = S.sbuf("CT", [P, LT], F32)
        dtv = S.sbuf("dtv", [P, NCH, 8], F32)
        aall = S.sbuf("aall", [P, NCH, 8], F32)
        dtb = S.sbuf("dtb", [P, 8], F32)
        aneg = S.sbuf("aneg", [P, 8], F32)
        dsk = S.sbuf("dsk", [P, 4], F32)
        S.dma("sp", dtv[:], dtd[:])
        S.dma("sp", dtb[:], dbd[:])
        S.dma("sp", aneg[:], ald[:])
        S.dma("sp", dsk[:], dsd[:])
        S.tt(dtv[:], dtv[:], View(dtb, dtb.t[:].rearrange("p (o e) -> p o e", o=1).to_broadcast([P, NCH, 8])), ALU.add)
        S.act(dtv[:], dtv[:], AF.Exp)
        S.act(dtv[:], dtv[:], AF.Ln, bias=C.ones[:, 0:1])
        S.act(aneg[:], aneg[:], AF.Exp)
        S.ts(aneg[:], aneg[:], -1.0, ALU.mult)
        S.tt(aall[:], dtv[:], View(aneg, aneg.t[:].rearrange("p (o e) -> p o e", o=1).to_broadcast([P, NCH, 8])), ALU.mult)
        ps_t = [S.psum("ps_t%d" % i, [P, 512]) for i in range(2)]
        with S.scope():
            raw = [S.sbuf("raw%d" % i, [P, 4, 516], F32) for i in range(2)]
            acc = [S.sbuf("acc%d" % i, [P, 512], F32) for i in range(2)]
            xsT = [S.sbuf("xsT%d" % i, [P, 512], F32) for i in range(2)]
            segs = [(xc, 0, LC)] + [(xl, LC, L)]
            ti = 0
            if dbg_stop < 0:
                segs = []
            for (src, base, seglen) in segs:
                for t0 in range(0, seglen, 512):
                    n = min(512, seglen - t0)
                    r = raw[ti % 2]
                    ti += 1
                    S.dma("sp", r[:, :, :n + 4], src[:, :, t0:t0 + n + 4].rearrange("c p t -> p c t"))
                    for c in range(4):
                        a = acc[c % 2]
                        eng = "dve"
                        S.ts(a[:, :n], r[:, c, 0:n], cw[:, c, 0:1], ALU.mult, e=eng)
                        for j in range(1, 5):
                            S.stt(a[:, :n], r[:, c, j:j + n], cw[:, c, j:j + 1], a[:, :n], ALU.mult, ALU.add, e=eng)
                        g0 = base + t0
                        if c < 2:
                            dst = xsT[c]
                            S.act(dst[:, :n], a[:, :n], AF.Silu, bias=cb[:, c:c + 1])
                        elif c == 2:
                            S.act(BT[:, g0:g0 + n], a[:, :n], AF.Silu, bias=cb[:, c:c + 1])
                        else:
                            S.act(CT[:, g0:g0 + n], a[:, :n], AF.Silu, bias=cb[:, c:c + 1])
                    for bl in range(n // P):
                        gc = (base + t0) // P + bl
                        pt = ps_t[bl % 2]
                        dbgv = None
                        S.transpose(pt[:, 0:P], xsT[0][:, bl * P:(bl + 1) * P], C.ident[:])
                        if dbgv == "T1":
                            S.copy(xs_tok[:, gc, 0:P], pt[:, 0:P], e="dve")
                            continue
                        S.transpose(pt[:, P:2 * P], xsT[1][:, bl * P:(bl + 1) * P], C.ident[:])
                        if dbgv == "T2":
                            S.copy(xs_tok[:, gc, :], pt[:, 0:2 * P], e="dve")
                            continue
                        S.transpose(pt[:, 2 * P:3 * P], BT[:, gc * P:(gc + 1) * P], C.ident[:])
                        S.copy(xs_tok[:, gc, :], pt[:, 0:2 * P], e="dve")
                        S.copy(B_tok[:, gc, :], pt[:, 2 * P:3 * P], e="dve")
        ps_arg = [S.psum("ps_arg%d" % i, [P, 512]) for i in range(2)]
        ps_cb = S.psum("ps_cb", [P, 512])
        ps_y = S.psum("ps_y", [P, 512])
        ps_st = S.psum("ps_st", [P, 512])
        ps_sm = S.psum("ps_sm", [P, 512])
        X = [S.sbuf("X%d" % i, [P, 4, P], F32) for i in range(2)]
        LTt = [S.sbuf("LT%d" % i, [P, 4, P], F32) for i in range(2)]
        CBm = S.sbuf("CBm", [P, P], F32)
        scT = [S.sbuf("scT%d" % i, [P, 4, P], F32) for i in range(2)]
        sm = S.sbuf("sm", [P, 8], F32)
        eacs = S.sbuf("eacs", [P, 4], F32)
        edec = S.sbuf("edec", [P, 4], F32)
        etot = S.sbuf("etot", [P, 4], F32)
        dif = S.sbuf("dif", [P, 4], F32)
        xdt = [S.sbuf("xdt%d" % i, [P, 4, 64], F32) for i in range(2)]
        xdtd = [S.sbuf("xdtd%d" % i, [P, 4, 64], F32) for i in range(2)]
        ST = S.sbuf("ST", [P, 4, 64], F32)
        yt = [S.sbuf("yt%d" % i, [P, 256], F32) for i in range(2)]
        y2 = [S.sbuf("y2%d" % i, [P, 256], F32) for i in range(2)]
        yfl = [S.sbuf("yfl%d" % i, [P, 256], F32) for i in range(2)]
        zt = [S.sbuf("zt%d" % i, [P, 256], F32) for i in range(2)]
        yfb = [S.sub("yf%d" % c, yf.t[c]) for c in range(NCH)]

        def bc4(v):
            return View(v.buf, v.ap.rearrange("p (h o) -> p h o", o=1).to_broadcast([P, 4, 64]))
        if dbg_stop < 2:
            for c in range(NCH):
                S.dma("sp", zt[c % 2][:], zd[:, c, :])
                if dbg_stop == 1:
                    S.tt(zt[c % 2][:], zt[c % 2][:], xs_tok[:, c, :], ALU.add)
                    S.tt(zt[c % 2][:, 0:P], zt[c % 2][:, 0:P], B_tok[:, c, :], ALU.add)
                S.dma("pool", S.sub("yo", yo.t[c])[:], zt[c % 2][:])
        for dr in range(2 if dbg_stop >= 2 else 0):
            order = list(range(NCH)) if dr == 0 else (list(range(NCC - 1, -1, -1)) + list(range(NCH - 1, NCC - 1, -1)))
            S.memset(ST[:], 0.0)
            for it, c in enumerate(order):
                bi = it % 2
                a4 = aall[:, c, dr * 4:(dr + 1) * 4]
                for h in range(4):
                    S.ts(X[bi][:, h, :], strict[dr][:], aall[:, c, dr * 4 + h:dr * 4 + h + 1], ALU.mult, e=("dve" if h % 2 else "pool"))
                for h in range(4):
                    S.matmul(ps_arg[bi][:, h * P:(h + 1) * P], X[bi][:, h, :], tri[dr][:])
                S.act(LTt[bi][:].rearrange("p h l -> p (h l)"), ps_arg[bi][:], AF.Exp)
                S.matmul(ps_cb[:, 0:P], BT[:, c * P:(c + 1) * P], CT[:, c * P:(c + 1) * P])
                S.tt(CBm[:], ps_cb[:, 0:P], tri[dr][:], ALU.mult)
                S.tt(scT[bi][:], LTt[bi][:], View(CBm, CBm.t[:].rearrange("p (o l) -> p o l", o=1).to_broadcast([P, 4, P])), ALU.mult)
                S.matmul(ps_sm[:, 0:4], tri[dr][:], a4)
                S.matmul(ps_sm[:, 4:8], C.ones[:], a4)
                S.copy(sm[:], ps_sm[:, 0:8])
                S.act(eacs[:], sm[:, 0:4], AF.Exp)
                S.act(etot[:], sm[:, 4:8], AF.Exp)
                S.tt(dif[:], sm[:, 4:8], sm[:, 0:4], ALU.subtract)
                S.act(edec[:], dif[:], AF.Exp)
                xv = xs_tok[:, c, :].rearrange("p (h d) -> p h d", h=4)
                S.tt(xdt[bi][:], xv, bc4(dtv[:, c, dr * 4:(dr + 1) * 4]), ALU.mult, e="pool")
                S.tt(xdtd[bi][:], xdt[bi][:], bc4(edec[:]), ALU.mult)
                for h in range(4):
                    S.matmul(ps_y[:, h * 64:(h + 1) * 64], scT[bi][:, h, :], xdt[bi][:, h, :])
                S.matmul(ps_y[:, 256:512], CT[:, c * P:(c + 1) * P], ST[:].rearrange("p h d -> p (h d)"))
                y = yt[bi]
                S.tt(y[:].rearrange("p (h d) -> p h d", h=4), ps_y[:, 256:512].rearrange("p (h d) -> p h d", h=4), bc4(eacs[:]), ALU.mult)
                S.tt(y[:], y[:], ps_y[:, 0:256], ALU.add)
                S.matmul(ps_st[:, 0:256], B_tok[:, c, :], xdtd[bi][:].rearrange("p h d -> p (h d)"))
                S.tt(ST[:], ST[:], bc4(etot[:]), ALU.mult)
                S.tt(ST[:].rearrange("p h d -> p (h d)"), ST[:].rearrange("p h d -> p (h d)"), ps_st[:, 0:256], ALU.add)
                if dr == 0:
                    S.dma("pool", yfb[c][:], y[:])
                else:
                    S.dma("sp", yfl[bi][:], yfb[c][:])
                    S.dma("sp", zt[bi][:], zd[:, c, :])
                    o = y2[bi]
                    S.tt(o[:].rearrange("p (h d) -> p h d", h=4), xv, bc4(dsk[:]), ALU.mult, e="pool")
                    S.tt(y[:], y[:], yfl[bi][:], ALU.add)
                    S.tt(o[:], o[:], y[:], ALU.add)
                    S.act(zt[bi][:], zt[bi][:], AF.Silu)
                    S.tt(o[:], o[:], zt[bi][:], ALU.mult)
                    S.dma("pool", S.sub("yo", yo.t[c])[:], o[:])
        S.barrier()
    return nc


def build_Mgdn(L, LC):
    LT = L + LC
    NCH = LT // P
    NCC = LC // P
    nc = bass.Bass("TRN2", target_bir_lowering=False)
    with ExitStack() as st:
        S = Sched(nc, st)
        ql = S.dram("qkvl", [6, P, L + 4], F32, kind="ExternalInput")
        qc = S.dram("qkvc", [6, P, LC + 4], F32, kind="ExternalInput")
        cwd = S.dram("cw", [P, 6, 5], F32, kind="ExternalInput")
        zd = S.dram("z", [P, NCH, 256], F32, kind="ExternalInput")
        ad = S.dram("araw", [P, NCH, 4], F32, kind="ExternalInput")
        bd = S.dram("braw", [P, NCH, 4], F32, kind="ExternalInput")
        ald = S.dram("alog", [P, 4], F32, kind="ExternalInput")
        dbd = S.dram("dtb", [P, 4], F32, kind="ExternalInput")
        nwd = S.dram("nw", [P, P], F32, kind="ExternalInput")
        oa = S.dram("oa", [NCH, P, 256], F32, kind="ExternalOutput")
        ofd = S.dram("of", [NCH, P, 256], F32, kind="Internal")
        C = mk_consts(S, nc)
        M = {k: tri_mask(S, nc, "m_" + k, k, blk=64) for k in ("le", "ge", "gt", "lt")}
        halfA = S.sbuf("halfA", [P, P], F32)
        halfB = S.sbuf("halfB", [P, P], F32)
        S.memset(halfA[:], 0.0)
        S.memset(halfB[:], 0.0)
        S.memset(halfA[0:64, :], 1.0)
        S.memset(halfB[64:128, :], 1.0)
        cw = S.sbuf("cw", [P, 6, 5], F32)
        S.dma("sp", cw[:], cwd[:])
        nw = S.sbuf("nw", [P, P], F32)
        S.dma("sp", nw[:], nwd[:])
        gall = S.sbuf("gall", [P, NCH, 4], F32)
        ball = S.sbuf("ball", [P, NCH, 4], F32)
        negb = S.sbuf("negb", [P, NCH, 4], F32)
        aneg = S.sbuf("aneg", [P, 4], F32)
        dtb = S.sbuf("dtb", [P, 4], F32)
        S.dma("sp", gall[:], ad[:])
        S.dma("sp", ball[:], bd[:])
        S.dma("sp", aneg[:], ald[:])
        S.dma("sp", dtb[:], dbd[:])

        def bcn(v):
            return View(v.buf, v.ap.rearrange("p (o e) -> p o e", o=1).to_broadcast([P, NCH, 4]))
        S.tt(gall[:], gall[:], bcn(dtb[:]), ALU.add)
        S.act(gall[:], gall[:], AF.Exp)
        S.act(gall[:], gall[:], AF.Ln, bias=C.ones[:, 0:1])
        S.act(aneg[:], aneg[:], AF.Exp)
        S.ts(aneg[:], aneg[:], -1.0, ALU.mult)
        S.tt(gall[:], gall[:], bcn(aneg[:]), ALU.mult)
        S.act(ball[:], ball[:], AF.Sigmoid)
        S.ts(negb[:], ball[:], -1.0, ALU.mult)
        BA = [S.psum("BA%d" % h, [P, 512]) for h in range(2)]
        B1 = [S.psum("B1%d" % h, [P, 512]) for h in range(2)]
        B2 = [S.psum("B2%d" % h, [P, 512]) for h in range(2)]
        B3 = [S.psum("B3%d" % h, [P, 512]) for h in range(2)]
        Wk = []
        for h in range(2):
            W = NS()
            for nm in ("X", "Dm", "Dv", "Ds", "kbg", "kdec", "vb", "vnew", "oq", "o", "of_", "zt", "t1"):
                setattr(W, nm, S.sbuf("%s%d" % (nm, h), [P, P], F32))
            for nm in ("NA", "RA", "uw"):
                setattr(W, nm, S.sbuf("%s%d" % (nm, h), [P, 2 * P], F32))
            W.NR = [S.sbuf("NR%d%d" % (h, i), [P, 2 * P], F32) for i in range(2)]
            W.Xc = [S.sbuf("Xc%d%d" % (h, i), [P, P], F32) for i in range(2)]
            W.esm = S.sbuf("esm%d" % h, [P, 4], F32)
            W.bg = S.sbuf("bg%d" % h, [P, 1], F32)
            W.ss = S.sbuf("ss%d" % h, [P, 1], F32)
            W.oo = [S.sbuf("oo%d%d" % (h, i), [P, P], F32) for i in range(2)]
            Wk.append(W)
        state = [S.sbuf("state%d" % h, [P, P], F32) for h in range(2)]
        raw = S.sbuf("raw", [P, 6, 516], F32)
        acc = [S.sbuf("acc%d" % i, [P, 512], F32) for i in range(2)]
        sqb = S.sbuf("sqb", [P, 512], F32)
        lnb = S.sbuf("lnb", [P, 512], F32)
        rsb = S.sbuf("rsb", [P, 512], F32)
        qkv = [S.sbuf("qkv%d" % i, [P, 6, 512], F32) for i in range(2)]
        ofb = [[S.sub("of%d_%d" % (c, h), ofd.t[c][:, h * P:(h + 1) * P]) for h in range(2)] for c in range(NCH)]

        def prep(src, t0, n, dst):
            S.dma("sp", raw[:, :, :n + 4], src[:, :, t0:t0 + n + 4].rearrange("c p t -> p c t"))
            for c in range(6):
                a = acc[c % 2]
                S.ts(a[:, :n], raw[:, c, 0:n], cw[:, c, 0:1], ALU.mult)
                for j in range(1, 5):
                    S.stt(a[:, :n], raw[:, c, j:j + n], cw[:, c, j:j + 1], a[:, :n], ALU.mult, ALU.add)
                if c >= 4:
                    S.act(dst[:, c, :n], a[:, :n], AF.Silu)
                else:
                    S.act(a[:, :n], a[:, :n], AF.Silu)
                    S.act(sqb[:, :n], a[:, :n], AF.Square)
                    pb = B3[c % 2]
                    S.matmul(pb[:, :n], C.ones[:], sqb[:, :n])
                    S.act(lnb[:, :n], pb[:, :n], AF.Ln, bias=C.eps[:, 0:1])
                    S.act(rsb[:, :n], lnb[:, :n], AF.Exp, scale=-0.5)
                    S.stt(dst[:, c, :n], a[:, :n], (128.0 ** -0.5) if c < 2 else 1.0, rsb[:, :n], ALU.mult, ALU.mult)

        def unit(hl, dr, gp, qv, kv, vv):
            col = dr * 2 + hl
            g = gall[:, gp, col:col + 1]
            nb = negb[:, gp, col:col + 1]
            bt = ball[:, gp, col:col + 1]
            W = Wk[hl]
            bA, b1, b2, b3 = BA[hl], B1[hl], B2[hl], B3[hl]
            Tri, Xm, Val, SVal = (M["le"], M["gt"], M["ge"], M["gt"]) if dr == 0 else (M["ge"], M["lt"], M["le"], M["lt"])
            S.ts(W.X[:], Xm[:], g, ALU.mult)
            S.matmul(bA[:, 0:128], Tri[:], W.X[:])
            S.matmul(bA[:, 128:129], Tri[:], g)
            S.matmul(bA[:, 129:130], Xm[:], g)
            S.matmul(bA[:, 130:131], halfA[:], g)
            S.matmul(bA[:, 131:132], halfB[:], g)
            S.matmul(b1[:, 0:128], kv, kv)
            S.matmul(b1[:, 128:256], qv, kv)
            S.transpose(bA[:, 256:384], kv, C.ident[:])
            S.transpose(bA[:, 384:512], vv, C.ident[:])
            yield
            S.act(W.Dm[:], bA[:, 0:128], AF.Exp)
            S.act(W.esm[:], bA[:, 128:132], AF.Exp)
            S.tt(W.bg[:], W.esm[:, 0:1], bt, ALU.mult)
            S.act(W.kdec[:], bA[:, 256:384], AF.Identity, scale=W.esm[:, 1:2])
            S.act(W.vb[:], bA[:, 384:512], AF.Identity, scale=bt)
            S.act(W.kbg[:], bA[:, 256:384], AF.Identity, scale=W.bg[:, 0:1])
            S.tt(W.Dv[:], W.Dm[:], Val[:], ALU.mult)
            S.tt(W.Ds[:], W.Dm[:], SVal[:], ALU.mult)
            S.stt(W.NA[:, 0:128], b1[:, 0:128], nb, W.Ds[:], ALU.mult, ALU.mult)
            S.tt(W.NA[:, 128:256], b1[:, 128:256], W.Dv[:], ALU.mult)
            yield
            S.transpose(b1[:, 256:384], W.NA[:, 0:128], C.ident[:])
            S.transpose(b1[:, 384:512], W.NA[:, 128:256], C.ident[:])
            S.copy(W.RA[:], b1[:, 256:512])
            X = W.Xc[0]
            S.tt(X[:], W.RA[:, 0:128], C.ident[:], ALU.add)
            yield
            Ncur = W.NA[:, 0:128]
            Rcur = W.RA[:, 0:128]
            for lev in range(5):
                NR = W.NR[lev % 2]
                S.matmul(b2[:, 0:128], Rcur, Ncur)
                if lev < 4:
                    S.matmul(b2[:, 128:256], Ncur, Rcur)
                    S.copy(NR[:], b2[:, 0:256])
                else:
                    S.copy(NR[:, 0:128], b2[:, 0:128])
                yield
                S.matmul(b2[:, 256:384], NR[:, 0:128], X[:])
                Xn = W.Xc[(lev + 1) % 2]
                S.tt(Xn[:], X[:], b2[:, 256:384], ALU.add)
                X = Xn
                Ncur = NR[:, 0:128]
                Rcur = NR[:, 128:256]
                yield
            S.matmul(b3[:, 0:128], X[:], W.vb[:])
            S.matmul(b3[:, 128:256], W.kbg[:], X[:])
            S.copy(W.uw[:], b3[:, 0:256])
            yield
            blocks = [(0, 64), (64, 128)] if dr == 0 else [(64, 128), (0, 64)]
            Sst = state[hl]
            for bi, (r0, r1) in enumerate(blocks):
                reg = b3[:, 256:512] if bi == 0 else b3[:, 0:256]
                S.matmul(reg[:, 0:128], W.uw[:, 128:256], Sst[:])
                S.matmul(reg[:, 128:256], qv, Sst[:])
                S.tt(W.vnew[r0:r1, :], W.uw[r0:r1, 0:128], reg[r0:r1, 0:128], ALU.subtract)
                S.ts(W.oq[r0:r1, :], reg[r0:r1, 128:256], W.esm[r0:r1, 0:1], ALU.mult)
                yield
                S.matmul(b1[:, 0:128], W.kdec[r0:r1, :], W.vnew[r0:r1, :])
                egX = W.esm[:, 2:3] if r0 == 0 else W.esm[:, 3:4]
                S.stt(Sst[:], Sst[:], egX, b1[:, 0:128], ALU.mult, ALU.add)
                yield
            S.matmul(b1[:, 128:256], W.RA[:, 128:256], W.vnew[:])
            S.tt(W.o[:], W.oq[:], b1[:, 128:256], ALU.add)
            if dr == 0:
                S.dma("pool", ofb[gp][hl][:], W.o[:])
            else:
                S.dma("sp", W.of_[:], ofb[gp][hl][:])
                S.dma("sp", W.zt[:], zd[:, gp, hl * P:(hl + 1) * P])
                S.tt(W.o[:], W.o[:], W.of_[:], ALU.add)
                S.act(W.t1[:], W.o[:], AF.Square, accum_out=W.ss[:, 0:1])
                yield
                S.act(W.ss[:], W.ss[:], AF.Ln, scale=1.0 / 128.0, bias=C.eps[:, 0:1])
                S.act(W.ss[:], W.ss[:], AF.Exp, scale=-0.5)
                S.act(W.zt[:], W.zt[:], AF.Silu)
                S.stt(W.t1[:], W.o[:], W.ss[:, 0:1], nw[:], ALU.mult, ALU.mult)
                oo = W.oo[gp % 2]
                S.tt(oo[:], W.t1[:], W.zt[:], ALU.mult)
                S.dma("pool", S.sub("oa", oa.t[gp][:, hl * P:(hl + 1) * P])[:], oo[:])
            yield

        for dr in range(2):
            for h in range(2):
                S.memset(state[h][:], 0.0)
            segs = [(qc, 0, LC), (ql, LC, L)]
            tl = []
            for (src, base, seglen) in segs:
                tt_ = [(src, base, t0, min(512, seglen - t0)) for t0 in range(0, seglen, 512)]
                if dr == 1:
                    tt_ = tt_[::-1]
                tl += tt_
            for ti, (src, base, t0, n) in enumerate(tl):
                dst = qkv[ti % 2]
                prep(src, t0, n, dst)
                prs = list(range(n // P))
                if dr == 1:
                    prs = prs[::-1]
                for pi in prs:
                    gp = (base + t0) // P + pi
                    sl = slice(pi * P, (pi + 1) * P)
                    gens = [unit(h, dr, gp, dst[:, 0 + h, sl], dst[:, 2 + h, sl], dst[:, 4 + h, sl]) for h in range(2)]
                    alive = [True, True]
                    while any(alive):
                        for h in range(2):
                            if alive[h]:
                                try:
                                    next(gens[h])
                                except StopIteration:
                                    alive[h] = False
        S.barrier()
    return nc


NFM = 36
NTK = 1568
GRP = [[0, 1], [2, 3], [4, 5], [6, 7]]


def build_fused(L, LC, depth=2):
    NT, NCX = L // 2, LC // 2
    TT = NT + NCX
    LT = L + LC
    NCH = LT // P
    NCC = LC // P
    LK = LT
    NKC = LK // P
    NLC = NT // P
    assert NCX == P
    nc = bass.Bass("TRN2", target_bir_lowering=False)
    with ExitStack() as st:
        S = Sched(nc, st)
        ccsem = st.enter_context(nc.semaphore("ccsem"))
        cc = [0]
        xT = S.dram("xT", [P, KC, TT], F32, kind="ExternalInput")
        cv = S.dram("cv", [P, KC, 2], F32, kind="ExternalInput")
        selv = S.dram("selv", [P, 2], F32, kind="ExternalInput")
        yT = S.dram("yT", [P, KC, TT], F32, kind="ExternalOutput")
        Wl = []
        for i in range(depth):
            W = NS()
            sfx = "_%d" % i
            W.wada1 = S.dram("wada1" + sfx, [D, 5 * D], F32, kind="ExternalInput")
            W.bada1 = S.dram("bada1" + sfx, [P, 40], F32, kind="ExternalInput")
            W.wada2 = S.dram("wada2" + sfx, [D, 6 * D], F32, kind="ExternalInput")
            W.bada2 = S.dram("bada2" + sfx, [P, 48], F32, kind="ExternalInput")
            W.ng = S.dram("ng" + sfx, [P, 48], F32, kind="ExternalInput")
            W.w1a = S.dram("w1a" + sfx, [D, 2 * DFF], F32, kind="ExternalInput")
            W.w2a = S.dram("w2a" + sfx, [DFF, D], F32, kind="ExternalInput")
            W.w1b = S.dram("w1b" + sfx, [D, 2 * DFF], F32, kind="ExternalInput")
            W.w2b = S.dram("w2b" + sfx, [DFF, D], F32, kind="ExternalInput")
            W.win = S.dram("win" + sfx, [D, NFM * P + NTK], F32, kind="ExternalInput")
            W.wg = S.dram("wg" + sfx, [D, 3 * D], F32, kind="ExternalInput")
            W.wb = S.dram("wb" + sfx, [1536, D], F32, kind="ExternalInput")
            W.wo = S.dram("wo" + sfx, [D, D], F32, kind="ExternalInput")
            W.snw = S.dram("snw" + sfx, [P, 4], F32, kind="ExternalInput")
            W.lam = S.dram("lam" + sfx, [P, 256], F32, kind="ExternalInput")
            W.dnw = S.dram("dnw" + sfx, [P, 1], F32, kind="ExternalInput")
            W.li = S.dram("li" + sfx, [P, 1], F32, kind="ExternalInput")
            W.scw = S.dram("scw" + sfx, [P, 4, 5], F32, kind="ExternalInput")
            W.scb = S.dram("scb" + sfx, [P, 4], F32, kind="ExternalInput")
            W.sdtb = S.dram("sdtb" + sfx, [P, 8], F32, kind="ExternalInput")
            W.salog = S.dram("salog" + sfx, [P, 8], F32, kind="ExternalInput")
            W.sdsk = S.dram("sdsk" + sfx, [P, 4], F32, kind="ExternalInput")
            W.gcw = S.dram("gcw" + sfx, [P, 6, 5], F32, kind="ExternalInput")
            W.galog = S.dram("galog" + sfx, [P, 4], F32, kind="ExternalInput")
            W.gdtb = S.dram("gdtb" + sfx, [P, 4], F32, kind="ExternalInput")
            W.gnw = S.dram("gnw" + sfx, [P, P], F32, kind="ExternalInput")
            Wl.append(W)
        Xs = S.dram("Xs", [P, KC, TT], F32)
        X1 = S.dram("X1s", [P, KC, TT], F32)
        X2 = S.dram("X2s", [P, KC, TT], F32)
        NB = NT // 256
        PTL = nc.dram_tensor("PTL", [NFM, P, NT], F32)
        PTLG = nc.dram_tensor("PTLG", [NFM, 2, P, NT], F32)
        PTC = nc.dram_tensor("PTC", [NFM, P, NCX], F32)
        PTCG = nc.dram_tensor("PTCG", [2, 2, 18, P, NCX], F32)
        PKL = nc.dram_tensor("PKL", [NB, 256, NTK], F32)
        PKLG = nc.dram_tensor("PKLG", [NB, 2, 256, NTK], F32)
        PKC = nc.dram_tensor("PKC", [NCX, NTK], F32)
        PKCG = nc.dram_tensor("PKCG", [2, NCX, NTK], F32)
        MOL = nc.dram_tensor("MOL", [6, 2, P, NT], F32)
        MOLG = nc.dram_tensor("MOLG", [6, 2, 2, P, NT], F32)
        MOC = nc.dram_tensor("MOC", [6, 2, P, NCX], F32)
        MOCG = nc.dram_tensor("MOCG", [2, 6, 2, P, NCX], F32)
        OF = S.dram("OFs", [NCH, P, 256], F32)
        OFD = [S.dram("OFD%d" % d_, [NCH, P, 256], F32) for d_ in range(2)]

        def mo_dst(c0, c1, s_, off, n):
            if off >= NT:
                return MOC.ap()[c0:c1, s_, :, off - NT:off - NT + n]
            return MOL.ap()[c0:c1, s_, :, off:off + n]

        def dsub(ap):
            return Buf("u", ap, "dram")[:] if False else View(Buf("u", ap, "dram"), ap)

        tiles = mk_tiles(NT, NCX)
        C = mk_consts(S, nc)
        sel = S.sbuf("sel", [P, 2], F32)
        S.dma("sp", sel[:], selv[:])
        dummy = S.sbuf("dummy", [P, 1], F32)

        def blend(dst, alt):
            S.ts(dst, dst, sel[:, 0:1], ALU.mult)
            S.stt(dst, alt, sel[:, 1:2], dst, ALU.mult, ALU.add)

        def gather_many(pairs):
            S.barrier()
            for (i_ap, o_ap) in pairs:
                cc[0] += 1
                nc.gpsimd.collective_compute("AllGather", ALU.bypass, replica_groups=GRP, ins=[i_ap], outs=[o_ap]).then_inc(ccsem, 1)
            nc.gpsimd.wait_ge(ccsem, cc[0])
            S.memset(dummy[:], 0.0, e="pool")
            S.barrier()

        def gather_P():
            pr = [(PTL.ap()[c], PTLG.ap()[c].rearrange("r p t -> (r p) t")) for c in range(NFM)]
            pr += [(PTC.ap()[h * 18:(h + 1) * 18].rearrange("c p t -> (c p) t"), PTCG.ap()[h].rearrange("r c p t -> (r c p) t")) for h in range(2)]
            pr += [(PKL.ap()[b], PKLG.ap()[b].rearrange("r t e -> (r t) e")) for b in range(NB)]
            pr += [(PKC.ap(), PKCG.ap().rearrange("r t e -> (r t) e"))]
            gather_many(pr)

        def gather_M():
            pr = [(MOL.ap()[c, s_], MOLG.ap()[c, s_].rearrange("r p t -> (r p) t")) for c in range(6) for s_ in range(2)]
            pr += [(MOC.ap().rearrange("c s p t -> (c s p) t"), MOCG.ap().rearrange("r c s p t -> (r c s p) t"))]
            gather_many(pr)

        def tokpos(gc):
            if gc < NCC:
                return gc, NT
            t = (gc - NCC) * P
            return t // NT, t % NT

        def mk_ps():
            PS = NS()
            PS.ss = S.psum("ps_ss", [P, 512])
            PS.ss2 = S.psum("ps_ss2", [P, 512])
            PS.g = [S.psum("ps_g%d" % i, [P, 512]) for i in range(2)]
            PS.u = [S.psum("ps_u%d" % i, [P, 512]) for i in range(2)]
            PS.y = [S.psum("ps_y%d" % i, [P, 512]) for i in range(2)]
            return PS

        def bc(v):
            return View(v.buf, v.ap.rearrange("p (c o) -> p c o", o=1).to_broadcast([P, KC, 2]))

        def ph_R1(W, xin, x1t):
            with S.scope():
                alloc_small(S, C)
                PS = mk_ps()
                mods = S.sbuf("mods", [P, 40, 2], F32)
                ngt = S.sbuf("ngt", [P, 6, KC], F32)
                S.dma("sp", ngt[:], W.ng[:].rearrange("p (m c) -> p m c", c=KC))
                A1 = S.sbuf("A1", [P, KC, 2], F32)
                G1 = S.sbuf("G1", [P, KC, 2], F32)
                A2 = S.sbuf("A2", [P, KC, 2], F32)
                with S.scope():
                    stg = [S.sbuf("stgm%d" % i, [P, 5 * D], F32) for i in range(KC)]
                    compute_mods(S, C, cv, W.wada1, W.bada1, 5, stg, PS.g[0], mods)
                S.stt(A1[:], mods[:, 8:16, :], 1.0, bc(ngt[:, 0, :]), ALU.add, ALU.mult)
                S.stt(G1[:], mods[:, 16:24, :], 0.5, bc(ngt[:, 1, :]), ALU.mult, ALU.mult)
                S.stt(A2[:], mods[:, 32:40, :], 1.0, bc(ngt[:, 2, :]), ALU.add, ALU.mult)
                B1 = mods[:, 0:8, :]
                B2 = mods[:, 24:32, :]
                with S.scope():
                    w1b = S.sbuf("w1b", [P, KC, 2 * DFF], BF16)
                    w2b = S.sbuf("w2b", [P, FC, D], BF16)
                    with S.scope():
                        stages = [S.sbuf("wst%d" % i, [P, 2048], F32) for i in range(3)]
                        load_w(S, W.w1a, w1b, D, 2 * DFF, stages)
                        load_w(S, W.w2a, w2b, DFF, D, stages)
                    alloc_ffn_work(S, C)
                    ffn_sweep(S, C, tiles, lambda j: xin[j][:], lambda j: x1t[j][:], w1b, w2b, A1[:], B1, G1[:], PS)
                with S.scope():
                    NW = NFM * P + NTK
                    winb = S.sbuf("winb", [P, KC, NW], BF16)
                    with S.scope():
                        stages = [S.sbuf("wst%d" % i, [P, 2048], F32) for i in range(3)]
                        load_w(S, W.win, winb, D, NW, stages)
                    xt2 = [S.sbuf("xq%d" % i, [P, KC, 512], F32) for i in range(2)]
                    h = S.sbuf("h2", [P, KC, 512], BF16)
                    ost = [S.sbuf("ost%d" % i, [P, 4, 512], F32) for i in range(3)]
                    tst = [S.sbuf("tst%d" % i, [P, NTK], F32) for i in range(2)]
                    pps = PS.g + PS.u + PS.y
                    gi = 0
                    ti = 0
                    for j, (s0, n, col) in enumerate(tiles):
                        xt = xt2[j % 2]
                        S.dma("sp", xt[:, :, :n], x1t[j][:])
                        norm_mod(S, C, xt, n, A2[:], B2, col, h, PS.ss)
                        for c0 in range(0, NFM, 4):
                            nn = min(4, NFM - c0)
                            o = ost[gi % 3]
                            gi += 1
                            for cc_ in range(nn):
                                pp = pps[(c0 + cc_) % 6]
                                for k in range(KC):
                                    S.matmul(pp[:, :n], winb[:, k, (c0 + cc_) * P:(c0 + cc_ + 1) * P], h[:, k, :n],
                                             start=(k == 0), stop=(k == KC - 1))
                                S.copy(o[:, cc_, :n], pp[:, :n], e=("act" if cc_ % 2 else "dve"))
                            pdst = PTL.ap()[c0:c0 + nn, :, s0:s0 + n] if col == 0 else PTC.ap()[c0:c0 + nn, :, 0:n]
                            S.dma("pool", dsub(pdst.rearrange("c p t -> p c t")), o[:, :nn, :n])
                        for sb in range(n // P):
                            tt_ = tst[ti % 2]
                            ti += 1
                            for q, c0 in enumerate(range(0, NTK, 512)):
                                w = min(512, NTK - c0)
                                pp = pps[q % 6]
                                for k in range(KC):
                                    S.matmul(pp[:, :w], h[:, k, sb * P:(sb + 1) * P], winb[:, k, NFM * P + c0:NFM * P + c0 + w],
                                             start=(k == 0), stop=(k == KC - 1))
                                S.copy(tt_[:, c0:c0 + w], pp[:, :w], e=("act" if q % 2 else "dve"))
                            trow = s0 + sb * P
                            kdst = PKL.ap()[trow // 256, trow % 256:trow % 256 + P, :] if col == 0 else PKC.ap()[0:P, :]
                            S.dma("pool", dsub(kdst), tt_[:])

        def lat_rc(t0):
            return t0 // NT, t0 % NT

        def load_fm(q, dst, alt, c0, nch, seg, t0, n, halo):
            seglen = L if seg == "lat" else LC
            a, b = max(0, t0 - halo), min(seglen, t0 + n + halo)
            if halo and (t0 - halo < 0):
                S.memset(dst[:, :, 0:halo], 0.0)
                S.memset(alt[:, :, 0:halo], 0.0)
            if halo and (t0 + n + halo > seglen):
                S.memset(dst[:, :, n + halo:n + 2 * halo], 0.0)
                S.memset(alt[:, :, n + halo:n + 2 * halo], 0.0)
            pieces = []
            per = NT if seg == "lat" else NCX
            base = 0 if seg == "lat" else NT
            p = a
            while p < b:
                r = p // per
                e = min(b, (r + 1) * per)
                pieces.append((r, p % per, e - p, p - (t0 - halo)))
                p = e
            for g, tgt in ((0, dst), (1, alt)):
                for (r, col0, ln, d0) in pieces:
                    if seg == "lat":
                        sap = PTLG.ap()[g * 18 + c0:g * 18 + c0 + nch, r, :, col0:col0 + ln]
                    else:
                        sap = PTCG.ap()[g, r, c0:c0 + nch, :, col0:col0 + ln]
                    S.dma(q, tgt[:, :, d0:d0 + ln], dsub(sap.rearrange("c p t -> p c t")))
            blend(dst[:, :, :], alt[:, :, :])

        def load_tok(q, dst, alt, gc, e0, ne):
            s, off = tokpos(gc)
            for g, tgt in ((0, dst), (1, alt)):
                if off >= NT:
                    sap = PKCG.ap()[s, 0:P, g * 784 + e0:g * 784 + e0 + ne]
                else:
                    sap = PKLG.ap()[off // 256, s, off % 256:off % 256 + P, g * 784 + e0:g * 784 + e0 + ne]
                S.dma(q, tgt, dsub(sap))
            blend(dst, alt)

        def load_small(smallst, alt):
            for g, tgt in ((0, smallst), (1, alt)):
                for r in range(2):
                    S.dma("sp", tgt[:, r, :], dsub(PKCG.ap()[r, 0:P, g * 784 + 768:g * 784 + 784]))
                    for bq in range(NB):
                        c_ = NCC + r * NLC + 2 * bq
                        S.dma("sp" if bq % 2 else "pool", tgt[:, c_:c_ + 2, :],
                              dsub(PKLG.ap()[bq, r, :, g * 784 + 768:g * 784 + 784].rearrange("(c p) e -> p c e", p=P)))
            blend(smallst[:], alt[:])

        def ph_Mdiff(W):
            with S.scope():
                C.sq = [S.sbuf("sq%d" % i, [P, 512], F32) for i in range(2)]
                C.lnt = C.sq[0]
                onesb = S.sbuf("onesb", [P, P], BF16)
                S.memset(onesb[:], 1.0)
                Q = [S.sbuf("Q%d" % h, [P, L], BF16) for h in range(2)]
                QC = [S.sbuf("QC%d" % h, [P, LC], BF16) for h in range(2)]
                K = [S.sbuf("K%d" % h, [P, LK], BF16) for h in range(2)]
                V = S.sbuf("V", [P, NKC, 256], BF16)
                lam = S.sbuf("lamt", [P, 4, 64], F32)
                S.dma("sp", lam[:], W.lam[:].rearrange("p (a b) -> p a b", b=64))
                nw = S.sbuf("nwt", [P, 1], F32)
                li = S.sbuf("lit", [P, 1], F32)
                S.dma("sp", nw[:], W.dnw[:])
                S.dma("sp", li[:], W.li[:])
                pr = S.sbuf("pr", [P, 2, 64], F32)
                s12 = S.sbuf("s12", [P, 2], F32)
                S.tt(pr[:, 0, :], lam[:, 0, :], lam[:, 1, :], ALU.mult)
                S.tt(pr[:, 1, :], lam[:, 2, :], lam[:, 3, :], ALU.mult)
                S.reduce(s12[:], pr[:], ALU.add)
                e12 = S.sbuf("e12", [P, 2], F32)
                S.act(e12[:], s12[:], AF.Exp)
                neglam = S.sbuf("neglam", [P, 1], F32)
                S.tt(neglam[:], e12[:, 1:2], e12[:, 0:1], ALU.subtract)
                S.tt(neglam[:], neglam[:], li[:], ALU.subtract)
                sc2 = S.sbuf("sc2", [P, 1], F32)
                S.ts(sc2[:], li[:], -1.0, ALU.mult, 1.0, ALU.add)
                S.tt(sc2[:], sc2[:], nw[:], ALU.mult)
                with S.scope():
                    cosb = S.sbuf("cosb", [P, L], F32)
                    sinb = S.sbuf("sinb", [P, L], F32)
                    rope_tables(S, nc, C, L, cosb, sinb)
                    with S.scope():
                        a = [S.sbuf("la%d" % i, [P, 1, 512], F32) for i in range(2)]
                        a2 = [S.sbuf("la2%d" % i, [P, 1, 512], F32) for i in range(2)]
                        b = [S.sbuf("lb%d" % i, [P, 1, 512], F32) for i in range(2)]
                        b2 = [S.sbuf("lb2%d" % i, [P, 1, 512], F32) for i in range(2)]
                        vst = [S.sbuf("vst%d" % i, [P, 4, 256], F32) for i in range(2)]
                        vs2 = [S.sbuf("vs2%d" % i, [P, 4, 256], F32) for i in range(2)]
                        i = 0
                        for h in range(2):
                            for (cq, csw, dst) in ((6 + h, 8 + h, Q[h]), (10 + h, 12 + h, K[h])):
                                for c0 in range(0, L, 512):
                                    ta, tb = a[i % 2], b[i % 2]
                                    load_fm("sp", ta[:], a2[i % 2][:], cq, 1, "lat", c0, 512, 0)
                                    load_fm("pool", tb[:], b2[i % 2][:], csw, 1, "lat", c0, 512, 0)
                                    i += 1
                                    S.tt(ta[:, 0, :], ta[:, 0, :], cosb[:, c0:c0 + 512], ALU.mult)
                                    S.tt(tb[:, 0, :], tb[:, 0, :], sinb[:, c0:c0 + 512], ALU.mult, e="pool")
                                    S.tt(dst[:, c0:c0 + 512], ta[:, 0, :], tb[:, 0, :], ALU.add)
                            ta = a[i % 2]
                            load_fm("sp", ta[:, :, :LC], a2[i % 2][:, :, :LC], 10 + h, 1, "ctx", 0, LC, 0)
                            i += 1
                            S.copy(K[h][:, L:LK], ta[:, 0, :LC])
                            ta = a[i % 2]
                            load_fm("sp", ta[:, :, :LC], a2[i % 2][:, :, :LC], 6 + h, 1, "ctx", 0, LC, 0)
                            i += 1
                            S.copy(QC[h][:], ta[:, 0, :LC])
                        vi = 0
                        for r in range(2):
                            for bq in range(NB):
                                t, t2 = vst[vi % 2], vs2[vi % 2]
                                vi += 1
                                for g, tgt in ((0, t), (1, t2)):
                                    S.dma("sp", tgt[:, 0:2, :], dsub(PKLG.ap()[bq, r, :, g * 784:g * 784 + 256].rearrange("(c p) e -> p c e", p=P)))
                                blend(t[:, 0:2, :], t2[:, 0:2, :])
                                kc = r * NLC + 2 * bq
                                S.copy(V[:, kc:kc + 2, :], t[:, 0:2, :], e="pool")
                            t, t2 = vst[vi % 2], vs2[vi % 2]
                            vi += 1
                            for g, tgt in ((0, t), (1, t2)):
                                S.dma("sp", tgt[:, 0, :], dsub(PKCG.ap()[r, 0:P, g * 784:g * 784 + 256]))
                            blend(t[:, 0, :], t2[:, 0, :])
                            S.copy(V[:, L // P + r, :], t[:, 0, :], e="pool")
                ps_s = [[S.psum("ps_s%d%d" % (j, i), [P, 512]) for i in range(2)] for j in range(2)]
                ps_o = [S.psum("ps_o%d" % j, [P, 512]) for j in range(2)]
                ps_z = [S.psum("ps_z%d" % j, [P, 512]) for j in range(2)]
                pt = [[S.sbuf("pt%d%d" % (j, i), [P, 512], BF16) for i in range(2)] for j in range(2)]
                rz = [S.sbuf("rz%d" % j, [P, 512], F32) for j in range(2)]
                t0_ = S.sbuf("t0", [P, 512], F32)
                t1_ = S.sbuf("t1", [P, 512], F32)
                rstd = S.sbuf("rstd", [P, 512], F32)
                oo = [S.sbuf("oo%d" % i, [P, 512], F32) for i in range(2)]
                jobs = []
                for h in range(2):
                    for q0 in range(0, L, 512):
                        s_, off = lat_rc(q0)
                        jobs.append((h, Q[h][:, q0:q0 + 512], 512, 0, NKC, [(mo_dst(2 + h, 3 + h, s_, off, 512)[0], 0, 512)]))
                    jobs.append((h, QC[h][:], LC, L // P, NKC, [(mo_dst(2 + h, 3 + h, r, NT, NCX)[0], r * NCX, NCX) for r in range(2)]))
                zacc = [S.sbuf("zacc%d" % j, [P, 512], F32) for j in range(2)]
                zeng = ["dve", "dve"]
                for ji, (h, qv, n, kc0, kc1, outs) in enumerate(jobs):
                    def qk(kc):
                        for j in range(2):
                            S.matmul(ps_s[j][kc % 2][:, :n], K[h][j * 64:(j + 1) * 64, kc * P:(kc + 1) * P], qv[j * 64:(j + 1) * 64, :],
                                     start=True, stop=True)
                    qk(kc0)
                    for kc in range(kc0, kc1):
                        bi = kc % 2
                        for j in range(2):
                            S.act(pt[j][bi][:, :n], ps_s[j][bi][:, :n], AF.Exp, scale=0.125)
                        if kc + 1 < kc1:
                            qk(kc + 1)
                        for j in range(2):
                            S.matmul(ps_o[j][:, :n], V[:, kc, h * P:(h + 1) * P], pt[j][bi][:, :n], start=(kc == kc0), stop=(kc == kc1 - 1))
                            if kc == kc0:
                                S.copy(zacc[j][:, :n], pt[j][bi][:, :n], e=zeng[j])
                            else:
                                S.tt(zacc[j][:, :n], zacc[j][:, :n], pt[j][bi][:, :n], ALU.add, e=zeng[j])
                    for j in range(2):
                        S.matmul(ps_z[j][:, :n], C.ones[:], zacc[j][:, :n], start=True, stop=True)
                    for j in range(2):
                        S.recip(rz[j][:, :n], ps_z[j][:, :n])
                    S.tt(t0_[:, :n], ps_o[0][:, :n], rz[0][:, :n], ALU.mult)
                    S.tt(t1_[:, :n], ps_o[1][:, :n], rz[1][:, :n], ALU.mult)
                    S.stt(t0_[:, :n], t1_[:, :n], neglam[:, 0:1], t0_[:, :n], ALU.mult, ALU.add)
                    rms_rstd(S, C, lambda c: t0_[:, :n], n, 1, P, ps_s[0][0], rstd[:, :n])
                    o = oo[ji % 2]
                    S.tt(t1_[:, :n], t0_[:, :n], rstd[:, :n], ALU.mult)
                    S.ts(o[:, :n], t1_[:, :n], sc2[:, 0:1], ALU.mult)
                    for (oap, o0, on) in outs:
                        S.dma("pool", dsub(oap), o[:, o0:o0 + on])

        def ph_Mssd(W):
            with S.scope():
                tri = {0: tri_mask(S, nc, "tri_f", "le"), 1: tri_mask(S, nc, "tri_b", "ge")}
                strict = {0: tri_mask(S, nc, "str_f", "gt"), 1: tri_mask(S, nc, "str_b", "lt")}
                cw = S.sbuf("cw", [P, 4, 5], F32)
                cb = S.sbuf("cb", [P, 4], F32)
                S.dma("sp", cw[:], W.scw[:])
                S.dma("sp", cb[:], W.scb[:])
                xs_tok = S.sbuf("xs_tok", [P, NCH, 256], F32)
                B_tok = S.sbuf("B_tok", [P, NCH, P], F32)
                BT = S.sbuf("BT", [P, LT], F32)
                CT = S.sbuf("CT", [P, LT], F32)
                dtv = S.sbuf("dtv", [P, NCH, 8], F32)
                aall = S.sbuf("aall", [P, NCH, 8], F32)
                dtb = S.sbuf("dtb", [P, 8], F32)
                aneg = S.sbuf("aneg", [P, 8], F32)
                dsk = S.sbuf("dsk", [P, 4], F32)
                with S.scope():
                    sm1 = S.sbuf("sm1", [P, NCH, 16], F32)
                    sm2 = S.sbuf("sm2", [P, NCH, 16], F32)
                    load_small(sm1, sm2)
                    S.copy(dtv[:], sm1[:, :, 8:16])
                S.dma("sp", dtb[:], W.sdtb[:])
                S.dma("sp", aneg[:], W.salog[:])
                S.dma("sp", dsk[:], W.sdsk[:])
                S.tt(dtv[:], dtv[:], View(dtb, dtb.t[:].rearrange("p (o e) -> p o e", o=1).to_broadcast([P, NCH, 8])), ALU.add)
                S.act(dtv[:], dtv[:], AF.Exp)
                S.act(dtv[:], dtv[:], AF.Ln, bias=C.ones[:, 0:1])
                S.act(aneg[:], aneg[:], AF.Exp)
                S.ts(aneg[:], aneg[:], -1.0, ALU.mult)
                S.tt(aall[:], dtv[:], View(aneg, aneg.t[:].rearrange("p (o e) -> p o e", o=1).to_broadcast([P, NCH, 8])), ALU.mult)
                ps_t = [S.psum("ps_t%d" % i, [P, 512]) for i in range(2)]
                with S.scope():
                    raw = [S.sbuf("raw%d" % i, [P, 4, 516], F32) for i in range(2)]
                    raw2 = [S.sbuf("rawb", [P, 4, 516], F32)] * 2
                    acc = [S.sbuf("acc%d" % i, [P, 512], F32) for i in range(2)]
                    xsT = [S.sbuf("xsT%d" % i, [P, 512], F32) for i in range(2)]
                    segs = [("ctx", 0, LC), ("lat", LC, L)]
                    ti = 0
                    for (seg, base, seglen) in segs:
                        for t0 in range(0, seglen, 512):
                            n = min(512, seglen - t0)
                            r = raw[ti % 2]
                            load_fm("sp" if ti % 2 else "pool", r[:, :, :n + 4], raw2[ti % 2][:, :, :n + 4], 14, 4, seg, t0, n, 2)
                            ti += 1
                            for c in range(4):
                                a = acc[c % 2]
                                S.ts(a[:, :n], r[:, c, 0:n], cw[:, c, 0:1], ALU.mult)
                                for j in range(1, 5):
                                    S.stt(a[:, :n], r[:, c, j:j + n], cw[:, c, j:j + 1], a[:, :n], ALU.mult, ALU.add)
                                g0 = base + t0
                                if c < 2:
                                    S.act(xsT[c][:, :n], a[:, :n], AF.Silu, bias=cb[:, c:c + 1])
                                elif c == 2:
                                    S.act(BT[:, g0:g0 + n], a[:, :n], AF.Silu, bias=cb[:, c:c + 1])
                                else:
                                    S.act(CT[:, g0:g0 + n], a[:, :n], AF.Silu, bias=cb[:, c:c + 1])
                            for bl in range(n // P):
                                gc = (base + t0) // P + bl
                                pt = ps_t[bl % 2]
                                S.transpose(pt[:, 0:P], xsT[0][:, bl * P:(bl + 1) * P], C.ident[:])
                                S.transpose(pt[:, P:2 * P], xsT[1][:, bl * P:(bl + 1) * P], C.ident[:])
                                S.transpose(pt[:, 2 * P:3 * P], BT[:, gc * P:(gc + 1) * P], C.ident[:])
                                S.copy(xs_tok[:, gc, :], pt[:, 0:2 * P], e="dve")
                                S.copy(B_tok[:, gc, :], pt[:, 2 * P:3 * P], e="dve")
                ps_arg = [S.psum("ps_arg%d" % i, [P, 512]) for i in range(2)]
                ps_cb = S.psum("ps_cb", [P, 512])
                ps_yd = ps_t
                ps_std = [S.psum("ps_st%d" % i, [P, 512]) for i in range(2)]
                ps_sm = S.psum("ps_sm", [P, 512])
                X = [S.sbuf("X%d" % i, [P, 4, P], F32) for i in range(2)]
                LTt = [S.sbuf("LT%d" % i, [P, 4, P], F32) for i in range(2)]
                CBm = [S.sbuf("CBm%d" % i, [P, P], F32) for i in range(2)]
                scT = [S.sbuf("scT%d" % i, [P, 4, P], F32) for i in range(2)]
                sm = [S.sbuf("sm%d" % i, [P, 8], F32) for i in range(2)]
                eacs = [S.sbuf("eacs%d" % i, [P, 4], F32) for i in range(2)]
                edec = [S.sbuf("edec%d" % i, [P, 4], F32) for i in range(2)]
                etot = [S.sbuf("etot%d" % i, [P, 4], F32) for i in range(2)]
                dif = [S.sbuf("dif%d" % i, [P, 4], F32) for i in range(2)]
                xdt = [S.sbuf("xdt%d" % i, [P, 4, 64], F32) for i in range(2)]
                xdtd = [S.sbuf("xdtd%d" % i, [P, 4, 64], F32) for i in range(2)]
                STd = [S.sbuf("ST%d" % i, [P, 4, 64], F32) for i in range(2)]
                yt = [[S.sbuf("yt%d%d" % (d_, i), [P, 256], F32) for i in range(2)] for d_ in range(2)]
                y2 = [S.sbuf("y2%d" % i, [P, 256], F32) for i in range(2)]
                yfl = [S.sbuf("yfl%d" % i, [P, 256], F32) for i in range(2)]
                ybl = [S.sbuf("ybl%d" % i, [P, 256], F32) for i in range(2)]
                zt = [S.sbuf("zt%d" % i, [P, 256], F32) for i in range(2)]
                zt2 = [S.sbuf("ztb%d" % i, [P, 256], F32) for i in range(2)]
                oT = [S.sbuf("oT%d" % i, [P, 2, P], F32) for i in range(2)]
                yfb = [[S.sub("yf%d_%d" % (d_, c), OF.t[c][:, d_ * P:(d_ + 1) * P] if False else OFD[d_].t[c]) for c in range(NCH)] for d_ in range(2)]

                def bc4(v):
                    return View(v.buf, v.ap.rearrange("p (h o) -> p h o", o=1).to_broadcast([P, 4, 64]))

                def scan(dr):
                    order = list(range(NCH)) if dr == 0 else (list(range(NCC - 1, -1, -1)) + list(range(NCH - 1, NCC - 1, -1)))
                    ST = STd[dr]
                    ps_y = ps_yd[dr]
                    S.memset(ST[:], 0.0)
                    for it, c in enumerate(order):
                        a4 = aall[:, c, dr * 4:(dr + 1) * 4]
                        for h in range(4):
                            S.ts(X[dr][:, h, :], strict[dr][:], aall[:, c, dr * 4 + h:dr * 4 + h + 1], ALU.mult)
                        for h in range(4):
                            S.matmul(ps_arg[dr][:, h * P:(h + 1) * P], X[dr][:, h, :], tri[dr][:])
                        yield
                        S.act(LTt[dr][:].rearrange("p h l -> p (h l)"), ps_arg[dr][:], AF.Exp)
                        S.matmul(ps_cb[:, 0:P], BT[:, c * P:(c + 1) * P], CT[:, c * P:(c + 1) * P])
                        S.matmul(ps_sm[:, 0:4], tri[dr][:], a4)
                        S.matmul(ps_sm[:, 4:8], C.ones[:], a4)
                        S.tt(CBm[dr][:], ps_cb[:, 0:P], tri[dr][:], ALU.mult)
                        S.copy(sm[dr][:], ps_sm[:, 0:8])
                        yield
                        S.tt(scT[dr][:], LTt[dr][:], View(CBm[dr], CBm[dr].t[:].rearrange("p (o l) -> p o l", o=1).to_broadcast([P, 4, P])), ALU.mult)
                        S.act(eacs[dr][:], sm[dr][:, 0:4], AF.Exp)
                        S.act(etot[dr][:], sm[dr][:, 4:8], AF.Exp)
                        S.tt(dif[dr][:], sm[dr][:, 4:8], sm[dr][:, 0:4], ALU.subtract)
                        S.act(edec[dr][:], dif[dr][:], AF.Exp)
                        xv = xs_tok[:, c, :].rearrange("p (h d) -> p h d", h=4)
                        S.tt(xdt[dr][:], xv, bc4(dtv[:, c, dr * 4:(dr + 1) * 4]), ALU.mult)
                        S.tt(xdtd[dr][:], xdt[dr][:], bc4(edec[dr][:]), ALU.mult)
                        yield
                        for h in range(4):
                            S.matmul(ps_y[:, h * 64:(h + 1) * 64], scT[dr][:, h, :], xdt[dr][:, h, :])
                        S.matmul(ps_y[:, 256:512], CT[:, c * P:(c + 1) * P], ST[:].rearrange("p h d -> p (h d)"))
                        S.matmul(ps_std[dr][:, 0:256], B_tok[:, c, :], xdtd[dr][:].rearrange("p h d -> p (h d)"))
                        yield
                        y = yt[dr][it % 2]
                        S.tt(y[:].rearrange("p (h d) -> p h d", h=4), ps_y[:, 256:512].rearrange("p (h d) -> p h d", h=4), bc4(eacs[dr][:]), ALU.mult)
                        S.tt(y[:], y[:], ps_y[:, 0:256], ALU.add)
                        S.tt(ST[:], ST[:], bc4(etot[dr][:]), ALU.mult)
                        S.tt(ST[:].rearrange("p h d -> p (h d)"), ST[:].rearrange("p h d -> p (h d)"), ps_std[dr][:, 0:256], ALU.add)
                        S.dma("pool", yfb[dr][c][:], y[:])
                        yield

                gens = [scan(0), scan(1)]
                alive = [True, True]
                while any(alive):
                    for d_ in range(2):
                        if alive[d_]:
                            try:
                                next(gens[d_])
                            except StopIteration:
                                alive[d_] = False
                for c in range(NCH):
                    bi = c % 2
                    S.dma("sp", yfl[bi][:], yfb[0][c][:])
                    S.dma("sp", ybl[bi][:], yfb[1][c][:])
                    load_tok("sp", zt[bi][:], zt2[bi][:], c, 512, 256)
                    o = y2[bi]
                    xv = xs_tok[:, c, :].rearrange("p (h d) -> p h d", h=4)
                    S.tt(o[:].rearrange("p (h d) -> p h d", h=4), xv, bc4(dsk[:]), ALU.mult)
                    S.tt(yfl[bi][:], yfl[bi][:], ybl[bi][:], ALU.add)
                    S.tt(o[:], o[:], yfl[bi][:], ALU.add)
                    S.act(zt[bi][:], zt[bi][:], AF.Silu)
                    S.tt(o[:], o[:], zt[bi][:], ALU.mult)
                    S.transpose(ps_cb[:, P:2 * P], o[:, 0:P], C.ident[:])
                    S.transpose(ps_cb[:, 2 * P:3 * P], o[:, P:2 * P], C.ident[:])
                    S.copy(oT[bi][:].rearrange("p a b -> p (a b)"), ps_cb[:, P:3 * P])
                    s_, off = tokpos(c)
                    S.dma("pool", dsub(mo_dst(4, 6, s_, off, P).rearrange("c p t -> p c t")), oT[bi][:])

        def ph_Mgdn(W):
            with S.scope():
                M = {k: tri_mask(S, nc, "m_" + k, k, blk=64) for k in ("le", "ge", "gt", "lt")}
                halfA = S.sbuf("halfA", [P, P], F32)
                halfB = S.sbuf("halfB", [P, P], F32)
                S.memset(halfA[:], 0.0)
                S.memset(halfB[:], 0.0)
                S.memset(halfA[0:64, :], 1.0)
                S.memset(halfB[64:128, :], 1.0)
                cw = S.sbuf("cw", [P, 6, 5], F32)
                S.dma("sp", cw[:], W.gcw[:])
                nw = S.sbuf("nw", [P, P], F32)
                S.dma("sp", nw[:], W.gnw[:])
                gall = S.sbuf("gall", [P, NCH, 4], F32)
                ball = S.sbuf("ball", [P, NCH, 4], F32)
                negb = S.sbuf("negb", [P, NCH, 4], F32)
                aneg = S.sbuf("aneg", [P, 4], F32)
                dtb = S.sbuf("dtb", [P, 4], F32)
                with S.scope():
                    sm1 = S.sbuf("sm1", [P, NCH, 16], F32)
                    sm2 = S.sbuf("sm2", [P, NCH, 16], F32)
                    load_small(sm1, sm2)
                    S.copy(gall[:], sm1[:, :, 0:4])
                    S.copy(ball[:], sm1[:, :, 4:8])
                S.dma("sp", aneg[:], W.galog[:])
                S.dma("sp", dtb[:], W.gdtb[:])

                def bcn(v):
                    return View(v.buf, v.ap.rearrange("p (o e) -> p o e", o=1).to_broadcast([P, NCH, 4]))
                S.tt(gall[:], gall[:], bcn(dtb[:]), ALU.add)
                S.act(gall[:], gall[:], AF.Exp)
                S.act(gall[:], gall[:], AF.Ln, bias=C.ones[:, 0:1])
                S.act(aneg[:], aneg[:], AF.Exp)
                S.ts(aneg[:], aneg[:], -1.0, ALU.mult)
                S.tt(gall[:], gall[:], bcn(aneg[:]), ALU.mult)
                S.act(ball[:], ball[:], AF.Sigmoid)
                S.ts(negb[:], ball[:], -1.0, ALU.mult)
                BA = [S.psum("BA%d" % h, [P, 512]) for h in range(4)]
                BD = [S.psum("BD%d" % h, [P, 512]) for h in range(4)]
                Wk = []
                for h in range(4):
                    Wn = NS()
                    for nm in ("X", "Dm", "Dv", "Ds", "kbg", "kdec", "vb", "vnew", "oq", "o", "of_", "zt", "zt2", "t1"):
                        setattr(Wn, nm, S.sbuf("%s%d" % (nm, h), [P, P], F32))
                    for nm in ("NA", "RA", "uw"):
                        setattr(Wn, nm, S.sbuf("%s%d" % (nm, h), [P, 2 * P], F32))
                    Wn.NR = [S.sbuf("NR%d%d" % (h, i), [P, 2 * P], F32) for i in range(2)]
                    Wn.Xc = [S.sbuf("Xc%d%d" % (h, i), [P, P], F32) for i in range(2)]
                    Wn.esm = S.sbuf("esm%d" % h, [P, 4], F32)
                    Wn.bg = S.sbuf("bg%d" % h, [P, 1], F32)
                    Wn.ss = S.sbuf("ss%d" % h, [P, 1], F32)
                    Wn.oo = [S.sbuf("oo%d%d" % (h, i), [P, P], F32) for i in range(2)]
                    Wn.oT = [S.sbuf("oT%d%d" % (h, i), [P, P], F32) for i in range(2)]
                    Wk.append(Wn)
                state = [S.sbuf("state%d" % h, [P, P], F32) for h in range(4)]
                rawd = [S.sbuf("raw%d" % i, [P, 6, 516], F32) for i in range(2)]
                raw2 = S.sbuf("rawb", [P, 6, 516], F32)
                acc = [S.sbuf("acc%d" % i, [P, 512], F32) for i in range(2)]
                sqb = S.sbuf("sqb", [P, 512], F32)
                lnb = S.sbuf("lnb", [P, 512], F32)
                rsb = S.sbuf("rsb", [P, 512], F32)
                qkv = [[S.sbuf("qkv%d%d" % (d_, i), [P, 6, 512], F32) for i in range(2)] for d_ in range(2)]
                ofb = [[[S.sub("of%d_%d_%d" % (d_, c, h), OFD[d_].t[c][:, h * P:(h + 1) * P]) for h in range(2)] for c in range(NCH)] for d_ in range(2)]

                def prep(dr, seg, t0, n, dst, q):
                    raw = rawd[dr]
                    load_fm(q, raw[:, :, :n + 4], raw2[:, :, :n + 4], 0, 6, seg, t0, n, 2)
                    for c in range(6):
                        a = acc[c % 2]
                        S.ts(a[:, :n], raw[:, c, 0:n], cw[:, c, 0:1], ALU.mult)
                        for j in range(1, 5):
                            S.stt(a[:, :n], raw[:, c, j:j + n], cw[:, c, j:j + 1], a[:, :n], ALU.mult, ALU.add)
                        if c >= 4:
                            S.act(dst[:, c, :n], a[:, :n], AF.Silu)
                        else:
                            S.act(a[:, :n], a[:, :n], AF.Silu)
                            S.act(sqb[:, :n], a[:, :n], AF.Square)
                            pb = BD[2 * dr + (c % 2)]
                            S.matmul(pb[:, :n], C.ones[:], sqb[:, :n])
                            S.act(lnb[:, :n], pb[:, :n], AF.Ln, bias=C.eps[:, 0:1])
                            S.act(rsb[:, :n], lnb[:, :n], AF.Exp, scale=-0.5)
                            S.stt(dst[:, c, :n], a[:, :n], (128.0 ** -0.5) if c < 2 else 1.0, rsb[:, :n], ALU.mult, ALU.mult)

                def unit(ch, hl, dr, gp, qv, kv, vv):
                    col = dr * 2 + hl
                    g = gall[:, gp, col:col + 1]
                    nb = negb[:, gp, col:col + 1]
                    bt = ball[:, gp, col:col + 1]
                    Wn = Wk[ch]
                    bA, b1, b2, b3 = BA[ch], BD[ch], BD[ch], BD[ch]
                    Tri, Xm, Val, SVal = (M["le"], M["gt"], M["ge"], M["gt"]) if dr == 0 else (M["ge"], M["lt"], M["le"], M["lt"])
                    S.ts(Wn.X[:], Xm[:], g, ALU.mult)
                    S.matmul(bA[:, 0:128], Tri[:], Wn.X[:])
                    S.matmul(bA[:, 128:129], Tri[:], g)
                    S.matmul(bA[:, 129:130], Xm[:], g)
                    S.matmul(bA[:, 130:131], halfA[:], g)
                    S.matmul(bA[:, 131:132], halfB[:], g)
                    S.matmul(b1[:, 0:128], kv, kv)
                    S.matmul(b1[:, 128:256], qv, kv)
                    S.transpose(bA[:, 256:384], kv, C.ident[:])
                    S.transpose(bA[:, 384:512], vv, C.ident[:])
                    yield
                    S.act(Wn.Dm[:], bA[:, 0:128], AF.Exp)
                    S.act(Wn.esm[:], bA[:, 128:132], AF.Exp)
                    S.tt(Wn.bg[:], Wn.esm[:, 0:1], bt, ALU.mult)
                    S.act(Wn.kdec[:], bA[:, 256:384], AF.Identity, scale=Wn.esm[:, 1:2])
                    S.act(Wn.vb[:], bA[:, 384:512], AF.Identity, scale=bt)
                    S.act(Wn.kbg[:], bA[:, 256:384], AF.Identity, scale=Wn.bg[:, 0:1])
                    S.tt(Wn.Dv[:], Wn.Dm[:], Val[:], ALU.mult)
                    S.tt(Wn.Ds[:], Wn.Dm[:], SVal[:], ALU.mult)
                    S.stt(Wn.NA[:, 0:128], b1[:, 0:128], nb, Wn.Ds[:], ALU.mult, ALU.mult)
                    S.tt(Wn.NA[:, 128:256], b1[:, 128:256], Wn.Dv[:], ALU.mult)
                    yield
                    S.transpose(b1[:, 256:384], Wn.NA[:, 0:128], C.ident[:])
                    S.transpose(b1[:, 384:512], Wn.NA[:, 128:256], C.ident[:])
                    S.copy(Wn.RA[:], b1[:, 256:512])
                    X = Wn.Xc[0]
                    S.tt(X[:], Wn.RA[:, 0:128], C.ident[:], ALU.add)
                    yield
                    Ncur = Wn.NA[:, 0:128]
                    Rcur = Wn.RA[:, 0:128]
                    for lev in range(5):
                        NR = Wn.NR[lev % 2]
                        S.matmul(b2[:, 0:128], Rcur, Ncur)
                        if lev < 4:
                            S.matmul(b2[:, 128:256], Ncur, Rcur)
                            S.copy(NR[:], b2[:, 0:256])
                        else:
                            S.copy(NR[:, 0:128], b2[:, 0:128])
                        yield
                        S.matmul(b2[:, 256:384], NR[:, 0:128], X[:])
                        Xn = Wn.Xc[(lev + 1) % 2]
                        S.tt(Xn[:], X[:], b2[:, 256:384], ALU.add)
                        X = Xn
                        Ncur = NR[:, 0:128]
                        Rcur = NR[:, 128:256]
                        yield
                    S.matmul(b3[:, 0:128], X[:], Wn.vb[:])
                    S.matmul(b3[:, 128:256], Wn.kbg[:], X[:])
                    S.copy(Wn.uw[:], b3[:, 0:256])
                    yield
                    blocks = [(0, 64), (64, 128)] if dr == 0 else [(64, 128), (0, 64)]
                    Sst = state[ch]
                    for bi, (r0, r1) in enumerate(blocks):
                        reg = b3[:, 256:512] if bi == 0 else b3[:, 0:256]
                        S.matmul(reg[:, 0:128], Wn.uw[:, 128:256], Sst[:])
                        S.matmul(reg[:, 128:256], qv, Sst[:])
                        S.tt(Wn.vnew[r0:r1, :], Wn.uw[r0:r1, 0:128], reg[r0:r1, 0:128], ALU.subtract)
                        S.ts(Wn.oq[r0:r1, :], reg[r0:r1, 128:256], Wn.esm[r0:r1, 0:1], ALU.mult)
                        yield
                        S.matmul(b1[:, 0:128], Wn.kdec[r0:r1, :], Wn.vnew[r0:r1, :])
                        egX = Wn.esm[:, 2:3] if r0 == 0 else Wn.esm[:, 3:4]
                        S.stt(Sst[:], Sst[:], egX, b1[:, 0:128], ALU.mult, ALU.add)
                        yield
                    S.matmul(b1[:, 128:256], Wn.RA[:, 128:256], Wn.vnew[:])
                    S.tt(Wn.o[:], Wn.oq[:], b1[:, 128:256], ALU.add)
                    S.dma("pool", ofb[dr][gp][hl][:], Wn.o[:])
                    yield

                for h in range(4):
                    S.memset(state[h][:], 0.0)
                segs = [("ctx", 0, LC), ("lat", LC, L)]
                tld = []
                for dr in range(2):
                    tl = []
                    for (seg, base, seglen) in segs:
                        tt_ = [(seg, base, t0, min(512, seglen - t0)) for t0 in range(0, seglen, 512)]
                        if dr == 1:
                            tt_ = tt_[::-1]
                        tl += tt_
                    tld.append(tl)
                for ti in range(len(tld[0])):
                    dsts = []
                    for dr in range(2):
                        (seg, base, t0, n) = tld[dr][ti]
                        dst = qkv[dr][ti % 2]
                        prep(dr, seg, t0, n, dst, "sp" if dr else "pool")
                        dsts.append(dst)
                    npairs = tld[0][ti][3] // P
                    for pj in range(npairs):
                        gens = []
                        for dr in range(2):
                            (seg, base, t0, n) = tld[dr][ti]
                            pi = pj if dr == 0 else npairs - 1 - pj
                            gp = (base + t0) // P + pi
                            sl = slice(pi * P, (pi + 1) * P)
                            for hl in range(2):
                                gens.append(unit(dr * 2 + hl, hl, dr, gp, dsts[dr][:, 0 + hl, sl], dsts[dr][:, 2 + hl, sl], dsts[dr][:, 4 + hl, sl]))
                        alive = [True] * 4
                        while any(alive):
                            for gi_ in range(4):
                                if alive[gi_]:
                                    try:
                                        next(gens[gi_])
                                    except StopIteration:
                                        alive[gi_] = False
                for gp in range(NCH):
                    for hl in range(2):
                        Wn = Wk[hl + 2 * (gp % 2)]
                        b1 = BD[hl + 2 * (gp % 2)]
                        S.dma("sp", Wn.o[:], ofb[0][gp][hl][:])
                        S.dma("sp", Wn.of_[:], ofb[1][gp][hl][:])
                        load_tok("sp", Wn.zt[:], Wn.zt2[:], gp, 256 + hl * P, P)
                        S.tt(Wn.o[:], Wn.o[:], Wn.of_[:], ALU.add)
                        S.act(Wn.t1[:], Wn.o[:], AF.Square, accum_out=Wn.ss[:, 0:1])
                        S.act(Wn.ss[:], Wn.ss[:], AF.Ln, scale=1.0 / 128.0, bias=C.eps[:, 0:1])
                        S.act(Wn.ss[:], Wn.ss[:], AF.Exp, scale=-0.5)
                        S.act(Wn.zt[:], Wn.zt[:], AF.Silu)
                        S.stt(Wn.t1[:], Wn.o[:], Wn.ss[:, 0:1], nw[:], ALU.mult, ALU.mult)
                        oo = Wn.oo[0]
                        S.tt(oo[:], Wn.t1[:], Wn.zt[:], ALU.mult)
                        S.transpose(b1[:, 256:384], oo[:], C.ident[:])
                        oT = Wn.oT[0]
                        S.copy(oT[:], b1[:, 256:384])
                        s_, off = tokpos(gp)
                        S.dma("pool", dsub(mo_dst(hl, hl + 1, s_, off, P)[0]), oT[:])

        def ph_R2(W, x1t, xout):
            with S.scope():
                alloc_small(S, C)
                PS = mk_ps()
                mods = S.sbuf("mods", [P, 48, 2], F32)
                ngt = S.sbuf("ngt", [P, 6, KC], F32)
                S.dma("sp", ngt[:], W.ng[:].rearrange("p (m c) -> p m c", c=KC))
                snt = S.sbuf("snt", [P, 4], F32)
                S.dma("sp", snt[:], W.snw[:])
                with S.scope():
                    stg = [S.sbuf("stgm%d" % i, [P, 6 * D], F32) for i in range(KC)]
                    compute_mods(S, C, cv, W.wada2, W.bada2, 6, stg, PS.g[0], mods)
                A2 = S.sbuf("A2", [P, KC, 2], F32)
                G3 = S.sbuf("G3", [P, KC, 2], F32)
                A4 = S.sbuf("A4", [P, KC, 2], F32)
                G5 = S.sbuf("G5", [P, KC, 2], F32)
                S.stt(A2[:], mods[:, 8:16, :], 1.0, bc(ngt[:, 2, :]), ALU.add, ALU.mult)
                S.tt(G3[:], mods[:, 16:24, :], bc(ngt[:, 3, :]), ALU.mult)
                S.stt(A4[:], mods[:, 32:40, :], 1.0, bc(ngt[:, 4, :]), ALU.add, ALU.mult)
                S.stt(G5[:], mods[:, 40:48, :], 0.5, bc(ngt[:, 5, :]), ALU.mult, ALU.mult)
                B2 = mods[:, 0:8, :]
                B4 = mods[:, 24:32, :]
                x2t = [S.sub("x2t%d" % j, X2.t[:, :, s0:s0 + n]) for j, (s0, n, col) in enumerate(tiles)]
                with S.scope():
                    wgb = S.sbuf("wgb", [P, KC, 3 * D], BF16)
                    wbb = S.sbuf("wbb", [P, 12, D], BF16)
                    wob = S.sbuf("wob", [P, KC, D], BF16)
                    with S.scope():
                        stages = [S.sbuf("wst%d" % i, [P, 2048], F32) for i in range(3)]
                        load_w(S, W.wg, wgb, D, 3 * D, stages)
                        load_w(S, W.wb, wbb, 1536, D, stages)
                        load_w(S, W.wo, wob, D, D, stages)
                    xt = S.sbuf("xm", [P, KC, 512], F32)
                    h = S.sbuf("hm", [P, KC, 512], BF16)
                    ost = S.sbuf("ostg", [P, 4, 512], F32)
                    ost2 = S.sbuf("ostg2", [P, 4, 512], F32)
                    ob16 = S.sbuf("ob16", [P, 12, 512], BF16)
                    yacc = S.sbuf("yacc", [P, 512], F32)
                    ybf = S.sbuf("ybf", [P, KC, 512], BF16)
                    yy = S.sbuf("yy", [P, KC, 512], F32)
                    gt = [S.sbuf("gt%d" % i, [P, 512], F32) for i in range(2)]
                    for j, (s0, n, col) in enumerate(tiles):
                        S.dma("sp", xt[:, :, :n], x1t[j][:])
                        norm_mod(S, C, xt, n, A2[:], B2, col, h, PS.ss)
                        for br in range(3):
                            for sc_, tgt in ((0, ost), (1, ost2)):
                                for r in range(2):
                                    if col == 0:
                                        sap = MOLG.ap()[2 * br:2 * br + 2, sc_, r, :, s0:s0 + n]
                                    else:
                                        sap = MOCG.ap()[r, 2 * br:2 * br + 2, sc_, :, 0:n]
                                    S.dma("sp" if r else "pool", tgt[:, 2 * r:2 * r + 2, :n], dsub(sap.rearrange("c p t -> p c t")))
                            blend(ost[:, :, :n], ost2[:, :, :n])
                            if br < 2:
                                S.copy(ob16[:, br * 4:(br + 1) * 4, :n], ost[:, :, :n], e="pool")
                            else:
                                rms_rstd(S, C, lambda c: ost[:, c, :n], n, 4, 512, PS.ss2, C.rstd2[:, :n])
                                for c in range(4):
                                    t = C.tmp[c % 2]
                                    S.tt(t[:, :n], ost[:, c, :n], C.rstd2[:, :n], ALU.mult)
                                    S.act(ob16[:, 8 + c, :n], t[:, :n], AF.Copy, scale=snt[:, c:c + 1])
                        for d in range(KC):
                            for br in range(3):
                                pg = PS.g[br % 2]
                                pu = PS.u[br % 2]
                                cg = br * KC + d
                                for k in range(KC):
                                    S.matmul(pg[:, :n], wgb[:, k, cg * P:(cg + 1) * P], h[:, k, :n], start=(k == 0), stop=(k == KC - 1))
                                for k in range(4):
                                    S.matmul(pu[:, :n], wbb[:, br * 4 + k, d * P:(d + 1) * P], ob16[:, br * 4 + k, :n], start=(k == 0), stop=(k == 3))
                                g = gt[br % 2]
                                S.act(g[:, :n], pg[:, :n], AF.Sigmoid)
                                if br == 0:
                                    S.tt(yacc[:, :n], g[:, :n], pu[:, :n], ALU.mult)
                                else:
                                    t = C.tmp[br % 2]
                                    S.tt(t[:, :n], g[:, :n], pu[:, :n], ALU.mult)
                                    if br == 1:
                                        S.tt(yacc[:, :n], yacc[:, :n], t[:, :n], ALU.add)
                                    else:
                                        S.tt(ybf[:, d, :n], yacc[:, :n], t[:, :n], ALU.add)
                        for d in range(KC):
                            py = PS.y[d % 2]
                            for k in range(KC):
                                S.matmul(py[:, :n], wob[:, k, d * P:(d + 1) * P], ybf[:, k, :n], start=(k == 0), stop=(k == KC - 1))
                            S.copy(yy[:, d, :n], py[:, :n], e="dve")
                        rms_rstd(S, C, lambda c: yy[:, c, :n], n, KC, D, PS.ss2, C.rstd2[:, :n])
                        for c in range(KC):
                            t = C.tmp[c % 2]
                            S.tt(t[:, :n], yy[:, c, :n], C.rstd2[:, :n], ALU.mult)
                            S.stt(xt[:, c, :n], t[:, :n], G3[:, c, col:col + 1], xt[:, c, :n], ALU.mult, ALU.add)
                        S.dma("pool", x2t[j][:], xt[:, :, :n])
                with S.scope():
                    w1b = S.sbuf("w1b", [P, KC, 2 * DFF], BF16)
                    w2b = S.sbuf("w2b", [P, FC, D], BF16)
                    with S.scope():
                        stages = [S.sbuf("wst%d" % i, [P, 2048], F32) for i in range(3)]
                        load_w(S, W.w1b, w1b, D, 2 * DFF, stages)
                        load_w(S, W.w2b, w2b, DFF, D, stages)
                    alloc_ffn_work(S, C)
                    ffn_sweep(S, C, tiles, lambda j: x2t[j][:], lambda j: xout[j][:], w1b, w2b, A4[:], B4, G5[:], PS)

        xin = [S.sub("xin%d" % j, xT.t[:, :, s0:s0 + n]) for j, (s0, n, col) in enumerate(tiles)]
        for i in range(depth):
            W = Wl[i]
            x1t = [S.sub("x1t%d_%d" % (i, j), X1.t[:, :, s0:s0 + n]) for j, (s0, n, col) in enumerate(tiles)]
            dbg = "Z"
            ph_R1(W, xin, x1t)
            if dbg >= "B":
                gather_P()
            if dbg >= "C":
                ph_Mdiff(W)
            if dbg >= "D":
                ph_Mssd(W)
            if dbg >= "E":
                ph_Mgdn(W)
            if dbg >= "F":
                gather_M()
            last = i == depth - 1
            dstT = yT if last else Xs
            xout = [S.sub("xo%d_%d" % (i, j), dstT.t[:, :, s0:s0 + n]) for j, (s0, n, col) in enumerate(tiles)]
            ph_R2(W, x1t, xout)
            xin = xout
        S.barrier()
    return nc


def fm(a):
    T, F = a.shape
    return np.ascontiguousarray(a.T.reshape(F // P, P, T).transpose(1, 0, 2))

def unfm(a):
    p, C, T = a.shape
    return np.ascontiguousarray(a.transpose(2, 1, 0).reshape(T, C * P))

def vec_fm(v):
    return np.ascontiguousarray(v.reshape(-1, P).T)

def r1_cols():
    sw = np.arange(512) ^ 1
    cols = []
    cols += list(range(0, 2048))
    cols += list(range(2064, 2576))
    cols += list(2064 + sw)
    cols += list(range(2576, 3088))
    cols += list(2576 + sw)
    cols += list(range(3088, 3600))
    cols += list(range(3600, 5136))
    cols += list(range(2048, 2064)) + list(range(5136, 5152)) + [0] * 96
    cols = np.array(cols)
    assert len(cols) == 49 * 128
    return cols

def r1_inputs(inp, i, core, NT, NCX, xcur, ctxcur):
    b, s = core // 2, core % 2
    tok = np.concatenate([xcur[b, s * NT:(s + 1) * NT], ctxcur[b, s * NCX:(s + 1) * NCX]], 0)
    cv = np.stack([inp["c"][b], inp["c_ctx"]], -1)
    cv = np.ascontiguousarray(cv.reshape(8, P, 2).transpose(1, 0, 2))
    return {
        "xT": fm(tok),
        "cv": cv,
        "wada": np.ascontiguousarray(inp["w_ada"][i][:, :5 * 1024]),
        "bada": vec_fm(inp["b_ada"][i][:5 * 1024]),
        "ng": np.ascontiguousarray(inp["norm_g"][i].reshape(6, 8, P).transpose(2, 0, 1).reshape(P, 48)),
        "w1": np.ascontiguousarray(inp["w_ffn_in"][i, 0]),
        "w2": np.ascontiguousarray(inp["w_ffn_out"][i, 0]),
        "win": np.ascontiguousarray(inp["w_in"][i][:, r1_cols()]),
    }

def r2_inputs(inp, i, core, x1T, oaT, obT, ocT):
    b = core // 2
    cv = np.stack([inp["c"][b], inp["c_ctx"]], -1)
    cv = np.ascontiguousarray(cv.reshape(8, P, 2).transpose(1, 0, 2))
    return {
        "x1T": x1T, "oaT": oaT, "obT": obT, "ocT": ocT, "cv": cv,
        "wada": np.ascontiguousarray(inp["w_ada"][i][:, 3 * 1024:]),
        "bada": vec_fm(inp["b_ada"][i][3 * 1024:]),
        "ng": np.ascontiguousarray(inp["norm_g"][i].reshape(6, 8, P).transpose(2, 0, 1).reshape(P, 48)),
        "snw": vec_fm(inp["ssd_norm_w"][i]),
        "wg": np.ascontiguousarray(inp["w_in"][i][:, 5152:8224]),
        "wb": np.ascontiguousarray(inp["w_branch"][i].reshape(1536, 1024)),
        "wo": np.ascontiguousarray(inp["w_out"][i]),
        "w1": np.ascontiguousarray(inp["w_ffn_in"][i, 1]),
        "w2": np.ascontiguousarray(inp["w_ffn_out"][i, 1]),
    }

def split_P(PT_cores, NT, NCX, b):
    a0, a1 = PT_cores[2 * b], PT_cores[2 * b + 1]
    lat = np.concatenate([a0[:, :, :NT], a1[:, :, :NT]], 2)
    cx = np.concatenate([a0[:, :, NT:], a1[:, :, NT:]], 2)
    return lat, cx

def mdiff_inputs(inp, i, core, lat, cx):
    b, hh = core // 2, core % 2
    hs = [2 * hh, 2 * hh + 1]
    L = lat.shape[2]; LC = cx.shape[2]
    qT = lat[[16 + h for h in hs]]
    qsT = lat[[20 + h for h in hs]]
    kT = np.concatenate([lat[[24 + h for h in hs]], cx[[24 + h for h in hs]]], 2)
    ksT = lat[[28 + h for h in hs]]
    qcT = cx[[16 + h for h in hs]]
    v = np.concatenate([lat[[32 + h for h in hs]], cx[[32 + h for h in hs]]], 2)
    LK = L + LC
    v = v.transpose(2, 0, 1).reshape(LK // P, P, 256).transpose(1, 0, 2)
    lam_init = 0.8 - 0.6 * np.exp(-0.3 * i)
    return {"qT": np.ascontiguousarray(qT), "qsT": np.ascontiguousarray(qsT), "kT": np.ascontiguousarray(kT),
            "ksT": np.ascontiguousarray(ksT), "qcT": np.ascontiguousarray(qcT), "v": np.ascontiguousarray(v),
            "lam": np.ascontiguousarray(np.broadcast_to(inp["diff_lambda"][i].reshape(1, 256), (P, 256))),
            "nw": np.ascontiguousarray(inp["diff_norm_w"][i].reshape(P, 1)),
            "li": np.full((P, 1), lam_init, np.float32)}

def pad2(a):
    return np.pad(a, ((0, 0), (0, 0), (2, 2)))

def tokmaj(a):
    Cc, p, T = a.shape
    return np.ascontiguousarray(a.transpose(2, 0, 1).reshape(T // P, P, Cc * P).transpose(1, 0, 2))

def mssd_inputs(inp, i, core, lat, cx):
    b, g = core // 2, core % 2
    ch = [40 + 2 * g, 41 + 2 * g, 44 + g, 46 + g]
    wcols = np.concatenate([np.arange(256 * g, 256 * g + 256), 512 + 128 * g + np.arange(128), 768 + 128 * g + np.arange(128)])
    cwv = inp["ssd_conv_w"][i][:, wcols]
    cbv = inp["ssd_conv_b"][i][wcols]
    zc = [36 + 2 * g, 37 + 2 * g]
    z = np.concatenate([tokmaj(cx[zc]), tokmaj(lat[zc])], 1)
    rows = [16 + d * 8 + 4 * g + h for d in range(2) for h in range(4)]
    dtl = lat[48][rows]; dtc = cx[48][rows]
    dt = np.concatenate([dtc, dtl], 1)
    LT = dt.shape[1]
    dt = np.ascontiguousarray(dt.T.reshape(LT // P, P, 8).transpose(1, 0, 2))
    hsel = [4 * g + h for h in range(4)]
    return {"xbcl": np.ascontiguousarray(pad2(lat[ch])), "xbcc": np.ascontiguousarray(pad2(cx[ch])),
            "cw": np.ascontiguousarray(cwv.T.reshape(4, P, 5).transpose(1, 0, 2)),
            "cb": np.ascontiguousarray(cbv.reshape(4, P).T),
            "z": z, "dt": dt,
            "dtb": np.ascontiguousarray(np.broadcast_to(inp["ssd_dt_bias"][i][:, hsel].reshape(1, 8), (P, 8))),
            "alog": np.ascontiguousarray(np.broadcast_to(inp["ssd_a_log"][i][:, hsel].reshape(1, 8), (P, 8))),
            "dskip": np.ascontiguousarray(np.broadcast_to(inp["ssd_d"][i][hsel].reshape(1, 4), (P, 4)))}

def mgdn_inputs(inp, i, core, lat, cx):
    b, hh = core // 2, core % 2
    hs = [2 * hh, 2 * hh + 1]
    ch = [0 + hs[0], 0 + hs[1], 4 + hs[0], 4 + hs[1], 8 + hs[0], 8 + hs[1]]
    wcols = np.concatenate([off + h * 128 + np.arange(128) for off in (0, 512, 1024) for h in hs])
    cwv = inp["gdn_conv_w"][i][:, wcols]
    zc = [12 + hs[0], 12 + hs[1]]
    z = np.concatenate([tokmaj(cx[zc]), tokmaj(lat[zc])], 1)
    def small(rows):
        v = np.concatenate([cx[48][rows], lat[48][rows]], 1)
        LT = v.shape[1]
        return np.ascontiguousarray(v.T.reshape(LT // P, P, 4).transpose(1, 0, 2))
    arows = [d * 4 + h for d in range(2) for h in hs]
    brows = [8 + d * 4 + h for d in range(2) for h in hs]
    return {"qkvl": np.ascontiguousarray(pad2(lat[ch])), "qkvc": np.ascontiguousarray(pad2(cx[ch])),
            "cw": np.ascontiguousarray(cwv.T.reshape(6, P, 5).transpose(1, 0, 2)),
            "z": z, "araw": small(arows), "braw": small(brows),
            "alog": np.ascontiguousarray(np.broadcast_to(inp["gdn_a_log"][i][:, hs].reshape(1, 4), (P, 4))),
            "dtb": np.ascontiguousarray(np.broadcast_to(inp["gdn_dt_bias"][i][:, hs].reshape(1, 4), (P, 4))),
            "nw": np.ascontiguousarray(np.broadcast_to(inp["gdn_norm_w"][i].reshape(1, P), (P, P)))}


def fused_cols():
    sw = np.arange(128) ^ 1
    ar = np.arange(128)
    fmc, tkc = [], []
    for g in (0, 1):
        hs = [2 * g, 2 * g + 1]
        for off in (0, 512, 1024):
            for h in hs:
                fmc += list(off + h * 128 + ar)
        for h in hs:
            fmc += list(2064 + h * 128 + ar)
        for h in hs:
            fmc += list(2064 + h * 128 + sw)
        for h in hs:
            fmc += list(2576 + h * 128 + ar)
        for h in hs:
            fmc += list(2576 + h * 128 + sw)
        fmc += list(4112 + g * 256 + np.arange(256))
        fmc += list(4112 + 512 + g * 128 + ar)
        fmc += list(4112 + 768 + g * 128 + ar)
    for g in (0, 1):
        hs = [2 * g, 2 * g + 1]
        for h in hs:
            tkc += list(3088 + h * 128 + ar)
        for h in hs:
            tkc += list(1536 + h * 128 + ar)
        tkc += list(3600 + g * 256 + np.arange(256))
        tkc += [2048 + d * 4 + h for d in range(2) for h in hs]
        tkc += [2056 + d * 4 + h for d in range(2) for h in hs]
        tkc += [5136 + d * 8 + 4 * g + h for d in range(2) for h in range(4)]
    cols = np.array(fmc + tkc)
    assert len(cols) == 36 * 128 + 1568
    return cols


def fused_inputs(inp, core, NT, NCX, depth=2):
    b, hh = core // 2, core % 2
    s = hh
    tok = np.concatenate([inp["x"][b, s * NT:(s + 1) * NT], inp["ctx"][b, s * NCX:(s + 1) * NCX]], 0)
    cv = np.stack([inp["c"][b], inp["c_ctx"]], -1)
    cv = np.ascontiguousarray(cv.reshape(8, P, 2).transpose(1, 0, 2))
    d = {"xT": fm(tok), "cv": cv, "selv": np.ascontiguousarray(np.broadcast_to(np.array([[1.0 - hh, float(hh)]], np.float32), (P, 2)))}
    cols = fused_cols()
    hs = [2 * hh, 2 * hh + 1]
    g = hh
    for i in range(depth):
        sfx = "_%d" % i
        d["wada1" + sfx] = np.ascontiguousarray(inp["w_ada"][i][:, :5 * 1024])
        d["bada1" + sfx] = vec_fm(inp["b_ada"][i][:5 * 1024])
        d["wada2" + sfx] = np.ascontiguousarray(inp["w_ada"][i][:, 3 * 1024:])
        d["bada2" + sfx] = vec_fm(inp["b_ada"][i][3 * 1024:])
        d["ng" + sfx] = np.ascontiguousarray(inp["norm_g"][i].reshape(6, 8, P).transpose(2, 0, 1).reshape(P, 48))
        d["w1a" + sfx] = np.ascontiguousarray(inp["w_ffn_in"][i, 0])
        d["w2a" + sfx] = np.ascontiguousarray(inp["w_ffn_out"][i, 0])
        d["w1b" + sfx] = np.ascontiguousarray(inp["w_ffn_in"][i, 1])
        d["w2b" + sfx] = np.ascontiguousarray(inp["w_ffn_out"][i, 1])
        d["win" + sfx] = np.ascontiguousarray(inp["w_in"][i][:, cols])
        d["wg" + sfx] = np.ascontiguousarray(inp["w_in"][i][:, 5152:8224])
        d["wb" + sfx] = np.ascontiguousarray(inp["w_branch"][i].reshape(1536, 1024))
        d["wo" + sfx] = np.ascontiguousarray(inp["w_out"][i])
        d["snw" + sfx] = vec_fm(inp["ssd_norm_w"][i])
        lam_init = 0.8 - 0.6 * np.exp(-0.3 * i)
        d["lam" + sfx] = np.ascontiguousarray(np.broadcast_to(inp["diff_lambda"][i].reshape(1, 256), (P, 256)))
        d["dnw" + sfx] = np.ascontiguousarray(inp["diff_norm_w"][i].reshape(P, 1))
        d["li" + sfx] = np.full((P, 1), lam_init, np.float32)
        wcols = np.concatenate([np.arange(256 * g, 256 * g + 256), 512 + 128 * g + np.arange(128), 768 + 128 * g + np.arange(128)])
        d["scw" + sfx] = np.ascontiguousarray(inp["ssd_conv_w"][i][:, wcols].T.reshape(4, P, 5).transpose(1, 0, 2))
        d["scb" + sfx] = np.ascontiguousarray(inp["ssd_conv_b"][i][wcols].reshape(4, P).T)
        hsel = [4 * g + h for h in range(4)]
        d["sdtb" + sfx] = np.ascontiguousarray(np.broadcast_to(inp["ssd_dt_bias"][i][:, hsel].reshape(1, 8), (P, 8)))
        d["salog" + sfx] = np.ascontiguousarray(np.broadcast_to(inp["ssd_a_log"][i][:, hsel].reshape(1, 8), (P, 8)))
        d["sdsk" + sfx] = np.ascontiguousarray(np.broadcast_to(inp["ssd_d"][i][hsel].reshape(1, 4), (P, 4)))
        gcols = np.concatenate([off + h * 128 + np.arange(128) for off in (0, 512, 1024) for h in hs])
        d["gcw" + sfx] = np.ascontiguousarray(inp["gdn_conv_w"][i][:, gcols].T.reshape(6, P, 5).transpose(1, 0, 2))
        d["galog" + sfx] = np.ascontiguousarray(np.broadcast_to(inp["gdn_a_log"][i][:, hs].reshape(1, 4), (P, 4)))
        d["gdtb" + sfx] = np.ascontiguousarray(np.broadcast_to(inp["gdn_dt_bias"][i][:, hs].reshape(1, 4), (P, 4)))
        d["gnw" + sfx] = np.ascontiguousarray(np.broadcast_to(inp["gdn_norm_w"][i].reshape(1, P), (P, P)))
    return d


from concourse.bass_utils import run_bass_kernel_spmd


def kernel(**inp):
    inp = {k: np.ascontiguousarray(np.asarray(v), dtype=np.float32) for k, v in inp.items()}
    B, L, Dm = inp["x"].shape
    LC = inp["ctx"].shape[1]
    NT, NCX = L // 2, LC // 2
    nc = build_fused(L, LC, 2)
    ims = [fused_inputs(inp, c, NT, NCX, 2) for c in range(8)]
    res = run_bass_kernel_spmd(nc, ims, core_ids=list(range(8))).results
    out = np.empty((B, L, Dm), np.float32)
    for c in range(8):
        b, s = c // 2, c % 2
        out[b, s * NT:(s + 1) * NT] = unfm(res[c]["yT"])[:NT]
    return out
```
